# Optimizing a Trainium2 kernel written in Bass

```python
import math
import jax
import jax.numpy as jnp
from jax import lax
import numpy as np


D_MODEL = 1024
BATCH = 2
SEQ = 8192
DEPTH = 2
DEC_BATCH = 8
DEC_SEQ = 16
PAST_LEN = 1024

CHUNK = 64
HEAD_DIM = 64
H_A = 8
H_B = 8
KV_B = 2
G_B = H_B // KV_B
H_C = 8
W_A = H_A * HEAD_DIM
W_B = H_B * HEAD_DIM
W_C = H_C * HEAD_DIM
D_MIX = W_A + W_B + W_C
IDX_H = 8
IDX_D = 32
TOPK_MAX = 256
C_BACK = 8
C_WIN = C_BACK * CHUNK
REL_CLIP = 128
T5_BUCKETS = 32
T5_MAX_DIST = 128
Q_BLOCK = 128
EPS = 1e-6
SCALE = HEAD_DIM ** -0.5
IDX_SCALE = IDX_D ** -0.5
IDX_W_SCALE = IDX_H ** -0.5
SPLIT_SIZES = (W_A, W_A, W_A, W_A, H_A,
               W_B, KV_B * HEAD_DIM, KV_B * HEAD_DIM, W_B, IDX_H * IDX_D, IDX_H, IDX_D,
               W_C, W_C, W_C, W_C)
SPLIT_POINTS = tuple(sum(SPLIT_SIZES[:i + 1]) for i in range(len(SPLIT_SIZES) - 1))
D_IN = sum(SPLIT_SIZES)

kernel_name = 'chunk_causal_hybrid_fox_dsa_band_step'


def _rmsnorm(x, g):
    xf = x.astype(jnp.float32)
    y = xf * lax.rsqrt(jnp.mean(xf * xf, axis=-1, keepdims=True) + EPS)
    return (y * g.astype(jnp.float32)).astype(x.dtype)


def _t5_bucket(rel):
    nb = T5_BUCKETS // 2
    max_exact = nb // 2
    ret = jnp.where(rel > 0, nb, 0)
    n = jnp.abs(rel)
    nf = jnp.maximum(n, 1).astype(jnp.float32)
    large = max_exact + (jnp.log(nf / max_exact) / math.log(T5_MAX_DIST / max_exact)
                         * (nb - max_exact)).astype(jnp.int32)
    large = jnp.minimum(large, nb - 1)
    return ret + jnp.where(n < max_exact, n, large)


def _split_proj(h, w_in, b_f):
    b, n, _ = h.shape
    y = jnp.einsum('bnd,de->bne', h, w_in)
    (qa, ka, va, za, fa, qb, kb, vb, zb, iq, iw, ik, qc, kc, vc, zc) = jnp.split(y, SPLIT_POINTS, axis=-1)
    heads = lambda t: t.reshape(b, n, -1, HEAD_DIM)
    logf = jax.nn.log_sigmoid((fa + b_f).astype(jnp.float32))
    return dict(qa=heads(qa), ka=heads(ka), va=heads(va), za=za, logf=logf,
                qb=heads(qb), kb=heads(kb), vb=heads(vb), zb=zb,
                iq=iq.reshape(b, n, IDX_H, IDX_D), iw=iw, ik=ik,
                qc=heads(qc), kc=heads(kc), vc=heads(vc), zc=zc)


def _forget_block(q, cq, qpos, k, v, ck):
    s = jnp.einsum('bqhd,bkhd->bhqk', q, k).astype(jnp.float32) * SCALE
    s = s + (jnp.swapaxes(cq, 1, 2)[..., None] - jnp.swapaxes(ck, 1, 2)[..., None, :])
    mask = jnp.arange(k.shape[1])[None, :] <= qpos[:, None]
    s = jnp.where(mask, s, -jnp.inf)
    p = jax.nn.softmax(s, axis=-1).astype(v.dtype)
    return jnp.einsum('bhqk,bkhd->bqhd', p, v)


def _dsa_block(q, iq, iw, qpos, k, v, ik, n_top, t5_bias):
    b, nq = q.shape[:2]
    kpos = jnp.arange(k.shape[1])
    adm = (kpos[None, :] // CHUNK) <= (qpos[:, None] // CHUNK)
    rs = jax.nn.relu(jnp.einsum('bqhe,bke->bqhk', iq, ik).astype(jnp.float32) * IDX_SCALE)
    score = jnp.einsum('bqh,bqhk->bqk', iw.astype(jnp.float32) * IDX_W_SCALE, rs)
    score = jnp.where(adm[None], score, -jnp.inf)
    top_val, top_idx = lax.top_k(score, n_top)
    valid = jnp.isfinite(top_val)
    gather = jax.vmap(lambda rows, idx: rows[idx])
    ks = gather(k, top_idx)
    vs = gather(v, top_idx)
    qg = q.reshape(b, nq, KV_B, G_B, HEAD_DIM)
    s = jnp.einsum('bqjgd,bqnjd->bqjgn', qg, ks).astype(jnp.float32) * SCALE
    bias = t5_bias[_t5_bucket(top_idx - qpos[None, :, None])]
    s = s + jnp.transpose(bias.reshape(b, nq, n_top, KV_B, G_B), (0, 1, 3, 4, 2)).astype(jnp.float32)
    s = jnp.where(valid[:, :, None, None, :], s, -jnp.inf)
    p = jax.nn.softmax(s, axis=-1).astype(v.dtype)
    o = jnp.einsum('bqjgn,bqnjd->bqjgd', p, vs)
    return o.reshape(b, nq, H_B, HEAD_DIM)


def _band_core(q, k, v, rel, valid, c_rel):
    s = jnp.einsum('bcqhd,bckhd->bchqk', q, k).astype(jnp.float32) * SCALE
    bias = c_rel[jnp.clip(rel, -REL_CLIP, REL_CLIP) + REL_CLIP]
    s = s + jnp.transpose(bias, (2, 0, 1)).astype(jnp.float32)
    s = jnp.where(valid[None, :, None, None, :], s, -jnp.inf)
    p = jax.nn.softmax(s, axis=-1).astype(v.dtype)
    return jnp.einsum('bchqk,bckhd->bcqhd', p, v)


def _band_prompt(q, k, v, c_rel):
    b, L, H, D = q.shape
    nc = L // CHUNK
    nb = C_BACK + 1
    shape = (b, nc, CHUNK, H, D)
    pad = jnp.zeros((b, C_BACK, CHUNK, H, D), k.dtype)
    kc = jnp.concatenate([pad, k.reshape(shape)], axis=1)
    vc = jnp.concatenate([pad, v.reshape(shape)], axis=1)
    band = jnp.arange(nc)[:, None] + jnp.arange(nb)[None, :]
    kband = kc[:, band].reshape(b, nc, nb * CHUNK, H, D)
    vband = vc[:, band].reshape(b, nc, nb * CHUNK, H, D)
    kin = jnp.arange(nb * CHUNK)
    rel = (kin[None, :] - C_BACK * CHUNK) - jnp.arange(CHUNK)[:, None]
    valid = (jnp.arange(nc)[:, None] - C_BACK) * CHUNK + kin[None, :] >= 0
    out = _band_core(q.reshape(shape), kband, vband, rel, valid, c_rel)
    return out.reshape(b, L, H, D)


def _merge(x, oa, ob, oc, p, w_out):
    b, n = oa.shape[:2]
    g = jnp.concatenate([oa.reshape(b, n, W_A) * jax.nn.silu(p['za']),
                         ob.reshape(b, n, W_B) * jax.nn.silu(p['zb']),
                         oc.reshape(b, n, W_C) * jax.nn.silu(p['zc'])], axis=-1)
    return x + jnp.einsum('bne,ed->bnd', g, w_out)


def _layer_prompt(x, g, w_in, b_f, t5_bias, c_rel, w_out):
    b, L, _ = x.shape
    p = _split_proj(_rmsnorm(x, g), w_in, b_f)
    nblk = L // Q_BLOCK
    n_top = min(TOPK_MAX, L // 4)
    c = jnp.cumsum(p['logf'], axis=1)
    to_blocks = lambda t: jnp.swapaxes(t.reshape((b, nblk, Q_BLOCK) + t.shape[2:]), 0, 1)
    from_blocks = lambda t: jnp.swapaxes(t, 0, 1).reshape((b, L) + t.shape[3:])

    def blk(args):
        i, qa, ca, qb, iq, iw = args
        qpos = i * Q_BLOCK + jnp.arange(Q_BLOCK)
        oa = _forget_block(qa, ca, qpos, p['ka'], p['va'], c)
        ob = _dsa_block(qb, iq, iw, qpos, p['kb'], p['vb'], p['ik'], n_top, t5_bias)
        return oa, ob

    oa, ob = lax.map(blk, (jnp.arange(nblk), to_blocks(p['qa']), to_blocks(c), to_blocks(p['qb']),
                           to_blocks(p['iq']), to_blocks(p['iw'])))
    oa, ob = from_blocks(oa), from_blocks(ob)
    oc = _band_prompt(p['qc'], p['kc'], p['vc'], c_rel)
    y = _merge(x, oa, ob, oc, p, w_out)
    nkeep = min(C_WIN, L)
    state = (p['ka'], p['va'], p['logf'], p['kb'], p['vb'], p['ik'],
             p['kc'][:, L - nkeep:], p['vc'][:, L - nkeep:])
    return y, state


def _layer_sample(x, ck_a, cv_a, clf_a, ck_b, cv_b, cik_b, ck_c, cv_c, g, w_in, b_f, t5_bias, c_rel, w_out):
    b, n, _ = x.shape
    P = ck_a.shape[1]
    p = _split_proj(_rmsnorm(x, g), w_in, b_f)
    qpos = P + jnp.arange(n)
    ka = jnp.concatenate([ck_a, p['ka']], axis=1)
    va = jnp.concatenate([cv_a, p['va']], axis=1)
    c = jnp.cumsum(jnp.concatenate([clf_a.astype(jnp.float32), p['logf']], axis=1), axis=1)
    oa = _forget_block(p['qa'], c[:, P:], qpos, ka, va, c)
    kb = jnp.concatenate([ck_b, p['kb']], axis=1)
    vb = jnp.concatenate([cv_b, p['vb']], axis=1)
    ik = jnp.concatenate([cik_b, p['ik']], axis=1)
    n_top = min(TOPK_MAX, (P + n) // 4)
    ob = _dsa_block(p['qb'], p['iq'], p['iw'], qpos, kb, vb, ik, n_top, t5_bias)
    Cc = ck_c.shape[1]
    kc = jnp.concatenate([ck_c, p['kc']], axis=1)
    vc = jnp.concatenate([cv_c, p['vc']], axis=1)
    kpos = jnp.concatenate([P - Cc + jnp.arange(Cc), qpos])
    rel = kpos[None, :] - qpos[:, None]
    valid = jnp.ones((1, Cc + n), dtype=bool)
    oc = _band_core(p['qc'][:, None], kc[:, None], vc[:, None], rel, valid, c_rel)[:, 0]
    y = _merge(x, oa, ob, oc, p, w_out)
    state = (p['ka'], p['va'], p['logf'], p['kb'], p['vb'], p['ik'], p['kc'], p['vc'])
    return y, state


def setup_inputs(seed: int = 0) -> dict:
    key = jax.random.key(seed)
    ks = jax.random.split(key, 18)
    nrm = lambda k, shape: jax.random.normal(k, shape, jnp.float32)
    c_rows = min(C_WIN, PAST_LEN)
    return {
        'x_prompt': nrm(ks[0], (BATCH, SEQ, D_MODEL)),
        'x_sample': nrm(ks[1], (DEC_BATCH, DEC_SEQ, D_MODEL)),
        'cache_a_k': nrm(ks[2], (DEPTH, DEC_BATCH, PAST_LEN, H_A, HEAD_DIM)),
        'cache_a_v': nrm(ks[3], (DEPTH, DEC_BATCH, PAST_LEN, H_A, HEAD_DIM)),
        'cache_a_logf': jax.nn.log_sigmoid(2.0 + nrm(ks[4], (DEPTH, DEC_BATCH, PAST_LEN, H_A))),
        'cache_b_k': nrm(ks[5], (DEPTH, DEC_BATCH, PAST_LEN, KV_B, HEAD_DIM)),
        'cache_b_v': nrm(ks[6], (DEPTH, DEC_BATCH, PAST_LEN, KV_B, HEAD_DIM)),
        'cache_b_idx_k': nrm(ks[7], (DEPTH, DEC_BATCH, PAST_LEN, IDX_D)),
        'cache_c_k': nrm(ks[8], (DEPTH, DEC_BATCH, c_rows, H_C, HEAD_DIM)),
        'cache_c_v': nrm(ks[9], (DEPTH, DEC_BATCH, c_rows, H_C, HEAD_DIM)),
        'norm_g': 1.0 + 0.05 * nrm(ks[10], (DEPTH, D_MODEL)),
        'w_in': nrm(ks[11], (DEPTH, D_MODEL, D_IN)) * D_MODEL ** -0.5,
        'b_f': 2.0 + 0.1 * nrm(ks[12], (DEPTH, H_A)),
        't5_bias': 0.5 * nrm(ks[13], (T5_BUCKETS, H_B)),
        'c_rel_bias': 0.5 * nrm(ks[14], (DEPTH, 2 * REL_CLIP + 1, H_C)),
        'w_out': nrm(ks[15], (DEPTH, D_MIX, D_MODEL)) * D_MIX ** -0.5,
        'final_g': 1.0 + 0.05 * nrm(ks[16], (D_MODEL,)),
    }


def reference(x_prompt, x_sample, cache_a_k, cache_a_v, cache_a_logf, cache_b_k, cache_b_v, cache_b_idx_k,
              cache_c_k, cache_c_v, norm_g, w_in, b_f, t5_bias, c_rel_bias, w_out, final_g):
    hp, hs = x_prompt, x_sample
    sp, ss = [], []
    for l in range(DEPTH):
        hp, st_p = _layer_prompt(hp, norm_g[l], w_in[l], b_f[l], t5_bias, c_rel_bias[l], w_out[l])
        hs, st_s = _layer_sample(hs, cache_a_k[l], cache_a_v[l], cache_a_logf[l], cache_b_k[l], cache_b_v[l],
                                 cache_b_idx_k[l], cache_c_k[l], cache_c_v[l], norm_g[l], w_in[l], b_f[l],
                                 t5_bias, c_rel_bias[l], w_out[l])
        sp.append(st_p)
        ss.append(st_s)
    stk = lambda sts, i: jnp.stack([s[i] for s in sts], axis=0)
    y_prompt = _rmsnorm(hp, final_g)
    y_sample = _rmsnorm(hs, final_g)
    p_a_k, p_a_v, p_a_logf = stk(sp, 0), stk(sp, 1), stk(sp, 2)
    p_b_k, p_b_v, p_b_idx_k = stk(sp, 3), stk(sp, 4), stk(sp, 5)
    p_c_k, p_c_v = stk(sp, 6), stk(sp, 7)
    s_a_k, s_a_v, s_a_logf = stk(ss, 0), stk(ss, 1), stk(ss, 2)
    s_b_k, s_b_v, s_b_idx_k = stk(ss, 3), stk(ss, 4), stk(ss, 5)
    s_c_k, s_c_v = stk(ss, 6), stk(ss, 7)
    return (y_prompt, y_sample,
            p_a_k, p_a_v, p_a_logf, p_b_k, p_b_v, p_b_idx_k, p_c_k, p_c_v,
            s_a_k, s_a_v, s_a_logf, s_b_k, s_b_v, s_b_idx_k, s_c_k, s_c_v)
```

```python
import math
import types
import numpy as np
from contextlib import ExitStack
import concourse.bass as bass
import concourse.mybir as mybir
from concourse.bass_utils import run_bass_kernel_spmd

F32 = mybir.dt.float32
BF16 = mybir.dt.bfloat16
ALU = mybir.AluOpType
AF = mybir.ActivationFunctionType

D_MODEL = 1024; BATCH = 2; SEQ = 8192; DEPTH = 2; DEC_BATCH = 8; DEC_SEQ = 16; PAST = 1024
HD = 64; H = 8; KVB = 2; IDX_H = 8; IDX_D = 32; TOPK = 256
SCALE = HD ** -0.5
IDXS = (IDX_D ** -0.5) * (IDX_H ** -0.5)
EPS = 1e-6
NEG = -32768.0
NG = SEQ // 512
NBIS = 24
LTAB = 384

_SPLIT = (512, 512, 512, 512, 8, 512, 128, 128, 512, 256, 8, 32, 512, 512, 512, 512)
_OFF = np.concatenate([[0], np.cumsum(_SPLIT)])
(QA, KA, VA, ZA, FA, QB, KB, VB, ZB, IQ, IW, IK, QC, KC, VC, ZC) = [int(o) for o in _OFF[:-1]]
FM_UNITS = ["qa", "ka", "za", "qb", "zb", "bx", "qc", "kc", "zc"]
TM_UNITS = ["tka", "tva", "tb", "tkc", "tvc"]
UNITS = FM_UNITS + TM_UNITS
NU = len(UNITS)


def _unit_cols(name):
    r = lambda a, n: list(range(a, a + n))
    pad = lambda l: l + [-1] * (512 - len(l))
    if name == "qa": return r(QA, 512)
    if name == "ka": return r(KA, 512)
    if name == "za": return r(ZA, 512)
    if name == "qb": return r(QB, 512)
    if name == "zb": return r(ZB, 512)
    if name == "qc": return r(QC, 512)
    if name == "kc": return r(KC, 512)
    if name == "zc": return r(ZC, 512)
    if name == "bx": return pad(r(KB, 128) + r(IQ, 256) + r(IK, 32) + r(IK, 32))
    if name == "tka": return r(KA, 512)
    if name == "tva": return r(VA, 512)
    if name == "tkc": return r(KC, 512)
    if name == "tvc": return r(VC, 512)
    if name == "tb": return pad(r(KB, 128) + r(VB, 128) + r(IK, 32) + r(FA, 8) + r(IW, 8))
    raise KeyError(name)


class Prog:
    STREAMS = ("pe", "act", "dve", "pool", "sp")

    def __init__(self, nc):
        self.nc = nc
        self.ops = []
        self.ndma = {}

    @staticmethod
    def _freeze(fn):
        if fn.__closure__ is None:
            return fn
        cells = []
        for c in fn.__closure__:
            try:
                cells.append(types.CellType(c.cell_contents))
            except ValueError:
                cells.append(c)
        return types.FunctionType(fn.__code__, fn.__globals__, fn.__name__, fn.__defaults__, tuple(cells))

    NSUB = 16

    def add(self, stream, fn, r=(), w=(), dma=False, cc=False):
        fn = self._freeze(fn)
        if cc:
            track = "dma_cc"
        elif dma:
            k = self.ndma.get(stream, 0)
            self.ndma[stream] = k + 1
            track = f"dma_{stream}#{k % self.NSUB}"
        else:
            track = stream
        self.ops.append((stream, track, fn, tuple(r), tuple(w)))

    def pe(self, fn, r=(), w=()): self.add("pe", fn, r, w)
    def act(self, fn, r=(), w=()): self.add("act", fn, r, w)
    def dve(self, fn, r=(), w=()): self.add("dve", fn, r, w)
    def pool(self, fn, r=(), w=()): self.add("pool", fn, r, w)

    def dma(self, stream, out, in_, r=(), w=(), **kw):
        self.add(stream, lambda e: e.dma_start(out=out, in_=in_, **kw), r, w, dma=True)

    def finalize_and_emit(self):
        nc = self.nc
        ops = self.ops
        n = len(ops)
        writers = {}
        readers = {}
        prev_on = {}
        deps = [None] * n
        signal = [False] * n
        qof = lambda t: t.split("#")[0]
        for i, (stream, track, fn, R, W) in enumerate(ops):
            d = set()
            is_dma = track.startswith("dma_")
            if is_dma:
                j = prev_on.get(track)
                if j is not None:
                    d.add(j)
                prev_on[track] = i
            for res in R:
                for tj, j in writers.get(res, {}).items():
                    if tj != track or is_dma or track != "pe":
                        d.add(j)
            for res in W:
                lazy = res.startswith("~")
                for tj, j in writers.get(res, {}).items():
                    if lazy:
                        if qof(tj) != qof(track):
                            d.add(j)
                    elif tj != track or is_dma:
                        d.add(j)
                for tj, j in readers.get(res, {}).items():
                    if lazy:
                        if qof(tj) != qof(track):
                            d.add(j)
                    elif tj != track or is_dma:
                        d.add(j)
            for res in R:
                readers.setdefault(res, {})[track] = i
            for res in W:
                if res.startswith("~"):
                    writers.setdefault(res, {})[track] = i
                else:
                    writers[res] = {track: i}
                    readers[res] = {}
            d.discard(i)
            deps[i] = d
            for j in d:
                signal[j] = True
        tracks = sorted({o[1] for o in ops})
        cnt = {t: 0 for t in tracks}
        val = [0] * n
        for i, (stream, track, fn, R, W) in enumerate(ops):
            if track == "dma_cc":
                cnt[track] += 1
                val[i] = cnt[track]
            elif track.startswith("dma_"):
                cnt[track] += 16
                val[i] = cnt[track]
            elif signal[i]:
                cnt[track] += 1
                val[i] = cnt[track]
        known = {s: {t: 0 for t in tracks} for s in self.STREAMS}
        waits = [None] * n
        for i, (stream, track, fn, R, W) in enumerate(ops):
            need = {}
            for j in deps[i]:
                tj = ops[j][1]
                need[tj] = max(need.get(tj, 0), val[j])
            wl = []
            for tj, v in need.items():
                if v > known[stream][tj]:
                    wl.append((tj, v))
                    known[stream][tj] = v
            waits[i] = wl
        self.stats = dict(cnt)
        by_stream = {s: [] for s in self.STREAMS}
        for i, o in enumerate(ops):
            by_stream[o[0]].append(i)
        with ExitStack() as es:
            sems = {t: es.enter_context(nc.semaphore("s_" + t.replace("#", "_"))) for t in tracks}
            block = es.enter_context(nc.Block())

            def run(eng, stream):
                for i in by_stream[stream]:
                    _, track, fn, R, W = ops[i]
                    for tj, v in waits[i]:
                        eng.wait_ge(sems[tj], v)
                    inst = fn(eng)
                    if track == "dma_cc":
                        inst.then_inc(sems[track], 1)
                    elif track.startswith("dma_"):
                        inst.then_inc(sems[track], 16)
                    elif signal[i]:
                        inst.then_inc(sems[track], 1)
                if stream == "sp":
                    for t in tracks:
                        if t.startswith("dma_") and cnt[t] > known[stream][t]:
                            eng.wait_ge(sems[t], cnt[t])

            @block.tensor
            def _(e): run(e, "pe")

            @block.scalar
            def _(e): run(e, "act")

            @block.vector
            def _(e): run(e, "dve")

            @block.gpsimd
            def _(e): run(e, "pool")

            @block.sync
            def _(e): run(e, "sp")


def build_program(do_sample=True, nm=4, nlayers=DEPTH):
    nc = bass.Bass("TRN2", target_bir_lowering=False)
    es = ExitStack()
    din = lambda name, shape, dt=F32: nc.dram_tensor(name, list(shape), dt, kind="ExternalInput").ap()
    dout = lambda name, shape: nc.dram_tensor(name, list(shape), F32, kind="ExternalOutput").ap()
    dscr = lambda name, shape, dt=BF16: nc.dram_tensor(name, list(shape), dt, kind="Internal").ap()
    xp = din("xp", [SEQ, D_MODEL])
    wu = din("wu", [DEPTH, NU, 128, 8, 512])
    wo = din("wo", [DEPTH, 6, 64, 4, 1024])
    gcol_d = din("gcol", [128, DEPTH * 8])
    fg_d = din("fg", [1, D_MODEL])
    bf_d = din("bfb", [1, DEPTH * 8])
    t5_d = din("t5", [32, 8])
    crel_d = din("crel", [DEPTH, 3, 128, 8])
    oh5_d = din("oh5", [32, LTAB])
    ohc_d = din("ohc", [3, 128, LTAB])
    cst_d = din("cst", [128, 8 * 128])
    y_q = dout("y_q", [2048, D_MODEL])
    xq = din("xq", [4, 512, D_MODEL]); xprev = din("xprev", [4, 512, D_MODEL])
    pc_d = din("pcore", [128, 1024])
    iota_d = din("iota5", [128, 512])
    o_ak = dout("o_ak", [DEPTH, SEQ, 512]); o_av = dout("o_av", [DEPTH, SEQ, 512])
    o_lf = dout("o_lf", [DEPTH, SEQ, 8])
    o_bk = dout("o_bk", [DEPTH, SEQ, 128]); o_bv = dout("o_bv", [DEPTH, SEQ, 128])
    o_ik = dout("o_ik", [DEPTH, SEQ, 32])
    o_ck = dout("o_ck", [DEPTH, 512, 512]); o_cv = dout("o_cv", [DEPTH, 512, 512])
    wub = dscr("wub", [DEPTH, NU, 128, 8, 512])
    wob = dscr("wob", [DEPTH, 6, 64, 4, 1024])
    hp1q = dscr("hp1q", [2048, D_MODEL], F32)
    hpg = dscr("hpg", [SEQ, D_MODEL], F32)
    ccs = dscr("ccs", [256, D_MODEL], F32)
    COMBS = dscr("combs", [4, 17, 2, 128, 512])
    AMASK = dscr("amask", [16, 128, 512])
    ccd = dscr("ccd", [1024, D_MODEL], F32)
    S_KCL = dscr("scr_kcl", [DEPTH, 8, 64, 1024]); S_VCL = dscr("scr_vcl", [DEPTH, 1024, 512])
    tab5 = dscr("tab5", [8, LTAB], F32)
    tabc = dscr("tabc", [DEPTH, 8, LTAB], F32)
    S_KA = dscr("scr_s_ka", [DEPTH, 8, 64, SEQ]); S_KC = dscr("scr_s_kc", [DEPTH, 8, 64, SEQ])
    S_KB = dscr("scr_s_kb", [DEPTH, 2, 64, SEQ]); S_IK = dscr("scr_s_ik", [DEPTH, 64, SEQ])
    S_VA = dscr("scr_s_va", [DEPTH, SEQ, 512]); S_VC = dscr("scr_s_vc", [DEPTH, SEQ, 512])
    S_VB = dscr("scr_s_vb", [DEPTH, SEQ, 128])

    xs = din("xs", [DEC_SEQ, D_MODEL])
    ca_k = din("ca_k", [DEPTH, PAST, 512]); ca_v = din("ca_v", [DEPTH, PAST, 512]); ca_lf = din("ca_lf", [DEPTH, PAST, 8])
    cb_k = din("cb_k", [DEPTH, PAST, 128]); cb_v = din("cb_v", [DEPTH, PAST, 128]); cb_ik = din("cb_ik", [DEPTH, PAST, 32])
    cc_k = din("cc_k", [DEPTH, 512, 512]); cc_v = din("cc_v", [DEPTH, 512, 512])
    y_s = dout("y_s", [DEC_SEQ, D_MODEL])
    s_ak = dout("s_ak", [DEPTH, DEC_SEQ, 512]); s_av = dout("s_av", [DEPTH, DEC_SEQ, 512]); s_lf = dout("s_lf", [DEPTH, DEC_SEQ, 8])
    s_bk = dout("s_bk", [DEPTH, DEC_SEQ, 128]); s_bv = dout("s_bv", [DEPTH, DEC_SEQ, 128]); s_ik = dout("s_ik", [DEPTH, DEC_SEQ, 32])
    s_ck = dout("s_ck", [DEPTH, DEC_SEQ, 512]); s_cv = dout("s_cv", [DEPTH, DEC_SEQ, 512])
    hs1 = dscr("hs1", [DEC_SEQ, D_MODEL], F32)
    MBS = [dscr(f"mbs{i}", [128, SEQ]) for i in range(4)]
    SS_KA = dscr("ss_ka", [DEPTH, 8, 64, 1152]); SS_KC = dscr("ss_kc", [DEPTH, 8, 64, 640])
    SS_KB = dscr("ss_kb", [DEPTH, 2, 64, 1152]); SS_IK = dscr("ss_ik", [DEPTH, 64, 1152])
    SS_VA = dscr("ss_va", [DEPTH, 1152, 512]); SS_VC = dscr("ss_vc", [DEPTH, 640, 512]); SS_VB = dscr("ss_vb", [DEPTH, 1152, 128])

    sb = lambda name, shape, dt: es.enter_context(nc.sbuf_tensor(name, list(shape), dt))
    wring = [sb(f"wring{i}", [128, 8, 512], BF16) for i in range(2)]
    SC = sb("SC", [128, 8192], F32)
    junk = sb("junk", [128, 8192], BF16)
    hT = sb("hT", [128, 8, 512], BF16)
    Q = sb("Q", [65, 8, 512], BF16)
    zg = {t: sb("zg" + t, [64, 8, 512], BF16) for t in "abc"}
    iqT = sb("iqT", [64, 4, 512], BF16)
    kst = sb("kst", [64, 8, 512], BF16)
    xt = [sb(f"xt{i}", [128, 1024], F32) for i in range(2)]
    xn = sb("xn", [128, 1024], BF16)
    st = [sb(f"st{i}", [128, 512], F32) for i in range(2)]
    vst = [sb(f"vst{i}", [128, 512], BF16) for i in range(2)]
    kbuf = [sb(f"kbuf{i}", [65, 4, 512], BF16) for i in range(2)]
    vbuf = [sb(f"vbuf{i}", [128, 4, 4, 65], BF16) for i in range(2)]
    Pt = [sb(f"Pt{i}", [128, 512], BF16) for i in range(4)]
    Mb = [sb(f"Mb{i}", [128, 512], BF16) for i in range(2)]
    Rr = [sb(f"Rr{i}", [128, 512], F32) for i in range(3)]
    ikbuf = [sb(f"ikbuf{i}", [64, 512], BF16) for i in range(2)]
    b5 = sb("b5", [128, 2, 8, 128], BF16)
    bc = sb("bc", [128, 2, 8, 128], BF16)
    cstf = sb("cstf", [128, 8, 128], F32)
    identb = sb("identb", [128, 128], BF16)
    i4b = sb("i4b", [128, 4, 128], BF16)
    ma0b = sb("ma0b", [128, 128], BF16); cm0b = sb("cm0b", [128, 128], BF16); cm4b = sb("cm4b", [128, 128], BF16)
    cstore = sb("cstore", [128, 64, 8], F32)
    nbias = sb("nbias", [128, 64, 8], F32)
    gcol = sb("gcol_s", [128, DEPTH * 8], F32)
    fgb = sb("fgb", [128, D_MODEL], F32)
    bfb = sb("bfb_s", [128, DEPTH * 8], F32)
    small = sb("small", [128, 64], F32)
    cntb = sb("cntb", [128, NBIS], F32)
    wabs = sb("wabs", [128, 4, 8], F32); wsgn = sb("wsgn", [128, 4, 8], F32)
    lfb = sb("lfb", [128, 8], F32)
    tot = sb("tot", [1, 8], F32)
    tots = sb("tots", [1, 17, 8], F32)
    totbc = sb("totbc", [128, 8], F32)
    lf4 = sb("lf4", [128, 4, 8], F32)
    cown = sb("cown", [128, 4, 8], F32)
    xacc = sb("xacc", [128, D_MODEL], F32)
    pcore = sb("pcore_s", [128, 1024], F32)
    iota5 = sb("iota5_s", [128, 512], F32)
    comb = [sb(f"comb{i}", [128, 4, 128], BF16) for i in range(2)]
    ones1 = sb("ones1", [65, 128], F32)
    cbc = sb("cbc", [128, 8], F32)
    rq = sb("rq", [128, 4, 8], F32)
    rT = sb("rT", [8, 512], BF16)
    rden = sb("rden", [65, 512], F32)
    otmp = sb("otmp", [64, 512], F32)
    hank = sb("hank", [128, 128], F32)
    t5s = sb("t5s", [32, 8], F32); oh5s = sb("oh5s", [32, LTAB], F32)
    crs = sb("crs", [128, 3, 8], F32); ohcs = sb("ohcs", [128, 3, LTAB], F32)
    tabs = sb("tabs", [8, LTAB], F32)
    ps = [es.enter_context(nc.psum_tensor(f"ps{i}", [128, 512], F32)) for i in range(8)]
    psn = [f"ps{i}" for i in range(8)]

    P = Prog(nc)
    _early = {}

    def nxt_early(key, n):
        v = _early.get(key, 0)
        _early[key] = v + 1
        return v % n

    IDENT = cstf[:, 0, :]; JM = cstf[:, 1, :]; TRI = cstf[:, 2, :]; E0ROW = cstf[:, 3, :]
    ADM = cstf[:, 7, :]
    E127 = cstf[:, 1, 0:1]

    P.dma("sp", pcore[:], pc_d, w=["qrelb", "krel", "qlimc", "sel01", "selb", "pvb"])
    P.dma("sp", iota5[:], iota_d, w=["iota5"])
    qrelb = pcore[:, 0:512]; krel = pcore[:, 512:528]; qlimc = pcore[:, 528:544]; sel01 = pcore[:, 544:680]
    selb = pcore[:, 680:684]; pvb = pcore[:, 684:685]
    P.dma("sp", cstf[:].rearrange("p a b -> p (a b)"), cst_d, w=["cstf"])
    P.dma("sp", gcol[:], gcol_d, w=["gcol"])
    P.dma("sp", fgb[:], fg_d.to_broadcast([128, D_MODEL]) if hasattr(fg_d, "to_broadcast") else bass.AP(fg_d.tensor, 0, [[0, 128], [1, D_MODEL]]), w=["fgb"])
    P.dma("sp", bfb[:], bass.AP(bf_d.tensor, 0, [[0, 128], [1, DEPTH * 8]]), w=["bfb"])
    P.dma("sp", t5s[:], t5_d, w=["t5s"])
    P.dma("sp", oh5s[:], oh5_d, w=["oh5s"])
    P.dma("sp", ohcs[:], ohc_d.rearrange("c p l -> p c l"), w=["ohcs"])
    P.dve(lambda e: e.tensor_copy(identb[:], IDENT), r=["cstf"], w=["identb"])
    for k in range(4):
        P.dve(lambda e, k=k: e.tensor_copy(i4b[:, k, :], IDENT), r=["cstf"], w=["i4b"])
    P.dve(lambda e: e.tensor_copy(ma0b[:], cstf[:, 4, :]), r=["cstf"], w=["ma0b"])
    P.dve(lambda e: e.tensor_copy(cm0b[:], cstf[:, 5, :]), r=["cstf"], w=["cm0b"])
    P.dve(lambda e: e.tensor_copy(cm4b[:], cstf[:, 6, :]), r=["cstf"], w=["cm4b"])
    onesf = cstf[:, 4, :]
    P.dve(lambda e: e.memset(onesf, 1.0), r=["ma0b"], w=["cstf", "onesf"])
    P.dve(lambda e: e.memset(ones1[:], 1.0), w=["ones1"])
    for i in range(2):
        P.pool(lambda e, i=i: e.memset(kbuf[i][:], 1.0), w=[f"kbuf{i}"])
        P.pool(lambda e, i=i: e.memset(vbuf[i][:], 1.0), w=[f"vbuf{i}"])

    stg = SC[:, 0:4096].rearrange("p (c n) -> p c n", c=8)
    stgb = junk[:, 0:4096].rearrange("p (c n) -> p c n", c=8)
    for l in range(nlayers):
        for u in range(NU):
            P.dma("sp", stg, wu[l, u], w=["SC"])
            P.act(lambda e: e.activation(stgb, stg, AF.Copy), r=["SC"], w=["junk"])
            P.dma("sp", wub[l, u], stgb, r=["junk"], w=[f"wub{l}_{u}"])
        for u in range(6):
            so = SC[0:64, 0:4096].rearrange("p (c n) -> p c n", c=4)
            sob = junk[0:64, 0:4096].rearrange("p (c n) -> p c n", c=4)
            P.dma("sp", so, wo[l, u], w=["SC"])
            P.act(lambda e, so=so, sob=sob: e.activation(sob, so, AF.Copy), r=["SC"], w=["junk"])
            P.dma("sp", wob[l, u], sob, r=["junk"], w=[f"wob{l}_{u}"])

    def build_tab(lhs_list, rhs_list, dst, rnames):
        for i, (a, b) in enumerate(zip(lhs_list, rhs_list)):
            P.pe(lambda e, a=a, b=b, i=i: e.matmul(ps[0][0:8, 0:LTAB], a, b, start=(i == 0), stop=(i == len(lhs_list) - 1)),
                 r=rnames, w=["ps0"])
        P.dve(lambda e: e.tensor_copy(tabs[:], ps[0][0:8, 0:LTAB]), r=["ps0"], w=["tabs"])
        P.dma("sp", dst, tabs[:], r=["tabs"], w=["tabdram"])

    def build_toeplitz(tab_ap2d, dst_tile):
        for k in range(2):
            for h in range(8):
                b0 = 128 * k
                src = bass.AP(tab_ap2d.tensor, tab_ap2d.offset + h * LTAB + b0, [[1, 128], [1, 128]])
                P.dma("sp", hank[:], src, r=["tabdram"], w=["hank"])
                P.pe(lambda e: e.matmul(ps[1][:, 0:128], JM, hank[:], start=True, stop=True), r=["hank", "cstf"], w=["ps1"])
                P.dve(lambda e, k=k, h=h: e.tensor_copy(dst_tile[:, k, h, :], ps[1][:, 0:128]), r=["ps1"], w=["btile"])

    build_tab([t5s[:]], [oh5s[:]], tab5, ["t5s", "oh5s"])
    build_toeplitz(tab5, b5)
    for qb in range(4):
        for rr in range(17):
            for jj in range(2):
                ci = nxt_early("cmb", 2)
                sc0 = sel01[:, (qb * 17 + rr) * 2:(qb * 17 + rr) * 2 + 1]
                sc1 = sel01[:, (qb * 17 + rr) * 2 + 1:(qb * 17 + rr) * 2 + 2]
                P.dve(lambda e: e.tensor_scalar(comb[ci][:, :, :], b5[:, 0, 4 * jj:4 * jj + 4, :], sc0, None, ALU.mult), r=["btile", "sel01"], w=[f"comb{ci}"])
                P.dve(lambda e: e.scalar_tensor_tensor(comb[ci][:, :, :], b5[:, 1, 4 * jj:4 * jj + 4, :], sc1, comb[ci][:, :, :], ALU.mult, ALU.add),
                      r=["btile", "sel01", f"comb{ci}"], w=[f"comb{ci}"])
                P.dma("pool", COMBS[qb, rr, jj], comb[ci][:].rearrange("p a b -> p (a b)"), r=[f"comb{ci}"], w=["~combs"])
    for rr in range(16):
        mi = nxt_early("mb", 2)
        P.dve(lambda e: e.tensor_scalar(Mb[mi][:, :], qrelb[:, :], krel[:, rr:rr + 1], NEG, ALU.is_lt, ALU.mult), r=["qrelb", "krel"], w=[f"Mb{mi}"])
        P.dma("pool", AMASK[rr], Mb[mi][:, :], r=[f"Mb{mi}"], w=["~amask"])

    wk = [0]

    def load_w(l, u):
        i = wk[0] % 2
        wk[0] += 1
        P.dma("sp", wring[i][:], wub[l, u], r=[f"wub{l}_{u}"], w=[f"wring{i}"])
        return wring[i], f"wring{i}"

    def load_wo(l, u):
        i = wk[0] % 2
        wk[0] += 1
        dst = wring[i][0:64].rearrange("p c n -> p (c n)").rearrange("p (c n) -> p c n", c=4)
        P.dma("sp", dst, wob[l, u], r=[f"wob{l}_{u}"], w=[f"wring{i}"])
        return dst, f"wring{i}"

    rot = {"s": 0, "pt": 0, "kv": 0, "st": 0, "x": 0, "ik": 0, "rr": 0, "mb": 0, "ips": 0, "cmb": 0}

    def nxt(key, n):
        v = rot[key] % n
        rot[key] += 1
        return v

    def norm_block(l, xsrc_ap, tb, nrow=128, rname=None, sb_src=None):
        if sb_src is not None:
            X, xname = sb_src
        else:
            xi = nxt("x", 2)
            X = xt[xi]; xname = f"xt{xi}"
        if sb_src is None:
            P.dma("sp", X[0:nrow, :], xsrc_ap, r=(list(rname) if isinstance(rname, (list, tuple)) else ([rname] if rname else [])), w=[xname])
        P.act(lambda e: e.activation(junk[0:nrow, 0:1024], X[0:nrow, :], AF.Square, accum_out=small[0:nrow, 0:1]),
              r=[xname], w=["junk", "small0"])
        P.dve(lambda e: e.tensor_scalar(small[0:nrow, 1:2], small[0:nrow, 0:1], 1.0 / D_MODEL, EPS, ALU.mult, ALU.add), r=["small0"], w=["small1"])
        P.act(lambda e: e.activation(small[0:nrow, 2:3], small[0:nrow, 1:2], AF.Sqrt), r=["small1"], w=["small2"])
        P.dve(lambda e: e.reciprocal(small[0:nrow, 3:4], small[0:nrow, 2:3]), r=["small2"], w=["small3"])
        P.dve(lambda e: e.tensor_scalar(xn[0:nrow, :], X[0:nrow, :], small[0:nrow, 3:4], None, ALU.mult), r=[xname, "small3"], w=["xn"])
        psb = ps[7].bitcast(BF16)
        for c in range(8):
            P.pe(lambda e, c=c: e.transpose(psb[:, c * 128:c * 128 + nrow], xn[0:nrow, c * 128:(c + 1) * 128], identb[0:nrow, 0:nrow]),
                 r=["xn", "identb"], w=["ps7"])
        for c in range(8):
            P.dve(lambda e, c=c: e.tensor_scalar(hT[:, c, tb * 128:tb * 128 + nrow], psb[:, c * 128:c * 128 + nrow],
                                                 gcol[:, l * 8 + c:l * 8 + c + 1], None, ALU.mult),
                  r=["ps7", "gcol"], w=["hT"])

    def fm_unit(l, uname, ntok, evac, blocks=tuple(range(8)), wres=None):
        W, wn = wres if wres is not None else load_w(l, UNITS.index(uname))
        for j in blocks:
            si = nxt("s", 4)
            for c in range(8):
                P.pe(lambda e, j=j, c=c, si=si: e.matmul(ps[si][0:64, 0:ntok], W[:, c, j * 64:(j + 1) * 64], hT[:, c, 0:ntok],
                                                        start=(c == 0), stop=(c == 7)), r=[wn, "hT"], w=[psn[si]])
            evac(j, ps[si], psn[si])

    def finish_head(O, oname, zt, zname, hsel, ncol, csl):
        P.dve(lambda e: e.reciprocal(rden[64:65, 0:ncol], O[64:65, 0:ncol]), r=[oname], w=["rden"])
        bi_ = nxt("s", 4)
        P.pe(lambda e: e.matmul(ps[bi_][0:64, 0:ncol], ones1[64:65, 0:64], rden[64:65, 0:ncol], start=True, stop=True),
             r=["rden", "ones1"], w=[psn[bi_]])
        P.dve(lambda e: e.tensor_copy(otmp[:, 0:ncol], ps[bi_][0:64, 0:ncol]), r=[psn[bi_]], w=["otmp"])
        P.dve(lambda e: e.tensor_tensor(otmp[:, 0:ncol], O[0:64, 0:ncol], otmp[:, 0:ncol], ALU.mult), r=[oname, "otmp"], w=["otmp"])
        if isinstance(hsel, tuple):
            zv = zt[:, hsel[0]:hsel[1], csl]
            ov = otmp[:, 0:ncol].rearrange("p (h q) -> p h q", h=hsel[1] - hsel[0])
        else:
            zv = zt[:, hsel, csl]
            ov = otmp[:, 0:ncol]
        P.dve(lambda e: e.tensor_tensor(zv, zv, ov, ALU.mult), r=[zname, "otmp"], w=[zname])

    def load_kv(Ksrc, Vsrc, h0, nh, k0, nk, krows_name):
        i = nxt("kv", 2)
        P.dma("sp", kbuf[i][0:64, 0:nh, 0:nk], Ksrc[h0:h0 + nh, :, k0:k0 + nk].rearrange("h d k -> d h k"), r=[krows_name], w=[f"kbuf{i}"])
        nb = (nk + 127) // 128
        for b in range(nb):
            n = min(128, nk - b * 128)
            P.dma("sp", vbuf[i][0:n, b, 0:nh, 0:64],
                  Vsrc[k0 + b * 128:k0 + b * 128 + n, h0 * 64:(h0 + nh) * 64].rearrange("k (h d) -> k h d", h=nh),
                  r=[krows_name], w=[f"vbuf{i}"])
        return i

    class Attn:
        def __init__(self):
            self.pend = []

        def tile(self, t):
            si = nxt("s", 4)
            S = ps[si]; n = t["n"]; qlo, qhi = t["qlo"], t["qhi"]
            nadd = len(t["adds"])
            P.pe(lambda e: e.matmul(S[0:n, qlo:qhi], t["kT"], t["qap"], start=True, stop=(nadd == 0)),
                 r=t["names"] + ["Q"], w=[psn[si]])
            for ai, (clo, chi, la, ra, an) in enumerate(t["adds"]):
                P.pe(lambda e, clo=clo, chi=chi, la=la, ra=ra, ai=ai: e.matmul(S[0:n, clo:chi], la, ra, start=False, stop=(ai == nadd - 1)),
                     r=an, w=[psn[si]])
            pi = nxt("pt", 4)
            if t["bias"] is not None:
                P.act(lambda e: e.activation(Pt[pi][0:n, qlo:qhi], S[0:n, qlo:qhi], AF.Exp, bias=t["bias"]),
                      r=[psn[si], "nbias"], w=[f"Pt{pi}"])
            else:
                P.act(lambda e: e.activation(Pt[pi][0:n, qlo:qhi], S[0:n, qlo:qhi], AF.Exp), r=[psn[si]], w=[f"Pt{pi}"])
            t["pi"] = pi
            self.pend.append(t)
            if len(self.pend) > 2:
                self.pv(self.pend.pop(0))

        def pv(self, t):
            n = t["n"]; qlo, qhi = t["qlo"], t["qhi"]; pi = t["pi"]; O = t["O"]
            P.pe(lambda e: e.matmul(O[0:65, qlo:qhi], t["v"], Pt[pi][0:n, qlo:qhi], start=t["first"], stop=t["last"]),
                 r=[f"Pt{pi}"] + t["names"], w=[t["oname"]])

        def flush(self):
            while self.pend:
                self.pv(self.pend.pop(0))


    for l in range(nlayers):
        P.pool(lambda e: e.memset(crs[:], 0.0), w=["crs"])
        P.dma("sp", crs[:], crel_d[l].rearrange("c p h -> p c h"), w=["crs"])
        build_tab([crs[:, c, :] for c in range(3)], [ohcs[:, c, :] for c in range(3)], tabc[l], ["crs", "ohcs"])
        build_toeplitz(tabc[l], bc)
        P.dve(lambda e: e.memset(tot[:], 0.0), w=["tot"])
        P.dve(lambda e: e.memset(tots[:], 0.0), w=["tots"])
        P.dve(lambda e: e.memset(totbc[:], 0.0), w=["totbc"])
        KAl, KBl, IKl, VAl, VBl = S_KA[l], S_KB[l], S_IK[l], S_VA[l], S_VB[l]
        KCl, VCl = S_KCL[l], S_VCL[l]
        hist = f"~hist{l}"
        chist = f"~chist{l}"

        def grow(gp):
            if l == 0:
                return xp[gp * 512:(gp + 1) * 512, :]
            return hpg[gp * 512:(gp + 1) * 512, :]

        def tm_unit(uname, handler, wres=None):
            W, wn = wres if wres is not None else load_w(l, UNITS.index(uname))
            for tb in range(4):
                si = nxt("s", 4)
                for c in range(8):
                    P.pe(lambda e: e.matmul(ps[si][:, :], hT[:, c, tb * 128:(tb + 1) * 128], W[:, c, :], start=(c == 0), stop=(c == 7)),
                         r=[wn, "hT"], w=[psn[si]])
                k = nxt("st", 2)
                S_ = st[k]; sn = f"st{k}"
                P.act(lambda e: e.activation(S_[:], ps[si][:], AF.Copy), r=[psn[si]], w=[sn])
                handler(tb, S_, sn, k)

        def logf_of(S_, sn):
            P.dve(lambda e: e.tensor_tensor(lfb[:], S_[:, 288:296], bfb[:, l * 8:(l + 1) * 8], ALU.add), r=[sn, "bfb"], w=["lfb"])
            P.act(lambda e: e.activation(lfb[:], lfb[:], AF.Exp, scale=-1.0), r=["lfb"], w=["lfb"])
            P.act(lambda e: e.activation(lfb[:], lfb[:], AF.Ln, bias=1.0), r=["lfb"], w=["lfb"])
            P.dve(lambda e: e.tensor_scalar(lfb[:], lfb[:], -1.0, None, ALU.mult), r=["lfb"], w=["lfb"])

        def cum_into(dst_ap, dname):
            P.pe(lambda e: e.matmul(ps[5][:, 0:8], TRI, lfb[:], start=True, stop=False), r=["lfb", "cstf"], w=["ps5"])
            P.pe(lambda e: e.matmul(ps[5][:, 0:8], ones1[0:1, 0:128], tot[0:1, :], start=False, stop=True), r=["tot", "ones1"], w=["ps5"])
            P.dve(lambda e: e.tensor_copy(dst_ap, ps[5][:, 0:8]), r=["ps5"], w=[dname])
            P.pe(lambda e: e.matmul(ps[5][0:1, 8:16], E127, dst_ap, start=True, stop=True), r=[dname, "cstf"], w=["ps5"])
            P.dve(lambda e: e.tensor_copy(tot[:], ps[5][0:1, 8:16]), r=["ps5"], w=["tot"])

        def evac_k_to(dst3, k0, hname):
            def f(j, pt, pn):
                P.act(lambda e: e.activation(kst[:, j, :], pt[0:64, :], AF.Copy), r=[pn], w=["kst"])
                if j == 7:
                    P.dma("pool", dst3[:, :, k0:k0 + 512].rearrange("h d k -> d h k"), kst[:], r=["kst"], w=[hname])
            return f

        def evac_q(scale):
            def f(j, pt, pn):
                P.act(lambda e: e.activation(Q[0:64, j, :], pt[0:64, :], AF.Copy, scale=scale), r=[pn], w=["Q"])
            return f

        def evac_z(zt, zn):
            def f(j, pt, pn):
                P.act(lambda e: e.activation(zt[:, j, :], pt[0:64, :], AF.Silu), r=[pn], w=[zn])
            return f

        scb = SC.bitcast(BF16)
        kres = {}
        for ui, un in enumerate(("tka", "tva", "tb", "ka")):
            v = scb[:, ui * 4096:(ui + 1) * 4096].rearrange("p (c n) -> p c n", c=8)
            P.dma("sp", v, wub[l, UNITS.index(un)], r=[f"wub{l}_{UNITS.index(un)}"], w=["SC"])
            kres[un] = (v, "SC")
        v = junk[:, 4096:8192].rearrange("p (c n) -> p c n", c=8)
        P.dma("sp", v, wub[l, UNITS.index("bx")], r=[f"wub{l}_{UNITS.index('bx')}"], w=["junk", "junkW"])
        kres["bx"] = (v, "junkW")
        for gp in range(4 * nm):
            t0 = gp * 512
            src = grow(gp)
            for tb in range(4):
                norm_block(l, src[tb * 128:(tb + 1) * 128, :], tb, rname=("~hpgw" if l > 0 else None))

            def h_kv(uname):
                def f(tb, S_, sn, k):
                    r0 = t0 + tb * 128
                    dst = {"tka": o_ak, "tva": o_av, "tkc": o_ck, "tvc": o_cv}[uname]
                    if uname in ("tka", "tva"):
                        P.dma("pool", dst[l, r0:r0 + 128, :], S_[:], r=[sn])
                    else:
                        P.dma("pool", dst[l, r0 - (SEQ - 512):r0 - (SEQ - 512) + 128, :], S_[:], r=[sn])
                    if uname == "tva":
                        V_, vn = vst[k], f"vst{k}"
                        P.dve(lambda e: e.tensor_copy(V_[:], S_[:]), r=[sn], w=[vn])
                        P.dma("pool", VAl[r0:r0 + 128, :], V_[:], r=[vn], w=[hist])
                return f

            def h_tb(tb, S_, sn, k):
                r0 = t0 + tb * 128
                P.dma("pool", o_bk[l, r0:r0 + 128, :], S_[:, 0:128], r=[sn])
                P.dma("pool", o_bv[l, r0:r0 + 128, :], S_[:, 128:256], r=[sn])
                P.dma("pool", o_ik[l, r0:r0 + 128, :], S_[:, 256:288], r=[sn])
                V_, vn = vst[k], f"vst{k}"
                P.dve(lambda e: e.tensor_copy(V_[:, 0:128], S_[:, 128:256]), r=[sn], w=[vn])
                P.dma("pool", VBl[r0:r0 + 128, :], V_[:, 0:128], r=[vn], w=[hist])
                P.dve(lambda e: e.tensor_tensor(lf4[:, tb, :], S_[:, 288:296], bfb[:, l * 8:(l + 1) * 8], ALU.add), r=[sn, "bfb"], w=["lf4"])

            tm_unit("tka", h_kv("tka"), wres=kres["tka"])
            tm_unit("tva", h_kv("tva"), wres=kres["tva"])
            tm_unit("tb", h_tb, wres=kres["tb"])
            lf4f = lf4[:].rearrange("p b h -> p (b h)")
            P.act(lambda e: e.activation(lf4f, lf4f, AF.Exp, scale=-1.0), r=["lf4"], w=["lf4"])
            P.act(lambda e: e.activation(lf4f, lf4f, AF.Ln, bias=1.0), r=["lf4"], w=["lf4"])
            P.dve(lambda e: e.tensor_scalar(lf4f, lf4f, -1.0, None, ALU.mult), r=["lf4"], w=["lf4"])
            P.dma("pool", o_lf[l, t0:t0 + 512, :].rearrange("(b p) h -> p b h", p=128), lf4[:], r=["lf4"])
            if gp == NG - 1:
                tm_unit("tkc", h_kv("tkc"))
                tm_unit("tvc", h_kv("tvc"))
            fm_unit(l, "ka", 512, evac_k_to(KAl, t0, hist), wres=kres["ka"])
            for b_ in range(4):
                for b2 in range(b_ + 1):
                    P.pe(lambda e: e.matmul(ps[5][:, b_ * 8:(b_ + 1) * 8], (TRI if b2 == b_ else onesf[:, :]), lf4[:, b2, :], start=(b2 == 0), stop=(b2 == b_)),
                         r=["lf4", "cstf", "onesf"], w=["ps5"])
            for b2 in range(4):
                P.pe(lambda e: e.matmul(ps[5][:, 32:40], onesf[:, :], lf4[:, b2, :], start=(b2 == 0), stop=(b2 == 3)), r=["lf4", "onesf"], w=["ps5"])
            for b_ in range(4):
                P.dve(lambda e: e.tensor_tensor(cstore[:, 4 * gp + b_, :], ps[5][:, b_ * 8:(b_ + 1) * 8], totbc[:, :], ALU.add), r=["ps5", "totbc"], w=["cstore"])
            P.dve(lambda e: e.tensor_tensor(totbc[:, :], ps[5][:, 32:40], totbc[:, :], ALU.add), r=["ps5", "totbc"], w=["totbc"])
            P.dve(lambda e: e.tensor_copy(tots[0:1, gp + 1, :], totbc[0:1, :]), r=["totbc"], w=["tots"])

            def evac_bx_k(j, pt, pn):
                if j < 2:
                    P.act(lambda e: e.activation(kst[:, j, :], pt[0:64, :], AF.Copy), r=[pn], w=["kst"])
                    if j == 1:
                        P.dma("pool", KBl[:, :, t0:t0 + 512].rearrange("h d k -> d h k"), kst[:, 0:2, :], r=["kst"], w=[hist])
                elif j == 6:
                    P.act(lambda e: e.activation(kst[:, 2, :], pt[0:64, :], AF.Copy), r=[pn], w=["kst"])
                    P.dma("pool", IKl[:, t0:t0 + 512], kst[:, 2, :], r=["kst"], w=[hist])
            fm_unit(l, "bx", 512, evac_bx_k, blocks=(0, 1, 6), wres=kres["bx"])

        for m in range(nm):
            own = (xq[m] if l == 0 else hp1q[m * 512:(m + 1) * 512, :])
            own_r = (None if l == 0 else [f"hp1qc{2 * m}", f"hp1qc{2 * m + 1}"])
            for part in range(2):
                for tb in range(4):
                    if part == 1:
                        norm_block(l, own[tb * 128:(tb + 1) * 128, :], tb, rname=own_r)
                    elif l == 0:
                        norm_block(l, xprev[m][tb * 128:(tb + 1) * 128, :], tb)
                    else:
                        first = True
                        for r in range(4):
                            gq = 4 * m - 1 + r
                            if gq < 0:
                                continue
                            xi = nxt("x", 2)
                            X = xt[xi]; xname = f"xt{xi}"
                            P.dma("sp", X[:], grow(gq)[tb * 128:(tb + 1) * 128, :], r=["~hpgw"], w=[xname])
                            if first:
                                P.dve(lambda e: e.tensor_scalar(xacc[:], X[:], selb[:, r:r + 1], None, ALU.mult), r=[xname, "selb"], w=["xacc"])
                            else:
                                P.dve(lambda e: e.scalar_tensor_tensor(xacc[:], X[:], selb[:, r:r + 1], xacc[:], ALU.mult, ALU.add), r=[xname, "selb", "xacc"], w=["xacc"])
                            first = False
                        norm_block(l, None, tb, sb_src=(xacc, "xacc"))

                def h_c(uname):
                    def f(tb, S_, sn, k):
                        if uname == "tvc":
                            V_, vn = vst[k], f"vst{k}"
                            P.dve(lambda e: e.tensor_copy(V_[:], S_[:]), r=[sn], w=[vn])
                            P.dma("pool", VCl[part * 512 + tb * 128:part * 512 + (tb + 1) * 128, :], V_[:], r=[vn], w=[chist])
                    return f
                tm_unit("tvc", h_c("tvc"))
                fm_unit(l, "kc", 512, evac_k_to(KCl, part * 512, chist))
            g0 = 16 * m
            nkb = 16 * m + 16
            for r in range(4):
                if r == 0:
                    P.dve(lambda e: e.tensor_scalar(tot[0:1, :], tots[0:1, 4 * m + r, :], selb[0:1, r:r + 1], None, ALU.mult), r=["tots", "selb"], w=["tot"])
                else:
                    P.dve(lambda e: e.scalar_tensor_tensor(tot[0:1, :], tots[0:1, 4 * m + r, :], selb[0:1, r:r + 1], tot[0:1, :], ALU.mult, ALU.add),
                          r=["tots", "selb", "tot"], w=["tot"])

            def h_own(tb, S_, sn, k):
                P.dve(lambda e: e.tensor_tensor(lf4[:, tb, :], S_[:, 288:296], bfb[:, l * 8:(l + 1) * 8], ALU.add), r=[sn, "bfb"], w=["lf4"])
                P.dve(lambda e: e.tensor_scalar(wsgn[:, tb, :], S_[:, 296:304], 0.0, 2.0, ALU.is_ge, ALU.mult), r=[sn], w=["wsgn"])
                P.dve(lambda e: e.tensor_scalar(wsgn[:, tb, :], wsgn[:, tb, :], -1.0, None, ALU.add), r=["wsgn"], w=["wsgn"])
                P.dve(lambda e: e.scalar_tensor_tensor(wabs[:, tb, :], S_[:, 296:304], IDXS, wsgn[:, tb, :], ALU.mult, ALU.mult), r=[sn, "wsgn"], w=["wabs"])
            tm_unit("tb", h_own)
            lf4q = lf4[:].rearrange("p b h -> p (b h)")
            P.act(lambda e: e.activation(lf4q, lf4q, AF.Exp, scale=-1.0), r=["lf4"], w=["lf4"])
            P.act(lambda e: e.activation(lf4q, lf4q, AF.Ln, bias=1.0), r=["lf4"], w=["lf4"])
            P.dve(lambda e: e.tensor_scalar(lf4q, lf4q, -1.0, None, ALU.mult), r=["lf4"], w=["lf4"])
            for b_ in range(4):
                P.pe(lambda e: e.matmul(ps[5][:, b_ * 8:(b_ + 1) * 8], ones1[0:1, 0:128], tot[0:1, :], start=True, stop=False), r=["tot", "ones1"], w=["ps5"])
                for b2 in range(b_ + 1):
                    P.pe(lambda e: e.matmul(ps[5][:, b_ * 8:(b_ + 1) * 8], (TRI if b2 == b_ else onesf[:, :]), lf4[:, b2, :], start=False, stop=(b2 == b_)),
                         r=["lf4", "cstf", "onesf"], w=["ps5"])
            P.dve(lambda e: e.tensor_copy(cown[:].rearrange("p b h -> p (b h)"), ps[5][:, 0:32]), r=["ps5"], w=["cown"])
            P.pe(lambda e: e.matmul(ps[5][:, 16:24], E0ROW, cstore[:, g0, :], start=True, stop=True), r=["cstore", "cstf"], w=["ps5"])
            P.dve(lambda e: e.tensor_copy(cbc[:], ps[5][:, 16:24]), r=["ps5"], w=["cbc"])
            for h in range(8):
                P.dve(lambda e: e.tensor_scalar(nbias[:, 0:nkb, h], cstore[:, 0:nkb, h], cbc[:, h:h + 1], -1.0, ALU.subtract, ALU.mult),
                      r=["cstore", "cbc"], w=["nbias"])
            for tb in range(4):
                P.dve(lambda e: e.tensor_tensor(rq[:, tb, :], cown[:, tb, :], cbc[:], ALU.subtract), r=["cown", "cbc"], w=["rq"])
                P.pe(lambda e: e.transpose(ps[6][0:8, tb * 128:(tb + 1) * 128], rq[:, tb, :], IDENT), r=["rq", "cstf"], w=["ps6"])
            P.dve(lambda e: e.tensor_copy(rT[:], ps[6][0:8, :]), r=["ps6"], w=["rT"])

            def evac_bx_q(j, pt, pn):
                P.act(lambda e: e.activation(iqT[:, j - 2, :], pt[0:64, :], AF.Copy), r=[pn], w=["iqT"])

            def a_half(half):
                at = Attn()
                for sbk in range(4 * m + 4):
                    bi = load_kv(KAl, VAl, half * 2, 2, sbk * 512, 512, hist)
                    trail = (sbk >= 4 * m)
                    for kb in range(4):
                        adds = []
                        if trail:
                            rr = (sbk - 4 * m) * 4 + kb
                            mi = nxt("mb", 2)
                            P.dma("sp", Mb[mi][:, :], AMASK[rr], r=["~amask"], w=[f"Mb{mi}"])
                            adds = [(0, 512, identb[:], Mb[mi][:, :], ["identb", f"Mb{mi}"])]
                        for i in range(2):
                            hh = half * 2 + i
                            at.tile(dict(kT=kbuf[bi][0:65, i, kb * 128:(kb + 1) * 128], v=vbuf[bi][:, kb, i, 0:65], n=128, qlo=0, qhi=512,
                                         qap=Q[0:65, hh, 0:512], adds=adds, bias=nbias[:, 4 * sbk + kb, hh:hh + 1],
                                         names=[f"kbuf{bi}", f"vbuf{bi}"], O=ps[4 + 2 * (half % 2) + i], oname=psn[4 + 2 * (half % 2) + i],
                                         first=(sbk == 0 and kb == 0), last=(sbk == 4 * m + 3 and kb == 3)))
                at.flush()
                for i in range(2):
                    finish_head(ps[4 + 2 * (half % 2) + i], psn[4 + 2 * (half % 2) + i], zg["a"], "zga", half * 2 + i, 512, slice(0, 512))

            def c_half(half):
                at = Attn()
                started = [False] * 4
                for sbl in range(2):
                    bi = load_kv(KCl, VCl, half * 2, 2, sbl * 512, 512, chist)
                    for kb in range(4):
                        r_ = 4 * sbl + kb
                        qb_lo, qb_hi = max(0, r_ - 4), min(3, r_)
                        qlo, qhi = qb_lo * 128, (qb_hi + 1) * 128
                        for i in range(2):
                            hh = half * 2 + i
                            adds = []
                            for qb in range(qb_lo, qb_hi + 1):
                                dl = r_ - 4 - qb
                                c0, c1 = qb * 128, (qb + 1) * 128
                                if dl == 0:
                                    adds.append((c0, c1, identb[:], bc[:, 0, hh, :], ["identb", "btile"]))
                                    adds.append((c0, c1, identb[:], cm0b[:], ["identb", "cm0b"]))
                                elif dl == -1:
                                    adds.append((c0, c1, identb[:], bc[:, 1, hh, :], ["identb", "btile"]))
                                elif dl == -4:
                                    adds.append((c0, c1, identb[:], cm4b[:], ["identb", "cm4b"]))
                            at.tile(dict(kT=kbuf[bi][0:64, i, kb * 128:(kb + 1) * 128], v=vbuf[bi][:, kb, i, 0:65], n=128, qlo=qlo, qhi=qhi,
                                         qap=Q[0:64, hh, qlo:qhi], adds=adds, bias=(pvb[:, 0:1] if (m == 0 and sbl == 0) else None),
                                         names=[f"kbuf{bi}", f"vbuf{bi}"], O=ps[4 + 2 * (half % 2) + i], oname=psn[4 + 2 * (half % 2) + i],
                                         first=(not started[i]), last=(sbl == 1 and kb == 3)))
                            started[i] = True
                at.flush()
                for i in range(2):
                    finish_head(ps[4 + 2 * (half % 2) + i], psn[4 + 2 * (half % 2) + i], zg["c"], "zgc", half * 2 + i, 512, slice(0, 512))

            def b_topk(qb):
                NK = (16 * m + 13 + qb) * 128
                qs = slice(qb * 128, (qb + 1) * 128)
                for k0 in range(0, NK, 512):
                    nk = min(512, NK - k0)
                    ii = nxt("ik", 2)
                    P.dma("sp", ikbuf[ii][:, 0:nk], IKl[:, k0:k0 + nk], r=[hist], w=[f"ikbuf{ii}"])
                    for h in range(8):
                        base = 32 * (h % 2)
                        pi_ = 1 + nxt("ips", 3)
                        P.pe(lambda e: e.matmul(ps[pi_][:, 0:nk], iqT[base:base + 32, h // 2, qs], ikbuf[ii][base:base + 32, 0:nk], start=True, stop=True),
                             r=["iqT", f"ikbuf{ii}"], w=[psn[pi_]])
                        ri = nxt("rr", 3)
                        P.act(lambda e: e.activation(Rr[ri][:, 0:nk], ps[pi_][:, 0:nk], AF.Relu, scale=wabs[:, qb, h:h + 1]), r=[psn[pi_], "wabs"], w=[f"Rr{ri}"])
                        if h == 0:
                            P.dve(lambda e: e.tensor_scalar(SC[:, k0:k0 + nk], Rr[ri][:, 0:nk], wsgn[:, qb, 0:1], None, ALU.mult), r=[f"Rr{ri}", "wsgn"], w=["SC"])
                        else:
                            P.dve(lambda e: e.scalar_tensor_tensor(SC[:, k0:k0 + nk], Rr[ri][:, 0:nk], wsgn[:, qb, h:h + 1], SC[:, k0:k0 + nk], ALU.mult, ALU.add),
                                  r=[f"Rr{ri}", "wsgn", "SC"], w=["SC"])
                for ch in range(4):
                    c0 = g0 * 128 + ch * 512
                    wd = min(512, NK - c0)
                    if wd <= 0:
                        continue
                    ri = nxt("rr", 3)
                    P.dve(lambda e: e.tensor_scalar(Rr[ri][:, 0:wd], iota5[:, 0:wd], qlimc[:, qb * 4 + ch:qb * 4 + ch + 1], -1e30, ALU.is_ge, ALU.mult),
                          r=["iota5", "qlimc"], w=[f"Rr{ri}"])
                    P.dve(lambda e: e.tensor_tensor(SC[:, c0:c0 + wd], SC[:, c0:c0 + wd], Rr[ri][:, 0:wd], ALU.add), r=["SC", f"Rr{ri}"], w=["SC"])
                P.dve(lambda e: e.memset(cntb[:], 0.0), w=["cntb"])
                P.dve(lambda e: e.memset(small[:, 8:9], 0.0), w=["cand"])
                for it in range(NBIS):
                    stp = 64.0 * (0.5 ** it)
                    P.dve(lambda e: e.tensor_scalar(junk[:, 0:NK], SC[:, 0:NK], small[:, 8:9], 0.0, ALU.is_ge, ALU.add, accum_out=cntb[:, it:it + 1]),
                          r=["SC", "cand", "cntb"], w=["junk", "cntb"])
                    a, b_ = (stp, -0.5 * stp) if it < NBIS - 1 else (stp, -stp)
                    P.dve(lambda e: e.tensor_scalar(small[:, 9:10], cntb[:, it:it + 1], float(TOPK), a, ALU.is_ge, ALU.mult), r=["cntb"], w=["fl"])
                    P.dve(lambda e: e.scalar_tensor_tensor(small[:, 8:9], small[:, 9:10], b_, small[:, 8:9], ALU.add, ALU.add), r=["fl", "cand"], w=["cand"])
                P.dve(lambda e: e.tensor_scalar(junk[:, 0:NK], SC[:, 0:NK], small[:, 8:9], NEG, ALU.is_lt, ALU.mult), r=["SC", "cand"], w=["junk"])
                P.dma("pool", MBS[qb][:, 0:NK], junk[:, 0:NK], r=["junk"], w=[f"mbs{qb}"])

            def b_attn(qb):
                qs = slice(qb * 128, (qb + 1) * 128)
                nkq = 16 * m + 13 + qb
                at = Attn()
                for sbk in range((nkq + 3) // 4):
                    k0 = sbk * 512
                    nk = min(512, nkq * 128 - k0)
                    mi = nxt("mb", 2)
                    P.dma("sp", Mb[mi][:, 0:nk], MBS[qb][:, k0:k0 + nk], r=[f"mbs{qb}"], w=[f"Mb{mi}"])
                    bi = load_kv(KBl, VBl, 0, 2, k0, nk, hist)
                    for kb in range(nk // 128):
                        gkb = sbk * 4 + kb
                        for jj in range(2):
                            adds = [(0, 512, Mb[mi][:, kb * 128:(kb + 1) * 128], i4b[:].rearrange("p a b -> p (a b)"), [f"Mb{mi}", "i4b"])]
                            if gkb >= g0 - 1:
                                rr = gkb - g0 + 1
                                ci = nxt("cmb", 2)
                                sc0 = sel01[:, (qb * 17 + rr) * 2:(qb * 17 + rr) * 2 + 1]
                                sc1 = sel01[:, (qb * 17 + rr) * 2 + 1:(qb * 17 + rr) * 2 + 2]
                                P.dma("sp", comb[ci][:].rearrange("p a b -> p (a b)"), COMBS[qb, rr, jj], r=["~combs"], w=[f"comb{ci}"])
                                adds.append((0, 512, identb[:], comb[ci][:].rearrange("p a b -> p (a b)"), ["identb", f"comb{ci}"]))
                            at.tile(dict(kT=kbuf[bi][0:64, jj, kb * 128:(kb + 1) * 128], v=vbuf[bi][:, kb, jj, 0:65], n=128, qlo=0, qhi=512,
                                         qap=Q[0:64, 4 * jj:4 * jj + 4, qs], adds=adds, bias=None,
                                         names=[f"kbuf{bi}", f"vbuf{bi}"], O=ps[4 + 2 * (qb % 2) + jj], oname=psn[4 + 2 * (qb % 2) + jj],
                                         first=(gkb == 0), last=(gkb == nkq - 1)))
                at.flush()
                for jj in range(2):
                    finish_head(ps[4 + 2 * (qb % 2) + jj], psn[4 + 2 * (qb % 2) + jj], zg["b"], "zgb", (4 * jj, 4 * jj + 4), 512, qs)

            fm_unit(l, "bx", 512, evac_bx_q, blocks=(2, 3, 4, 5))
            b_topk(0)
            fm_unit(l, "za", 512, evac_z(zg["a"], "zga"))
            fm_unit(l, "qa", 512, evac_q(SCALE))
            for h in range(8):
                P.dma("sp", Q[64:65, h, :], rT[h:h + 1, :], r=["rT"], w=["Q"])
            a_half(0)
            a_half(1)
            b_topk(1)
            a_half(2)
            a_half(3)
            fm_unit(l, "zc", 512, evac_z(zg["c"], "zgc"))
            fm_unit(l, "qc", 512, evac_q(SCALE))
            for cq in range(4):
                c_half(cq)
            fm_unit(l, "zb", 512, evac_z(zg["b"], "zgb"))
            fm_unit(l, "qb", 512, evac_q(SCALE))
            b_attn(0)
            b_topk(2)
            b_attn(1)
            b_topk(3)
            b_attn(2)
            b_attn(3)

            allz = [zg["a"], zg["b"], zg["c"]]
            alln = ["zga", "zgb", "zgc"]
            for u in range(6):
                Wo_, won = load_wo(l, u)
                for hq in range(4):
                    hidx = u * 4 + hq
                    zt, zn = allz[hidx // 8], alln[hidx // 8]
                    for tb in range(4):
                        for n_ in range(2):
                            P.pe(lambda e: e.matmul(ps[tb * 2 + n_][:, :], zt[:, hidx % 8, tb * 128:(tb + 1) * 128], Wo_[:, hq, n_ * 512:(n_ + 1) * 512],
                                                    start=(hidx == 0), stop=(hidx == 23)), r=[zn, won], w=[psn[tb * 2 + n_]])
            for tb in range(4):
                xi = nxt("x", 2)
                X = xt[xi]; xname = f"xt{xi}"
                P.dma("sp", X[:], own[tb * 128:(tb + 1) * 128, :], r=(own_r if own_r else []), w=[xname])
                for n_ in range(2):
                    P.dve(lambda e: e.tensor_tensor(X[:, n_ * 512:(n_ + 1) * 512], X[:, n_ * 512:(n_ + 1) * 512], ps[tb * 2 + n_][:, :], ALU.add),
                          r=[xname, psn[tb * 2 + n_]], w=[xname])
                r0 = m * 512 + tb * 128
                if l < nlayers - 1:
                    P.dma("pool", hp1q[r0:r0 + 128, :], X[:], r=[xname], w=[f"hp1qc{r0 // 256}"])
                else:
                    P.act(lambda e: e.activation(junk[:, 0:1024], X[:], AF.Square, accum_out=small[:, 16:17]), r=[xname], w=["junk", "fs0"])
                    P.dve(lambda e: e.tensor_scalar(small[:, 17:18], small[:, 16:17], 1.0 / D_MODEL, EPS, ALU.mult, ALU.add), r=["fs0"], w=["fs1"])
                    P.act(lambda e: e.activation(small[:, 18:19], small[:, 17:18], AF.Sqrt), r=["fs1"], w=["fs2"])
                    P.dve(lambda e: e.reciprocal(small[:, 19:20], small[:, 18:19]), r=["fs2"], w=["fs3"])
                    P.dve(lambda e: e.scalar_tensor_tensor(X[:], X[:], small[:, 19:20], fgb[:], ALU.mult, ALU.mult), r=[xname, "fs3", "fgb"], w=[xname])
                    P.dma("pool", y_q[r0:r0 + 128, :], X[:], r=[xname])
            if l < nlayers - 1:
                for hf in range(2):
                    cidx = 2 * m + hf
                    P.dma("pool", ccs, hp1q[cidx * 256:(cidx + 1) * 256, :], r=[f"hp1qc{cidx}"], w=["ccs"])
                    P.add("pool", lambda e: e.collective_compute("AllGather", ALU.bypass, replica_groups=[[0, 1, 2, 3], [4, 5, 6, 7]],
                                                                 ins=[ccs.opt()], outs=[ccd.opt()]), r=["ccs"], w=["ccd"], cc=True)
                    for r in range(4):
                        a0 = (4 * m + r) * 512 + hf * 256
                        P.dma("pool", hpg[a0:a0 + 256, :], ccd[r * 256:(r + 1) * 256, :], r=["ccd"], w=["~hpgw"])
        if do_sample:
            NS = DEC_SEQ
            shist = f"~shist{l}"
            psb7 = ps[7].bitcast(BF16)

            def prep_cache(src2d, nrows, ncols, kdst, vdst, nheads):
                for b in range(nrows // 128):
                    xi = nxt("x", 2)
                    X = xt[xi]; xname = f"xt{xi}"
                    P.dma("sp", X[:, 0:ncols], src2d[b * 128:(b + 1) * 128, :], w=[xname])
                    P.dve(lambda e: e.tensor_copy(xn[:, 0:ncols], X[:, 0:ncols]), r=[xname], w=["xn"])
                    if vdst is not None:
                        P.dma("pool", vdst[b * 128:(b + 1) * 128, :], xn[:, 0:ncols], r=["xn"], w=[shist])
                    if kdst is not None:
                        for hh in range(nheads):
                            P.pe(lambda e: e.transpose(psb7[0:64, hh * 128:(hh + 1) * 128], xn[:, hh * 64:(hh + 1) * 64], identb[:]),
                                 r=["xn", "identb"], w=["ps7"])
                        P.act(lambda e: e.activation(kst[:, 0:nheads, 0:128], psb7[0:64, 0:nheads * 128].rearrange("p (h k) -> p h k", h=nheads), AF.Copy),
                              r=["ps7"], w=["kst"])
                        P.dma("pool", kdst[:, :, b * 128:(b + 1) * 128].rearrange("h d k -> d h k"), kst[:, 0:nheads, 0:128], r=["kst"], w=[shist])

            prep_cache(ca_k[l], PAST, 512, SS_KA[l], None, 8)
            prep_cache(ca_v[l], PAST, 512, None, SS_VA[l], 8)
            prep_cache(cb_k[l], PAST, 128, SS_KB[l], None, 2)
            prep_cache(cb_v[l], PAST, 128, None, SS_VB[l], 2)
            prep_cache(cc_k[l], 512, 512, SS_KC[l], None, 8)
            prep_cache(cc_v[l], 512, 512, None, SS_VC[l], 8)
            for b in range(PAST // 128):
                xi = nxt("x", 2)
                X = xt[xi]; xname = f"xt{xi}"
                P.dma("sp", X[:, 0:32], cb_ik[l, b * 128:(b + 1) * 128, :], w=[xname])
                P.dve(lambda e: e.tensor_copy(xn[:, 0:32], X[:, 0:32]), r=[xname], w=["xn"])
                P.dve(lambda e: e.tensor_copy(xn[:, 32:64], X[:, 0:32]), r=[xname], w=["xn"])
                P.pe(lambda e: e.transpose(psb7[0:64, 0:128], xn[:, 0:64], identb[:]), r=["xn", "identb"], w=["ps7"])
                P.act(lambda e: e.activation(kst[:, 0, 0:128], psb7[0:64, 0:128], AF.Copy), r=["ps7"], w=["kst"])
                P.dma("pool", SS_IK[l][:, b * 128:(b + 1) * 128], kst[:, 0, 0:128], r=["kst"], w=[shist])
            P.dve(lambda e: e.memset(tot[:], 0.0), w=["tot"])

            def cum_block(kb_, n):
                P.pe(lambda e: e.matmul(ps[5][0:n, 0:8], cstf[0:n, 2, 0:n], lfb[0:n, :], start=True, stop=False), r=["lfb", "cstf"], w=["ps5"])
                P.pe(lambda e: e.matmul(ps[5][0:n, 0:8], ones1[0:1, 0:n], tot[0:1, :], start=False, stop=True), r=["tot", "ones1"], w=["ps5"])
                P.dve(lambda e: e.tensor_copy(cstore[0:n, kb_, :], ps[5][0:n, 0:8]), r=["ps5"], w=["cstore"])
                P.pe(lambda e: e.matmul(ps[5][0:1, 8:16], cstf[0:n, 1, 128 - n:129 - n], cstore[0:n, kb_, :], start=True, stop=True), r=["cstore", "cstf"], w=["ps5"])
                P.dve(lambda e: e.tensor_copy(tot[:], ps[5][0:1, 8:16]), r=["ps5"], w=["tot"])

            for b in range(PAST // 128):
                P.dma("sp", lfb[:], ca_lf[l, b * 128:(b + 1) * 128, :], w=["lfb"])
                cum_block(b, 128)
            norm_block(l, (xs if l == 0 else hs1)[:, :], 0, nrow=NS, rname=("hs1" if l > 0 else None))
            for uname in TM_UNITS:
                W, wn = load_w(l, UNITS.index(uname))
                si = nxt("s", 4)
                for c in range(8):
                    P.pe(lambda e: e.matmul(ps[si][0:NS, :], hT[:, c, 0:NS], W[:, c, :], start=(c == 0), stop=(c == 7)), r=[wn, "hT"], w=[psn[si]])
                k = nxt("st", 2)
                S_ = st[k]; sn = f"st{k}"
                P.act(lambda e: e.activation(S_[0:NS, :], ps[si][0:NS, :], AF.Copy), r=[psn[si]], w=[sn])
                V_, vn = vst[k], f"vst{k}"
                if uname in ("tka", "tva", "tkc", "tvc"):
                    dst = {"tka": s_ak, "tva": s_av, "tkc": s_ck, "tvc": s_cv}[uname]
                    P.dma("pool", dst[l], S_[0:NS, :], r=[sn])
                    if uname in ("tva", "tvc"):
                        P.dve(lambda e: e.tensor_copy(V_[0:NS, :], S_[0:NS, :]), r=[sn], w=[vn])
                        vd = SS_VA[l][PAST:PAST + NS, :] if uname == "tva" else SS_VC[l][512:512 + NS, :]
                        P.dma("pool", vd, V_[0:NS, :], r=[vn], w=[shist])
                else:
                    P.dma("pool", s_bk[l], S_[0:NS, 0:128], r=[sn])
                    P.dma("pool", s_bv[l], S_[0:NS, 128:256], r=[sn])
                    P.dma("pool", s_ik[l], S_[0:NS, 256:288], r=[sn])
                    P.dve(lambda e: e.tensor_copy(V_[0:NS, 0:128], S_[0:NS, 128:256]), r=[sn], w=[vn])
                    P.dma("pool", SS_VB[l][PAST:PAST + NS, :], V_[0:NS, 0:128], r=[vn], w=[shist])
                    P.dve(lambda e: e.tensor_tensor(lfb[0:NS, :], S_[0:NS, 288:296], bfb[0:NS, l * 8:(l + 1) * 8], ALU.add), r=[sn, "bfb"], w=["lfb"])
                    P.act(lambda e: e.activation(lfb[0:NS, :], lfb[0:NS, :], AF.Exp, scale=-1.0), r=["lfb"], w=["lfb"])
                    P.act(lambda e: e.activation(lfb[0:NS, :], lfb[0:NS, :], AF.Ln, bias=1.0), r=["lfb"], w=["lfb"])
                    P.dve(lambda e: e.tensor_scalar(lfb[0:NS, :], lfb[0:NS, :], -1.0, None, ALU.mult), r=["lfb"], w=["lfb"])
                    P.dma("pool", s_lf[l], lfb[0:NS, :], r=["lfb"])
                    cum_block(8, NS)
                    P.dve(lambda e: e.tensor_scalar(wsgn[0:NS, 0, :], S_[0:NS, 296:304], 0.0, 2.0, ALU.is_ge, ALU.mult), r=[sn], w=["wsgn"])
                    P.dve(lambda e: e.tensor_scalar(wsgn[0:NS, 0, :], wsgn[0:NS, 0, :], -1.0, None, ALU.add), r=["wsgn"], w=["wsgn"])
                    P.dve(lambda e: e.scalar_tensor_tensor(wabs[0:NS, 0, :], S_[0:NS, 296:304], IDXS, wsgn[0:NS, 0, :], ALU.mult, ALU.mult), r=[sn, "wsgn"], w=["wabs"])
            P.pe(lambda e: e.matmul(ps[5][:, 16:24], cstf[0:NS, 3, :], cstore[0:NS, 8, :], start=True, stop=True), r=["cstore", "cstf"], w=["ps5"])
            P.dve(lambda e: e.tensor_copy(cbc[:], ps[5][:, 16:24]), r=["ps5"], w=["cbc"])
            for h in range(8):
                P.dve(lambda e: e.tensor_scalar(nbias[:, 0:9, h], cstore[:, 0:9, h], cbc[:, h:h + 1], -1.0, ALU.subtract, ALU.mult), r=["cstore", "cbc"], w=["nbias"])
            P.dve(lambda e: e.tensor_tensor(rq[0:NS, 0, :], cstore[0:NS, 8, :], cbc[0:NS, :], ALU.subtract), r=["cstore", "cbc"], w=["rq"])
            P.pe(lambda e: e.transpose(ps[6][0:8, 0:NS], rq[0:NS, 0, :], cstf[0:NS, 0, 0:NS]), r=["rq", "cstf"], w=["ps6"])
            P.dve(lambda e: e.tensor_copy(rT[:, 0:NS], ps[6][0:8, 0:NS]), r=["ps6"], w=["rT"])

            def s_evac_k(dst3, koff):
                def f(j, pt, pn):
                    P.act(lambda e: e.activation(kst[:, j, 0:NS], pt[0:64, 0:NS], AF.Copy), r=[pn], w=["kst"])
                    if j == 7:
                        P.dma("pool", dst3[:, :, koff:koff + NS].rearrange("h d k -> d h k"), kst[:, :, 0:NS], r=["kst"], w=[shist])
                return f

            def s_evac_q(j, pt, pn):
                P.act(lambda e: e.activation(Q[0:64, j, 0:NS], pt[0:64, 0:NS], AF.Copy, scale=SCALE), r=[pn], w=["Q"])

            def s_evac_z(zt, zn):
                def f(j, pt, pn):
                    P.act(lambda e: e.activation(zt[:, j, 0:NS], pt[0:64, 0:NS], AF.Silu), r=[pn], w=[zn])
                return f

            fm_unit(l, "ka", NS, s_evac_k(SS_KA[l], PAST))
            fm_unit(l, "za", NS, s_evac_z(zg["a"], "zga"))
            fm_unit(l, "qa", NS, s_evac_q)
            for h in range(8):
                P.dma("sp", Q[64:65, h, 0:NS], rT[h:h + 1, 0:NS], r=["rT"], w=["Q"])
            for half in range(2):
                at = Attn()
                for sbk in range(3):
                    nk = 512 if sbk < 2 else NS
                    bi = load_kv(SS_KA[l], SS_VA[l], half * 4, 4, sbk * 512, nk, shist)
                    for kb in range((nk + 127) // 128):
                        n = min(128, nk - kb * 128)
                        adds = [(0, NS, identb[0:NS, 0:NS], ma0b[0:NS, 0:NS], ["identb", "ma0b"])] if sbk == 2 else []
                        for i in range(4):
                            hh = half * 4 + i
                            at.tile(dict(kT=kbuf[bi][0:65, i, kb * 128:kb * 128 + n], v=vbuf[bi][0:n, kb, i, 0:65], n=n, qlo=0, qhi=NS,
                                         qap=Q[0:65, hh, 0:NS], adds=adds, bias=nbias[0:n, 4 * sbk + kb, hh:hh + 1],
                                         names=[f"kbuf{bi}", f"vbuf{bi}"], O=ps[4 + i], oname=psn[4 + i],
                                         first=(sbk == 0 and kb == 0), last=(sbk == 2)))
                at.flush()
                for i in range(4):
                    finish_head(ps[4 + i], psn[4 + i], zg["a"], "zga", half * 4 + i, NS, slice(0, NS))
            fm_unit(l, "kc", NS, s_evac_k(SS_KC[l], 512))
            fm_unit(l, "zc", NS, s_evac_z(zg["c"], "zgc"))
            fm_unit(l, "qc", NS, s_evac_q)
            for half in range(2):
                at = Attn()
                for sbk in range(2):
                    nk = 512 if sbk < 1 else NS
                    bi = load_kv(SS_KC[l], SS_VC[l], half * 4, 4, sbk * 512, nk, shist)
                    for kb in range((nk + 127) // 128):
                        n = min(128, nk - kb * 128)
                        for i in range(4):
                            hh = half * 4 + i
                            adds = []
                            if sbk == 0 and kb == 3:
                                adds = [(0, NS, identb[:], bc[:, 1, hh, 0:NS], ["identb", "btile"])]
                            if sbk == 1:
                                adds = [(0, NS, identb[0:NS, 0:NS], bc[0:NS, 0, hh, 0:NS], ["identb", "btile"])]
                            at.tile(dict(kT=kbuf[bi][0:64, i, kb * 128:kb * 128 + n], v=vbuf[bi][0:n, kb, i, 0:65], n=n, qlo=0, qhi=NS,
                                         qap=Q[0:64, hh, 0:NS], adds=adds, bias=None,
                                         names=[f"kbuf{bi}", f"vbuf{bi}"], O=ps[4 + i], oname=psn[4 + i],
                                         first=(sbk == 0 and kb == 0), last=(sbk == 1)))
                at.flush()
                for i in range(4):
                    finish_head(ps[4 + i], psn[4 + i], zg["c"], "zgc", half * 4 + i, NS, slice(0, NS))
            def s_evac_bx(j, pt, pn):
                if j < 2:
                    P.act(lambda e: e.activation(kst[:, j, 0:NS], pt[0:64, 0:NS], AF.Copy), r=[pn], w=["kst"])
                    if j == 1:
                        P.dma("pool", SS_KB[l][:, :, PAST:PAST + NS].rearrange("h d k -> d h k"), kst[:, 0:2, 0:NS], r=["kst"], w=[shist])
                elif j < 6:
                    P.act(lambda e: e.activation(iqT[:, j - 2, 0:NS], pt[0:64, 0:NS], AF.Copy), r=[pn], w=["iqT"])
                elif j == 6:
                    P.act(lambda e: e.activation(kst[:, 2, 0:NS], pt[0:64, 0:NS], AF.Copy), r=[pn], w=["kst"])
                    P.dma("pool", SS_IK[l][:, PAST:PAST + NS], kst[:, 2, 0:NS], r=["kst"], w=[shist])
            fm_unit(l, "bx", NS, s_evac_bx)
            fm_unit(l, "zb", NS, s_evac_z(zg["b"], "zgb"))
            fm_unit(l, "qb", NS, s_evac_q)
            NK = PAST + NS
            for k0 in range(0, NK, 512):
                nk = min(512, NK - k0)
                ii = nxt("ik", 2)
                P.dma("sp", ikbuf[ii][:, 0:nk], SS_IK[l][:, k0:k0 + nk], r=[shist], w=[f"ikbuf{ii}"])
                for h in range(8):
                    base = 32 * (h % 2)
                    pi_ = 2 + nxt("ips", 2)
                    P.pe(lambda e: e.matmul(ps[pi_][0:NS, 0:nk], iqT[base:base + 32, h // 2, 0:NS], ikbuf[ii][base:base + 32, 0:nk], start=True, stop=True),
                         r=["iqT", f"ikbuf{ii}"], w=[psn[pi_]])
                    ri = nxt("rr", 2)
                    P.act(lambda e: e.activation(Rr[ri][0:NS, 0:nk], ps[pi_][0:NS, 0:nk], AF.Relu, scale=wabs[0:NS, 0, h:h + 1]), r=[psn[pi_], "wabs"], w=[f"Rr{ri}"])
                    if h == 0:
                        P.dve(lambda e: e.tensor_scalar(SC[0:NS, k0:k0 + nk], Rr[ri][0:NS, 0:nk], wsgn[0:NS, 0, 0:1], None, ALU.mult), r=[f"Rr{ri}", "wsgn"], w=["SC"])
                    else:
                        P.dve(lambda e: e.scalar_tensor_tensor(SC[0:NS, k0:k0 + nk], Rr[ri][0:NS, 0:nk], wsgn[0:NS, 0, h:h + 1], SC[0:NS, k0:k0 + nk], ALU.mult, ALU.add),
                              r=[f"Rr{ri}", "wsgn", "SC"], w=["SC"])
            P.dve(lambda e: e.memset(cntb[:], 0.0), w=["cntb"])
            P.dve(lambda e: e.memset(small[:, 8:9], 0.0), w=["cand"])
            for it in range(NBIS):
                stp = 64.0 * (0.5 ** it)
                P.dve(lambda e: e.tensor_scalar(junk[0:NS, 0:NK], SC[0:NS, 0:NK], small[0:NS, 8:9], 0.0, ALU.is_ge, ALU.add, accum_out=cntb[0:NS, it:it + 1]),
                      r=["SC", "cand", "cntb"], w=["junk", "cntb"])
                a, b_ = (stp, -0.5 * stp) if it < NBIS - 1 else (stp, -stp)
                P.dve(lambda e: e.tensor_scalar(small[0:NS, 9:10], cntb[0:NS, it:it + 1], float(TOPK), a, ALU.is_ge, ALU.mult), r=["cntb"], w=["fl"])
                P.dve(lambda e: e.scalar_tensor_tensor(small[0:NS, 8:9], small[0:NS, 9:10], b_, small[0:NS, 8:9], ALU.add, ALU.add), r=["fl", "cand"], w=["cand"])
            at = Attn()
            for sbk in range(3):
                k0 = sbk * 512
                nk = min(512, NK - k0)
                mi = nxt("mb", 2)
                P.dve(lambda e: e.tensor_scalar(Mb[mi][0:NS, 0:nk], SC[0:NS, k0:k0 + nk], small[0:NS, 8:9], NEG, ALU.is_lt, ALU.mult), r=["SC", "cand"], w=[f"Mb{mi}"])
                bi = load_kv(SS_KB[l], SS_VB[l], 0, 2, k0, nk, shist)
                for kb in range((nk + 127) // 128):
                    n = min(128, nk - kb * 128)
                    gkb = sbk * 4 + kb
                    for j in range(2):
                        adds = [(0, 4 * NS, Mb[mi][0:NS, kb * 128:kb * 128 + n], i4b[0:NS, :, 0:NS], [f"Mb{mi}", "i4b"])]
                        if gkb >= 7:
                            for hq in range(4):
                                if gkb == 7:
                                    adds.append((hq * NS, (hq + 1) * NS, identb[:], b5[:, 1, 4 * j + hq, 0:NS], ["identb", "btile"]))
                                else:
                                    adds.append((hq * NS, (hq + 1) * NS, identb[0:NS, 0:NS], b5[0:NS, 0, 4 * j + hq, 0:NS], ["identb", "btile"]))
                        at.tile(dict(kT=kbuf[bi][0:64, j, kb * 128:kb * 128 + n], v=vbuf[bi][0:n, kb, j, 0:65], n=n, qlo=0, qhi=4 * NS,
                                     qap=Q[0:64, 4 * j:4 * j + 4, 0:NS], adds=adds, bias=None,
                                     names=[f"kbuf{bi}", f"vbuf{bi}"], O=ps[4 + j], oname=psn[4 + j],
                                     first=(gkb == 0), last=(gkb == 8)))
            at.flush()
            for j in range(2):
                finish_head(ps[4 + j], psn[4 + j], zg["b"], "zgb", (4 * j, 4 * j + 4), 4 * NS, slice(0, NS))
            allz = [zg["a"], zg["b"], zg["c"]]
            alln = ["zga", "zgb", "zgc"]
            for u in range(6):
                Wo_, won = load_wo(l, u)
                for hq in range(4):
                    hidx = u * 4 + hq
                    zt, zn = allz[hidx // 8], alln[hidx // 8]
                    for n_ in range(2):
                        P.pe(lambda e: e.matmul(ps[n_][0:NS, :], zt[:, hidx % 8, 0:NS], Wo_[:, hq, n_ * 512:(n_ + 1) * 512], start=(hidx == 0), stop=(hidx == 23)),
                             r=[zn, won], w=[psn[n_]])
            xi = nxt("x", 2)
            X = xt[xi]; xname = f"xt{xi}"
            P.dma("sp", X[0:NS, :], (xs if l == 0 else hs1)[:, :], r=(["hs1"] if l > 0 else []), w=[xname])
            for n_ in range(2):
                P.dve(lambda e: e.tensor_tensor(X[0:NS, n_ * 512:(n_ + 1) * 512], X[0:NS, n_ * 512:(n_ + 1) * 512], ps[n_][0:NS, :], ALU.add), r=[xname, psn[n_]], w=[xname])
            if l < nlayers - 1:
                P.dma("pool", hs1[:, :], X[0:NS, :], r=[xname], w=["hs1"])
            else:
                P.act(lambda e: e.activation(junk[0:NS, 0:1024], X[0:NS, :], AF.Square, accum_out=small[0:NS, 16:17]), r=[xname], w=["junk", "fs0"])
                P.dve(lambda e: e.tensor_scalar(small[0:NS, 17:18], small[0:NS, 16:17], 1.0 / D_MODEL, EPS, ALU.mult, ALU.add), r=["fs0"], w=["fs1"])
                P.act(lambda e: e.activation(small[0:NS, 18:19], small[0:NS, 17:18], AF.Sqrt), r=["fs1"], w=["fs2"])
                P.dve(lambda e: e.reciprocal(small[0:NS, 19:20], small[0:NS, 18:19]), r=["fs2"], w=["fs3"])
                P.dve(lambda e: e.scalar_tensor_tensor(X[0:NS, :], X[0:NS, :], small[0:NS, 19:20], fgb[0:NS, :], ALU.mult, ALU.mult), r=[xname, "fs3", "fgb"], w=[xname])
                P.dma("pool", y_s[:, :], X[0:NS, :], r=[xname])
    P.finalize_and_emit()
    return nc, es, P


def _t5_bucket_np(rel):
    nb = 16
    max_exact = 8
    ret = np.where(rel > 0, nb, 0)
    n = np.abs(rel)
    nf = np.maximum(n, 1).astype(np.float32)
    large = max_exact + (np.log(nf / max_exact) / math.log(128 / max_exact) * (nb - max_exact)).astype(np.int32)
    large = np.minimum(large, nb - 1)
    return ret + np.where(n < max_exact, n, large)


def _constants():
    p = np.arange(128)[:, None]
    f = np.arange(128)[None, :]
    cst = np.zeros((128, 8, 128), np.float32)
    cst[:, 0] = (p == f)
    cst[:, 1] = (p + f == 127)
    cst[:, 2] = (p <= f)
    cst[0, 3, :] = 1.0
    cst[:, 4] = np.where(p > f, NEG, 0.0)
    cst[:, 5] = np.where((p >= 64) & (f < 64), NEG, 0.0)
    cst[:, 6] = np.where((p < 64) & (f >= 64), NEG, 0.0)
    cst[:, 7] = np.where((p < 64) & (f >= 64), -1e30, 0.0)
    rel = 127 - np.arange(LTAB)
    bk = _t5_bucket_np(rel.astype(np.int32))
    oh5 = np.zeros((32, LTAB), np.float32)
    oh5[bk, np.arange(LTAB)] += 1.0
    far = int(_t5_bucket_np(np.array([-100000], np.int32))[0])
    oh5[far, :] -= 1.0
    idx = np.clip(rel, -128, 128) + 128
    ohc = np.zeros((3 * 128, LTAB), np.float32)
    ohc[idx, np.arange(LTAB)] += 1.0
    ohc[0, :] -= 1.0
    return cst.reshape(128, 8 * 128), oh5, ohc.reshape(3, 128, LTAB)


_PROG = {}


def _get_prog(key=(True, 4, DEPTH)):
    if key not in _PROG:
        _PROG[key] = build_program(*key)
    return _PROG[key]


def _host_inputs(x_prompt, norm_g, w_in, b_f, t5_bias, c_rel_bias, w_out, final_g):
    cst, oh5, ohc = _constants()
    wus = np.zeros((DEPTH, NU, 128, 8, 512), np.float32)
    for l in range(DEPTH):
        for u, name in enumerate(UNITS):
            cols = np.array(_unit_cols(name))
            m = cols >= 0
            w = np.zeros((D_MODEL, 512), np.float32)
            w[:, m] = w_in[l][:, cols[m]]
            wus[l, u] = w.reshape(8, 128, 512).transpose(1, 0, 2)
    wos = np.ascontiguousarray(w_out.reshape(DEPTH, 6, 4, 64, D_MODEL).transpose(0, 1, 3, 2, 4))
    gcol = np.ascontiguousarray(norm_g.reshape(DEPTH, 8, 128).transpose(2, 0, 1).reshape(128, DEPTH * 8))
    crel = np.zeros((DEPTH, 384, 8), np.float32)
    crel[:, :257] = c_rel_bias
    common = dict(wu=wus, wo=wos, gcol=gcol, fg=np.ascontiguousarray(final_g.reshape(1, D_MODEL)),
                  bfb=np.ascontiguousarray(b_f.reshape(1, DEPTH * 8)), t5=np.ascontiguousarray(t5_bias),
                  crel=crel.reshape(DEPTH, 3, 128, 8), oh5=oh5, ohc=ohc, cst=cst)
    return common


def _percore(j):
    pc = np.zeros((128, 1024), np.float32)
    p = np.arange(128)
    pc[:, 0:512] = (j * 512 + np.arange(512))[None, :]
    for rr in range(16):
        pc[:, 512 + rr] = rr * 128 + p
    for qb in range(4):
        for ch in range(4):
            pc[:, 528 + qb * 4 + ch] = (8 * j + 2 * qb + (p >= 64) + 1) * 64 - ch * 512
        for rr in range(17):
            pc[:, 544 + (qb * 17 + rr) * 2] = 1.0 if (rr - 1) == 4 * j + qb else 0.0
            pc[:, 544 + (qb * 17 + rr) * 2 + 1] = 1.0 if (rr - 1) == 4 * j + qb - 1 else 0.0
    for r in range(4):
        pc[:, 680 + r] = 1.0 if j == r else 0.0
    pc[:, 684] = -30000.0 if j == 0 else 0.0
    return pc


def _in_maps(x_prompt, x_sample, cache_a_k, cache_a_v, cache_a_logf, cache_b_k, cache_b_v, cache_b_idx_k,
             cache_c_k, cache_c_v, norm_g, w_in, b_f, t5_bias, c_rel_bias, w_out, final_g):
    f = lambda a: np.ascontiguousarray(np.asarray(a, dtype=np.float32))
    x_prompt, norm_g, w_in, b_f, t5_bias, c_rel_bias, w_out, final_g = map(f, (x_prompt, norm_g, w_in, b_f, t5_bias, c_rel_bias, w_out, final_g))
    common = _host_inputs(x_prompt, norm_g, w_in, b_f, t5_bias, c_rel_bias, w_out, final_g)
    common["iota5"] = np.ascontiguousarray(np.broadcast_to(np.arange(512, dtype=np.float32)[None, :], (128, 512)))
    in_maps = []
    for c in range(8):
        b, j = c // 4, c % 4
        m = dict(common)
        xb = x_prompt[b].reshape(NG, 512, D_MODEL)
        m["xp"] = x_prompt[b]
        m["xq"] = np.ascontiguousarray(xb[j::4])
        xpv = np.zeros((4, 512, D_MODEL), np.float32)
        for mm in range(4):
            if 4 * mm + j - 1 >= 0:
                xpv[mm] = xb[4 * mm + j - 1]
        m["xprev"] = xpv
        m["pcore"] = _percore(j)
        m["xs"] = f(x_sample[c])
        m["ca_k"] = f(cache_a_k[:, c]).reshape(DEPTH, PAST, 512); m["ca_v"] = f(cache_a_v[:, c]).reshape(DEPTH, PAST, 512)
        m["ca_lf"] = f(cache_a_logf[:, c]).reshape(DEPTH, PAST, 8)
        m["cb_k"] = f(cache_b_k[:, c]).reshape(DEPTH, PAST, 128); m["cb_v"] = f(cache_b_v[:, c]).reshape(DEPTH, PAST, 128)
        m["cb_ik"] = f(cache_b_idx_k[:, c]).reshape(DEPTH, PAST, 32)
        m["cc_k"] = f(cache_c_k[:, c]).reshape(DEPTH, 512, 512); m["cc_v"] = f(cache_c_v[:, c]).reshape(DEPTH, 512, 512)
        in_maps.append(m)
    return in_maps


def kernel(x_prompt, x_sample, cache_a_k, cache_a_v, cache_a_logf, cache_b_k, cache_b_v, cache_b_idx_k,
           cache_c_k, cache_c_v, norm_g, w_in, b_f, t5_bias, c_rel_bias, w_out, final_g):
    nc, es, P = _get_prog()
    in_maps = _in_maps(x_prompt, x_sample, cache_a_k, cache_a_v, cache_a_logf, cache_b_k, cache_b_v, cache_b_idx_k,
                       cache_c_k, cache_c_v, norm_g, w_in, b_f, t5_bias, c_rel_bias, w_out, final_g)
    res = run_bass_kernel_spmd(nc, in_maps, core_ids=list(range(8)))
    R = res.results
    st = lambda name, shp: np.stack([R[4 * b][name] for b in range(2)], axis=1).reshape(shp)
    y_prompt = np.zeros((BATCH, NG, 512, D_MODEL), np.float32)
    for c in range(8):
        b, j = c // 4, c % 4
        y_prompt[b, j::4] = R[c]["y_q"].reshape(4, 512, D_MODEL)
    y_prompt = y_prompt.reshape(BATCH, SEQ, D_MODEL)
    ss = lambda name, shp: np.stack([R[b][name] for b in range(DEC_BATCH)], axis=1).reshape(shp)
    y_sample = np.stack([R[b]["y_s"] for b in range(DEC_BATCH)], axis=0)
    outs = [y_prompt, y_sample,
            st("o_ak", (DEPTH, BATCH, SEQ, H, HD)), st("o_av", (DEPTH, BATCH, SEQ, H, HD)), st("o_lf", (DEPTH, BATCH, SEQ, H)),
            st("o_bk", (DEPTH, BATCH, SEQ, KVB, HD)), st("o_bv", (DEPTH, BATCH, SEQ, KVB, HD)), st("o_ik", (DEPTH, BATCH, SEQ, IDX_D)),
            st("o_ck", (DEPTH, BATCH, 512, H, HD)), st("o_cv", (DEPTH, BATCH, 512, H, HD)),
            ss("s_ak", (DEPTH, DEC_BATCH, DEC_SEQ, H, HD)), ss("s_av", (DEPTH, DEC_BATCH, DEC_SEQ, H, HD)), ss("s_lf", (DEPTH, DEC_BATCH, DEC_SEQ, H)),
            ss("s_bk", (DEPTH, DEC_BATCH, DEC_SEQ, KVB, HD)), ss("s_bv", (DEPTH, DEC_BATCH, DEC_SEQ, KVB, HD)), ss("s_ik", (DEPTH, DEC_BATCH, DEC_SEQ, IDX_D)),
            ss("s_ck", (DEPTH, DEC_BATCH, DEC_SEQ, H, HD)), ss("s_cv", (DEPTH, DEC_BATCH, DEC_SEQ, H, HD))]
    return tuple(outs)
```

```python
import math
import types
import numpy as np
from contextlib import ExitStack
import concourse.bass as bass
import concourse.mybir as mybir
from concourse.bass_utils import run_bass_kernel_spmd

F32 = mybir.dt.float32
BF16 = mybir.dt.bfloat16
ALU = mybir.AluOpType
AF = mybir.ActivationFunctionType

D_MODEL = 1024; BATCH = 2; SEQ = 8192; DEPTH = 2; DEC_BATCH = 8; DEC_SEQ = 16; PAST = 1024
HD = 64; H = 8; KVB = 2; IDX_H = 8; IDX_D = 32; TOPK = 256
SCALE = HD ** -0.5
IDXS = (IDX_D ** -0.5) * (IDX_H ** -0.5)
EPS = 1e-6
NEG = -32768.0
NG = SEQ // 512
NBIS = 24
LTAB = 384

_SPLIT = (512, 512, 512, 512, 8, 512, 128, 128, 512, 256, 8, 32, 512, 512, 512, 512)
_OFF = np.concatenate([[0], np.cumsum(_SPLIT)])
(QA, KA, VA, ZA, FA, QB, KB, VB, ZB, IQ, IW, IK, QC, KC, VC, ZC) = [int(o) for o in _OFF[:-1]]
FM_UNITS = ["qa", "ka", "za", "qb", "zb", "bx", "qc", "kc", "zc"]
TM_UNITS = ["tka", "tva", "tb", "tkc", "tvc"]
UNITS = FM_UNITS + TM_UNITS
NU = len(UNITS)


def _unit_cols(name):
    r = lambda a, n: list(range(a, a + n))
    pad = lambda l: l + [-1] * (512 - len(l))
    if name == "qa": return r(QA, 512)
    if name == "ka": return r(KA, 512)
    if name == "za": return r(ZA, 512)
    if name == "qb": return r(QB, 512)
    if name == "zb": return r(ZB, 512)
    if name == "qc": return r(QC, 512)
    if name == "kc": return r(KC, 512)
    if name == "zc": return r(ZC, 512)
    if name == "bx": return pad(r(KB, 128) + r(IQ, 256) + r(IK, 32) + r(IK, 32))
    if name == "tka": return r(KA, 512)
    if name == "tva": return r(VA, 512)
    if name == "tkc": return r(KC, 512)
    if name == "tvc": return r(VC, 512)
    if name == "tb": return pad(r(KB, 128) + r(VB, 128) + r(IK, 32) + r(FA, 8) + r(IW, 8))
    raise KeyError(name)


class Prog:
    STREAMS = ("pe", "act", "dve", "pool", "sp")

    def __init__(self, nc):
        self.nc = nc
        self.ops = []
        self.ndma = {}

    @staticmethod
    def _freeze(fn):
        if fn.__closure__ is None:
            return fn
        cells = []
        for c in fn.__closure__:
            try:
                cells.append(types.CellType(c.cell_contents))
            except ValueError:
                cells.append(c)
        return types.FunctionType(fn.__code__, fn.__globals__, fn.__name__, fn.__defaults__, tuple(cells))

    NSUB = 16

    def add(self, stream, fn, r=(), w=(), dma=False, cc=False):
        fn = self._freeze(fn)
        if cc:
            track = "dma_cc"
        elif dma:
            k = self.ndma.get(stream, 0)
            self.ndma[stream] = k + 1
            track = f"dma_{stream}#{k % self.NSUB}"
        else:
            track = stream
        self.ops.append((stream, track, fn, tuple(r), tuple(w)))

    def pe(self, fn, r=(), w=()): self.add("pe", fn, r, w)
    def act(self, fn, r=(), w=()): self.add("act", fn, r, w)
    def dve(self, fn, r=(), w=()): self.add("dve", fn, r, w)
    def pool(self, fn, r=(), w=()): self.add("pool", fn, r, w)

    def dma(self, stream, out, in_, r=(), w=(), **kw):
        self.add(stream, lambda e: e.dma_start(out=out, in_=in_, **kw), r, w, dma=True)

    def finalize_and_emit(self):
        nc = self.nc
        ops = self.ops
        n = len(ops)
        writers = {}
        readers = {}
        prev_on = {}
        deps = [None] * n
        signal = [False] * n
        qof = lambda t: t.split("#")[0]
        for i, (stream, track, fn, R, W) in enumerate(ops):
            d = set()
            is_dma = track.startswith("dma_")
            if is_dma:
                j = prev_on.get(track)
                if j is not None:
                    d.add(j)
                prev_on[track] = i
            for res in R:
                for tj, j in writers.get(res, {}).items():
                    if tj != track or is_dma or track != "pe":
                        d.add(j)
            for res in W:
                lazy = res.startswith("~")
                for tj, j in writers.get(res, {}).items():
                    if lazy:
                        if qof(tj) != qof(track):
                            d.add(j)
                    elif tj != track or is_dma:
                        d.add(j)
                for tj, j in readers.get(res, {}).items():
                    if lazy:
                        if qof(tj) != qof(track):
                            d.add(j)
                    elif tj != track or is_dma:
                        d.add(j)
            for res in R:
                readers.setdefault(res, {})[track] = i
            for res in W:
                if res.startswith("~"):
                    writers.setdefault(res, {})[track] = i
                else:
                    writers[res] = {track: i}
                    readers[res] = {}
            d.discard(i)
            deps[i] = d
            for j in d:
                signal[j] = True
        tracks = sorted({o[1] for o in ops})
        cnt = {t: 0 for t in tracks}
        val = [0] * n
        for i, (stream, track, fn, R, W) in enumerate(ops):
            if track == "dma_cc":
                cnt[track] += 1
                val[i] = cnt[track]
            elif track.startswith("dma_"):
                cnt[track] += 16
                val[i] = cnt[track]
            elif signal[i]:
                cnt[track] += 1
                val[i] = cnt[track]
        known = {s: {t: 0 for t in tracks} for s in self.STREAMS}
        waits = [None] * n
        for i, (stream, track, fn, R, W) in enumerate(ops):
            need = {}
            for j in deps[i]:
                tj = ops[j][1]
                need[tj] = max(need.get(tj, 0), val[j])
            wl = []
            for tj, v in need.items():
                if v > known[stream][tj]:
                    wl.append((tj, v))
                    known[stream][tj] = v
            waits[i] = wl
        self.stats = dict(cnt)
        by_stream = {s: [] for s in self.STREAMS}
        for i, o in enumerate(ops):
            by_stream[o[0]].append(i)
        with ExitStack() as es:
            sems = {t: es.enter_context(nc.semaphore("s_" + t.replace("#", "_"))) for t in tracks}
            block = es.enter_context(nc.Block())

            def run(eng, stream):
                for i in by_stream[stream]:
                    _, track, fn, R, W = ops[i]
                    for tj, v in waits[i]:
                        eng.wait_ge(sems[tj], v)
                    inst = fn(eng)
                    if track == "dma_cc":
                        inst.then_inc(sems[track], 1)
                    elif track.startswith("dma_"):
                        inst.then_inc(sems[track], 16)
                    elif signal[i]:
                        inst.then_inc(sems[track], 1)
                if stream == "sp":
                    for t in tracks:
                        if t.startswith("dma_") and cnt[t] > known[stream][t]:
                            eng.wait_ge(sems[t], cnt[t])

            @block.tensor
            def _(e): run(e, "pe")

            @block.scalar
            def _(e): run(e, "act")

            @block.vector
            def _(e): run(e, "dve")

            @block.gpsimd
            def _(e): run(e, "pool")

            @block.sync
            def _(e): run(e, "sp")


def build_program(do_sample=True, nm=4, nlayers=DEPTH):
    nc = bass.Bass("TRN2", target_bir_lowering=False)
    es = ExitStack()
    din = lambda name, shape, dt=F32: nc.dram_tensor(name, list(shape), dt, kind="ExternalInput").ap()
    dout = lambda name, shape: nc.dram_tensor(name, list(shape), F32, kind="ExternalOutput").ap()
    dscr = lambda name, shape, dt=BF16: nc.dram_tensor(name, list(shape), dt, kind="Internal").ap()
    xp = din("xp", [SEQ, D_MODEL])
    wu = din("wu", [DEPTH, NU, 128, 8, 512])
    wo = din("wo", [DEPTH, 6, 64, 4, 1024])
    gcol_d = din("gcol", [128, DEPTH * 8])
    fg_d = din("fg", [1, D_MODEL])
    bf_d = din("bfb", [1, DEPTH * 8])
    t5_d = din("t5", [32, 8])
    crel_d = din("crel", [DEPTH, 3, 128, 8])
    oh5_d = din("oh5", [32, LTAB])
    ohc_d = din("ohc", [3, 128, LTAB])
    cst_d = din("cst", [128, 8 * 128])
    y_q = dout("y_q", [2048, D_MODEL])
    xq = din("xq", [4, 512, D_MODEL]); xprev = din("xprev", [4, 512, D_MODEL])
    pc_d = din("pcore", [128, 1024])
    iota_d = din("iota5", [128, 512])
    o_ak = dout("o_ak", [DEPTH, SEQ, 512]); o_av = dout("o_av", [DEPTH, SEQ, 512])
    o_lf = dout("o_lf", [DEPTH, SEQ, 8])
    o_bk = dout("o_bk", [DEPTH, SEQ, 128]); o_bv = dout("o_bv", [DEPTH, SEQ, 128])
    o_ik = dout("o_ik", [DEPTH, SEQ, 32])
    o_ck = dout("o_ck", [DEPTH, 512, 512]); o_cv = dout("o_cv", [DEPTH, 512, 512])
    wub = dscr("wub", [DEPTH, NU, 128, 8, 512])
    wob = dscr("wob", [DEPTH, 6, 64, 4, 1024])
    hp1q = dscr("hp1q", [2048, D_MODEL], F32)
    hpg = dscr("hpg", [SEQ, D_MODEL], F32)
    ccs = dscr("ccs", [256, D_MODEL], F32)
    COMBS = dscr("combs", [4, 17, 2, 128, 512])
    AMASK = dscr("amask", [16, 128, 512])
    ccd = dscr("ccd", [1024, D_MODEL], F32)
    S_KCL = dscr("scr_kcl", [DEPTH, 8, 64, 1024]); S_VCL = dscr("scr_vcl", [DEPTH, 1024, 512])
    tab5 = dscr("tab5", [8, LTAB], F32)
    tabc = dscr("tabc", [DEPTH, 8, LTAB], F32)
    S_KA = dscr("scr_s_ka", [DEPTH, 8, 64, SEQ]); S_KC = dscr("scr_s_kc", [DEPTH, 8, 64, SEQ])
    S_KB = dscr("scr_s_kb", [DEPTH, 2, 64, SEQ]); S_IK = dscr("scr_s_ik", [DEPTH, 64, SEQ])
    S_VA = dscr("scr_s_va", [DEPTH, SEQ, 512]); S_VC = dscr("scr_s_vc", [DEPTH, SEQ, 512])
    S_VB = dscr("scr_s_vb", [DEPTH, SEQ, 128])

    xs = din("xs", [DEC_SEQ, D_MODEL])
    ca_k = din("ca_k", [DEPTH, PAST, 512]); ca_v = din("ca_v", [DEPTH, PAST, 512]); ca_lf = din("ca_lf", [DEPTH, PAST, 8])
    cb_k = din("cb_k", [DEPTH, PAST, 128]); cb_v = din("cb_v", [DEPTH, PAST, 128]); cb_ik = din("cb_ik", [DEPTH, PAST, 32])
    cc_k = din("cc_k", [DEPTH, 512, 512]); cc_v = din("cc_v", [DEPTH, 512, 512])
    y_s = dout("y_s", [DEC_SEQ, D_MODEL])
    s_ak = dout("s_ak", [DEPTH, DEC_SEQ, 512]); s_av = dout("s_av", [DEPTH, DEC_SEQ, 512]); s_lf = dout("s_lf", [DEPTH, DEC_SEQ, 8])
    s_bk = dout("s_bk", [DEPTH, DEC_SEQ, 128]); s_bv = dout("s_bv", [DEPTH, DEC_SEQ, 128]); s_ik = dout("s_ik", [DEPTH, DEC_SEQ, 32])
    s_ck = dout("s_ck", [DEPTH, DEC_SEQ, 512]); s_cv = dout("s_cv", [DEPTH, DEC_SEQ, 512])
    hs1 = dscr("hs1", [DEC_SEQ, D_MODEL], F32)
    MBS = [dscr(f"mbs{i}", [128, SEQ]) for i in range(4)]
    SS_KA = dscr("ss_ka", [DEPTH, 8, 64, 1152]); SS_KC = dscr("ss_kc", [DEPTH, 8, 64, 640])
    SS_KB = dscr("ss_kb", [DEPTH, 2, 64, 1152]); SS_IK = dscr("ss_ik", [DEPTH, 64, 1152])
    SS_VA = dscr("ss_va", [DEPTH, 1152, 512]); SS_VC = dscr("ss_vc", [DEPTH, 640, 512]); SS_VB = dscr("ss_vb", [DEPTH, 1152, 128])

    sb = lambda name, shape, dt: es.enter_context(nc.sbuf_tensor(name, list(shape), dt))
    wring = [sb(f"wring{i}", [128, 8, 512], BF16) for i in range(2)]
    SC = sb("SC", [128, 8192], F32)
    junk = sb("junk", [128, 8192], BF16)
    hT = sb("hT", [128, 8, 512], BF16)
    Q = sb("Q", [65, 8, 512], BF16)
    zg = {t: sb("zg" + t, [64, 8, 512], BF16) for t in "abc"}
    iqT = sb("iqT", [64, 4, 512], BF16)
    kst = sb("kst", [64, 8, 512], BF16)
    xt = [sb(f"xt{i}", [128, 1024], F32) for i in range(2)]
    xn = sb("xn", [128, 1024], BF16)
    st = [sb(f"st{i}", [128, 512], F32) for i in range(2)]
    vst = [sb(f"vst{i}", [128, 512], BF16) for i in range(2)]
    kbuf = [sb(f"kbuf{i}", [65, 4, 512], BF16) for i in range(2)]
    vbuf = [sb(f"vbuf{i}", [128, 4, 4, 65], BF16) for i in range(2)]
    Pt = [sb(f"Pt{i}", [128, 512], BF16) for i in range(4)]
    Mb = [sb(f"Mb{i}", [128, 512], BF16) for i in range(2)]
    Rr = [sb(f"Rr{i}", [128, 512], F32) for i in range(3)]
    ikbuf = [sb(f"ikbuf{i}", [64, 512], BF16) for i in range(2)]
    b5 = sb("b5", [128, 2, 8, 128], BF16)
    bc = sb("bc", [128, 2, 8, 128], BF16)
    cstf = sb("cstf", [128, 8, 128], F32)
    identb = sb("identb", [128, 128], BF16)
    i4b = sb("i4b", [128, 4, 128], BF16)
    ma0b = sb("ma0b", [128, 128], BF16); cm0b = sb("cm0b", [128, 128], BF16); cm4b = sb("cm4b", [128, 128], BF16)
    cstore = sb("cstore", [128, 64, 8], F32)
    nbias = sb("nbias", [128, 64, 8], F32)
    gcol = sb("gcol_s", [128, DEPTH * 8], F32)
    fgb = sb("fgb", [128, D_MODEL], F32)
    bfb = sb("bfb_s", [128, DEPTH * 8], F32)
    small = sb("small", [128, 64], F32)
    cntb = sb("cntb", [128, NBIS], F32)
    wabs = sb("wabs", [128, 4, 8], F32); wsgn = sb("wsgn", [128, 4, 8], F32)
    lfb = sb("lfb", [128, 8], F32)
    tot = sb("tot", [1, 8], F32)
    tots = sb("tots", [1, 17, 8], F32)
    totbc = sb("totbc", [128, 8], F32)
    lf4 = sb("lf4", [128, 4, 8], F32)
    cown = sb("cown", [128, 4, 8], F32)
    xacc = sb("xacc", [128, D_MODEL], F32)
    pcore = sb("pcore_s", [128, 1024], F32)
    iota5 = sb("iota5_s", [128, 512], F32)
    comb = [sb(f"comb{i}", [128, 4, 128], BF16) for i in range(2)]
    ones1 = sb("ones1", [65, 128], F32)
    cbc = sb("cbc", [128, 8], F32)
    rq = sb("rq", [128, 4, 8], F32)
    rT = sb("rT", [8, 512], BF16)
    rden = sb("rden", [65, 512], F32)
    otmp = sb("otmp", [64, 512], F32)
    hank = sb("hank", [128, 128], F32)
    t5s = sb("t5s", [32, 8], F32); oh5s = sb("oh5s", [32, LTAB], F32)
    crs = sb("crs", [128, 3, 8], F32); ohcs = sb("ohcs", [128, 3, LTAB], F32)
    tabs = sb("tabs", [8, LTAB], F32)
    ps = [es.enter_context(nc.psum_tensor(f"ps{i}", [128, 512], F32)) for i in range(8)]
    psn = [f"ps{i}" for i in range(8)]

    P = Prog(nc)
    _early = {}

    def nxt_early(key, n):
        v = _early.get(key, 0)
        _early[key] = v + 1
        return v % n

    IDENT = cstf[:, 0, :]; JM = cstf[:, 1, :]; TRI = cstf[:, 2, :]; E0ROW = cstf[:, 3, :]
    ADM = cstf[:, 7, :]
    E127 = cstf[:, 1, 0:1]

    P.dma("sp", pcore[:], pc_d, w=["qrelb", "krel", "qlimc", "sel01", "selb", "pvb"])
    P.dma("sp", iota5[:], iota_d, w=["iota5"])
    qrelb = pcore[:, 0:512]; krel = pcore[:, 512:528]; qlimc = pcore[:, 528:544]; sel01 = pcore[:, 544:680]
    selb = pcore[:, 680:684]; pvb = pcore[:, 684:685]
    P.dma("sp", cstf[:].rearrange("p a b -> p (a b)"), cst_d, w=["cstf"])
    P.dma("sp", gcol[:], gcol_d, w=["gcol"])
    P.dma("sp", fgb[:], fg_d.to_broadcast([128, D_MODEL]) if hasattr(fg_d, "to_broadcast") else bass.AP(fg_d.tensor, 0, [[0, 128], [1, D_MODEL]]), w=["fgb"])
    P.dma("sp", bfb[:], bass.AP(bf_d.tensor, 0, [[0, 128], [1, DEPTH * 8]]), w=["bfb"])
    P.dma("sp", t5s[:], t5_d, w=["t5s"])
    P.dma("sp", oh5s[:], oh5_d, w=["oh5s"])
    P.dma("sp", ohcs[:], ohc_d.rearrange("c p l -> p c l"), w=["ohcs"])
    P.dve(lambda e: e.tensor_copy(identb[:], IDENT), r=["cstf"], w=["identb"])
    for k in range(4):
        P.dve(lambda e, k=k: e.tensor_copy(i4b[:, k, :], IDENT), r=["cstf"], w=["i4b"])
    P.dve(lambda e: e.tensor_copy(ma0b[:], cstf[:, 4, :]), r=["cstf"], w=["ma0b"])
    P.dve(lambda e: e.tensor_copy(cm0b[:], cstf[:, 5, :]), r=["cstf"], w=["cm0b"])
    P.dve(lambda e: e.tensor_copy(cm4b[:], cstf[:, 6, :]), r=["cstf"], w=["cm4b"])
    onesf = cstf[:, 4, :]
    P.dve(lambda e: e.memset(onesf, 1.0), r=["ma0b"], w=["cstf", "onesf"])
    P.dve(lambda e: e.memset(ones1[:], 1.0), w=["ones1"])
    for i in range(2):
        P.pool(lambda e, i=i: e.memset(kbuf[i][:], 1.0), w=[f"kbuf{i}"])
        P.pool(lambda e, i=i: e.memset(vbuf[i][:], 1.0), w=[f"vbuf{i}"])

    stg = SC[:, 0:4096].rearrange("p (c n) -> p c n", c=8)
    stgb = junk[:, 0:4096].rearrange("p (c n) -> p c n", c=8)
    for l in range(nlayers):
        for u in range(NU):
            P.dma("sp", stg, wu[l, u], w=["SC"])
            P.act(lambda e: e.activation(stgb, stg, AF.Copy), r=["SC"], w=["junk"])
            P.dma("sp", wub[l, u], stgb, r=["junk"], w=[f"wub{l}_{u}"])
        for u in range(6):
            so = SC[0:64, 0:4096].rearrange("p (c n) -> p c n", c=4)
            sob = junk[0:64, 0:4096].rearrange("p (c n) -> p c n", c=4)
            P.dma("sp", so, wo[l, u], w=["SC"])
            P.act(lambda e, so=so, sob=sob: e.activation(sob, so, AF.Copy), r=["SC"], w=["junk"])
            P.dma("sp", wob[l, u], sob, r=["junk"], w=[f"wob{l}_{u}"])

    def build_tab(lhs_list, rhs_list, dst, rnames):
        for i, (a, b) in enumerate(zip(lhs_list, rhs_list)):
            P.pe(lambda e, a=a, b=b, i=i: e.matmul(ps[0][0:8, 0:LTAB], a, b, start=(i == 0), stop=(i == len(lhs_list) - 1)),
                 r=rnames, w=["ps0"])
        P.dve(lambda e: e.tensor_copy(tabs[:], ps[0][0:8, 0:LTAB]), r=["ps0"], w=["tabs"])
        P.dma("sp", dst, tabs[:], r=["tabs"], w=["tabdram"])

    def build_toeplitz(tab_ap2d, dst_tile):
        for k in range(2):
            for h in range(8):
                b0 = 128 * k
                src = bass.AP(tab_ap2d.tensor, tab_ap2d.offset + h * LTAB + b0, [[1, 128], [1, 128]])
                P.dma("sp", hank[:], src, r=["tabdram"], w=["hank"])
                P.pe(lambda e: e.matmul(ps[1][:, 0:128], JM, hank[:], start=True, stop=True), r=["hank", "cstf"], w=["ps1"])
                P.dve(lambda e, k=k, h=h: e.tensor_copy(dst_tile[:, k, h, :], ps[1][:, 0:128]), r=["ps1"], w=["btile"])

    build_tab([t5s[:]], [oh5s[:]], tab5, ["t5s", "oh5s"])
    build_toeplitz(tab5, b5)
    for qb in range(4):
        for rr in range(17):
            for jj in range(2):
                ci = nxt_early("cmb", 2)
                sc0 = sel01[:, (qb * 17 + rr) * 2:(qb * 17 + rr) * 2 + 1]
                sc1 = sel01[:, (qb * 17 + rr) * 2 + 1:(qb * 17 + rr) * 2 + 2]
                P.dve(lambda e: e.tensor_scalar(comb[ci][:, :, :], b5[:, 0, 4 * jj:4 * jj + 4, :], sc0, None, ALU.mult), r=["btile", "sel01"], w=[f"comb{ci}"])
                P.dve(lambda e: e.scalar_tensor_tensor(comb[ci][:, :, :], b5[:, 1, 4 * jj:4 * jj + 4, :], sc1, comb[ci][:, :, :], ALU.mult, ALU.add),
                      r=["btile", "sel01", f"comb{ci}"], w=[f"comb{ci}"])
                P.dma("pool", COMBS[qb, rr, jj], comb[ci][:].rearrange("p a b -> p (a b)"), r=[f"comb{ci}"], w=["~combs"])
    for rr in range(16):
        mi = nxt_early("mb", 2)
        P.dve(lambda e: e.tensor_scalar(Mb[mi][:, :], qrelb[:, :], krel[:, rr:rr + 1], NEG, ALU.is_lt, ALU.mult), r=["qrelb", "krel"], w=[f"Mb{mi}"])
        P.dma("pool", AMASK[rr], Mb[mi][:, :], r=[f"Mb{mi}"], w=["~amask"])

    wk = [0]

    def load_w(l, u):
        i = wk[0] % 2
        wk[0] += 1
        P.dma("sp", wring[i][:], wub[l, u], r=[f"wub{l}_{u}"], w=[f"wring{i}"])
        return wring[i], f"wring{i}"

    def load_wo(l, u):
        i = wk[0] % 2
        wk[0] += 1
        dst = wring[i][0:64].rearrange("p c n -> p (c n)").rearrange("p (c n) -> p c n", c=4)
        P.dma("sp", dst, wob[l, u], r=[f"wob{l}_{u}"], w=[f"wring{i}"])
        return dst, f"wring{i}"

    rot = {"s": 0, "pt": 0, "kv": 0, "st": 0, "x": 0, "ik": 0, "rr": 0, "mb": 0, "ips": 0, "cmb": 0}

    def nxt(key, n):
        v = rot[key] % n
        rot[key] += 1
        return v

    def norm_block(l, xsrc_ap, tb, nrow=128, rname=None, sb_src=None):
        if sb_src is not None:
            X, xname = sb_src
        else:
            xi = nxt("x", 2)
            X = xt[xi]; xname = f"xt{xi}"
        if sb_src is None:
            P.dma("sp", X[0:nrow, :], xsrc_ap, r=(list(rname) if isinstance(rname, (list, tuple)) else ([rname] if rname else [])), w=[xname])
        P.act(lambda e: e.activation(junk[0:nrow, 0:1024], X[0:nrow, :], AF.Square, accum_out=small[0:nrow, 0:1]),
              r=[xname], w=["junk", "small0"])
        P.dve(lambda e: e.tensor_scalar(small[0:nrow, 1:2], small[0:nrow, 0:1], 1.0 / D_MODEL, EPS, ALU.mult, ALU.add), r=["small0"], w=["small1"])
        P.act(lambda e: e.activation(small[0:nrow, 2:3], small[0:nrow, 1:2], AF.Sqrt), r=["small1"], w=["small2"])
        P.dve(lambda e: e.reciprocal(small[0:nrow, 3:4], small[0:nrow, 2:3]), r=["small2"], w=["small3"])
        P.dve(lambda e: e.tensor_scalar(xn[0:nrow, :], X[0:nrow, :], small[0:nrow, 3:4], None, ALU.mult), r=[xname, "small3"], w=["xn"])
        psb = ps[7].bitcast(BF16)
        for c in range(8):
            P.pe(lambda e, c=c: e.transpose(psb[:, c * 128:c * 128 + nrow], xn[0:nrow, c * 128:(c + 1) * 128], identb[0:nrow, 0:nrow]),
                 r=["xn", "identb"], w=["ps7"])
        for c in range(8):
            P.dve(lambda e, c=c: e.tensor_scalar(hT[:, c, tb * 128:tb * 128 + nrow], psb[:, c * 128:c * 128 + nrow],
                                                 gcol[:, l * 8 + c:l * 8 + c + 1], None, ALU.mult),
                  r=["ps7", "gcol"], w=["hT"])

    def fm_unit(l, uname, ntok, evac, blocks=tuple(range(8)), wres=None):
        W, wn = wres if wres is not None else load_w(l, UNITS.index(uname))
        for j in blocks:
            si = nxt("s", 4)
            for c in range(8):
                P.pe(lambda e, j=j, c=c, si=si: e.matmul(ps[si][0:64, 0:ntok], W[:, c, j * 64:(j + 1) * 64], hT[:, c, 0:ntok],
                                                        start=(c == 0), stop=(c == 7)), r=[wn, "hT"], w=[psn[si]])
            evac(j, ps[si], psn[si])

    def finish_head(O, oname, zt, zname, hsel, ncol, csl):
        P.dve(lambda e: e.reciprocal(rden[64:65, 0:ncol], O[64:65, 0:ncol]), r=[oname], w=["rden"])
        bi_ = nxt("s", 4)
        P.pe(lambda e: e.matmul(ps[bi_][0:64, 0:ncol], ones1[64:65, 0:64], rden[64:65, 0:ncol], start=True, stop=True),
             r=["rden", "ones1"], w=[psn[bi_]])
        P.dve(lambda e: e.tensor_copy(otmp[:, 0:ncol], ps[bi_][0:64, 0:ncol]), r=[psn[bi_]], w=["otmp"])
        P.dve(lambda e: e.tensor_tensor(otmp[:, 0:ncol], O[0:64, 0:ncol], otmp[:, 0:ncol], ALU.mult), r=[oname, "otmp"], w=["otmp"])
        if isinstance(hsel, tuple):
            zv = zt[:, hsel[0]:hsel[1], csl]
            ov = otmp[:, 0:ncol].rearrange("p (h q) -> p h q", h=hsel[1] - hsel[0])
        else:
            zv = zt[:, hsel, csl]
            ov = otmp[:, 0:ncol]
        P.dve(lambda e: e.tensor_tensor(zv, zv, ov, ALU.mult), r=[zname, "otmp"], w=[zname])

    def load_kv(Ksrc, Vsrc, h0, nh, k0, nk, krows_name):
        i = nxt("kv", 2)
        P.dma("sp", kbuf[i][0:64, 0:nh, 0:nk], Ksrc[h0:h0 + nh, :, k0:k0 + nk].rearrange("h d k -> d h k"), r=[krows_name], w=[f"kbuf{i}"])
        nb = (nk + 127) // 128
        for b in range(nb):
            n = min(128, nk - b * 128)
            P.dma("sp", vbuf[i][0:n, b, 0:nh, 0:64],
                  Vsrc[k0 + b * 128:k0 + b * 128 + n, h0 * 64:(h0 + nh) * 64].rearrange("k (h d) -> k h d", h=nh),
                  r=[krows_name], w=[f"vbuf{i}"])
        return i

    class Attn:
        def __init__(self):
            self.pend = []

        def tile(self, t):
            si = nxt("s", 4)
            S = ps[si]; n = t["n"]; qlo, qhi = t["qlo"], t["qhi"]
            nadd = len(t["adds"])
            P.pe(lambda e: e.matmul(S[0:n, qlo:qhi], t["kT"], t["qap"], start=True, stop=(nadd == 0)),
                 r=t["names"] + ["Q"], w=[psn[si]])
            for ai, (clo, chi, la, ra, an) in enumerate(t["adds"]):
                P.pe(lambda e, clo=clo, chi=chi, la=la, ra=ra, ai=ai: e.matmul(S[0:n, clo:chi], la, ra, start=False, stop=(ai == nadd - 1)),
                     r=an, w=[psn[si]])
            pi = nxt("pt", 4)
            if t["bias"] is not None:
                P.act(lambda e: e.activation(Pt[pi][0:n, qlo:qhi], S[0:n, qlo:qhi], AF.Exp, bias=t["bias"]),
                      r=[psn[si], "nbias"], w=[f"Pt{pi}"])
            else:
                P.act(lambda e: e.activation(Pt[pi][0:n, qlo:qhi], S[0:n, qlo:qhi], AF.Exp), r=[psn[si]], w=[f"Pt{pi}"])
            t["pi"] = pi
            self.pend.append(t)
            if len(self.pend) > 2:
                self.pv(self.pend.pop(0))

        def pv(self, t):
            n = t["n"]; qlo, qhi = t["qlo"], t["qhi"]; pi = t["pi"]; O = t["O"]
            P.pe(lambda e: e.matmul(O[0:65, qlo:qhi], t["v"], Pt[pi][0:n, qlo:qhi], start=t["first"], stop=t["last"]),
                 r=[f"Pt{pi}"] + t["names"], w=[t["oname"]])

        def flush(self):
            while self.pend:
                self.pv(self.pend.pop(0))


    for l in range(nlayers):
        P.pool(lambda e: e.memset(crs[:], 0.0), w=["crs"])
        P.dma("sp", crs[:], crel_d[l].rearrange("c p h -> p c h"), w=["crs"])
        build_tab([crs[:, c, :] for c in range(3)], [ohcs[:, c, :] for c in range(3)], tabc[l], ["crs", "ohcs"])
        build_toeplitz(tabc[l], bc)
        P.dve(lambda e: e.memset(tot[:], 0.0), w=["tot"])
        P.dve(lambda e: e.memset(tots[:], 0.0), w=["tots"])
        P.dve(lambda e: e.memset(totbc[:], 0.0), w=["totbc"])
        KAl, KBl, IKl, VAl, VBl = S_KA[l], S_KB[l], S_IK[l], S_VA[l], S_VB[l]
        KCl, VCl = S_KCL[l], S_VCL[l]
        hist = f"~hist{l}"
        chist = f"~chist{l}"

        def grow(gp):
            if l == 0:
                return xp[gp * 512:(gp + 1) * 512, :]
            return hpg[gp * 512:(gp + 1) * 512, :]

        def tm_unit(uname, handler, wres=None):
            W, wn = wres if wres is not None else load_w(l, UNITS.index(uname))
            for tb in range(4):
                si = nxt("s", 4)
                for c in range(8):
                    P.pe(lambda e: e.matmul(ps[si][:, :], hT[:, c, tb * 128:(tb + 1) * 128], W[:, c, :], start=(c == 0), stop=(c == 7)),
                         r=[wn, "hT"], w=[psn[si]])
                k = nxt("st", 2)
                S_ = st[k]; sn = f"st{k}"
                P.act(lambda e: e.activation(S_[:], ps[si][:], AF.Copy), r=[psn[si]], w=[sn])
                handler(tb, S_, sn, k)

        def logf_of(S_, sn):
            P.dve(lambda e: e.tensor_tensor(lfb[:], S_[:, 288:296], bfb[:, l * 8:(l + 1) * 8], ALU.add), r=[sn, "bfb"], w=["lfb"])
            P.act(lambda e: e.activation(lfb[:], lfb[:], AF.Exp, scale=-1.0), r=["lfb"], w=["lfb"])
            P.act(lambda e: e.activation(lfb[:], lfb[:], AF.Ln, bias=1.0), r=["lfb"], w=["lfb"])
            P.dve(lambda e: e.tensor_scalar(lfb[:], lfb[:], -1.0, None, ALU.mult), r=["lfb"], w=["lfb"])

        def cum_into(dst_ap, dname):
            P.pe(lambda e: e.matmul(ps[5][:, 0:8], TRI, lfb[:], start=True, stop=False), r=["lfb", "cstf"], w=["ps5"])
            P.pe(lambda e: e.matmul(ps[5][:, 0:8], ones1[0:1, 0:128], tot[0:1, :], start=False, stop=True), r=["tot", "ones1"], w=["ps5"])
            P.dve(lambda e: e.tensor_copy(dst_ap, ps[5][:, 0:8]), r=["ps5"], w=[dname])
            P.pe(lambda e: e.matmul(ps[5][0:1, 8:16], E127, dst_ap, start=True, stop=True), r=[dname, "cstf"], w=["ps5"])
            P.dve(lambda e: e.tensor_copy(tot[:], ps[5][0:1, 8:16]), r=["ps5"], w=["tot"])

        def evac_k_to(dst3, k0, hname):
            def f(j, pt, pn):
                P.act(lambda e: e.activation(kst[:, j, :], pt[0:64, :], AF.Copy), r=[pn], w=["kst"])
                if j == 7:
                    P.dma("pool", dst3[:, :, k0:k0 + 512].rearrange("h d k -> d h k"), kst[:], r=["kst"], w=[hname])
            return f

        def evac_q(scale):
            def f(j, pt, pn):
                P.act(lambda e: e.activation(Q[0:64, j, :], pt[0:64, :], AF.Copy, scale=scale), r=[pn], w=["Q"])
            return f

        def evac_z(zt, zn):
            def f(j, pt, pn):
                P.act(lambda e: e.activation(zt[:, j, :], pt[0:64, :], AF.Silu), r=[pn], w=[zn])
            return f

        scb = SC.bitcast(BF16)
        kres = {}
        for ui, un in enumerate(("tka", "tva", "tb", "ka")):
            v = scb[:, ui * 4096:(ui + 1) * 4096].rearrange("p (c n) -> p c n", c=8)
            P.dma("sp", v, wub[l, UNITS.index(un)], r=[f"wub{l}_{UNITS.index(un)}"], w=["SC"])
            kres[un] = (v, "SC")
        v = junk[:, 4096:8192].rearrange("p (c n) -> p c n", c=8)
        P.dma("sp", v, wub[l, UNITS.index("bx")], r=[f"wub{l}_{UNITS.index('bx')}"], w=["junk", "junkW"])
        kres["bx"] = (v, "junkW")
        for gp in range(4 * nm):
            t0 = gp * 512
            src = grow(gp)
            for tb in range(4):
                norm_block(l, src[tb * 128:(tb + 1) * 128, :], tb, rname=("~hpgw" if l > 0 else None))

            def h_kv(uname):
                def f(tb, S_, sn, k):
                    r0 = t0 + tb * 128
                    dst = {"tka": o_ak, "tva": o_av, "tkc": o_ck, "tvc": o_cv}[uname]
                    if uname in ("tka", "tva"):
                        P.dma("pool", dst[l, r0:r0 + 128, :], S_[:], r=[sn])
                    else:
                        P.dma("pool", dst[l, r0 - (SEQ - 512):r0 - (SEQ - 512) + 128, :], S_[:], r=[sn])
                    if uname == "tva":
                        V_, vn = vst[k], f"vst{k}"
                        P.dve(lambda e: e.tensor_copy(V_[:], S_[:]), r=[sn], w=[vn])
                        P.dma("pool", VAl[r0:r0 + 128, :], V_[:], r=[vn], w=[hist])
                return f

            def h_tb(tb, S_, sn, k):
                r0 = t0 + tb * 128
                P.dma("pool", o_bk[l, r0:r0 + 128, :], S_[:, 0:128], r=[sn])
                P.dma("pool", o_bv[l, r0:r0 + 128, :], S_[:, 128:256], r=[sn])
                P.dma("pool", o_ik[l, r0:r0 + 128, :], S_[:, 256:288], r=[sn])
                V_, vn = vst[k], f"vst{k}"
                P.dve(lambda e: e.tensor_copy(V_[:, 0:128], S_[:, 128:256]), r=[sn], w=[vn])
                P.dma("pool", VBl[r0:r0 + 128, :], V_[:, 0:128], r=[vn], w=[hist])
                P.dve(lambda e: e.tensor_tensor(lf4[:, tb, :], S_[:, 288:296], bfb[:, l * 8:(l + 1) * 8], ALU.add), r=[sn, "bfb"], w=["lf4"])

            tm_unit("tka", h_kv("tka"), wres=kres["tka"])
            tm_unit("tva", h_kv("tva"), wres=kres["tva"])
            tm_unit("tb", h_tb, wres=kres["tb"])
            lf4f = lf4[:].rearrange("p b h -> p (b h)")
            P.act(lambda e: e.activation(lf4f, lf4f, AF.Exp, scale=-1.0), r=["lf4"], w=["lf4"])
            P.act(lambda e: e.activation(lf4f, lf4f, AF.Ln, bias=1.0), r=["lf4"], w=["lf4"])
            P.dve(lambda e: e.tensor_scalar(lf4f, lf4f, -1.0, None, ALU.mult), r=["lf4"], w=["lf4"])
            P.dma("pool", o_lf[l, t0:t0 + 512, :].rearrange("(b p) h -> p b h", p=128), lf4[:], r=["lf4"])
            if gp == NG - 1:
                tm_unit("tkc", h_kv("tkc"))
                tm_unit("tvc", h_kv("tvc"))
            fm_unit(l, "ka", 512, evac_k_to(KAl, t0, hist), wres=kres["ka"])
            for b_ in range(4):
                for b2 in range(b_ + 1):
                    P.pe(lambda e: e.matmul(ps[5][:, b_ * 8:(b_ + 1) * 8], (TRI if b2 == b_ else onesf[:, :]), lf4[:, b2, :], start=(b2 == 0), stop=(b2 == b_)),
                         r=["lf4", "cstf", "onesf"], w=["ps5"])
            for b2 in range(4):
                P.pe(lambda e: e.matmul(ps[5][:, 32:40], onesf[:, :], lf4[:, b2, :], start=(b2 == 0), stop=(b2 == 3)), r=["lf4", "onesf"], w=["ps5"])
            for b_ in range(4):
                P.dve(lambda e: e.tensor_tensor(cstore[:, 4 * gp + b_, :], ps[5][:, b_ * 8:(b_ + 1) * 8], totbc[:, :], ALU.add), r=["ps5", "totbc"], w=["cstore"])
            P.dve(lambda e: e.tensor_tensor(totbc[:, :], ps[5][:, 32:40], totbc[:, :], ALU.add), r=["ps5", "totbc"], w=["totbc"])
            P.dve(lambda e: e.tensor_copy(tots[0:1, gp + 1, :], totbc[0:1, :]), r=["totbc"], w=["tots"])

            def evac_bx_k(j, pt, pn):
                if j < 2:
                    P.act(lambda e: e.activation(kst[:, j, :], pt[0:64, :], AF.Copy), r=[pn], w=["kst"])
                    if j == 1:
                        P.dma("pool", KBl[:, :, t0:t0 + 512].rearrange("h d k -> d h k"), kst[:, 0:2, :], r=["kst"], w=[hist])
                elif j == 6:
                    P.act(lambda e: e.activation(kst[:, 2, :], pt[0:64, :], AF.Copy), r=[pn], w=["kst"])
                    P.dma("pool", IKl[:, t0:t0 + 512], kst[:, 2, :], r=["kst"], w=[hist])
            fm_unit(l, "bx", 512, evac_bx_k, blocks=(0, 1, 6), wres=kres["bx"])

        for m in range(nm):
            own = (xq[m] if l == 0 else hp1q[m * 512:(m + 1) * 512, :])
            own_r = (None if l == 0 else [f"hp1qc{2 * m}", f"hp1qc{2 * m + 1}"])
            for part in range(2):
                for tb in range(4):
                    if part == 1:
                        norm_block(l, own[tb * 128:(tb + 1) * 128, :], tb, rname=own_r)
                    elif l == 0:
                        norm_block(l, xprev[m][tb * 128:(tb + 1) * 128, :], tb)
                    else:
                        first = True
                        for r in range(4):
                            gq = 4 * m - 1 + r
                            if gq < 0:
                                continue
                            xi = nxt("x", 2)
                            X = xt[xi]; xname = f"xt{xi}"
                            P.dma("sp", X[:], grow(gq)[tb * 128:(tb + 1) * 128, :], r=["~hpgw"], w=[xname])
                            if first:
                                P.dve(lambda e: e.tensor_scalar(xacc[:], X[:], selb[:, r:r + 1], None, ALU.mult), r=[xname, "selb"], w=["xacc"])
                            else:
                                P.dve(lambda e: e.scalar_tensor_tensor(xacc[:], X[:], selb[:, r:r + 1], xacc[:], ALU.mult, ALU.add), r=[xname, "selb", "xacc"], w=["xacc"])
                            first = False
                        norm_block(l, None, tb, sb_src=(xacc, "xacc"))

                def h_c(uname):
                    def f(tb, S_, sn, k):
                        if uname == "tvc":
                            V_, vn = vst[k], f"vst{k}"
                            P.dve(lambda e: e.tensor_copy(V_[:], S_[:]), r=[sn], w=[vn])
                            P.dma("pool", VCl[part * 512 + tb * 128:part * 512 + (tb + 1) * 128, :], V_[:], r=[vn], w=[chist])
                    return f
                tm_unit("tvc", h_c("tvc"))
                fm_unit(l, "kc", 512, evac_k_to(KCl, part * 512, chist))
            g0 = 16 * m
            nkb = 16 * m + 16
            for r in range(4):
                if r == 0:
                    P.dve(lambda e: e.tensor_scalar(tot[0:1, :], tots[0:1, 4 * m + r, :], selb[0:1, r:r + 1], None, ALU.mult), r=["tots", "selb"], w=["tot"])
                else:
                    P.dve(lambda e: e.scalar_tensor_tensor(tot[0:1, :], tots[0:1, 4 * m + r, :], selb[0:1, r:r + 1], tot[0:1, :], ALU.mult, ALU.add),
                          r=["tots", "selb", "tot"], w=["tot"])

            def h_own(tb, S_, sn, k):
                P.dve(lambda e: e.tensor_tensor(lf4[:, tb, :], S_[:, 288:296], bfb[:, l * 8:(l + 1) * 8], ALU.add), r=[sn, "bfb"], w=["lf4"])
                P.dve(lambda e: e.tensor_scalar(wsgn[:, tb, :], S_[:, 296:304], 0.0, 2.0, ALU.is_ge, ALU.mult), r=[sn], w=["wsgn"])
                P.dve(lambda e: e.tensor_scalar(wsgn[:, tb, :], wsgn[:, tb, :], -1.0, None, ALU.add), r=["wsgn"], w=["wsgn"])
                P.dve(lambda e: e.scalar_tensor_tensor(wabs[:, tb, :], S_[:, 296:304], IDXS, wsgn[:, tb, :], ALU.mult, ALU.mult), r=[sn, "wsgn"], w=["wabs"])
            tm_unit("tb", h_own)
            lf4q = lf4[:].rearrange("p b h -> p (b h)")
            P.act(lambda e: e.activation(lf4q, lf4q, AF.Exp, scale=-1.0), r=["lf4"], w=["lf4"])
            P.act(lambda e: e.activation(lf4q, lf4q, AF.Ln, bias=1.0), r=["lf4"], w=["lf4"])
            P.dve(lambda e: e.tensor_scalar(lf4q, lf4q, -1.0, None, ALU.mult), r=["lf4"], w=["lf4"])
            for b_ in range(4):
                P.pe(lambda e: e.matmul(ps[5][:, b_ * 8:(b_ + 1) * 8], ones1[0:1, 0:128], tot[0:1, :], start=True, stop=False), r=["tot", "ones1"], w=["ps5"])
                for b2 in range(b_ + 1):
                    P.pe(lambda e: e.matmul(ps[5][:, b_ * 8:(b_ + 1) * 8], (TRI if b2 == b_ else onesf[:, :]), lf4[:, b2, :], start=False, stop=(b2 == b_)),
                         r=["lf4", "cstf", "onesf"], w=["ps5"])
            P.dve(lambda e: e.tensor_copy(cown[:].rearrange("p b h -> p (b h)"), ps[5][:, 0:32]), r=["ps5"], w=["cown"])
            P.pe(lambda e: e.matmul(ps[5][:, 16:24], E0ROW, cstore[:, g0, :], start=True, stop=True), r=["cstore", "cstf"], w=["ps5"])
            P.dve(lambda e: e.tensor_copy(cbc[:], ps[5][:, 16:24]), r=["ps5"], w=["cbc"])
            for h in range(8):
                P.dve(lambda e: e.tensor_scalar(nbias[:, 0:nkb, h], cstore[:, 0:nkb, h], cbc[:, h:h + 1], -1.0, ALU.subtract, ALU.mult),
                      r=["cstore", "cbc"], w=["nbias"])
            for tb in range(4):
                P.dve(lambda e: e.tensor_tensor(rq[:, tb, :], cown[:, tb, :], cbc[:], ALU.subtract), r=["cown", "cbc"], w=["rq"])
                P.pe(lambda e: e.transpose(ps[6][0:8, tb * 128:(tb + 1) * 128], rq[:, tb, :], IDENT), r=["rq", "cstf"], w=["ps6"])
            P.dve(lambda e: e.tensor_copy(rT[:], ps[6][0:8, :]), r=["ps6"], w=["rT"])

            def evac_bx_q(j, pt, pn):
                P.act(lambda e: e.activation(iqT[:, j - 2, :], pt[0:64, :], AF.Copy), r=[pn], w=["iqT"])

            def a_half(half):
                at = Attn()
                for sbk in range(4 * m + 4):
                    bi = load_kv(KAl, VAl, half * 4, 4, sbk * 512, 512, hist)
                    trail = (sbk >= 4 * m)
                    for kb in range(4):
                        adds = []
                        if trail:
                            rr = (sbk - 4 * m) * 4 + kb
                            mi = nxt("mb", 2)
                            P.dma("sp", Mb[mi][:, :], AMASK[rr], r=["~amask"], w=[f"Mb{mi}"])
                            adds = [(0, 512, identb[:], Mb[mi][:, :], ["identb", f"Mb{mi}"])]
                        for i in range(4):
                            hh = half * 4 + i
                            at.tile(dict(kT=kbuf[bi][0:65, i, kb * 128:(kb + 1) * 128], v=vbuf[bi][:, kb, i, 0:65], n=128, qlo=0, qhi=512,
                                         qap=Q[0:65, hh, 0:512], adds=adds, bias=nbias[:, 4 * sbk + kb, hh:hh + 1],
                                         names=[f"kbuf{bi}", f"vbuf{bi}"], O=ps[4 + i], oname=psn[4 + i],
                                         first=(sbk == 0 and kb == 0), last=(sbk == 4 * m + 3 and kb == 3)))
                at.flush()
                for i in range(4):
                    finish_head(ps[4 + i], psn[4 + i], zg["a"], "zga", half * 4 + i, 512, slice(0, 512))

            def c_half(half):
                at = Attn()
                started = [False] * 4
                for sbl in range(2):
                    bi = load_kv(KCl, VCl, half * 4, 4, sbl * 512, 512, chist)
                    for kb in range(4):
                        r_ = 4 * sbl + kb
                        qb_lo, qb_hi = max(0, r_ - 4), min(3, r_)
                        qlo, qhi = qb_lo * 128, (qb_hi + 1) * 128
                        for i in range(4):
                            hh = half * 4 + i
                            adds = []
                            for qb in range(qb_lo, qb_hi + 1):
                                dl = r_ - 4 - qb
                                c0, c1 = qb * 128, (qb + 1) * 128
                                if dl == 0:
                                    adds.append((c0, c1, identb[:], bc[:, 0, hh, :], ["identb", "btile"]))
                                    adds.append((c0, c1, identb[:], cm0b[:], ["identb", "cm0b"]))
                                elif dl == -1:
                                    adds.append((c0, c1, identb[:], bc[:, 1, hh, :], ["identb", "btile"]))
                                elif dl == -4:
                                    adds.append((c0, c1, identb[:], cm4b[:], ["identb", "cm4b"]))
                            at.tile(dict(kT=kbuf[bi][0:64, i, kb * 128:(kb + 1) * 128], v=vbuf[bi][:, kb, i, 0:65], n=128, qlo=qlo, qhi=qhi,
                                         qap=Q[0:64, hh, qlo:qhi], adds=adds, bias=(pvb[:, 0:1] if (m == 0 and sbl == 0) else None),
                                         names=[f"kbuf{bi}", f"vbuf{bi}"], O=ps[4 + i], oname=psn[4 + i],
                                         first=(not started[i]), last=(sbl == 1 and kb == 3)))
                            started[i] = True
                at.flush()
                for i in range(4):
                    finish_head(ps[4 + i], psn[4 + i], zg["c"], "zgc", half * 4 + i, 512, slice(0, 512))

            def b_topk(qb):
                NK = (16 * m + 13 + qb) * 128
                qs = slice(qb * 128, (qb + 1) * 128)
                for k0 in range(0, NK, 512):
                    nk = min(512, NK - k0)
                    ii = nxt("ik", 2)
                    P.dma("sp", ikbuf[ii][:, 0:nk], IKl[:, k0:k0 + nk], r=[hist], w=[f"ikbuf{ii}"])
                    for h in range(8):
                        base = 32 * (h % 2)
                        pi_ = 1 + nxt("ips", 3)
                        P.pe(lambda e: e.matmul(ps[pi_][:, 0:nk], iqT[base:base + 32, h // 2, qs], ikbuf[ii][base:base + 32, 0:nk], start=True, stop=True),
                             r=["iqT", f"ikbuf{ii}"], w=[psn[pi_]])
                        ri = nxt("rr", 3)
                        P.act(lambda e: e.activation(Rr[ri][:, 0:nk], ps[pi_][:, 0:nk], AF.Relu, scale=wabs[:, qb, h:h + 1]), r=[psn[pi_], "wabs"], w=[f"Rr{ri}"])
                        if h == 0:
                            P.dve(lambda e: e.tensor_scalar(SC[:, k0:k0 + nk], Rr[ri][:, 0:nk], wsgn[:, qb, 0:1], None, ALU.mult), r=[f"Rr{ri}", "wsgn"], w=["SC"])
                        else:
                            P.dve(lambda e: e.scalar_tensor_tensor(SC[:, k0:k0 + nk], Rr[ri][:, 0:nk], wsgn[:, qb, h:h + 1], SC[:, k0:k0 + nk], ALU.mult, ALU.add),
                                  r=[f"Rr{ri}", "wsgn", "SC"], w=["SC"])
                for ch in range(4):
                    c0 = g0 * 128 + ch * 512
                    wd = min(512, NK - c0)
                    if wd <= 0:
                        continue
                    ri = nxt("rr", 3)
                    P.dve(lambda e: e.tensor_scalar(Rr[ri][:, 0:wd], iota5[:, 0:wd], qlimc[:, qb * 4 + ch:qb * 4 + ch + 1], -1e30, ALU.is_ge, ALU.mult),
                          r=["iota5", "qlimc"], w=[f"Rr{ri}"])
                    P.dve(lambda e: e.tensor_tensor(SC[:, c0:c0 + wd], SC[:, c0:c0 + wd], Rr[ri][:, 0:wd], ALU.add), r=["SC", f"Rr{ri}"], w=["SC"])
                P.dve(lambda e: e.memset(cntb[:], 0.0), w=["cntb"])
                P.dve(lambda e: e.memset(small[:, 8:9], 0.0), w=["cand"])
                for it in range(NBIS):
                    stp = 64.0 * (0.5 ** it)
                    P.dve(lambda e: e.tensor_scalar(junk[:, 0:NK], SC[:, 0:NK], small[:, 8:9], 0.0, ALU.is_ge, ALU.add, accum_out=cntb[:, it:it + 1]),
                          r=["SC", "cand", "cntb"], w=["junk", "cntb"])
                    a, b_ = (stp, -0.5 * stp) if it < NBIS - 1 else (stp, -stp)
                    P.dve(lambda e: e.tensor_scalar(small[:, 9:10], cntb[:, it:it + 1], float(TOPK), a, ALU.is_ge, ALU.mult), r=["cntb"], w=["fl"])
                    P.dve(lambda e: e.scalar_tensor_tensor(small[:, 8:9], small[:, 9:10], b_, small[:, 8:9], ALU.add, ALU.add), r=["fl", "cand"], w=["cand"])
                P.dve(lambda e: e.tensor_scalar(junk[:, 0:NK], SC[:, 0:NK], small[:, 8:9], NEG, ALU.is_lt, ALU.mult), r=["SC", "cand"], w=["junk"])
                P.dma("pool", MBS[qb][:, 0:NK], junk[:, 0:NK], r=["junk"], w=[f"mbs{qb}"])

            def b_attn(qb):
                qs = slice(qb * 128, (qb + 1) * 128)
                nkq = 16 * m + 13 + qb
                at = Attn()
                for sbk in range((nkq + 3) // 4):
                    k0 = sbk * 512
                    nk = min(512, nkq * 128 - k0)
                    mi = nxt("mb", 2)
                    P.dma("sp", Mb[mi][:, 0:nk], MBS[qb][:, k0:k0 + nk], r=[f"mbs{qb}"], w=[f"Mb{mi}"])
                    bi = load_kv(KBl, VBl, 0, 2, k0, nk, hist)
                    for kb in range(nk // 128):
                        gkb = sbk * 4 + kb
                        for jj in range(2):
                            adds = [(0, 512, Mb[mi][:, kb * 128:(kb + 1) * 128], i4b[:].rearrange("p a b -> p (a b)"), [f"Mb{mi}", "i4b"])]
                            if gkb >= g0 - 1:
                                rr = gkb - g0 + 1
                                ci = nxt("cmb", 2)
                                sc0 = sel01[:, (qb * 17 + rr) * 2:(qb * 17 + rr) * 2 + 1]
                                sc1 = sel01[:, (qb * 17 + rr) * 2 + 1:(qb * 17 + rr) * 2 + 2]
                                P.dma("sp", comb[ci][:].rearrange("p a b -> p (a b)"), COMBS[qb, rr, jj], r=["~combs"], w=[f"comb{ci}"])
                                adds.append((0, 512, identb[:], comb[ci][:].rearrange("p a b -> p (a b)"), ["identb", f"comb{ci}"]))
                            at.tile(dict(kT=kbuf[bi][0:64, jj, kb * 128:(kb + 1) * 128], v=vbuf[bi][:, kb, jj, 0:65], n=128, qlo=0, qhi=512,
                                         qap=Q[0:64, 4 * jj:4 * jj + 4, qs], adds=adds, bias=None,
                                         names=[f"kbuf{bi}", f"vbuf{bi}"], O=ps[4 + 2 * (qb % 2) + jj], oname=psn[4 + 2 * (qb % 2) + jj],
                                         first=(gkb == 0), last=(gkb == nkq - 1)))
                at.flush()
                for jj in range(2):
                    finish_head(ps[4 + 2 * (qb % 2) + jj], psn[4 + 2 * (qb % 2) + jj], zg["b"], "zgb", (4 * jj, 4 * jj + 4), 512, qs)

            fm_unit(l, "bx", 512, evac_bx_q, blocks=(2, 3, 4, 5))
            b_topk(0)
            fm_unit(l, "za", 512, evac_z(zg["a"], "zga"))
            fm_unit(l, "qa", 512, evac_q(SCALE))
            for h in range(8):
                P.dma("sp", Q[64:65, h, :], rT[h:h + 1, :], r=["rT"], w=["Q"])
            a_half(0)
            b_topk(1)
            a_half(1)
            fm_unit(l, "zc", 512, evac_z(zg["c"], "zgc"))
            fm_unit(l, "qc", 512, evac_q(SCALE))
            c_half(0)
            c_half(1)
            fm_unit(l, "zb", 512, evac_z(zg["b"], "zgb"))
            fm_unit(l, "qb", 512, evac_q(SCALE))
            b_attn(0)
            b_topk(2)
            b_attn(1)
            b_topk(3)
            b_attn(2)
            b_attn(3)

            allz = [zg["a"], zg["b"], zg["c"]]
            alln = ["zga", "zgb", "zgc"]
            for u in range(6):
                Wo_, won = load_wo(l, u)
                for hq in range(4):
                    hidx = u * 4 + hq
                    zt, zn = allz[hidx // 8], alln[hidx // 8]
                    for tb in range(4):
                        for n_ in range(2):
                            P.pe(lambda e: e.matmul(ps[tb * 2 + n_][:, :], zt[:, hidx % 8, tb * 128:(tb + 1) * 128], Wo_[:, hq, n_ * 512:(n_ + 1) * 512],
                                                    start=(hidx == 0), stop=(hidx == 23)), r=[zn, won], w=[psn[tb * 2 + n_]])
            for tb in range(4):
                xi = nxt("x", 2)
                X = xt[xi]; xname = f"xt{xi}"
                P.dma("sp", X[:], own[tb * 128:(tb + 1) * 128, :], r=(own_r if own_r else []), w=[xname])
                for n_ in range(2):
                    P.dve(lambda e: e.tensor_tensor(X[:, n_ * 512:(n_ + 1) * 512], X[:, n_ * 512:(n_ + 1) * 512], ps[tb * 2 + n_][:, :], ALU.add),
                          r=[xname, psn[tb * 2 + n_]], w=[xname])
                r0 = m * 512 + tb * 128
                if l < nlayers - 1:
                    P.dma("pool", hp1q[r0:r0 + 128, :], X[:], r=[xname], w=[f"hp1qc{r0 // 256}"])
                else:
                    P.act(lambda e: e.activation(junk[:, 0:1024], X[:], AF.Square, accum_out=small[:, 16:17]), r=[xname], w=["junk", "fs0"])
                    P.dve(lambda e: e.tensor_scalar(small[:, 17:18], small[:, 16:17], 1.0 / D_MODEL, EPS, ALU.mult, ALU.add), r=["fs0"], w=["fs1"])
                    P.act(lambda e: e.activation(small[:, 18:19], small[:, 17:18], AF.Sqrt), r=["fs1"], w=["fs2"])
                    P.dve(lambda e: e.reciprocal(small[:, 19:20], small[:, 18:19]), r=["fs2"], w=["fs3"])
                    P.dve(lambda e: e.scalar_tensor_tensor(X[:], X[:], small[:, 19:20], fgb[:], ALU.mult, ALU.mult), r=[xname, "fs3", "fgb"], w=[xname])
                    P.dma("pool", y_q[r0:r0 + 128, :], X[:], r=[xname])
            if l < nlayers - 1:
                for hf in range(2):
                    cidx = 2 * m + hf
                    P.dma("pool", ccs, hp1q[cidx * 256:(cidx + 1) * 256, :], r=[f"hp1qc{cidx}"], w=["ccs"])
                    P.add("pool", lambda e: e.collective_compute("AllGather", ALU.bypass, replica_groups=[[0, 1, 2, 3], [4, 5, 6, 7]],
                                                                 ins=[ccs.opt()], outs=[ccd.opt()]), r=["ccs"], w=["ccd"], cc=True)
                    for r in range(4):
                        a0 = (4 * m + r) * 512 + hf * 256
                        P.dma("pool", hpg[a0:a0 + 256, :], ccd[r * 256:(r + 1) * 256, :], r=["ccd"], w=["~hpgw"])
        if do_sample:
            NS = DEC_SEQ
            shist = f"~shist{l}"
            psb7 = ps[7].bitcast(BF16)

            def prep_cache(src2d, nrows, ncols, kdst, vdst, nheads):
                for b in range(nrows // 128):
                    xi = nxt("x", 2)
                    X = xt[xi]; xname = f"xt{xi}"
                    P.dma("sp", X[:, 0:ncols], src2d[b * 128:(b + 1) * 128, :], w=[xname])
                    P.dve(lambda e: e.tensor_copy(xn[:, 0:ncols], X[:, 0:ncols]), r=[xname], w=["xn"])
                    if vdst is not None:
                        P.dma("pool", vdst[b * 128:(b + 1) * 128, :], xn[:, 0:ncols], r=["xn"], w=[shist])
                    if kdst is not None:
                        for hh in range(nheads):
                            P.pe(lambda e: e.transpose(psb7[0:64, hh * 128:(hh + 1) * 128], xn[:, hh * 64:(hh + 1) * 64], identb[:]),
                                 r=["xn", "identb"], w=["ps7"])
                        P.act(lambda e: e.activation(kst[:, 0:nheads, 0:128], psb7[0:64, 0:nheads * 128].rearrange("p (h k) -> p h k", h=nheads), AF.Copy),
                              r=["ps7"], w=["kst"])
                        P.dma("pool", kdst[:, :, b * 128:(b + 1) * 128].rearrange("h d k -> d h k"), kst[:, 0:nheads, 0:128], r=["kst"], w=[shist])

            prep_cache(ca_k[l], PAST, 512, SS_KA[l], None, 8)
            prep_cache(ca_v[l], PAST, 512, None, SS_VA[l], 8)
            prep_cache(cb_k[l], PAST, 128, SS_KB[l], None, 2)
            prep_cache(cb_v[l], PAST, 128, None, SS_VB[l], 2)
            prep_cache(cc_k[l], 512, 512, SS_KC[l], None, 8)
            prep_cache(cc_v[l], 512, 512, None, SS_VC[l], 8)
            for b in range(PAST // 128):
                xi = nxt("x", 2)
                X = xt[xi]; xname = f"xt{xi}"
                P.dma("sp", X[:, 0:32], cb_ik[l, b * 128:(b + 1) * 128, :], w=[xname])
                P.dve(lambda e: e.tensor_copy(xn[:, 0:32], X[:, 0:32]), r=[xname], w=["xn"])
                P.dve(lambda e: e.tensor_copy(xn[:, 32:64], X[:, 0:32]), r=[xname], w=["xn"])
                P.pe(lambda e: e.transpose(psb7[0:64, 0:128], xn[:, 0:64], identb[:]), r=["xn", "identb"], w=["ps7"])
                P.act(lambda e: e.activation(kst[:, 0, 0:128], psb7[0:64, 0:128], AF.Copy), r=["ps7"], w=["kst"])
                P.dma("pool", SS_IK[l][:, b * 128:(b + 1) * 128], kst[:, 0, 0:128], r=["kst"], w=[shist])
            P.dve(lambda e: e.memset(tot[:], 0.0), w=["tot"])

            def cum_block(kb_, n):
                P.pe(lambda e: e.matmul(ps[5][0:n, 0:8], cstf[0:n, 2, 0:n], lfb[0:n, :], start=True, stop=False), r=["lfb", "cstf"], w=["ps5"])
                P.pe(lambda e: e.matmul(ps[5][0:n, 0:8], ones1[0:1, 0:n], tot[0:1, :], start=False, stop=True), r=["tot", "ones1"], w=["ps5"])
                P.dve(lambda e: e.tensor_copy(cstore[0:n, kb_, :], ps[5][0:n, 0:8]), r=["ps5"], w=["cstore"])
                P.pe(lambda e: e.matmul(ps[5][0:1, 8:16], cstf[0:n, 1, 128 - n:129 - n], cstore[0:n, kb_, :], start=True, stop=True), r=["cstore", "cstf"], w=["ps5"])
                P.dve(lambda e: e.tensor_copy(tot[:], ps[5][0:1, 8:16]), r=["ps5"], w=["tot"])

            for b in range(PAST // 128):
                P.dma("sp", lfb[:], ca_lf[l, b * 128:(b + 1) * 128, :], w=["lfb"])
                cum_block(b, 128)
            norm_block(l, (xs if l == 0 else hs1)[:, :], 0, nrow=NS, rname=("hs1" if l > 0 else None))
            for uname in TM_UNITS:
                W, wn = load_w(l, UNITS.index(uname))
                si = nxt("s", 4)
                for c in range(8):
                    P.pe(lambda e: e.matmul(ps[si][0:NS, :], hT[:, c, 0:NS], W[:, c, :], start=(c == 0), stop=(c == 7)), r=[wn, "hT"], w=[psn[si]])
                k = nxt("st", 2)
                S_ = st[k]; sn = f"st{k}"
                P.act(lambda e: e.activation(S_[0:NS, :], ps[si][0:NS, :], AF.Copy), r=[psn[si]], w=[sn])
                V_, vn = vst[k], f"vst{k}"
                if uname in ("tka", "tva", "tkc", "tvc"):
                    dst = {"tka": s_ak, "tva": s_av, "tkc": s_ck, "tvc": s_cv}[uname]
                    P.dma("pool", dst[l], S_[0:NS, :], r=[sn])
                    if uname in ("tva", "tvc"):
                        P.dve(lambda e: e.tensor_copy(V_[0:NS, :], S_[0:NS, :]), r=[sn], w=[vn])
                        vd = SS_VA[l][PAST:PAST + NS, :] if uname == "tva" else SS_VC[l][512:512 + NS, :]
                        P.dma("pool", vd, V_[0:NS, :], r=[vn], w=[shist])
                else:
                    P.dma("pool", s_bk[l], S_[0:NS, 0:128], r=[sn])
                    P.dma("pool", s_bv[l], S_[0:NS, 128:256], r=[sn])
                    P.dma("pool", s_ik[l], S_[0:NS, 256:288], r=[sn])
                    P.dve(lambda e: e.tensor_copy(V_[0:NS, 0:128], S_[0:NS, 128:256]), r=[sn], w=[vn])
                    P.dma("pool", SS_VB[l][PAST:PAST + NS, :], V_[0:NS, 0:128], r=[vn], w=[shist])
                    P.dve(lambda e: e.tensor_tensor(lfb[0:NS, :], S_[0:NS, 288:296], bfb[0:NS, l * 8:(l + 1) * 8], ALU.add), r=[sn, "bfb"], w=["lfb"])
                    P.act(lambda e: e.activation(lfb[0:NS, :], lfb[0:NS, :], AF.Exp, scale=-1.0), r=["lfb"], w=["lfb"])
                    P.act(lambda e: e.activation(lfb[0:NS, :], lfb[0:NS, :], AF.Ln, bias=1.0), r=["lfb"], w=["lfb"])
                    P.dve(lambda e: e.tensor_scalar(lfb[0:NS, :], lfb[0:NS, :], -1.0, None, ALU.mult), r=["lfb"], w=["lfb"])
                    P.dma("pool", s_lf[l], lfb[0:NS, :], r=["lfb"])
                    cum_block(8, NS)
                    P.dve(lambda e: e.tensor_scalar(wsgn[0:NS, 0, :], S_[0:NS, 296:304], 0.0, 2.0, ALU.is_ge, ALU.mult), r=[sn], w=["wsgn"])
                    P.dve(lambda e: e.tensor_scalar(wsgn[0:NS, 0, :], wsgn[0:NS, 0, :], -1.0, None, ALU.add), r=["wsgn"], w=["wsgn"])
                    P.dve(lambda e: e.scalar_tensor_tensor(wabs[0:NS, 0, :], S_[0:NS, 296:304], IDXS, wsgn[0:NS, 0, :], ALU.mult, ALU.mult), r=[sn, "wsgn"], w=["wabs"])
            P.pe(lambda e: e.matmul(ps[5][:, 16:24], cstf[0:NS, 3, :], cstore[0:NS, 8, :], start=True, stop=True), r=["cstore", "cstf"], w=["ps5"])
            P.dve(lambda e: e.tensor_copy(cbc[:], ps[5][:, 16:24]), r=["ps5"], w=["cbc"])
            for h in range(8):
                P.dve(lambda e: e.tensor_scalar(nbias[:, 0:9, h], cstore[:, 0:9, h], cbc[:, h:h + 1], -1.0, ALU.subtract, ALU.mult), r=["cstore", "cbc"], w=["nbias"])
            P.dve(lambda e: e.tensor_tensor(rq[0:NS, 0, :], cstore[0:NS, 8, :], cbc[0:NS, :], ALU.subtract), r=["cstore", "cbc"], w=["rq"])
            P.pe(lambda e: e.transpose(ps[6][0:8, 0:NS], rq[0:NS, 0, :], cstf[0:NS, 0, 0:NS]), r=["rq", "cstf"], w=["ps6"])
            P.dve(lambda e: e.tensor_copy(rT[:, 0:NS], ps[6][0:8, 0:NS]), r=["ps6"], w=["rT"])

            def s_evac_k(dst3, koff):
                def f(j, pt, pn):
                    P.act(lambda e: e.activation(kst[:, j, 0:NS], pt[0:64, 0:NS], AF.Copy), r=[pn], w=["kst"])
                    if j == 7:
                        P.dma("pool", dst3[:, :, koff:koff + NS].rearrange("h d k -> d h k"), kst[:, :, 0:NS], r=["kst"], w=[shist])
                return f

            def s_evac_q(j, pt, pn):
                P.act(lambda e: e.activation(Q[0:64, j, 0:NS], pt[0:64, 0:NS], AF.Copy, scale=SCALE), r=[pn], w=["Q"])

            def s_evac_z(zt, zn):
                def f(j, pt, pn):
                    P.act(lambda e: e.activation(zt[:, j, 0:NS], pt[0:64, 0:NS], AF.Silu), r=[pn], w=[zn])
                return f

            fm_unit(l, "ka", NS, s_evac_k(SS_KA[l], PAST))
            fm_unit(l, "za", NS, s_evac_z(zg["a"], "zga"))
            fm_unit(l, "qa", NS, s_evac_q)
            for h in range(8):
                P.dma("sp", Q[64:65, h, 0:NS], rT[h:h + 1, 0:NS], r=["rT"], w=["Q"])
            for half in range(2):
                at = Attn()
                for sbk in range(3):
                    nk = 512 if sbk < 2 else NS
                    bi = load_kv(SS_KA[l], SS_VA[l], half * 4, 4, sbk * 512, nk, shist)
                    for kb in range((nk + 127) // 128):
                        n = min(128, nk - kb * 128)
                        adds = [(0, NS, identb[0:NS, 0:NS], ma0b[0:NS, 0:NS], ["identb", "ma0b"])] if sbk == 2 else []
                        for i in range(4):
                            hh = half * 4 + i
                            at.tile(dict(kT=kbuf[bi][0:65, i, kb * 128:kb * 128 + n], v=vbuf[bi][0:n, kb, i, 0:65], n=n, qlo=0, qhi=NS,
                                         qap=Q[0:65, hh, 0:NS], adds=adds, bias=nbias[0:n, 4 * sbk + kb, hh:hh + 1],
                                         names=[f"kbuf{bi}", f"vbuf{bi}"], O=ps[4 + i], oname=psn[4 + i],
                                         first=(sbk == 0 and kb == 0), last=(sbk == 2)))
                at.flush()
                for i in range(4):
                    finish_head(ps[4 + i], psn[4 + i], zg["a"], "zga", half * 4 + i, NS, slice(0, NS))
            fm_unit(l, "kc", NS, s_evac_k(SS_KC[l], 512))
            fm_unit(l, "zc", NS, s_evac_z(zg["c"], "zgc"))
            fm_unit(l, "qc", NS, s_evac_q)
            for half in range(2):
                at = Attn()
                for sbk in range(2):
                    nk = 512 if sbk < 1 else NS
                    bi = load_kv(SS_KC[l], SS_VC[l], half * 4, 4, sbk * 512, nk, shist)
                    for kb in range((nk + 127) // 128):
                        n = min(128, nk - kb * 128)
                        for i in range(4):
                            hh = half * 4 + i
                            adds = []
                            if sbk == 0 and kb == 3:
                                adds = [(0, NS, identb[:], bc[:, 1, hh, 0:NS], ["identb", "btile"])]
                            if sbk == 1:
                                adds = [(0, NS, identb[0:NS, 0:NS], bc[0:NS, 0, hh, 0:NS], ["identb", "btile"])]
                            at.tile(dict(kT=kbuf[bi][0:64, i, kb * 128:kb * 128 + n], v=vbuf[bi][0:n, kb, i, 0:65], n=n, qlo=0, qhi=NS,
                                         qap=Q[0:64, hh, 0:NS], adds=adds, bias=None,
                                         names=[f"kbuf{bi}", f"vbuf{bi}"], O=ps[4 + i], oname=psn[4 + i],
                                         first=(sbk == 0 and kb == 0), last=(sbk == 1)))
                at.flush()
                for i in range(4):
                    finish_head(ps[4 + i], psn[4 + i], zg["c"], "zgc", half * 4 + i, NS, slice(0, NS))
            def s_evac_bx(j, pt, pn):
                if j < 2:
                    P.act(lambda e: e.activation(kst[:, j, 0:NS], pt[0:64, 0:NS], AF.Copy), r=[pn], w=["kst"])
                    if j == 1:
                        P.dma("pool", SS_KB[l][:, :, PAST:PAST + NS].rearrange("h d k -> d h k"), kst[:, 0:2, 0:NS], r=["kst"], w=[shist])
                elif j < 6:
                    P.act(lambda e: e.activation(iqT[:, j - 2, 0:NS], pt[0:64, 0:NS], AF.Copy), r=[pn], w=["iqT"])
                elif j == 6:
                    P.act(lambda e: e.activation(kst[:, 2, 0:NS], pt[0:64, 0:NS], AF.Copy), r=[pn], w=["kst"])
                    P.dma("pool", SS_IK[l][:, PAST:PAST + NS], kst[:, 2, 0:NS], r=["kst"], w=[shist])
            fm_unit(l, "bx", NS, s_evac_bx)
            fm_unit(l, "zb", NS, s_evac_z(zg["b"], "zgb"))
            fm_unit(l, "qb", NS, s_evac_q)
            NK = PAST + NS
            for k0 in range(0, NK, 512):
                nk = min(512, NK - k0)
                ii = nxt("ik", 2)
                P.dma("sp", ikbuf[ii][:, 0:nk], SS_IK[l][:, k0:k0 + nk], r=[shist], w=[f"ikbuf{ii}"])
                for h in range(8):
                    base = 32 * (h % 2)
                    pi_ = 2 + nxt("ips", 2)
                    P.pe(lambda e: e.matmul(ps[pi_][0:NS, 0:nk], iqT[base:base + 32, h // 2, 0:NS], ikbuf[ii][base:base + 32, 0:nk], start=True, stop=True),
                         r=["iqT", f"ikbuf{ii}"], w=[psn[pi_]])
                    ri = nxt("rr", 2)
                    P.act(lambda e: e.activation(Rr[ri][0:NS, 0:nk], ps[pi_][0:NS, 0:nk], AF.Relu, scale=wabs[0:NS, 0, h:h + 1]), r=[psn[pi_], "wabs"], w=[f"Rr{ri}"])
                    if h == 0:
                        P.dve(lambda e: e.tensor_scalar(SC[0:NS, k0:k0 + nk], Rr[ri][0:NS, 0:nk], wsgn[0:NS, 0, 0:1], None, ALU.mult), r=[f"Rr{ri}", "wsgn"], w=["SC"])
                    else:
                        P.dve(lambda e: e.scalar_tensor_tensor(SC[0:NS, k0:k0 + nk], Rr[ri][0:NS, 0:nk], wsgn[0:NS, 0, h:h + 1], SC[0:NS, k0:k0 + nk], ALU.mult, ALU.add),
                              r=[f"Rr{ri}", "wsgn", "SC"], w=["SC"])
            P.dve(lambda e: e.memset(cntb[:], 0.0), w=["cntb"])
            P.dve(lambda e: e.memset(small[:, 8:9], 0.0), w=["cand"])
            for it in range(NBIS):
                stp = 64.0 * (0.5 ** it)
                P.dve(lambda e: e.tensor_scalar(junk[0:NS, 0:NK], SC[0:NS, 0:NK], small[0:NS, 8:9], 0.0, ALU.is_ge, ALU.add, accum_out=cntb[0:NS, it:it + 1]),
                      r=["SC", "cand", "cntb"], w=["junk", "cntb"])
                a, b_ = (stp, -0.5 * stp) if it < NBIS - 1 else (stp, -stp)
                P.dve(lambda e: e.tensor_scalar(small[0:NS, 9:10], cntb[0:NS, it:it + 1], float(TOPK), a, ALU.is_ge, ALU.mult), r=["cntb"], w=["fl"])
                P.dve(lambda e: e.scalar_tensor_tensor(small[0:NS, 8:9], small[0:NS, 9:10], b_, small[0:NS, 8:9], ALU.add, ALU.add), r=["fl", "cand"], w=["cand"])
            at = Attn()
            for sbk in range(3):
                k0 = sbk * 512
                nk = min(512, NK - k0)
                mi = nxt("mb", 2)
                P.dve(lambda e: e.tensor_scalar(Mb[mi][0:NS, 0:nk], SC[0:NS, k0:k0 + nk], small[0:NS, 8:9], NEG, ALU.is_lt, ALU.mult), r=["SC", "cand"], w=[f"Mb{mi}"])
                bi = load_kv(SS_KB[l], SS_VB[l], 0, 2, k0, nk, shist)
                for kb in range((nk + 127) // 128):
                    n = min(128, nk - kb * 128)
                    gkb = sbk * 4 + kb
                    for j in range(2):
                        adds = [(0, 4 * NS, Mb[mi][0:NS, kb * 128:kb * 128 + n], i4b[0:NS, :, 0:NS], [f"Mb{mi}", "i4b"])]
                        if gkb >= 7:
                            for hq in range(4):
                                if gkb == 7:
                                    adds.append((hq * NS, (hq + 1) * NS, identb[:], b5[:, 1, 4 * j + hq, 0:NS], ["identb", "btile"]))
                                else:
                                    adds.append((hq * NS, (hq + 1) * NS, identb[0:NS, 0:NS], b5[0:NS, 0, 4 * j + hq, 0:NS], ["identb", "btile"]))
                        at.tile(dict(kT=kbuf[bi][0:64, j, kb * 128:kb * 128 + n], v=vbuf[bi][0:n, kb, j, 0:65], n=n, qlo=0, qhi=4 * NS,
                                     qap=Q[0:64, 4 * j:4 * j + 4, 0:NS], adds=adds, bias=None,
                                     names=[f"kbuf{bi}", f"vbuf{bi}"], O=ps[4 + j], oname=psn[4 + j],
                                     first=(gkb == 0), last=(gkb == 8)))
            at.flush()
            for j in range(2):
                finish_head(ps[4 + j], psn[4 + j], zg["b"], "zgb", (4 * j, 4 * j + 4), 4 * NS, slice(0, NS))
            allz = [zg["a"], zg["b"], zg["c"]]
            alln = ["zga", "zgb", "zgc"]
            for u in range(6):
                Wo_, won = load_wo(l, u)
                for hq in range(4):
                    hidx = u * 4 + hq
                    zt, zn = allz[hidx // 8], alln[hidx // 8]
                    for n_ in range(2):
                        P.pe(lambda e: e.matmul(ps[n_][0:NS, :], zt[:, hidx % 8, 0:NS], Wo_[:, hq, n_ * 512:(n_ + 1) * 512], start=(hidx == 0), stop=(hidx == 23)),
                             r=[zn, won], w=[psn[n_]])
            xi = nxt("x", 2)
            X = xt[xi]; xname = f"xt{xi}"
            P.dma("sp", X[0:NS, :], (xs if l == 0 else hs1)[:, :], r=(["hs1"] if l > 0 else []), w=[xname])
            for n_ in range(2):
                P.dve(lambda e: e.tensor_tensor(X[0:NS, n_ * 512:(n_ + 1) * 512], X[0:NS, n_ * 512:(n_ + 1) * 512], ps[n_][0:NS, :], ALU.add), r=[xname, psn[n_]], w=[xname])
            if l < nlayers - 1:
                P.dma("pool", hs1[:, :], X[0:NS, :], r=[xname], w=["hs1"])
            else:
                P.act(lambda e: e.activation(junk[0:NS, 0:1024], X[0:NS, :], AF.Square, accum_out=small[0:NS, 16:17]), r=[xname], w=["junk", "fs0"])
                P.dve(lambda e: e.tensor_scalar(small[0:NS, 17:18], small[0:NS, 16:17], 1.0 / D_MODEL, EPS, ALU.mult, ALU.add), r=["fs0"], w=["fs1"])
                P.act(lambda e: e.activation(small[0:NS, 18:19], small[0:NS, 17:18], AF.Sqrt), r=["fs1"], w=["fs2"])
                P.dve(lambda e: e.reciprocal(small[0:NS, 19:20], small[0:NS, 18:19]), r=["fs2"], w=["fs3"])
                P.dve(lambda e: e.scalar_tensor_tensor(X[0:NS, :], X[0:NS, :], small[0:NS, 19:20], fgb[0:NS, :], ALU.mult, ALU.mult), r=[xname, "fs3", "fgb"], w=[xname])
                P.dma("pool", y_s[:, :], X[0:NS, :], r=[xname])
    P.finalize_and_emit()
    return nc, es, P


def _t5_bucket_np(rel):
    nb = 16
    max_exact = 8
    ret = np.where(rel > 0, nb, 0)
    n = np.abs(rel)
    nf = np.maximum(n, 1).astype(np.float32)
    large = max_exact + (np.log(nf / max_exact) / math.log(128 / max_exact) * (nb - max_exact)).astype(np.int32)
    large = np.minimum(large, nb - 1)
    return ret + np.where(n < max_exact, n, large)


def _constants():
    p = np.arange(128)[:, None]
    f = np.arange(128)[None, :]
    cst = np.zeros((128, 8, 128), np.float32)
    cst[:, 0] = (p == f)
    cst[:, 1] = (p + f == 127)
    cst[:, 2] = (p <= f)
    cst[0, 3, :] = 1.0
    cst[:, 4] = np.where(p > f, NEG, 0.0)
    cst[:, 5] = np.where((p >= 64) & (f < 64), NEG, 0.0)
    cst[:, 6] = np.where((p < 64) & (f >= 64), NEG, 0.0)
    cst[:, 7] = np.where((p < 64) & (f >= 64), -1e30, 0.0)
    rel = 127 - np.arange(LTAB)
    bk = _t5_bucket_np(rel.astype(np.int32))
    oh5 = np.zeros((32, LTAB), np.float32)
    oh5[bk, np.arange(LTAB)] += 1.0
    far = int(_t5_bucket_np(np.array([-100000], np.int32))[0])
    oh5[far, :] -= 1.0
    idx = np.clip(rel, -128, 128) + 128
    ohc = np.zeros((3 * 128, LTAB), np.float32)
    ohc[idx, np.arange(LTAB)] += 1.0
    ohc[0, :] -= 1.0
    return cst.reshape(128, 8 * 128), oh5, ohc.reshape(3, 128, LTAB)


_PROG = {}


def _get_prog(key=(True, 4, DEPTH)):
    if key not in _PROG:
        _PROG[key] = build_program(*key)
    return _PROG[key]


def _host_inputs(x_prompt, norm_g, w_in, b_f, t5_bias, c_rel_bias, w_out, final_g):
    cst, oh5, ohc = _constants()
    wus = np.zeros((DEPTH, NU, 128, 8, 512), np.float32)
    for l in range(DEPTH):
        for u, name in enumerate(UNITS):
            cols = np.array(_unit_cols(name))
            m = cols >= 0
            w = np.zeros((D_MODEL, 512), np.float32)
            w[:, m] = w_in[l][:, cols[m]]
            wus[l, u] = w.reshape(8, 128, 512).transpose(1, 0, 2)
    wos = np.ascontiguousarray(w_out.reshape(DEPTH, 6, 4, 64, D_MODEL).transpose(0, 1, 3, 2, 4))
    gcol = np.ascontiguousarray(norm_g.reshape(DEPTH, 8, 128).transpose(2, 0, 1).reshape(128, DEPTH * 8))
    crel = np.zeros((DEPTH, 384, 8), np.float32)
    crel[:, :257] = c_rel_bias
    common = dict(wu=wus, wo=wos, gcol=gcol, fg=np.ascontiguousarray(final_g.reshape(1, D_MODEL)),
                  bfb=np.ascontiguousarray(b_f.reshape(1, DEPTH * 8)), t5=np.ascontiguousarray(t5_bias),
                  crel=crel.reshape(DEPTH, 3, 128, 8), oh5=oh5, ohc=ohc, cst=cst)
    return common


def _percore(j):
    pc = np.zeros((128, 1024), np.float32)
    p = np.arange(128)
    pc[:, 0:512] = (j * 512 + np.arange(512))[None, :]
    for rr in range(16):
        pc[:, 512 + rr] = rr * 128 + p
    for qb in range(4):
        for ch in range(4):
            pc[:, 528 + qb * 4 + ch] = (8 * j + 2 * qb + (p >= 64) + 1) * 64 - ch * 512
        for rr in range(17):
            pc[:, 544 + (qb * 17 + rr) * 2] = 1.0 if (rr - 1) == 4 * j + qb else 0.0
            pc[:, 544 + (qb * 17 + rr) * 2 + 1] = 1.0 if (rr - 1) == 4 * j + qb - 1 else 0.0
    for r in range(4):
        pc[:, 680 + r] = 1.0 if j == r else 0.0
    pc[:, 684] = -30000.0 if j == 0 else 0.0
    return pc


def _in_maps(x_prompt, x_sample, cache_a_k, cache_a_v, cache_a_logf, cache_b_k, cache_b_v, cache_b_idx_k,
             cache_c_k, cache_c_v, norm_g, w_in, b_f, t5_bias, c_rel_bias, w_out, final_g):
    f = lambda a: np.ascontiguousarray(np.asarray(a, dtype=np.float32))
    x_prompt, norm_g, w_in, b_f, t5_bias, c_rel_bias, w_out, final_g = map(f, (x_prompt, norm_g, w_in, b_f, t5_bias, c_rel_bias, w_out, final_g))
    common = _host_inputs(x_prompt, norm_g, w_in, b_f, t5_bias, c_rel_bias, w_out, final_g)
    common["iota5"] = np.ascontiguousarray(np.broadcast_to(np.arange(512, dtype=np.float32)[None, :], (128, 512)))
    in_maps = []
    for c in range(8):
        b, j = c // 4, c % 4
        m = dict(common)
        xb = x_prompt[b].reshape(NG, 512, D_MODEL)
        m["xp"] = x_prompt[b]
        m["xq"] = np.ascontiguousarray(xb[j::4])
        xpv = np.zeros((4, 512, D_MODEL), np.float32)
        for mm in range(4):
            if 4 * mm + j - 1 >= 0:
                xpv[mm] = xb[4 * mm + j - 1]
        m["xprev"] = xpv
        m["pcore"] = _percore(j)
        m["xs"] = f(x_sample[c])
        m["ca_k"] = f(cache_a_k[:, c]).reshape(DEPTH, PAST, 512); m["ca_v"] = f(cache_a_v[:, c]).reshape(DEPTH, PAST, 512)
        m["ca_lf"] = f(cache_a_logf[:, c]).reshape(DEPTH, PAST, 8)
        m["cb_k"] = f(cache_b_k[:, c]).reshape(DEPTH, PAST, 128); m["cb_v"] = f(cache_b_v[:, c]).reshape(DEPTH, PAST, 128)
        m["cb_ik"] = f(cache_b_idx_k[:, c]).reshape(DEPTH, PAST, 32)
        m["cc_k"] = f(cache_c_k[:, c]).reshape(DEPTH, 512, 512); m["cc_v"] = f(cache_c_v[:, c]).reshape(DEPTH, 512, 512)
        in_maps.append(m)
    return in_maps


def kernel(x_prompt, x_sample, cache_a_k, cache_a_v, cache_a_logf, cache_b_k, cache_b_v, cache_b_idx_k,
           cache_c_k, cache_c_v, norm_g, w_in, b_f, t5_bias, c_rel_bias, w_out, final_g):
    nc, es, P = _get_prog()
    in_maps = _in_maps(x_prompt, x_sample, cache_a_k, cache_a_v, cache_a_logf, cache_b_k, cache_b_v, cache_b_idx_k,
                       cache_c_k, cache_c_v, norm_g, w_in, b_f, t5_bias, c_rel_bias, w_out, final_g)
    res = run_bass_kernel_spmd(nc, in_maps, core_ids=list(range(8)))
    R = res.results
    st = lambda name, shp: np.stack([R[4 * b][name] for b in range(2)], axis=1).reshape(shp)
    y_prompt = np.zeros((BATCH, NG, 512, D_MODEL), np.float32)
    for c in range(8):
        b, j = c // 4, c % 4
        y_prompt[b, j::4] = R[c]["y_q"].reshape(4, 512, D_MODEL)
    y_prompt = y_prompt.reshape(BATCH, SEQ, D_MODEL)
    ss = lambda name, shp: np.stack([R[b][name] for b in range(DEC_BATCH)], axis=1).reshape(shp)
    y_sample = np.stack([R[b]["y_s"] for b in range(DEC_BATCH)], axis=0)
    outs = [y_prompt, y_sample,
            st("o_ak", (DEPTH, BATCH, SEQ, H, HD)), st("o_av", (DEPTH, BATCH, SEQ, H, HD)), st("o_lf", (DEPTH, BATCH, SEQ, H)),
            st("o_bk", (DEPTH, BATCH, SEQ, KVB, HD)), st("o_bv", (DEPTH, BATCH, SEQ, KVB, HD)), st("o_ik", (DEPTH, BATCH, SEQ, IDX_D)),
            st("o_ck", (DEPTH, BATCH, 512, H, HD)), st("o_cv", (DEPTH, BATCH, 512, H, HD)),
            ss("s_ak", (DEPTH, DEC_BATCH, DEC_SEQ, H, HD)), ss("s_av", (DEPTH, DEC_BATCH, DEC_SEQ, H, HD)), ss("s_lf", (DEPTH, DEC_BATCH, DEC_SEQ, H)),
            ss("s_bk", (DEPTH, DEC_BATCH, DEC_SEQ, KVB, HD)), ss("s_bv", (DEPTH, DEC_BATCH, DEC_SEQ, KVB, HD)), ss("s_ik", (DEPTH, DEC_BATCH, DEC_SEQ, IDX_D)),
            ss("s_ck", (DEPTH, DEC_BATCH, DEC_SEQ, H, HD)), ss("s_cv", (DEPTH, DEC_BATCH, DEC_SEQ, H, HD))]
    return tuple(outs)
```

```python
import math
import types
import numpy as np
from contextlib import ExitStack
import concourse.bass as bass
import concourse.mybir as mybir
from concourse.bass_utils import run_bass_kernel_spmd

F32 = mybir.dt.float32
BF16 = mybir.dt.bfloat16
ALU = mybir.AluOpType
AF = mybir.ActivationFunctionType

D_MODEL = 1024; BATCH = 2; SEQ = 8192; DEPTH = 2; DEC_BATCH = 8; DEC_SEQ = 16; PAST = 1024
HD = 64; H = 8; KVB = 2; IDX_H = 8; IDX_D = 32; TOPK = 256
SCALE = HD ** -0.5
IDXS = (IDX_D ** -0.5) * (IDX_H ** -0.5)
EPS = 1e-6
NEG = -32768.0
NG = SEQ // 512
NBIS = 24
LTAB = 384

_SPLIT = (512, 512, 512, 512, 8, 512, 128, 128, 512, 256, 8, 32, 512, 512, 512, 512)
_OFF = np.concatenate([[0], np.cumsum(_SPLIT)])
(QA, KA, VA, ZA, FA, QB, KB, VB, ZB, IQ, IW, IK, QC, KC, VC, ZC) = [int(o) for o in _OFF[:-1]]
FM_UNITS = ["qa", "ka", "za", "qb", "zb", "bx", "qc", "kc", "zc"]
TM_UNITS = ["tka", "tva", "tb", "tkc", "tvc"]
UNITS = FM_UNITS + TM_UNITS
NU = len(UNITS)


def _unit_cols(name):
    r = lambda a, n: list(range(a, a + n))
    pad = lambda l: l + [-1] * (512 - len(l))
    if name == "qa": return r(QA, 512)
    if name == "ka": return r(KA, 512)
    if name == "za": return r(ZA, 512)
    if name == "qb": return r(QB, 512)
    if name == "zb": return r(ZB, 512)
    if name == "qc": return r(QC, 512)
    if name == "kc": return r(KC, 512)
    if name == "zc": return r(ZC, 512)
    if name == "bx": return pad(r(KB, 128) + r(IQ, 256) + r(IK, 32) + r(IK, 32))
    if name == "tka": return r(KA, 512)
    if name == "tva": return r(VA, 512)
    if name == "tkc": return r(KC, 512)
    if name == "tvc": return r(VC, 512)
    if name == "tb": return pad(r(KB, 128) + r(VB, 128) + r(IK, 32) + r(FA, 8) + r(IW, 8))
    raise KeyError(name)


class Prog:
    STREAMS = ("pe", "act", "dve", "pool", "sp")

    def __init__(self, nc):
        self.nc = nc
        self.ops = []
        self.ndma = {}

    @staticmethod
    def _freeze(fn):
        if fn.__closure__ is None:
            return fn
        cells = []
        for c in fn.__closure__:
            try:
                cells.append(types.CellType(c.cell_contents))
            except ValueError:
                cells.append(c)
        return types.FunctionType(fn.__code__, fn.__globals__, fn.__name__, fn.__defaults__, tuple(cells))

    NSUB = 16

    def add(self, stream, fn, r=(), w=(), dma=False, cc=False):
        fn = self._freeze(fn)
        if cc:
            track = "dma_cc"
        elif dma:
            k = self.ndma.get(stream, 0)
            self.ndma[stream] = k + 1
            track = f"dma_{stream}#{k % self.NSUB}"
        else:
            track = stream
        self.ops.append((stream, track, fn, tuple(r), tuple(w)))

    def pe(self, fn, r=(), w=()): self.add("pe", fn, r, w)
    def act(self, fn, r=(), w=()): self.add("act", fn, r, w)
    def dve(self, fn, r=(), w=()): self.add("dve", fn, r, w)
    def pool(self, fn, r=(), w=()): self.add("pool", fn, r, w)

    def dma(self, stream, out, in_, r=(), w=(), **kw):
        self.add(stream, lambda e: e.dma_start(out=out, in_=in_, **kw), r, w, dma=True)

    def finalize_and_emit(self):
        nc = self.nc
        ops = self.ops
        n = len(ops)
        writers = {}
        readers = {}
        prev_on = {}
        deps = [None] * n
        signal = [False] * n
        qof = lambda t: t.split("#")[0]
        for i, (stream, track, fn, R, W) in enumerate(ops):
            d = set()
            is_dma = track.startswith("dma_")
            if is_dma:
                j = prev_on.get(track)
                if j is not None:
                    d.add(j)
                prev_on[track] = i
            for res in R:
                for tj, j in writers.get(res, {}).items():
                    if tj != track or is_dma or track != "pe":
                        d.add(j)
            for res in W:
                lazy = res.startswith("~")
                for tj, j in writers.get(res, {}).items():
                    if lazy:
                        if qof(tj) != qof(track):
                            d.add(j)
                    elif tj != track or is_dma:
                        d.add(j)
                for tj, j in readers.get(res, {}).items():
                    if lazy:
                        if qof(tj) != qof(track):
                            d.add(j)
                    elif tj != track or is_dma:
                        d.add(j)
            for res in R:
                readers.setdefault(res, {})[track] = i
            for res in W:
                if res.startswith("~"):
                    writers.setdefault(res, {})[track] = i
                else:
                    writers[res] = {track: i}
                    readers[res] = {}
            d.discard(i)
            deps[i] = d
            for j in d:
                signal[j] = True
        tracks = sorted({o[1] for o in ops})
        cnt = {t: 0 for t in tracks}
        val = [0] * n
        for i, (stream, track, fn, R, W) in enumerate(ops):
            if track == "dma_cc":
                cnt[track] += 1
                val[i] = cnt[track]
            elif track.startswith("dma_"):
                cnt[track] += 16
                val[i] = cnt[track]
            elif signal[i]:
                cnt[track] += 1
                val[i] = cnt[track]
        known = {s: {t: 0 for t in tracks} for s in self.STREAMS}
        waits = [None] * n
        for i, (stream, track, fn, R, W) in enumerate(ops):
            need = {}
            for j in deps[i]:
                tj = ops[j][1]
                need[tj] = max(need.get(tj, 0), val[j])
            wl = []
            for tj, v in need.items():
                if v > known[stream][tj]:
                    wl.append((tj, v))
                    known[stream][tj] = v
            waits[i] = wl
        self.stats = dict(cnt)
        by_stream = {s: [] for s in self.STREAMS}
        for i, o in enumerate(ops):
            by_stream[o[0]].append(i)
        with ExitStack() as es:
            sems = {t: es.enter_context(nc.semaphore("s_" + t.replace("#", "_"))) for t in tracks}
            block = es.enter_context(nc.Block())

            def run(eng, stream):
                for i in by_stream[stream]:
                    _, track, fn, R, W = ops[i]
                    for tj, v in waits[i]:
                        eng.wait_ge(sems[tj], v)
                    inst = fn(eng)
                    if track == "dma_cc":
                        inst.then_inc(sems[track], 1)
                    elif track.startswith("dma_"):
                        inst.then_inc(sems[track], 16)
                    elif signal[i]:
                        inst.then_inc(sems[track], 1)
                if stream == "sp":
                    for t in tracks:
                        if t.startswith("dma_") and cnt[t] > known[stream][t]:
                            eng.wait_ge(sems[t], cnt[t])

            @block.tensor
            def _(e): run(e, "pe")

            @block.scalar
            def _(e): run(e, "act")

            @block.vector
            def _(e): run(e, "dve")

            @block.gpsimd
            def _(e): run(e, "pool")

            @block.sync
            def _(e): run(e, "sp")


def build_program(do_sample=True, nm=4, nlayers=DEPTH):
    nc = bass.Bass("TRN2", target_bir_lowering=False)
    es = ExitStack()
    din = lambda name, shape, dt=F32: nc.dram_tensor(name, list(shape), dt, kind="ExternalInput").ap()
    dout = lambda name, shape: nc.dram_tensor(name, list(shape), F32, kind="ExternalOutput").ap()
    dscr = lambda name, shape, dt=BF16: nc.dram_tensor(name, list(shape), dt, kind="Internal").ap()
    xp = din("xp", [SEQ, D_MODEL])
    wu = din("wu", [DEPTH, NU, 128, 8, 512])
    wo = din("wo", [DEPTH, 6, 64, 4, 1024])
    gcol_d = din("gcol", [128, DEPTH * 8])
    fg_d = din("fg", [1, D_MODEL])
    bf_d = din("bfb", [1, DEPTH * 8])
    t5_d = din("t5", [32, 8])
    crel_d = din("crel", [DEPTH, 3, 128, 8])
    oh5_d = din("oh5", [32, LTAB])
    ohc_d = din("ohc", [3, 128, LTAB])
    cst_d = din("cst", [128, 8 * 128])
    y_q = dout("y_q", [2048, D_MODEL])
    xq = din("xq", [4, 512, D_MODEL]); xprev = din("xprev", [4, 512, D_MODEL])
    pc_d = din("pcore", [128, 1024])
    iota_d = din("iota5", [128, 512])
    o_ak = dout("o_ak", [DEPTH, SEQ, 512]); o_av = dout("o_av", [DEPTH, SEQ, 512])
    o_lf = dout("o_lf", [DEPTH, SEQ, 8])
    o_bk = dout("o_bk", [DEPTH, SEQ, 128]); o_bv = dout("o_bv", [DEPTH, SEQ, 128])
    o_ik = dout("o_ik", [DEPTH, SEQ, 32])
    o_ck = dout("o_ck", [DEPTH, 512, 512]); o_cv = dout("o_cv", [DEPTH, 512, 512])
    wub = dscr("wub", [DEPTH, NU, 128, 8, 512])
    wob = dscr("wob", [DEPTH, 6, 64, 4, 1024])
    hp1q = dscr("hp1q", [2048, D_MODEL], F32)
    hpg = dscr("hpg", [SEQ, D_MODEL], F32)
    ccs = dscr("ccs", [256, D_MODEL], F32)
    COMBS = dscr("combs", [4, 17, 2, 128, 512])
    AMASK = dscr("amask", [16, 128, 512])
    ccd = dscr("ccd", [1024, D_MODEL], F32)
    S_KCL = dscr("scr_kcl", [DEPTH, 8, 64, 1024]); S_VCL = dscr("scr_vcl", [DEPTH, 1024, 512])
    tab5 = dscr("tab5", [8, LTAB], F32)
    tabc = dscr("tabc", [DEPTH, 8, LTAB], F32)
    S_KA = dscr("scr_s_ka", [DEPTH, 8, 64, SEQ]); S_KC = dscr("scr_s_kc", [DEPTH, 8, 64, SEQ])
    S_KB = dscr("scr_s_kb", [DEPTH, 2, 64, SEQ]); S_IK = dscr("scr_s_ik", [DEPTH, 64, SEQ])
    S_VA = dscr("scr_s_va", [DEPTH, SEQ, 512]); S_VC = dscr("scr_s_vc", [DEPTH, SEQ, 512])
    S_VB = dscr("scr_s_vb", [DEPTH, SEQ, 128])

    xs = din("xs", [DEC_SEQ, D_MODEL])
    ca_k = din("ca_k", [DEPTH, PAST, 512]); ca_v = din("ca_v", [DEPTH, PAST, 512]); ca_lf = din("ca_lf", [DEPTH, PAST, 8])
    cb_k = din("cb_k", [DEPTH, PAST, 128]); cb_v = din("cb_v", [DEPTH, PAST, 128]); cb_ik = din("cb_ik", [DEPTH, PAST, 32])
    cc_k = din("cc_k", [DEPTH, 512, 512]); cc_v = din("cc_v", [DEPTH, 512, 512])
    y_s = dout("y_s", [DEC_SEQ, D_MODEL])
    s_ak = dout("s_ak", [DEPTH, DEC_SEQ, 512]); s_av = dout("s_av", [DEPTH, DEC_SEQ, 512]); s_lf = dout("s_lf", [DEPTH, DEC_SEQ, 8])
    s_bk = dout("s_bk", [DEPTH, DEC_SEQ, 128]); s_bv = dout("s_bv", [DEPTH, DEC_SEQ, 128]); s_ik = dout("s_ik", [DEPTH, DEC_SEQ, 32])
    s_ck = dout("s_ck", [DEPTH, DEC_SEQ, 512]); s_cv = dout("s_cv", [DEPTH, DEC_SEQ, 512])
    hs1 = dscr("hs1", [DEC_SEQ, D_MODEL], F32)
    MBS = [dscr(f"mbs{i}", [128, SEQ]) for i in range(4)]
    SS_KA = dscr("ss_ka", [DEPTH, 8, 64, 1152]); SS_KC = dscr("ss_kc", [DEPTH, 8, 64, 640])
    SS_KB = dscr("ss_kb", [DEPTH, 2, 64, 1152]); SS_IK = dscr("ss_ik", [DEPTH, 64, 1152])
    SS_VA = dscr("ss_va", [DEPTH, 1152, 512]); SS_VC = dscr("ss_vc", [DEPTH, 640, 512]); SS_VB = dscr("ss_vb", [DEPTH, 1152, 128])

    sb = lambda name, shape, dt: es.enter_context(nc.sbuf_tensor(name, list(shape), dt))
    wring = [sb(f"wring{i}", [128, 8, 512], BF16) for i in range(2)]
    SC = sb("SC", [128, 8192], F32)
    junk = sb("junk", [128, 8192], BF16)
    hT = sb("hT", [128, 8, 512], BF16)
    Q = sb("Q", [65, 8, 512], BF16)
    zg = {t: sb("zg" + t, [64, 8, 512], BF16) for t in "abc"}
    iqT = sb("iqT", [64, 4, 512], BF16)
    kst = sb("kst", [64, 8, 512], BF16)
    xt = [sb(f"xt{i}", [128, 1024], F32) for i in range(2)]
    xn = sb("xn", [128, 1024], BF16)
    st = [sb(f"st{i}", [128, 512], F32) for i in range(2)]
    vst = [sb(f"vst{i}", [128, 512], BF16) for i in range(2)]
    kbuf = [sb(f"kbuf{i}", [65, 4, 512], BF16) for i in range(2)]
    vbuf = [sb(f"vbuf{i}", [128, 4, 4, 65], BF16) for i in range(2)]
    Pt = [sb(f"Pt{i}", [128, 512], BF16) for i in range(4)]
    Mb = [sb(f"Mb{i}", [128, 512], BF16) for i in range(2)]
    Rr = [sb(f"Rr{i}", [128, 512], F32) for i in range(3)]
    ikbuf = [sb(f"ikbuf{i}", [64, 512], BF16) for i in range(2)]
    b5 = sb("b5", [128, 2, 8, 128], BF16)
    bc = sb("bc", [128, 2, 8, 128], BF16)
    cstf = sb("cstf", [128, 8, 128], F32)
    identb = sb("identb", [128, 128], BF16)
    i4b = sb("i4b", [128, 4, 128], BF16)
    ma0b = sb("ma0b", [128, 128], BF16); cm0b = sb("cm0b", [128, 128], BF16); cm4b = sb("cm4b", [128, 128], BF16)
    cstore = sb("cstore", [128, 64, 8], F32)
    nbias = sb("nbias", [128, 64, 8], F32)
    gcol = sb("gcol_s", [128, DEPTH * 8], F32)
    fgb = sb("fgb", [128, D_MODEL], F32)
    bfb = sb("bfb_s", [128, DEPTH * 8], F32)
    small = sb("small", [128, 64], F32)
    cntb = sb("cntb", [128, NBIS], F32)
    wabs = sb("wabs", [128, 4, 8], F32); wsgn = sb("wsgn", [128, 4, 8], F32)
    lfb = sb("lfb", [128, 8], F32)
    tot = sb("tot", [1, 8], F32)
    tots = sb("tots", [1, 17, 8], F32)
    totbc = sb("totbc", [128, 8], F32)
    lf4 = sb("lf4", [128, 4, 8], F32)
    cown = sb("cown", [128, 4, 8], F32)
    xacc = sb("xacc", [128, D_MODEL], F32)
    pcore = sb("pcore_s", [128, 1024], F32)
    iota5 = sb("iota5_s", [128, 512], F32)
    comb = [sb(f"comb{i}", [128, 4, 128], BF16) for i in range(2)]
    ones1 = sb("ones1", [65, 128], F32)
    cbc = sb("cbc", [128, 8], F32)
    rq = sb("rq", [128, 4, 8], F32)
    rT = sb("rT", [8, 512], BF16)
    rden = sb("rden", [65, 512], F32)
    otmp = sb("otmp", [64, 512], F32)
    hank = sb("hank", [128, 128], F32)
    t5s = sb("t5s", [32, 8], F32); oh5s = sb("oh5s", [32, LTAB], F32)
    crs = sb("crs", [128, 3, 8], F32); ohcs = sb("ohcs", [128, 3, LTAB], F32)
    tabs = sb("tabs", [8, LTAB], F32)
    ps = [es.enter_context(nc.psum_tensor(f"ps{i}", [128, 512], F32)) for i in range(8)]
    psn = [f"ps{i}" for i in range(8)]

    P = Prog(nc)
    _early = {}

    def nxt_early(key, n):
        v = _early.get(key, 0)
        _early[key] = v + 1
        return v % n

    IDENT = cstf[:, 0, :]; JM = cstf[:, 1, :]; TRI = cstf[:, 2, :]; E0ROW = cstf[:, 3, :]
    ADM = cstf[:, 7, :]
    E127 = cstf[:, 1, 0:1]

    P.dma("sp", pcore[:], pc_d, w=["qrelb", "krel", "qlimc", "sel01", "selb", "pvb"])
    P.dma("sp", iota5[:], iota_d, w=["iota5"])
    qrelb = pcore[:, 0:512]; krel = pcore[:, 512:528]; qlimc = pcore[:, 528:544]; sel01 = pcore[:, 544:680]
    selb = pcore[:, 680:684]; pvb = pcore[:, 684:685]
    P.dma("sp", cstf[:].rearrange("p a b -> p (a b)"), cst_d, w=["cstf"])
    P.dma("sp", gcol[:], gcol_d, w=["gcol"])
    P.dma("sp", fgb[:], fg_d.to_broadcast([128, D_MODEL]) if hasattr(fg_d, "to_broadcast") else bass.AP(fg_d.tensor, 0, [[0, 128], [1, D_MODEL]]), w=["fgb"])
    P.dma("sp", bfb[:], bass.AP(bf_d.tensor, 0, [[0, 128], [1, DEPTH * 8]]), w=["bfb"])
    P.dma("sp", t5s[:], t5_d, w=["t5s"])
    P.dma("sp", oh5s[:], oh5_d, w=["oh5s"])
    P.dma("sp", ohcs[:], ohc_d.rearrange("c p l -> p c l"), w=["ohcs"])
    P.dve(lambda e: e.tensor_copy(identb[:], IDENT), r=["cstf"], w=["identb"])
    for k in range(4):
        P.dve(lambda e, k=k: e.tensor_copy(i4b[:, k, :], IDENT), r=["cstf"], w=["i4b"])
    P.dve(lambda e: e.tensor_copy(ma0b[:], cstf[:, 4, :]), r=["cstf"], w=["ma0b"])
    P.dve(lambda e: e.tensor_copy(cm0b[:], cstf[:, 5, :]), r=["cstf"], w=["cm0b"])
    P.dve(lambda e: e.tensor_copy(cm4b[:], cstf[:, 6, :]), r=["cstf"], w=["cm4b"])
    onesf = cstf[:, 4, :]
    P.dve(lambda e: e.memset(onesf, 1.0), r=["ma0b"], w=["cstf", "onesf"])
    P.dve(lambda e: e.memset(ones1[:], 1.0), w=["ones1"])
    for i in range(2):
        P.pool(lambda e, i=i: e.memset(kbuf[i][:], 1.0), w=[f"kbuf{i}"])
        P.pool(lambda e, i=i: e.memset(vbuf[i][:], 1.0), w=[f"vbuf{i}"])

    stg = SC[:, 0:4096].rearrange("p (c n) -> p c n", c=8)
    stgb = junk[:, 0:4096].rearrange("p (c n) -> p c n", c=8)
    for l in range(nlayers):
        for u in range(NU):
            P.dma("sp", stg, wu[l, u], w=["SC"])
            P.act(lambda e: e.activation(stgb, stg, AF.Copy), r=["SC"], w=["junk"])
            P.dma("sp", wub[l, u], stgb, r=["junk"], w=[f"wub{l}_{u}"])
        for u in range(6):
            so = SC[0:64, 0:4096].rearrange("p (c n) -> p c n", c=4)
            sob = junk[0:64, 0:4096].rearrange("p (c n) -> p c n", c=4)
            P.dma("sp", so, wo[l, u], w=["SC"])
            P.act(lambda e, so=so, sob=sob: e.activation(sob, so, AF.Copy), r=["SC"], w=["junk"])
            P.dma("sp", wob[l, u], sob, r=["junk"], w=[f"wob{l}_{u}"])

    def build_tab(lhs_list, rhs_list, dst, rnames):
        for i, (a, b) in enumerate(zip(lhs_list, rhs_list)):
            P.pe(lambda e, a=a, b=b, i=i: e.matmul(ps[0][0:8, 0:LTAB], a, b, start=(i == 0), stop=(i == len(lhs_list) - 1)),
                 r=rnames, w=["ps0"])
        P.dve(lambda e: e.tensor_copy(tabs[:], ps[0][0:8, 0:LTAB]), r=["ps0"], w=["tabs"])
        P.dma("sp", dst, tabs[:], r=["tabs"], w=["tabdram"])

    def build_toeplitz(tab_ap2d, dst_tile):
        for k in range(2):
            for h in range(8):
                b0 = 128 * k
                src = bass.AP(tab_ap2d.tensor, tab_ap2d.offset + h * LTAB + b0, [[1, 128], [1, 128]])
                P.dma("sp", hank[:], src, r=["tabdram"], w=["hank"])
                P.pe(lambda e: e.matmul(ps[1][:, 0:128], JM, hank[:], start=True, stop=True), r=["hank", "cstf"], w=["ps1"])
                P.dve(lambda e, k=k, h=h: e.tensor_copy(dst_tile[:, k, h, :], ps[1][:, 0:128]), r=["ps1"], w=["btile"])

    build_tab([t5s[:]], [oh5s[:]], tab5, ["t5s", "oh5s"])
    build_toeplitz(tab5, b5)
    for qb in range(4):
        for rr in range(17):
            for jj in range(2):
                ci = nxt_early("cmb", 2)
                sc0 = sel01[:, (qb * 17 + rr) * 2:(qb * 17 + rr) * 2 + 1]
                sc1 = sel01[:, (qb * 17 + rr) * 2 + 1:(qb * 17 + rr) * 2 + 2]
                P.dve(lambda e: e.tensor_scalar(comb[ci][:, :, :], b5[:, 0, 4 * jj:4 * jj + 4, :], sc0, None, ALU.mult), r=["btile", "sel01"], w=[f"comb{ci}"])
                P.dve(lambda e: e.scalar_tensor_tensor(comb[ci][:, :, :], b5[:, 1, 4 * jj:4 * jj + 4, :], sc1, comb[ci][:, :, :], ALU.mult, ALU.add),
                      r=["btile", "sel01", f"comb{ci}"], w=[f"comb{ci}"])
                P.dma("pool", COMBS[qb, rr, jj], comb[ci][:].rearrange("p a b -> p (a b)"), r=[f"comb{ci}"], w=["~combs"])
    for rr in range(16):
        mi = nxt_early("mb", 2)
        P.dve(lambda e: e.tensor_scalar(Mb[mi][:, :], qrelb[:, :], krel[:, rr:rr + 1], NEG, ALU.is_lt, ALU.mult), r=["qrelb", "krel"], w=[f"Mb{mi}"])
        P.dma("pool", AMASK[rr], Mb[mi][:, :], r=[f"Mb{mi}"], w=["~amask"])

    wk = [0]

    def load_w(l, u):
        i = wk[0] % 2
        wk[0] += 1
        P.dma("sp", wring[i][:], wub[l, u], r=[f"wub{l}_{u}"], w=[f"wring{i}"])
        return wring[i], f"wring{i}"

    def load_wo(l, u):
        i = wk[0] % 2
        wk[0] += 1
        dst = wring[i][0:64].rearrange("p c n -> p (c n)").rearrange("p (c n) -> p c n", c=4)
        P.dma("sp", dst, wob[l, u], r=[f"wob{l}_{u}"], w=[f"wring{i}"])
        return dst, f"wring{i}"

    rot = {"s": 0, "pt": 0, "kv": 0, "st": 0, "x": 0, "ik": 0, "rr": 0, "mb": 0, "ips": 0, "cmb": 0}

    def nxt(key, n):
        v = rot[key] % n
        rot[key] += 1
        return v

    def norm_block(l, xsrc_ap, tb, nrow=128, rname=None, sb_src=None):
        if sb_src is not None:
            X, xname = sb_src
        else:
            xi = nxt("x", 2)
            X = xt[xi]; xname = f"xt{xi}"
        if sb_src is None:
            P.dma("sp", X[0:nrow, :], xsrc_ap, r=(list(rname) if isinstance(rname, (list, tuple)) else ([rname] if rname else [])), w=[xname])
        P.act(lambda e: e.activation(junk[0:nrow, 0:1024], X[0:nrow, :], AF.Square, accum_out=small[0:nrow, 0:1]),
              r=[xname], w=["junk", "small0"])
        P.dve(lambda e: e.tensor_scalar(small[0:nrow, 1:2], small[0:nrow, 0:1], 1.0 / D_MODEL, EPS, ALU.mult, ALU.add), r=["small0"], w=["small1"])
        P.act(lambda e: e.activation(small[0:nrow, 2:3], small[0:nrow, 1:2], AF.Sqrt), r=["small1"], w=["small2"])
        P.dve(lambda e: e.reciprocal(small[0:nrow, 3:4], small[0:nrow, 2:3]), r=["small2"], w=["small3"])
        P.dve(lambda e: e.tensor_scalar(xn[0:nrow, :], X[0:nrow, :], small[0:nrow, 3:4], None, ALU.mult), r=[xname, "small3"], w=["xn"])
        psb = ps[7].bitcast(BF16)
        for c in range(8):
            P.pe(lambda e, c=c: e.transpose(psb[:, c * 128:c * 128 + nrow], xn[0:nrow, c * 128:(c + 1) * 128], identb[0:nrow, 0:nrow]),
                 r=["xn", "identb"], w=["ps7"])
        for c in range(8):
            P.dve(lambda e, c=c: e.tensor_scalar(hT[:, c, tb * 128:tb * 128 + nrow], psb[:, c * 128:c * 128 + nrow],
                                                 gcol[:, l * 8 + c:l * 8 + c + 1], None, ALU.mult),
                  r=["ps7", "gcol"], w=["hT"])

    def fm_unit(l, uname, ntok, evac, blocks=tuple(range(8)), wres=None):
        W, wn = wres if wres is not None else load_w(l, UNITS.index(uname))
        for j in blocks:
            si = nxt("s", 4)
            for c in range(8):
                P.pe(lambda e, j=j, c=c, si=si: e.matmul(ps[si][0:64, 0:ntok], W[:, c, j * 64:(j + 1) * 64], hT[:, c, 0:ntok],
                                                        start=(c == 0), stop=(c == 7)), r=[wn, "hT"], w=[psn[si]])
            evac(j, ps[si], psn[si])

    def finish_head(O, oname, zt, zname, hsel, ncol, csl):
        P.dve(lambda e: e.reciprocal(rden[64:65, 0:ncol], O[64:65, 0:ncol]), r=[oname], w=["rden"])
        bi_ = nxt("s", 4)
        P.pe(lambda e: e.matmul(ps[bi_][0:64, 0:ncol], ones1[64:65, 0:64], rden[64:65, 0:ncol], start=True, stop=True),
             r=["rden", "ones1"], w=[psn[bi_]])
        P.dve(lambda e: e.tensor_copy(otmp[:, 0:ncol], ps[bi_][0:64, 0:ncol]), r=[psn[bi_]], w=["otmp"])
        P.dve(lambda e: e.tensor_tensor(otmp[:, 0:ncol], O[0:64, 0:ncol], otmp[:, 0:ncol], ALU.mult), r=[oname, "otmp"], w=["otmp"])
        if isinstance(hsel, tuple):
            zv = zt[:, hsel[0]:hsel[1], csl]
            ov = otmp[:, 0:ncol].rearrange("p (h q) -> p h q", h=hsel[1] - hsel[0])
        else:
            zv = zt[:, hsel, csl]
            ov = otmp[:, 0:ncol]
        P.dve(lambda e: e.tensor_tensor(zv, zv, ov, ALU.mult), r=[zname, "otmp"], w=[zname])

    def load_kv(Ksrc, Vsrc, h0, nh, k0, nk, krows_name):
        i = nxt("kv", 2)
        P.dma("sp", kbuf[i][0:64, 0:nh, 0:nk], Ksrc[h0:h0 + nh, :, k0:k0 + nk].rearrange("h d k -> d h k"), r=[krows_name], w=[f"kbuf{i}"])
        nb = (nk + 127) // 128
        for b in range(nb):
            n = min(128, nk - b * 128)
            P.dma("sp", vbuf[i][0:n, b, 0:nh, 0:64],
                  Vsrc[k0 + b * 128:k0 + b * 128 + n, h0 * 64:(h0 + nh) * 64].rearrange("k (h d) -> k h d", h=nh),
                  r=[krows_name], w=[f"vbuf{i}"])
        return i

    class Attn:
        def __init__(self):
            self.pend = []

        def tile(self, t):
            si = nxt("s", 4)
            S = ps[si]; n = t["n"]; qlo, qhi = t["qlo"], t["qhi"]
            nadd = len(t["adds"])
            P.pe(lambda e: e.matmul(S[0:n, qlo:qhi], t["kT"], t["qap"], start=True, stop=(nadd == 0)),
                 r=t["names"] + ["Q"], w=[psn[si]])
            for ai, (clo, chi, la, ra, an) in enumerate(t["adds"]):
                P.pe(lambda e, clo=clo, chi=chi, la=la, ra=ra, ai=ai: e.matmul(S[0:n, clo:chi], la, ra, start=False, stop=(ai == nadd - 1)),
                     r=an, w=[psn[si]])
            pi = nxt("pt", 4)
            if t["bias"] is not None:
                P.act(lambda e: e.activation(Pt[pi][0:n, qlo:qhi], S[0:n, qlo:qhi], AF.Exp, bias=t["bias"]),
                      r=[psn[si], "nbias"], w=[f"Pt{pi}"])
            else:
                P.act(lambda e: e.activation(Pt[pi][0:n, qlo:qhi], S[0:n, qlo:qhi], AF.Exp), r=[psn[si]], w=[f"Pt{pi}"])
            t["pi"] = pi
            self.pend.append(t)
            if len(self.pend) > 2:
                self.pv(self.pend.pop(0))

        def pv(self, t):
            n = t["n"]; qlo, qhi = t["qlo"], t["qhi"]; pi = t["pi"]; O = t["O"]
            P.pe(lambda e: e.matmul(O[0:65, qlo:qhi], t["v"], Pt[pi][0:n, qlo:qhi], start=t["first"], stop=t["last"]),
                 r=[f"Pt{pi}"] + t["names"], w=[t["oname"]])

        def flush(self):
            while self.pend:
                self.pv(self.pend.pop(0))


    for l in range(nlayers):
        P.pool(lambda e: e.memset(crs[:], 0.0), w=["crs"])
        P.dma("sp", crs[:], crel_d[l].rearrange("c p h -> p c h"), w=["crs"])
        build_tab([crs[:, c, :] for c in range(3)], [ohcs[:, c, :] for c in range(3)], tabc[l], ["crs", "ohcs"])
        build_toeplitz(tabc[l], bc)
        P.dve(lambda e: e.memset(tot[:], 0.0), w=["tot"])
        P.dve(lambda e: e.memset(tots[:], 0.0), w=["tots"])
        P.dve(lambda e: e.memset(totbc[:], 0.0), w=["totbc"])
        KAl, KBl, IKl, VAl, VBl = S_KA[l], S_KB[l], S_IK[l], S_VA[l], S_VB[l]
        KCl, VCl = S_KCL[l], S_VCL[l]
        hist = f"~hist{l}"
        chist = f"~chist{l}"

        def grow(gp):
            if l == 0:
                return xp[gp * 512:(gp + 1) * 512, :]
            return hpg[gp * 512:(gp + 1) * 512, :]

        def tm_unit(uname, handler, wres=None):
            W, wn = wres if wres is not None else load_w(l, UNITS.index(uname))
            for tb in range(4):
                si = nxt("s", 4)
                for c in range(8):
                    P.pe(lambda e: e.matmul(ps[si][:, :], hT[:, c, tb * 128:(tb + 1) * 128], W[:, c, :], start=(c == 0), stop=(c == 7)),
                         r=[wn, "hT"], w=[psn[si]])
                k = nxt("st", 2)
                S_ = st[k]; sn = f"st{k}"
                P.act(lambda e: e.activation(S_[:], ps[si][:], AF.Copy), r=[psn[si]], w=[sn])
                handler(tb, S_, sn, k)

        def logf_of(S_, sn):
            P.dve(lambda e: e.tensor_tensor(lfb[:], S_[:, 288:296], bfb[:, l * 8:(l + 1) * 8], ALU.add), r=[sn, "bfb"], w=["lfb"])
            P.act(lambda e: e.activation(lfb[:], lfb[:], AF.Exp, scale=-1.0), r=["lfb"], w=["lfb"])
            P.act(lambda e: e.activation(lfb[:], lfb[:], AF.Ln, bias=1.0), r=["lfb"], w=["lfb"])
            P.dve(lambda e: e.tensor_scalar(lfb[:], lfb[:], -1.0, None, ALU.mult), r=["lfb"], w=["lfb"])

        def cum_into(dst_ap, dname):
            P.pe(lambda e: e.matmul(ps[5][:, 0:8], TRI, lfb[:], start=True, stop=False), r=["lfb", "cstf"], w=["ps5"])
            P.pe(lambda e: e.matmul(ps[5][:, 0:8], ones1[0:1, 0:128], tot[0:1, :], start=False, stop=True), r=["tot", "ones1"], w=["ps5"])
            P.dve(lambda e: e.tensor_copy(dst_ap, ps[5][:, 0:8]), r=["ps5"], w=[dname])
            P.pe(lambda e: e.matmul(ps[5][0:1, 8:16], E127, dst_ap, start=True, stop=True), r=[dname, "cstf"], w=["ps5"])
            P.dve(lambda e: e.tensor_copy(tot[:], ps[5][0:1, 8:16]), r=["ps5"], w=["tot"])

        def evac_k_to(dst3, k0, hname):
            def f(j, pt, pn):
                P.act(lambda e: e.activation(kst[:, j, :], pt[0:64, :], AF.Copy), r=[pn], w=["kst"])
                if j == 7:
                    P.dma("pool", dst3[:, :, k0:k0 + 512].rearrange("h d k -> d h k"), kst[:], r=["kst"], w=[hname])
            return f

        def evac_q(scale):
            def f(j, pt, pn):
                P.act(lambda e: e.activation(Q[0:64, j, :], pt[0:64, :], AF.Copy, scale=scale), r=[pn], w=["Q"])
            return f

        def evac_z(zt, zn):
            def f(j, pt, pn):
                P.act(lambda e: e.activation(zt[:, j, :], pt[0:64, :], AF.Silu), r=[pn], w=[zn])
            return f

        scb = SC.bitcast(BF16)
        kres = {}
        for ui, un in enumerate(("tka", "tva", "tb", "ka")):
            v = scb[:, ui * 4096:(ui + 1) * 4096].rearrange("p (c n) -> p c n", c=8)
            P.dma("sp", v, wub[l, UNITS.index(un)], r=[f"wub{l}_{UNITS.index(un)}"], w=["SC"])
            kres[un] = (v, "SC")
        v = junk[:, 4096:8192].rearrange("p (c n) -> p c n", c=8)
        P.dma("sp", v, wub[l, UNITS.index("bx")], r=[f"wub{l}_{UNITS.index('bx')}"], w=["junk", "junkW"])
        kres["bx"] = (v, "junkW")
        for gp in range(4 * nm):
            t0 = gp * 512
            src = grow(gp)
            for tb in range(4):
                norm_block(l, src[tb * 128:(tb + 1) * 128, :], tb, rname=("~hpgw" if l > 0 else None))

            def h_kv(uname):
                def f(tb, S_, sn, k):
                    r0 = t0 + tb * 128
                    dst = {"tka": o_ak, "tva": o_av, "tkc": o_ck, "tvc": o_cv}[uname]
                    if uname in ("tka", "tva"):
                        P.dma("pool", dst[l, r0:r0 + 128, :], S_[:], r=[sn])
                    else:
                        P.dma("pool", dst[l, r0 - (SEQ - 512):r0 - (SEQ - 512) + 128, :], S_[:], r=[sn])
                    if uname == "tva":
                        V_, vn = vst[k], f"vst{k}"
                        P.dve(lambda e: e.tensor_copy(V_[:], S_[:]), r=[sn], w=[vn])
                        P.dma("pool", VAl[r0:r0 + 128, :], V_[:], r=[vn], w=[hist])
                return f

            def h_tb(tb, S_, sn, k):
                r0 = t0 + tb * 128
                P.dma("pool", o_bk[l, r0:r0 + 128, :], S_[:, 0:128], r=[sn])
                P.dma("pool", o_bv[l, r0:r0 + 128, :], S_[:, 128:256], r=[sn])
                P.dma("pool", o_ik[l, r0:r0 + 128, :], S_[:, 256:288], r=[sn])
                V_, vn = vst[k], f"vst{k}"
                P.dve(lambda e: e.tensor_copy(V_[:, 0:128], S_[:, 128:256]), r=[sn], w=[vn])
                P.dma("pool", VBl[r0:r0 + 128, :], V_[:, 0:128], r=[vn], w=[hist])
                P.dve(lambda e: e.tensor_tensor(lf4[:, tb, :], S_[:, 288:296], bfb[:, l * 8:(l + 1) * 8], ALU.add), r=[sn, "bfb"], w=["lf4"])

            tm_unit("tka", h_kv("tka"), wres=kres["tka"])
            tm_unit("tva", h_kv("tva"), wres=kres["tva"])
            tm_unit("tb", h_tb, wres=kres["tb"])
            lf4f = lf4[:].rearrange("p b h -> p (b h)")
            P.act(lambda e: e.activation(lf4f, lf4f, AF.Exp, scale=-1.0), r=["lf4"], w=["lf4"])
            P.act(lambda e: e.activation(lf4f, lf4f, AF.Ln, bias=1.0), r=["lf4"], w=["lf4"])
            P.dve(lambda e: e.tensor_scalar(lf4f, lf4f, -1.0, None, ALU.mult), r=["lf4"], w=["lf4"])
            P.dma("pool", o_lf[l, t0:t0 + 512, :].rearrange("(b p) h -> p b h", p=128), lf4[:], r=["lf4"])
            if gp == NG - 1:
                tm_unit("tkc", h_kv("tkc"))
                tm_unit("tvc", h_kv("tvc"))
            fm_unit(l, "ka", 512, evac_k_to(KAl, t0, hist), wres=kres["ka"])
            for b_ in range(4):
                for b2 in range(b_ + 1):
                    P.pe(lambda e: e.matmul(ps[5][:, b_ * 8:(b_ + 1) * 8], (TRI if b2 == b_ else onesf[:, :]), lf4[:, b2, :], start=(b2 == 0), stop=(b2 == b_)),
                         r=["lf4", "cstf", "onesf"], w=["ps5"])
            for b2 in range(4):
                P.pe(lambda e: e.matmul(ps[5][:, 32:40], onesf[:, :], lf4[:, b2, :], start=(b2 == 0), stop=(b2 == 3)), r=["lf4", "onesf"], w=["ps5"])
            for b_ in range(4):
                P.dve(lambda e: e.tensor_tensor(cstore[:, 4 * gp + b_, :], ps[5][:, b_ * 8:(b_ + 1) * 8], totbc[:, :], ALU.add), r=["ps5", "totbc"], w=["cstore"])
            P.dve(lambda e: e.tensor_tensor(totbc[:, :], ps[5][:, 32:40], totbc[:, :], ALU.add), r=["ps5", "totbc"], w=["totbc"])
            P.dve(lambda e: e.tensor_copy(tots[0:1, gp + 1, :], totbc[0:1, :]), r=["totbc"], w=["tots"])

            def evac_bx_k(j, pt, pn):
                if j < 2:
                    P.act(lambda e: e.activation(kst[:, j, :], pt[0:64, :], AF.Copy), r=[pn], w=["kst"])
                    if j == 1:
                        P.dma("pool", KBl[:, :, t0:t0 + 512].rearrange("h d k -> d h k"), kst[:, 0:2, :], r=["kst"], w=[hist])
                elif j == 6:
                    P.act(lambda e: e.activation(kst[:, 2, :], pt[0:64, :], AF.Copy), r=[pn], w=["kst"])
                    P.dma("pool", IKl[:, t0:t0 + 512], kst[:, 2, :], r=["kst"], w=[hist])
            fm_unit(l, "bx", 512, evac_bx_k, blocks=(0, 1, 6), wres=kres["bx"])

        for m in range(nm):
            own = (xq[m] if l == 0 else hp1q[m * 512:(m + 1) * 512, :])
            own_r = (None if l == 0 else [f"hp1qc{2 * m}", f"hp1qc{2 * m + 1}"])
            for part in range(2):
                for tb in range(4):
                    if part == 1:
                        norm_block(l, own[tb * 128:(tb + 1) * 128, :], tb, rname=own_r)
                    elif l == 0:
                        norm_block(l, xprev[m][tb * 128:(tb + 1) * 128, :], tb)
                    else:
                        first = True
                        for r in range(4):
                            gq = 4 * m - 1 + r
                            if gq < 0:
                                continue
                            xi = nxt("x", 2)
                            X = xt[xi]; xname = f"xt{xi}"
                            P.dma("sp", X[:], grow(gq)[tb * 128:(tb + 1) * 128, :], r=["~hpgw"], w=[xname])
                            if first:
                                P.dve(lambda e: e.tensor_scalar(xacc[:], X[:], selb[:, r:r + 1], None, ALU.mult), r=[xname, "selb"], w=["xacc"])
                            else:
                                P.dve(lambda e: e.scalar_tensor_tensor(xacc[:], X[:], selb[:, r:r + 1], xacc[:], ALU.mult, ALU.add), r=[xname, "selb", "xacc"], w=["xacc"])
                            first = False
                        norm_block(l, None, tb, sb_src=(xacc, "xacc"))

                def h_c(uname):
                    def f(tb, S_, sn, k):
                        if uname == "tvc":
                            V_, vn = vst[k], f"vst{k}"
                            P.dve(lambda e: e.tensor_copy(V_[:], S_[:]), r=[sn], w=[vn])
                            P.dma("pool", VCl[part * 512 + tb * 128:part * 512 + (tb + 1) * 128, :], V_[:], r=[vn], w=[chist])
                    return f
                tm_unit("tvc", h_c("tvc"))
                fm_unit(l, "kc", 512, evac_k_to(KCl, part * 512, chist))
            g0 = 16 * m
            nkb = 16 * m + 16
            for r in range(4):
                if r == 0:
                    P.dve(lambda e: e.tensor_scalar(tot[0:1, :], tots[0:1, 4 * m + r, :], selb[0:1, r:r + 1], None, ALU.mult), r=["tots", "selb"], w=["tot"])
                else:
                    P.dve(lambda e: e.scalar_tensor_tensor(tot[0:1, :], tots[0:1, 4 * m + r, :], selb[0:1, r:r + 1], tot[0:1, :], ALU.mult, ALU.add),
                          r=["tots", "selb", "tot"], w=["tot"])

            def h_own(tb, S_, sn, k):
                P.dve(lambda e: e.tensor_tensor(lf4[:, tb, :], S_[:, 288:296], bfb[:, l * 8:(l + 1) * 8], ALU.add), r=[sn, "bfb"], w=["lf4"])
                P.dve(lambda e: e.tensor_scalar(wsgn[:, tb, :], S_[:, 296:304], 0.0, 2.0, ALU.is_ge, ALU.mult), r=[sn], w=["wsgn"])
                P.dve(lambda e: e.tensor_scalar(wsgn[:, tb, :], wsgn[:, tb, :], -1.0, None, ALU.add), r=["wsgn"], w=["wsgn"])
                P.dve(lambda e: e.scalar_tensor_tensor(wabs[:, tb, :], S_[:, 296:304], IDXS, wsgn[:, tb, :], ALU.mult, ALU.mult), r=[sn, "wsgn"], w=["wabs"])
            tm_unit("tb", h_own)
            lf4q = lf4[:].rearrange("p b h -> p (b h)")
            P.act(lambda e: e.activation(lf4q, lf4q, AF.Exp, scale=-1.0), r=["lf4"], w=["lf4"])
            P.act(lambda e: e.activation(lf4q, lf4q, AF.Ln, bias=1.0), r=["lf4"], w=["lf4"])
            P.dve(lambda e: e.tensor_scalar(lf4q, lf4q, -1.0, None, ALU.mult), r=["lf4"], w=["lf4"])
            for b_ in range(4):
                P.pe(lambda e: e.matmul(ps[5][:, b_ * 8:(b_ + 1) * 8], ones1[0:1, 0:128], tot[0:1, :], start=True, stop=False), r=["tot", "ones1"], w=["ps5"])
                for b2 in range(b_ + 1):
                    P.pe(lambda e: e.matmul(ps[5][:, b_ * 8:(b_ + 1) * 8], (TRI if b2 == b_ else onesf[:, :]), lf4[:, b2, :], start=False, stop=(b2 == b_)),
                         r=["lf4", "cstf", "onesf"], w=["ps5"])
            P.dve(lambda e: e.tensor_copy(cown[:].rearrange("p b h -> p (b h)"), ps[5][:, 0:32]), r=["ps5"], w=["cown"])
            P.pe(lambda e: e.matmul(ps[5][:, 16:24], E0ROW, cstore[:, g0, :], start=True, stop=True), r=["cstore", "cstf"], w=["ps5"])
            P.dve(lambda e: e.tensor_copy(cbc[:], ps[5][:, 16:24]), r=["ps5"], w=["cbc"])
            for h in range(8):
                P.dve(lambda e: e.tensor_scalar(nbias[:, 0:nkb, h], cstore[:, 0:nkb, h], cbc[:, h:h + 1], -1.0, ALU.subtract, ALU.mult),
                      r=["cstore", "cbc"], w=["nbias"])
            for tb in range(4):
                P.dve(lambda e: e.tensor_tensor(rq[:, tb, :], cown[:, tb, :], cbc[:], ALU.subtract), r=["cown", "cbc"], w=["rq"])
                P.pe(lambda e: e.transpose(ps[6][0:8, tb * 128:(tb + 1) * 128], rq[:, tb, :], IDENT), r=["rq", "cstf"], w=["ps6"])
            P.dve(lambda e: e.tensor_copy(rT[:], ps[6][0:8, :]), r=["ps6"], w=["rT"])

            def evac_bx_q(j, pt, pn):
                P.act(lambda e: e.activation(iqT[:, j - 2, :], pt[0:64, :], AF.Copy), r=[pn], w=["iqT"])

            def a_half(half):
                at = Attn()
                for sbk in range(4 * m + 4):
                    bi = load_kv(KAl, VAl, half * 4, 4, sbk * 512, 512, hist)
                    trail = (sbk >= 4 * m)
                    for kb in range(4):
                        adds = []
                        if trail:
                            rr = (sbk - 4 * m) * 4 + kb
                            mi = nxt("mb", 2)
                            P.dma("sp", Mb[mi][:, :], AMASK[rr], r=["~amask"], w=[f"Mb{mi}"])
                            adds = [(0, 512, identb[:], Mb[mi][:, :], ["identb", f"Mb{mi}"])]
                        for i in range(4):
                            hh = half * 4 + i
                            at.tile(dict(kT=kbuf[bi][0:65, i, kb * 128:(kb + 1) * 128], v=vbuf[bi][:, kb, i, 0:65], n=128, qlo=0, qhi=512,
                                         qap=Q[0:65, hh, 0:512], adds=adds, bias=nbias[:, 4 * sbk + kb, hh:hh + 1],
                                         names=[f"kbuf{bi}", f"vbuf{bi}"], O=ps[4 + i], oname=psn[4 + i],
                                         first=(sbk == 0 and kb == 0), last=(sbk == 4 * m + 3 and kb == 3)))
                at.flush()
                for i in range(4):
                    finish_head(ps[4 + i], psn[4 + i], zg["a"], "zga", half * 4 + i, 512, slice(0, 512))

            def c_half(half):
                at = Attn()
                started = [False] * 4
                for sbl in range(2):
                    bi = load_kv(KCl, VCl, half * 4, 4, sbl * 512, 512, chist)
                    for kb in range(4):
                        r_ = 4 * sbl + kb
                        qb_lo, qb_hi = max(0, r_ - 4), min(3, r_)
                        qlo, qhi = qb_lo * 128, (qb_hi + 1) * 128
                        for i in range(4):
                            hh = half * 4 + i
                            adds = []
                            for qb in range(qb_lo, qb_hi + 1):
                                dl = r_ - 4 - qb
                                c0, c1 = qb * 128, (qb + 1) * 128
                                if dl == 0:
                                    adds.append((c0, c1, identb[:], bc[:, 0, hh, :], ["identb", "btile"]))
                                    adds.append((c0, c1, identb[:], cm0b[:], ["identb", "cm0b"]))
                                elif dl == -1:
                                    adds.append((c0, c1, identb[:], bc[:, 1, hh, :], ["identb", "btile"]))
                                elif dl == -4:
                                    adds.append((c0, c1, identb[:], cm4b[:], ["identb", "cm4b"]))
                            at.tile(dict(kT=kbuf[bi][0:64, i, kb * 128:(kb + 1) * 128], v=vbuf[bi][:, kb, i, 0:65], n=128, qlo=qlo, qhi=qhi,
                                         qap=Q[0:64, hh, qlo:qhi], adds=adds, bias=(pvb[:, 0:1] if (m == 0 and sbl == 0) else None),
                                         names=[f"kbuf{bi}", f"vbuf{bi}"], O=ps[4 + i], oname=psn[4 + i],
                                         first=(not started[i]), last=(sbl == 1 and kb == 3)))
                            started[i] = True
                at.flush()
                for i in range(4):
                    finish_head(ps[4 + i], psn[4 + i], zg["c"], "zgc", half * 4 + i, 512, slice(0, 512))

            def b_topk(qb):
                NK = (16 * m + 13 + qb) * 128
                qs = slice(qb * 128, (qb + 1) * 128)
                for k0 in range(0, NK, 512):
                    nk = min(512, NK - k0)
                    ii = nxt("ik", 2)
                    P.dma("sp", ikbuf[ii][:, 0:nk], IKl[:, k0:k0 + nk], r=[hist], w=[f"ikbuf{ii}"])
                    for h in range(8):
                        base = 32 * (h % 2)
                        pi_ = 1 + nxt("ips", 3)
                        P.pe(lambda e: e.matmul(ps[pi_][:, 0:nk], iqT[base:base + 32, h // 2, qs], ikbuf[ii][base:base + 32, 0:nk], start=True, stop=True),
                             r=["iqT", f"ikbuf{ii}"], w=[psn[pi_]])
                        ri = nxt("rr", 3)
                        P.act(lambda e: e.activation(Rr[ri][:, 0:nk], ps[pi_][:, 0:nk], AF.Relu, scale=wabs[:, qb, h:h + 1]), r=[psn[pi_], "wabs"], w=[f"Rr{ri}"])
                        if h == 0:
                            P.dve(lambda e: e.tensor_scalar(SC[:, k0:k0 + nk], Rr[ri][:, 0:nk], wsgn[:, qb, 0:1], None, ALU.mult), r=[f"Rr{ri}", "wsgn"], w=["SC"])
                        else:
                            P.dve(lambda e: e.scalar_tensor_tensor(SC[:, k0:k0 + nk], Rr[ri][:, 0:nk], wsgn[:, qb, h:h + 1], SC[:, k0:k0 + nk], ALU.mult, ALU.add),
                                  r=[f"Rr{ri}", "wsgn", "SC"], w=["SC"])
                for ch in range(4):
                    c0 = g0 * 128 + ch * 512
                    wd = min(512, NK - c0)
                    if wd <= 0:
                        continue
                    ri = nxt("rr", 3)
                    P.dve(lambda e: e.tensor_scalar(Rr[ri][:, 0:wd], iota5[:, 0:wd], qlimc[:, qb * 4 + ch:qb * 4 + ch + 1], -1e30, ALU.is_ge, ALU.mult),
                          r=["iota5", "qlimc"], w=[f"Rr{ri}"])
                    P.dve(lambda e: e.tensor_tensor(SC[:, c0:c0 + wd], SC[:, c0:c0 + wd], Rr[ri][:, 0:wd], ALU.add), r=["SC", f"Rr{ri}"], w=["SC"])
                P.dve(lambda e: e.memset(cntb[:], 0.0), w=["cntb"])
                P.dve(lambda e: e.memset(small[:, 8:9], 0.0), w=["cand"])
                for it in range(NBIS):
                    stp = 64.0 * (0.5 ** it)
                    P.dve(lambda e: e.tensor_scalar(junk[:, 0:NK], SC[:, 0:NK], small[:, 8:9], 0.0, ALU.is_ge, ALU.add, accum_out=cntb[:, it:it + 1]),
                          r=["SC", "cand", "cntb"], w=["junk", "cntb"])
                    a, b_ = (stp, -0.5 * stp) if it < NBIS - 1 else (stp, -stp)
                    P.dve(lambda e: e.tensor_scalar(small[:, 9:10], cntb[:, it:it + 1], float(TOPK), a, ALU.is_ge, ALU.mult), r=["cntb"], w=["fl"])
                    P.dve(lambda e: e.scalar_tensor_tensor(small[:, 8:9], small[:, 9:10], b_, small[:, 8:9], ALU.add, ALU.add), r=["fl", "cand"], w=["cand"])
                P.dve(lambda e: e.tensor_scalar(junk[:, 0:NK], SC[:, 0:NK], small[:, 8:9], NEG, ALU.is_lt, ALU.mult), r=["SC", "cand"], w=["junk"])
                P.dma("pool", MBS[qb][:, 0:NK], junk[:, 0:NK], r=["junk"], w=[f"mbs{qb}"])

            def b_attn(qb):
                qs = slice(qb * 128, (qb + 1) * 128)
                nkq = 16 * m + 13 + qb
                at = Attn()
                for sbk in range((nkq + 3) // 4):
                    k0 = sbk * 512
                    nk = min(512, nkq * 128 - k0)
                    mi = nxt("mb", 2)
                    P.dma("sp", Mb[mi][:, 0:nk], MBS[qb][:, k0:k0 + nk], r=[f"mbs{qb}"], w=[f"Mb{mi}"])
                    bi = load_kv(KBl, VBl, 0, 2, k0, nk, hist)
                    for kb in range(nk // 128):
                        gkb = sbk * 4 + kb
                        for jj in range(2):
                            adds = [(0, 512, Mb[mi][:, kb * 128:(kb + 1) * 128], i4b[:].rearrange("p a b -> p (a b)"), [f"Mb{mi}", "i4b"])]
                            if gkb >= g0 - 1:
                                rr = gkb - g0 + 1
                                ci = nxt("cmb", 2)
                                sc0 = sel01[:, (qb * 17 + rr) * 2:(qb * 17 + rr) * 2 + 1]
                                sc1 = sel01[:, (qb * 17 + rr) * 2 + 1:(qb * 17 + rr) * 2 + 2]
                                P.dma("sp", comb[ci][:].rearrange("p a b -> p (a b)"), COMBS[qb, rr, jj], r=["~combs"], w=[f"comb{ci}"])
                                adds.append((0, 512, identb[:], comb[ci][:].rearrange("p a b -> p (a b)"), ["identb", f"comb{ci}"]))
                            at.tile(dict(kT=kbuf[bi][0:64, jj, kb * 128:(kb + 1) * 128], v=vbuf[bi][:, kb, jj, 0:65], n=128, qlo=0, qhi=512,
                                         qap=Q[0:64, 4 * jj:4 * jj + 4, qs], adds=adds, bias=None,
                                         names=[f"kbuf{bi}", f"vbuf{bi}"], O=ps[4 + jj], oname=psn[4 + jj],
                                         first=(gkb == 0), last=(gkb == nkq - 1)))
                at.flush()
                for jj in range(2):
                    finish_head(ps[4 + jj], psn[4 + jj], zg["b"], "zgb", (4 * jj, 4 * jj + 4), 512, qs)

            fm_unit(l, "bx", 512, evac_bx_q, blocks=(2, 3, 4, 5))
            b_topk(0)
            fm_unit(l, "za", 512, evac_z(zg["a"], "zga"))
            fm_unit(l, "qa", 512, evac_q(SCALE))
            for h in range(8):
                P.dma("sp", Q[64:65, h, :], rT[h:h + 1, :], r=["rT"], w=["Q"])
            a_half(0)
            b_topk(1)
            a_half(1)
            fm_unit(l, "zc", 512, evac_z(zg["c"], "zgc"))
            fm_unit(l, "qc", 512, evac_q(SCALE))
            c_half(0)
            c_half(1)
            fm_unit(l, "zb", 512, evac_z(zg["b"], "zgb"))
            fm_unit(l, "qb", 512, evac_q(SCALE))
            b_attn(0)
            b_topk(2)
            b_attn(1)
            b_topk(3)
            b_attn(2)
            b_attn(3)

            allz = [zg["a"], zg["b"], zg["c"]]
            alln = ["zga", "zgb", "zgc"]
            for u in range(6):
                Wo_, won = load_wo(l, u)
                for hq in range(4):
                    hidx = u * 4 + hq
                    zt, zn = allz[hidx // 8], alln[hidx // 8]
                    for tb in range(4):
                        for n_ in range(2):
                            P.pe(lambda e: e.matmul(ps[tb * 2 + n_][:, :], zt[:, hidx % 8, tb * 128:(tb + 1) * 128], Wo_[:, hq, n_ * 512:(n_ + 1) * 512],
                                                    start=(hidx == 0), stop=(hidx == 23)), r=[zn, won], w=[psn[tb * 2 + n_]])
            for tb in range(4):
                xi = nxt("x", 2)
                X = xt[xi]; xname = f"xt{xi}"
                P.dma("sp", X[:], own[tb * 128:(tb + 1) * 128, :], r=(own_r if own_r else []), w=[xname])
                for n_ in range(2):
                    P.dve(lambda e: e.tensor_tensor(X[:, n_ * 512:(n_ + 1) * 512], X[:, n_ * 512:(n_ + 1) * 512], ps[tb * 2 + n_][:, :], ALU.add),
                          r=[xname, psn[tb * 2 + n_]], w=[xname])
                r0 = m * 512 + tb * 128
                if l < nlayers - 1:
                    P.dma("pool", hp1q[r0:r0 + 128, :], X[:], r=[xname], w=[f"hp1qc{r0 // 256}"])
                else:
                    P.act(lambda e: e.activation(junk[:, 0:1024], X[:], AF.Square, accum_out=small[:, 16:17]), r=[xname], w=["junk", "fs0"])
                    P.dve(lambda e: e.tensor_scalar(small[:, 17:18], small[:, 16:17], 1.0 / D_MODEL, EPS, ALU.mult, ALU.add), r=["fs0"], w=["fs1"])
                    P.act(lambda e: e.activation(small[:, 18:19], small[:, 17:18], AF.Sqrt), r=["fs1"], w=["fs2"])
                    P.dve(lambda e: e.reciprocal(small[:, 19:20], small[:, 18:19]), r=["fs2"], w=["fs3"])
                    P.dve(lambda e: e.scalar_tensor_tensor(X[:], X[:], small[:, 19:20], fgb[:], ALU.mult, ALU.mult), r=[xname, "fs3", "fgb"], w=[xname])
                    P.dma("pool", y_q[r0:r0 + 128, :], X[:], r=[xname])
            if l < nlayers - 1:
                for hf in range(2):
                    cidx = 2 * m + hf
                    P.dma("pool", ccs, hp1q[cidx * 256:(cidx + 1) * 256, :], r=[f"hp1qc{cidx}"], w=["ccs"])
                    P.add("pool", lambda e: e.collective_compute("AllGather", ALU.bypass, replica_groups=[[0, 1, 2, 3], [4, 5, 6, 7]],
                                                                 ins=[ccs.opt()], outs=[ccd.opt()]), r=["ccs"], w=["ccd"], cc=True)
                    for r in range(4):
                        a0 = (4 * m + r) * 512 + hf * 256
                        P.dma("pool", hpg[a0:a0 + 256, :], ccd[r * 256:(r + 1) * 256, :], r=["ccd"], w=["~hpgw"])
        if do_sample:
            NS = DEC_SEQ
            shist = f"~shist{l}"
            psb7 = ps[7].bitcast(BF16)

            def prep_cache(src2d, nrows, ncols, kdst, vdst, nheads):
                for b in range(nrows // 128):
                    xi = nxt("x", 2)
                    X = xt[xi]; xname = f"xt{xi}"
                    P.dma("sp", X[:, 0:ncols], src2d[b * 128:(b + 1) * 128, :], w=[xname])
                    P.dve(lambda e: e.tensor_copy(xn[:, 0:ncols], X[:, 0:ncols]), r=[xname], w=["xn"])
                    if vdst is not None:
                        P.dma("sp", vdst[b * 128:(b + 1) * 128, :], xn[:, 0:ncols], r=["xn"], w=[shist])
                    if kdst is not None:
                        for hh in range(nheads):
                            P.pe(lambda e: e.transpose(psb7[0:64, hh * 128:(hh + 1) * 128], xn[:, hh * 64:(hh + 1) * 64], identb[:]),
                                 r=["xn", "identb"], w=["ps7"])
                        P.act(lambda e: e.activation(kst[:, 0:nheads, 0:128], psb7[0:64, 0:nheads * 128].rearrange("p (h k) -> p h k", h=nheads), AF.Copy),
                              r=["ps7"], w=["kst"])
                        P.dma("sp", kdst[:, :, b * 128:(b + 1) * 128].rearrange("h d k -> d h k"), kst[:, 0:nheads, 0:128], r=["kst"], w=[shist])

            prep_cache(ca_k[l], PAST, 512, SS_KA[l], None, 8)
            prep_cache(ca_v[l], PAST, 512, None, SS_VA[l], 8)
            prep_cache(cb_k[l], PAST, 128, SS_KB[l], None, 2)
            prep_cache(cb_v[l], PAST, 128, None, SS_VB[l], 2)
            prep_cache(cc_k[l], 512, 512, SS_KC[l], None, 8)
            prep_cache(cc_v[l], 512, 512, None, SS_VC[l], 8)
            for b in range(PAST // 128):
                xi = nxt("x", 2)
                X = xt[xi]; xname = f"xt{xi}"
                P.dma("sp", X[:, 0:32], cb_ik[l, b * 128:(b + 1) * 128, :], w=[xname])
                P.dve(lambda e: e.tensor_copy(xn[:, 0:32], X[:, 0:32]), r=[xname], w=["xn"])
                P.dve(lambda e: e.tensor_copy(xn[:, 32:64], X[:, 0:32]), r=[xname], w=["xn"])
                P.pe(lambda e: e.transpose(psb7[0:64, 0:128], xn[:, 0:64], identb[:]), r=["xn", "identb"], w=["ps7"])
                P.act(lambda e: e.activation(kst[:, 0, 0:128], psb7[0:64, 0:128], AF.Copy), r=["ps7"], w=["kst"])
                P.dma("sp", SS_IK[l][:, b * 128:(b + 1) * 128], kst[:, 0, 0:128], r=["kst"], w=[shist])
            P.dve(lambda e: e.memset(tot[:], 0.0), w=["tot"])

            def cum_block(kb_, n):
                P.pe(lambda e: e.matmul(ps[5][0:n, 0:8], cstf[0:n, 2, 0:n], lfb[0:n, :], start=True, stop=False), r=["lfb", "cstf"], w=["ps5"])
                P.pe(lambda e: e.matmul(ps[5][0:n, 0:8], ones1[0:1, 0:n], tot[0:1, :], start=False, stop=True), r=["tot", "ones1"], w=["ps5"])
                P.dve(lambda e: e.tensor_copy(cstore[0:n, kb_, :], ps[5][0:n, 0:8]), r=["ps5"], w=["cstore"])
                P.pe(lambda e: e.matmul(ps[5][0:1, 8:16], cstf[0:n, 1, 128 - n:129 - n], cstore[0:n, kb_, :], start=True, stop=True), r=["cstore", "cstf"], w=["ps5"])
                P.dve(lambda e: e.tensor_copy(tot[:], ps[5][0:1, 8:16]), r=["ps5"], w=["tot"])

            for b in range(PAST // 128):
                P.dma("sp", lfb[:], ca_lf[l, b * 128:(b + 1) * 128, :], w=["lfb"])
                cum_block(b, 128)
            norm_block(l, (xs if l == 0 else hs1)[:, :], 0, nrow=NS, rname=("hs1" if l > 0 else None))
            for uname in TM_UNITS:
                W, wn = load_w(l, UNITS.index(uname))
                si = nxt("s", 4)
                for c in range(8):
                    P.pe(lambda e: e.matmul(ps[si][0:NS, :], hT[:, c, 0:NS], W[:, c, :], start=(c == 0), stop=(c == 7)), r=[wn, "hT"], w=[psn[si]])
                k = nxt("st", 2)
                S_ = st[k]; sn = f"st{k}"
                P.act(lambda e: e.activation(S_[0:NS, :], ps[si][0:NS, :], AF.Copy), r=[psn[si]], w=[sn])
                V_, vn = vst[k], f"vst{k}"
                if uname in ("tka", "tva", "tkc", "tvc"):
                    dst = {"tka": s_ak, "tva": s_av, "tkc": s_ck, "tvc": s_cv}[uname]
                    P.dma("sp", dst[l], S_[0:NS, :], r=[sn])
                    if uname in ("tva", "tvc"):
                        P.dve(lambda e: e.tensor_copy(V_[0:NS, :], S_[0:NS, :]), r=[sn], w=[vn])
                        vd = SS_VA[l][PAST:PAST + NS, :] if uname == "tva" else SS_VC[l][512:512 + NS, :]
                        P.dma("sp", vd, V_[0:NS, :], r=[vn], w=[shist])
                else:
                    P.dma("sp", s_bk[l], S_[0:NS, 0:128], r=[sn])
                    P.dma("sp", s_bv[l], S_[0:NS, 128:256], r=[sn])
                    P.dma("sp", s_ik[l], S_[0:NS, 256:288], r=[sn])
                    P.dve(lambda e: e.tensor_copy(V_[0:NS, 0:128], S_[0:NS, 128:256]), r=[sn], w=[vn])
                    P.dma("sp", SS_VB[l][PAST:PAST + NS, :], V_[0:NS, 0:128], r=[vn], w=[shist])
                    P.dve(lambda e: e.tensor_tensor(lfb[0:NS, :], S_[0:NS, 288:296], bfb[0:NS, l * 8:(l + 1) * 8], ALU.add), r=[sn, "bfb"], w=["lfb"])
                    P.act(lambda e: e.activation(lfb[0:NS, :], lfb[0:NS, :], AF.Exp, scale=-1.0), r=["lfb"], w=["lfb"])
                    P.act(lambda e: e.activation(lfb[0:NS, :], lfb[0:NS, :], AF.Ln, bias=1.0), r=["lfb"], w=["lfb"])
                    P.dve(lambda e: e.tensor_scalar(lfb[0:NS, :], lfb[0:NS, :], -1.0, None, ALU.mult), r=["lfb"], w=["lfb"])
                    P.dma("sp", s_lf[l], lfb[0:NS, :], r=["lfb"])
                    cum_block(8, NS)
                    P.dve(lambda e: e.tensor_scalar(wsgn[0:NS, 0, :], S_[0:NS, 296:304], 0.0, 2.0, ALU.is_ge, ALU.mult), r=[sn], w=["wsgn"])
                    P.dve(lambda e: e.tensor_scalar(wsgn[0:NS, 0, :], wsgn[0:NS, 0, :], -1.0, None, ALU.add), r=["wsgn"], w=["wsgn"])
                    P.dve(lambda e: e.scalar_tensor_tensor(wabs[0:NS, 0, :], S_[0:NS, 296:304], IDXS, wsgn[0:NS, 0, :], ALU.mult, ALU.mult), r=[sn, "wsgn"], w=["wabs"])
            P.pe(lambda e: e.matmul(ps[5][:, 16:24], cstf[0:NS, 3, :], cstore[0:NS, 8, :], start=True, stop=True), r=["cstore", "cstf"], w=["ps5"])
            P.dve(lambda e: e.tensor_copy(cbc[:], ps[5][:, 16:24]), r=["ps5"], w=["cbc"])
            for h in range(8):
                P.dve(lambda e: e.tensor_scalar(nbias[:, 0:9, h], cstore[:, 0:9, h], cbc[:, h:h + 1], -1.0, ALU.subtract, ALU.mult), r=["cstore", "cbc"], w=["nbias"])
            P.dve(lambda e: e.tensor_tensor(rq[0:NS, 0, :], cstore[0:NS, 8, :], cbc[0:NS, :], ALU.subtract), r=["cstore", "cbc"], w=["rq"])
            P.pe(lambda e: e.transpose(ps[6][0:8, 0:NS], rq[0:NS, 0, :], cstf[0:NS, 0, 0:NS]), r=["rq", "cstf"], w=["ps6"])
            P.dve(lambda e: e.tensor_copy(rT[:, 0:NS], ps[6][0:8, 0:NS]), r=["ps6"], w=["rT"])

            def s_evac_k(dst3, koff):
                def f(j, pt, pn):
                    P.act(lambda e: e.activation(kst[:, j, 0:NS], pt[0:64, 0:NS], AF.Copy), r=[pn], w=["kst"])
                    if j == 7:
                        P.dma("sp", dst3[:, :, koff:koff + NS].rearrange("h d k -> d h k"), kst[:, :, 0:NS], r=["kst"], w=[shist])
                return f

            def s_evac_q(j, pt, pn):
                P.act(lambda e: e.activation(Q[0:64, j, 0:NS], pt[0:64, 0:NS], AF.Copy, scale=SCALE), r=[pn], w=["Q"])

            def s_evac_z(zt, zn):
                def f(j, pt, pn):
                    P.act(lambda e: e.activation(zt[:, j, 0:NS], pt[0:64, 0:NS], AF.Silu), r=[pn], w=[zn])
                return f

            fm_unit(l, "ka", NS, s_evac_k(SS_KA[l], PAST))
            fm_unit(l, "za", NS, s_evac_z(zg["a"], "zga"))
            fm_unit(l, "qa", NS, s_evac_q)
            for h in range(8):
                P.dma("sp", Q[64:65, h, 0:NS], rT[h:h + 1, 0:NS], r=["rT"], w=["Q"])
            for half in range(2):
                at = Attn()
                for sbk in range(3):
                    nk = 512 if sbk < 2 else NS
                    bi = load_kv(SS_KA[l], SS_VA[l], half * 4, 4, sbk * 512, nk, shist)
                    for kb in range((nk + 127) // 128):
                        n = min(128, nk - kb * 128)
                        adds = [(0, NS, identb[0:NS, 0:NS], ma0b[0:NS, 0:NS], ["identb", "ma0b"])] if sbk == 2 else []
                        for i in range(4):
                            hh = half * 4 + i
                            at.tile(dict(kT=kbuf[bi][0:65, i, kb * 128:kb * 128 + n], v=vbuf[bi][0:n, kb, i, 0:65], n=n, qlo=0, qhi=NS,
                                         qap=Q[0:65, hh, 0:NS], adds=adds, bias=nbias[0:n, 4 * sbk + kb, hh:hh + 1],
                                         names=[f"kbuf{bi}", f"vbuf{bi}"], O=ps[4 + i], oname=psn[4 + i],
                                         first=(sbk == 0 and kb == 0), last=(sbk == 2)))
                at.flush()
                for i in range(4):
                    finish_head(ps[4 + i], psn[4 + i], zg["a"], "zga", half * 4 + i, NS, slice(0, NS))
            fm_unit(l, "kc", NS, s_evac_k(SS_KC[l], 512))
            fm_unit(l, "zc", NS, s_evac_z(zg["c"], "zgc"))
            fm_unit(l, "qc", NS, s_evac_q)
            for half in range(2):
                at = Attn()
                for sbk in range(2):
                    nk = 512 if sbk < 1 else NS
                    bi = load_kv(SS_KC[l], SS_VC[l], half * 4, 4, sbk * 512, nk, shist)
                    for kb in range((nk + 127) // 128):
                        n = min(128, nk - kb * 128)
                        for i in range(4):
                            hh = half * 4 + i
                            adds = []
                            if sbk == 0 and kb == 3:
                                adds = [(0, NS, identb[:], bc[:, 1, hh, 0:NS], ["identb", "btile"])]
                            if sbk == 1:
                                adds = [(0, NS, identb[0:NS, 0:NS], bc[0:NS, 0, hh, 0:NS], ["identb", "btile"])]
                            at.tile(dict(kT=kbuf[bi][0:64, i, kb * 128:kb * 128 + n], v=vbuf[bi][0:n, kb, i, 0:65], n=n, qlo=0, qhi=NS,
                                         qap=Q[0:64, hh, 0:NS], adds=adds, bias=None,
                                         names=[f"kbuf{bi}", f"vbuf{bi}"], O=ps[4 + i], oname=psn[4 + i],
                                         first=(sbk == 0 and kb == 0), last=(sbk == 1)))
                at.flush()
                for i in range(4):
                    finish_head(ps[4 + i], psn[4 + i], zg["c"], "zgc", half * 4 + i, NS, slice(0, NS))
            def s_evac_bx(j, pt, pn):
                if j < 2:
                    P.act(lambda e: e.activation(kst[:, j, 0:NS], pt[0:64, 0:NS], AF.Copy), r=[pn], w=["kst"])
                    if j == 1:
                        P.dma("sp", SS_KB[l][:, :, PAST:PAST + NS].rearrange("h d k -> d h k"), kst[:, 0:2, 0:NS], r=["kst"], w=[shist])
                elif j < 6:
                    P.act(lambda e: e.activation(iqT[:, j - 2, 0:NS], pt[0:64, 0:NS], AF.Copy), r=[pn], w=["iqT"])
                elif j == 6:
                    P.act(lambda e: e.activation(kst[:, 2, 0:NS], pt[0:64, 0:NS], AF.Copy), r=[pn], w=["kst"])
                    P.dma("sp", SS_IK[l][:, PAST:PAST + NS], kst[:, 2, 0:NS], r=["kst"], w=[shist])
            fm_unit(l, "bx", NS, s_evac_bx)
            fm_unit(l, "zb", NS, s_evac_z(zg["b"], "zgb"))
            fm_unit(l, "qb", NS, s_evac_q)
            NK = PAST + NS
            for k0 in range(0, NK, 512):
                nk = min(512, NK - k0)
                ii = nxt("ik", 2)
                P.dma("sp", ikbuf[ii][:, 0:nk], SS_IK[l][:, k0:k0 + nk], r=[shist], w=[f"ikbuf{ii}"])
                for h in range(8):
                    base = 32 * (h % 2)
                    pi_ = 2 + nxt("ips", 2)
                    P.pe(lambda e: e.matmul(ps[pi_][0:NS, 0:nk], iqT[base:base + 32, h // 2, 0:NS], ikbuf[ii][base:base + 32, 0:nk], start=True, stop=True),
                         r=["iqT", f"ikbuf{ii}"], w=[psn[pi_]])
                    ri = nxt("rr", 2)
                    P.act(lambda e: e.activation(Rr[ri][0:NS, 0:nk], ps[pi_][0:NS, 0:nk], AF.Relu, scale=wabs[0:NS, 0, h:h + 1]), r=[psn[pi_], "wabs"], w=[f"Rr{ri}"])
                    if h == 0:
                        P.dve(lambda e: e.tensor_scalar(SC[0:NS, k0:k0 + nk], Rr[ri][0:NS, 0:nk], wsgn[0:NS, 0, 0:1], None, ALU.mult), r=[f"Rr{ri}", "wsgn"], w=["SC"])
                    else:
                        P.dve(lambda e: e.scalar_tensor_tensor(SC[0:NS, k0:k0 + nk], Rr[ri][0:NS, 0:nk], wsgn[0:NS, 0, h:h + 1], SC[0:NS, k0:k0 + nk], ALU.mult, ALU.add),
                              r=[f"Rr{ri}", "wsgn", "SC"], w=["SC"])
            P.dve(lambda e: e.memset(cntb[:], 0.0), w=["cntb"])
            P.dve(lambda e: e.memset(small[:, 8:9], 0.0), w=["cand"])
            for it in range(NBIS):
                stp = 64.0 * (0.5 ** it)
                P.dve(lambda e: e.tensor_scalar(junk[0:NS, 0:NK], SC[0:NS, 0:NK], small[0:NS, 8:9], 0.0, ALU.is_ge, ALU.add, accum_out=cntb[0:NS, it:it + 1]),
                      r=["SC", "cand", "cntb"], w=["junk", "cntb"])
                a, b_ = (stp, -0.5 * stp) if it < NBIS - 1 else (stp, -stp)
                P.dve(lambda e: e.tensor_scalar(small[0:NS, 9:10], cntb[0:NS, it:it + 1], float(TOPK), a, ALU.is_ge, ALU.mult), r=["cntb"], w=["fl"])
                P.dve(lambda e: e.scalar_tensor_tensor(small[0:NS, 8:9], small[0:NS, 9:10], b_, small[0:NS, 8:9], ALU.add, ALU.add), r=["fl", "cand"], w=["cand"])
            at = Attn()
            for sbk in range(3):
                k0 = sbk * 512
                nk = min(512, NK - k0)
                mi = nxt("mb", 2)
                P.dve(lambda e: e.tensor_scalar(Mb[mi][0:NS, 0:nk], SC[0:NS, k0:k0 + nk], small[0:NS, 8:9], NEG, ALU.is_lt, ALU.mult), r=["SC", "cand"], w=[f"Mb{mi}"])
                bi = load_kv(SS_KB[l], SS_VB[l], 0, 2, k0, nk, shist)
                for kb in range((nk + 127) // 128):
                    n = min(128, nk - kb * 128)
                    gkb = sbk * 4 + kb
                    for j in range(2):
                        adds = [(0, 4 * NS, Mb[mi][0:NS, kb * 128:kb * 128 + n], i4b[0:NS, :, 0:NS], [f"Mb{mi}", "i4b"])]
                        if gkb >= 7:
                            for hq in range(4):
                                if gkb == 7:
                                    adds.append((hq * NS, (hq + 1) * NS, identb[:], b5[:, 1, 4 * j + hq, 0:NS], ["identb", "btile"]))
                                else:
                                    adds.append((hq * NS, (hq + 1) * NS, identb[0:NS, 0:NS], b5[0:NS, 0, 4 * j + hq, 0:NS], ["identb", "btile"]))
                        at.tile(dict(kT=kbuf[bi][0:64, j, kb * 128:kb * 128 + n], v=vbuf[bi][0:n, kb, j, 0:65], n=n, qlo=0, qhi=4 * NS,
                                     qap=Q[0:64, 4 * j:4 * j + 4, 0:NS], adds=adds, bias=None,
                                     names=[f"kbuf{bi}", f"vbuf{bi}"], O=ps[4 + j], oname=psn[4 + j],
                                     first=(gkb == 0), last=(gkb == 8)))
            at.flush()
            for j in range(2):
                finish_head(ps[4 + j], psn[4 + j], zg["b"], "zgb", (4 * j, 4 * j + 4), 4 * NS, slice(0, NS))
            allz = [zg["a"], zg["b"], zg["c"]]
            alln = ["zga", "zgb", "zgc"]
            for u in range(6):
                Wo_, won = load_wo(l, u)
                for hq in range(4):
                    hidx = u * 4 + hq
                    zt, zn = allz[hidx // 8], alln[hidx // 8]
                    for n_ in range(2):
                        P.pe(lambda e: e.matmul(ps[n_][0:NS, :], zt[:, hidx % 8, 0:NS], Wo_[:, hq, n_ * 512:(n_ + 1) * 512], start=(hidx == 0), stop=(hidx == 23)),
                             r=[zn, won], w=[psn[n_]])
            xi = nxt("x", 2)
            X = xt[xi]; xname = f"xt{xi}"
            P.dma("sp", X[0:NS, :], (xs if l == 0 else hs1)[:, :], r=(["hs1"] if l > 0 else []), w=[xname])
            for n_ in range(2):
                P.dve(lambda e: e.tensor_tensor(X[0:NS, n_ * 512:(n_ + 1) * 512], X[0:NS, n_ * 512:(n_ + 1) * 512], ps[n_][0:NS, :], ALU.add), r=[xname, psn[n_]], w=[xname])
            if l < nlayers - 1:
                P.dma("sp", hs1[:, :], X[0:NS, :], r=[xname], w=["hs1"])
            else:
                P.act(lambda e: e.activation(junk[0:NS, 0:1024], X[0:NS, :], AF.Square, accum_out=small[0:NS, 16:17]), r=[xname], w=["junk", "fs0"])
                P.dve(lambda e: e.tensor_scalar(small[0:NS, 17:18], small[0:NS, 16:17], 1.0 / D_MODEL, EPS, ALU.mult, ALU.add), r=["fs0"], w=["fs1"])
                P.act(lambda e: e.activation(small[0:NS, 18:19], small[0:NS, 17:18], AF.Sqrt), r=["fs1"], w=["fs2"])
                P.dve(lambda e: e.reciprocal(small[0:NS, 19:20], small[0:NS, 18:19]), r=["fs2"], w=["fs3"])
                P.dve(lambda e: e.scalar_tensor_tensor(X[0:NS, :], X[0:NS, :], small[0:NS, 19:20], fgb[0:NS, :], ALU.mult, ALU.mult), r=[xname, "fs3", "fgb"], w=[xname])
                P.dma("sp", y_s[:, :], X[0:NS, :], r=[xname])
    P.finalize_and_emit()
    return nc, es, P


def _t5_bucket_np(rel):
    nb = 16
    max_exact = 8
    ret = np.where(rel > 0, nb, 0)
    n = np.abs(rel)
    nf = np.maximum(n, 1).astype(np.float32)
    large = max_exact + (np.log(nf / max_exact) / math.log(128 / max_exact) * (nb - max_exact)).astype(np.int32)
    large = np.minimum(large, nb - 1)
    return ret + np.where(n < max_exact, n, large)


def _constants():
    p = np.arange(128)[:, None]
    f = np.arange(128)[None, :]
    cst = np.zeros((128, 8, 128), np.float32)
    cst[:, 0] = (p == f)
    cst[:, 1] = (p + f == 127)
    cst[:, 2] = (p <= f)
    cst[0, 3, :] = 1.0
    cst[:, 4] = np.where(p > f, NEG, 0.0)
    cst[:, 5] = np.where((p >= 64) & (f < 64), NEG, 0.0)
    cst[:, 6] = np.where((p < 64) & (f >= 64), NEG, 0.0)
    cst[:, 7] = np.where((p < 64) & (f >= 64), -1e30, 0.0)
    rel = 127 - np.arange(LTAB)
    bk = _t5_bucket_np(rel.astype(np.int32))
    oh5 = np.zeros((32, LTAB), np.float32)
    oh5[bk, np.arange(LTAB)] += 1.0
    far = int(_t5_bucket_np(np.array([-100000], np.int32))[0])
    oh5[far, :] -= 1.0
    idx = np.clip(rel, -128, 128) + 128
    ohc = np.zeros((3 * 128, LTAB), np.float32)
    ohc[idx, np.arange(LTAB)] += 1.0
    ohc[0, :] -= 1.0
    return cst.reshape(128, 8 * 128), oh5, ohc.reshape(3, 128, LTAB)


_PROG = {}


def _get_prog(key=(True, 4, DEPTH)):
    if key not in _PROG:
        _PROG[key] = build_program(*key)
    return _PROG[key]


def _host_inputs(x_prompt, norm_g, w_in, b_f, t5_bias, c_rel_bias, w_out, final_g):
    cst, oh5, ohc = _constants()
    wus = np.zeros((DEPTH, NU, 128, 8, 512), np.float32)
    for l in range(DEPTH):
        for u, name in enumerate(UNITS):
            cols = np.array(_unit_cols(name))
            m = cols >= 0
            w = np.zeros((D_MODEL, 512), np.float32)
            w[:, m] = w_in[l][:, cols[m]]
            wus[l, u] = w.reshape(8, 128, 512).transpose(1, 0, 2)
    wos = np.ascontiguousarray(w_out.reshape(DEPTH, 6, 4, 64, D_MODEL).transpose(0, 1, 3, 2, 4))
    gcol = np.ascontiguousarray(norm_g.reshape(DEPTH, 8, 128).transpose(2, 0, 1).reshape(128, DEPTH * 8))
    crel = np.zeros((DEPTH, 384, 8), np.float32)
    crel[:, :257] = c_rel_bias
    common = dict(wu=wus, wo=wos, gcol=gcol, fg=np.ascontiguousarray(final_g.reshape(1, D_MODEL)),
                  bfb=np.ascontiguousarray(b_f.reshape(1, DEPTH * 8)), t5=np.ascontiguousarray(t5_bias),
                  crel=crel.reshape(DEPTH, 3, 128, 8), oh5=oh5, ohc=ohc, cst=cst)
    return common


def _percore(j):
    pc = np.zeros((128, 1024), np.float32)
    p = np.arange(128)
    pc[:, 0:512] = (j * 512 + np.arange(512))[None, :]
    for rr in range(16):
        pc[:, 512 + rr] = rr * 128 + p
    for qb in range(4):
        for ch in range(4):
            pc[:, 528 + qb * 4 + ch] = (8 * j + 2 * qb + (p >= 64) + 1) * 64 - ch * 512
        for rr in range(17):
            pc[:, 544 + (qb * 17 + rr) * 2] = 1.0 if (rr - 1) == 4 * j + qb else 0.0
            pc[:, 544 + (qb * 17 + rr) * 2 + 1] = 1.0 if (rr - 1) == 4 * j + qb - 1 else 0.0
    for r in range(4):
        pc[:, 680 + r] = 1.0 if j == r else 0.0
    pc[:, 684] = -30000.0 if j == 0 else 0.0
    return pc


def _in_maps(x_prompt, x_sample, cache_a_k, cache_a_v, cache_a_logf, cache_b_k, cache_b_v, cache_b_idx_k,
             cache_c_k, cache_c_v, norm_g, w_in, b_f, t5_bias, c_rel_bias, w_out, final_g):
    f = lambda a: np.ascontiguousarray(np.asarray(a, dtype=np.float32))
    x_prompt, norm_g, w_in, b_f, t5_bias, c_rel_bias, w_out, final_g = map(f, (x_prompt, norm_g, w_in, b_f, t5_bias, c_rel_bias, w_out, final_g))
    common = _host_inputs(x_prompt, norm_g, w_in, b_f, t5_bias, c_rel_bias, w_out, final_g)
    common["iota5"] = np.ascontiguousarray(np.broadcast_to(np.arange(512, dtype=np.float32)[None, :], (128, 512)))
    in_maps = []
    for c in range(8):
        b, j = c // 4, c % 4
        m = dict(common)
        xb = x_prompt[b].reshape(NG, 512, D_MODEL)
        m["xp"] = x_prompt[b]
        m["xq"] = np.ascontiguousarray(xb[j::4])
        xpv = np.zeros((4, 512, D_MODEL), np.float32)
        for mm in range(4):
            if 4 * mm + j - 1 >= 0:
                xpv[mm] = xb[4 * mm + j - 1]
        m["xprev"] = xpv
        m["pcore"] = _percore(j)
        m["xs"] = f(x_sample[c])
        m["ca_k"] = f(cache_a_k[:, c]).reshape(DEPTH, PAST, 512); m["ca_v"] = f(cache_a_v[:, c]).reshape(DEPTH, PAST, 512)
        m["ca_lf"] = f(cache_a_logf[:, c]).reshape(DEPTH, PAST, 8)
        m["cb_k"] = f(cache_b_k[:, c]).reshape(DEPTH, PAST, 128); m["cb_v"] = f(cache_b_v[:, c]).reshape(DEPTH, PAST, 128)
        m["cb_ik"] = f(cache_b_idx_k[:, c]).reshape(DEPTH, PAST, 32)
        m["cc_k"] = f(cache_c_k[:, c]).reshape(DEPTH, 512, 512); m["cc_v"] = f(cache_c_v[:, c]).reshape(DEPTH, 512, 512)
        in_maps.append(m)
    return in_maps


def kernel(x_prompt, x_sample, cache_a_k, cache_a_v, cache_a_logf, cache_b_k, cache_b_v, cache_b_idx_k,
           cache_c_k, cache_c_v, norm_g, w_in, b_f, t5_bias, c_rel_bias, w_out, final_g):
    nc, es, P = _get_prog()
    in_maps = _in_maps(x_prompt, x_sample, cache_a_k, cache_a_v, cache_a_logf, cache_b_k, cache_b_v, cache_b_idx_k,
                       cache_c_k, cache_c_v, norm_g, w_in, b_f, t5_bias, c_rel_bias, w_out, final_g)
    res = run_bass_kernel_spmd(nc, in_maps, core_ids=list(range(8)))
    R = res.results
    st = lambda name, shp: np.stack([R[4 * b][name] for b in range(2)], axis=1).reshape(shp)
    y_prompt = np.zeros((BATCH, NG, 512, D_MODEL), np.float32)
    for c in range(8):
        b, j = c // 4, c % 4
        y_prompt[b, j::4] = R[c]["y_q"].reshape(4, 512, D_MODEL)
    y_prompt = y_prompt.reshape(BATCH, SEQ, D_MODEL)
    ss = lambda name, shp: np.stack([R[b][name] for b in range(DEC_BATCH)], axis=1).reshape(shp)
    y_sample = np.stack([R[b]["y_s"] for b in range(DEC_BATCH)], axis=0)
    outs = [y_prompt, y_sample,
            st("o_ak", (DEPTH, BATCH, SEQ, H, HD)), st("o_av", (DEPTH, BATCH, SEQ, H, HD)), st("o_lf", (DEPTH, BATCH, SEQ, H)),
            st("o_bk", (DEPTH, BATCH, SEQ, KVB, HD)), st("o_bv", (DEPTH, BATCH, SEQ, KVB, HD)), st("o_ik", (DEPTH, BATCH, SEQ, IDX_D)),
            st("o_ck", (DEPTH, BATCH, 512, H, HD)), st("o_cv", (DEPTH, BATCH, 512, H, HD)),
            ss("s_ak", (DEPTH, DEC_BATCH, DEC_SEQ, H, HD)), ss("s_av", (DEPTH, DEC_BATCH, DEC_SEQ, H, HD)), ss("s_lf", (DEPTH, DEC_BATCH, DEC_SEQ, H)),
            ss("s_bk", (DEPTH, DEC_BATCH, DEC_SEQ, KVB, HD)), ss("s_bv", (DEPTH, DEC_BATCH, DEC_SEQ, KVB, HD)), ss("s_ik", (DEPTH, DEC_BATCH, DEC_SEQ, IDX_D)),
            ss("s_ck", (DEPTH, DEC_BATCH, DEC_SEQ, H, HD)), ss("s_cv", (DEPTH, DEC_BATCH, DEC_SEQ, H, HD))]
    return tuple(outs)
```

```python
import math
import types
import numpy as np
from contextlib import ExitStack
import concourse.bass as bass
import concourse.mybir as mybir
from concourse.bass_utils import run_bass_kernel_spmd

F32 = mybir.dt.float32
BF16 = mybir.dt.bfloat16
ALU = mybir.AluOpType
AF = mybir.ActivationFunctionType

D_MODEL = 1024; BATCH = 2; SEQ = 8192; DEPTH = 2; DEC_BATCH = 8; DEC_SEQ = 16; PAST = 1024
HD = 64; H = 8; KVB = 2; IDX_H = 8; IDX_D = 32; TOPK = 256
SCALE = HD ** -0.5
IDXS = (IDX_D ** -0.5) * (IDX_H ** -0.5)
EPS = 1e-6
NEG = -32768.0
NG = SEQ // 512
NBIS = 24
LTAB = 384

_SPLIT = (512, 512, 512, 512, 8, 512, 128, 128, 512, 256, 8, 32, 512, 512, 512, 512)
_OFF = np.concatenate([[0], np.cumsum(_SPLIT)])
(QA, KA, VA, ZA, FA, QB, KB, VB, ZB, IQ, IW, IK, QC, KC, VC, ZC) = [int(o) for o in _OFF[:-1]]
FM_UNITS = ["qa", "ka", "za", "qb", "zb", "bx", "qc", "kc", "zc"]
TM_UNITS = ["tka", "tva", "tb", "tkc", "tvc"]
UNITS = FM_UNITS + TM_UNITS
NU = len(UNITS)


def _unit_cols(name):
    r = lambda a, n: list(range(a, a + n))
    pad = lambda l: l + [-1] * (512 - len(l))
    if name == "qa": return r(QA, 512)
    if name == "ka": return r(KA, 512)
    if name == "za": return r(ZA, 512)
    if name == "qb": return r(QB, 512)
    if name == "zb": return r(ZB, 512)
    if name == "qc": return r(QC, 512)
    if name == "kc": return r(KC, 512)
    if name == "zc": return r(ZC, 512)
    if name == "bx": return pad(r(KB, 128) + r(IQ, 256) + r(IK, 32) + r(IK, 32))
    if name == "tka": return r(KA, 512)
    if name == "tva": return r(VA, 512)
    if name == "tkc": return r(KC, 512)
    if name == "tvc": return r(VC, 512)
    if name == "tb": return pad(r(KB, 128) + r(VB, 128) + r(IK, 32) + r(FA, 8) + r(IW, 8))
    raise KeyError(name)


class Prog:
    STREAMS = ("pe", "act", "dve", "pool", "sp")

    def __init__(self, nc):
        self.nc = nc
        self.ops = []
        self.ndma = {}

    @staticmethod
    def _freeze(fn):
        if fn.__closure__ is None:
            return fn
        cells = []
        for c in fn.__closure__:
            try:
                cells.append(types.CellType(c.cell_contents))
            except ValueError:
                cells.append(c)
        return types.FunctionType(fn.__code__, fn.__globals__, fn.__name__, fn.__defaults__, tuple(cells))

    NSUB = 16

    def add(self, stream, fn, r=(), w=(), dma=False, cc=False):
        fn = self._freeze(fn)
        if cc:
            track = "dma_cc"
        elif dma:
            k = self.ndma.get(stream, 0)
            self.ndma[stream] = k + 1
            track = f"dma_{stream}#{k % self.NSUB}"
        else:
            track = stream
        self.ops.append((stream, track, fn, tuple(r), tuple(w)))

    def pe(self, fn, r=(), w=()): self.add("pe", fn, r, w)
    def act(self, fn, r=(), w=()): self.add("act", fn, r, w)
    def dve(self, fn, r=(), w=()): self.add("dve", fn, r, w)
    def pool(self, fn, r=(), w=()): self.add("pool", fn, r, w)

    def dma(self, stream, out, in_, r=(), w=(), **kw):
        self.add(stream, lambda e: e.dma_start(out=out, in_=in_, **kw), r, w, dma=True)

    def finalize_and_emit(self):
        nc = self.nc
        ops = self.ops
        n = len(ops)
        writers = {}
        readers = {}
        prev_on = {}
        deps = [None] * n
        signal = [False] * n
        qof = lambda t: t.split("#")[0]
        for i, (stream, track, fn, R, W) in enumerate(ops):
            d = set()
            is_dma = track.startswith("dma_")
            if is_dma:
                j = prev_on.get(track)
                if j is not None:
                    d.add(j)
                prev_on[track] = i
            for res in R:
                for tj, j in writers.get(res, {}).items():
                    if tj != track or is_dma or track != "pe":
                        d.add(j)
            for res in W:
                lazy = res.startswith("~")
                for tj, j in writers.get(res, {}).items():
                    if lazy:
                        if qof(tj) != qof(track):
                            d.add(j)
                    elif tj != track or is_dma:
                        d.add(j)
                for tj, j in readers.get(res, {}).items():
                    if lazy:
                        if qof(tj) != qof(track):
                            d.add(j)
                    elif tj != track or is_dma:
                        d.add(j)
            for res in R:
                readers.setdefault(res, {})[track] = i
            for res in W:
                if res.startswith("~"):
                    writers.setdefault(res, {})[track] = i
                else:
                    writers[res] = {track: i}
                    readers[res] = {}
            d.discard(i)
            deps[i] = d
            for j in d:
                signal[j] = True
        tracks = sorted({o[1] for o in ops})
        cnt = {t: 0 for t in tracks}
        val = [0] * n
        for i, (stream, track, fn, R, W) in enumerate(ops):
            if track == "dma_cc":
                cnt[track] += 1
                val[i] = cnt[track]
            elif track.startswith("dma_"):
                cnt[track] += 16
                val[i] = cnt[track]
            elif signal[i]:
                cnt[track] += 1
                val[i] = cnt[track]
        known = {s: {t: 0 for t in tracks} for s in self.STREAMS}
        waits = [None] * n
        for i, (stream, track, fn, R, W) in enumerate(ops):
            need = {}
            for j in deps[i]:
                tj = ops[j][1]
                need[tj] = max(need.get(tj, 0), val[j])
            wl = []
            for tj, v in need.items():
                if v > known[stream][tj]:
                    wl.append((tj, v))
                    known[stream][tj] = v
            waits[i] = wl
        self.stats = dict(cnt)
        by_stream = {s: [] for s in self.STREAMS}
        for i, o in enumerate(ops):
            by_stream[o[0]].append(i)
        with ExitStack() as es:
            sems = {t: es.enter_context(nc.semaphore("s_" + t.replace("#", "_"))) for t in tracks}
            block = es.enter_context(nc.Block())

            def run(eng, stream):
                for i in by_stream[stream]:
                    _, track, fn, R, W = ops[i]
                    for tj, v in waits[i]:
                        eng.wait_ge(sems[tj], v)
                    inst = fn(eng)
                    if track == "dma_cc":
                        inst.then_inc(sems[track], 1)
                    elif track.startswith("dma_"):
                        inst.then_inc(sems[track], 16)
                    elif signal[i]:
                        inst.then_inc(sems[track], 1)
                if stream == "sp":
                    for t in tracks:
                        if t.startswith("dma_") and cnt[t] > known[stream][t]:
                            eng.wait_ge(sems[t], cnt[t])

            @block.tensor
            def _(e): run(e, "pe")

            @block.scalar
            def _(e): run(e, "act")

            @block.vector
            def _(e): run(e, "dve")

            @block.gpsimd
            def _(e): run(e, "pool")

            @block.sync
            def _(e): run(e, "sp")


def build_program(do_sample=True, nm=4, nlayers=DEPTH):
    nc = bass.Bass("TRN2", target_bir_lowering=False)
    es = ExitStack()
    din = lambda name, shape, dt=F32: nc.dram_tensor(name, list(shape), dt, kind="ExternalInput").ap()
    dout = lambda name, shape: nc.dram_tensor(name, list(shape), F32, kind="ExternalOutput").ap()
    dscr = lambda name, shape, dt=BF16: nc.dram_tensor(name, list(shape), dt, kind="Internal").ap()
    xp = din("xp", [SEQ, D_MODEL])
    wu = din("wu", [DEPTH, NU, 128, 8, 512])
    wo = din("wo", [DEPTH, 6, 64, 4, 1024])
    gcol_d = din("gcol", [128, DEPTH * 8])
    fg_d = din("fg", [1, D_MODEL])
    bf_d = din("bfb", [1, DEPTH * 8])
    t5_d = din("t5", [32, 8])
    crel_d = din("crel", [DEPTH, 3, 128, 8])
    oh5_d = din("oh5", [32, LTAB])
    ohc_d = din("ohc", [3, 128, LTAB])
    cst_d = din("cst", [128, 8 * 128])
    y_q = dout("y_q", [2048, D_MODEL])
    xq = din("xq", [4, 512, D_MODEL]); xprev = din("xprev", [4, 512, D_MODEL])
    pc_d = din("pcore", [128, 1024])
    iota_d = din("iota5", [128, 512])
    o_ak = dout("o_ak", [DEPTH, SEQ, 512]); o_av = dout("o_av", [DEPTH, SEQ, 512])
    o_lf = dout("o_lf", [DEPTH, SEQ, 8])
    o_bk = dout("o_bk", [DEPTH, SEQ, 128]); o_bv = dout("o_bv", [DEPTH, SEQ, 128])
    o_ik = dout("o_ik", [DEPTH, SEQ, 32])
    o_ck = dout("o_ck", [DEPTH, 512, 512]); o_cv = dout("o_cv", [DEPTH, 512, 512])
    wub = dscr("wub", [DEPTH, NU, 128, 8, 512])
    wob = dscr("wob", [DEPTH, 6, 64, 4, 1024])
    hp1q = dscr("hp1q", [2048, D_MODEL], F32)
    hpg = dscr("hpg", [SEQ, D_MODEL], F32)
    ccs = dscr("ccs", [256, D_MODEL], F32)
    COMBS = dscr("combs", [4, 17, 2, 128, 512])
    AMASK = dscr("amask", [16, 128, 512])
    ccd = dscr("ccd", [1024, D_MODEL], F32)
    S_KCL = dscr("scr_kcl", [DEPTH, 8, 64, 1024]); S_VCL = dscr("scr_vcl", [DEPTH, 1024, 512])
    tab5 = dscr("tab5", [8, LTAB], F32)
    tabc = dscr("tabc", [DEPTH, 8, LTAB], F32)
    S_KA = dscr("scr_s_ka", [DEPTH, 8, 64, SEQ]); S_KC = dscr("scr_s_kc", [DEPTH, 8, 64, SEQ])
    S_KB = dscr("scr_s_kb", [DEPTH, 2, 64, SEQ]); S_IK = dscr("scr_s_ik", [DEPTH, 64, SEQ])
    S_VA = dscr("scr_s_va", [DEPTH, SEQ, 512]); S_VC = dscr("scr_s_vc", [DEPTH, SEQ, 512])
    S_VB = dscr("scr_s_vb", [DEPTH, SEQ, 128])

    xs = din("xs", [DEC_SEQ, D_MODEL])
    ca_k = din("ca_k", [DEPTH, PAST, 512]); ca_v = din("ca_v", [DEPTH, PAST, 512]); ca_lf = din("ca_lf", [DEPTH, PAST, 8])
    cb_k = din("cb_k", [DEPTH, PAST, 128]); cb_v = din("cb_v", [DEPTH, PAST, 128]); cb_ik = din("cb_ik", [DEPTH, PAST, 32])
    cc_k = din("cc_k", [DEPTH, 512, 512]); cc_v = din("cc_v", [DEPTH, 512, 512])
    y_s = dout("y_s", [DEC_SEQ, D_MODEL])
    s_ak = dout("s_ak", [DEPTH, DEC_SEQ, 512]); s_av = dout("s_av", [DEPTH, DEC_SEQ, 512]); s_lf = dout("s_lf", [DEPTH, DEC_SEQ, 8])
    s_bk = dout("s_bk", [DEPTH, DEC_SEQ, 128]); s_bv = dout("s_bv", [DEPTH, DEC_SEQ, 128]); s_ik = dout("s_ik", [DEPTH, DEC_SEQ, 32])
    s_ck = dout("s_ck", [DEPTH, DEC_SEQ, 512]); s_cv = dout("s_cv", [DEPTH, DEC_SEQ, 512])
    hs1 = dscr("hs1", [DEC_SEQ, D_MODEL], F32)
    MBS = [dscr(f"mbs{i}", [128, SEQ]) for i in range(4)]
    SS_KA = dscr("ss_ka", [DEPTH, 8, 64, 1152]); SS_KC = dscr("ss_kc", [DEPTH, 8, 64, 640])
    SS_KB = dscr("ss_kb", [DEPTH, 2, 64, 1152]); SS_IK = dscr("ss_ik", [DEPTH, 64, 1152])
    SS_VA = dscr("ss_va", [DEPTH, 1152, 512]); SS_VC = dscr("ss_vc", [DEPTH, 640, 512]); SS_VB = dscr("ss_vb", [DEPTH, 1152, 128])

    sb = lambda name, shape, dt: es.enter_context(nc.sbuf_tensor(name, list(shape), dt))
    wring = [sb(f"wring{i}", [128, 8, 512], BF16) for i in range(2)]
    SC = sb("SC", [128, 8192], F32)
    junk = sb("junk", [128, 8192], BF16)
    hT = sb("hT", [128, 8, 512], BF16)
    Q = sb("Q", [65, 8, 512], BF16)
    zg = {t: sb("zg" + t, [64, 8, 512], BF16) for t in "abc"}
    iqT = sb("iqT", [64, 4, 512], BF16)
    kst = sb("kst", [64, 8, 512], BF16)
    xt = [sb(f"xt{i}", [128, 1024], F32) for i in range(2)]
    xn = sb("xn", [128, 1024], BF16)
    st = [sb(f"st{i}", [128, 512], F32) for i in range(2)]
    vst = [sb(f"vst{i}", [128, 512], BF16) for i in range(2)]
    kbuf = [sb(f"kbuf{i}", [65, 4, 512], BF16) for i in range(2)]
    vbuf = [sb(f"vbuf{i}", [128, 4, 4, 65], BF16) for i in range(2)]
    Pt = [sb(f"Pt{i}", [128, 512], BF16) for i in range(4)]
    Mb = [sb(f"Mb{i}", [128, 512], BF16) for i in range(2)]
    Rr = [sb(f"Rr{i}", [128, 512], F32) for i in range(3)]
    ikbuf = [sb(f"ikbuf{i}", [64, 512], BF16) for i in range(2)]
    b5 = sb("b5", [128, 2, 8, 128], BF16)
    bc = sb("bc", [128, 2, 8, 128], BF16)
    cstf = sb("cstf", [128, 8, 128], F32)
    identb = sb("identb", [128, 128], BF16)
    i4b = sb("i4b", [128, 4, 128], BF16)
    ma0b = sb("ma0b", [128, 128], BF16); cm0b = sb("cm0b", [128, 128], BF16); cm4b = sb("cm4b", [128, 128], BF16)
    cstore = sb("cstore", [128, 64, 8], F32)
    nbias = sb("nbias", [128, 64, 8], F32)
    gcol = sb("gcol_s", [128, DEPTH * 8], F32)
    fgb = sb("fgb", [128, D_MODEL], F32)
    bfb = sb("bfb_s", [128, DEPTH * 8], F32)
    small = sb("small", [128, 64], F32)
    cntb = sb("cntb", [128, NBIS], F32)
    wabs = sb("wabs", [128, 4, 8], F32); wsgn = sb("wsgn", [128, 4, 8], F32)
    lfb = sb("lfb", [128, 8], F32)
    tot = sb("tot", [1, 8], F32)
    tots = sb("tots", [1, 17, 8], F32)
    totbc = sb("totbc", [128, 8], F32)
    lf4 = sb("lf4", [128, 4, 8], F32)
    cown = sb("cown", [128, 4, 8], F32)
    xacc = sb("xacc", [128, D_MODEL], F32)
    pcore = sb("pcore_s", [128, 1024], F32)
    iota5 = sb("iota5_s", [128, 512], F32)
    comb = [sb(f"comb{i}", [128, 4, 128], BF16) for i in range(2)]
    ones1 = sb("ones1", [65, 128], F32)
    cbc = sb("cbc", [128, 8], F32)
    rq = sb("rq", [128, 4, 8], F32)
    rT = sb("rT", [8, 512], BF16)
    rden = sb("rden", [65, 512], F32)
    otmp = sb("otmp", [64, 512], F32)
    hank = sb("hank", [128, 128], F32)
    t5s = sb("t5s", [32, 8], F32); oh5s = sb("oh5s", [32, LTAB], F32)
    crs = sb("crs", [128, 3, 8], F32); ohcs = sb("ohcs", [128, 3, LTAB], F32)
    tabs = sb("tabs", [8, LTAB], F32)
    ps = [es.enter_context(nc.psum_tensor(f"ps{i}", [128, 512], F32)) for i in range(8)]
    psn = [f"ps{i}" for i in range(8)]

    P = Prog(nc)
    _early = {}

    def nxt_early(key, n):
        v = _early.get(key, 0)
        _early[key] = v + 1
        return v % n

    IDENT = cstf[:, 0, :]; JM = cstf[:, 1, :]; TRI = cstf[:, 2, :]; E0ROW = cstf[:, 3, :]
    ADM = cstf[:, 7, :]
    E127 = cstf[:, 1, 0:1]

    P.dma("sp", pcore[:], pc_d, w=["qrelb", "krel", "qlimc", "sel01", "selb", "pvb"])
    P.dma("sp", iota5[:], iota_d, w=["iota5"])
    qrelb = pcore[:, 0:512]; krel = pcore[:, 512:528]; qlimc = pcore[:, 528:544]; sel01 = pcore[:, 544:680]
    selb = pcore[:, 680:684]; pvb = pcore[:, 684:685]
    P.dma("sp", cstf[:].rearrange("p a b -> p (a b)"), cst_d, w=["cstf"])
    P.dma("sp", gcol[:], gcol_d, w=["gcol"])
    P.dma("sp", fgb[:], fg_d.to_broadcast([128, D_MODEL]) if hasattr(fg_d, "to_broadcast") else bass.AP(fg_d.tensor, 0, [[0, 128], [1, D_MODEL]]), w=["fgb"])
    P.dma("sp", bfb[:], bass.AP(bf_d.tensor, 0, [[0, 128], [1, DEPTH * 8]]), w=["bfb"])
    P.dma("sp", t5s[:], t5_d, w=["t5s"])
    P.dma("sp", oh5s[:], oh5_d, w=["oh5s"])
    P.dma("sp", ohcs[:], ohc_d.rearrange("c p l -> p c l"), w=["ohcs"])
    P.dve(lambda e: e.tensor_copy(identb[:], IDENT), r=["cstf"], w=["identb"])
    for k in range(4):
        P.dve(lambda e, k=k: e.tensor_copy(i4b[:, k, :], IDENT), r=["cstf"], w=["i4b"])
    P.dve(lambda e: e.tensor_copy(ma0b[:], cstf[:, 4, :]), r=["cstf"], w=["ma0b"])
    P.dve(lambda e: e.tensor_copy(cm0b[:], cstf[:, 5, :]), r=["cstf"], w=["cm0b"])
    P.dve(lambda e: e.tensor_copy(cm4b[:], cstf[:, 6, :]), r=["cstf"], w=["cm4b"])
    onesf = cstf[:, 4, :]
    P.dve(lambda e: e.memset(onesf, 1.0), r=["ma0b"], w=["cstf", "onesf"])
    P.dve(lambda e: e.memset(ones1[:], 1.0), w=["ones1"])
    for i in range(2):
        P.pool(lambda e, i=i: e.memset(kbuf[i][:], 1.0), w=[f"kbuf{i}"])
        P.pool(lambda e, i=i: e.memset(vbuf[i][:], 1.0), w=[f"vbuf{i}"])

    stg = SC[:, 0:4096].rearrange("p (c n) -> p c n", c=8)
    stgb = junk[:, 0:4096].rearrange("p (c n) -> p c n", c=8)
    for l in range(nlayers):
        for u in range(NU):
            P.dma("sp", stg, wu[l, u], w=["SC"])
            P.act(lambda e: e.activation(stgb, stg, AF.Copy), r=["SC"], w=["junk"])
            P.dma("sp", wub[l, u], stgb, r=["junk"], w=[f"wub{l}_{u}"])
        for u in range(6):
            so = SC[0:64, 0:4096].rearrange("p (c n) -> p c n", c=4)
            sob = junk[0:64, 0:4096].rearrange("p (c n) -> p c n", c=4)
            P.dma("sp", so, wo[l, u], w=["SC"])
            P.act(lambda e, so=so, sob=sob: e.activation(sob, so, AF.Copy), r=["SC"], w=["junk"])
            P.dma("sp", wob[l, u], sob, r=["junk"], w=[f"wob{l}_{u}"])

    def build_tab(lhs_list, rhs_list, dst, rnames):
        for i, (a, b) in enumerate(zip(lhs_list, rhs_list)):
            P.pe(lambda e, a=a, b=b, i=i: e.matmul(ps[0][0:8, 0:LTAB], a, b, start=(i == 0), stop=(i == len(lhs_list) - 1)),
                 r=rnames, w=["ps0"])
        P.dve(lambda e: e.tensor_copy(tabs[:], ps[0][0:8, 0:LTAB]), r=["ps0"], w=["tabs"])
        P.dma("sp", dst, tabs[:], r=["tabs"], w=["tabdram"])

    def build_toeplitz(tab_ap2d, dst_tile):
        for k in range(2):
            for h in range(8):
                b0 = 128 * k
                src = bass.AP(tab_ap2d.tensor, tab_ap2d.offset + h * LTAB + b0, [[1, 128], [1, 128]])
                P.dma("sp", hank[:], src, r=["tabdram"], w=["hank"])
                P.pe(lambda e: e.matmul(ps[1][:, 0:128], JM, hank[:], start=True, stop=True), r=["hank", "cstf"], w=["ps1"])
                P.dve(lambda e, k=k, h=h: e.tensor_copy(dst_tile[:, k, h, :], ps[1][:, 0:128]), r=["ps1"], w=["btile"])

    build_tab([t5s[:]], [oh5s[:]], tab5, ["t5s", "oh5s"])
    build_toeplitz(tab5, b5)
    for qb in range(4):
        for rr in range(17):
            for jj in range(2):
                ci = nxt_early("cmb", 2)
                sc0 = sel01[:, (qb * 17 + rr) * 2:(qb * 17 + rr) * 2 + 1]
                sc1 = sel01[:, (qb * 17 + rr) * 2 + 1:(qb * 17 + rr) * 2 + 2]
                P.dve(lambda e: e.tensor_scalar(comb[ci][:, :, :], b5[:, 0, 4 * jj:4 * jj + 4, :], sc0, None, ALU.mult), r=["btile", "sel01"], w=[f"comb{ci}"])
                P.dve(lambda e: e.scalar_tensor_tensor(comb[ci][:, :, :], b5[:, 1, 4 * jj:4 * jj + 4, :], sc1, comb[ci][:, :, :], ALU.mult, ALU.add),
                      r=["btile", "sel01", f"comb{ci}"], w=[f"comb{ci}"])
                P.dma("pool", COMBS[qb, rr, jj], comb[ci][:].rearrange("p a b -> p (a b)"), r=[f"comb{ci}"], w=["~combs"])
    for rr in range(16):
        mi = nxt_early("mb", 2)
        P.dve(lambda e: e.tensor_scalar(Mb[mi][:, :], qrelb[:, :], krel[:, rr:rr + 1], NEG, ALU.is_lt, ALU.mult), r=["qrelb", "krel"], w=[f"Mb{mi}"])
        P.dma("pool", AMASK[rr], Mb[mi][:, :], r=[f"Mb{mi}"], w=["~amask"])

    wk = [0]

    def load_w(l, u):
        i = wk[0] % 2
        wk[0] += 1
        P.dma("sp", wring[i][:], wub[l, u], r=[f"wub{l}_{u}"], w=[f"wring{i}"])
        return wring[i], f"wring{i}"

    def load_wo(l, u):
        i = wk[0] % 2
        wk[0] += 1
        dst = wring[i][0:64].rearrange("p c n -> p (c n)").rearrange("p (c n) -> p c n", c=4)
        P.dma("sp", dst, wob[l, u], r=[f"wob{l}_{u}"], w=[f"wring{i}"])
        return dst, f"wring{i}"

    rot = {"s": 0, "pt": 0, "kv": 0, "st": 0, "x": 0, "ik": 0, "rr": 0, "mb": 0, "ips": 0, "cmb": 0}

    def nxt(key, n):
        v = rot[key] % n
        rot[key] += 1
        return v

    def norm_block(l, xsrc_ap, tb, nrow=128, rname=None, sb_src=None):
        if sb_src is not None:
            X, xname = sb_src
        else:
            xi = nxt("x", 2)
            X = xt[xi]; xname = f"xt{xi}"
        if sb_src is None:
            P.dma("sp", X[0:nrow, :], xsrc_ap, r=(list(rname) if isinstance(rname, (list, tuple)) else ([rname] if rname else [])), w=[xname])
        P.act(lambda e: e.activation(junk[0:nrow, 0:1024], X[0:nrow, :], AF.Square, accum_out=small[0:nrow, 0:1]),
              r=[xname], w=["junk", "small0"])
        P.dve(lambda e: e.tensor_scalar(small[0:nrow, 1:2], small[0:nrow, 0:1], 1.0 / D_MODEL, EPS, ALU.mult, ALU.add), r=["small0"], w=["small1"])
        P.act(lambda e: e.activation(small[0:nrow, 2:3], small[0:nrow, 1:2], AF.Sqrt), r=["small1"], w=["small2"])
        P.dve(lambda e: e.reciprocal(small[0:nrow, 3:4], small[0:nrow, 2:3]), r=["small2"], w=["small3"])
        P.dve(lambda e: e.tensor_scalar(xn[0:nrow, :], X[0:nrow, :], small[0:nrow, 3:4], None, ALU.mult), r=[xname, "small3"], w=["xn"])
        psb = ps[7].bitcast(BF16)
        for c in range(8):
            P.pe(lambda e, c=c: e.transpose(psb[:, c * 128:c * 128 + nrow], xn[0:nrow, c * 128:(c + 1) * 128], identb[0:nrow, 0:nrow]),
                 r=["xn", "identb"], w=["ps7"])
        for c in range(8):
            P.dve(lambda e, c=c: e.tensor_scalar(hT[:, c, tb * 128:tb * 128 + nrow], psb[:, c * 128:c * 128 + nrow],
                                                 gcol[:, l * 8 + c:l * 8 + c + 1], None, ALU.mult),
                  r=["ps7", "gcol"], w=["hT"])

    def fm_unit(l, uname, ntok, evac, blocks=tuple(range(8)), wres=None):
        W, wn = wres if wres is not None else load_w(l, UNITS.index(uname))
        for j in blocks:
            si = nxt("s", 4)
            for c in range(8):
                P.pe(lambda e, j=j, c=c, si=si: e.matmul(ps[si][0:64, 0:ntok], W[:, c, j * 64:(j + 1) * 64], hT[:, c, 0:ntok],
                                                        start=(c == 0), stop=(c == 7)), r=[wn, "hT"], w=[psn[si]])
            evac(j, ps[si], psn[si])

    def finish_head(O, oname, zt, zname, hsel, ncol, csl):
        P.dve(lambda e: e.reciprocal(rden[64:65, 0:ncol], O[64:65, 0:ncol]), r=[oname], w=["rden"])
        bi_ = nxt("s", 4)
        P.pe(lambda e: e.matmul(ps[bi_][0:64, 0:ncol], ones1[64:65, 0:64], rden[64:65, 0:ncol], start=True, stop=True),
             r=["rden", "ones1"], w=[psn[bi_]])
        P.dve(lambda e: e.tensor_copy(otmp[:, 0:ncol], ps[bi_][0:64, 0:ncol]), r=[psn[bi_]], w=["otmp"])
        P.dve(lambda e: e.tensor_tensor(otmp[:, 0:ncol], O[0:64, 0:ncol], otmp[:, 0:ncol], ALU.mult), r=[oname, "otmp"], w=["otmp"])
        if isinstance(hsel, tuple):
            zv = zt[:, hsel[0]:hsel[1], csl]
            ov = otmp[:, 0:ncol].rearrange("p (h q) -> p h q", h=hsel[1] - hsel[0])
        else:
            zv = zt[:, hsel, csl]
            ov = otmp[:, 0:ncol]
        P.dve(lambda e: e.tensor_tensor(zv, zv, ov, ALU.mult), r=[zname, "otmp"], w=[zname])

    def load_kv(Ksrc, Vsrc, h0, nh, k0, nk, krows_name):
        i = nxt("kv", 2)
        P.dma("sp", kbuf[i][0:64, 0:nh, 0:nk], Ksrc[h0:h0 + nh, :, k0:k0 + nk].rearrange("h d k -> d h k"), r=[krows_name], w=[f"kbuf{i}"])
        nb = (nk + 127) // 128
        for b in range(nb):
            n = min(128, nk - b * 128)
            P.dma("sp", vbuf[i][0:n, b, 0:nh, 0:64],
                  Vsrc[k0 + b * 128:k0 + b * 128 + n, h0 * 64:(h0 + nh) * 64].rearrange("k (h d) -> k h d", h=nh),
                  r=[krows_name], w=[f"vbuf{i}"])
        return i

    class Attn:
        def __init__(self):
            self.pend = []

        def tile(self, t):
            si = nxt("s", 4)
            S = ps[si]; n = t["n"]; qlo, qhi = t["qlo"], t["qhi"]
            nadd = len(t["adds"])
            P.pe(lambda e: e.matmul(S[0:n, qlo:qhi], t["kT"], t["qap"], start=True, stop=(nadd == 0)),
                 r=t["names"] + ["Q"], w=[psn[si]])
            for ai, (clo, chi, la, ra, an) in enumerate(t["adds"]):
                P.pe(lambda e, clo=clo, chi=chi, la=la, ra=ra, ai=ai: e.matmul(S[0:n, clo:chi], la, ra, start=False, stop=(ai == nadd - 1)),
                     r=an, w=[psn[si]])
            pi = nxt("pt", 4)
            if t["bias"] is not None:
                P.act(lambda e: e.activation(Pt[pi][0:n, qlo:qhi], S[0:n, qlo:qhi], AF.Exp, bias=t["bias"]),
                      r=[psn[si], "nbias"], w=[f"Pt{pi}"])
            else:
                P.act(lambda e: e.activation(Pt[pi][0:n, qlo:qhi], S[0:n, qlo:qhi], AF.Exp), r=[psn[si]], w=[f"Pt{pi}"])
            t["pi"] = pi
            self.pend.append(t)
            if len(self.pend) > 2:
                self.pv(self.pend.pop(0))

        def pv(self, t):
            n = t["n"]; qlo, qhi = t["qlo"], t["qhi"]; pi = t["pi"]; O = t["O"]
            P.pe(lambda e: e.matmul(O[0:65, qlo:qhi], t["v"], Pt[pi][0:n, qlo:qhi], start=t["first"], stop=t["last"]),
                 r=[f"Pt{pi}"] + t["names"], w=[t["oname"]])

        def flush(self):
            while self.pend:
                self.pv(self.pend.pop(0))


    for l in range(nlayers):
        P.pool(lambda e: e.memset(crs[:], 0.0), w=["crs"])
        P.dma("sp", crs[:], crel_d[l].rearrange("c p h -> p c h"), w=["crs"])
        build_tab([crs[:, c, :] for c in range(3)], [ohcs[:, c, :] for c in range(3)], tabc[l], ["crs", "ohcs"])
        build_toeplitz(tabc[l], bc)
        P.dve(lambda e: e.memset(tot[:], 0.0), w=["tot"])
        P.dve(lambda e: e.memset(tots[:], 0.0), w=["tots"])
        P.dve(lambda e: e.memset(totbc[:], 0.0), w=["totbc"])
        KAl, KBl, IKl, VAl, VBl = S_KA[l], S_KB[l], S_IK[l], S_VA[l], S_VB[l]
        KCl, VCl = S_KCL[l], S_VCL[l]
        hist = f"~hist{l}"
        chist = f"~chist{l}"
        deferred_exchange = []

        def grow(gp):
            if l == 0:
                return xp[gp * 512:(gp + 1) * 512, :]
            return hpg[gp * 512:(gp + 1) * 512, :]

        def tm_unit(uname, handler, wres=None):
            W, wn = wres if wres is not None else load_w(l, UNITS.index(uname))
            for tb in range(4):
                si = nxt("s", 4)
                for c in range(8):
                    P.pe(lambda e: e.matmul(ps[si][:, :], hT[:, c, tb * 128:(tb + 1) * 128], W[:, c, :], start=(c == 0), stop=(c == 7)),
                         r=[wn, "hT"], w=[psn[si]])
                k = nxt("st", 2)
                S_ = st[k]; sn = f"st{k}"
                P.act(lambda e: e.activation(S_[:], ps[si][:], AF.Copy), r=[psn[si]], w=[sn])
                handler(tb, S_, sn, k)

        def logf_of(S_, sn):
            P.dve(lambda e: e.tensor_tensor(lfb[:], S_[:, 288:296], bfb[:, l * 8:(l + 1) * 8], ALU.add), r=[sn, "bfb"], w=["lfb"])
            P.act(lambda e: e.activation(lfb[:], lfb[:], AF.Exp, scale=-1.0), r=["lfb"], w=["lfb"])
            P.act(lambda e: e.activation(lfb[:], lfb[:], AF.Ln, bias=1.0), r=["lfb"], w=["lfb"])
            P.dve(lambda e: e.tensor_scalar(lfb[:], lfb[:], -1.0, None, ALU.mult), r=["lfb"], w=["lfb"])

        def cum_into(dst_ap, dname):
            P.pe(lambda e: e.matmul(ps[5][:, 0:8], TRI, lfb[:], start=True, stop=False), r=["lfb", "cstf"], w=["ps5"])
            P.pe(lambda e: e.matmul(ps[5][:, 0:8], ones1[0:1, 0:128], tot[0:1, :], start=False, stop=True), r=["tot", "ones1"], w=["ps5"])
            P.dve(lambda e: e.tensor_copy(dst_ap, ps[5][:, 0:8]), r=["ps5"], w=[dname])
            P.pe(lambda e: e.matmul(ps[5][0:1, 8:16], E127, dst_ap, start=True, stop=True), r=[dname, "cstf"], w=["ps5"])
            P.dve(lambda e: e.tensor_copy(tot[:], ps[5][0:1, 8:16]), r=["ps5"], w=["tot"])

        def evac_k_to(dst3, k0, hname):
            def f(j, pt, pn):
                P.act(lambda e: e.activation(kst[:, j, :], pt[0:64, :], AF.Copy), r=[pn], w=["kst"])
                if j == 7:
                    P.dma("pool", dst3[:, :, k0:k0 + 512].rearrange("h d k -> d h k"), kst[:], r=["kst"], w=[hname])
            return f

        def evac_q(scale):
            def f(j, pt, pn):
                P.act(lambda e: e.activation(Q[0:64, j, :], pt[0:64, :], AF.Copy, scale=scale), r=[pn], w=["Q"])
            return f

        def evac_z(zt, zn):
            def f(j, pt, pn):
                P.act(lambda e: e.activation(zt[:, j, :], pt[0:64, :], AF.Silu), r=[pn], w=[zn])
            return f

        scb = SC.bitcast(BF16)
        kres = {}
        for ui, un in enumerate(("tka", "tva", "tb", "ka")):
            v = scb[:, ui * 4096:(ui + 1) * 4096].rearrange("p (c n) -> p c n", c=8)
            P.dma("sp", v, wub[l, UNITS.index(un)], r=[f"wub{l}_{UNITS.index(un)}"], w=["SC"])
            kres[un] = (v, "SC")
        v = junk[:, 4096:8192].rearrange("p (c n) -> p c n", c=8)
        P.dma("sp", v, wub[l, UNITS.index("bx")], r=[f"wub{l}_{UNITS.index('bx')}"], w=["junk", "junkW"])
        kres["bx"] = (v, "junkW")
        for gp in range(4 * nm):
            t0 = gp * 512
            src = grow(gp)
            for tb in range(4):
                norm_block(l, src[tb * 128:(tb + 1) * 128, :], tb, rname=("~hpgw" if l > 0 else None))

            def h_kv(uname):
                def f(tb, S_, sn, k):
                    r0 = t0 + tb * 128
                    dst = {"tka": o_ak, "tva": o_av, "tkc": o_ck, "tvc": o_cv}[uname]
                    if uname in ("tka", "tva"):
                        P.dma("pool", dst[l, r0:r0 + 128, :], S_[:], r=[sn])
                    else:
                        P.dma("pool", dst[l, r0 - (SEQ - 512):r0 - (SEQ - 512) + 128, :], S_[:], r=[sn])
                    if uname == "tva":
                        V_, vn = vst[k], f"vst{k}"
                        P.dve(lambda e: e.tensor_copy(V_[:], S_[:]), r=[sn], w=[vn])
                        P.dma("pool", VAl[r0:r0 + 128, :], V_[:], r=[vn], w=[hist])
                return f

            def h_tb(tb, S_, sn, k):
                r0 = t0 + tb * 128
                P.dma("pool", o_bk[l, r0:r0 + 128, :], S_[:, 0:128], r=[sn])
                P.dma("pool", o_bv[l, r0:r0 + 128, :], S_[:, 128:256], r=[sn])
                P.dma("pool", o_ik[l, r0:r0 + 128, :], S_[:, 256:288], r=[sn])
                V_, vn = vst[k], f"vst{k}"
                P.dve(lambda e: e.tensor_copy(V_[:, 0:128], S_[:, 128:256]), r=[sn], w=[vn])
                P.dma("pool", VBl[r0:r0 + 128, :], V_[:, 0:128], r=[vn], w=[hist])
                P.dve(lambda e: e.tensor_tensor(lf4[:, tb, :], S_[:, 288:296], bfb[:, l * 8:(l + 1) * 8], ALU.add), r=[sn, "bfb"], w=["lf4"])

            tm_unit("tka", h_kv("tka"), wres=kres["tka"])
            tm_unit("tva", h_kv("tva"), wres=kres["tva"])
            tm_unit("tb", h_tb, wres=kres["tb"])
            lf4f = lf4[:].rearrange("p b h -> p (b h)")
            P.act(lambda e: e.activation(lf4f, lf4f, AF.Exp, scale=-1.0), r=["lf4"], w=["lf4"])
            P.act(lambda e: e.activation(lf4f, lf4f, AF.Ln, bias=1.0), r=["lf4"], w=["lf4"])
            P.dve(lambda e: e.tensor_scalar(lf4f, lf4f, -1.0, None, ALU.mult), r=["lf4"], w=["lf4"])
            P.dma("pool", o_lf[l, t0:t0 + 512, :].rearrange("(b p) h -> p b h", p=128), lf4[:], r=["lf4"])
            if gp == NG - 1:
                tm_unit("tkc", h_kv("tkc"))
                tm_unit("tvc", h_kv("tvc"))
            fm_unit(l, "ka", 512, evac_k_to(KAl, t0, hist), wres=kres["ka"])
            for b_ in range(4):
                for b2 in range(b_ + 1):
                    P.pe(lambda e: e.matmul(ps[5][:, b_ * 8:(b_ + 1) * 8], (TRI if b2 == b_ else onesf[:, :]), lf4[:, b2, :], start=(b2 == 0), stop=(b2 == b_)),
                         r=["lf4", "cstf", "onesf"], w=["ps5"])
            for b2 in range(4):
                P.pe(lambda e: e.matmul(ps[5][:, 32:40], onesf[:, :], lf4[:, b2, :], start=(b2 == 0), stop=(b2 == 3)), r=["lf4", "onesf"], w=["ps5"])
            for b_ in range(4):
                P.dve(lambda e: e.tensor_tensor(cstore[:, 4 * gp + b_, :], ps[5][:, b_ * 8:(b_ + 1) * 8], totbc[:, :], ALU.add), r=["ps5", "totbc"], w=["cstore"])
            P.dve(lambda e: e.tensor_tensor(totbc[:, :], ps[5][:, 32:40], totbc[:, :], ALU.add), r=["ps5", "totbc"], w=["totbc"])
            P.dve(lambda e: e.tensor_copy(tots[0:1, gp + 1, :], totbc[0:1, :]), r=["totbc"], w=["tots"])

            def evac_bx_k(j, pt, pn):
                if j < 2:
                    P.act(lambda e: e.activation(kst[:, j, :], pt[0:64, :], AF.Copy), r=[pn], w=["kst"])
                    if j == 1:
                        P.dma("pool", KBl[:, :, t0:t0 + 512].rearrange("h d k -> d h k"), kst[:, 0:2, :], r=["kst"], w=[hist])
                elif j == 6:
                    P.act(lambda e: e.activation(kst[:, 2, :], pt[0:64, :], AF.Copy), r=[pn], w=["kst"])
                    P.dma("pool", IKl[:, t0:t0 + 512], kst[:, 2, :], r=["kst"], w=[hist])
            fm_unit(l, "bx", 512, evac_bx_k, blocks=(0, 1, 6), wres=kres["bx"])

        for m in range(nm):
            own = (xq[m] if l == 0 else hp1q[m * 512:(m + 1) * 512, :])
            own_r = (None if l == 0 else [f"hp1qc{2 * m}", f"hp1qc{2 * m + 1}"])
            for part in range(2):
                for tb in range(4):
                    if part == 1:
                        norm_block(l, own[tb * 128:(tb + 1) * 128, :], tb, rname=own_r)
                    elif l == 0:
                        norm_block(l, xprev[m][tb * 128:(tb + 1) * 128, :], tb)
                    else:
                        first = True
                        for r in range(4):
                            gq = 4 * m - 1 + r
                            if gq < 0:
                                continue
                            xi = nxt("x", 2)
                            X = xt[xi]; xname = f"xt{xi}"
                            P.dma("sp", X[:], grow(gq)[tb * 128:(tb + 1) * 128, :], r=["~hpgw"], w=[xname])
                            if first:
                                P.dve(lambda e: e.tensor_scalar(xacc[:], X[:], selb[:, r:r + 1], None, ALU.mult), r=[xname, "selb"], w=["xacc"])
                            else:
                                P.dve(lambda e: e.scalar_tensor_tensor(xacc[:], X[:], selb[:, r:r + 1], xacc[:], ALU.mult, ALU.add), r=[xname, "selb", "xacc"], w=["xacc"])
                            first = False
                        norm_block(l, None, tb, sb_src=(xacc, "xacc"))

                def h_c(uname):
                    def f(tb, S_, sn, k):
                        if uname == "tvc":
                            V_, vn = vst[k], f"vst{k}"
                            P.dve(lambda e: e.tensor_copy(V_[:], S_[:]), r=[sn], w=[vn])
                            P.dma("pool", VCl[part * 512 + tb * 128:part * 512 + (tb + 1) * 128, :], V_[:], r=[vn], w=[chist])
                    return f
                tm_unit("tvc", h_c("tvc"))
                fm_unit(l, "kc", 512, evac_k_to(KCl, part * 512, chist))
            g0 = 16 * m
            nkb = 16 * m + 16
            for r in range(4):
                if r == 0:
                    P.dve(lambda e: e.tensor_scalar(tot[0:1, :], tots[0:1, 4 * m + r, :], selb[0:1, r:r + 1], None, ALU.mult), r=["tots", "selb"], w=["tot"])
                else:
                    P.dve(lambda e: e.scalar_tensor_tensor(tot[0:1, :], tots[0:1, 4 * m + r, :], selb[0:1, r:r + 1], tot[0:1, :], ALU.mult, ALU.add),
                          r=["tots", "selb", "tot"], w=["tot"])

            def h_own(tb, S_, sn, k):
                P.dve(lambda e: e.tensor_tensor(lf4[:, tb, :], S_[:, 288:296], bfb[:, l * 8:(l + 1) * 8], ALU.add), r=[sn, "bfb"], w=["lf4"])
                P.dve(lambda e: e.tensor_scalar(wsgn[:, tb, :], S_[:, 296:304], 0.0, 2.0, ALU.is_ge, ALU.mult), r=[sn], w=["wsgn"])
                P.dve(lambda e: e.tensor_scalar(wsgn[:, tb, :], wsgn[:, tb, :], -1.0, None, ALU.add), r=["wsgn"], w=["wsgn"])
                P.dve(lambda e: e.scalar_tensor_tensor(wabs[:, tb, :], S_[:, 296:304], IDXS, wsgn[:, tb, :], ALU.mult, ALU.mult), r=[sn, "wsgn"], w=["wabs"])
            tm_unit("tb", h_own)
            lf4q = lf4[:].rearrange("p b h -> p (b h)")
            P.act(lambda e: e.activation(lf4q, lf4q, AF.Exp, scale=-1.0), r=["lf4"], w=["lf4"])
            P.act(lambda e: e.activation(lf4q, lf4q, AF.Ln, bias=1.0), r=["lf4"], w=["lf4"])
            P.dve(lambda e: e.tensor_scalar(lf4q, lf4q, -1.0, None, ALU.mult), r=["lf4"], w=["lf4"])
            for b_ in range(4):
                P.pe(lambda e: e.matmul(ps[5][:, b_ * 8:(b_ + 1) * 8], ones1[0:1, 0:128], tot[0:1, :], start=True, stop=False), r=["tot", "ones1"], w=["ps5"])
                for b2 in range(b_ + 1):
                    P.pe(lambda e: e.matmul(ps[5][:, b_ * 8:(b_ + 1) * 8], (TRI if b2 == b_ else onesf[:, :]), lf4[:, b2, :], start=False, stop=(b2 == b_)),
                         r=["lf4", "cstf", "onesf"], w=["ps5"])
            P.dve(lambda e: e.tensor_copy(cown[:].rearrange("p b h -> p (b h)"), ps[5][:, 0:32]), r=["ps5"], w=["cown"])
            P.pe(lambda e: e.matmul(ps[5][:, 16:24], E0ROW, cstore[:, g0, :], start=True, stop=True), r=["cstore", "cstf"], w=["ps5"])
            P.dve(lambda e: e.tensor_copy(cbc[:], ps[5][:, 16:24]), r=["ps5"], w=["cbc"])
            for h in range(8):
                P.dve(lambda e: e.tensor_scalar(nbias[:, 0:nkb, h], cstore[:, 0:nkb, h], cbc[:, h:h + 1], -1.0, ALU.subtract, ALU.mult),
                      r=["cstore", "cbc"], w=["nbias"])
            for tb in range(4):
                P.dve(lambda e: e.tensor_tensor(rq[:, tb, :], cown[:, tb, :], cbc[:], ALU.subtract), r=["cown", "cbc"], w=["rq"])
                P.pe(lambda e: e.transpose(ps[6][0:8, tb * 128:(tb + 1) * 128], rq[:, tb, :], IDENT), r=["rq", "cstf"], w=["ps6"])
            P.dve(lambda e: e.tensor_copy(rT[:], ps[6][0:8, :]), r=["ps6"], w=["rT"])

            def evac_bx_q(j, pt, pn):
                P.act(lambda e: e.activation(iqT[:, j - 2, :], pt[0:64, :], AF.Copy), r=[pn], w=["iqT"])

            def a_half(half):
                at = Attn()
                for sbk in range(4 * m + 4):
                    bi = load_kv(KAl, VAl, half * 4, 4, sbk * 512, 512, hist)
                    trail = (sbk >= 4 * m)
                    for kb in range(4):
                        adds = []
                        if trail:
                            rr = (sbk - 4 * m) * 4 + kb
                            mi = nxt("mb", 2)
                            P.dma("sp", Mb[mi][:, :], AMASK[rr], r=["~amask"], w=[f"Mb{mi}"])
                            adds = [(0, 512, identb[:], Mb[mi][:, :], ["identb", f"Mb{mi}"])]
                        for i in range(4):
                            hh = half * 4 + i
                            at.tile(dict(kT=kbuf[bi][0:65, i, kb * 128:(kb + 1) * 128], v=vbuf[bi][:, kb, i, 0:65], n=128, qlo=0, qhi=512,
                                         qap=Q[0:65, hh, 0:512], adds=adds, bias=nbias[:, 4 * sbk + kb, hh:hh + 1],
                                         names=[f"kbuf{bi}", f"vbuf{bi}"], O=ps[4 + i], oname=psn[4 + i],
                                         first=(sbk == 0 and kb == 0), last=(sbk == 4 * m + 3 and kb == 3)))
                at.flush()
                for i in range(4):
                    finish_head(ps[4 + i], psn[4 + i], zg["a"], "zga", half * 4 + i, 512, slice(0, 512))

            def c_half(half):
                at = Attn()
                started = [False] * 4
                for sbl in range(2):
                    bi = load_kv(KCl, VCl, half * 4, 4, sbl * 512, 512, chist)
                    for kb in range(4):
                        r_ = 4 * sbl + kb
                        qb_lo, qb_hi = max(0, r_ - 4), min(3, r_)
                        qlo, qhi = qb_lo * 128, (qb_hi + 1) * 128
                        for i in range(4):
                            hh = half * 4 + i
                            adds = []
                            for qb in range(qb_lo, qb_hi + 1):
                                dl = r_ - 4 - qb
                                c0, c1 = qb * 128, (qb + 1) * 128
                                if dl == 0:
                                    adds.append((c0, c1, identb[:], bc[:, 0, hh, :], ["identb", "btile"]))
                                    adds.append((c0, c1, identb[:], cm0b[:], ["identb", "cm0b"]))
                                elif dl == -1:
                                    adds.append((c0, c1, identb[:], bc[:, 1, hh, :], ["identb", "btile"]))
                                elif dl == -4:
                                    adds.append((c0, c1, identb[:], cm4b[:], ["identb", "cm4b"]))
                            at.tile(dict(kT=kbuf[bi][0:64, i, kb * 128:(kb + 1) * 128], v=vbuf[bi][:, kb, i, 0:65], n=128, qlo=qlo, qhi=qhi,
                                         qap=Q[0:64, hh, qlo:qhi], adds=adds, bias=(pvb[:, 0:1] if (m == 0 and sbl == 0) else None),
                                         names=[f"kbuf{bi}", f"vbuf{bi}"], O=ps[4 + i], oname=psn[4 + i],
                                         first=(not started[i]), last=(sbl == 1 and kb == 3)))
                            started[i] = True
                at.flush()
                for i in range(4):
                    finish_head(ps[4 + i], psn[4 + i], zg["c"], "zgc", half * 4 + i, 512, slice(0, 512))

            def b_topk(qb):
                NK = (16 * m + 13 + qb) * 128
                qs = slice(qb * 128, (qb + 1) * 128)
                for k0 in range(0, NK, 512):
                    nk = min(512, NK - k0)
                    ii = nxt("ik", 2)
                    P.dma("sp", ikbuf[ii][:, 0:nk], IKl[:, k0:k0 + nk], r=[hist], w=[f"ikbuf{ii}"])
                    for h in range(8):
                        base = 32 * (h % 2)
                        pi_ = 1 + nxt("ips", 3)
                        P.pe(lambda e: e.matmul(ps[pi_][:, 0:nk], iqT[base:base + 32, h // 2, qs], ikbuf[ii][base:base + 32, 0:nk], start=True, stop=True),
                             r=["iqT", f"ikbuf{ii}"], w=[psn[pi_]])
                        ri = nxt("rr", 3)
                        P.act(lambda e: e.activation(Rr[ri][:, 0:nk], ps[pi_][:, 0:nk], AF.Relu, scale=wabs[:, qb, h:h + 1]), r=[psn[pi_], "wabs"], w=[f"Rr{ri}"])
                        if h == 0:
                            P.dve(lambda e: e.tensor_scalar(SC[:, k0:k0 + nk], Rr[ri][:, 0:nk], wsgn[:, qb, 0:1], None, ALU.mult), r=[f"Rr{ri}", "wsgn"], w=["SC"])
                        else:
                            P.dve(lambda e: e.scalar_tensor_tensor(SC[:, k0:k0 + nk], Rr[ri][:, 0:nk], wsgn[:, qb, h:h + 1], SC[:, k0:k0 + nk], ALU.mult, ALU.add),
                                  r=[f"Rr{ri}", "wsgn", "SC"], w=["SC"])
                for ch in range(4):
                    c0 = g0 * 128 + ch * 512
                    wd = min(512, NK - c0)
                    if wd <= 0:
                        continue
                    ri = nxt("rr", 3)
                    P.dve(lambda e: e.tensor_scalar(Rr[ri][:, 0:wd], iota5[:, 0:wd], qlimc[:, qb * 4 + ch:qb * 4 + ch + 1], -1e30, ALU.is_ge, ALU.mult),
                          r=["iota5", "qlimc"], w=[f"Rr{ri}"])
                    P.dve(lambda e: e.tensor_tensor(SC[:, c0:c0 + wd], SC[:, c0:c0 + wd], Rr[ri][:, 0:wd], ALU.add), r=["SC", f"Rr{ri}"], w=["SC"])
                P.dve(lambda e: e.memset(cntb[:], 0.0), w=["cntb"])
                P.dve(lambda e: e.memset(small[:, 8:9], 0.0), w=["cand"])
                for it in range(NBIS):
                    stp = 64.0 * (0.5 ** it)
                    P.dve(lambda e: e.tensor_scalar(junk[:, 0:NK], SC[:, 0:NK], small[:, 8:9], 0.0, ALU.is_ge, ALU.add, accum_out=cntb[:, it:it + 1]),
                          r=["SC", "cand", "cntb"], w=["junk", "cntb"])
                    a, b_ = (stp, -0.5 * stp) if it < NBIS - 1 else (stp, -stp)
                    P.dve(lambda e: e.tensor_scalar(small[:, 9:10], cntb[:, it:it + 1], float(TOPK), a, ALU.is_ge, ALU.mult), r=["cntb"], w=["fl"])
                    P.dve(lambda e: e.scalar_tensor_tensor(small[:, 8:9], small[:, 9:10], b_, small[:, 8:9], ALU.add, ALU.add), r=["fl", "cand"], w=["cand"])
                P.dve(lambda e: e.tensor_scalar(junk[:, 0:NK], SC[:, 0:NK], small[:, 8:9], NEG, ALU.is_lt, ALU.mult), r=["SC", "cand"], w=["junk"])
                P.dma("pool", MBS[qb][:, 0:NK], junk[:, 0:NK], r=["junk"], w=[f"mbs{qb}"])

            def b_attn(qb):
                qs = slice(qb * 128, (qb + 1) * 128)
                nkq = 16 * m + 13 + qb
                at = Attn()
                for sbk in range((nkq + 3) // 4):
                    k0 = sbk * 512
                    nk = min(512, nkq * 128 - k0)
                    mi = nxt("mb", 2)
                    P.dma("sp", Mb[mi][:, 0:nk], MBS[qb][:, k0:k0 + nk], r=[f"mbs{qb}"], w=[f"Mb{mi}"])
                    bi = load_kv(KBl, VBl, 0, 2, k0, nk, hist)
                    for kb in range(nk // 128):
                        gkb = sbk * 4 + kb
                        for jj in range(2):
                            adds = [(0, 512, Mb[mi][:, kb * 128:(kb + 1) * 128], i4b[:].rearrange("p a b -> p (a b)"), [f"Mb{mi}", "i4b"])]
                            if gkb >= g0 - 1:
                                rr = gkb - g0 + 1
                                ci = nxt("cmb", 2)
                                sc0 = sel01[:, (qb * 17 + rr) * 2:(qb * 17 + rr) * 2 + 1]
                                sc1 = sel01[:, (qb * 17 + rr) * 2 + 1:(qb * 17 + rr) * 2 + 2]
                                P.dma("sp", comb[ci][:].rearrange("p a b -> p (a b)"), COMBS[qb, rr, jj], r=["~combs"], w=[f"comb{ci}"])
                                adds.append((0, 512, identb[:], comb[ci][:].rearrange("p a b -> p (a b)"), ["identb", f"comb{ci}"]))
                            at.tile(dict(kT=kbuf[bi][0:64, jj, kb * 128:(kb + 1) * 128], v=vbuf[bi][:, kb, jj, 0:65], n=128, qlo=0, qhi=512,
                                         qap=Q[0:64, 4 * jj:4 * jj + 4, qs], adds=adds, bias=None,
                                         names=[f"kbuf{bi}", f"vbuf{bi}"], O=ps[4 + jj], oname=psn[4 + jj],
                                         first=(gkb == 0), last=(gkb == nkq - 1)))
                at.flush()
                for jj in range(2):
                    finish_head(ps[4 + jj], psn[4 + jj], zg["b"], "zgb", (4 * jj, 4 * jj + 4), 512, qs)

            fm_unit(l, "bx", 512, evac_bx_q, blocks=(2, 3, 4, 5))
            b_topk(0)
            fm_unit(l, "za", 512, evac_z(zg["a"], "zga"))
            fm_unit(l, "qa", 512, evac_q(SCALE))
            for h in range(8):
                P.dma("sp", Q[64:65, h, :], rT[h:h + 1, :], r=["rT"], w=["Q"])
            a_half(0)
            b_topk(1)
            a_half(1)
            fm_unit(l, "zc", 512, evac_z(zg["c"], "zgc"))
            fm_unit(l, "qc", 512, evac_q(SCALE))
            c_half(0)
            c_half(1)
            fm_unit(l, "zb", 512, evac_z(zg["b"], "zgb"))
            fm_unit(l, "qb", 512, evac_q(SCALE))
            b_attn(0)
            b_topk(2)
            b_attn(1)
            b_topk(3)
            b_attn(2)
            b_attn(3)

            allz = [zg["a"], zg["b"], zg["c"]]
            alln = ["zga", "zgb", "zgc"]
            for u in range(6):
                Wo_, won = load_wo(l, u)
                for hq in range(4):
                    hidx = u * 4 + hq
                    zt, zn = allz[hidx // 8], alln[hidx // 8]
                    for tb in range(4):
                        for n_ in range(2):
                            P.pe(lambda e: e.matmul(ps[tb * 2 + n_][:, :], zt[:, hidx % 8, tb * 128:(tb + 1) * 128], Wo_[:, hq, n_ * 512:(n_ + 1) * 512],
                                                    start=(hidx == 0), stop=(hidx == 23)), r=[zn, won], w=[psn[tb * 2 + n_]])
            for tb in range(4):
                xi = nxt("x", 2)
                X = xt[xi]; xname = f"xt{xi}"
                P.dma("sp", X[:], own[tb * 128:(tb + 1) * 128, :], r=(own_r if own_r else []), w=[xname])
                for n_ in range(2):
                    P.dve(lambda e: e.tensor_tensor(X[:, n_ * 512:(n_ + 1) * 512], X[:, n_ * 512:(n_ + 1) * 512], ps[tb * 2 + n_][:, :], ALU.add),
                          r=[xname, psn[tb * 2 + n_]], w=[xname])
                r0 = m * 512 + tb * 128
                if l < nlayers - 1:
                    P.dma("pool", hp1q[r0:r0 + 128, :], X[:], r=[xname], w=[f"hp1qc{r0 // 256}"])
                else:
                    P.act(lambda e: e.activation(junk[:, 0:1024], X[:], AF.Square, accum_out=small[:, 16:17]), r=[xname], w=["junk", "fs0"])
                    P.dve(lambda e: e.tensor_scalar(small[:, 17:18], small[:, 16:17], 1.0 / D_MODEL, EPS, ALU.mult, ALU.add), r=["fs0"], w=["fs1"])
                    P.act(lambda e: e.activation(small[:, 18:19], small[:, 17:18], AF.Sqrt), r=["fs1"], w=["fs2"])
                    P.dve(lambda e: e.reciprocal(small[:, 19:20], small[:, 18:19]), r=["fs2"], w=["fs3"])
                    P.dve(lambda e: e.scalar_tensor_tensor(X[:], X[:], small[:, 19:20], fgb[:], ALU.mult, ALU.mult), r=[xname, "fs3", "fgb"], w=[xname])
                    P.dma("pool", y_q[r0:r0 + 128, :], X[:], r=[xname])
            def emit_exchange(m=m):
                for hf in range(2):
                    cidx = 2 * m + hf
                    P.dma("pool", ccs, hp1q[cidx * 256:(cidx + 1) * 256, :], r=[f"hp1qc{cidx}"], w=["ccs"])
                    P.add("pool", lambda e: e.collective_compute("AllGather", ALU.bypass, replica_groups=[[0, 1, 2, 3], [4, 5, 6, 7]],
                                                                 ins=[ccs.opt()], outs=[ccd.opt()]), r=["ccs"], w=["ccd"], cc=True)
                    for r in range(4):
                        a0 = (4 * m + r) * 512 + hf * 256
                        P.dma("pool", hpg[a0:a0 + 256, :], ccd[r * 256:(r + 1) * 256, :], r=["ccd"], w=["~hpgw"])
            if l < nlayers - 1:
                if m < nm - 1:
                    emit_exchange()
                else:
                    deferred_exchange.append(emit_exchange)
        if do_sample:
            NS = DEC_SEQ
            shist = f"~shist{l}"
            psb7 = ps[7].bitcast(BF16)

            def prep_cache(src2d, nrows, ncols, kdst, vdst, nheads):
                for b in range(nrows // 128):
                    xi = nxt("x", 2)
                    X = xt[xi]; xname = f"xt{xi}"
                    P.dma("sp", X[:, 0:ncols], src2d[b * 128:(b + 1) * 128, :], w=[xname])
                    P.dve(lambda e: e.tensor_copy(xn[:, 0:ncols], X[:, 0:ncols]), r=[xname], w=["xn"])
                    if vdst is not None:
                        P.dma("pool", vdst[b * 128:(b + 1) * 128, :], xn[:, 0:ncols], r=["xn"], w=[shist])
                    if kdst is not None:
                        for hh in range(nheads):
                            P.pe(lambda e: e.transpose(psb7[0:64, hh * 128:(hh + 1) * 128], xn[:, hh * 64:(hh + 1) * 64], identb[:]),
                                 r=["xn", "identb"], w=["ps7"])
                        P.act(lambda e: e.activation(kst[:, 0:nheads, 0:128], psb7[0:64, 0:nheads * 128].rearrange("p (h k) -> p h k", h=nheads), AF.Copy),
                              r=["ps7"], w=["kst"])
                        P.dma("pool", kdst[:, :, b * 128:(b + 1) * 128].rearrange("h d k -> d h k"), kst[:, 0:nheads, 0:128], r=["kst"], w=[shist])

            prep_cache(ca_k[l], PAST, 512, SS_KA[l], None, 8)
            prep_cache(ca_v[l], PAST, 512, None, SS_VA[l], 8)
            prep_cache(cb_k[l], PAST, 128, SS_KB[l], None, 2)
            prep_cache(cb_v[l], PAST, 128, None, SS_VB[l], 2)
            prep_cache(cc_k[l], 512, 512, SS_KC[l], None, 8)
            prep_cache(cc_v[l], 512, 512, None, SS_VC[l], 8)
            for b in range(PAST // 128):
                xi = nxt("x", 2)
                X = xt[xi]; xname = f"xt{xi}"
                P.dma("sp", X[:, 0:32], cb_ik[l, b * 128:(b + 1) * 128, :], w=[xname])
                P.dve(lambda e: e.tensor_copy(xn[:, 0:32], X[:, 0:32]), r=[xname], w=["xn"])
                P.dve(lambda e: e.tensor_copy(xn[:, 32:64], X[:, 0:32]), r=[xname], w=["xn"])
                P.pe(lambda e: e.transpose(psb7[0:64, 0:128], xn[:, 0:64], identb[:]), r=["xn", "identb"], w=["ps7"])
                P.act(lambda e: e.activation(kst[:, 0, 0:128], psb7[0:64, 0:128], AF.Copy), r=["ps7"], w=["kst"])
                P.dma("pool", SS_IK[l][:, b * 128:(b + 1) * 128], kst[:, 0, 0:128], r=["kst"], w=[shist])
            P.dve(lambda e: e.memset(tot[:], 0.0), w=["tot"])

            def cum_block(kb_, n):
                P.pe(lambda e: e.matmul(ps[5][0:n, 0:8], cstf[0:n, 2, 0:n], lfb[0:n, :], start=True, stop=False), r=["lfb", "cstf"], w=["ps5"])
                P.pe(lambda e: e.matmul(ps[5][0:n, 0:8], ones1[0:1, 0:n], tot[0:1, :], start=False, stop=True), r=["tot", "ones1"], w=["ps5"])
                P.dve(lambda e: e.tensor_copy(cstore[0:n, kb_, :], ps[5][0:n, 0:8]), r=["ps5"], w=["cstore"])
                P.pe(lambda e: e.matmul(ps[5][0:1, 8:16], cstf[0:n, 1, 128 - n:129 - n], cstore[0:n, kb_, :], start=True, stop=True), r=["cstore", "cstf"], w=["ps5"])
                P.dve(lambda e: e.tensor_copy(tot[:], ps[5][0:1, 8:16]), r=["ps5"], w=["tot"])

            for b in range(PAST // 128):
                P.dma("sp", lfb[:], ca_lf[l, b * 128:(b + 1) * 128, :], w=["lfb"])
                cum_block(b, 128)
            norm_block(l, (xs if l == 0 else hs1)[:, :], 0, nrow=NS, rname=("hs1" if l > 0 else None))
            for uname in TM_UNITS:
                W, wn = load_w(l, UNITS.index(uname))
                si = nxt("s", 4)
                for c in range(8):
                    P.pe(lambda e: e.matmul(ps[si][0:NS, :], hT[:, c, 0:NS], W[:, c, :], start=(c == 0), stop=(c == 7)), r=[wn, "hT"], w=[psn[si]])
                k = nxt("st", 2)
                S_ = st[k]; sn = f"st{k}"
                P.act(lambda e: e.activation(S_[0:NS, :], ps[si][0:NS, :], AF.Copy), r=[psn[si]], w=[sn])
                V_, vn = vst[k], f"vst{k}"
                if uname in ("tka", "tva", "tkc", "tvc"):
                    dst = {"tka": s_ak, "tva": s_av, "tkc": s_ck, "tvc": s_cv}[uname]
                    P.dma("pool", dst[l], S_[0:NS, :], r=[sn])
                    if uname in ("tva", "tvc"):
                        P.dve(lambda e: e.tensor_copy(V_[0:NS, :], S_[0:NS, :]), r=[sn], w=[vn])
                        vd = SS_VA[l][PAST:PAST + NS, :] if uname == "tva" else SS_VC[l][512:512 + NS, :]
                        P.dma("pool", vd, V_[0:NS, :], r=[vn], w=[shist])
                else:
                    P.dma("pool", s_bk[l], S_[0:NS, 0:128], r=[sn])
                    P.dma("pool", s_bv[l], S_[0:NS, 128:256], r=[sn])
                    P.dma("pool", s_ik[l], S_[0:NS, 256:288], r=[sn])
                    P.dve(lambda e: e.tensor_copy(V_[0:NS, 0:128], S_[0:NS, 128:256]), r=[sn], w=[vn])
                    P.dma("pool", SS_VB[l][PAST:PAST + NS, :], V_[0:NS, 0:128], r=[vn], w=[shist])
                    P.dve(lambda e: e.tensor_tensor(lfb[0:NS, :], S_[0:NS, 288:296], bfb[0:NS, l * 8:(l + 1) * 8], ALU.add), r=[sn, "bfb"], w=["lfb"])
                    P.act(lambda e: e.activation(lfb[0:NS, :], lfb[0:NS, :], AF.Exp, scale=-1.0), r=["lfb"], w=["lfb"])
                    P.act(lambda e: e.activation(lfb[0:NS, :], lfb[0:NS, :], AF.Ln, bias=1.0), r=["lfb"], w=["lfb"])
                    P.dve(lambda e: e.tensor_scalar(lfb[0:NS, :], lfb[0:NS, :], -1.0, None, ALU.mult), r=["lfb"], w=["lfb"])
                    P.dma("pool", s_lf[l], lfb[0:NS, :], r=["lfb"])
                    cum_block(8, NS)
                    P.dve(lambda e: e.tensor_scalar(wsgn[0:NS, 0, :], S_[0:NS, 296:304], 0.0, 2.0, ALU.is_ge, ALU.mult), r=[sn], w=["wsgn"])
                    P.dve(lambda e: e.tensor_scalar(wsgn[0:NS, 0, :], wsgn[0:NS, 0, :], -1.0, None, ALU.add), r=["wsgn"], w=["wsgn"])
                    P.dve(lambda e: e.scalar_tensor_tensor(wabs[0:NS, 0, :], S_[0:NS, 296:304], IDXS, wsgn[0:NS, 0, :], ALU.mult, ALU.mult), r=[sn, "wsgn"], w=["wabs"])
            P.pe(lambda e: e.matmul(ps[5][:, 16:24], cstf[0:NS, 3, :], cstore[0:NS, 8, :], start=True, stop=True), r=["cstore", "cstf"], w=["ps5"])
            P.dve(lambda e: e.tensor_copy(cbc[:], ps[5][:, 16:24]), r=["ps5"], w=["cbc"])
            for h in range(8):
                P.dve(lambda e: e.tensor_scalar(nbias[:, 0:9, h], cstore[:, 0:9, h], cbc[:, h:h + 1], -1.0, ALU.subtract, ALU.mult), r=["cstore", "cbc"], w=["nbias"])
            P.dve(lambda e: e.tensor_tensor(rq[0:NS, 0, :], cstore[0:NS, 8, :], cbc[0:NS, :], ALU.subtract), r=["cstore", "cbc"], w=["rq"])
            P.pe(lambda e: e.transpose(ps[6][0:8, 0:NS], rq[0:NS, 0, :], cstf[0:NS, 0, 0:NS]), r=["rq", "cstf"], w=["ps6"])
            P.dve(lambda e: e.tensor_copy(rT[:, 0:NS], ps[6][0:8, 0:NS]), r=["ps6"], w=["rT"])

            def s_evac_k(dst3, koff):
                def f(j, pt, pn):
                    P.act(lambda e: e.activation(kst[:, j, 0:NS], pt[0:64, 0:NS], AF.Copy), r=[pn], w=["kst"])
                    if j == 7:
                        P.dma("pool", dst3[:, :, koff:koff + NS].rearrange("h d k -> d h k"), kst[:, :, 0:NS], r=["kst"], w=[shist])
                return f

            def s_evac_q(j, pt, pn):
                P.act(lambda e: e.activation(Q[0:64, j, 0:NS], pt[0:64, 0:NS], AF.Copy, scale=SCALE), r=[pn], w=["Q"])

            def s_evac_z(zt, zn):
                def f(j, pt, pn):
                    P.act(lambda e: e.activation(zt[:, j, 0:NS], pt[0:64, 0:NS], AF.Silu), r=[pn], w=[zn])
                return f

            fm_unit(l, "ka", NS, s_evac_k(SS_KA[l], PAST))
            fm_unit(l, "za", NS, s_evac_z(zg["a"], "zga"))
            fm_unit(l, "qa", NS, s_evac_q)
            for h in range(8):
                P.dma("sp", Q[64:65, h, 0:NS], rT[h:h + 1, 0:NS], r=["rT"], w=["Q"])
            for half in range(2):
                at = Attn()
                for sbk in range(3):
                    nk = 512 if sbk < 2 else NS
                    bi = load_kv(SS_KA[l], SS_VA[l], half * 4, 4, sbk * 512, nk, shist)
                    for kb in range((nk + 127) // 128):
                        n = min(128, nk - kb * 128)
                        adds = [(0, NS, identb[0:NS, 0:NS], ma0b[0:NS, 0:NS], ["identb", "ma0b"])] if sbk == 2 else []
                        for i in range(4):
                            hh = half * 4 + i
                            at.tile(dict(kT=kbuf[bi][0:65, i, kb * 128:kb * 128 + n], v=vbuf[bi][0:n, kb, i, 0:65], n=n, qlo=0, qhi=NS,
                                         qap=Q[0:65, hh, 0:NS], adds=adds, bias=nbias[0:n, 4 * sbk + kb, hh:hh + 1],
                                         names=[f"kbuf{bi}", f"vbuf{bi}"], O=ps[4 + i], oname=psn[4 + i],
                                         first=(sbk == 0 and kb == 0), last=(sbk == 2)))
                at.flush()
                for i in range(4):
                    finish_head(ps[4 + i], psn[4 + i], zg["a"], "zga", half * 4 + i, NS, slice(0, NS))
            fm_unit(l, "kc", NS, s_evac_k(SS_KC[l], 512))
            fm_unit(l, "zc", NS, s_evac_z(zg["c"], "zgc"))
            fm_unit(l, "qc", NS, s_evac_q)
            for half in range(2):
                at = Attn()
                for sbk in range(2):
                    nk = 512 if sbk < 1 else NS
                    bi = load_kv(SS_KC[l], SS_VC[l], half * 4, 4, sbk * 512, nk, shist)
                    for kb in range((nk + 127) // 128):
                        n = min(128, nk - kb * 128)
                        for i in range(4):
                            hh = half * 4 + i
                            adds = []
                            if sbk == 0 and kb == 3:
                                adds = [(0, NS, identb[:], bc[:, 1, hh, 0:NS], ["identb", "btile"])]
                            if sbk == 1:
                                adds = [(0, NS, identb[0:NS, 0:NS], bc[0:NS, 0, hh, 0:NS], ["identb", "btile"])]
                            at.tile(dict(kT=kbuf[bi][0:64, i, kb * 128:kb * 128 + n], v=vbuf[bi][0:n, kb, i, 0:65], n=n, qlo=0, qhi=NS,
                                         qap=Q[0:64, hh, 0:NS], adds=adds, bias=None,
                                         names=[f"kbuf{bi}", f"vbuf{bi}"], O=ps[4 + i], oname=psn[4 + i],
                                         first=(sbk == 0 and kb == 0), last=(sbk == 1)))
                at.flush()
                for i in range(4):
                    finish_head(ps[4 + i], psn[4 + i], zg["c"], "zgc", half * 4 + i, NS, slice(0, NS))
            def s_evac_bx(j, pt, pn):
                if j < 2:
                    P.act(lambda e: e.activation(kst[:, j, 0:NS], pt[0:64, 0:NS], AF.Copy), r=[pn], w=["kst"])
                    if j == 1:
                        P.dma("pool", SS_KB[l][:, :, PAST:PAST + NS].rearrange("h d k -> d h k"), kst[:, 0:2, 0:NS], r=["kst"], w=[shist])
                elif j < 6:
                    P.act(lambda e: e.activation(iqT[:, j - 2, 0:NS], pt[0:64, 0:NS], AF.Copy), r=[pn], w=["iqT"])
                elif j == 6:
                    P.act(lambda e: e.activation(kst[:, 2, 0:NS], pt[0:64, 0:NS], AF.Copy), r=[pn], w=["kst"])
                    P.dma("pool", SS_IK[l][:, PAST:PAST + NS], kst[:, 2, 0:NS], r=["kst"], w=[shist])
            fm_unit(l, "bx", NS, s_evac_bx)
            fm_unit(l, "zb", NS, s_evac_z(zg["b"], "zgb"))
            fm_unit(l, "qb", NS, s_evac_q)
            NK = PAST + NS
            for k0 in range(0, NK, 512):
                nk = min(512, NK - k0)
                ii = nxt("ik", 2)
                P.dma("sp", ikbuf[ii][:, 0:nk], SS_IK[l][:, k0:k0 + nk], r=[shist], w=[f"ikbuf{ii}"])
                for h in range(8):
                    base = 32 * (h % 2)
                    pi_ = 2 + nxt("ips", 2)
                    P.pe(lambda e: e.matmul(ps[pi_][0:NS, 0:nk], iqT[base:base + 32, h // 2, 0:NS], ikbuf[ii][base:base + 32, 0:nk], start=True, stop=True),
                         r=["iqT", f"ikbuf{ii}"], w=[psn[pi_]])
                    ri = nxt("rr", 2)
                    P.act(lambda e: e.activation(Rr[ri][0:NS, 0:nk], ps[pi_][0:NS, 0:nk], AF.Relu, scale=wabs[0:NS, 0, h:h + 1]), r=[psn[pi_], "wabs"], w=[f"Rr{ri}"])
                    if h == 0:
                        P.dve(lambda e: e.tensor_scalar(SC[0:NS, k0:k0 + nk], Rr[ri][0:NS, 0:nk], wsgn[0:NS, 0, 0:1], None, ALU.mult), r=[f"Rr{ri}", "wsgn"], w=["SC"])
                    else:
                        P.dve(lambda e: e.scalar_tensor_tensor(SC[0:NS, k0:k0 + nk], Rr[ri][0:NS, 0:nk], wsgn[0:NS, 0, h:h + 1], SC[0:NS, k0:k0 + nk], ALU.mult, ALU.add),
                              r=[f"Rr{ri}", "wsgn", "SC"], w=["SC"])
            P.dve(lambda e: e.memset(cntb[:], 0.0), w=["cntb"])
            P.dve(lambda e: e.memset(small[:, 8:9], 0.0), w=["cand"])
            for it in range(NBIS):
                stp = 64.0 * (0.5 ** it)
                P.dve(lambda e: e.tensor_scalar(junk[0:NS, 0:NK], SC[0:NS, 0:NK], small[0:NS, 8:9], 0.0, ALU.is_ge, ALU.add, accum_out=cntb[0:NS, it:it + 1]),
                      r=["SC", "cand", "cntb"], w=["junk", "cntb"])
                a, b_ = (stp, -0.5 * stp) if it < NBIS - 1 else (stp, -stp)
                P.dve(lambda e: e.tensor_scalar(small[0:NS, 9:10], cntb[0:NS, it:it + 1], float(TOPK), a, ALU.is_ge, ALU.mult), r=["cntb"], w=["fl"])
                P.dve(lambda e: e.scalar_tensor_tensor(small[0:NS, 8:9], small[0:NS, 9:10], b_, small[0:NS, 8:9], ALU.add, ALU.add), r=["fl", "cand"], w=["cand"])
            at = Attn()
            for sbk in range(3):
                k0 = sbk * 512
                nk = min(512, NK - k0)
                mi = nxt("mb", 2)
                P.dve(lambda e: e.tensor_scalar(Mb[mi][0:NS, 0:nk], SC[0:NS, k0:k0 + nk], small[0:NS, 8:9], NEG, ALU.is_lt, ALU.mult), r=["SC", "cand"], w=[f"Mb{mi}"])
                bi = load_kv(SS_KB[l], SS_VB[l], 0, 2, k0, nk, shist)
                for kb in range((nk + 127) // 128):
                    n = min(128, nk - kb * 128)
                    gkb = sbk * 4 + kb
                    for j in range(2):
                        adds = [(0, 4 * NS, Mb[mi][0:NS, kb * 128:kb * 128 + n], i4b[0:NS, :, 0:NS], [f"Mb{mi}", "i4b"])]
                        if gkb >= 7:
                            for hq in range(4):
                                if gkb == 7:
                                    adds.append((hq * NS, (hq + 1) * NS, identb[:], b5[:, 1, 4 * j + hq, 0:NS], ["identb", "btile"]))
                                else:
                                    adds.append((hq * NS, (hq + 1) * NS, identb[0:NS, 0:NS], b5[0:NS, 0, 4 * j + hq, 0:NS], ["identb", "btile"]))
                        at.tile(dict(kT=kbuf[bi][0:64, j, kb * 128:kb * 128 + n], v=vbuf[bi][0:n, kb, j, 0:65], n=n, qlo=0, qhi=4 * NS,
                                     qap=Q[0:64, 4 * j:4 * j + 4, 0:NS], adds=adds, bias=None,
                                     names=[f"kbuf{bi}", f"vbuf{bi}"], O=ps[4 + j], oname=psn[4 + j],
                                     first=(gkb == 0), last=(gkb == 8)))
            at.flush()
            for j in range(2):
                finish_head(ps[4 + j], psn[4 + j], zg["b"], "zgb", (4 * j, 4 * j + 4), 4 * NS, slice(0, NS))
            allz = [zg["a"], zg["b"], zg["c"]]
            alln = ["zga", "zgb", "zgc"]
            for u in range(6):
                Wo_, won = load_wo(l, u)
                for hq in range(4):
                    hidx = u * 4 + hq
                    zt, zn = allz[hidx // 8], alln[hidx // 8]
                    for n_ in range(2):
                        P.pe(lambda e: e.matmul(ps[n_][0:NS, :], zt[:, hidx % 8, 0:NS], Wo_[:, hq, n_ * 512:(n_ + 1) * 512], start=(hidx == 0), stop=(hidx == 23)),
                             r=[zn, won], w=[psn[n_]])
            xi = nxt("x", 2)
            X = xt[xi]; xname = f"xt{xi}"
            P.dma("sp", X[0:NS, :], (xs if l == 0 else hs1)[:, :], r=(["hs1"] if l > 0 else []), w=[xname])
            for n_ in range(2):
                P.dve(lambda e: e.tensor_tensor(X[0:NS, n_ * 512:(n_ + 1) * 512], X[0:NS, n_ * 512:(n_ + 1) * 512], ps[n_][0:NS, :], ALU.add), r=[xname, psn[n_]], w=[xname])
            if l < nlayers - 1:
                P.dma("pool", hs1[:, :], X[0:NS, :], r=[xname], w=["hs1"])
            else:
                P.act(lambda e: e.activation(junk[0:NS, 0:1024], X[0:NS, :], AF.Square, accum_out=small[0:NS, 16:17]), r=[xname], w=["junk", "fs0"])
                P.dve(lambda e: e.tensor_scalar(small[0:NS, 17:18], small[0:NS, 16:17], 1.0 / D_MODEL, EPS, ALU.mult, ALU.add), r=["fs0"], w=["fs1"])
                P.act(lambda e: e.activation(small[0:NS, 18:19], small[0:NS, 17:18], AF.Sqrt), r=["fs1"], w=["fs2"])
                P.dve(lambda e: e.reciprocal(small[0:NS, 19:20], small[0:NS, 18:19]), r=["fs2"], w=["fs3"])
                P.dve(lambda e: e.scalar_tensor_tensor(X[0:NS, :], X[0:NS, :], small[0:NS, 19:20], fgb[0:NS, :], ALU.mult, ALU.mult), r=[xname, "fs3", "fgb"], w=[xname])
                P.dma("pool", y_s[:, :], X[0:NS, :], r=[xname])
        for fx in deferred_exchange:
            fx()
    P.finalize_and_emit()
    return nc, es, P


def _t5_bucket_np(rel):
    nb = 16
    max_exact = 8
    ret = np.where(rel > 0, nb, 0)
    n = np.abs(rel)
    nf = np.maximum(n, 1).astype(np.float32)
    large = max_exact + (np.log(nf / max_exact) / math.log(128 / max_exact) * (nb - max_exact)).astype(np.int32)
    large = np.minimum(large, nb - 1)
    return ret + np.where(n < max_exact, n, large)


def _constants():
    p = np.arange(128)[:, None]
    f = np.arange(128)[None, :]
    cst = np.zeros((128, 8, 128), np.float32)
    cst[:, 0] = (p == f)
    cst[:, 1] = (p + f == 127)
    cst[:, 2] = (p <= f)
    cst[0, 3, :] = 1.0
    cst[:, 4] = np.where(p > f, NEG, 0.0)
    cst[:, 5] = np.where((p >= 64) & (f < 64), NEG, 0.0)
    cst[:, 6] = np.where((p < 64) & (f >= 64), NEG, 0.0)
    cst[:, 7] = np.where((p < 64) & (f >= 64), -1e30, 0.0)
    rel = 127 - np.arange(LTAB)
    bk = _t5_bucket_np(rel.astype(np.int32))
    oh5 = np.zeros((32, LTAB), np.float32)
    oh5[bk, np.arange(LTAB)] += 1.0
    far = int(_t5_bucket_np(np.array([-100000], np.int32))[0])
    oh5[far, :] -= 1.0
    idx = np.clip(rel, -128, 128) + 128
    ohc = np.zeros((3 * 128, LTAB), np.float32)
    ohc[idx, np.arange(LTAB)] += 1.0
    ohc[0, :] -= 1.0
    return cst.reshape(128, 8 * 128), oh5, ohc.reshape(3, 128, LTAB)


_PROG = {}


def _get_prog(key=(True, 4, DEPTH)):
    if key not in _PROG:
        _PROG[key] = build_program(*key)
    return _PROG[key]


def _host_inputs(x_prompt, norm_g, w_in, b_f, t5_bias, c_rel_bias, w_out, final_g):
    cst, oh5, ohc = _constants()
    wus = np.zeros((DEPTH, NU, 128, 8, 512), np.float32)
    for l in range(DEPTH):
        for u, name in enumerate(UNITS):
            cols = np.array(_unit_cols(name))
            m = cols >= 0
            w = np.zeros((D_MODEL, 512), np.float32)
            w[:, m] = w_in[l][:, cols[m]]
            wus[l, u] = w.reshape(8, 128, 512).transpose(1, 0, 2)
    wos = np.ascontiguousarray(w_out.reshape(DEPTH, 6, 4, 64, D_MODEL).transpose(0, 1, 3, 2, 4))
    gcol = np.ascontiguousarray(norm_g.reshape(DEPTH, 8, 128).transpose(2, 0, 1).reshape(128, DEPTH * 8))
    crel = np.zeros((DEPTH, 384, 8), np.float32)
    crel[:, :257] = c_rel_bias
    common = dict(wu=wus, wo=wos, gcol=gcol, fg=np.ascontiguousarray(final_g.reshape(1, D_MODEL)),
                  bfb=np.ascontiguousarray(b_f.reshape(1, DEPTH * 8)), t5=np.ascontiguousarray(t5_bias),
                  crel=crel.reshape(DEPTH, 3, 128, 8), oh5=oh5, ohc=ohc, cst=cst)
    return common


def _percore(j):
    pc = np.zeros((128, 1024), np.float32)
    p = np.arange(128)
    pc[:, 0:512] = (j * 512 + np.arange(512))[None, :]
    for rr in range(16):
        pc[:, 512 + rr] = rr * 128 + p
    for qb in range(4):
        for ch in range(4):
            pc[:, 528 + qb * 4 + ch] = (8 * j + 2 * qb + (p >= 64) + 1) * 64 - ch * 512
        for rr in range(17):
            pc[:, 544 + (qb * 17 + rr) * 2] = 1.0 if (rr - 1) == 4 * j + qb else 0.0
            pc[:, 544 + (qb * 17 + rr) * 2 + 1] = 1.0 if (rr - 1) == 4 * j + qb - 1 else 0.0
    for r in range(4):
        pc[:, 680 + r] = 1.0 if j == r else 0.0
    pc[:, 684] = -30000.0 if j == 0 else 0.0
    return pc


def _in_maps(x_prompt, x_sample, cache_a_k, cache_a_v, cache_a_logf, cache_b_k, cache_b_v, cache_b_idx_k,
             cache_c_k, cache_c_v, norm_g, w_in, b_f, t5_bias, c_rel_bias, w_out, final_g):
    f = lambda a: np.ascontiguousarray(np.asarray(a, dtype=np.float32))
    x_prompt, norm_g, w_in, b_f, t5_bias, c_rel_bias, w_out, final_g = map(f, (x_prompt, norm_g, w_in, b_f, t5_bias, c_rel_bias, w_out, final_g))
    common = _host_inputs(x_prompt, norm_g, w_in, b_f, t5_bias, c_rel_bias, w_out, final_g)
    common["iota5"] = np.ascontiguousarray(np.broadcast_to(np.arange(512, dtype=np.float32)[None, :], (128, 512)))
    in_maps = []
    for c in range(8):
        b, j = c // 4, c % 4
        m = dict(common)
        xb = x_prompt[b].reshape(NG, 512, D_MODEL)
        m["xp"] = x_prompt[b]
        m["xq"] = np.ascontiguousarray(xb[j::4])
        xpv = np.zeros((4, 512, D_MODEL), np.float32)
        for mm in range(4):
            if 4 * mm + j - 1 >= 0:
                xpv[mm] = xb[4 * mm + j - 1]
        m["xprev"] = xpv
        m["pcore"] = _percore(j)
        m["xs"] = f(x_sample[c])
        m["ca_k"] = f(cache_a_k[:, c]).reshape(DEPTH, PAST, 512); m["ca_v"] = f(cache_a_v[:, c]).reshape(DEPTH, PAST, 512)
        m["ca_lf"] = f(cache_a_logf[:, c]).reshape(DEPTH, PAST, 8)
        m["cb_k"] = f(cache_b_k[:, c]).reshape(DEPTH, PAST, 128); m["cb_v"] = f(cache_b_v[:, c]).reshape(DEPTH, PAST, 128)
        m["cb_ik"] = f(cache_b_idx_k[:, c]).reshape(DEPTH, PAST, 32)
        m["cc_k"] = f(cache_c_k[:, c]).reshape(DEPTH, 512, 512); m["cc_v"] = f(cache_c_v[:, c]).reshape(DEPTH, 512, 512)
        in_maps.append(m)
    return in_maps


def kernel(x_prompt, x_sample, cache_a_k, cache_a_v, cache_a_logf, cache_b_k, cache_b_v, cache_b_idx_k,
           cache_c_k, cache_c_v, norm_g, w_in, b_f, t5_bias, c_rel_bias, w_out, final_g):
    nc, es, P = _get_prog()
    in_maps = _in_maps(x_prompt, x_sample, cache_a_k, cache_a_v, cache_a_logf, cache_b_k, cache_b_v, cache_b_idx_k,
                       cache_c_k, cache_c_v, norm_g, w_in, b_f, t5_bias, c_rel_bias, w_out, final_g)
    res = run_bass_kernel_spmd(nc, in_maps, core_ids=list(range(8)))
    R = res.results
    st = lambda name, shp: np.stack([R[4 * b][name] for b in range(2)], axis=1).reshape(shp)
    y_prompt = np.zeros((BATCH, NG, 512, D_MODEL), np.float32)
    for c in range(8):
        b, j = c // 4, c % 4
        y_prompt[b, j::4] = R[c]["y_q"].reshape(4, 512, D_MODEL)
    y_prompt = y_prompt.reshape(BATCH, SEQ, D_MODEL)
    ss = lambda name, shp: np.stack([R[b][name] for b in range(DEC_BATCH)], axis=1).reshape(shp)
    y_sample = np.stack([R[b]["y_s"] for b in range(DEC_BATCH)], axis=0)
    outs = [y_prompt, y_sample,
            st("o_ak", (DEPTH, BATCH, SEQ, H, HD)), st("o_av", (DEPTH, BATCH, SEQ, H, HD)), st("o_lf", (DEPTH, BATCH, SEQ, H)),
            st("o_bk", (DEPTH, BATCH, SEQ, KVB, HD)), st("o_bv", (DEPTH, BATCH, SEQ, KVB, HD)), st("o_ik", (DEPTH, BATCH, SEQ, IDX_D)),
            st("o_ck", (DEPTH, BATCH, 512, H, HD)), st("o_cv", (DEPTH, BATCH, 512, H, HD)),
            ss("s_ak", (DEPTH, DEC_BATCH, DEC_SEQ, H, HD)), ss("s_av", (DEPTH, DEC_BATCH, DEC_SEQ, H, HD)), ss("s_lf", (DEPTH, DEC_BATCH, DEC_SEQ, H)),
            ss("s_bk", (DEPTH, DEC_BATCH, DEC_SEQ, KVB, HD)), ss("s_bv", (DEPTH, DEC_BATCH, DEC_SEQ, KVB, HD)), ss("s_ik", (DEPTH, DEC_BATCH, DEC_SEQ, IDX_D)),
            ss("s_ck", (DEPTH, DEC_BATCH, DEC_SEQ, H, HD)), ss("s_cv", (DEPTH, DEC_BATCH, DEC_SEQ, H, HD))]
    return tuple(outs)
```

```python
import math
import types
import numpy as np
from contextlib import ExitStack
import concourse.bass as bass
import concourse.mybir as mybir
from concourse.bass_utils import run_bass_kernel_spmd

F32 = mybir.dt.float32
BF16 = mybir.dt.bfloat16
ALU = mybir.AluOpType
AF = mybir.ActivationFunctionType

D_MODEL = 1024; BATCH = 2; SEQ = 8192; DEPTH = 2; DEC_BATCH = 8; DEC_SEQ = 16; PAST = 1024
HD = 64; H = 8; KVB = 2; IDX_H = 8; IDX_D = 32; TOPK = 256
SCALE = HD ** -0.5
IDXS = (IDX_D ** -0.5) * (IDX_H ** -0.5)
EPS = 1e-6
NEG = -32768.0
NG = SEQ // 512
NBIS = 24
LTAB = 384

_SPLIT = (512, 512, 512, 512, 8, 512, 128, 128, 512, 256, 8, 32, 512, 512, 512, 512)
_OFF = np.concatenate([[0], np.cumsum(_SPLIT)])
(QA, KA, VA, ZA, FA, QB, KB, VB, ZB, IQ, IW, IK, QC, KC, VC, ZC) = [int(o) for o in _OFF[:-1]]
FM_UNITS = ["qa", "ka", "za", "qb", "zb", "bx", "qc", "kc", "zc"]
TM_UNITS = ["tka", "tva", "tb", "tkc", "tvc"]
UNITS = FM_UNITS + TM_UNITS
NU = len(UNITS)


def _unit_cols(name):
    r = lambda a, n: list(range(a, a + n))
    pad = lambda l: l + [-1] * (512 - len(l))
    if name == "qa": return r(QA, 512)
    if name == "ka": return r(KA, 512)
    if name == "za": return r(ZA, 512)
    if name == "qb": return r(QB, 512)
    if name == "zb": return r(ZB, 512)
    if name == "qc": return r(QC, 512)
    if name == "kc": return r(KC, 512)
    if name == "zc": return r(ZC, 512)
    if name == "bx": return pad(r(KB, 128) + r(IQ, 256) + r(IK, 32) + r(IK, 32))
    if name == "tka": return r(KA, 512)
    if name == "tva": return r(VA, 512)
    if name == "tkc": return r(KC, 512)
    if name == "tvc": return r(VC, 512)
    if name == "tb": return pad(r(KB, 128) + r(VB, 128) + r(IK, 32) + r(FA, 8) + r(IW, 8))
    raise KeyError(name)


class Prog:
    STREAMS = ("pe", "act", "dve", "pool", "sp")

    def __init__(self, nc):
        self.nc = nc
        self.ops = []
        self.ndma = {}

    @staticmethod
    def _freeze(fn):
        if fn.__closure__ is None:
            return fn
        cells = []
        for c in fn.__closure__:
            try:
                cells.append(types.CellType(c.cell_contents))
            except ValueError:
                cells.append(c)
        return types.FunctionType(fn.__code__, fn.__globals__, fn.__name__, fn.__defaults__, tuple(cells))

    NSUB = 16

    def add(self, stream, fn, r=(), w=(), dma=False, cc=False):
        fn = self._freeze(fn)
        if cc:
            track = "dma_cc"
        elif dma:
            k = self.ndma.get(stream, 0)
            self.ndma[stream] = k + 1
            track = f"dma_{stream}#{k % self.NSUB}"
        else:
            track = stream
        self.ops.append((stream, track, fn, tuple(r), tuple(w)))

    def pe(self, fn, r=(), w=()): self.add("pe", fn, r, w)
    def act(self, fn, r=(), w=()): self.add("act", fn, r, w)
    def dve(self, fn, r=(), w=()): self.add("dve", fn, r, w)
    def pool(self, fn, r=(), w=()): self.add("pool", fn, r, w)

    def dma(self, stream, out, in_, r=(), w=(), **kw):
        self.add(stream, lambda e: e.dma_start(out=out, in_=in_, **kw), r, w, dma=True)

    def finalize_and_emit(self):
        nc = self.nc
        ops = self.ops
        n = len(ops)
        writers = {}
        readers = {}
        prev_on = {}
        deps = [None] * n
        signal = [False] * n
        qof = lambda t: t.split("#")[0]
        for i, (stream, track, fn, R, W) in enumerate(ops):
            d = set()
            is_dma = track.startswith("dma_")
            if is_dma:
                j = prev_on.get(track)
                if j is not None:
                    d.add(j)
                prev_on[track] = i
            for res in R:
                for tj, j in writers.get(res, {}).items():
                    if tj != track or is_dma or track != "pe":
                        d.add(j)
            for res in W:
                lazy = res.startswith("~")
                for tj, j in writers.get(res, {}).items():
                    if lazy:
                        if qof(tj) != qof(track):
                            d.add(j)
                    elif tj != track or is_dma:
                        d.add(j)
                for tj, j in readers.get(res, {}).items():
                    if lazy:
                        if qof(tj) != qof(track):
                            d.add(j)
                    elif tj != track or is_dma:
                        d.add(j)
            for res in R:
                readers.setdefault(res, {})[track] = i
            for res in W:
                if res.startswith("~"):
                    writers.setdefault(res, {})[track] = i
                else:
                    writers[res] = {track: i}
                    readers[res] = {}
            d.discard(i)
            deps[i] = d
            for j in d:
                signal[j] = True
        tracks = sorted({o[1] for o in ops})
        cnt = {t: 0 for t in tracks}
        val = [0] * n
        for i, (stream, track, fn, R, W) in enumerate(ops):
            if track == "dma_cc":
                cnt[track] += 1
                val[i] = cnt[track]
            elif track.startswith("dma_"):
                cnt[track] += 16
                val[i] = cnt[track]
            elif signal[i]:
                cnt[track] += 1
                val[i] = cnt[track]
        known = {s: {t: 0 for t in tracks} for s in self.STREAMS}
        waits = [None] * n
        for i, (stream, track, fn, R, W) in enumerate(ops):
            need = {}
            for j in deps[i]:
                tj = ops[j][1]
                need[tj] = max(need.get(tj, 0), val[j])
            wl = []
            for tj, v in need.items():
                if v > known[stream][tj]:
                    wl.append((tj, v))
                    known[stream][tj] = v
            waits[i] = wl
        self.stats = dict(cnt)
        by_stream = {s: [] for s in self.STREAMS}
        for i, o in enumerate(ops):
            by_stream[o[0]].append(i)
        with ExitStack() as es:
            sems = {t: es.enter_context(nc.semaphore("s_" + t.replace("#", "_"))) for t in tracks}
            block = es.enter_context(nc.Block())

            def run(eng, stream):
                for i in by_stream[stream]:
                    _, track, fn, R, W = ops[i]
                    for tj, v in waits[i]:
                        eng.wait_ge(sems[tj], v)
                    inst = fn(eng)
                    if track == "dma_cc":
                        inst.then_inc(sems[track], 1)
                    elif track.startswith("dma_"):
                        inst.then_inc(sems[track], 16)
                    elif signal[i]:
                        inst.then_inc(sems[track], 1)
                if stream == "sp":
                    for t in tracks:
                        if t.startswith("dma_") and cnt[t] > known[stream][t]:
                            eng.wait_ge(sems[t], cnt[t])

            @block.tensor
            def _(e): run(e, "pe")

            @block.scalar
            def _(e): run(e, "act")

            @block.vector
            def _(e): run(e, "dve")

            @block.gpsimd
            def _(e): run(e, "pool")

            @block.sync
            def _(e): run(e, "sp")


def build_program(do_sample=True, nm=4, nlayers=DEPTH):
    nc = bass.Bass("TRN2", target_bir_lowering=False)
    es = ExitStack()
    din = lambda name, shape, dt=F32: nc.dram_tensor(name, list(shape), dt, kind="ExternalInput").ap()
    dout = lambda name, shape: nc.dram_tensor(name, list(shape), F32, kind="ExternalOutput").ap()
    dscr = lambda name, shape, dt=BF16: nc.dram_tensor(name, list(shape), dt, kind="Internal").ap()
    xp = din("xp", [SEQ, D_MODEL])
    wu = din("wu", [DEPTH, NU, 128, 8, 512])
    wo = din("wo", [DEPTH, 6, 64, 4, 1024])
    gcol_d = din("gcol", [128, DEPTH * 8])
    fg_d = din("fg", [1, D_MODEL])
    bf_d = din("bfb", [1, DEPTH * 8])
    t5_d = din("t5", [32, 8])
    crel_d = din("crel", [DEPTH, 3, 128, 8])
    oh5_d = din("oh5", [32, LTAB])
    ohc_d = din("ohc", [3, 128, LTAB])
    cst_d = din("cst", [128, 8 * 128])
    y_q = dout("y_q", [2048, D_MODEL])
    xq = din("xq", [4, 512, D_MODEL]); xprev = din("xprev", [4, 512, D_MODEL])
    pc_d = din("pcore", [128, 1024])
    iota_d = din("iota5", [128, 512])
    o_ak = dout("o_ak", [DEPTH, SEQ, 512]); o_av = dout("o_av", [DEPTH, SEQ, 512])
    o_lf = dout("o_lf", [DEPTH, SEQ, 8])
    o_bk = dout("o_bk", [DEPTH, SEQ, 128]); o_bv = dout("o_bv", [DEPTH, SEQ, 128])
    o_ik = dout("o_ik", [DEPTH, SEQ, 32])
    o_ck = dout("o_ck", [DEPTH, 512, 512]); o_cv = dout("o_cv", [DEPTH, 512, 512])
    wub = dscr("wub", [DEPTH, NU, 128, 8, 512])
    wob = dscr("wob", [DEPTH, 6, 64, 4, 1024])
    hp1q = dscr("hp1q", [2048, D_MODEL], F32)
    hpg = dscr("hpg", [SEQ, D_MODEL], F32)
    ccs = dscr("ccs", [256, D_MODEL], F32)
    COMBS = dscr("combs", [4, 17, 2, 128, 512])
    AMASK = dscr("amask", [16, 128, 512])
    ccd = dscr("ccd", [1024, D_MODEL], F32)
    S_KCL = dscr("scr_kcl", [DEPTH, 8, 64, 1024]); S_VCL = dscr("scr_vcl", [DEPTH, 1024, 512])
    tab5 = dscr("tab5", [8, LTAB], F32)
    tabc = dscr("tabc", [DEPTH, 8, LTAB], F32)
    S_KA = dscr("scr_s_ka", [DEPTH, 8, 64, SEQ]); S_KC = dscr("scr_s_kc", [DEPTH, 8, 64, SEQ])
    S_KB = dscr("scr_s_kb", [DEPTH, 2, 64, SEQ]); S_IK = dscr("scr_s_ik", [DEPTH, 64, SEQ])
    S_VA = dscr("scr_s_va", [DEPTH, SEQ, 512]); S_VC = dscr("scr_s_vc", [DEPTH, SEQ, 512])
    S_VB = dscr("scr_s_vb", [DEPTH, SEQ, 128])

    xs = din("xs", [DEC_SEQ, D_MODEL])
    ca_k = din("ca_k", [DEPTH, PAST, 512]); ca_v = din("ca_v", [DEPTH, PAST, 512]); ca_lf = din("ca_lf", [DEPTH, PAST, 8])
    cb_k = din("cb_k", [DEPTH, PAST, 128]); cb_v = din("cb_v", [DEPTH, PAST, 128]); cb_ik = din("cb_ik", [DEPTH, PAST, 32])
    cc_k = din("cc_k", [DEPTH, 512, 512]); cc_v = din("cc_v", [DEPTH, 512, 512])
    y_s = dout("y_s", [DEC_SEQ, D_MODEL])
    s_ak = dout("s_ak", [DEPTH, DEC_SEQ, 512]); s_av = dout("s_av", [DEPTH, DEC_SEQ, 512]); s_lf = dout("s_lf", [DEPTH, DEC_SEQ, 8])
    s_bk = dout("s_bk", [DEPTH, DEC_SEQ, 128]); s_bv = dout("s_bv", [DEPTH, DEC_SEQ, 128]); s_ik = dout("s_ik", [DEPTH, DEC_SEQ, 32])
    s_ck = dout("s_ck", [DEPTH, DEC_SEQ, 512]); s_cv = dout("s_cv", [DEPTH, DEC_SEQ, 512])
    hs1 = dscr("hs1", [DEC_SEQ, D_MODEL], F32)
    MBS = [dscr(f"mbs{i}", [128, SEQ]) for i in range(4)]
    SS_KA = dscr("ss_ka", [DEPTH, 8, 64, 1152]); SS_KC = dscr("ss_kc", [DEPTH, 8, 64, 640])
    SS_KB = dscr("ss_kb", [DEPTH, 2, 64, 1152]); SS_IK = dscr("ss_ik", [DEPTH, 64, 1152])
    SS_VA = dscr("ss_va", [DEPTH, 1152, 512]); SS_VC = dscr("ss_vc", [DEPTH, 640, 512]); SS_VB = dscr("ss_vb", [DEPTH, 1152, 128])

    sb = lambda name, shape, dt: es.enter_context(nc.sbuf_tensor(name, list(shape), dt))
    wring = [sb(f"wring{i}", [128, 8, 512], BF16) for i in range(2)]
    SC = sb("SC", [128, 8192], F32)
    junk = sb("junk", [128, 8192], BF16)
    hT = sb("hT", [128, 8, 512], BF16)
    Q = sb("Q", [65, 8, 512], BF16)
    zg = {t: sb("zg" + t, [64, 8, 512], BF16) for t in "abc"}
    iqT = sb("iqT", [64, 4, 512], BF16)
    kst = sb("kst", [64, 8, 512], BF16)
    xt = [sb(f"xt{i}", [128, 1024], F32) for i in range(2)]
    xn = sb("xn", [128, 1024], BF16)
    st = [sb(f"st{i}", [128, 512], F32) for i in range(2)]
    vst = [sb(f"vst{i}", [128, 512], BF16) for i in range(2)]
    kbuf = [sb(f"kbuf{i}", [65, 4, 512], BF16) for i in range(2)]
    vbuf = [sb(f"vbuf{i}", [128, 4, 4, 65], BF16) for i in range(2)]
    Pt = [sb(f"Pt{i}", [128, 512], BF16) for i in range(4)]
    Mb = [sb(f"Mb{i}", [128, 512], BF16) for i in range(2)]
    Rr = [sb(f"Rr{i}", [128, 512], F32) for i in range(3)]
    ikbuf = [sb(f"ikbuf{i}", [64, 512], BF16) for i in range(2)]
    b5 = sb("b5", [128, 2, 8, 128], BF16)
    bc = sb("bc", [128, 2, 8, 128], BF16)
    cstf = sb("cstf", [128, 8, 128], F32)
    identb = sb("identb", [128, 128], BF16)
    i4b = sb("i4b", [128, 4, 128], BF16)
    ma0b = sb("ma0b", [128, 128], BF16); cm0b = sb("cm0b", [128, 128], BF16); cm4b = sb("cm4b", [128, 128], BF16)
    cstore = sb("cstore", [128, 64, 8], F32)
    nbias = sb("nbias", [128, 64, 8], F32)
    gcol = sb("gcol_s", [128, DEPTH * 8], F32)
    fgb = sb("fgb", [128, D_MODEL], F32)
    bfb = sb("bfb_s", [128, DEPTH * 8], F32)
    small = sb("small", [128, 64], F32)
    cntb = sb("cntb", [128, NBIS], F32)
    wabs = sb("wabs", [128, 4, 8], F32); wsgn = sb("wsgn", [128, 4, 8], F32)
    lfb = sb("lfb", [128, 8], F32)
    tot = sb("tot", [1, 8], F32)
    tots = sb("tots", [1, 17, 8], F32)
    totbc = sb("totbc", [128, 8], F32)
    lf4 = sb("lf4", [128, 4, 8], F32)
    cown = sb("cown", [128, 4, 8], F32)
    xacc = sb("xacc", [128, D_MODEL], F32)
    pcore = sb("pcore_s", [128, 1024], F32)
    iota5 = sb("iota5_s", [128, 512], F32)
    comb = [sb(f"comb{i}", [128, 4, 128], BF16) for i in range(2)]
    ones1 = sb("ones1", [65, 128], F32)
    cbc = sb("cbc", [128, 8], F32)
    rq = sb("rq", [128, 4, 8], F32)
    rT = sb("rT", [8, 512], BF16)
    rden = sb("rden", [65, 512], F32)
    otmp = sb("otmp", [64, 512], F32)
    hank = sb("hank", [128, 128], F32)
    t5s = sb("t5s", [32, 8], F32); oh5s = sb("oh5s", [32, LTAB], F32)
    crs = sb("crs", [128, 3, 8], F32); ohcs = sb("ohcs", [128, 3, LTAB], F32)
    tabs = sb("tabs", [8, LTAB], F32)
    ps = [es.enter_context(nc.psum_tensor(f"ps{i}", [128, 512], F32)) for i in range(8)]
    psn = [f"ps{i}" for i in range(8)]

    P = Prog(nc)
    _early = {}

    def nxt_early(key, n):
        v = _early.get(key, 0)
        _early[key] = v + 1
        return v % n

    IDENT = cstf[:, 0, :]; JM = cstf[:, 1, :]; TRI = cstf[:, 2, :]; E0ROW = cstf[:, 3, :]
    ADM = cstf[:, 7, :]
    E127 = cstf[:, 1, 0:1]

    P.dma("sp", pcore[:], pc_d, w=["qrelb", "krel", "qlimc", "sel01", "selb", "pvb"])
    P.dma("sp", iota5[:], iota_d, w=["iota5"])
    qrelb = pcore[:, 0:512]; krel = pcore[:, 512:528]; qlimc = pcore[:, 528:544]; sel01 = pcore[:, 544:680]
    selb = pcore[:, 680:684]; pvb = pcore[:, 684:685]
    P.dma("sp", cstf[:].rearrange("p a b -> p (a b)"), cst_d, w=["cstf"])
    P.dma("sp", gcol[:], gcol_d, w=["gcol"])
    P.dma("sp", fgb[:], fg_d.to_broadcast([128, D_MODEL]) if hasattr(fg_d, "to_broadcast") else bass.AP(fg_d.tensor, 0, [[0, 128], [1, D_MODEL]]), w=["fgb"])
    P.dma("sp", bfb[:], bass.AP(bf_d.tensor, 0, [[0, 128], [1, DEPTH * 8]]), w=["bfb"])
    P.dma("sp", t5s[:], t5_d, w=["t5s"])
    P.dma("sp", oh5s[:], oh5_d, w=["oh5s"])
    P.dma("sp", ohcs[:], ohc_d.rearrange("c p l -> p c l"), w=["ohcs"])
    P.dve(lambda e: e.tensor_copy(identb[:], IDENT), r=["cstf"], w=["identb"])
    for k in range(4):
        P.dve(lambda e, k=k: e.tensor_copy(i4b[:, k, :], IDENT), r=["cstf"], w=["i4b"])
    P.dve(lambda e: e.tensor_copy(ma0b[:], cstf[:, 4, :]), r=["cstf"], w=["ma0b"])
    P.dve(lambda e: e.tensor_copy(cm0b[:], cstf[:, 5, :]), r=["cstf"], w=["cm0b"])
    P.dve(lambda e: e.tensor_copy(cm4b[:], cstf[:, 6, :]), r=["cstf"], w=["cm4b"])
    onesf = cstf[:, 4, :]
    P.dve(lambda e: e.memset(onesf, 1.0), r=["ma0b"], w=["cstf", "onesf"])
    P.dve(lambda e: e.memset(ones1[:], 1.0), w=["ones1"])
    for i in range(2):
        P.pool(lambda e, i=i: e.memset(kbuf[i][:], 1.0), w=[f"kbuf{i}"])
        P.pool(lambda e, i=i: e.memset(vbuf[i][:], 1.0), w=[f"vbuf{i}"])

    stg = SC[:, 0:4096].rearrange("p (c n) -> p c n", c=8)
    stgb = junk[:, 0:4096].rearrange("p (c n) -> p c n", c=8)
    for l in range(nlayers):
        for u in range(NU):
            P.dma("sp", stg, wu[l, u], w=["SC"])
            P.act(lambda e: e.activation(stgb, stg, AF.Copy), r=["SC"], w=["junk"])
            P.dma("sp", wub[l, u], stgb, r=["junk"], w=[f"wub{l}_{u}"])
        for u in range(6):
            so = SC[0:64, 0:4096].rearrange("p (c n) -> p c n", c=4)
            sob = junk[0:64, 0:4096].rearrange("p (c n) -> p c n", c=4)
            P.dma("sp", so, wo[l, u], w=["SC"])
            P.act(lambda e, so=so, sob=sob: e.activation(sob, so, AF.Copy), r=["SC"], w=["junk"])
            P.dma("sp", wob[l, u], sob, r=["junk"], w=[f"wob{l}_{u}"])

    def build_tab(lhs_list, rhs_list, dst, rnames):
        for i, (a, b) in enumerate(zip(lhs_list, rhs_list)):
            P.pe(lambda e, a=a, b=b, i=i: e.matmul(ps[0][0:8, 0:LTAB], a, b, start=(i == 0), stop=(i == len(lhs_list) - 1)),
                 r=rnames, w=["ps0"])
        P.dve(lambda e: e.tensor_copy(tabs[:], ps[0][0:8, 0:LTAB]), r=["ps0"], w=["tabs"])
        P.dma("sp", dst, tabs[:], r=["tabs"], w=["tabdram"])

    def build_toeplitz(tab_ap2d, dst_tile):
        for k in range(2):
            for h in range(8):
                b0 = 128 * k
                src = bass.AP(tab_ap2d.tensor, tab_ap2d.offset + h * LTAB + b0, [[1, 128], [1, 128]])
                P.dma("sp", hank[:], src, r=["tabdram"], w=["hank"])
                P.pe(lambda e: e.matmul(ps[1][:, 0:128], JM, hank[:], start=True, stop=True), r=["hank", "cstf"], w=["ps1"])
                P.dve(lambda e, k=k, h=h: e.tensor_copy(dst_tile[:, k, h, :], ps[1][:, 0:128]), r=["ps1"], w=["btile"])

    build_tab([t5s[:]], [oh5s[:]], tab5, ["t5s", "oh5s"])
    build_toeplitz(tab5, b5)
    for qb in range(4):
        for rr in range(17):
            if (rr - 1 - qb) % 4 not in (0, 3):
                continue
            for jj in range(2):
                ci = nxt_early("cmb", 2)
                sc0 = sel01[:, (qb * 17 + rr) * 2:(qb * 17 + rr) * 2 + 1]
                sc1 = sel01[:, (qb * 17 + rr) * 2 + 1:(qb * 17 + rr) * 2 + 2]
                P.dve(lambda e: e.tensor_scalar(comb[ci][:, :, :], b5[:, 0, 4 * jj:4 * jj + 4, :], sc0, None, ALU.mult), r=["btile", "sel01"], w=[f"comb{ci}"])
                P.dve(lambda e: e.scalar_tensor_tensor(comb[ci][:, :, :], b5[:, 1, 4 * jj:4 * jj + 4, :], sc1, comb[ci][:, :, :], ALU.mult, ALU.add),
                      r=["btile", "sel01", f"comb{ci}"], w=[f"comb{ci}"])
                P.dma("pool", COMBS[qb, rr, jj], comb[ci][:].rearrange("p a b -> p (a b)"), r=[f"comb{ci}"], w=["~combs"])
    for rr in range(16):
        mi = nxt_early("mb", 2)
        P.dve(lambda e: e.tensor_scalar(Mb[mi][:, :], qrelb[:, :], krel[:, rr:rr + 1], NEG, ALU.is_lt, ALU.mult), r=["qrelb", "krel"], w=[f"Mb{mi}"])
        P.dma("pool", AMASK[rr], Mb[mi][:, :], r=[f"Mb{mi}"], w=["~amask"])

    wk = [0]

    def load_w(l, u):
        i = wk[0] % 2
        wk[0] += 1
        P.dma("sp", wring[i][:], wub[l, u], r=[f"wub{l}_{u}"], w=[f"wring{i}"])
        return wring[i], f"wring{i}"

    def load_wo(l, u):
        i = wk[0] % 2
        wk[0] += 1
        dst = wring[i][0:64].rearrange("p c n -> p (c n)").rearrange("p (c n) -> p c n", c=4)
        P.dma("sp", dst, wob[l, u], r=[f"wob{l}_{u}"], w=[f"wring{i}"])
        return dst, f"wring{i}"

    rot = {"s": 0, "pt": 0, "kv": 0, "st": 0, "x": 0, "ik": 0, "rr": 0, "mb": 0, "ips": 0, "cmb": 0}

    def nxt(key, n):
        v = rot[key] % n
        rot[key] += 1
        return v

    def norm_block(l, xsrc_ap, tb, nrow=128, rname=None, sb_src=None):
        if sb_src is not None:
            X, xname = sb_src
        else:
            xi = nxt("x", 2)
            X = xt[xi]; xname = f"xt{xi}"
        if sb_src is None:
            P.dma("sp", X[0:nrow, :], xsrc_ap, r=(list(rname) if isinstance(rname, (list, tuple)) else ([rname] if rname else [])), w=[xname])
        P.act(lambda e: e.activation(junk[0:nrow, 0:1024], X[0:nrow, :], AF.Square, accum_out=small[0:nrow, 0:1]),
              r=[xname], w=["junk", "small0"])
        P.dve(lambda e: e.tensor_scalar(small[0:nrow, 1:2], small[0:nrow, 0:1], 1.0 / D_MODEL, EPS, ALU.mult, ALU.add), r=["small0"], w=["small1"])
        P.act(lambda e: e.activation(small[0:nrow, 2:3], small[0:nrow, 1:2], AF.Sqrt), r=["small1"], w=["small2"])
        P.dve(lambda e: e.reciprocal(small[0:nrow, 3:4], small[0:nrow, 2:3]), r=["small2"], w=["small3"])
        P.dve(lambda e: e.tensor_scalar(xn[0:nrow, :], X[0:nrow, :], small[0:nrow, 3:4], None, ALU.mult), r=[xname, "small3"], w=["xn"])
        psb = ps[7].bitcast(BF16)
        for c in range(8):
            P.pe(lambda e, c=c: e.transpose(psb[:, c * 128:c * 128 + nrow], xn[0:nrow, c * 128:(c + 1) * 128], identb[0:nrow, 0:nrow]),
                 r=["xn", "identb"], w=["ps7"])
        for c in range(8):
            P.dve(lambda e, c=c: e.tensor_scalar(hT[:, c, tb * 128:tb * 128 + nrow], psb[:, c * 128:c * 128 + nrow],
                                                 gcol[:, l * 8 + c:l * 8 + c + 1], None, ALU.mult),
                  r=["ps7", "gcol"], w=["hT"])

    def fm_unit(l, uname, ntok, evac, blocks=tuple(range(8)), wres=None):
        W, wn = wres if wres is not None else load_w(l, UNITS.index(uname))
        for j in blocks:
            si = nxt("s", 4)
            for c in range(8):
                P.pe(lambda e, j=j, c=c, si=si: e.matmul(ps[si][0:64, 0:ntok], W[:, c, j * 64:(j + 1) * 64], hT[:, c, 0:ntok],
                                                        start=(c == 0), stop=(c == 7)), r=[wn, "hT"], w=[psn[si]])
            evac(j, ps[si], psn[si])

    def finish_head(O, oname, zt, zname, hsel, ncol, csl):
        P.dve(lambda e: e.reciprocal(rden[64:65, 0:ncol], O[64:65, 0:ncol]), r=[oname], w=["rden"])
        bi_ = nxt("s", 4)
        P.pe(lambda e: e.matmul(ps[bi_][0:64, 0:ncol], ones1[64:65, 0:64], rden[64:65, 0:ncol], start=True, stop=True),
             r=["rden", "ones1"], w=[psn[bi_]])
        P.dve(lambda e: e.tensor_copy(otmp[:, 0:ncol], ps[bi_][0:64, 0:ncol]), r=[psn[bi_]], w=["otmp"])
        P.dve(lambda e: e.tensor_tensor(otmp[:, 0:ncol], O[0:64, 0:ncol], otmp[:, 0:ncol], ALU.mult), r=[oname, "otmp"], w=["otmp"])
        if isinstance(hsel, tuple):
            zv = zt[:, hsel[0]:hsel[1], csl]
            ov = otmp[:, 0:ncol].rearrange("p (h q) -> p h q", h=hsel[1] - hsel[0])
        else:
            zv = zt[:, hsel, csl]
            ov = otmp[:, 0:ncol]
        P.dve(lambda e: e.tensor_tensor(zv, zv, ov, ALU.mult), r=[zname, "otmp"], w=[zname])

    def load_kv(Ksrc, Vsrc, h0, nh, k0, nk, krows_name):
        i = nxt("kv", 2)
        P.dma("sp", kbuf[i][0:64, 0:nh, 0:nk], Ksrc[h0:h0 + nh, :, k0:k0 + nk].rearrange("h d k -> d h k"), r=[krows_name], w=[f"kbuf{i}"])
        nb = (nk + 127) // 128
        for b in range(nb):
            n = min(128, nk - b * 128)
            P.dma("sp", vbuf[i][0:n, b, 0:nh, 0:64],
                  Vsrc[k0 + b * 128:k0 + b * 128 + n, h0 * 64:(h0 + nh) * 64].rearrange("k (h d) -> k h d", h=nh),
                  r=[krows_name], w=[f"vbuf{i}"])
        return i

    class Attn:
        def __init__(self):
            self.pend = []

        def tile(self, t):
            si = nxt("s", 4)
            S = ps[si]; n = t["n"]; qlo, qhi = t["qlo"], t["qhi"]
            nadd = len(t["adds"])
            P.pe(lambda e: e.matmul(S[0:n, qlo:qhi], t["kT"], t["qap"], start=True, stop=(nadd == 0)),
                 r=t["names"] + ["Q"], w=[psn[si]])
            for ai, (clo, chi, la, ra, an) in enumerate(t["adds"]):
                P.pe(lambda e, clo=clo, chi=chi, la=la, ra=ra, ai=ai: e.matmul(S[0:n, clo:chi], la, ra, start=False, stop=(ai == nadd - 1)),
                     r=an, w=[psn[si]])
            pi = nxt("pt", 4)
            if t["bias"] is not None:
                P.act(lambda e: e.activation(Pt[pi][0:n, qlo:qhi], S[0:n, qlo:qhi], AF.Exp, bias=t["bias"]),
                      r=[psn[si], "nbias"], w=[f"Pt{pi}"])
            else:
                P.act(lambda e: e.activation(Pt[pi][0:n, qlo:qhi], S[0:n, qlo:qhi], AF.Exp), r=[psn[si]], w=[f"Pt{pi}"])
            t["pi"] = pi
            self.pend.append(t)
            if len(self.pend) > 2:
                self.pv(self.pend.pop(0))

        def pv(self, t):
            n = t["n"]; qlo, qhi = t["qlo"], t["qhi"]; pi = t["pi"]; O = t["O"]
            P.pe(lambda e: e.matmul(O[0:65, qlo:qhi], t["v"], Pt[pi][0:n, qlo:qhi], start=t["first"], stop=t["last"]),
                 r=[f"Pt{pi}"] + t["names"], w=[t["oname"]])

        def flush(self):
            while self.pend:
                self.pv(self.pend.pop(0))


    for l in range(nlayers):
        P.pool(lambda e: e.memset(crs[:], 0.0), w=["crs"])
        P.dma("sp", crs[:], crel_d[l].rearrange("c p h -> p c h"), w=["crs"])
        build_tab([crs[:, c, :] for c in range(3)], [ohcs[:, c, :] for c in range(3)], tabc[l], ["crs", "ohcs"])
        build_toeplitz(tabc[l], bc)
        P.dve(lambda e: e.memset(tot[:], 0.0), w=["tot"])
        P.dve(lambda e: e.memset(tots[:], 0.0), w=["tots"])
        P.dve(lambda e: e.memset(totbc[:], 0.0), w=["totbc"])
        KAl, KBl, IKl, VAl, VBl = S_KA[l], S_KB[l], S_IK[l], S_VA[l], S_VB[l]
        KCl, VCl = S_KCL[l], S_VCL[l]
        hist = f"~hist{l}"
        chist = f"~chist{l}"
        deferred_exchange = []

        def grow(gp):
            if l == 0:
                return xp[gp * 512:(gp + 1) * 512, :]
            return hpg[gp * 512:(gp + 1) * 512, :]

        def tm_unit(uname, handler, wres=None):
            W, wn = wres if wres is not None else load_w(l, UNITS.index(uname))
            for tb in range(4):
                si = nxt("s", 4)
                for c in range(8):
                    P.pe(lambda e: e.matmul(ps[si][:, :], hT[:, c, tb * 128:(tb + 1) * 128], W[:, c, :], start=(c == 0), stop=(c == 7)),
                         r=[wn, "hT"], w=[psn[si]])
                k = nxt("st", 2)
                S_ = st[k]; sn = f"st{k}"
                P.act(lambda e: e.activation(S_[:], ps[si][:], AF.Copy), r=[psn[si]], w=[sn])
                handler(tb, S_, sn, k)

        def logf_of(S_, sn):
            P.dve(lambda e: e.tensor_tensor(lfb[:], S_[:, 288:296], bfb[:, l * 8:(l + 1) * 8], ALU.add), r=[sn, "bfb"], w=["lfb"])
            P.act(lambda e: e.activation(lfb[:], lfb[:], AF.Exp, scale=-1.0), r=["lfb"], w=["lfb"])
            P.act(lambda e: e.activation(lfb[:], lfb[:], AF.Ln, bias=1.0), r=["lfb"], w=["lfb"])
            P.dve(lambda e: e.tensor_scalar(lfb[:], lfb[:], -1.0, None, ALU.mult), r=["lfb"], w=["lfb"])

        def cum_into(dst_ap, dname):
            P.pe(lambda e: e.matmul(ps[5][:, 0:8], TRI, lfb[:], start=True, stop=False), r=["lfb", "cstf"], w=["ps5"])
            P.pe(lambda e: e.matmul(ps[5][:, 0:8], ones1[0:1, 0:128], tot[0:1, :], start=False, stop=True), r=["tot", "ones1"], w=["ps5"])
            P.dve(lambda e: e.tensor_copy(dst_ap, ps[5][:, 0:8]), r=["ps5"], w=[dname])
            P.pe(lambda e: e.matmul(ps[5][0:1, 8:16], E127, dst_ap, start=True, stop=True), r=[dname, "cstf"], w=["ps5"])
            P.dve(lambda e: e.tensor_copy(tot[:], ps[5][0:1, 8:16]), r=["ps5"], w=["tot"])

        def evac_k_to(dst3, k0, hname):
            def f(j, pt, pn):
                P.act(lambda e: e.activation(kst[:, j, :], pt[0:64, :], AF.Copy), r=[pn], w=["kst"])
                if j == 7:
                    P.dma("pool", dst3[:, :, k0:k0 + 512].rearrange("h d k -> d h k"), kst[:], r=["kst"], w=[hname])
            return f

        def evac_q(scale):
            def f(j, pt, pn):
                P.act(lambda e: e.activation(Q[0:64, j, :], pt[0:64, :], AF.Copy, scale=scale), r=[pn], w=["Q"])
            return f

        def evac_z(zt, zn):
            def f(j, pt, pn):
                P.act(lambda e: e.activation(zt[:, j, :], pt[0:64, :], AF.Silu), r=[pn], w=[zn])
            return f

        scb = SC.bitcast(BF16)
        kres = {}
        for ui, un in enumerate(("tka", "tva", "tb", "ka")):
            v = scb[:, ui * 4096:(ui + 1) * 4096].rearrange("p (c n) -> p c n", c=8)
            P.dma("sp", v, wub[l, UNITS.index(un)], r=[f"wub{l}_{UNITS.index(un)}"], w=["SC"])
            kres[un] = (v, "SC")
        v = junk[:, 4096:8192].rearrange("p (c n) -> p c n", c=8)
        P.dma("sp", v, wub[l, UNITS.index("bx")], r=[f"wub{l}_{UNITS.index('bx')}"], w=["junk", "junkW"])
        kres["bx"] = (v, "junkW")
        for gp in range(4 * nm):
            t0 = gp * 512
            src = grow(gp)
            for tb in range(4):
                norm_block(l, src[tb * 128:(tb + 1) * 128, :], tb, rname=("~hpgw" if l > 0 else None))

            def h_kv(uname):
                def f(tb, S_, sn, k):
                    r0 = t0 + tb * 128
                    dst = {"tka": o_ak, "tva": o_av, "tkc": o_ck, "tvc": o_cv}[uname]
                    if uname in ("tka", "tva"):
                        P.dma("pool", dst[l, r0:r0 + 128, :], S_[:], r=[sn])
                    else:
                        P.dma("pool", dst[l, r0 - (SEQ - 512):r0 - (SEQ - 512) + 128, :], S_[:], r=[sn])
                    if uname == "tva":
                        V_, vn = vst[k], f"vst{k}"
                        P.dve(lambda e: e.tensor_copy(V_[:], S_[:]), r=[sn], w=[vn])
                        P.dma("pool", VAl[r0:r0 + 128, :], V_[:], r=[vn], w=[hist])
                return f

            def h_tb(tb, S_, sn, k):
                r0 = t0 + tb * 128
                P.dma("pool", o_bk[l, r0:r0 + 128, :], S_[:, 0:128], r=[sn])
                P.dma("pool", o_bv[l, r0:r0 + 128, :], S_[:, 128:256], r=[sn])
                P.dma("pool", o_ik[l, r0:r0 + 128, :], S_[:, 256:288], r=[sn])
                V_, vn = vst[k], f"vst{k}"
                P.dve(lambda e: e.tensor_copy(V_[:, 0:128], S_[:, 128:256]), r=[sn], w=[vn])
                P.dma("pool", VBl[r0:r0 + 128, :], V_[:, 0:128], r=[vn], w=[hist])
                P.dve(lambda e: e.tensor_tensor(lf4[:, tb, :], S_[:, 288:296], bfb[:, l * 8:(l + 1) * 8], ALU.add), r=[sn, "bfb"], w=["lf4"])

            tm_unit("tka", h_kv("tka"), wres=kres["tka"])
            tm_unit("tva", h_kv("tva"), wres=kres["tva"])
            tm_unit("tb", h_tb, wres=kres["tb"])
            lf4f = lf4[:].rearrange("p b h -> p (b h)")
            P.act(lambda e: e.activation(lf4f, lf4f, AF.Exp, scale=-1.0), r=["lf4"], w=["lf4"])
            P.act(lambda e: e.activation(lf4f, lf4f, AF.Ln, bias=1.0), r=["lf4"], w=["lf4"])
            P.dve(lambda e: e.tensor_scalar(lf4f, lf4f, -1.0, None, ALU.mult), r=["lf4"], w=["lf4"])
            P.dma("pool", o_lf[l, t0:t0 + 512, :].rearrange("(b p) h -> p b h", p=128), lf4[:], r=["lf4"])
            if gp == NG - 1:
                tm_unit("tkc", h_kv("tkc"))
                tm_unit("tvc", h_kv("tvc"))
            fm_unit(l, "ka", 512, evac_k_to(KAl, t0, hist), wres=kres["ka"])
            for b_ in range(4):
                for b2 in range(b_ + 1):
                    P.pe(lambda e: e.matmul(ps[5][:, b_ * 8:(b_ + 1) * 8], (TRI if b2 == b_ else onesf[:, :]), lf4[:, b2, :], start=(b2 == 0), stop=(b2 == b_)),
                         r=["lf4", "cstf", "onesf"], w=["ps5"])
            for b2 in range(4):
                P.pe(lambda e: e.matmul(ps[5][:, 32:40], onesf[:, :], lf4[:, b2, :], start=(b2 == 0), stop=(b2 == 3)), r=["lf4", "onesf"], w=["ps5"])
            for b_ in range(4):
                P.dve(lambda e: e.tensor_tensor(cstore[:, 4 * gp + b_, :], ps[5][:, b_ * 8:(b_ + 1) * 8], totbc[:, :], ALU.add), r=["ps5", "totbc"], w=["cstore"])
            P.dve(lambda e: e.tensor_tensor(totbc[:, :], ps[5][:, 32:40], totbc[:, :], ALU.add), r=["ps5", "totbc"], w=["totbc"])
            P.dve(lambda e: e.tensor_copy(tots[0:1, gp + 1, :], totbc[0:1, :]), r=["totbc"], w=["tots"])

            def evac_bx_k(j, pt, pn):
                if j < 2:
                    P.act(lambda e: e.activation(kst[:, j, :], pt[0:64, :], AF.Copy), r=[pn], w=["kst"])
                    if j == 1:
                        P.dma("pool", KBl[:, :, t0:t0 + 512].rearrange("h d k -> d h k"), kst[:, 0:2, :], r=["kst"], w=[hist])
                elif j == 6:
                    P.act(lambda e: e.activation(kst[:, 2, :], pt[0:64, :], AF.Copy), r=[pn], w=["kst"])
                    P.dma("pool", IKl[:, t0:t0 + 512], kst[:, 2, :], r=["kst"], w=[hist])
            fm_unit(l, "bx", 512, evac_bx_k, blocks=(0, 1, 6), wres=kres["bx"])

        for m in range(nm):
            own = (xq[m] if l == 0 else hp1q[m * 512:(m + 1) * 512, :])
            own_r = (None if l == 0 else [f"hp1qc{2 * m}", f"hp1qc{2 * m + 1}"])
            for part in range(2):
                for tb in range(4):
                    if part == 1:
                        norm_block(l, own[tb * 128:(tb + 1) * 128, :], tb, rname=own_r)
                    elif l == 0:
                        norm_block(l, xprev[m][tb * 128:(tb + 1) * 128, :], tb)
                    else:
                        first = True
                        for r in range(4):
                            gq = 4 * m - 1 + r
                            if gq < 0:
                                continue
                            xi = nxt("x", 2)
                            X = xt[xi]; xname = f"xt{xi}"
                            P.dma("sp", X[:], grow(gq)[tb * 128:(tb + 1) * 128, :], r=["~hpgw"], w=[xname])
                            if first:
                                P.dve(lambda e: e.tensor_scalar(xacc[:], X[:], selb[:, r:r + 1], None, ALU.mult), r=[xname, "selb"], w=["xacc"])
                            else:
                                P.dve(lambda e: e.scalar_tensor_tensor(xacc[:], X[:], selb[:, r:r + 1], xacc[:], ALU.mult, ALU.add), r=[xname, "selb", "xacc"], w=["xacc"])
                            first = False
                        norm_block(l, None, tb, sb_src=(xacc, "xacc"))

                def h_c(uname):
                    def f(tb, S_, sn, k):
                        if uname == "tvc":
                            V_, vn = vst[k], f"vst{k}"
                            P.dve(lambda e: e.tensor_copy(V_[:], S_[:]), r=[sn], w=[vn])
                            P.dma("pool", VCl[part * 512 + tb * 128:part * 512 + (tb + 1) * 128, :], V_[:], r=[vn], w=[chist])
                    return f
                tm_unit("tvc", h_c("tvc"))
                fm_unit(l, "kc", 512, evac_k_to(KCl, part * 512, chist))
            g0 = 16 * m
            nkb = 16 * m + 16
            for r in range(4):
                if r == 0:
                    P.dve(lambda e: e.tensor_scalar(tot[0:1, :], tots[0:1, 4 * m + r, :], selb[0:1, r:r + 1], None, ALU.mult), r=["tots", "selb"], w=["tot"])
                else:
                    P.dve(lambda e: e.scalar_tensor_tensor(tot[0:1, :], tots[0:1, 4 * m + r, :], selb[0:1, r:r + 1], tot[0:1, :], ALU.mult, ALU.add),
                          r=["tots", "selb", "tot"], w=["tot"])

            def h_own(tb, S_, sn, k):
                P.dve(lambda e: e.tensor_tensor(lf4[:, tb, :], S_[:, 288:296], bfb[:, l * 8:(l + 1) * 8], ALU.add), r=[sn, "bfb"], w=["lf4"])
                P.dve(lambda e: e.tensor_scalar(wsgn[:, tb, :], S_[:, 296:304], 0.0, 2.0, ALU.is_ge, ALU.mult), r=[sn], w=["wsgn"])
                P.dve(lambda e: e.tensor_scalar(wsgn[:, tb, :], wsgn[:, tb, :], -1.0, None, ALU.add), r=["wsgn"], w=["wsgn"])
                P.dve(lambda e: e.scalar_tensor_tensor(wabs[:, tb, :], S_[:, 296:304], IDXS, wsgn[:, tb, :], ALU.mult, ALU.mult), r=[sn, "wsgn"], w=["wabs"])
            tm_unit("tb", h_own)
            lf4q = lf4[:].rearrange("p b h -> p (b h)")
            P.act(lambda e: e.activation(lf4q, lf4q, AF.Exp, scale=-1.0), r=["lf4"], w=["lf4"])
            P.act(lambda e: e.activation(lf4q, lf4q, AF.Ln, bias=1.0), r=["lf4"], w=["lf4"])
            P.dve(lambda e: e.tensor_scalar(lf4q, lf4q, -1.0, None, ALU.mult), r=["lf4"], w=["lf4"])
            for b_ in range(4):
                P.pe(lambda e: e.matmul(ps[5][:, b_ * 8:(b_ + 1) * 8], ones1[0:1, 0:128], tot[0:1, :], start=True, stop=False), r=["tot", "ones1"], w=["ps5"])
                for b2 in range(b_ + 1):
                    P.pe(lambda e: e.matmul(ps[5][:, b_ * 8:(b_ + 1) * 8], (TRI if b2 == b_ else onesf[:, :]), lf4[:, b2, :], start=False, stop=(b2 == b_)),
                         r=["lf4", "cstf", "onesf"], w=["ps5"])
            P.dve(lambda e: e.tensor_copy(cown[:].rearrange("p b h -> p (b h)"), ps[5][:, 0:32]), r=["ps5"], w=["cown"])
            P.pe(lambda e: e.matmul(ps[5][:, 16:24], E0ROW, cstore[:, g0, :], start=True, stop=True), r=["cstore", "cstf"], w=["ps5"])
            P.dve(lambda e: e.tensor_copy(cbc[:], ps[5][:, 16:24]), r=["ps5"], w=["cbc"])
            for h in range(8):
                P.dve(lambda e: e.tensor_scalar(nbias[:, 0:nkb, h], cstore[:, 0:nkb, h], cbc[:, h:h + 1], -1.0, ALU.subtract, ALU.mult),
                      r=["cstore", "cbc"], w=["nbias"])
            for tb in range(4):
                P.dve(lambda e: e.tensor_tensor(rq[:, tb, :], cown[:, tb, :], cbc[:], ALU.subtract), r=["cown", "cbc"], w=["rq"])
                P.pe(lambda e: e.transpose(ps[6][0:8, tb * 128:(tb + 1) * 128], rq[:, tb, :], IDENT), r=["rq", "cstf"], w=["ps6"])
            P.dve(lambda e: e.tensor_copy(rT[:], ps[6][0:8, :]), r=["ps6"], w=["rT"])

            def evac_bx_q(j, pt, pn):
                P.act(lambda e: e.activation(iqT[:, j - 2, :], pt[0:64, :], AF.Copy), r=[pn], w=["iqT"])

            def a_half(half):
                at = Attn()
                for sbk in range(4 * m + 4):
                    bi = load_kv(KAl, VAl, half * 4, 4, sbk * 512, 512, hist)
                    trail = (sbk >= 4 * m)
                    for kb in range(4):
                        adds = []
                        if trail:
                            rr = (sbk - 4 * m) * 4 + kb
                            mi = nxt("mb", 2)
                            P.dma("sp", Mb[mi][:, :], AMASK[rr], r=["~amask"], w=[f"Mb{mi}"])
                            adds = [(0, 512, identb[:], Mb[mi][:, :], ["identb", f"Mb{mi}"])]
                        for i in range(4):
                            hh = half * 4 + i
                            at.tile(dict(kT=kbuf[bi][0:65, i, kb * 128:(kb + 1) * 128], v=vbuf[bi][:, kb, i, 0:65], n=128, qlo=0, qhi=512,
                                         qap=Q[0:65, hh, 0:512], adds=adds, bias=nbias[:, 4 * sbk + kb, hh:hh + 1],
                                         names=[f"kbuf{bi}", f"vbuf{bi}"], O=ps[4 + i], oname=psn[4 + i],
                                         first=(sbk == 0 and kb == 0), last=(sbk == 4 * m + 3 and kb == 3)))
                at.flush()
                for i in range(4):
                    finish_head(ps[4 + i], psn[4 + i], zg["a"], "zga", half * 4 + i, 512, slice(0, 512))

            def c_half(half):
                at = Attn()
                started = [False] * 4
                for sbl in range(2):
                    bi = load_kv(KCl, VCl, half * 4, 4, sbl * 512, 512, chist)
                    for kb in range(4):
                        r_ = 4 * sbl + kb
                        qb_lo, qb_hi = max(0, r_ - 4), min(3, r_)
                        qlo, qhi = qb_lo * 128, (qb_hi + 1) * 128
                        for i in range(4):
                            hh = half * 4 + i
                            adds = []
                            for qb in range(qb_lo, qb_hi + 1):
                                dl = r_ - 4 - qb
                                c0, c1 = qb * 128, (qb + 1) * 128
                                if dl == 0:
                                    adds.append((c0, c1, identb[:], bc[:, 0, hh, :], ["identb", "btile"]))
                                    adds.append((c0, c1, identb[:], cm0b[:], ["identb", "cm0b"]))
                                elif dl == -1:
                                    adds.append((c0, c1, identb[:], bc[:, 1, hh, :], ["identb", "btile"]))
                                elif dl == -4:
                                    adds.append((c0, c1, identb[:], cm4b[:], ["identb", "cm4b"]))
                            at.tile(dict(kT=kbuf[bi][0:64, i, kb * 128:(kb + 1) * 128], v=vbuf[bi][:, kb, i, 0:65], n=128, qlo=qlo, qhi=qhi,
                                         qap=Q[0:64, hh, qlo:qhi], adds=adds, bias=(pvb[:, 0:1] if (m == 0 and sbl == 0) else None),
                                         names=[f"kbuf{bi}", f"vbuf{bi}"], O=ps[4 + i], oname=psn[4 + i],
                                         first=(not started[i]), last=(sbl == 1 and kb == 3)))
                            started[i] = True
                at.flush()
                for i in range(4):
                    finish_head(ps[4 + i], psn[4 + i], zg["c"], "zgc", half * 4 + i, 512, slice(0, 512))

            def b_topk(qb):
                NK = (16 * m + 13 + qb) * 128
                qs = slice(qb * 128, (qb + 1) * 128)
                for k0 in range(0, NK, 512):
                    nk = min(512, NK - k0)
                    ii = nxt("ik", 2)
                    P.dma("sp", ikbuf[ii][:, 0:nk], IKl[:, k0:k0 + nk], r=[hist], w=[f"ikbuf{ii}"])
                    for h in range(8):
                        base = 32 * (h % 2)
                        pi_ = 1 + nxt("ips", 3)
                        P.pe(lambda e: e.matmul(ps[pi_][:, 0:nk], iqT[base:base + 32, h // 2, qs], ikbuf[ii][base:base + 32, 0:nk], start=True, stop=True),
                             r=["iqT", f"ikbuf{ii}"], w=[psn[pi_]])
                        ri = nxt("rr", 3)
                        P.act(lambda e: e.activation(Rr[ri][:, 0:nk], ps[pi_][:, 0:nk], AF.Relu, scale=wabs[:, qb, h:h + 1]), r=[psn[pi_], "wabs"], w=[f"Rr{ri}"])
                        if h == 0:
                            P.dve(lambda e: e.tensor_scalar(SC[:, k0:k0 + nk], Rr[ri][:, 0:nk], wsgn[:, qb, 0:1], None, ALU.mult), r=[f"Rr{ri}", "wsgn"], w=["SC"])
                        else:
                            P.dve(lambda e: e.scalar_tensor_tensor(SC[:, k0:k0 + nk], Rr[ri][:, 0:nk], wsgn[:, qb, h:h + 1], SC[:, k0:k0 + nk], ALU.mult, ALU.add),
                                  r=[f"Rr{ri}", "wsgn", "SC"], w=["SC"])
                for ch in range(4):
                    c0 = g0 * 128 + ch * 512
                    wd = min(512, NK - c0)
                    if wd <= 0:
                        continue
                    ri = nxt("rr", 3)
                    P.dve(lambda e: e.tensor_scalar(Rr[ri][:, 0:wd], iota5[:, 0:wd], qlimc[:, qb * 4 + ch:qb * 4 + ch + 1], -1e30, ALU.is_ge, ALU.mult),
                          r=["iota5", "qlimc"], w=[f"Rr{ri}"])
                    P.dve(lambda e: e.tensor_tensor(SC[:, c0:c0 + wd], SC[:, c0:c0 + wd], Rr[ri][:, 0:wd], ALU.add), r=["SC", f"Rr{ri}"], w=["SC"])
                P.dve(lambda e: e.memset(cntb[:], 0.0), w=["cntb"])
                P.dve(lambda e: e.memset(small[:, 8:9], 0.0), w=["cand"])
                for it in range(NBIS):
                    stp = 64.0 * (0.5 ** it)
                    P.dve(lambda e: e.tensor_scalar(junk[:, 0:NK], SC[:, 0:NK], small[:, 8:9], 0.0, ALU.is_ge, ALU.add, accum_out=cntb[:, it:it + 1]),
                          r=["SC", "cand", "cntb"], w=["junk", "cntb"])
                    a, b_ = (stp, -0.5 * stp) if it < NBIS - 1 else (stp, -stp)
                    P.dve(lambda e: e.tensor_scalar(small[:, 9:10], cntb[:, it:it + 1], float(TOPK), a, ALU.is_ge, ALU.mult), r=["cntb"], w=["fl"])
                    P.dve(lambda e: e.scalar_tensor_tensor(small[:, 8:9], small[:, 9:10], b_, small[:, 8:9], ALU.add, ALU.add), r=["fl", "cand"], w=["cand"])
                P.dve(lambda e: e.tensor_scalar(junk[:, 0:NK], SC[:, 0:NK], small[:, 8:9], NEG, ALU.is_lt, ALU.mult), r=["SC", "cand"], w=["junk"])
                P.dma("pool", MBS[qb][:, 0:NK], junk[:, 0:NK], r=["junk"], w=[f"mbs{qb}"])

            def b_attn(qb):
                qs = slice(qb * 128, (qb + 1) * 128)
                nkq = 16 * m + 13 + qb
                at = Attn()
                for sbk in range((nkq + 3) // 4):
                    k0 = sbk * 512
                    nk = min(512, nkq * 128 - k0)
                    mi = nxt("mb", 2)
                    P.dma("sp", Mb[mi][:, 0:nk], MBS[qb][:, k0:k0 + nk], r=[f"mbs{qb}"], w=[f"Mb{mi}"])
                    bi = load_kv(KBl, VBl, 0, 2, k0, nk, hist)
                    for kb in range(nk // 128):
                        gkb = sbk * 4 + kb
                        for jj in range(2):
                            adds = [(0, 512, Mb[mi][:, kb * 128:(kb + 1) * 128], i4b[:].rearrange("p a b -> p (a b)"), [f"Mb{mi}", "i4b"])]
                            if gkb >= g0 - 1 and (gkb - g0 - qb) % 4 in (0, 3):
                                rr = gkb - g0 + 1
                                ci = nxt("cmb", 2)
                                sc0 = sel01[:, (qb * 17 + rr) * 2:(qb * 17 + rr) * 2 + 1]
                                sc1 = sel01[:, (qb * 17 + rr) * 2 + 1:(qb * 17 + rr) * 2 + 2]
                                P.dma("sp", comb[ci][:].rearrange("p a b -> p (a b)"), COMBS[qb, rr, jj], r=["~combs"], w=[f"comb{ci}"])
                                adds.append((0, 512, identb[:], comb[ci][:].rearrange("p a b -> p (a b)"), ["identb", f"comb{ci}"]))
                            at.tile(dict(kT=kbuf[bi][0:64, jj, kb * 128:(kb + 1) * 128], v=vbuf[bi][:, kb, jj, 0:65], n=128, qlo=0, qhi=512,
                                         qap=Q[0:64, 4 * jj:4 * jj + 4, qs], adds=adds, bias=None,
                                         names=[f"kbuf{bi}", f"vbuf{bi}"], O=ps[4 + jj], oname=psn[4 + jj],
                                         first=(gkb == 0), last=(gkb == nkq - 1)))
                at.flush()
                for jj in range(2):
                    finish_head(ps[4 + jj], psn[4 + jj], zg["b"], "zgb", (4 * jj, 4 * jj + 4), 512, qs)

            fm_unit(l, "bx", 512, evac_bx_q, blocks=(2, 3, 4, 5))
            b_topk(0)
            fm_unit(l, "za", 512, evac_z(zg["a"], "zga"))
            fm_unit(l, "qa", 512, evac_q(SCALE))
            for h in range(8):
                P.dma("sp", Q[64:65, h, :], rT[h:h + 1, :], r=["rT"], w=["Q"])
            a_half(0)
            b_topk(1)
            a_half(1)
            fm_unit(l, "zc", 512, evac_z(zg["c"], "zgc"))
            fm_unit(l, "qc", 512, evac_q(SCALE))
            c_half(0)
            c_half(1)
            fm_unit(l, "zb", 512, evac_z(zg["b"], "zgb"))
            fm_unit(l, "qb", 512, evac_q(SCALE))
            b_attn(0)
            b_topk(2)
            b_attn(1)
            b_topk(3)
            b_attn(2)
            b_attn(3)

            allz = [zg["a"], zg["b"], zg["c"]]
            alln = ["zga", "zgb", "zgc"]
            for u in range(6):
                Wo_, won = load_wo(l, u)
                for hq in range(4):
                    hidx = u * 4 + hq
                    zt, zn = allz[hidx // 8], alln[hidx // 8]
                    for tb in range(4):
                        for n_ in range(2):
                            P.pe(lambda e: e.matmul(ps[tb * 2 + n_][:, :], zt[:, hidx % 8, tb * 128:(tb + 1) * 128], Wo_[:, hq, n_ * 512:(n_ + 1) * 512],
                                                    start=(hidx == 0), stop=(hidx == 23)), r=[zn, won], w=[psn[tb * 2 + n_]])
            for tb in range(4):
                xi = nxt("x", 2)
                X = xt[xi]; xname = f"xt{xi}"
                P.dma("sp", X[:], own[tb * 128:(tb + 1) * 128, :], r=(own_r if own_r else []), w=[xname])
                for n_ in range(2):
                    P.dve(lambda e: e.tensor_tensor(X[:, n_ * 512:(n_ + 1) * 512], X[:, n_ * 512:(n_ + 1) * 512], ps[tb * 2 + n_][:, :], ALU.add),
                          r=[xname, psn[tb * 2 + n_]], w=[xname])
                r0 = m * 512 + tb * 128
                if l < nlayers - 1:
                    P.dma("pool", hp1q[r0:r0 + 128, :], X[:], r=[xname], w=[f"hp1qc{r0 // 256}"])
                else:
                    P.act(lambda e: e.activation(junk[:, 0:1024], X[:], AF.Square, accum_out=small[:, 16:17]), r=[xname], w=["junk", "fs0"])
                    P.dve(lambda e: e.tensor_scalar(small[:, 17:18], small[:, 16:17], 1.0 / D_MODEL, EPS, ALU.mult, ALU.add), r=["fs0"], w=["fs1"])
                    P.act(lambda e: e.activation(small[:, 18:19], small[:, 17:18], AF.Sqrt), r=["fs1"], w=["fs2"])
                    P.dve(lambda e: e.reciprocal(small[:, 19:20], small[:, 18:19]), r=["fs2"], w=["fs3"])
                    P.dve(lambda e: e.scalar_tensor_tensor(X[:], X[:], small[:, 19:20], fgb[:], ALU.mult, ALU.mult), r=[xname, "fs3", "fgb"], w=[xname])
                    P.dma("pool", y_q[r0:r0 + 128, :], X[:], r=[xname])
            def emit_exchange(m=m):
                for hf in range(2):
                    cidx = 2 * m + hf
                    P.dma("pool", ccs, hp1q[cidx * 256:(cidx + 1) * 256, :], r=[f"hp1qc{cidx}"], w=["ccs"])
                    P.add("pool", lambda e: e.collective_compute("AllGather", ALU.bypass, replica_groups=[[0, 1, 2, 3], [4, 5, 6, 7]],
                                                                 ins=[ccs.opt()], outs=[ccd.opt()]), r=["ccs"], w=["ccd"], cc=True)
                    for r in range(4):
                        a0 = (4 * m + r) * 512 + hf * 256
                        P.dma("pool", hpg[a0:a0 + 256, :], ccd[r * 256:(r + 1) * 256, :], r=["ccd"], w=["~hpgw"])
            if l < nlayers - 1:
                if m < nm - 1:
                    emit_exchange()
                else:
                    deferred_exchange.append(emit_exchange)
        if do_sample:
            NS = DEC_SEQ
            shist = f"~shist{l}"
            psb7 = ps[7].bitcast(BF16)

            def prep_cache(src2d, nrows, ncols, kdst, vdst, nheads):
                for b in range(nrows // 128):
                    xi = nxt("x", 2)
                    X = xt[xi]; xname = f"xt{xi}"
                    P.dma("sp", X[:, 0:ncols], src2d[b * 128:(b + 1) * 128, :], w=[xname])
                    P.dve(lambda e: e.tensor_copy(xn[:, 0:ncols], X[:, 0:ncols]), r=[xname], w=["xn"])
                    if vdst is not None:
                        P.dma("pool", vdst[b * 128:(b + 1) * 128, :], xn[:, 0:ncols], r=["xn"], w=[shist])
                    if kdst is not None:
                        for hh in range(nheads):
                            P.pe(lambda e: e.transpose(psb7[0:64, hh * 128:(hh + 1) * 128], xn[:, hh * 64:(hh + 1) * 64], identb[:]),
                                 r=["xn", "identb"], w=["ps7"])
                        P.act(lambda e: e.activation(kst[:, 0:nheads, 0:128], psb7[0:64, 0:nheads * 128].rearrange("p (h k) -> p h k", h=nheads), AF.Copy),
                              r=["ps7"], w=["kst"])
                        P.dma("pool", kdst[:, :, b * 128:(b + 1) * 128].rearrange("h d k -> d h k"), kst[:, 0:nheads, 0:128], r=["kst"], w=[shist])

            prep_cache(ca_k[l], PAST, 512, SS_KA[l], None, 8)
            prep_cache(ca_v[l], PAST, 512, None, SS_VA[l], 8)
            prep_cache(cb_k[l], PAST, 128, SS_KB[l], None, 2)
            prep_cache(cb_v[l], PAST, 128, None, SS_VB[l], 2)
            prep_cache(cc_k[l], 512, 512, SS_KC[l], None, 8)
            prep_cache(cc_v[l], 512, 512, None, SS_VC[l], 8)
            for b in range(PAST // 128):
                xi = nxt("x", 2)
                X = xt[xi]; xname = f"xt{xi}"
                P.dma("sp", X[:, 0:32], cb_ik[l, b * 128:(b + 1) * 128, :], w=[xname])
                P.dve(lambda e: e.tensor_copy(xn[:, 0:32], X[:, 0:32]), r=[xname], w=["xn"])
                P.dve(lambda e: e.tensor_copy(xn[:, 32:64], X[:, 0:32]), r=[xname], w=["xn"])
                P.pe(lambda e: e.transpose(psb7[0:64, 0:128], xn[:, 0:64], identb[:]), r=["xn", "identb"], w=["ps7"])
                P.act(lambda e: e.activation(kst[:, 0, 0:128], psb7[0:64, 0:128], AF.Copy), r=["ps7"], w=["kst"])
                P.dma("pool", SS_IK[l][:, b * 128:(b + 1) * 128], kst[:, 0, 0:128], r=["kst"], w=[shist])
            P.dve(lambda e: e.memset(tot[:], 0.0), w=["tot"])

            def cum_block(kb_, n):
                P.pe(lambda e: e.matmul(ps[5][0:n, 0:8], cstf[0:n, 2, 0:n], lfb[0:n, :], start=True, stop=False), r=["lfb", "cstf"], w=["ps5"])
                P.pe(lambda e: e.matmul(ps[5][0:n, 0:8], ones1[0:1, 0:n], tot[0:1, :], start=False, stop=True), r=["tot", "ones1"], w=["ps5"])
                P.dve(lambda e: e.tensor_copy(cstore[0:n, kb_, :], ps[5][0:n, 0:8]), r=["ps5"], w=["cstore"])
                P.pe(lambda e: e.matmul(ps[5][0:1, 8:16], cstf[0:n, 1, 128 - n:129 - n], cstore[0:n, kb_, :], start=True, stop=True), r=["cstore", "cstf"], w=["ps5"])
                P.dve(lambda e: e.tensor_copy(tot[:], ps[5][0:1, 8:16]), r=["ps5"], w=["tot"])

            for b in range(PAST // 128):
                P.dma("sp", lfb[:], ca_lf[l, b * 128:(b + 1) * 128, :], w=["lfb"])
                cum_block(b, 128)
            norm_block(l, (xs if l == 0 else hs1)[:, :], 0, nrow=NS, rname=("hs1" if l > 0 else None))
            for uname in TM_UNITS:
                W, wn = load_w(l, UNITS.index(uname))
                si = nxt("s", 4)
                for c in range(8):
                    P.pe(lambda e: e.matmul(ps[si][0:NS, :], hT[:, c, 0:NS], W[:, c, :], start=(c == 0), stop=(c == 7)), r=[wn, "hT"], w=[psn[si]])
                k = nxt("st", 2)
                S_ = st[k]; sn = f"st{k}"
                P.act(lambda e: e.activation(S_[0:NS, :], ps[si][0:NS, :], AF.Copy), r=[psn[si]], w=[sn])
                V_, vn = vst[k], f"vst{k}"
                if uname in ("tka", "tva", "tkc", "tvc"):
                    dst = {"tka": s_ak, "tva": s_av, "tkc": s_ck, "tvc": s_cv}[uname]
                    P.dma("pool", dst[l], S_[0:NS, :], r=[sn])
                    if uname in ("tva", "tvc"):
                        P.dve(lambda e: e.tensor_copy(V_[0:NS, :], S_[0:NS, :]), r=[sn], w=[vn])
                        vd = SS_VA[l][PAST:PAST + NS, :] if uname == "tva" else SS_VC[l][512:512 + NS, :]
                        P.dma("pool", vd, V_[0:NS, :], r=[vn], w=[shist])
                else:
                    P.dma("pool", s_bk[l], S_[0:NS, 0:128], r=[sn])
                    P.dma("pool", s_bv[l], S_[0:NS, 128:256], r=[sn])
                    P.dma("pool", s_ik[l], S_[0:NS, 256:288], r=[sn])
                    P.dve(lambda e: e.tensor_copy(V_[0:NS, 0:128], S_[0:NS, 128:256]), r=[sn], w=[vn])
                    P.dma("pool", SS_VB[l][PAST:PAST + NS, :], V_[0:NS, 0:128], r=[vn], w=[shist])
                    P.dve(lambda e: e.tensor_tensor(lfb[0:NS, :], S_[0:NS, 288:296], bfb[0:NS, l * 8:(l + 1) * 8], ALU.add), r=[sn, "bfb"], w=["lfb"])
                    P.act(lambda e: e.activation(lfb[0:NS, :], lfb[0:NS, :], AF.Exp, scale=-1.0), r=["lfb"], w=["lfb"])
                    P.act(lambda e: e.activation(lfb[0:NS, :], lfb[0:NS, :], AF.Ln, bias=1.0), r=["lfb"], w=["lfb"])
                    P.dve(lambda e: e.tensor_scalar(lfb[0:NS, :], lfb[0:NS, :], -1.0, None, ALU.mult), r=["lfb"], w=["lfb"])
                    P.dma("pool", s_lf[l], lfb[0:NS, :], r=["lfb"])
                    cum_block(8, NS)
                    P.dve(lambda e: e.tensor_scalar(wsgn[0:NS, 0, :], S_[0:NS, 296:304], 0.0, 2.0, ALU.is_ge, ALU.mult), r=[sn], w=["wsgn"])
                    P.dve(lambda e: e.tensor_scalar(wsgn[0:NS, 0, :], wsgn[0:NS, 0, :], -1.0, None, ALU.add), r=["wsgn"], w=["wsgn"])
                    P.dve(lambda e: e.scalar_tensor_tensor(wabs[0:NS, 0, :], S_[0:NS, 296:304], IDXS, wsgn[0:NS, 0, :], ALU.mult, ALU.mult), r=[sn, "wsgn"], w=["wabs"])
            P.pe(lambda e: e.matmul(ps[5][:, 16:24], cstf[0:NS, 3, :], cstore[0:NS, 8, :], start=True, stop=True), r=["cstore", "cstf"], w=["ps5"])
            P.dve(lambda e: e.tensor_copy(cbc[:], ps[5][:, 16:24]), r=["ps5"], w=["cbc"])
            for h in range(8):
                P.dve(lambda e: e.tensor_scalar(nbias[:, 0:9, h], cstore[:, 0:9, h], cbc[:, h:h + 1], -1.0, ALU.subtract, ALU.mult), r=["cstore", "cbc"], w=["nbias"])
            P.dve(lambda e: e.tensor_tensor(rq[0:NS, 0, :], cstore[0:NS, 8, :], cbc[0:NS, :], ALU.subtract), r=["cstore", "cbc"], w=["rq"])
            P.pe(lambda e: e.transpose(ps[6][0:8, 0:NS], rq[0:NS, 0, :], cstf[0:NS, 0, 0:NS]), r=["rq", "cstf"], w=["ps6"])
            P.dve(lambda e: e.tensor_copy(rT[:, 0:NS], ps[6][0:8, 0:NS]), r=["ps6"], w=["rT"])

            def s_evac_k(dst3, koff):
                def f(j, pt, pn):
                    P.act(lambda e: e.activation(kst[:, j, 0:NS], pt[0:64, 0:NS], AF.Copy), r=[pn], w=["kst"])
                    if j == 7:
                        P.dma("pool", dst3[:, :, koff:koff + NS].rearrange("h d k -> d h k"), kst[:, :, 0:NS], r=["kst"], w=[shist])
                return f

            def s_evac_q(j, pt, pn):
                P.act(lambda e: e.activation(Q[0:64, j, 0:NS], pt[0:64, 0:NS], AF.Copy, scale=SCALE), r=[pn], w=["Q"])

            def s_evac_z(zt, zn):
                def f(j, pt, pn):
                    P.act(lambda e: e.activation(zt[:, j, 0:NS], pt[0:64, 0:NS], AF.Silu), r=[pn], w=[zn])
                return f

            fm_unit(l, "ka", NS, s_evac_k(SS_KA[l], PAST))
            fm_unit(l, "za", NS, s_evac_z(zg["a"], "zga"))
            fm_unit(l, "qa", NS, s_evac_q)
            for h in range(8):
                P.dma("sp", Q[64:65, h, 0:NS], rT[h:h + 1, 0:NS], r=["rT"], w=["Q"])
            for half in range(2):
                at = Attn()
                for sbk in range(3):
                    nk = 512 if sbk < 2 else NS
                    bi = load_kv(SS_KA[l], SS_VA[l], half * 4, 4, sbk * 512, nk, shist)
                    for kb in range((nk + 127) // 128):
                        n = min(128, nk - kb * 128)
                        adds = [(0, NS, identb[0:NS, 0:NS], ma0b[0:NS, 0:NS], ["identb", "ma0b"])] if sbk == 2 else []
                        for i in range(4):
                            hh = half * 4 + i
                            at.tile(dict(kT=kbuf[bi][0:65, i, kb * 128:kb * 128 + n], v=vbuf[bi][0:n, kb, i, 0:65], n=n, qlo=0, qhi=NS,
                                         qap=Q[0:65, hh, 0:NS], adds=adds, bias=nbias[0:n, 4 * sbk + kb, hh:hh + 1],
                                         names=[f"kbuf{bi}", f"vbuf{bi}"], O=ps[4 + i], oname=psn[4 + i],
                                         first=(sbk == 0 and kb == 0), last=(sbk == 2)))
                at.flush()
                for i in range(4):
                    finish_head(ps[4 + i], psn[4 + i], zg["a"], "zga", half * 4 + i, NS, slice(0, NS))
            fm_unit(l, "kc", NS, s_evac_k(SS_KC[l], 512))
            fm_unit(l, "zc", NS, s_evac_z(zg["c"], "zgc"))
            fm_unit(l, "qc", NS, s_evac_q)
            for half in range(2):
                at = Attn()
                for sbk in range(2):
                    nk = 512 if sbk < 1 else NS
                    bi = load_kv(SS_KC[l], SS_VC[l], half * 4, 4, sbk * 512, nk, shist)
                    for kb in range((nk + 127) // 128):
                        n = min(128, nk - kb * 128)
                        for i in range(4):
                            hh = half * 4 + i
                            adds = []
                            if sbk == 0 and kb == 3:
                                adds = [(0, NS, identb[:], bc[:, 1, hh, 0:NS], ["identb", "btile"])]
                            if sbk == 1:
                                adds = [(0, NS, identb[0:NS, 0:NS], bc[0:NS, 0, hh, 0:NS], ["identb", "btile"])]
                            at.tile(dict(kT=kbuf[bi][0:64, i, kb * 128:kb * 128 + n], v=vbuf[bi][0:n, kb, i, 0:65], n=n, qlo=0, qhi=NS,
                                         qap=Q[0:64, hh, 0:NS], adds=adds, bias=None,
                                         names=[f"kbuf{bi}", f"vbuf{bi}"], O=ps[4 + i], oname=psn[4 + i],
                                         first=(sbk == 0 and kb == 0), last=(sbk == 1)))
                at.flush()
                for i in range(4):
                    finish_head(ps[4 + i], psn[4 + i], zg["c"], "zgc", half * 4 + i, NS, slice(0, NS))
            def s_evac_bx(j, pt, pn):
                if j < 2:
                    P.act(lambda e: e.activation(kst[:, j, 0:NS], pt[0:64, 0:NS], AF.Copy), r=[pn], w=["kst"])
                    if j == 1:
                        P.dma("pool", SS_KB[l][:, :, PAST:PAST + NS].rearrange("h d k -> d h k"), kst[:, 0:2, 0:NS], r=["kst"], w=[shist])
                elif j < 6:
                    P.act(lambda e: e.activation(iqT[:, j - 2, 0:NS], pt[0:64, 0:NS], AF.Copy), r=[pn], w=["iqT"])
                elif j == 6:
                    P.act(lambda e: e.activation(kst[:, 2, 0:NS], pt[0:64, 0:NS], AF.Copy), r=[pn], w=["kst"])
                    P.dma("pool", SS_IK[l][:, PAST:PAST + NS], kst[:, 2, 0:NS], r=["kst"], w=[shist])
            fm_unit(l, "bx", NS, s_evac_bx)
            fm_unit(l, "zb", NS, s_evac_z(zg["b"], "zgb"))
            fm_unit(l, "qb", NS, s_evac_q)
            NK = PAST + NS
            for k0 in range(0, NK, 512):
                nk = min(512, NK - k0)
                ii = nxt("ik", 2)
                P.dma("sp", ikbuf[ii][:, 0:nk], SS_IK[l][:, k0:k0 + nk], r=[shist], w=[f"ikbuf{ii}"])
                for h in range(8):
                    base = 32 * (h % 2)
                    pi_ = 2 + nxt("ips", 2)
                    P.pe(lambda e: e.matmul(ps[pi_][0:NS, 0:nk], iqT[base:base + 32, h // 2, 0:NS], ikbuf[ii][base:base + 32, 0:nk], start=True, stop=True),
                         r=["iqT", f"ikbuf{ii}"], w=[psn[pi_]])
                    ri = nxt("rr", 2)
                    P.act(lambda e: e.activation(Rr[ri][0:NS, 0:nk], ps[pi_][0:NS, 0:nk], AF.Relu, scale=wabs[0:NS, 0, h:h + 1]), r=[psn[pi_], "wabs"], w=[f"Rr{ri}"])
                    if h == 0:
                        P.dve(lambda e: e.tensor_scalar(SC[0:NS, k0:k0 + nk], Rr[ri][0:NS, 0:nk], wsgn[0:NS, 0, 0:1], None, ALU.mult), r=[f"Rr{ri}", "wsgn"], w=["SC"])
                    else:
                        P.dve(lambda e: e.scalar_tensor_tensor(SC[0:NS, k0:k0 + nk], Rr[ri][0:NS, 0:nk], wsgn[0:NS, 0, h:h + 1], SC[0:NS, k0:k0 + nk], ALU.mult, ALU.add),
                              r=[f"Rr{ri}", "wsgn", "SC"], w=["SC"])
            P.dve(lambda e: e.memset(cntb[:], 0.0), w=["cntb"])
            P.dve(lambda e: e.memset(small[:, 8:9], 0.0), w=["cand"])
            for it in range(NBIS):
                stp = 64.0 * (0.5 ** it)
                P.dve(lambda e: e.tensor_scalar(junk[0:NS, 0:NK], SC[0:NS, 0:NK], small[0:NS, 8:9], 0.0, ALU.is_ge, ALU.add, accum_out=cntb[0:NS, it:it + 1]),
                      r=["SC", "cand", "cntb"], w=["junk", "cntb"])
                a, b_ = (stp, -0.5 * stp) if it < NBIS - 1 else (stp, -stp)
                P.dve(lambda e: e.tensor_scalar(small[0:NS, 9:10], cntb[0:NS, it:it + 1], float(TOPK), a, ALU.is_ge, ALU.mult), r=["cntb"], w=["fl"])
                P.dve(lambda e: e.scalar_tensor_tensor(small[0:NS, 8:9], small[0:NS, 9:10], b_, small[0:NS, 8:9], ALU.add, ALU.add), r=["fl", "cand"], w=["cand"])
            at = Attn()
            for sbk in range(3):
                k0 = sbk * 512
                nk = min(512, NK - k0)
                mi = nxt("mb", 2)
                P.dve(lambda e: e.tensor_scalar(Mb[mi][0:NS, 0:nk], SC[0:NS, k0:k0 + nk], small[0:NS, 8:9], NEG, ALU.is_lt, ALU.mult), r=["SC", "cand"], w=[f"Mb{mi}"])
                bi = load_kv(SS_KB[l], SS_VB[l], 0, 2, k0, nk, shist)
                for kb in range((nk + 127) // 128):
                    n = min(128, nk - kb * 128)
                    gkb = sbk * 4 + kb
                    for j in range(2):
                        adds = [(0, 4 * NS, Mb[mi][0:NS, kb * 128:kb * 128 + n], i4b[0:NS, :, 0:NS], [f"Mb{mi}", "i4b"])]
                        if gkb >= 7:
                            for hq in range(4):
                                if gkb == 7:
                                    adds.append((hq * NS, (hq + 1) * NS, identb[:], b5[:, 1, 4 * j + hq, 0:NS], ["identb", "btile"]))
                                else:
                                    adds.append((hq * NS, (hq + 1) * NS, identb[0:NS, 0:NS], b5[0:NS, 0, 4 * j + hq, 0:NS], ["identb", "btile"]))
                        at.tile(dict(kT=kbuf[bi][0:64, j, kb * 128:kb * 128 + n], v=vbuf[bi][0:n, kb, j, 0:65], n=n, qlo=0, qhi=4 * NS,
                                     qap=Q[0:64, 4 * j:4 * j + 4, 0:NS], adds=adds, bias=None,
                                     names=[f"kbuf{bi}", f"vbuf{bi}"], O=ps[4 + j], oname=psn[4 + j],
                                     first=(gkb == 0), last=(gkb == 8)))
            at.flush()
            for j in range(2):
                finish_head(ps[4 + j], psn[4 + j], zg["b"], "zgb", (4 * j, 4 * j + 4), 4 * NS, slice(0, NS))
            allz = [zg["a"], zg["b"], zg["c"]]
            alln = ["zga", "zgb", "zgc"]
            for u in range(6):
                Wo_, won = load_wo(l, u)
                for hq in range(4):
                    hidx = u * 4 + hq
                    zt, zn = allz[hidx // 8], alln[hidx // 8]
                    for n_ in range(2):
                        P.pe(lambda e: e.matmul(ps[n_][0:NS, :], zt[:, hidx % 8, 0:NS], Wo_[:, hq, n_ * 512:(n_ + 1) * 512], start=(hidx == 0), stop=(hidx == 23)),
                             r=[zn, won], w=[psn[n_]])
            xi = nxt("x", 2)
            X = xt[xi]; xname = f"xt{xi}"
            P.dma("sp", X[0:NS, :], (xs if l == 0 else hs1)[:, :], r=(["hs1"] if l > 0 else []), w=[xname])
            for n_ in range(2):
                P.dve(lambda e: e.tensor_tensor(X[0:NS, n_ * 512:(n_ + 1) * 512], X[0:NS, n_ * 512:(n_ + 1) * 512], ps[n_][0:NS, :], ALU.add), r=[xname, psn[n_]], w=[xname])
            if l < nlayers - 1:
                P.dma("pool", hs1[:, :], X[0:NS, :], r=[xname], w=["hs1"])
            else:
                P.act(lambda e: e.activation(junk[0:NS, 0:1024], X[0:NS, :], AF.Square, accum_out=small[0:NS, 16:17]), r=[xname], w=["junk", "fs0"])
                P.dve(lambda e: e.tensor_scalar(small[0:NS, 17:18], small[0:NS, 16:17], 1.0 / D_MODEL, EPS, ALU.mult, ALU.add), r=["fs0"], w=["fs1"])
                P.act(lambda e: e.activation(small[0:NS, 18:19], small[0:NS, 17:18], AF.Sqrt), r=["fs1"], w=["fs2"])
                P.dve(lambda e: e.reciprocal(small[0:NS, 19:20], small[0:NS, 18:19]), r=["fs2"], w=["fs3"])
                P.dve(lambda e: e.scalar_tensor_tensor(X[0:NS, :], X[0:NS, :], small[0:NS, 19:20], fgb[0:NS, :], ALU.mult, ALU.mult), r=[xname, "fs3", "fgb"], w=[xname])
                P.dma("pool", y_s[:, :], X[0:NS, :], r=[xname])
        for fx in deferred_exchange:
            fx()
    P.finalize_and_emit()
    return nc, es, P


def _t5_bucket_np(rel):
    nb = 16
    max_exact = 8
    ret = np.where(rel > 0, nb, 0)
    n = np.abs(rel)
    nf = np.maximum(n, 1).astype(np.float32)
    large = max_exact + (np.log(nf / max_exact) / math.log(128 / max_exact) * (nb - max_exact)).astype(np.int32)
    large = np.minimum(large, nb - 1)
    return ret + np.where(n < max_exact, n, large)


def _constants():
    p = np.arange(128)[:, None]
    f = np.arange(128)[None, :]
    cst = np.zeros((128, 8, 128), np.float32)
    cst[:, 0] = (p == f)
    cst[:, 1] = (p + f == 127)
    cst[:, 2] = (p <= f)
    cst[0, 3, :] = 1.0
    cst[:, 4] = np.where(p > f, NEG, 0.0)
    cst[:, 5] = np.where((p >= 64) & (f < 64), NEG, 0.0)
    cst[:, 6] = np.where((p < 64) & (f >= 64), NEG, 0.0)
    cst[:, 7] = np.where((p < 64) & (f >= 64), -1e30, 0.0)
    rel = 127 - np.arange(LTAB)
    bk = _t5_bucket_np(rel.astype(np.int32))
    oh5 = np.zeros((32, LTAB), np.float32)
    oh5[bk, np.arange(LTAB)] += 1.0
    far = int(_t5_bucket_np(np.array([-100000], np.int32))[0])
    oh5[far, :] -= 1.0
    idx = np.clip(rel, -128, 128) + 128
    ohc = np.zeros((3 * 128, LTAB), np.float32)
    ohc[idx, np.arange(LTAB)] += 1.0
    ohc[0, :] -= 1.0
    return cst.reshape(128, 8 * 128), oh5, ohc.reshape(3, 128, LTAB)


_PROG = {}


def _get_prog(key=(True, 4, DEPTH)):
    if key not in _PROG:
        _PROG[key] = build_program(*key)
    return _PROG[key]


def _host_inputs(x_prompt, norm_g, w_in, b_f, t5_bias, c_rel_bias, w_out, final_g):
    cst, oh5, ohc = _constants()
    wus = np.zeros((DEPTH, NU, 128, 8, 512), np.float32)
    for l in range(DEPTH):
        for u, name in enumerate(UNITS):
            cols = np.array(_unit_cols(name))
            m = cols >= 0
            w = np.zeros((D_MODEL, 512), np.float32)
            w[:, m] = w_in[l][:, cols[m]]
            wus[l, u] = w.reshape(8, 128, 512).transpose(1, 0, 2)
    wos = np.ascontiguousarray(w_out.reshape(DEPTH, 6, 4, 64, D_MODEL).transpose(0, 1, 3, 2, 4))
    gcol = np.ascontiguousarray(norm_g.reshape(DEPTH, 8, 128).transpose(2, 0, 1).reshape(128, DEPTH * 8))
    crel = np.zeros((DEPTH, 384, 8), np.float32)
    crel[:, :257] = c_rel_bias
    common = dict(wu=wus, wo=wos, gcol=gcol, fg=np.ascontiguousarray(final_g.reshape(1, D_MODEL)),
                  bfb=np.ascontiguousarray(b_f.reshape(1, DEPTH * 8)), t5=np.ascontiguousarray(t5_bias),
                  crel=crel.reshape(DEPTH, 3, 128, 8), oh5=oh5, ohc=ohc, cst=cst)
    return common


def _percore(j):
    pc = np.zeros((128, 1024), np.float32)
    p = np.arange(128)
    pc[:, 0:512] = (j * 512 + np.arange(512))[None, :]
    for rr in range(16):
        pc[:, 512 + rr] = rr * 128 + p
    for qb in range(4):
        for ch in range(4):
            pc[:, 528 + qb * 4 + ch] = (8 * j + 2 * qb + (p >= 64) + 1) * 64 - ch * 512
        for rr in range(17):
            pc[:, 544 + (qb * 17 + rr) * 2] = 1.0 if (rr - 1) == 4 * j + qb else 0.0
            pc[:, 544 + (qb * 17 + rr) * 2 + 1] = 1.0 if (rr - 1) == 4 * j + qb - 1 else 0.0
    for r in range(4):
        pc[:, 680 + r] = 1.0 if j == r else 0.0
    pc[:, 684] = -30000.0 if j == 0 else 0.0
    return pc


def _in_maps(x_prompt, x_sample, cache_a_k, cache_a_v, cache_a_logf, cache_b_k, cache_b_v, cache_b_idx_k,
             cache_c_k, cache_c_v, norm_g, w_in, b_f, t5_bias, c_rel_bias, w_out, final_g):
    f = lambda a: np.ascontiguousarray(np.asarray(a, dtype=np.float32))
    x_prompt, norm_g, w_in, b_f, t5_bias, c_rel_bias, w_out, final_g = map(f, (x_prompt, norm_g, w_in, b_f, t5_bias, c_rel_bias, w_out, final_g))
    common = _host_inputs(x_prompt, norm_g, w_in, b_f, t5_bias, c_rel_bias, w_out, final_g)
    common["iota5"] = np.ascontiguousarray(np.broadcast_to(np.arange(512, dtype=np.float32)[None, :], (128, 512)))
    in_maps = []
    for c in range(8):
        b, j = c // 4, c % 4
        m = dict(common)
        xb = x_prompt[b].reshape(NG, 512, D_MODEL)
        m["xp"] = x_prompt[b]
        m["xq"] = np.ascontiguousarray(xb[j::4])
        xpv = np.zeros((4, 512, D_MODEL), np.float32)
        for mm in range(4):
            if 4 * mm + j - 1 >= 0:
                xpv[mm] = xb[4 * mm + j - 1]
        m["xprev"] = xpv
        m["pcore"] = _percore(j)
        m["xs"] = f(x_sample[c])
        m["ca_k"] = f(cache_a_k[:, c]).reshape(DEPTH, PAST, 512); m["ca_v"] = f(cache_a_v[:, c]).reshape(DEPTH, PAST, 512)
        m["ca_lf"] = f(cache_a_logf[:, c]).reshape(DEPTH, PAST, 8)
        m["cb_k"] = f(cache_b_k[:, c]).reshape(DEPTH, PAST, 128); m["cb_v"] = f(cache_b_v[:, c]).reshape(DEPTH, PAST, 128)
        m["cb_ik"] = f(cache_b_idx_k[:, c]).reshape(DEPTH, PAST, 32)
        m["cc_k"] = f(cache_c_k[:, c]).reshape(DEPTH, 512, 512); m["cc_v"] = f(cache_c_v[:, c]).reshape(DEPTH, 512, 512)
        in_maps.append(m)
    return in_maps


def kernel(x_prompt, x_sample, cache_a_k, cache_a_v, cache_a_logf, cache_b_k, cache_b_v, cache_b_idx_k,
           cache_c_k, cache_c_v, norm_g, w_in, b_f, t5_bias, c_rel_bias, w_out, final_g):
    nc, es, P = _get_prog()
    in_maps = _in_maps(x_prompt, x_sample, cache_a_k, cache_a_v, cache_a_logf, cache_b_k, cache_b_v, cache_b_idx_k,
                       cache_c_k, cache_c_v, norm_g, w_in, b_f, t5_bias, c_rel_bias, w_out, final_g)
    res = run_bass_kernel_spmd(nc, in_maps, core_ids=list(range(8)))
    R = res.results
    st = lambda name, shp: np.stack([R[4 * b][name] for b in range(2)], axis=1).reshape(shp)
    y_prompt = np.zeros((BATCH, NG, 512, D_MODEL), np.float32)
    for c in range(8):
        b, j = c // 4, c % 4
        y_prompt[b, j::4] = R[c]["y_q"].reshape(4, 512, D_MODEL)
    y_prompt = y_prompt.reshape(BATCH, SEQ, D_MODEL)
    ss = lambda name, shp: np.stack([R[b][name] for b in range(DEC_BATCH)], axis=1).reshape(shp)
    y_sample = np.stack([R[b]["y_s"] for b in range(DEC_BATCH)], axis=0)
    outs = [y_prompt, y_sample,
            st("o_ak", (DEPTH, BATCH, SEQ, H, HD)), st("o_av", (DEPTH, BATCH, SEQ, H, HD)), st("o_lf", (DEPTH, BATCH, SEQ, H)),
            st("o_bk", (DEPTH, BATCH, SEQ, KVB, HD)), st("o_bv", (DEPTH, BATCH, SEQ, KVB, HD)), st("o_ik", (DEPTH, BATCH, SEQ, IDX_D)),
            st("o_ck", (DEPTH, BATCH, 512, H, HD)), st("o_cv", (DEPTH, BATCH, 512, H, HD)),
            ss("s_ak", (DEPTH, DEC_BATCH, DEC_SEQ, H, HD)), ss("s_av", (DEPTH, DEC_BATCH, DEC_SEQ, H, HD)), ss("s_lf", (DEPTH, DEC_BATCH, DEC_SEQ, H)),
            ss("s_bk", (DEPTH, DEC_BATCH, DEC_SEQ, KVB, HD)), ss("s_bv", (DEPTH, DEC_BATCH, DEC_SEQ, KVB, HD)), ss("s_ik", (DEPTH, DEC_BATCH, DEC_SEQ, IDX_D)),
            ss("s_ck", (DEPTH, DEC_BATCH, DEC_SEQ, H, HD)), ss("s_cv", (DEPTH, DEC_BATCH, DEC_SEQ, H, HD))]
    return tuple(outs)
```

```python
import math
import types
import numpy as np
from contextlib import ExitStack
import concourse.bass as bass
import concourse.mybir as mybir
from concourse.bass_utils import run_bass_kernel_spmd

F32 = mybir.dt.float32
BF16 = mybir.dt.bfloat16
ALU = mybir.AluOpType
AF = mybir.ActivationFunctionType

D_MODEL = 1024; BATCH = 2; SEQ = 8192; DEPTH = 2; DEC_BATCH = 8; DEC_SEQ = 16; PAST = 1024
HD = 64; H = 8; KVB = 2; IDX_H = 8; IDX_D = 32; TOPK = 256
SCALE = HD ** -0.5
IDXS = (IDX_D ** -0.5) * (IDX_H ** -0.5)
EPS = 1e-6
NEG = -32768.0
NG = SEQ // 512
NBIS = 24
LTAB = 384

_SPLIT = (512, 512, 512, 512, 8, 512, 128, 128, 512, 256, 8, 32, 512, 512, 512, 512)
_OFF = np.concatenate([[0], np.cumsum(_SPLIT)])
(QA, KA, VA, ZA, FA, QB, KB, VB, ZB, IQ, IW, IK, QC, KC, VC, ZC) = [int(o) for o in _OFF[:-1]]
FM_UNITS = ["qa", "ka", "za", "qb", "zb", "bx", "qc", "kc", "zc"]
TM_UNITS = ["tka", "tva", "tb", "tkc", "tvc"]
UNITS = FM_UNITS + TM_UNITS
NU = len(UNITS)


def _unit_cols(name):
    r = lambda a, n: list(range(a, a + n))
    pad = lambda l: l + [-1] * (512 - len(l))
    if name == "qa": return r(QA, 512)
    if name == "ka": return r(KA, 512)
    if name == "za": return r(ZA, 512)
    if name == "qb": return r(QB, 512)
    if name == "zb": return r(ZB, 512)
    if name == "qc": return r(QC, 512)
    if name == "kc": return r(KC, 512)
    if name == "zc": return r(ZC, 512)
    if name == "bx": return pad(r(KB, 128) + r(IQ, 256) + r(IK, 32) + r(IK, 32))
    if name == "tka": return r(KA, 512)
    if name == "tva": return r(VA, 512)
    if name == "tkc": return r(KC, 512)
    if name == "tvc": return r(VC, 512)
    if name == "tb": return pad(r(KB, 128) + r(VB, 128) + r(IK, 32) + r(FA, 8) + r(IW, 8))
    raise KeyError(name)


class Prog:
    STREAMS = ("pe", "act", "dve", "pool", "sp")

    def __init__(self, nc):
        self.nc = nc
        self.ops = []
        self.ndma = {}

    @staticmethod
    def _freeze(fn):
        if fn.__closure__ is None:
            return fn
        cells = []
        for c in fn.__closure__:
            try:
                cells.append(types.CellType(c.cell_contents))
            except ValueError:
                cells.append(c)
        return types.FunctionType(fn.__code__, fn.__globals__, fn.__name__, fn.__defaults__, tuple(cells))

    NSUB = 16

    def add(self, stream, fn, r=(), w=(), dma=False, cc=False):
        fn = self._freeze(fn)
        if cc:
            track = "dma_cc"
        elif dma:
            k = self.ndma.get(stream, 0)
            self.ndma[stream] = k + 1
            track = f"dma_{stream}#{k % self.NSUB}"
        else:
            track = stream
        self.ops.append((stream, track, fn, tuple(r), tuple(w)))

    def pe(self, fn, r=(), w=()): self.add("pe", fn, r, w)
    def act(self, fn, r=(), w=()): self.add("act", fn, r, w)
    def dve(self, fn, r=(), w=()): self.add("dve", fn, r, w)
    def pool(self, fn, r=(), w=()): self.add("pool", fn, r, w)

    def dma(self, stream, out, in_, r=(), w=(), **kw):
        self.add(stream, lambda e: e.dma_start(out=out, in_=in_, **kw), r, w, dma=True)

    def finalize_and_emit(self):
        nc = self.nc
        ops = self.ops
        n = len(ops)
        writers = {}
        readers = {}
        prev_on = {}
        deps = [None] * n
        signal = [False] * n
        qof = lambda t: t.split("#")[0]
        for i, (stream, track, fn, R, W) in enumerate(ops):
            d = set()
            is_dma = track.startswith("dma_")
            if is_dma:
                j = prev_on.get(track)
                if j is not None:
                    d.add(j)
                prev_on[track] = i
            for res in R:
                for tj, j in writers.get(res, {}).items():
                    if tj != track or is_dma or track != "pe":
                        d.add(j)
            for res in W:
                lazy = res.startswith("~")
                for tj, j in writers.get(res, {}).items():
                    if lazy:
                        if qof(tj) != qof(track):
                            d.add(j)
                    elif tj != track or is_dma:
                        d.add(j)
                for tj, j in readers.get(res, {}).items():
                    if lazy:
                        if qof(tj) != qof(track):
                            d.add(j)
                    elif tj != track or is_dma:
                        d.add(j)
            for res in R:
                readers.setdefault(res, {})[track] = i
            for res in W:
                if res.startswith("~"):
                    writers.setdefault(res, {})[track] = i
                else:
                    writers[res] = {track: i}
                    readers[res] = {}
            d.discard(i)
            deps[i] = d
            for j in d:
                signal[j] = True
        tracks = sorted({o[1] for o in ops})
        cnt = {t: 0 for t in tracks}
        val = [0] * n
        for i, (stream, track, fn, R, W) in enumerate(ops):
            if track == "dma_cc":
                cnt[track] += 1
                val[i] = cnt[track]
            elif track.startswith("dma_"):
                cnt[track] += 16
                val[i] = cnt[track]
            elif signal[i]:
                cnt[track] += 1
                val[i] = cnt[track]
        known = {s: {t: 0 for t in tracks} for s in self.STREAMS}
        waits = [None] * n
        for i, (stream, track, fn, R, W) in enumerate(ops):
            need = {}
            for j in deps[i]:
                tj = ops[j][1]
                need[tj] = max(need.get(tj, 0), val[j])
            wl = []
            for tj, v in need.items():
                if v > known[stream][tj]:
                    wl.append((tj, v))
                    known[stream][tj] = v
            waits[i] = wl
        self.stats = dict(cnt)
        by_stream = {s: [] for s in self.STREAMS}
        for i, o in enumerate(ops):
            by_stream[o[0]].append(i)
        with ExitStack() as es:
            sems = {t: es.enter_context(nc.semaphore("s_" + t.replace("#", "_"))) for t in tracks}
            block = es.enter_context(nc.Block())

            def run(eng, stream):
                for i in by_stream[stream]:
                    _, track, fn, R, W = ops[i]
                    for tj, v in waits[i]:
                        eng.wait_ge(sems[tj], v)
                    inst = fn(eng)
                    if track == "dma_cc":
                        inst.then_inc(sems[track], 1)
                    elif track.startswith("dma_"):
                        inst.then_inc(sems[track], 16)
                    elif signal[i]:
                        inst.then_inc(sems[track], 1)
                if stream == "sp":
                    for t in tracks:
                        if t.startswith("dma_") and cnt[t] > known[stream][t]:
                            eng.wait_ge(sems[t], cnt[t])

            @block.tensor
            def _(e): run(e, "pe")

            @block.scalar
            def _(e): run(e, "act")

            @block.vector
            def _(e): run(e, "dve")

            @block.gpsimd
            def _(e): run(e, "pool")

            @block.sync
            def _(e): run(e, "sp")


def build_program(do_sample=True, nm=4, nlayers=DEPTH):
    nc = bass.Bass("TRN2", target_bir_lowering=False)
    es = ExitStack()
    din = lambda name, shape, dt=F32: nc.dram_tensor(name, list(shape), dt, kind="ExternalInput").ap()
    dout = lambda name, shape: nc.dram_tensor(name, list(shape), F32, kind="ExternalOutput").ap()
    dscr = lambda name, shape, dt=BF16: nc.dram_tensor(name, list(shape), dt, kind="Internal").ap()
    xp = din("xp", [SEQ, D_MODEL])
    wu = din("wu", [DEPTH, NU, 128, 8, 512])
    wo = din("wo", [DEPTH, 6, 64, 4, 1024])
    gcol_d = din("gcol", [128, DEPTH * 8])
    fg_d = din("fg", [1, D_MODEL])
    bf_d = din("bfb", [1, DEPTH * 8])
    t5_d = din("t5", [32, 8])
    crel_d = din("crel", [DEPTH, 3, 128, 8])
    oh5_d = din("oh5", [32, LTAB])
    ohc_d = din("ohc", [3, 128, LTAB])
    cst_d = din("cst", [128, 8 * 128])
    y_q = dout("y_q", [2048, D_MODEL])
    xq = din("xq", [4, 512, D_MODEL]); xprev = din("xprev", [4, 512, D_MODEL])
    pc_d = din("pcore", [128, 1024])
    iota_d = din("iota5", [128, 512])
    o_ak = dout("o_ak", [DEPTH, SEQ, 512]); o_av = dout("o_av", [DEPTH, SEQ, 512])
    o_lf = dout("o_lf", [DEPTH, SEQ, 8])
    o_bk = dout("o_bk", [DEPTH, SEQ, 128]); o_bv = dout("o_bv", [DEPTH, SEQ, 128])
    o_ik = dout("o_ik", [DEPTH, SEQ, 32])
    o_ck = dout("o_ck", [DEPTH, 512, 512]); o_cv = dout("o_cv", [DEPTH, 512, 512])
    wub = dscr("wub", [DEPTH, NU, 128, 8, 512])
    wob = dscr("wob", [DEPTH, 6, 64, 4, 1024])
    hp1q = dscr("hp1q", [2048, D_MODEL], F32)
    hpg = dscr("hpg", [SEQ, D_MODEL], F32)
    ccs = dscr("ccs", [256, D_MODEL], F32)
    COMBS = dscr("combs", [4, 17, 2, 128, 512])
    AMASK = dscr("amask", [16, 128, 512])
    ccd = dscr("ccd", [1024, D_MODEL], F32)
    S_KCL = dscr("scr_kcl", [DEPTH, 8, 64, 1024]); S_VCL = dscr("scr_vcl", [DEPTH, 1024, 512])
    tab5 = dscr("tab5", [8, LTAB], F32)
    tabc = dscr("tabc", [DEPTH, 8, LTAB], F32)
    S_KA = dscr("scr_s_ka", [DEPTH, 8, 64, SEQ]); S_KC = dscr("scr_s_kc", [DEPTH, 8, 64, SEQ])
    S_KB = dscr("scr_s_kb", [DEPTH, 2, 64, SEQ]); S_IK = dscr("scr_s_ik", [DEPTH, 64, SEQ])
    S_VA = dscr("scr_s_va", [DEPTH, SEQ, 512]); S_VC = dscr("scr_s_vc", [DEPTH, SEQ, 512])
    S_VB = dscr("scr_s_vb", [DEPTH, SEQ, 128])

    xs = din("xs", [DEC_SEQ, D_MODEL])
    ca_k = din("ca_k", [DEPTH, PAST, 512]); ca_v = din("ca_v", [DEPTH, PAST, 512]); ca_lf = din("ca_lf", [DEPTH, PAST, 8])
    cb_k = din("cb_k", [DEPTH, PAST, 128]); cb_v = din("cb_v", [DEPTH, PAST, 128]); cb_ik = din("cb_ik", [DEPTH, PAST, 32])
    cc_k = din("cc_k", [DEPTH, 512, 512]); cc_v = din("cc_v", [DEPTH, 512, 512])
    y_s = dout("y_s", [DEC_SEQ, D_MODEL])
    s_ak = dout("s_ak", [DEPTH, DEC_SEQ, 512]); s_av = dout("s_av", [DEPTH, DEC_SEQ, 512]); s_lf = dout("s_lf", [DEPTH, DEC_SEQ, 8])
    s_bk = dout("s_bk", [DEPTH, DEC_SEQ, 128]); s_bv = dout("s_bv", [DEPTH, DEC_SEQ, 128]); s_ik = dout("s_ik", [DEPTH, DEC_SEQ, 32])
    s_ck = dout("s_ck", [DEPTH, DEC_SEQ, 512]); s_cv = dout("s_cv", [DEPTH, DEC_SEQ, 512])
    hs1 = dscr("hs1", [DEC_SEQ, D_MODEL], F32)
    MBS = [dscr(f"mbs{i}", [128, SEQ]) for i in range(4)]
    SS_KA = dscr("ss_ka", [DEPTH, 8, 64, 1152]); SS_KC = dscr("ss_kc", [DEPTH, 8, 64, 640])
    SS_KB = dscr("ss_kb", [DEPTH, 2, 64, 1152]); SS_IK = dscr("ss_ik", [DEPTH, 64, 1152])
    SS_VA = dscr("ss_va", [DEPTH, 1152, 512]); SS_VC = dscr("ss_vc", [DEPTH, 640, 512]); SS_VB = dscr("ss_vb", [DEPTH, 1152, 128])

    sb = lambda name, shape, dt: es.enter_context(nc.sbuf_tensor(name, list(shape), dt))
    wring = [sb(f"wring{i}", [128, 8, 512], BF16) for i in range(2)]
    SC = sb("SC", [128, 8192], F32)
    junk = sb("junk", [128, 8192], BF16)
    hT = sb("hT", [128, 8, 512], BF16)
    Q = sb("Q", [65, 8, 512], BF16)
    zg = {t: sb("zg" + t, [64, 8, 512], BF16) for t in "abc"}
    iqT = sb("iqT", [64, 4, 512], BF16)
    kst = sb("kst", [64, 8, 512], BF16)
    xt = [sb(f"xt{i}", [128, 1024], F32) for i in range(2)]
    xn = sb("xn", [128, 1024], BF16)
    st = [sb(f"st{i}", [128, 512], F32) for i in range(2)]
    vst = [sb(f"vst{i}", [128, 512], BF16) for i in range(2)]
    kbuf = [sb(f"kbuf{i}", [65, 4, 512], BF16) for i in range(2)]
    vbuf = [sb(f"vbuf{i}", [128, 4, 4, 65], BF16) for i in range(2)]
    Pt = [sb(f"Pt{i}", [128, 512], BF16) for i in range(4)]
    Mb = [sb(f"Mb{i}", [128, 512], BF16) for i in range(2)]
    Rr = [sb(f"Rr{i}", [128, 512], F32) for i in range(3)]
    ikbuf = [sb(f"ikbuf{i}", [64, 512], BF16) for i in range(2)]
    b5 = sb("b5", [128, 2, 8, 128], BF16)
    bc = sb("bc", [128, 2, 8, 128], BF16)
    cstf = sb("cstf", [128, 8, 128], F32)
    identb = sb("identb", [128, 128], BF16)
    i4b = sb("i4b", [128, 4, 128], BF16)
    ma0b = sb("ma0b", [128, 128], BF16); cm0b = sb("cm0b", [128, 128], BF16); cm4b = sb("cm4b", [128, 128], BF16)
    cstore = sb("cstore", [128, 64, 8], F32)
    nbias = sb("nbias", [128, 64, 8], F32)
    gcol = sb("gcol_s", [128, DEPTH * 8], F32)
    fgb = sb("fgb", [128, D_MODEL], F32)
    bfb = sb("bfb_s", [128, DEPTH * 8], F32)
    small = sb("small", [128, 64], F32)
    cntb = sb("cntb", [128, NBIS], F32)
    wabs = sb("wabs", [128, 4, 8], F32); wsgn = sb("wsgn", [128, 4, 8], F32)
    lfb = sb("lfb", [128, 8], F32)
    tot = sb("tot", [1, 8], F32)
    tots = sb("tots", [1, 17, 8], F32)
    totbc = sb("totbc", [128, 8], F32)
    lf4 = sb("lf4", [128, 4, 8], F32)
    cown = sb("cown", [128, 4, 8], F32)
    xacc = sb("xacc", [128, D_MODEL], F32)
    pcore = sb("pcore_s", [128, 1024], F32)
    iota5 = sb("iota5_s", [128, 512], F32)
    comb = [sb(f"comb{i}", [128, 4, 128], BF16) for i in range(2)]
    ones1 = sb("ones1", [65, 128], F32)
    cbc = sb("cbc", [128, 8], F32)
    rq = sb("rq", [128, 4, 8], F32)
    rT = sb("rT", [8, 512], BF16)
    rden = sb("rden", [65, 512], F32)
    otmp = sb("otmp", [64, 512], F32)
    hank = sb("hank", [128, 128], F32)
    t5s = sb("t5s", [32, 8], F32); oh5s = sb("oh5s", [32, LTAB], F32)
    crs = sb("crs", [128, 3, 8], F32); ohcs = sb("ohcs", [128, 3, LTAB], F32)
    tabs = sb("tabs", [8, LTAB], F32)
    ps = [es.enter_context(nc.psum_tensor(f"ps{i}", [128, 512], F32)) for i in range(8)]
    psn = [f"ps{i}" for i in range(8)]

    P = Prog(nc)
    _early = {}

    def nxt_early(key, n):
        v = _early.get(key, 0)
        _early[key] = v + 1
        return v % n

    IDENT = cstf[:, 0, :]; JM = cstf[:, 1, :]; TRI = cstf[:, 2, :]; E0ROW = cstf[:, 3, :]
    ADM = cstf[:, 7, :]
    E127 = cstf[:, 1, 0:1]

    P.dma("sp", pcore[:], pc_d, w=["qrelb", "krel", "qlimc", "sel01", "selb", "pvb"])
    P.dma("sp", iota5[:], iota_d, w=["iota5"])
    qrelb = pcore[:, 0:512]; krel = pcore[:, 512:528]; qlimc = pcore[:, 528:544]; sel01 = pcore[:, 544:680]
    selb = pcore[:, 680:684]; pvb = pcore[:, 684:685]
    P.dma("sp", cstf[:].rearrange("p a b -> p (a b)"), cst_d, w=["cstf"])
    P.dma("sp", gcol[:], gcol_d, w=["gcol"])
    P.dma("sp", fgb[:], fg_d.to_broadcast([128, D_MODEL]) if hasattr(fg_d, "to_broadcast") else bass.AP(fg_d.tensor, 0, [[0, 128], [1, D_MODEL]]), w=["fgb"])
    P.dma("sp", bfb[:], bass.AP(bf_d.tensor, 0, [[0, 128], [1, DEPTH * 8]]), w=["bfb"])
    P.dma("sp", t5s[:], t5_d, w=["t5s"])
    P.dma("sp", oh5s[:], oh5_d, w=["oh5s"])
    P.dma("sp", ohcs[:], ohc_d.rearrange("c p l -> p c l"), w=["ohcs"])
    P.dve(lambda e: e.tensor_copy(identb[:], IDENT), r=["cstf"], w=["identb"])
    for k in range(4):
        P.dve(lambda e, k=k: e.tensor_copy(i4b[:, k, :], IDENT), r=["cstf"], w=["i4b"])
    P.dve(lambda e: e.tensor_copy(ma0b[:], cstf[:, 4, :]), r=["cstf"], w=["ma0b"])
    P.dve(lambda e: e.tensor_copy(cm0b[:], cstf[:, 5, :]), r=["cstf"], w=["cm0b"])
    P.dve(lambda e: e.tensor_copy(cm4b[:], cstf[:, 6, :]), r=["cstf"], w=["cm4b"])
    onesf = cstf[:, 4, :]
    P.dve(lambda e: e.memset(onesf, 1.0), r=["ma0b"], w=["cstf", "onesf"])
    P.dve(lambda e: e.memset(ones1[:], 1.0), w=["ones1"])
    for i in range(2):
        P.pool(lambda e, i=i: e.memset(kbuf[i][:], 1.0), w=[f"kbuf{i}"])
        P.pool(lambda e, i=i: e.memset(vbuf[i][:], 1.0), w=[f"vbuf{i}"])

    stg = SC[:, 0:4096].rearrange("p (c n) -> p c n", c=8)
    stgb = junk[:, 0:4096].rearrange("p (c n) -> p c n", c=8)
    pend_w = []
    for l in range(nlayers):
        if l > 0:
            pend_w += [("wu", l, u, q) for u in range(NU) for q in range(4)]
            pend_w += [("wo", l, u, q) for u in range(6) for q in range(4)]
            continue
        for u in range(NU):
            P.dma("sp", stg, wu[l, u], w=["SC"])
            P.act(lambda e: e.activation(stgb, stg, AF.Copy), r=["SC"], w=["junk"])
            P.dma("sp", wub[l, u], stgb, r=["junk"], w=[f"wub{l}_{u}"])
        for u in range(6):
            so = SC[0:64, 0:4096].rearrange("p (c n) -> p c n", c=4)
            sob = junk[0:64, 0:4096].rearrange("p (c n) -> p c n", c=4)
            P.dma("sp", so, wo[l, u], w=["SC"])
            P.act(lambda e, so=so, sob=sob: e.activation(sob, so, AF.Copy), r=["SC"], w=["junk"])
            P.dma("sp", wob[l, u], sob, r=["junk"], w=[f"wob{l}_{u}"])

    def convert_pieces(n):
        for _ in range(min(n, len(pend_w))):
            kind, l1, u, q = pend_w.pop(0)
            xi = nxt("x", 2)
            X = xt[xi]; xname = f"xt{xi}"
            if kind == "wu":
                P.dma("sp", X[:].rearrange("p (c n) -> p c n", c=2), wu[l1, u][:, 2 * q:2 * q + 2, :], w=[xname])
                P.act(lambda e: e.activation(xn[:, :], X[:, :], AF.Copy), r=[xname], w=["xn"])
                P.dma("pool", wub[l1, u][:, 2 * q:2 * q + 2, :], xn[:].rearrange("p (c n) -> p c n", c=2), r=["xn"], w=[f"~wub{l1}_{u}"])
            else:
                P.dma("sp", X[0:64, :], wo[l1, u][:, q, :], w=[xname])
                P.act(lambda e: e.activation(xn[0:64, :], X[0:64, :], AF.Copy), r=[xname], w=["xn"])
                P.dma("pool", wob[l1, u][:, q, :], xn[0:64, :], r=["xn"], w=[f"~wob{l1}_{u}"])

    def build_tab(lhs_list, rhs_list, dst, rnames):
        for i, (a, b) in enumerate(zip(lhs_list, rhs_list)):
            P.pe(lambda e, a=a, b=b, i=i: e.matmul(ps[0][0:8, 0:LTAB], a, b, start=(i == 0), stop=(i == len(lhs_list) - 1)),
                 r=rnames, w=["ps0"])
        P.dve(lambda e: e.tensor_copy(tabs[:], ps[0][0:8, 0:LTAB]), r=["ps0"], w=["tabs"])
        P.dma("sp", dst, tabs[:], r=["tabs"], w=["tabdram"])

    def build_toeplitz(tab_ap2d, dst_tile):
        for k in range(2):
            for h in range(8):
                b0 = 128 * k
                src = bass.AP(tab_ap2d.tensor, tab_ap2d.offset + h * LTAB + b0, [[1, 128], [1, 128]])
                P.dma("sp", hank[:], src, r=["tabdram"], w=["hank"])
                P.pe(lambda e: e.matmul(ps[1][:, 0:128], JM, hank[:], start=True, stop=True), r=["hank", "cstf"], w=["ps1"])
                P.dve(lambda e, k=k, h=h: e.tensor_copy(dst_tile[:, k, h, :], ps[1][:, 0:128]), r=["ps1"], w=["btile"])

    build_tab([t5s[:]], [oh5s[:]], tab5, ["t5s", "oh5s"])
    build_toeplitz(tab5, b5)
    for qb in range(4):
        for rr in range(17):
            if (rr - 1 - qb) % 4 not in (0, 3):
                continue
            for jj in range(2):
                ci = nxt_early("cmb", 2)
                sc0 = sel01[:, (qb * 17 + rr) * 2:(qb * 17 + rr) * 2 + 1]
                sc1 = sel01[:, (qb * 17 + rr) * 2 + 1:(qb * 17 + rr) * 2 + 2]
                P.dve(lambda e: e.tensor_scalar(comb[ci][:, :, :], b5[:, 0, 4 * jj:4 * jj + 4, :], sc0, None, ALU.mult), r=["btile", "sel01"], w=[f"comb{ci}"])
                P.dve(lambda e: e.scalar_tensor_tensor(comb[ci][:, :, :], b5[:, 1, 4 * jj:4 * jj + 4, :], sc1, comb[ci][:, :, :], ALU.mult, ALU.add),
                      r=["btile", "sel01", f"comb{ci}"], w=[f"comb{ci}"])
                P.dma("pool", COMBS[qb, rr, jj], comb[ci][:].rearrange("p a b -> p (a b)"), r=[f"comb{ci}"], w=["~combs"])
    for rr in range(16):
        mi = nxt_early("mb", 2)
        P.dve(lambda e: e.tensor_scalar(Mb[mi][:, :], qrelb[:, :], krel[:, rr:rr + 1], NEG, ALU.is_lt, ALU.mult), r=["qrelb", "krel"], w=[f"Mb{mi}"])
        P.dma("pool", AMASK[rr], Mb[mi][:, :], r=[f"Mb{mi}"], w=["~amask"])

    wk = [0]

    def load_w(l, u):
        i = wk[0] % 2
        wk[0] += 1
        P.dma("sp", wring[i][:], wub[l, u], r=[f"wub{l}_{u}", f"~wub{l}_{u}"], w=[f"wring{i}"])
        return wring[i], f"wring{i}"

    def load_wo(l, u):
        i = wk[0] % 2
        wk[0] += 1
        dst = wring[i][0:64].rearrange("p c n -> p (c n)").rearrange("p (c n) -> p c n", c=4)
        P.dma("sp", dst, wob[l, u], r=[f"wob{l}_{u}", f"~wob{l}_{u}"], w=[f"wring{i}"])
        return dst, f"wring{i}"

    rot = {"s": 0, "pt": 0, "kv": 0, "st": 0, "x": 0, "ik": 0, "rr": 0, "mb": 0, "ips": 0, "cmb": 0}

    def nxt(key, n):
        v = rot[key] % n
        rot[key] += 1
        return v

    def norm_block(l, xsrc_ap, tb, nrow=128, rname=None, sb_src=None):
        if sb_src is not None:
            X, xname = sb_src
        else:
            xi = nxt("x", 2)
            X = xt[xi]; xname = f"xt{xi}"
        if sb_src is None:
            P.dma("sp", X[0:nrow, :], xsrc_ap, r=(list(rname) if isinstance(rname, (list, tuple)) else ([rname] if rname else [])), w=[xname])
        P.act(lambda e: e.activation(junk[0:nrow, 0:1024], X[0:nrow, :], AF.Square, accum_out=small[0:nrow, 0:1]),
              r=[xname], w=["junk", "small0"])
        P.dve(lambda e: e.tensor_scalar(small[0:nrow, 1:2], small[0:nrow, 0:1], 1.0 / D_MODEL, EPS, ALU.mult, ALU.add), r=["small0"], w=["small1"])
        P.act(lambda e: e.activation(small[0:nrow, 2:3], small[0:nrow, 1:2], AF.Sqrt), r=["small1"], w=["small2"])
        P.dve(lambda e: e.reciprocal(small[0:nrow, 3:4], small[0:nrow, 2:3]), r=["small2"], w=["small3"])
        P.dve(lambda e: e.tensor_scalar(xn[0:nrow, :], X[0:nrow, :], small[0:nrow, 3:4], None, ALU.mult), r=[xname, "small3"], w=["xn"])
        psb = ps[7].bitcast(BF16)
        for c in range(8):
            P.pe(lambda e, c=c: e.transpose(psb[:, c * 128:c * 128 + nrow], xn[0:nrow, c * 128:(c + 1) * 128], identb[0:nrow, 0:nrow]),
                 r=["xn", "identb"], w=["ps7"])
        for c in range(8):
            P.dve(lambda e, c=c: e.tensor_scalar(hT[:, c, tb * 128:tb * 128 + nrow], psb[:, c * 128:c * 128 + nrow],
                                                 gcol[:, l * 8 + c:l * 8 + c + 1], None, ALU.mult),
                  r=["ps7", "gcol"], w=["hT"])

    def fm_unit(l, uname, ntok, evac, blocks=tuple(range(8)), wres=None):
        W, wn = wres if wres is not None else load_w(l, UNITS.index(uname))
        for j in blocks:
            si = nxt("s", 4)
            for c in range(8):
                P.pe(lambda e, j=j, c=c, si=si: e.matmul(ps[si][0:64, 0:ntok], W[:, c, j * 64:(j + 1) * 64], hT[:, c, 0:ntok],
                                                        start=(c == 0), stop=(c == 7)), r=[wn, "hT"], w=[psn[si]])
            evac(j, ps[si], psn[si])

    def finish_head(O, oname, zt, zname, hsel, ncol, csl):
        if isinstance(hsel, tuple):
            nh_ = hsel[1] - hsel[0]
            zv = zt[:, hsel[0]:hsel[1], csl]
            ov = otmp[:, 0:ncol].rearrange("p (h q) -> p h q", h=nh_)
            Ov = O[0:64, 0:ncol].rearrange("p (h q) -> p h q", h=nh_)
        else:
            zv = zt[:, hsel, csl]
            ov = otmp[:, 0:ncol]
            Ov = O[0:64, 0:ncol]
        P.dve(lambda e: e.reciprocal(rden[64:65, 0:ncol], O[64:65, 0:ncol]), r=[oname], w=["rden"])
        P.dve(lambda e: e.tensor_tensor(ov, Ov, zv, ALU.mult), r=[oname, zname], w=["otmp"])
        bi_ = nxt("s", 4)
        P.pe(lambda e: e.matmul(ps[bi_][0:64, 0:ncol], ones1[64:65, 0:64], rden[64:65, 0:ncol], start=True, stop=True),
             r=["rden", "ones1"], w=[psn[bi_]])
        bv = ps[bi_][0:64, 0:ncol].rearrange("p (h q) -> p h q", h=nh_) if isinstance(hsel, tuple) else ps[bi_][0:64, 0:ncol]
        P.dve(lambda e: e.tensor_tensor(zv, ov, bv, ALU.mult), r=["otmp", psn[bi_]], w=[zname])

    def load_kv(Ksrc, Vsrc, h0, nh, k0, nk, krows_name):
        i = nxt("kv", 2)
        P.dma("sp", kbuf[i][0:64, 0:nh, 0:nk], Ksrc[h0:h0 + nh, :, k0:k0 + nk].rearrange("h d k -> d h k"), r=[krows_name], w=[f"kbuf{i}"])
        nb = (nk + 127) // 128
        for b in range(nb):
            n = min(128, nk - b * 128)
            P.dma("sp", vbuf[i][0:n, b, 0:nh, 0:64],
                  Vsrc[k0 + b * 128:k0 + b * 128 + n, h0 * 64:(h0 + nh) * 64].rearrange("k (h d) -> k h d", h=nh),
                  r=[krows_name], w=[f"vbuf{i}"])
        return i

    class Attn:
        def __init__(self):
            self.pend = []

        def tile(self, t):
            si = nxt("s", 4)
            S = ps[si]; n = t["n"]; qlo, qhi = t["qlo"], t["qhi"]
            nadd = len(t["adds"])
            P.pe(lambda e: e.matmul(S[0:n, qlo:qhi], t["kT"], t["qap"], start=True, stop=(nadd == 0)),
                 r=t["names"] + ["Q"], w=[psn[si]])
            for ai, (clo, chi, la, ra, an) in enumerate(t["adds"]):
                P.pe(lambda e, clo=clo, chi=chi, la=la, ra=ra, ai=ai: e.matmul(S[0:n, clo:chi], la, ra, start=False, stop=(ai == nadd - 1)),
                     r=an, w=[psn[si]])
            pi = nxt("pt", 4)
            if t["bias"] is not None:
                P.act(lambda e: e.activation(Pt[pi][0:n, qlo:qhi], S[0:n, qlo:qhi], AF.Exp, bias=t["bias"]),
                      r=[psn[si], "nbias"], w=[f"Pt{pi}"])
            else:
                P.act(lambda e: e.activation(Pt[pi][0:n, qlo:qhi], S[0:n, qlo:qhi], AF.Exp), r=[psn[si]], w=[f"Pt{pi}"])
            t["pi"] = pi
            self.pend.append(t)
            if len(self.pend) > 2:
                self.pv(self.pend.pop(0))

        def pv(self, t):
            n = t["n"]; qlo, qhi = t["qlo"], t["qhi"]; pi = t["pi"]; O = t["O"]
            P.pe(lambda e: e.matmul(O[0:65, qlo:qhi], t["v"], Pt[pi][0:n, qlo:qhi], start=t["first"], stop=t["last"]),
                 r=[f"Pt{pi}"] + t["names"], w=[t["oname"]])

        def flush(self):
            while self.pend:
                self.pv(self.pend.pop(0))


    for l in range(nlayers):
        P.pool(lambda e: e.memset(crs[:], 0.0), w=["crs"])
        P.dma("sp", crs[:], crel_d[l].rearrange("c p h -> p c h"), w=["crs"])
        build_tab([crs[:, c, :] for c in range(3)], [ohcs[:, c, :] for c in range(3)], tabc[l], ["crs", "ohcs"])
        build_toeplitz(tabc[l], bc)
        P.dve(lambda e: e.memset(tot[:], 0.0), w=["tot"])
        P.dve(lambda e: e.memset(tots[:], 0.0), w=["tots"])
        P.dve(lambda e: e.memset(totbc[:], 0.0), w=["totbc"])
        KAl, KBl, IKl, VAl, VBl = S_KA[l], S_KB[l], S_IK[l], S_VA[l], S_VB[l]
        KCl, VCl = S_KCL[l], S_VCL[l]
        hist = f"~hist{l}"
        chist = f"~chist{l}"
        deferred_exchange = []

        def grow(gp):
            if l == 0:
                return xp[gp * 512:(gp + 1) * 512, :]
            return hpg[gp * 512:(gp + 1) * 512, :]

        def tm_unit(uname, handler, wres=None):
            W, wn = wres if wres is not None else load_w(l, UNITS.index(uname))
            for tb in range(4):
                si = nxt("s", 4)
                for c in range(8):
                    P.pe(lambda e: e.matmul(ps[si][:, :], hT[:, c, tb * 128:(tb + 1) * 128], W[:, c, :], start=(c == 0), stop=(c == 7)),
                         r=[wn, "hT"], w=[psn[si]])
                k = nxt("st", 2)
                S_ = st[k]; sn = f"st{k}"
                P.act(lambda e: e.activation(S_[:], ps[si][:], AF.Copy), r=[psn[si]], w=[sn])
                handler(tb, S_, sn, k)

        def logf_of(S_, sn):
            P.dve(lambda e: e.tensor_tensor(lfb[:], S_[:, 288:296], bfb[:, l * 8:(l + 1) * 8], ALU.add), r=[sn, "bfb"], w=["lfb"])
            P.act(lambda e: e.activation(lfb[:], lfb[:], AF.Exp, scale=-1.0), r=["lfb"], w=["lfb"])
            P.act(lambda e: e.activation(lfb[:], lfb[:], AF.Ln, bias=1.0), r=["lfb"], w=["lfb"])
            P.dve(lambda e: e.tensor_scalar(lfb[:], lfb[:], -1.0, None, ALU.mult), r=["lfb"], w=["lfb"])

        def cum_into(dst_ap, dname):
            P.pe(lambda e: e.matmul(ps[5][:, 0:8], TRI, lfb[:], start=True, stop=False), r=["lfb", "cstf"], w=["ps5"])
            P.pe(lambda e: e.matmul(ps[5][:, 0:8], ones1[0:1, 0:128], tot[0:1, :], start=False, stop=True), r=["tot", "ones1"], w=["ps5"])
            P.dve(lambda e: e.tensor_copy(dst_ap, ps[5][:, 0:8]), r=["ps5"], w=[dname])
            P.pe(lambda e: e.matmul(ps[5][0:1, 8:16], E127, dst_ap, start=True, stop=True), r=[dname, "cstf"], w=["ps5"])
            P.dve(lambda e: e.tensor_copy(tot[:], ps[5][0:1, 8:16]), r=["ps5"], w=["tot"])

        def evac_k_to(dst3, k0, hname):
            def f(j, pt, pn):
                P.act(lambda e: e.activation(kst[:, j, :], pt[0:64, :], AF.Copy), r=[pn], w=["kst"])
                if j == 7:
                    P.dma("pool", dst3[:, :, k0:k0 + 512].rearrange("h d k -> d h k"), kst[:], r=["kst"], w=[hname])
            return f

        def evac_q(scale):
            def f(j, pt, pn):
                P.act(lambda e: e.activation(Q[0:64, j, :], pt[0:64, :], AF.Copy, scale=scale), r=[pn], w=["Q"])
            return f

        def evac_z(zt, zn):
            def f(j, pt, pn):
                P.act(lambda e: e.activation(zt[:, j, :], pt[0:64, :], AF.Silu), r=[pn], w=[zn])
            return f

        scb = SC.bitcast(BF16)
        kres = {}
        for ui, un in enumerate(("tka", "tva", "tb", "ka")):
            v = scb[:, ui * 4096:(ui + 1) * 4096].rearrange("p (c n) -> p c n", c=8)
            P.dma("sp", v, wub[l, UNITS.index(un)], r=[f"wub{l}_{UNITS.index(un)}", f"~wub{l}_{UNITS.index(un)}"], w=["SC"])
            kres[un] = (v, "SC")
        v = junk[:, 4096:8192].rearrange("p (c n) -> p c n", c=8)
        P.dma("sp", v, wub[l, UNITS.index("bx")], r=[f"wub{l}_{UNITS.index('bx')}", f"~wub{l}_{UNITS.index('bx')}"], w=["junk", "junkW"])
        kres["bx"] = (v, "junkW")
        for gp in range(4 * nm):
            t0 = gp * 512
            src = grow(gp)
            for tb in range(4):
                norm_block(l, src[tb * 128:(tb + 1) * 128, :], tb, rname=("~hpgw" if l > 0 else None))

            def h_kv(uname):
                def f(tb, S_, sn, k):
                    r0 = t0 + tb * 128
                    dst = {"tka": o_ak, "tva": o_av, "tkc": o_ck, "tvc": o_cv}[uname]
                    if uname in ("tka", "tva"):
                        P.dma("pool", dst[l, r0:r0 + 128, :], S_[:], r=[sn])
                    else:
                        P.dma("pool", dst[l, r0 - (SEQ - 512):r0 - (SEQ - 512) + 128, :], S_[:], r=[sn])
                    if uname == "tva":
                        V_, vn = vst[k], f"vst{k}"
                        P.dve(lambda e: e.tensor_copy(V_[:], S_[:]), r=[sn], w=[vn])
                        P.dma("pool", VAl[r0:r0 + 128, :], V_[:], r=[vn], w=[hist])
                return f

            def h_tb(tb, S_, sn, k):
                r0 = t0 + tb * 128
                P.dma("pool", o_bk[l, r0:r0 + 128, :], S_[:, 0:128], r=[sn])
                P.dma("pool", o_bv[l, r0:r0 + 128, :], S_[:, 128:256], r=[sn])
                P.dma("pool", o_ik[l, r0:r0 + 128, :], S_[:, 256:288], r=[sn])
                V_, vn = vst[k], f"vst{k}"
                P.dve(lambda e: e.tensor_copy(V_[:, 0:128], S_[:, 128:256]), r=[sn], w=[vn])
                P.dma("pool", VBl[r0:r0 + 128, :], V_[:, 0:128], r=[vn], w=[hist])
                P.dve(lambda e: e.tensor_tensor(lf4[:, tb, :], S_[:, 288:296], bfb[:, l * 8:(l + 1) * 8], ALU.add), r=[sn, "bfb"], w=["lf4"])

            tm_unit("tka", h_kv("tka"), wres=kres["tka"])
            tm_unit("tva", h_kv("tva"), wres=kres["tva"])
            tm_unit("tb", h_tb, wres=kres["tb"])
            lf4f = lf4[:].rearrange("p b h -> p (b h)")
            P.act(lambda e: e.activation(lf4f, lf4f, AF.Exp, scale=-1.0), r=["lf4"], w=["lf4"])
            P.act(lambda e: e.activation(lf4f, lf4f, AF.Ln, bias=1.0), r=["lf4"], w=["lf4"])
            P.dve(lambda e: e.tensor_scalar(lf4f, lf4f, -1.0, None, ALU.mult), r=["lf4"], w=["lf4"])
            P.dma("pool", o_lf[l, t0:t0 + 512, :].rearrange("(b p) h -> p b h", p=128), lf4[:], r=["lf4"])
            if gp == NG - 1:
                tm_unit("tkc", h_kv("tkc"))
                tm_unit("tvc", h_kv("tvc"))
            fm_unit(l, "ka", 512, evac_k_to(KAl, t0, hist), wres=kres["ka"])
            for b_ in range(4):
                for b2 in range(b_ + 1):
                    P.pe(lambda e: e.matmul(ps[5][:, b_ * 8:(b_ + 1) * 8], (TRI if b2 == b_ else onesf[:, :]), lf4[:, b2, :], start=(b2 == 0), stop=(b2 == b_)),
                         r=["lf4", "cstf", "onesf"], w=["ps5"])
            for b2 in range(4):
                P.pe(lambda e: e.matmul(ps[5][:, 32:40], onesf[:, :], lf4[:, b2, :], start=(b2 == 0), stop=(b2 == 3)), r=["lf4", "onesf"], w=["ps5"])
            for b_ in range(4):
                P.dve(lambda e: e.tensor_tensor(cstore[:, 4 * gp + b_, :], ps[5][:, b_ * 8:(b_ + 1) * 8], totbc[:, :], ALU.add), r=["ps5", "totbc"], w=["cstore"])
            P.dve(lambda e: e.tensor_tensor(totbc[:, :], ps[5][:, 32:40], totbc[:, :], ALU.add), r=["ps5", "totbc"], w=["totbc"])
            P.dve(lambda e: e.tensor_copy(tots[0:1, gp + 1, :], totbc[0:1, :]), r=["totbc"], w=["tots"])

            def evac_bx_k(j, pt, pn):
                if j < 2:
                    P.act(lambda e: e.activation(kst[:, j, :], pt[0:64, :], AF.Copy), r=[pn], w=["kst"])
                    if j == 1:
                        P.dma("pool", KBl[:, :, t0:t0 + 512].rearrange("h d k -> d h k"), kst[:, 0:2, :], r=["kst"], w=[hist])
                elif j == 6:
                    P.act(lambda e: e.activation(kst[:, 2, :], pt[0:64, :], AF.Copy), r=[pn], w=["kst"])
                    P.dma("pool", IKl[:, t0:t0 + 512], kst[:, 2, :], r=["kst"], w=[hist])
            fm_unit(l, "bx", 512, evac_bx_k, blocks=(0, 1, 6), wres=kres["bx"])
            if l == 0:
                convert_pieces(5 if gp < 4 * nm - 1 else len(pend_w))

        for m in range(nm):
            own = (xq[m] if l == 0 else hp1q[m * 512:(m + 1) * 512, :])
            own_r = (None if l == 0 else [f"hp1qc{2 * m}", f"hp1qc{2 * m + 1}"])
            for part in range(2):
                for tb in range(4):
                    if part == 1:
                        norm_block(l, own[tb * 128:(tb + 1) * 128, :], tb, rname=own_r)
                    elif l == 0:
                        norm_block(l, xprev[m][tb * 128:(tb + 1) * 128, :], tb)
                    else:
                        first = True
                        for r in range(4):
                            gq = 4 * m - 1 + r
                            if gq < 0:
                                continue
                            xi = nxt("x", 2)
                            X = xt[xi]; xname = f"xt{xi}"
                            P.dma("sp", X[:], grow(gq)[tb * 128:(tb + 1) * 128, :], r=["~hpgw"], w=[xname])
                            if first:
                                P.dve(lambda e: e.tensor_scalar(xacc[:], X[:], selb[:, r:r + 1], None, ALU.mult), r=[xname, "selb"], w=["xacc"])
                            else:
                                P.dve(lambda e: e.scalar_tensor_tensor(xacc[:], X[:], selb[:, r:r + 1], xacc[:], ALU.mult, ALU.add), r=[xname, "selb", "xacc"], w=["xacc"])
                            first = False
                        norm_block(l, None, tb, sb_src=(xacc, "xacc"))

                def h_c(uname):
                    def f(tb, S_, sn, k):
                        if uname == "tvc":
                            V_, vn = vst[k], f"vst{k}"
                            P.dve(lambda e: e.tensor_copy(V_[:], S_[:]), r=[sn], w=[vn])
                            P.dma("pool", VCl[part * 512 + tb * 128:part * 512 + (tb + 1) * 128, :], V_[:], r=[vn], w=[chist])
                    return f
                tm_unit("tvc", h_c("tvc"))
                fm_unit(l, "kc", 512, evac_k_to(KCl, part * 512, chist))
            g0 = 16 * m
            nkb = 16 * m + 16
            for r in range(4):
                if r == 0:
                    P.dve(lambda e: e.tensor_scalar(tot[0:1, :], tots[0:1, 4 * m + r, :], selb[0:1, r:r + 1], None, ALU.mult), r=["tots", "selb"], w=["tot"])
                else:
                    P.dve(lambda e: e.scalar_tensor_tensor(tot[0:1, :], tots[0:1, 4 * m + r, :], selb[0:1, r:r + 1], tot[0:1, :], ALU.mult, ALU.add),
                          r=["tots", "selb", "tot"], w=["tot"])

            def h_own(tb, S_, sn, k):
                P.dve(lambda e: e.tensor_tensor(lf4[:, tb, :], S_[:, 288:296], bfb[:, l * 8:(l + 1) * 8], ALU.add), r=[sn, "bfb"], w=["lf4"])
                P.dve(lambda e: e.tensor_scalar(wsgn[:, tb, :], S_[:, 296:304], 0.0, 2.0, ALU.is_ge, ALU.mult), r=[sn], w=["wsgn"])
                P.dve(lambda e: e.tensor_scalar(wsgn[:, tb, :], wsgn[:, tb, :], -1.0, None, ALU.add), r=["wsgn"], w=["wsgn"])
                P.dve(lambda e: e.scalar_tensor_tensor(wabs[:, tb, :], S_[:, 296:304], IDXS, wsgn[:, tb, :], ALU.mult, ALU.mult), r=[sn, "wsgn"], w=["wabs"])
            tm_unit("tb", h_own)
            lf4q = lf4[:].rearrange("p b h -> p (b h)")
            P.act(lambda e: e.activation(lf4q, lf4q, AF.Exp, scale=-1.0), r=["lf4"], w=["lf4"])
            P.act(lambda e: e.activation(lf4q, lf4q, AF.Ln, bias=1.0), r=["lf4"], w=["lf4"])
            P.dve(lambda e: e.tensor_scalar(lf4q, lf4q, -1.0, None, ALU.mult), r=["lf4"], w=["lf4"])
            for b_ in range(4):
                P.pe(lambda e: e.matmul(ps[5][:, b_ * 8:(b_ + 1) * 8], ones1[0:1, 0:128], tot[0:1, :], start=True, stop=False), r=["tot", "ones1"], w=["ps5"])
                for b2 in range(b_ + 1):
                    P.pe(lambda e: e.matmul(ps[5][:, b_ * 8:(b_ + 1) * 8], (TRI if b2 == b_ else onesf[:, :]), lf4[:, b2, :], start=False, stop=(b2 == b_)),
                         r=["lf4", "cstf", "onesf"], w=["ps5"])
            P.dve(lambda e: e.tensor_copy(cown[:].rearrange("p b h -> p (b h)"), ps[5][:, 0:32]), r=["ps5"], w=["cown"])
            P.pe(lambda e: e.matmul(ps[5][:, 16:24], E0ROW, cstore[:, g0, :], start=True, stop=True), r=["cstore", "cstf"], w=["ps5"])
            P.dve(lambda e: e.tensor_copy(cbc[:], ps[5][:, 16:24]), r=["ps5"], w=["cbc"])
            for h in range(8):
                P.dve(lambda e: e.tensor_scalar(nbias[:, 0:nkb, h], cstore[:, 0:nkb, h], cbc[:, h:h + 1], -1.0, ALU.subtract, ALU.mult),
                      r=["cstore", "cbc"], w=["nbias"])
            for tb in range(4):
                P.dve(lambda e: e.tensor_tensor(rq[:, tb, :], cown[:, tb, :], cbc[:], ALU.subtract), r=["cown", "cbc"], w=["rq"])
                P.pe(lambda e: e.transpose(ps[6][0:8, tb * 128:(tb + 1) * 128], rq[:, tb, :], IDENT), r=["rq", "cstf"], w=["ps6"])
            P.dve(lambda e: e.tensor_copy(rT[:], ps[6][0:8, :]), r=["ps6"], w=["rT"])

            def evac_bx_q(j, pt, pn):
                P.act(lambda e: e.activation(iqT[:, j - 2, :], pt[0:64, :], AF.Copy), r=[pn], w=["iqT"])

            def a_half(half):
                at = Attn()
                for sbk in range(4 * m + 4):
                    bi = load_kv(KAl, VAl, half * 4, 4, sbk * 512, 512, hist)
                    trail = (sbk >= 4 * m)
                    for kb in range(4):
                        adds = []
                        if trail:
                            rr = (sbk - 4 * m) * 4 + kb
                            mi = nxt("mb", 2)
                            P.dma("sp", Mb[mi][:, :], AMASK[rr], r=["~amask"], w=[f"Mb{mi}"])
                            adds = [(0, 512, identb[:], Mb[mi][:, :], ["identb", f"Mb{mi}"])]
                        for i in range(4):
                            hh = half * 4 + i
                            at.tile(dict(kT=kbuf[bi][0:65, i, kb * 128:(kb + 1) * 128], v=vbuf[bi][:, kb, i, 0:65], n=128, qlo=0, qhi=512,
                                         qap=Q[0:65, hh, 0:512], adds=adds, bias=nbias[:, 4 * sbk + kb, hh:hh + 1],
                                         names=[f"kbuf{bi}", f"vbuf{bi}"], O=ps[4 + i], oname=psn[4 + i],
                                         first=(sbk == 0 and kb == 0), last=(sbk == 4 * m + 3 and kb == 3)))
                at.flush()
                for i in range(4):
                    finish_head(ps[4 + i], psn[4 + i], zg["a"], "zga", half * 4 + i, 512, slice(0, 512))

            def c_half(half):
                at = Attn()
                started = [False] * 4
                for sbl in range(2):
                    bi = load_kv(KCl, VCl, half * 4, 4, sbl * 512, 512, chist)
                    for kb in range(4):
                        r_ = 4 * sbl + kb
                        qb_lo, qb_hi = max(0, r_ - 4), min(3, r_)
                        qlo, qhi = qb_lo * 128, (qb_hi + 1) * 128
                        for i in range(4):
                            hh = half * 4 + i
                            adds = []
                            for qb in range(qb_lo, qb_hi + 1):
                                dl = r_ - 4 - qb
                                c0, c1 = qb * 128, (qb + 1) * 128
                                if dl == 0:
                                    adds.append((c0, c1, identb[:], bc[:, 0, hh, :], ["identb", "btile"]))
                                    adds.append((c0, c1, identb[:], cm0b[:], ["identb", "cm0b"]))
                                elif dl == -1:
                                    adds.append((c0, c1, identb[:], bc[:, 1, hh, :], ["identb", "btile"]))
                                elif dl == -4:
                                    adds.append((c0, c1, identb[:], cm4b[:], ["identb", "cm4b"]))
                            at.tile(dict(kT=kbuf[bi][0:64, i, kb * 128:(kb + 1) * 128], v=vbuf[bi][:, kb, i, 0:65], n=128, qlo=qlo, qhi=qhi,
                                         qap=Q[0:64, hh, qlo:qhi], adds=adds, bias=(pvb[:, 0:1] if (m == 0 and sbl == 0) else None),
                                         names=[f"kbuf{bi}", f"vbuf{bi}"], O=ps[4 + i], oname=psn[4 + i],
                                         first=(not started[i]), last=(sbl == 1 and kb == 3)))
                            started[i] = True
                at.flush()
                for i in range(4):
                    finish_head(ps[4 + i], psn[4 + i], zg["c"], "zgc", half * 4 + i, 512, slice(0, 512))

            def b_topk(qb):
                NK = (16 * m + 13 + qb) * 128
                qs = slice(qb * 128, (qb + 1) * 128)
                for k0 in range(0, NK, 512):
                    nk = min(512, NK - k0)
                    ii = nxt("ik", 2)
                    P.dma("sp", ikbuf[ii][:, 0:nk], IKl[:, k0:k0 + nk], r=[hist], w=[f"ikbuf{ii}"])
                    for h in range(8):
                        base = 32 * (h % 2)
                        pi_ = 1 + nxt("ips", 3)
                        P.pe(lambda e: e.matmul(ps[pi_][:, 0:nk], iqT[base:base + 32, h // 2, qs], ikbuf[ii][base:base + 32, 0:nk], start=True, stop=True),
                             r=["iqT", f"ikbuf{ii}"], w=[psn[pi_]])
                        ri = nxt("rr", 3)
                        P.act(lambda e: e.activation(Rr[ri][:, 0:nk], ps[pi_][:, 0:nk], AF.Relu, scale=wabs[:, qb, h:h + 1]), r=[psn[pi_], "wabs"], w=[f"Rr{ri}"])
                        if h == 0:
                            P.dve(lambda e: e.tensor_scalar(SC[:, k0:k0 + nk], Rr[ri][:, 0:nk], wsgn[:, qb, 0:1], None, ALU.mult), r=[f"Rr{ri}", "wsgn"], w=["SC"])
                        else:
                            P.dve(lambda e: e.scalar_tensor_tensor(SC[:, k0:k0 + nk], Rr[ri][:, 0:nk], wsgn[:, qb, h:h + 1], SC[:, k0:k0 + nk], ALU.mult, ALU.add),
                                  r=[f"Rr{ri}", "wsgn", "SC"], w=["SC"])
                for ch in range(4):
                    c0 = g0 * 128 + ch * 512
                    wd = min(512, NK - c0)
                    if wd <= 0:
                        continue
                    ri = nxt("rr", 3)
                    P.dve(lambda e: e.tensor_scalar(Rr[ri][:, 0:wd], iota5[:, 0:wd], qlimc[:, qb * 4 + ch:qb * 4 + ch + 1], -1e30, ALU.is_ge, ALU.mult),
                          r=["iota5", "qlimc"], w=[f"Rr{ri}"])
                    P.dve(lambda e: e.tensor_tensor(SC[:, c0:c0 + wd], SC[:, c0:c0 + wd], Rr[ri][:, 0:wd], ALU.add), r=["SC", f"Rr{ri}"], w=["SC"])
                P.dve(lambda e: e.memset(cntb[:], 0.0), w=["cntb"])
                P.dve(lambda e: e.memset(small[:, 8:9], 0.0), w=["cand"])
                for it in range(NBIS):
                    stp = 64.0 * (0.5 ** it)
                    P.dve(lambda e: e.tensor_scalar(junk[:, 0:NK], SC[:, 0:NK], small[:, 8:9], 0.0, ALU.is_ge, ALU.add, accum_out=cntb[:, it:it + 1]),
                          r=["SC", "cand", "cntb"], w=["junk", "cntb"])
                    a, b_ = (stp, -0.5 * stp) if it < NBIS - 1 else (stp, -stp)
                    P.dve(lambda e: e.tensor_scalar(small[:, 9:10], cntb[:, it:it + 1], float(TOPK), a, ALU.is_ge, ALU.mult), r=["cntb"], w=["fl"])
                    P.dve(lambda e: e.scalar_tensor_tensor(small[:, 8:9], small[:, 9:10], b_, small[:, 8:9], ALU.add, ALU.add), r=["fl", "cand"], w=["cand"])
                P.dve(lambda e: e.tensor_scalar(junk[:, 0:NK], SC[:, 0:NK], small[:, 8:9], NEG, ALU.is_lt, ALU.mult), r=["SC", "cand"], w=["junk"])
                P.dma("pool", MBS[qb][:, 0:NK], junk[:, 0:NK], r=["junk"], w=[f"mbs{qb}"])

            def b_attn(qb):
                qs = slice(qb * 128, (qb + 1) * 128)
                nkq = 16 * m + 13 + qb
                at = Attn()
                for sbk in range((nkq + 3) // 4):
                    k0 = sbk * 512
                    nk = min(512, nkq * 128 - k0)
                    mi = nxt("mb", 2)
                    P.dma("sp", Mb[mi][:, 0:nk], MBS[qb][:, k0:k0 + nk], r=[f"mbs{qb}"], w=[f"Mb{mi}"])
                    bi = load_kv(KBl, VBl, 0, 2, k0, nk, hist)
                    for kb in range(nk // 128):
                        gkb = sbk * 4 + kb
                        for jj in range(2):
                            adds = [(0, 512, Mb[mi][:, kb * 128:(kb + 1) * 128], i4b[:].rearrange("p a b -> p (a b)"), [f"Mb{mi}", "i4b"])]
                            if gkb >= g0 - 1 and (gkb - g0 - qb) % 4 in (0, 3):
                                rr = gkb - g0 + 1
                                ci = nxt("cmb", 2)
                                sc0 = sel01[:, (qb * 17 + rr) * 2:(qb * 17 + rr) * 2 + 1]
                                sc1 = sel01[:, (qb * 17 + rr) * 2 + 1:(qb * 17 + rr) * 2 + 2]
                                P.dma("sp", comb[ci][:].rearrange("p a b -> p (a b)"), COMBS[qb, rr, jj], r=["~combs"], w=[f"comb{ci}"])
                                adds.append((0, 512, identb[:], comb[ci][:].rearrange("p a b -> p (a b)"), ["identb", f"comb{ci}"]))
                            at.tile(dict(kT=kbuf[bi][0:64, jj, kb * 128:(kb + 1) * 128], v=vbuf[bi][:, kb, jj, 0:65], n=128, qlo=0, qhi=512,
                                         qap=Q[0:64, 4 * jj:4 * jj + 4, qs], adds=adds, bias=None,
                                         names=[f"kbuf{bi}", f"vbuf{bi}"], O=ps[4 + jj], oname=psn[4 + jj],
                                         first=(gkb == 0), last=(gkb == nkq - 1)))
                at.flush()
                for jj in range(2):
                    finish_head(ps[4 + jj], psn[4 + jj], zg["b"], "zgb", (4 * jj, 4 * jj + 4), 512, qs)

            fm_unit(l, "bx", 512, evac_bx_q, blocks=(2, 3, 4, 5))
            b_topk(0)
            fm_unit(l, "za", 512, evac_z(zg["a"], "zga"))
            fm_unit(l, "qa", 512, evac_q(SCALE))
            for h in range(8):
                P.dma("sp", Q[64:65, h, :], rT[h:h + 1, :], r=["rT"], w=["Q"])
            a_half(0)
            b_topk(1)
            a_half(1)
            fm_unit(l, "zc", 512, evac_z(zg["c"], "zgc"))
            fm_unit(l, "qc", 512, evac_q(SCALE))
            c_half(0)
            c_half(1)
            fm_unit(l, "zb", 512, evac_z(zg["b"], "zgb"))
            fm_unit(l, "qb", 512, evac_q(SCALE))
            b_attn(0)
            b_topk(2)
            b_attn(1)
            b_topk(3)
            b_attn(2)
            b_attn(3)

            allz = [zg["a"], zg["b"], zg["c"]]
            alln = ["zga", "zgb", "zgc"]
            for u in range(6):
                Wo_, won = load_wo(l, u)
                for hq in range(4):
                    hidx = u * 4 + hq
                    zt, zn = allz[hidx // 8], alln[hidx // 8]
                    for tb in range(4):
                        for n_ in range(2):
                            P.pe(lambda e: e.matmul(ps[tb * 2 + n_][:, :], zt[:, hidx % 8, tb * 128:(tb + 1) * 128], Wo_[:, hq, n_ * 512:(n_ + 1) * 512],
                                                    start=(hidx == 0), stop=(hidx == 23)), r=[zn, won], w=[psn[tb * 2 + n_]])
            for tb in range(4):
                xi = nxt("x", 2)
                X = xt[xi]; xname = f"xt{xi}"
                P.dma("sp", X[:], own[tb * 128:(tb + 1) * 128, :], r=(own_r if own_r else []), w=[xname])
                for n_ in range(2):
                    P.dve(lambda e: e.tensor_tensor(X[:, n_ * 512:(n_ + 1) * 512], X[:, n_ * 512:(n_ + 1) * 512], ps[tb * 2 + n_][:, :], ALU.add),
                          r=[xname, psn[tb * 2 + n_]], w=[xname])
                r0 = m * 512 + tb * 128
                if l < nlayers - 1:
                    P.dma("pool", hp1q[r0:r0 + 128, :], X[:], r=[xname], w=[f"hp1qc{r0 // 256}"])
                else:
                    P.act(lambda e: e.activation(junk[:, 0:1024], X[:], AF.Square, accum_out=small[:, 16:17]), r=[xname], w=["junk", "fs0"])
                    P.dve(lambda e: e.tensor_scalar(small[:, 17:18], small[:, 16:17], 1.0 / D_MODEL, EPS, ALU.mult, ALU.add), r=["fs0"], w=["fs1"])
                    P.act(lambda e: e.activation(small[:, 18:19], small[:, 17:18], AF.Sqrt), r=["fs1"], w=["fs2"])
                    P.dve(lambda e: e.reciprocal(small[:, 19:20], small[:, 18:19]), r=["fs2"], w=["fs3"])
                    P.dve(lambda e: e.scalar_tensor_tensor(X[:], X[:], small[:, 19:20], fgb[:], ALU.mult, ALU.mult), r=[xname, "fs3", "fgb"], w=[xname])
                    P.dma("pool", y_q[r0:r0 + 128, :], X[:], r=[xname])
            def emit_exchange(m=m):
                for hf in range(2):
                    cidx = 2 * m + hf
                    P.dma("pool", ccs, hp1q[cidx * 256:(cidx + 1) * 256, :], r=[f"hp1qc{cidx}"], w=["ccs"])
                    P.add("pool", lambda e: e.collective_compute("AllGather", ALU.bypass, replica_groups=[[0, 1, 2, 3], [4, 5, 6, 7]],
                                                                 ins=[ccs.opt()], outs=[ccd.opt()]), r=["ccs"], w=["ccd"], cc=True)
                    for r in range(4):
                        a0 = (4 * m + r) * 512 + hf * 256
                        P.dma("pool", hpg[a0:a0 + 256, :], ccd[r * 256:(r + 1) * 256, :], r=["ccd"], w=["~hpgw"])
            if l < nlayers - 1:
                if m < nm - 1:
                    emit_exchange()
                else:
                    deferred_exchange.append(emit_exchange)
        if do_sample:
            NS = DEC_SEQ
            shist = f"~shist{l}"
            psb7 = ps[7].bitcast(BF16)

            def prep_cache(src2d, nrows, ncols, kdst, vdst, nheads):
                for b in range(nrows // 128):
                    xi = nxt("x", 2)
                    X = xt[xi]; xname = f"xt{xi}"
                    P.dma("sp", X[:, 0:ncols], src2d[b * 128:(b + 1) * 128, :], w=[xname])
                    P.dve(lambda e: e.tensor_copy(xn[:, 0:ncols], X[:, 0:ncols]), r=[xname], w=["xn"])
                    if vdst is not None:
                        P.dma("pool", vdst[b * 128:(b + 1) * 128, :], xn[:, 0:ncols], r=["xn"], w=[shist])
                    if kdst is not None:
                        for hh in range(nheads):
                            P.pe(lambda e: e.transpose(psb7[0:64, hh * 128:(hh + 1) * 128], xn[:, hh * 64:(hh + 1) * 64], identb[:]),
                                 r=["xn", "identb"], w=["ps7"])
                        P.act(lambda e: e.activation(kst[:, 0:nheads, 0:128], psb7[0:64, 0:nheads * 128].rearrange("p (h k) -> p h k", h=nheads), AF.Copy),
                              r=["ps7"], w=["kst"])
                        P.dma("pool", kdst[:, :, b * 128:(b + 1) * 128].rearrange("h d k -> d h k"), kst[:, 0:nheads, 0:128], r=["kst"], w=[shist])

            prep_cache(ca_k[l], PAST, 512, SS_KA[l], None, 8)
            prep_cache(ca_v[l], PAST, 512, None, SS_VA[l], 8)
            prep_cache(cb_k[l], PAST, 128, SS_KB[l], None, 2)
            prep_cache(cb_v[l], PAST, 128, None, SS_VB[l], 2)
            prep_cache(cc_k[l], 512, 512, SS_KC[l], None, 8)
            prep_cache(cc_v[l], 512, 512, None, SS_VC[l], 8)
            for b in range(PAST // 128):
                xi = nxt("x", 2)
                X = xt[xi]; xname = f"xt{xi}"
                P.dma("sp", X[:, 0:32], cb_ik[l, b * 128:(b + 1) * 128, :], w=[xname])
                P.dve(lambda e: e.tensor_copy(xn[:, 0:32], X[:, 0:32]), r=[xname], w=["xn"])
                P.dve(lambda e: e.tensor_copy(xn[:, 32:64], X[:, 0:32]), r=[xname], w=["xn"])
                P.pe(lambda e: e.transpose(psb7[0:64, 0:128], xn[:, 0:64], identb[:]), r=["xn", "identb"], w=["ps7"])
                P.act(lambda e: e.activation(kst[:, 0, 0:128], psb7[0:64, 0:128], AF.Copy), r=["ps7"], w=["kst"])
                P.dma("pool", SS_IK[l][:, b * 128:(b + 1) * 128], kst[:, 0, 0:128], r=["kst"], w=[shist])
            P.dve(lambda e: e.memset(tot[:], 0.0), w=["tot"])

            def cum_block(kb_, n):
                P.pe(lambda e: e.matmul(ps[5][0:n, 0:8], cstf[0:n, 2, 0:n], lfb[0:n, :], start=True, stop=False), r=["lfb", "cstf"], w=["ps5"])
                P.pe(lambda e: e.matmul(ps[5][0:n, 0:8], ones1[0:1, 0:n], tot[0:1, :], start=False, stop=True), r=["tot", "ones1"], w=["ps5"])
                P.dve(lambda e: e.tensor_copy(cstore[0:n, kb_, :], ps[5][0:n, 0:8]), r=["ps5"], w=["cstore"])
                P.pe(lambda e: e.matmul(ps[5][0:1, 8:16], cstf[0:n, 1, 128 - n:129 - n], cstore[0:n, kb_, :], start=True, stop=True), r=["cstore", "cstf"], w=["ps5"])
                P.dve(lambda e: e.tensor_copy(tot[:], ps[5][0:1, 8:16]), r=["ps5"], w=["tot"])

            for b in range(PAST // 128):
                P.dma("sp", lfb[:], ca_lf[l, b * 128:(b + 1) * 128, :], w=["lfb"])
                cum_block(b, 128)
            norm_block(l, (xs if l == 0 else hs1)[:, :], 0, nrow=NS, rname=("hs1" if l > 0 else None))
            for uname in TM_UNITS:
                W, wn = load_w(l, UNITS.index(uname))
                si = nxt("s", 4)
                for c in range(8):
                    P.pe(lambda e: e.matmul(ps[si][0:NS, :], hT[:, c, 0:NS], W[:, c, :], start=(c == 0), stop=(c == 7)), r=[wn, "hT"], w=[psn[si]])
                k = nxt("st", 2)
                S_ = st[k]; sn = f"st{k}"
                P.act(lambda e: e.activation(S_[0:NS, :], ps[si][0:NS, :], AF.Copy), r=[psn[si]], w=[sn])
                V_, vn = vst[k], f"vst{k}"
                if uname in ("tka", "tva", "tkc", "tvc"):
                    dst = {"tka": s_ak, "tva": s_av, "tkc": s_ck, "tvc": s_cv}[uname]
                    P.dma("pool", dst[l], S_[0:NS, :], r=[sn])
                    if uname in ("tva", "tvc"):
                        P.dve(lambda e: e.tensor_copy(V_[0:NS, :], S_[0:NS, :]), r=[sn], w=[vn])
                        vd = SS_VA[l][PAST:PAST + NS, :] if uname == "tva" else SS_VC[l][512:512 + NS, :]
                        P.dma("pool", vd, V_[0:NS, :], r=[vn], w=[shist])
                else:
                    P.dma("pool", s_bk[l], S_[0:NS, 0:128], r=[sn])
                    P.dma("pool", s_bv[l], S_[0:NS, 128:256], r=[sn])
                    P.dma("pool", s_ik[l], S_[0:NS, 256:288], r=[sn])
                    P.dve(lambda e: e.tensor_copy(V_[0:NS, 0:128], S_[0:NS, 128:256]), r=[sn], w=[vn])
                    P.dma("pool", SS_VB[l][PAST:PAST + NS, :], V_[0:NS, 0:128], r=[vn], w=[shist])
                    P.dve(lambda e: e.tensor_tensor(lfb[0:NS, :], S_[0:NS, 288:296], bfb[0:NS, l * 8:(l + 1) * 8], ALU.add), r=[sn, "bfb"], w=["lfb"])
                    P.act(lambda e: e.activation(lfb[0:NS, :], lfb[0:NS, :], AF.Exp, scale=-1.0), r=["lfb"], w=["lfb"])
                    P.act(lambda e: e.activation(lfb[0:NS, :], lfb[0:NS, :], AF.Ln, bias=1.0), r=["lfb"], w=["lfb"])
                    P.dve(lambda e: e.tensor_scalar(lfb[0:NS, :], lfb[0:NS, :], -1.0, None, ALU.mult), r=["lfb"], w=["lfb"])
                    P.dma("pool", s_lf[l], lfb[0:NS, :], r=["lfb"])
                    cum_block(8, NS)
                    P.dve(lambda e: e.tensor_scalar(wsgn[0:NS, 0, :], S_[0:NS, 296:304], 0.0, 2.0, ALU.is_ge, ALU.mult), r=[sn], w=["wsgn"])
                    P.dve(lambda e: e.tensor_scalar(wsgn[0:NS, 0, :], wsgn[0:NS, 0, :], -1.0, None, ALU.add), r=["wsgn"], w=["wsgn"])
                    P.dve(lambda e: e.scalar_tensor_tensor(wabs[0:NS, 0, :], S_[0:NS, 296:304], IDXS, wsgn[0:NS, 0, :], ALU.mult, ALU.mult), r=[sn, "wsgn"], w=["wabs"])
            P.pe(lambda e: e.matmul(ps[5][:, 16:24], cstf[0:NS, 3, :], cstore[0:NS, 8, :], start=True, stop=True), r=["cstore", "cstf"], w=["ps5"])
            P.dve(lambda e: e.tensor_copy(cbc[:], ps[5][:, 16:24]), r=["ps5"], w=["cbc"])
            for h in range(8):
                P.dve(lambda e: e.tensor_scalar(nbias[:, 0:9, h], cstore[:, 0:9, h], cbc[:, h:h + 1], -1.0, ALU.subtract, ALU.mult), r=["cstore", "cbc"], w=["nbias"])
            P.dve(lambda e: e.tensor_tensor(rq[0:NS, 0, :], cstore[0:NS, 8, :], cbc[0:NS, :], ALU.subtract), r=["cstore", "cbc"], w=["rq"])
            P.pe(lambda e: e.transpose(ps[6][0:8, 0:NS], rq[0:NS, 0, :], cstf[0:NS, 0, 0:NS]), r=["rq", "cstf"], w=["ps6"])
            P.dve(lambda e: e.tensor_copy(rT[:, 0:NS], ps[6][0:8, 0:NS]), r=["ps6"], w=["rT"])

            def s_evac_k(dst3, koff):
                def f(j, pt, pn):
                    P.act(lambda e: e.activation(kst[:, j, 0:NS], pt[0:64, 0:NS], AF.Copy), r=[pn], w=["kst"])
                    if j == 7:
                        P.dma("pool", dst3[:, :, koff:koff + NS].rearrange("h d k -> d h k"), kst[:, :, 0:NS], r=["kst"], w=[shist])
                return f

            def s_evac_q(j, pt, pn):
                P.act(lambda e: e.activation(Q[0:64, j, 0:NS], pt[0:64, 0:NS], AF.Copy, scale=SCALE), r=[pn], w=["Q"])

            def s_evac_z(zt, zn):
                def f(j, pt, pn):
                    P.act(lambda e: e.activation(zt[:, j, 0:NS], pt[0:64, 0:NS], AF.Silu), r=[pn], w=[zn])
                return f

            fm_unit(l, "ka", NS, s_evac_k(SS_KA[l], PAST))
            fm_unit(l, "za", NS, s_evac_z(zg["a"], "zga"))
            fm_unit(l, "qa", NS, s_evac_q)
            for h in range(8):
                P.dma("sp", Q[64:65, h, 0:NS], rT[h:h + 1, 0:NS], r=["rT"], w=["Q"])
            for half in range(2):
                at = Attn()
                for sbk in range(3):
                    nk = 512 if sbk < 2 else NS
                    bi = load_kv(SS_KA[l], SS_VA[l], half * 4, 4, sbk * 512, nk, shist)
                    for kb in range((nk + 127) // 128):
                        n = min(128, nk - kb * 128)
                        adds = [(0, NS, identb[0:NS, 0:NS], ma0b[0:NS, 0:NS], ["identb", "ma0b"])] if sbk == 2 else []
                        for i in range(4):
                            hh = half * 4 + i
                            at.tile(dict(kT=kbuf[bi][0:65, i, kb * 128:kb * 128 + n], v=vbuf[bi][0:n, kb, i, 0:65], n=n, qlo=0, qhi=NS,
                                         qap=Q[0:65, hh, 0:NS], adds=adds, bias=nbias[0:n, 4 * sbk + kb, hh:hh + 1],
                                         names=[f"kbuf{bi}", f"vbuf{bi}"], O=ps[4 + i], oname=psn[4 + i],
                                         first=(sbk == 0 and kb == 0), last=(sbk == 2)))
                at.flush()
                for i in range(4):
                    finish_head(ps[4 + i], psn[4 + i], zg["a"], "zga", half * 4 + i, NS, slice(0, NS))
            fm_unit(l, "kc", NS, s_evac_k(SS_KC[l], 512))
            fm_unit(l, "zc", NS, s_evac_z(zg["c"], "zgc"))
            fm_unit(l, "qc", NS, s_evac_q)
            for half in range(2):
                at = Attn()
                for sbk in range(2):
                    nk = 512 if sbk < 1 else NS
                    bi = load_kv(SS_KC[l], SS_VC[l], half * 4, 4, sbk * 512, nk, shist)
                    for kb in range((nk + 127) // 128):
                        n = min(128, nk - kb * 128)
                        for i in range(4):
                            hh = half * 4 + i
                            adds = []
                            if sbk == 0 and kb == 3:
                                adds = [(0, NS, identb[:], bc[:, 1, hh, 0:NS], ["identb", "btile"])]
                            if sbk == 1:
                                adds = [(0, NS, identb[0:NS, 0:NS], bc[0:NS, 0, hh, 0:NS], ["identb", "btile"])]
                            at.tile(dict(kT=kbuf[bi][0:64, i, kb * 128:kb * 128 + n], v=vbuf[bi][0:n, kb, i, 0:65], n=n, qlo=0, qhi=NS,
                                         qap=Q[0:64, hh, 0:NS], adds=adds, bias=None,
                                         names=[f"kbuf{bi}", f"vbuf{bi}"], O=ps[4 + i], oname=psn[4 + i],
                                         first=(sbk == 0 and kb == 0), last=(sbk == 1)))
                at.flush()
                for i in range(4):
                    finish_head(ps[4 + i], psn[4 + i], zg["c"], "zgc", half * 4 + i, NS, slice(0, NS))
            def s_evac_bx(j, pt, pn):
                if j < 2:
                    P.act(lambda e: e.activation(kst[:, j, 0:NS], pt[0:64, 0:NS], AF.Copy), r=[pn], w=["kst"])
                    if j == 1:
                        P.dma("pool", SS_KB[l][:, :, PAST:PAST + NS].rearrange("h d k -> d h k"), kst[:, 0:2, 0:NS], r=["kst"], w=[shist])
                elif j < 6:
                    P.act(lambda e: e.activation(iqT[:, j - 2, 0:NS], pt[0:64, 0:NS], AF.Copy), r=[pn], w=["iqT"])
                elif j == 6:
                    P.act(lambda e: e.activation(kst[:, 2, 0:NS], pt[0:64, 0:NS], AF.Copy), r=[pn], w=["kst"])
                    P.dma("pool", SS_IK[l][:, PAST:PAST + NS], kst[:, 2, 0:NS], r=["kst"], w=[shist])
            fm_unit(l, "bx", NS, s_evac_bx)
            fm_unit(l, "zb", NS, s_evac_z(zg["b"], "zgb"))
            fm_unit(l, "qb", NS, s_evac_q)
            NK = PAST + NS
            for k0 in range(0, NK, 512):
                nk = min(512, NK - k0)
                ii = nxt("ik", 2)
                P.dma("sp", ikbuf[ii][:, 0:nk], SS_IK[l][:, k0:k0 + nk], r=[shist], w=[f"ikbuf{ii}"])
                for h in range(8):
                    base = 32 * (h % 2)
                    pi_ = 2 + nxt("ips", 2)
                    P.pe(lambda e: e.matmul(ps[pi_][0:NS, 0:nk], iqT[base:base + 32, h // 2, 0:NS], ikbuf[ii][base:base + 32, 0:nk], start=True, stop=True),
                         r=["iqT", f"ikbuf{ii}"], w=[psn[pi_]])
                    ri = nxt("rr", 2)
                    P.act(lambda e: e.activation(Rr[ri][0:NS, 0:nk], ps[pi_][0:NS, 0:nk], AF.Relu, scale=wabs[0:NS, 0, h:h + 1]), r=[psn[pi_], "wabs"], w=[f"Rr{ri}"])
                    if h == 0:
                        P.dve(lambda e: e.tensor_scalar(SC[0:NS, k0:k0 + nk], Rr[ri][0:NS, 0:nk], wsgn[0:NS, 0, 0:1], None, ALU.mult), r=[f"Rr{ri}", "wsgn"], w=["SC"])
                    else:
                        P.dve(lambda e: e.scalar_tensor_tensor(SC[0:NS, k0:k0 + nk], Rr[ri][0:NS, 0:nk], wsgn[0:NS, 0, h:h + 1], SC[0:NS, k0:k0 + nk], ALU.mult, ALU.add),
                              r=[f"Rr{ri}", "wsgn", "SC"], w=["SC"])
            P.dve(lambda e: e.memset(cntb[:], 0.0), w=["cntb"])
            P.dve(lambda e: e.memset(small[:, 8:9], 0.0), w=["cand"])
            for it in range(NBIS):
                stp = 64.0 * (0.5 ** it)
                P.dve(lambda e: e.tensor_scalar(junk[0:NS, 0:NK], SC[0:NS, 0:NK], small[0:NS, 8:9], 0.0, ALU.is_ge, ALU.add, accum_out=cntb[0:NS, it:it + 1]),
                      r=["SC", "cand", "cntb"], w=["junk", "cntb"])
                a, b_ = (stp, -0.5 * stp) if it < NBIS - 1 else (stp, -stp)
                P.dve(lambda e: e.tensor_scalar(small[0:NS, 9:10], cntb[0:NS, it:it + 1], float(TOPK), a, ALU.is_ge, ALU.mult), r=["cntb"], w=["fl"])
                P.dve(lambda e: e.scalar_tensor_tensor(small[0:NS, 8:9], small[0:NS, 9:10], b_, small[0:NS, 8:9], ALU.add, ALU.add), r=["fl", "cand"], w=["cand"])
            at = Attn()
            for sbk in range(3):
                k0 = sbk * 512
                nk = min(512, NK - k0)
                mi = nxt("mb", 2)
                P.dve(lambda e: e.tensor_scalar(Mb[mi][0:NS, 0:nk], SC[0:NS, k0:k0 + nk], small[0:NS, 8:9], NEG, ALU.is_lt, ALU.mult), r=["SC", "cand"], w=[f"Mb{mi}"])
                bi = load_kv(SS_KB[l], SS_VB[l], 0, 2, k0, nk, shist)
                for kb in range((nk + 127) // 128):
                    n = min(128, nk - kb * 128)
                    gkb = sbk * 4 + kb
                    for j in range(2):
                        adds = [(0, 4 * NS, Mb[mi][0:NS, kb * 128:kb * 128 + n], i4b[0:NS, :, 0:NS], [f"Mb{mi}", "i4b"])]
                        if gkb >= 7:
                            for hq in range(4):
                                if gkb == 7:
                                    adds.append((hq * NS, (hq + 1) * NS, identb[:], b5[:, 1, 4 * j + hq, 0:NS], ["identb", "btile"]))
                                else:
                                    adds.append((hq * NS, (hq + 1) * NS, identb[0:NS, 0:NS], b5[0:NS, 0, 4 * j + hq, 0:NS], ["identb", "btile"]))
                        at.tile(dict(kT=kbuf[bi][0:64, j, kb * 128:kb * 128 + n], v=vbuf[bi][0:n, kb, j, 0:65], n=n, qlo=0, qhi=4 * NS,
                                     qap=Q[0:64, 4 * j:4 * j + 4, 0:NS], adds=adds, bias=None,
                                     names=[f"kbuf{bi}", f"vbuf{bi}"], O=ps[4 + j], oname=psn[4 + j],
                                     first=(gkb == 0), last=(gkb == 8)))
            at.flush()
            for j in range(2):
                finish_head(ps[4 + j], psn[4 + j], zg["b"], "zgb", (4 * j, 4 * j + 4), 4 * NS, slice(0, NS))
            allz = [zg["a"], zg["b"], zg["c"]]
            alln = ["zga", "zgb", "zgc"]
            for u in range(6):
                Wo_, won = load_wo(l, u)
                for hq in range(4):
                    hidx = u * 4 + hq
                    zt, zn = allz[hidx // 8], alln[hidx // 8]
                    for n_ in range(2):
                        P.pe(lambda e: e.matmul(ps[n_][0:NS, :], zt[:, hidx % 8, 0:NS], Wo_[:, hq, n_ * 512:(n_ + 1) * 512], start=(hidx == 0), stop=(hidx == 23)),
                             r=[zn, won], w=[psn[n_]])
            xi = nxt("x", 2)
            X = xt[xi]; xname = f"xt{xi}"
            P.dma("sp", X[0:NS, :], (xs if l == 0 else hs1)[:, :], r=(["hs1"] if l > 0 else []), w=[xname])
            for n_ in range(2):
                P.dve(lambda e: e.tensor_tensor(X[0:NS, n_ * 512:(n_ + 1) * 512], X[0:NS, n_ * 512:(n_ + 1) * 512], ps[n_][0:NS, :], ALU.add), r=[xname, psn[n_]], w=[xname])
            if l < nlayers - 1:
                P.dma("pool", hs1[:, :], X[0:NS, :], r=[xname], w=["hs1"])
            else:
                P.act(lambda e: e.activation(junk[0:NS, 0:1024], X[0:NS, :], AF.Square, accum_out=small[0:NS, 16:17]), r=[xname], w=["junk", "fs0"])
                P.dve(lambda e: e.tensor_scalar(small[0:NS, 17:18], small[0:NS, 16:17], 1.0 / D_MODEL, EPS, ALU.mult, ALU.add), r=["fs0"], w=["fs1"])
                P.act(lambda e: e.activation(small[0:NS, 18:19], small[0:NS, 17:18], AF.Sqrt), r=["fs1"], w=["fs2"])
                P.dve(lambda e: e.reciprocal(small[0:NS, 19:20], small[0:NS, 18:19]), r=["fs2"], w=["fs3"])
                P.dve(lambda e: e.scalar_tensor_tensor(X[0:NS, :], X[0:NS, :], small[0:NS, 19:20], fgb[0:NS, :], ALU.mult, ALU.mult), r=[xname, "fs3", "fgb"], w=[xname])
                P.dma("pool", y_s[:, :], X[0:NS, :], r=[xname])
        for fx in deferred_exchange:
            fx()
    P.finalize_and_emit()
    return nc, es, P


def _t5_bucket_np(rel):
    nb = 16
    max_exact = 8
    ret = np.where(rel > 0, nb, 0)
    n = np.abs(rel)
    nf = np.maximum(n, 1).astype(np.float32)
    large = max_exact + (np.log(nf / max_exact) / math.log(128 / max_exact) * (nb - max_exact)).astype(np.int32)
    large = np.minimum(large, nb - 1)
    return ret + np.where(n < max_exact, n, large)


def _constants():
    p = np.arange(128)[:, None]
    f = np.arange(128)[None, :]
    cst = np.zeros((128, 8, 128), np.float32)
    cst[:, 0] = (p == f)
    cst[:, 1] = (p + f == 127)
    cst[:, 2] = (p <= f)
    cst[0, 3, :] = 1.0
    cst[:, 4] = np.where(p > f, NEG, 0.0)
    cst[:, 5] = np.where((p >= 64) & (f < 64), NEG, 0.0)
    cst[:, 6] = np.where((p < 64) & (f >= 64), NEG, 0.0)
    cst[:, 7] = np.where((p < 64) & (f >= 64), -1e30, 0.0)
    rel = 127 - np.arange(LTAB)
    bk = _t5_bucket_np(rel.astype(np.int32))
    oh5 = np.zeros((32, LTAB), np.float32)
    oh5[bk, np.arange(LTAB)] += 1.0
    far = int(_t5_bucket_np(np.array([-100000], np.int32))[0])
    oh5[far, :] -= 1.0
    idx = np.clip(rel, -128, 128) + 128
    ohc = np.zeros((3 * 128, LTAB), np.float32)
    ohc[idx, np.arange(LTAB)] += 1.0
    ohc[0, :] -= 1.0
    return cst.reshape(128, 8 * 128), oh5, ohc.reshape(3, 128, LTAB)


_PROG = {}


def _get_prog(key=(True, 4, DEPTH)):
    if key not in _PROG:
        _PROG[key] = build_program(*key)
    return _PROG[key]


def _host_inputs(x_prompt, norm_g, w_in, b_f, t5_bias, c_rel_bias, w_out, final_g):
    cst, oh5, ohc = _constants()
    wus = np.zeros((DEPTH, NU, 128, 8, 512), np.float32)
    for l in range(DEPTH):
        for u, name in enumerate(UNITS):
            cols = np.array(_unit_cols(name))
            m = cols >= 0
            w = np.zeros((D_MODEL, 512), np.float32)
            w[:, m] = w_in[l][:, cols[m]]
            wus[l, u] = w.reshape(8, 128, 512).transpose(1, 0, 2)
    wos = np.ascontiguousarray(w_out.reshape(DEPTH, 6, 4, 64, D_MODEL).transpose(0, 1, 3, 2, 4))
    gcol = np.ascontiguousarray(norm_g.reshape(DEPTH, 8, 128).transpose(2, 0, 1).reshape(128, DEPTH * 8))
    crel = np.zeros((DEPTH, 384, 8), np.float32)
    crel[:, :257] = c_rel_bias
    common = dict(wu=wus, wo=wos, gcol=gcol, fg=np.ascontiguousarray(final_g.reshape(1, D_MODEL)),
                  bfb=np.ascontiguousarray(b_f.reshape(1, DEPTH * 8)), t5=np.ascontiguousarray(t5_bias),
                  crel=crel.reshape(DEPTH, 3, 128, 8), oh5=oh5, ohc=ohc, cst=cst)
    return common


def _percore(j):
    pc = np.zeros((128, 1024), np.float32)
    p = np.arange(128)
    pc[:, 0:512] = (j * 512 + np.arange(512))[None, :]
    for rr in range(16):
        pc[:, 512 + rr] = rr * 128 + p
    for qb in range(4):
        for ch in range(4):
            pc[:, 528 + qb * 4 + ch] = (8 * j + 2 * qb + (p >= 64) + 1) * 64 - ch * 512
        for rr in range(17):
            pc[:, 544 + (qb * 17 + rr) * 2] = 1.0 if (rr - 1) == 4 * j + qb else 0.0
            pc[:, 544 + (qb * 17 + rr) * 2 + 1] = 1.0 if (rr - 1) == 4 * j + qb - 1 else 0.0
    for r in range(4):
        pc[:, 680 + r] = 1.0 if j == r else 0.0
    pc[:, 684] = -30000.0 if j == 0 else 0.0
    return pc


def _in_maps(x_prompt, x_sample, cache_a_k, cache_a_v, cache_a_logf, cache_b_k, cache_b_v, cache_b_idx_k,
             cache_c_k, cache_c_v, norm_g, w_in, b_f, t5_bias, c_rel_bias, w_out, final_g):
    f = lambda a: np.ascontiguousarray(np.asarray(a, dtype=np.float32))
    x_prompt, norm_g, w_in, b_f, t5_bias, c_rel_bias, w_out, final_g = map(f, (x_prompt, norm_g, w_in, b_f, t5_bias, c_rel_bias, w_out, final_g))
    common = _host_inputs(x_prompt, norm_g, w_in, b_f, t5_bias, c_rel_bias, w_out, final_g)
    common["iota5"] = np.ascontiguousarray(np.broadcast_to(np.arange(512, dtype=np.float32)[None, :], (128, 512)))
    in_maps = []
    for c in range(8):
        b, j = c // 4, c % 4
        m = dict(common)
        xb = x_prompt[b].reshape(NG, 512, D_MODEL)
        m["xp"] = x_prompt[b]
        m["xq"] = np.ascontiguousarray(xb[j::4])
        xpv = np.zeros((4, 512, D_MODEL), np.float32)
        for mm in range(4):
            if 4 * mm + j - 1 >= 0:
                xpv[mm] = xb[4 * mm + j - 1]
        m["xprev"] = xpv
        m["pcore"] = _percore(j)
        m["xs"] = f(x_sample[c])
        m["ca_k"] = f(cache_a_k[:, c]).reshape(DEPTH, PAST, 512); m["ca_v"] = f(cache_a_v[:, c]).reshape(DEPTH, PAST, 512)
        m["ca_lf"] = f(cache_a_logf[:, c]).reshape(DEPTH, PAST, 8)
        m["cb_k"] = f(cache_b_k[:, c]).reshape(DEPTH, PAST, 128); m["cb_v"] = f(cache_b_v[:, c]).reshape(DEPTH, PAST, 128)
        m["cb_ik"] = f(cache_b_idx_k[:, c]).reshape(DEPTH, PAST, 32)
        m["cc_k"] = f(cache_c_k[:, c]).reshape(DEPTH, 512, 512); m["cc_v"] = f(cache_c_v[:, c]).reshape(DEPTH, 512, 512)
        in_maps.append(m)
    return in_maps


def kernel(x_prompt, x_sample, cache_a_k, cache_a_v, cache_a_logf, cache_b_k, cache_b_v, cache_b_idx_k,
           cache_c_k, cache_c_v, norm_g, w_in, b_f, t5_bias, c_rel_bias, w_out, final_g):
    nc, es, P = _get_prog()
    in_maps = _in_maps(x_prompt, x_sample, cache_a_k, cache_a_v, cache_a_logf, cache_b_k, cache_b_v, cache_b_idx_k,
                       cache_c_k, cache_c_v, norm_g, w_in, b_f, t5_bias, c_rel_bias, w_out, final_g)
    res = run_bass_kernel_spmd(nc, in_maps, core_ids=list(range(8)))
    R = res.results
    st = lambda name, shp: np.stack([R[4 * b][name] for b in range(2)], axis=1).reshape(shp)
    y_prompt = np.zeros((BATCH, NG, 512, D_MODEL), np.float32)
    for c in range(8):
        b, j = c // 4, c % 4
        y_prompt[b, j::4] = R[c]["y_q"].reshape(4, 512, D_MODEL)
    y_prompt = y_prompt.reshape(BATCH, SEQ, D_MODEL)
    ss = lambda name, shp: np.stack([R[b][name] for b in range(DEC_BATCH)], axis=1).reshape(shp)
    y_sample = np.stack([R[b]["y_s"] for b in range(DEC_BATCH)], axis=0)
    outs = [y_prompt, y_sample,
            st("o_ak", (DEPTH, BATCH, SEQ, H, HD)), st("o_av", (DEPTH, BATCH, SEQ, H, HD)), st("o_lf", (DEPTH, BATCH, SEQ, H)),
            st("o_bk", (DEPTH, BATCH, SEQ, KVB, HD)), st("o_bv", (DEPTH, BATCH, SEQ, KVB, HD)), st("o_ik", (DEPTH, BATCH, SEQ, IDX_D)),
            st("o_ck", (DEPTH, BATCH, 512, H, HD)), st("o_cv", (DEPTH, BATCH, 512, H, HD)),
            ss("s_ak", (DEPTH, DEC_BATCH, DEC_SEQ, H, HD)), ss("s_av", (DEPTH, DEC_BATCH, DEC_SEQ, H, HD)), ss("s_lf", (DEPTH, DEC_BATCH, DEC_SEQ, H)),
            ss("s_bk", (DEPTH, DEC_BATCH, DEC_SEQ, KVB, HD)), ss("s_bv", (DEPTH, DEC_BATCH, DEC_SEQ, KVB, HD)), ss("s_ik", (DEPTH, DEC_BATCH, DEC_SEQ, IDX_D)),
            ss("s_ck", (DEPTH, DEC_BATCH, DEC_SEQ, H, HD)), ss("s_cv", (DEPTH, DEC_BATCH, DEC_SEQ, H, HD))]
    return tuple(outs)
```

```python
import math
import types
import numpy as np
from contextlib import ExitStack
import concourse.bass as bass
import concourse.mybir as mybir
from concourse.bass_utils import run_bass_kernel_spmd

F32 = mybir.dt.float32
BF16 = mybir.dt.bfloat16
ALU = mybir.AluOpType
AF = mybir.ActivationFunctionType

D_MODEL = 1024; BATCH = 2; SEQ = 8192; DEPTH = 2; DEC_BATCH = 8; DEC_SEQ = 16; PAST = 1024
HD = 64; H = 8; KVB = 2; IDX_H = 8; IDX_D = 32; TOPK = 256
SCALE = HD ** -0.5
IDXS = (IDX_D ** -0.5) * (IDX_H ** -0.5)
EPS = 1e-6
NEG = -32768.0
NG = SEQ // 512
NBIS = 24
LTAB = 384

_SPLIT = (512, 512, 512, 512, 8, 512, 128, 128, 512, 256, 8, 32, 512, 512, 512, 512)
_OFF = np.concatenate([[0], np.cumsum(_SPLIT)])
(QA, KA, VA, ZA, FA, QB, KB, VB, ZB, IQ, IW, IK, QC, KC, VC, ZC) = [int(o) for o in _OFF[:-1]]
FM_UNITS = ["qa", "ka", "za", "qb", "zb", "bx", "qc", "kc", "zc"]
TM_UNITS = ["tka", "tva", "tb", "tkc", "tvc"]
UNITS = FM_UNITS + TM_UNITS
NU = len(UNITS)


def _unit_cols(name):
    r = lambda a, n: list(range(a, a + n))
    pad = lambda l: l + [-1] * (512 - len(l))
    if name == "qa": return r(QA, 512)
    if name == "ka": return r(KA, 512)
    if name == "za": return r(ZA, 512)
    if name == "qb": return r(QB, 512)
    if name == "zb": return r(ZB, 512)
    if name == "qc": return r(QC, 512)
    if name == "kc": return r(KC, 512)
    if name == "zc": return r(ZC, 512)
    if name == "bx": return pad(r(KB, 128) + r(IQ, 256) + r(IK, 32) + r(IK, 32))
    if name == "tka": return r(KA, 512)
    if name == "tva": return r(VA, 512)
    if name == "tkc": return r(KC, 512)
    if name == "tvc": return r(VC, 512)
    if name == "tb": return pad(r(KB, 128) + r(VB, 128) + r(IK, 32) + r(FA, 8) + r(IW, 8))
    raise KeyError(name)


class Prog:
    STREAMS = ("pe", "act", "dve", "pool", "sp")

    def __init__(self, nc):
        self.nc = nc
        self.ops = []
        self.ndma = {}

    @staticmethod
    def _freeze(fn):
        if fn.__closure__ is None:
            return fn
        cells = []
        for c in fn.__closure__:
            try:
                cells.append(types.CellType(c.cell_contents))
            except ValueError:
                cells.append(c)
        return types.FunctionType(fn.__code__, fn.__globals__, fn.__name__, fn.__defaults__, tuple(cells))

    NSUB = 16

    def add(self, stream, fn, r=(), w=(), dma=False, cc=False):
        fn = self._freeze(fn)
        if cc:
            track = "dma_cc"
        elif dma:
            k = self.ndma.get(stream, 0)
            self.ndma[stream] = k + 1
            track = f"dma_{stream}#{k % self.NSUB}"
        else:
            track = stream
        self.ops.append((stream, track, fn, tuple(r), tuple(w)))

    def pe(self, fn, r=(), w=()): self.add("pe", fn, r, w)
    def act(self, fn, r=(), w=()): self.add("act", fn, r, w)
    def dve(self, fn, r=(), w=()): self.add("dve", fn, r, w)
    def pool(self, fn, r=(), w=()): self.add("pool", fn, r, w)

    def dma(self, stream, out, in_, r=(), w=(), **kw):
        self.add(stream, lambda e: e.dma_start(out=out, in_=in_, **kw), r, w, dma=True)

    def finalize_and_emit(self):
        nc = self.nc
        ops = self.ops
        n = len(ops)
        writers = {}
        readers = {}
        prev_on = {}
        deps = [None] * n
        signal = [False] * n
        qof = lambda t: t.split("#")[0]
        for i, (stream, track, fn, R, W) in enumerate(ops):
            d = set()
            is_dma = track.startswith("dma_")
            if is_dma:
                j = prev_on.get(track)
                if j is not None:
                    d.add(j)
                prev_on[track] = i
            for res in R:
                for tj, j in writers.get(res, {}).items():
                    if tj != track or is_dma or track != "pe":
                        d.add(j)
            for res in W:
                lazy = res.startswith("~")
                for tj, j in writers.get(res, {}).items():
                    if lazy:
                        if qof(tj) != qof(track):
                            d.add(j)
                    elif tj != track or is_dma:
                        d.add(j)
                for tj, j in readers.get(res, {}).items():
                    if lazy:
                        if qof(tj) != qof(track):
                            d.add(j)
                    elif tj != track or is_dma:
                        d.add(j)
            for res in R:
                readers.setdefault(res, {})[track] = i
            for res in W:
                if res.startswith("~"):
                    writers.setdefault(res, {})[track] = i
                else:
                    writers[res] = {track: i}
                    readers[res] = {}
            d.discard(i)
            deps[i] = d
            for j in d:
                signal[j] = True
        tracks = sorted({o[1] for o in ops})
        cnt = {t: 0 for t in tracks}
        val = [0] * n
        for i, (stream, track, fn, R, W) in enumerate(ops):
            if track == "dma_cc":
                cnt[track] += 1
                val[i] = cnt[track]
            elif track.startswith("dma_"):
                cnt[track] += 16
                val[i] = cnt[track]
            elif signal[i]:
                cnt[track] += 1
                val[i] = cnt[track]
        known = {s: {t: 0 for t in tracks} for s in self.STREAMS}
        waits = [None] * n
        for i, (stream, track, fn, R, W) in enumerate(ops):
            need = {}
            for j in deps[i]:
                tj = ops[j][1]
                need[tj] = max(need.get(tj, 0), val[j])
            wl = []
            for tj, v in need.items():
                if v > known[stream][tj]:
                    wl.append((tj, v))
                    known[stream][tj] = v
            waits[i] = wl
        self.stats = dict(cnt)
        by_stream = {s: [] for s in self.STREAMS}
        for i, o in enumerate(ops):
            by_stream[o[0]].append(i)
        with ExitStack() as es:
            sems = {t: es.enter_context(nc.semaphore("s_" + t.replace("#", "_"))) for t in tracks}
            block = es.enter_context(nc.Block())

            def run(eng, stream):
                for i in by_stream[stream]:
                    _, track, fn, R, W = ops[i]
                    for tj, v in waits[i]:
                        eng.wait_ge(sems[tj], v)
                    inst = fn(eng)
                    if track == "dma_cc":
                        inst.then_inc(sems[track], 1)
                    elif track.startswith("dma_"):
                        inst.then_inc(sems[track], 16)
                    elif signal[i]:
                        inst.then_inc(sems[track], 1)
                if stream == "sp":
                    for t in tracks:
                        if t.startswith("dma_") and cnt[t] > known[stream][t]:
                            eng.wait_ge(sems[t], cnt[t])

            @block.tensor
            def _(e): run(e, "pe")

            @block.scalar
            def _(e): run(e, "act")

            @block.vector
            def _(e): run(e, "dve")

            @block.gpsimd
            def _(e): run(e, "pool")

            @block.sync
            def _(e): run(e, "sp")


def build_program(do_sample=True, nm=4, nlayers=DEPTH):
    nc = bass.Bass("TRN2", target_bir_lowering=False)
    es = ExitStack()
    din = lambda name, shape, dt=F32: nc.dram_tensor(name, list(shape), dt, kind="ExternalInput").ap()
    dout = lambda name, shape: nc.dram_tensor(name, list(shape), F32, kind="ExternalOutput").ap()
    dscr = lambda name, shape, dt=BF16: nc.dram_tensor(name, list(shape), dt, kind="Internal").ap()
    xp = din("xp", [SEQ, D_MODEL])
    wu = din("wu", [DEPTH, NU, 128, 8, 512])
    wo = din("wo", [DEPTH, 6, 64, 4, 1024])
    gcol_d = din("gcol", [128, DEPTH * 8])
    fg_d = din("fg", [1, D_MODEL])
    bf_d = din("bfb", [1, DEPTH * 8])
    t5_d = din("t5", [32, 8])
    crel_d = din("crel", [DEPTH, 3, 128, 8])
    oh5_d = din("oh5", [32, LTAB])
    ohc_d = din("ohc", [3, 128, LTAB])
    cst_d = din("cst", [128, 8 * 128])
    y_q = dout("y_q", [2048, D_MODEL])
    xq = din("xq", [4, 512, D_MODEL]); xprev = din("xprev", [4, 512, D_MODEL])
    pc_d = din("pcore", [128, 1024])
    iota_d = din("iota5", [128, 512])
    o_ak = dout("o_ak", [DEPTH, SEQ, 512]); o_av = dout("o_av", [DEPTH, SEQ, 512])
    o_lf = dout("o_lf", [DEPTH, SEQ, 8])
    o_bk = dout("o_bk", [DEPTH, SEQ, 128]); o_bv = dout("o_bv", [DEPTH, SEQ, 128])
    o_ik = dout("o_ik", [DEPTH, SEQ, 32])
    o_ck = dout("o_ck", [DEPTH, 512, 512]); o_cv = dout("o_cv", [DEPTH, 512, 512])
    wub = dscr("wub", [DEPTH, NU, 128, 8, 512])
    wob = dscr("wob", [DEPTH, 6, 64, 4, 1024])
    hp1q = dscr("hp1q", [2048, D_MODEL], F32)
    hpg = dscr("hpg", [SEQ, D_MODEL], F32)
    ccs = dscr("ccs", [256, D_MODEL], F32)
    COMBS = dscr("combs", [4, 17, 2, 128, 512])
    AMASK = dscr("amask", [16, 128, 512])
    ccd = dscr("ccd", [1024, D_MODEL], F32)
    S_KCL = dscr("scr_kcl", [DEPTH, 8, 64, 1024]); S_VCL = dscr("scr_vcl", [DEPTH, 1024, 512])
    tab5 = dscr("tab5", [8, LTAB], F32)
    tabc = dscr("tabc", [DEPTH, 8, LTAB], F32)
    S_KA = dscr("scr_s_ka", [DEPTH, 8, 64, SEQ]); S_KC = dscr("scr_s_kc", [DEPTH, 8, 64, SEQ])
    S_KB = dscr("scr_s_kb", [DEPTH, 2, 64, SEQ]); S_IK = dscr("scr_s_ik", [DEPTH, 64, SEQ])
    S_VA = dscr("scr_s_va", [DEPTH, SEQ, 512]); S_VC = dscr("scr_s_vc", [DEPTH, SEQ, 512])
    S_VB = dscr("scr_s_vb", [DEPTH, SEQ, 128])

    xs = din("xs", [DEC_SEQ, D_MODEL])
    ca_k = din("ca_k", [DEPTH, PAST, 512]); ca_v = din("ca_v", [DEPTH, PAST, 512]); ca_lf = din("ca_lf", [DEPTH, PAST, 8])
    cb_k = din("cb_k", [DEPTH, PAST, 128]); cb_v = din("cb_v", [DEPTH, PAST, 128]); cb_ik = din("cb_ik", [DEPTH, PAST, 32])
    cc_k = din("cc_k", [DEPTH, 512, 512]); cc_v = din("cc_v", [DEPTH, 512, 512])
    y_s = dout("y_s", [DEC_SEQ, D_MODEL])
    s_ak = dout("s_ak", [DEPTH, DEC_SEQ, 512]); s_av = dout("s_av", [DEPTH, DEC_SEQ, 512]); s_lf = dout("s_lf", [DEPTH, DEC_SEQ, 8])
    s_bk = dout("s_bk", [DEPTH, DEC_SEQ, 128]); s_bv = dout("s_bv", [DEPTH, DEC_SEQ, 128]); s_ik = dout("s_ik", [DEPTH, DEC_SEQ, 32])
    s_ck = dout("s_ck", [DEPTH, DEC_SEQ, 512]); s_cv = dout("s_cv", [DEPTH, DEC_SEQ, 512])
    hs1 = dscr("hs1", [DEC_SEQ, D_MODEL], F32)
    MBS = [dscr(f"mbs{i}", [128, SEQ]) for i in range(4)]
    SS_KA = dscr("ss_ka", [DEPTH, 8, 64, 1152]); SS_KC = dscr("ss_kc", [DEPTH, 8, 64, 640])
    SS_KB = dscr("ss_kb", [DEPTH, 2, 64, 1152]); SS_IK = dscr("ss_ik", [DEPTH, 64, 1152])
    SS_VA = dscr("ss_va", [DEPTH, 1152, 512]); SS_VC = dscr("ss_vc", [DEPTH, 640, 512]); SS_VB = dscr("ss_vb", [DEPTH, 1152, 128])

    sb = lambda name, shape, dt: es.enter_context(nc.sbuf_tensor(name, list(shape), dt))
    wring = [sb(f"wring{i}", [128, 8, 512], BF16) for i in range(2)]
    SC = sb("SC", [128, 8192], F32)
    junk = sb("junk", [128, 8192], BF16)
    hT = sb("hT", [128, 8, 512], BF16)
    Q = sb("Q", [65, 8, 512], BF16)
    zg = {t: sb("zg" + t, [64, 8, 512], BF16) for t in "abc"}
    iqT = sb("iqT", [64, 4, 512], BF16)
    kst = sb("kst", [64, 8, 512], BF16)
    xt = [sb(f"xt{i}", [128, 1024], F32) for i in range(2)]
    xn = sb("xn", [128, 1024], BF16)
    st = [sb(f"st{i}", [128, 512], F32) for i in range(2)]
    vst = [sb(f"vst{i}", [128, 512], BF16) for i in range(2)]
    kbuf = [sb(f"kbuf{i}", [65, 4, 512], BF16) for i in range(2)]
    vbuf = [sb(f"vbuf{i}", [128, 4, 4, 65], BF16) for i in range(2)]
    Pt = [sb(f"Pt{i}", [128, 512], BF16) for i in range(4)]
    Mb = [sb(f"Mb{i}", [128, 512], BF16) for i in range(2)]
    Rr = [sb(f"Rr{i}", [128, 512], F32) for i in range(3)]
    ikbuf = [sb(f"ikbuf{i}", [64, 512], BF16) for i in range(2)]
    b5 = sb("b5", [128, 2, 8, 128], BF16)
    bc = sb("bc", [128, 2, 8, 128], BF16)
    cstf = sb("cstf", [128, 8, 128], F32)
    identb = sb("identb", [128, 128], BF16)
    i4b = sb("i4b", [128, 4, 128], BF16)
    ma0b = sb("ma0b", [128, 128], BF16); cm0b = sb("cm0b", [128, 128], BF16); cm4b = sb("cm4b", [128, 128], BF16)
    cstore = sb("cstore", [128, 64, 8], F32)
    nbias = sb("nbias", [128, 64, 8], F32)
    gcol = sb("gcol_s", [128, DEPTH * 8], F32)
    fgb = sb("fgb", [128, D_MODEL], F32)
    bfb = sb("bfb_s", [128, DEPTH * 8], F32)
    small = sb("small", [128, 64], F32)
    cntb = sb("cntb", [128, NBIS], F32)
    wabs = sb("wabs", [128, 4, 8], F32); wsgn = sb("wsgn", [128, 4, 8], F32)
    lfb = sb("lfb", [128, 8], F32)
    tot = sb("tot", [1, 8], F32)
    tots = sb("tots", [1, 17, 8], F32)
    totbc = sb("totbc", [128, 8], F32)
    lf4 = sb("lf4", [128, 4, 8], F32)
    cown = sb("cown", [128, 4, 8], F32)
    xacc = sb("xacc", [128, D_MODEL], F32)
    pcore = sb("pcore_s", [128, 1024], F32)
    iota5 = sb("iota5_s", [128, 512], F32)
    comb = [sb(f"comb{i}", [128, 4, 128], BF16) for i in range(2)]
    ones1 = sb("ones1", [65, 128], F32)
    cbc = sb("cbc", [128, 8], F32)
    rq = sb("rq", [128, 4, 8], F32)
    rT = sb("rT", [8, 512], BF16)
    rden = sb("rden", [65, 512], F32)
    otmp = sb("otmp", [64, 512], F32)
    hank = sb("hank", [128, 128], F32)
    t5s = sb("t5s", [32, 8], F32); oh5s = sb("oh5s", [32, LTAB], F32)
    crs = sb("crs", [128, 3, 8], F32); ohcs = sb("ohcs", [128, 3, LTAB], F32)
    tabs = sb("tabs", [8, LTAB], F32)
    ps = [es.enter_context(nc.psum_tensor(f"ps{i}", [128, 512], F32)) for i in range(8)]
    psn = [f"ps{i}" for i in range(8)]

    P = Prog(nc)
    _early = {}

    def nxt_early(key, n):
        v = _early.get(key, 0)
        _early[key] = v + 1
        return v % n

    IDENT = cstf[:, 0, :]; JM = cstf[:, 1, :]; TRI = cstf[:, 2, :]; E0ROW = cstf[:, 3, :]
    ADM = cstf[:, 7, :]
    E127 = cstf[:, 1, 0:1]

    P.dma("sp", pcore[:], pc_d, w=["qrelb", "krel", "qlimc", "sel01", "selb", "pvb"])
    P.dma("sp", iota5[:], iota_d, w=["iota5"])
    qrelb = pcore[:, 0:512]; krel = pcore[:, 512:528]; qlimc = pcore[:, 528:544]; sel01 = pcore[:, 544:680]
    selb = pcore[:, 680:684]; pvb = pcore[:, 684:685]
    P.dma("sp", cstf[:].rearrange("p a b -> p (a b)"), cst_d, w=["cstf"])
    P.dma("sp", gcol[:], gcol_d, w=["gcol"])
    P.dma("sp", fgb[:], fg_d.to_broadcast([128, D_MODEL]) if hasattr(fg_d, "to_broadcast") else bass.AP(fg_d.tensor, 0, [[0, 128], [1, D_MODEL]]), w=["fgb"])
    P.dma("sp", bfb[:], bass.AP(bf_d.tensor, 0, [[0, 128], [1, DEPTH * 8]]), w=["bfb"])
    P.dma("sp", t5s[:], t5_d, w=["t5s"])
    P.dma("sp", oh5s[:], oh5_d, w=["oh5s"])
    P.dma("sp", ohcs[:], ohc_d.rearrange("c p l -> p c l"), w=["ohcs"])
    P.dve(lambda e: e.tensor_copy(identb[:], IDENT), r=["cstf"], w=["identb"])
    for k in range(4):
        P.dve(lambda e, k=k: e.tensor_copy(i4b[:, k, :], IDENT), r=["cstf"], w=["i4b"])
    P.dve(lambda e: e.tensor_copy(ma0b[:], cstf[:, 4, :]), r=["cstf"], w=["ma0b"])
    P.dve(lambda e: e.tensor_copy(cm0b[:], cstf[:, 5, :]), r=["cstf"], w=["cm0b"])
    P.dve(lambda e: e.tensor_copy(cm4b[:], cstf[:, 6, :]), r=["cstf"], w=["cm4b"])
    onesf = cstf[:, 4, :]
    P.dve(lambda e: e.memset(onesf, 1.0), r=["ma0b"], w=["cstf", "onesf"])
    P.dve(lambda e: e.memset(ones1[:], 1.0), w=["ones1"])
    for i in range(2):
        P.pool(lambda e, i=i: e.memset(kbuf[i][:], 1.0), w=[f"kbuf{i}"])
        P.pool(lambda e, i=i: e.memset(vbuf[i][:], 1.0), w=[f"vbuf{i}"])

    stg = SC[:, 0:4096].rearrange("p (c n) -> p c n", c=8)
    stgb = junk[:, 0:4096].rearrange("p (c n) -> p c n", c=8)
    for l in range(nlayers):
        for u in range(NU):
            P.dma("sp", stg, wu[l, u], w=["SC"])
            P.act(lambda e: e.activation(stgb, stg, AF.Copy), r=["SC"], w=["junk"])
            P.dma("sp", wub[l, u], stgb, r=["junk"], w=[f"wub{l}_{u}"])
        for u in range(6):
            so = SC[0:64, 0:4096].rearrange("p (c n) -> p c n", c=4)
            sob = junk[0:64, 0:4096].rearrange("p (c n) -> p c n", c=4)
            P.dma("sp", so, wo[l, u], w=["SC"])
            P.act(lambda e, so=so, sob=sob: e.activation(sob, so, AF.Copy), r=["SC"], w=["junk"])
            P.dma("sp", wob[l, u], sob, r=["junk"], w=[f"wob{l}_{u}"])

    def build_tab(lhs_list, rhs_list, dst, rnames):
        for i, (a, b) in enumerate(zip(lhs_list, rhs_list)):
            P.pe(lambda e, a=a, b=b, i=i: e.matmul(ps[0][0:8, 0:LTAB], a, b, start=(i == 0), stop=(i == len(lhs_list) - 1)),
                 r=rnames, w=["ps0"])
        P.dve(lambda e: e.tensor_copy(tabs[:], ps[0][0:8, 0:LTAB]), r=["ps0"], w=["tabs"])
        P.dma("sp", dst, tabs[:], r=["tabs"], w=["tabdram"])

    def build_toeplitz(tab_ap2d, dst_tile):
        for k in range(2):
            for h in range(8):
                b0 = 128 * k
                src = bass.AP(tab_ap2d.tensor, tab_ap2d.offset + h * LTAB + b0, [[1, 128], [1, 128]])
                P.dma("sp", hank[:], src, r=["tabdram"], w=["hank"])
                P.pe(lambda e: e.matmul(ps[1][:, 0:128], JM, hank[:], start=True, stop=True), r=["hank", "cstf"], w=["ps1"])
                P.dve(lambda e, k=k, h=h: e.tensor_copy(dst_tile[:, k, h, :], ps[1][:, 0:128]), r=["ps1"], w=["btile"])

    build_tab([t5s[:]], [oh5s[:]], tab5, ["t5s", "oh5s"])
    build_toeplitz(tab5, b5)
    for qb in range(4):
        for rr in range(17):
            if (rr - 1 - qb) % 4 not in (0, 3):
                continue
            for jj in range(2):
                ci = nxt_early("cmb", 2)
                sc0 = sel01[:, (qb * 17 + rr) * 2:(qb * 17 + rr) * 2 + 1]
                sc1 = sel01[:, (qb * 17 + rr) * 2 + 1:(qb * 17 + rr) * 2 + 2]
                P.dve(lambda e: e.tensor_scalar(comb[ci][:, :, :], b5[:, 0, 4 * jj:4 * jj + 4, :], sc0, None, ALU.mult), r=["btile", "sel01"], w=[f"comb{ci}"])
                P.dve(lambda e: e.scalar_tensor_tensor(comb[ci][:, :, :], b5[:, 1, 4 * jj:4 * jj + 4, :], sc1, comb[ci][:, :, :], ALU.mult, ALU.add),
                      r=["btile", "sel01", f"comb{ci}"], w=[f"comb{ci}"])
                P.dma("pool", COMBS[qb, rr, jj], comb[ci][:].rearrange("p a b -> p (a b)"), r=[f"comb{ci}"], w=["~combs"])
    for rr in range(16):
        mi = nxt_early("mb", 2)
        P.dve(lambda e: e.tensor_scalar(Mb[mi][:, :], qrelb[:, :], krel[:, rr:rr + 1], NEG, ALU.is_lt, ALU.mult), r=["qrelb", "krel"], w=[f"Mb{mi}"])
        P.dma("pool", AMASK[rr], Mb[mi][:, :], r=[f"Mb{mi}"], w=["~amask"])

    wk = [0]

    def load_w(l, u):
        i = wk[0] % 2
        wk[0] += 1
        P.dma("sp", wring[i][:], wub[l, u], r=[f"wub{l}_{u}"], w=[f"wring{i}"])
        return wring[i], f"wring{i}"

    def load_wo(l, u):
        i = wk[0] % 2
        wk[0] += 1
        dst = wring[i][0:64].rearrange("p c n -> p (c n)").rearrange("p (c n) -> p c n", c=4)
        P.dma("sp", dst, wob[l, u], r=[f"wob{l}_{u}"], w=[f"wring{i}"])
        return dst, f"wring{i}"

    rot = {"s": 0, "pt": 0, "kv": 0, "st": 0, "x": 0, "ik": 0, "rr": 0, "mb": 0, "ips": 0, "cmb": 0}

    def nxt(key, n):
        v = rot[key] % n
        rot[key] += 1
        return v

    def norm_block(l, xsrc_ap, tb, nrow=128, rname=None, sb_src=None):
        if sb_src is not None:
            X, xname = sb_src
        else:
            xi = nxt("x", 2)
            X = xt[xi]; xname = f"xt{xi}"
        if sb_src is None:
            P.dma("sp", X[0:nrow, :], xsrc_ap, r=(list(rname) if isinstance(rname, (list, tuple)) else ([rname] if rname else [])), w=[xname])
        P.act(lambda e: e.activation(junk[0:nrow, 0:1024], X[0:nrow, :], AF.Square, accum_out=small[0:nrow, 0:1]),
              r=[xname], w=["junk", "small0"])
        P.dve(lambda e: e.tensor_scalar(small[0:nrow, 1:2], small[0:nrow, 0:1], 1.0 / D_MODEL, EPS, ALU.mult, ALU.add), r=["small0"], w=["small1"])
        P.act(lambda e: e.activation(small[0:nrow, 2:3], small[0:nrow, 1:2], AF.Sqrt), r=["small1"], w=["small2"])
        P.dve(lambda e: e.reciprocal(small[0:nrow, 3:4], small[0:nrow, 2:3]), r=["small2"], w=["small3"])
        P.dve(lambda e: e.tensor_scalar(xn[0:nrow, :], X[0:nrow, :], small[0:nrow, 3:4], None, ALU.mult), r=[xname, "small3"], w=["xn"])
        psb = ps[7].bitcast(BF16)
        for c in range(8):
            P.pe(lambda e, c=c: e.transpose(psb[:, c * 128:c * 128 + nrow], xn[0:nrow, c * 128:(c + 1) * 128], identb[0:nrow, 0:nrow]),
                 r=["xn", "identb"], w=["ps7"])
        for c in range(8):
            P.dve(lambda e, c=c: e.tensor_scalar(hT[:, c, tb * 128:tb * 128 + nrow], psb[:, c * 128:c * 128 + nrow],
                                                 gcol[:, l * 8 + c:l * 8 + c + 1], None, ALU.mult),
                  r=["ps7", "gcol"], w=["hT"])

    def fm_unit(l, uname, ntok, evac, blocks=tuple(range(8)), wres=None):
        W, wn = wres if wres is not None else load_w(l, UNITS.index(uname))
        for j in blocks:
            si = nxt("s", 4)
            for c in range(8):
                P.pe(lambda e, j=j, c=c, si=si: e.matmul(ps[si][0:64, 0:ntok], W[:, c, j * 64:(j + 1) * 64], hT[:, c, 0:ntok],
                                                        start=(c == 0), stop=(c == 7)), r=[wn, "hT"], w=[psn[si]])
            evac(j, ps[si], psn[si])

    def finish_head(O, oname, zt, zname, hsel, ncol, csl):
        if isinstance(hsel, tuple):
            nh_ = hsel[1] - hsel[0]
            zv = zt[:, hsel[0]:hsel[1], csl]
            ov = otmp[:, 0:ncol].rearrange("p (h q) -> p h q", h=nh_)
            Ov = O[0:64, 0:ncol].rearrange("p (h q) -> p h q", h=nh_)
        else:
            zv = zt[:, hsel, csl]
            ov = otmp[:, 0:ncol]
            Ov = O[0:64, 0:ncol]
        P.act(lambda e: e.activation(rden[64:65, 0:ncol], O[64:65, 0:ncol], AF.Ln), r=[oname], w=["rden"])
        P.act(lambda e: e.activation(rden[64:65, 0:ncol], rden[64:65, 0:ncol], AF.Exp, scale=-1.0), r=["rden"], w=["rden"])
        P.dve(lambda e: e.tensor_tensor(ov, Ov, zv, ALU.mult), r=[oname, zname], w=["otmp"])
        bi_ = nxt("s", 4)
        P.pe(lambda e: e.matmul(ps[bi_][0:64, 0:ncol], ones1[64:65, 0:64], rden[64:65, 0:ncol], start=True, stop=True),
             r=["rden", "ones1"], w=[psn[bi_]])
        bv = ps[bi_][0:64, 0:ncol].rearrange("p (h q) -> p h q", h=nh_) if isinstance(hsel, tuple) else ps[bi_][0:64, 0:ncol]
        P.dve(lambda e: e.tensor_tensor(zv, ov, bv, ALU.mult), r=["otmp", psn[bi_]], w=[zname])

    def load_kv(Ksrc, Vsrc, h0, nh, k0, nk, krows_name):
        i = nxt("kv", 2)
        P.dma("sp", kbuf[i][0:64, 0:nh, 0:nk], Ksrc[h0:h0 + nh, :, k0:k0 + nk].rearrange("h d k -> d h k"), r=[krows_name], w=[f"kbuf{i}"])
        nb = (nk + 127) // 128
        for b in range(nb):
            n = min(128, nk - b * 128)
            P.dma("sp", vbuf[i][0:n, b, 0:nh, 0:64],
                  Vsrc[k0 + b * 128:k0 + b * 128 + n, h0 * 64:(h0 + nh) * 64].rearrange("k (h d) -> k h d", h=nh),
                  r=[krows_name], w=[f"vbuf{i}"])
        return i

    class Attn:
        def __init__(self):
            self.pend = []

        def tile(self, t):
            si = nxt("s", 4)
            S = ps[si]; n = t["n"]; qlo, qhi = t["qlo"], t["qhi"]
            nadd = len(t["adds"])
            P.pe(lambda e: e.matmul(S[0:n, qlo:qhi], t["kT"], t["qap"], start=True, stop=(nadd == 0)),
                 r=t["names"] + ["Q"], w=[psn[si]])
            for ai, (clo, chi, la, ra, an) in enumerate(t["adds"]):
                P.pe(lambda e, clo=clo, chi=chi, la=la, ra=ra, ai=ai: e.matmul(S[0:n, clo:chi], la, ra, start=False, stop=(ai == nadd - 1)),
                     r=an, w=[psn[si]])
            pi = nxt("pt", 4)
            if t["bias"] is not None:
                P.act(lambda e: e.activation(Pt[pi][0:n, qlo:qhi], S[0:n, qlo:qhi], AF.Exp, bias=t["bias"]),
                      r=[psn[si], "nbias"], w=[f"Pt{pi}"])
            else:
                P.act(lambda e: e.activation(Pt[pi][0:n, qlo:qhi], S[0:n, qlo:qhi], AF.Exp), r=[psn[si]], w=[f"Pt{pi}"])
            t["pi"] = pi
            self.pend.append(t)
            if len(self.pend) > 2:
                self.pv(self.pend.pop(0))

        def pv(self, t):
            n = t["n"]; qlo, qhi = t["qlo"], t["qhi"]; pi = t["pi"]; O = t["O"]
            P.pe(lambda e: e.matmul(O[0:65, qlo:qhi], t["v"], Pt[pi][0:n, qlo:qhi], start=t["first"], stop=t["last"]),
                 r=[f"Pt{pi}"] + t["names"], w=[t["oname"]])

        def flush(self):
            while self.pend:
                self.pv(self.pend.pop(0))


    for l in range(nlayers):
        P.pool(lambda e: e.memset(crs[:], 0.0), w=["crs"])
        P.dma("sp", crs[:], crel_d[l].rearrange("c p h -> p c h"), w=["crs"])
        build_tab([crs[:, c, :] for c in range(3)], [ohcs[:, c, :] for c in range(3)], tabc[l], ["crs", "ohcs"])
        build_toeplitz(tabc[l], bc)
        P.dve(lambda e: e.memset(tot[:], 0.0), w=["tot"])
        P.dve(lambda e: e.memset(tots[:], 0.0), w=["tots"])
        P.dve(lambda e: e.memset(totbc[:], 0.0), w=["totbc"])
        KAl, KBl, IKl, VAl, VBl = S_KA[l], S_KB[l], S_IK[l], S_VA[l], S_VB[l]
        KCl, VCl = S_KCL[l], S_VCL[l]
        hist = f"~hist{l}"
        chist = f"~chist{l}"
        deferred_exchange = []

        def grow(gp):
            if l == 0:
                return xp[gp * 512:(gp + 1) * 512, :]
            return hpg[gp * 512:(gp + 1) * 512, :]

        def tm_unit(uname, handler, wres=None):
            W, wn = wres if wres is not None else load_w(l, UNITS.index(uname))
            for tb in range(4):
                si = nxt("s", 4)
                for c in range(8):
                    P.pe(lambda e: e.matmul(ps[si][:, :], hT[:, c, tb * 128:(tb + 1) * 128], W[:, c, :], start=(c == 0), stop=(c == 7)),
                         r=[wn, "hT"], w=[psn[si]])
                k = nxt("st", 2)
                S_ = st[k]; sn = f"st{k}"
                P.act(lambda e: e.activation(S_[:], ps[si][:], AF.Copy), r=[psn[si]], w=[sn])
                handler(tb, S_, sn, k)

        def logf_of(S_, sn):
            P.dve(lambda e: e.tensor_tensor(lfb[:], S_[:, 288:296], bfb[:, l * 8:(l + 1) * 8], ALU.add), r=[sn, "bfb"], w=["lfb"])
            P.act(lambda e: e.activation(lfb[:], lfb[:], AF.Exp, scale=-1.0), r=["lfb"], w=["lfb"])
            P.act(lambda e: e.activation(lfb[:], lfb[:], AF.Ln, bias=1.0), r=["lfb"], w=["lfb"])
            P.dve(lambda e: e.tensor_scalar(lfb[:], lfb[:], -1.0, None, ALU.mult), r=["lfb"], w=["lfb"])

        def cum_into(dst_ap, dname):
            P.pe(lambda e: e.matmul(ps[5][:, 0:8], TRI, lfb[:], start=True, stop=False), r=["lfb", "cstf"], w=["ps5"])
            P.pe(lambda e: e.matmul(ps[5][:, 0:8], ones1[0:1, 0:128], tot[0:1, :], start=False, stop=True), r=["tot", "ones1"], w=["ps5"])
            P.dve(lambda e: e.tensor_copy(dst_ap, ps[5][:, 0:8]), r=["ps5"], w=[dname])
            P.pe(lambda e: e.matmul(ps[5][0:1, 8:16], E127, dst_ap, start=True, stop=True), r=[dname, "cstf"], w=["ps5"])
            P.dve(lambda e: e.tensor_copy(tot[:], ps[5][0:1, 8:16]), r=["ps5"], w=["tot"])

        def evac_k_to(dst3, k0, hname):
            def f(j, pt, pn):
                P.act(lambda e: e.activation(kst[:, j, :], pt[0:64, :], AF.Copy), r=[pn], w=["kst"])
                if j == 7:
                    P.dma("pool", dst3[:, :, k0:k0 + 512].rearrange("h d k -> d h k"), kst[:], r=["kst"], w=[hname])
            return f

        def evac_q(scale):
            def f(j, pt, pn):
                P.act(lambda e: e.activation(Q[0:64, j, :], pt[0:64, :], AF.Copy, scale=scale), r=[pn], w=["Q"])
            return f

        def evac_z(zt, zn):
            def f(j, pt, pn):
                P.act(lambda e: e.activation(zt[:, j, :], pt[0:64, :], AF.Silu), r=[pn], w=[zn])
            return f

        scb = SC.bitcast(BF16)
        kres = {}
        for ui, un in enumerate(("tka", "tva", "tb", "ka")):
            v = scb[:, ui * 4096:(ui + 1) * 4096].rearrange("p (c n) -> p c n", c=8)
            P.dma("sp", v, wub[l, UNITS.index(un)], r=[f"wub{l}_{UNITS.index(un)}"], w=["SC"])
            kres[un] = (v, "SC")
        v = junk[:, 4096:8192].rearrange("p (c n) -> p c n", c=8)
        P.dma("sp", v, wub[l, UNITS.index("bx")], r=[f"wub{l}_{UNITS.index('bx')}"], w=["junk", "junkW"])
        kres["bx"] = (v, "junkW")
        for gp in range(4 * nm):
            t0 = gp * 512
            src = grow(gp)
            for tb in range(4):
                norm_block(l, src[tb * 128:(tb + 1) * 128, :], tb, rname=("~hpgw" if l > 0 else None))

            def h_kv(uname):
                def f(tb, S_, sn, k):
                    r0 = t0 + tb * 128
                    dst = {"tka": o_ak, "tva": o_av, "tkc": o_ck, "tvc": o_cv}[uname]
                    if uname in ("tka", "tva"):
                        P.dma("pool", dst[l, r0:r0 + 128, :], S_[:], r=[sn])
                    else:
                        P.dma("pool", dst[l, r0 - (SEQ - 512):r0 - (SEQ - 512) + 128, :], S_[:], r=[sn])
                    if uname == "tva":
                        V_, vn = vst[k], f"vst{k}"
                        P.dve(lambda e: e.tensor_copy(V_[:], S_[:]), r=[sn], w=[vn])
                        P.dma("pool", VAl[r0:r0 + 128, :], V_[:], r=[vn], w=[hist])
                return f

            def h_tb(tb, S_, sn, k):
                r0 = t0 + tb * 128
                P.dma("pool", o_bk[l, r0:r0 + 128, :], S_[:, 0:128], r=[sn])
                P.dma("pool", o_bv[l, r0:r0 + 128, :], S_[:, 128:256], r=[sn])
                P.dma("pool", o_ik[l, r0:r0 + 128, :], S_[:, 256:288], r=[sn])
                V_, vn = vst[k], f"vst{k}"
                P.dve(lambda e: e.tensor_copy(V_[:, 0:128], S_[:, 128:256]), r=[sn], w=[vn])
                P.dma("pool", VBl[r0:r0 + 128, :], V_[:, 0:128], r=[vn], w=[hist])
                P.dve(lambda e: e.tensor_tensor(lf4[:, tb, :], S_[:, 288:296], bfb[:, l * 8:(l + 1) * 8], ALU.add), r=[sn, "bfb"], w=["lf4"])

            tm_unit("tka", h_kv("tka"), wres=kres["tka"])
            tm_unit("tva", h_kv("tva"), wres=kres["tva"])
            tm_unit("tb", h_tb, wres=kres["tb"])
            lf4f = lf4[:].rearrange("p b h -> p (b h)")
            P.act(lambda e: e.activation(lf4f, lf4f, AF.Exp, scale=-1.0), r=["lf4"], w=["lf4"])
            P.act(lambda e: e.activation(lf4f, lf4f, AF.Ln, bias=1.0), r=["lf4"], w=["lf4"])
            P.dve(lambda e: e.tensor_scalar(lf4f, lf4f, -1.0, None, ALU.mult), r=["lf4"], w=["lf4"])
            P.dma("pool", o_lf[l, t0:t0 + 512, :].rearrange("(b p) h -> p b h", p=128), lf4[:], r=["lf4"])
            if gp == NG - 1:
                tm_unit("tkc", h_kv("tkc"))
                tm_unit("tvc", h_kv("tvc"))
            fm_unit(l, "ka", 512, evac_k_to(KAl, t0, hist), wres=kres["ka"])
            for b_ in range(4):
                for b2 in range(b_ + 1):
                    P.pe(lambda e: e.matmul(ps[5][:, b_ * 8:(b_ + 1) * 8], (TRI if b2 == b_ else onesf[:, :]), lf4[:, b2, :], start=(b2 == 0), stop=(b2 == b_)),
                         r=["lf4", "cstf", "onesf"], w=["ps5"])
            for b2 in range(4):
                P.pe(lambda e: e.matmul(ps[5][:, 32:40], onesf[:, :], lf4[:, b2, :], start=(b2 == 0), stop=(b2 == 3)), r=["lf4", "onesf"], w=["ps5"])
            for b_ in range(4):
                P.dve(lambda e: e.tensor_tensor(cstore[:, 4 * gp + b_, :], ps[5][:, b_ * 8:(b_ + 1) * 8], totbc[:, :], ALU.add), r=["ps5", "totbc"], w=["cstore"])
            P.dve(lambda e: e.tensor_tensor(totbc[:, :], ps[5][:, 32:40], totbc[:, :], ALU.add), r=["ps5", "totbc"], w=["totbc"])
            P.dve(lambda e: e.tensor_copy(tots[0:1, gp + 1, :], totbc[0:1, :]), r=["totbc"], w=["tots"])

            def evac_bx_k(j, pt, pn):
                if j < 2:
                    P.act(lambda e: e.activation(kst[:, j, :], pt[0:64, :], AF.Copy), r=[pn], w=["kst"])
                    if j == 1:
                        P.dma("pool", KBl[:, :, t0:t0 + 512].rearrange("h d k -> d h k"), kst[:, 0:2, :], r=["kst"], w=[hist])
                elif j == 6:
                    P.act(lambda e: e.activation(kst[:, 2, :], pt[0:64, :], AF.Copy), r=[pn], w=["kst"])
                    P.dma("pool", IKl[:, t0:t0 + 512], kst[:, 2, :], r=["kst"], w=[hist])
            fm_unit(l, "bx", 512, evac_bx_k, blocks=(0, 1, 6), wres=kres["bx"])

        for m in range(nm):
            own = (xq[m] if l == 0 else hp1q[m * 512:(m + 1) * 512, :])
            own_r = (None if l == 0 else [f"hp1qc{2 * m}", f"hp1qc{2 * m + 1}"])
            for part in range(2):
                for tb in range(4):
                    if part == 1:
                        norm_block(l, own[tb * 128:(tb + 1) * 128, :], tb, rname=own_r)
                    elif l == 0:
                        norm_block(l, xprev[m][tb * 128:(tb + 1) * 128, :], tb)
                    else:
                        first = True
                        for r in range(4):
                            gq = 4 * m - 1 + r
                            if gq < 0:
                                continue
                            xi = nxt("x", 2)
                            X = xt[xi]; xname = f"xt{xi}"
                            P.dma("sp", X[:], grow(gq)[tb * 128:(tb + 1) * 128, :], r=["~hpgw"], w=[xname])
                            if first:
                                P.dve(lambda e: e.tensor_scalar(xacc[:], X[:], selb[:, r:r + 1], None, ALU.mult), r=[xname, "selb"], w=["xacc"])
                            else:
                                P.dve(lambda e: e.scalar_tensor_tensor(xacc[:], X[:], selb[:, r:r + 1], xacc[:], ALU.mult, ALU.add), r=[xname, "selb", "xacc"], w=["xacc"])
                            first = False
                        norm_block(l, None, tb, sb_src=(xacc, "xacc"))

                def h_c(uname):
                    def f(tb, S_, sn, k):
                        if uname == "tvc":
                            V_, vn = vst[k], f"vst{k}"
                            P.dve(lambda e: e.tensor_copy(V_[:], S_[:]), r=[sn], w=[vn])
                            P.dma("pool", VCl[part * 512 + tb * 128:part * 512 + (tb + 1) * 128, :], V_[:], r=[vn], w=[chist])
                    return f
                tm_unit("tvc", h_c("tvc"))
                fm_unit(l, "kc", 512, evac_k_to(KCl, part * 512, chist))
            g0 = 16 * m
            nkb = 16 * m + 16
            for r in range(4):
                if r == 0:
                    P.dve(lambda e: e.tensor_scalar(tot[0:1, :], tots[0:1, 4 * m + r, :], selb[0:1, r:r + 1], None, ALU.mult), r=["tots", "selb"], w=["tot"])
                else:
                    P.dve(lambda e: e.scalar_tensor_tensor(tot[0:1, :], tots[0:1, 4 * m + r, :], selb[0:1, r:r + 1], tot[0:1, :], ALU.mult, ALU.add),
                          r=["tots", "selb", "tot"], w=["tot"])

            def h_own(tb, S_, sn, k):
                P.dve(lambda e: e.tensor_tensor(lf4[:, tb, :], S_[:, 288:296], bfb[:, l * 8:(l + 1) * 8], ALU.add), r=[sn, "bfb"], w=["lf4"])
                P.dve(lambda e: e.tensor_scalar(wsgn[:, tb, :], S_[:, 296:304], 0.0, 2.0, ALU.is_ge, ALU.mult), r=[sn], w=["wsgn"])
                P.dve(lambda e: e.tensor_scalar(wsgn[:, tb, :], wsgn[:, tb, :], -1.0, None, ALU.add), r=["wsgn"], w=["wsgn"])
                P.dve(lambda e: e.scalar_tensor_tensor(wabs[:, tb, :], S_[:, 296:304], IDXS, wsgn[:, tb, :], ALU.mult, ALU.mult), r=[sn, "wsgn"], w=["wabs"])
            tm_unit("tb", h_own)
            lf4q = lf4[:].rearrange("p b h -> p (b h)")
            P.act(lambda e: e.activation(lf4q, lf4q, AF.Exp, scale=-1.0), r=["lf4"], w=["lf4"])
            P.act(lambda e: e.activation(lf4q, lf4q, AF.Ln, bias=1.0), r=["lf4"], w=["lf4"])
            P.dve(lambda e: e.tensor_scalar(lf4q, lf4q, -1.0, None, ALU.mult), r=["lf4"], w=["lf4"])
            for b_ in range(4):
                P.pe(lambda e: e.matmul(ps[5][:, b_ * 8:(b_ + 1) * 8], ones1[0:1, 0:128], tot[0:1, :], start=True, stop=False), r=["tot", "ones1"], w=["ps5"])
                for b2 in range(b_ + 1):
                    P.pe(lambda e: e.matmul(ps[5][:, b_ * 8:(b_ + 1) * 8], (TRI if b2 == b_ else onesf[:, :]), lf4[:, b2, :], start=False, stop=(b2 == b_)),
                         r=["lf4", "cstf", "onesf"], w=["ps5"])
            P.dve(lambda e: e.tensor_copy(cown[:].rearrange("p b h -> p (b h)"), ps[5][:, 0:32]), r=["ps5"], w=["cown"])
            P.pe(lambda e: e.matmul(ps[5][:, 16:24], E0ROW, cstore[:, g0, :], start=True, stop=True), r=["cstore", "cstf"], w=["ps5"])
            P.dve(lambda e: e.tensor_copy(cbc[:], ps[5][:, 16:24]), r=["ps5"], w=["cbc"])
            for h in range(8):
                P.dve(lambda e: e.tensor_scalar(nbias[:, 0:nkb, h], cstore[:, 0:nkb, h], cbc[:, h:h + 1], -1.0, ALU.subtract, ALU.mult),
                      r=["cstore", "cbc"], w=["nbias"])
            for tb in range(4):
                P.dve(lambda e: e.tensor_tensor(rq[:, tb, :], cown[:, tb, :], cbc[:], ALU.subtract), r=["cown", "cbc"], w=["rq"])
                P.pe(lambda e: e.transpose(ps[6][0:8, tb * 128:(tb + 1) * 128], rq[:, tb, :], IDENT), r=["rq", "cstf"], w=["ps6"])
            P.dve(lambda e: e.tensor_copy(rT[:], ps[6][0:8, :]), r=["ps6"], w=["rT"])

            def evac_bx_q(j, pt, pn):
                P.act(lambda e: e.activation(iqT[:, j - 2, :], pt[0:64, :], AF.Copy), r=[pn], w=["iqT"])

            def a_half(half):
                at = Attn()
                for sbk in range(4 * m + 4):
                    bi = load_kv(KAl, VAl, half * 4, 4, sbk * 512, 512, hist)
                    trail = (sbk >= 4 * m)
                    for kb in range(4):
                        adds = []
                        if trail:
                            rr = (sbk - 4 * m) * 4 + kb
                            mi = nxt("mb", 2)
                            P.dma("sp", Mb[mi][:, :], AMASK[rr], r=["~amask"], w=[f"Mb{mi}"])
                            adds = [(0, 512, identb[:], Mb[mi][:, :], ["identb", f"Mb{mi}"])]
                        for i in range(4):
                            hh = half * 4 + i
                            at.tile(dict(kT=kbuf[bi][0:65, i, kb * 128:(kb + 1) * 128], v=vbuf[bi][:, kb, i, 0:65], n=128, qlo=0, qhi=512,
                                         qap=Q[0:65, hh, 0:512], adds=adds, bias=nbias[:, 4 * sbk + kb, hh:hh + 1],
                                         names=[f"kbuf{bi}", f"vbuf{bi}"], O=ps[4 + i], oname=psn[4 + i],
                                         first=(sbk == 0 and kb == 0), last=(sbk == 4 * m + 3 and kb == 3)))
                at.flush()
                for i in range(4):
                    finish_head(ps[4 + i], psn[4 + i], zg["a"], "zga", half * 4 + i, 512, slice(0, 512))

            def c_half(half):
                at = Attn()
                started = [False] * 4
                for sbl in range(2):
                    bi = load_kv(KCl, VCl, half * 4, 4, sbl * 512, 512, chist)
                    for kb in range(4):
                        r_ = 4 * sbl + kb
                        qb_lo, qb_hi = max(0, r_ - 4), min(3, r_)
                        qlo, qhi = qb_lo * 128, (qb_hi + 1) * 128
                        for i in range(4):
                            hh = half * 4 + i
                            adds = []
                            for qb in range(qb_lo, qb_hi + 1):
                                dl = r_ - 4 - qb
                                c0, c1 = qb * 128, (qb + 1) * 128
                                if dl == 0:
                                    adds.append((c0, c1, identb[:], bc[:, 0, hh, :], ["identb", "btile"]))
                                    adds.append((c0, c1, identb[:], cm0b[:], ["identb", "cm0b"]))
                                elif dl == -1:
                                    adds.append((c0, c1, identb[:], bc[:, 1, hh, :], ["identb", "btile"]))
                                elif dl == -4:
                                    adds.append((c0, c1, identb[:], cm4b[:], ["identb", "cm4b"]))
                            at.tile(dict(kT=kbuf[bi][0:64, i, kb * 128:(kb + 1) * 128], v=vbuf[bi][:, kb, i, 0:65], n=128, qlo=qlo, qhi=qhi,
                                         qap=Q[0:64, hh, qlo:qhi], adds=adds, bias=(pvb[:, 0:1] if (m == 0 and sbl == 0) else None),
                                         names=[f"kbuf{bi}", f"vbuf{bi}"], O=ps[4 + i], oname=psn[4 + i],
                                         first=(not started[i]), last=(sbl == 1 and kb == 3)))
                            started[i] = True
                at.flush()
                for i in range(4):
                    finish_head(ps[4 + i], psn[4 + i], zg["c"], "zgc", half * 4 + i, 512, slice(0, 512))

            def b_topk(qb):
                NK = (16 * m + 13 + qb) * 128
                qs = slice(qb * 128, (qb + 1) * 128)
                for k0 in range(0, NK, 512):
                    nk = min(512, NK - k0)
                    ii = nxt("ik", 2)
                    P.dma("sp", ikbuf[ii][:, 0:nk], IKl[:, k0:k0 + nk], r=[hist], w=[f"ikbuf{ii}"])
                    for h in range(8):
                        base = 32 * (h % 2)
                        pi_ = 1 + nxt("ips", 3)
                        P.pe(lambda e: e.matmul(ps[pi_][:, 0:nk], iqT[base:base + 32, h // 2, qs], ikbuf[ii][base:base + 32, 0:nk], start=True, stop=True),
                             r=["iqT", f"ikbuf{ii}"], w=[psn[pi_]])
                        ri = nxt("rr", 3)
                        P.act(lambda e: e.activation(Rr[ri][:, 0:nk], ps[pi_][:, 0:nk], AF.Relu, scale=wabs[:, qb, h:h + 1]), r=[psn[pi_], "wabs"], w=[f"Rr{ri}"])
                        if h == 0:
                            P.dve(lambda e: e.tensor_scalar(SC[:, k0:k0 + nk], Rr[ri][:, 0:nk], wsgn[:, qb, 0:1], None, ALU.mult), r=[f"Rr{ri}", "wsgn"], w=["SC"])
                        else:
                            P.dve(lambda e: e.scalar_tensor_tensor(SC[:, k0:k0 + nk], Rr[ri][:, 0:nk], wsgn[:, qb, h:h + 1], SC[:, k0:k0 + nk], ALU.mult, ALU.add),
                                  r=[f"Rr{ri}", "wsgn", "SC"], w=["SC"])
                for ch in range(4):
                    c0 = g0 * 128 + ch * 512
                    wd = min(512, NK - c0)
                    if wd <= 0:
                        continue
                    ri = nxt("rr", 3)
                    P.dve(lambda e: e.tensor_scalar(Rr[ri][:, 0:wd], iota5[:, 0:wd], qlimc[:, qb * 4 + ch:qb * 4 + ch + 1], -1e30, ALU.is_ge, ALU.mult),
                          r=["iota5", "qlimc"], w=[f"Rr{ri}"])
                    P.dve(lambda e: e.tensor_tensor(SC[:, c0:c0 + wd], SC[:, c0:c0 + wd], Rr[ri][:, 0:wd], ALU.add), r=["SC", f"Rr{ri}"], w=["SC"])
                P.dve(lambda e: e.memset(cntb[:], 0.0), w=["cntb"])
                P.dve(lambda e: e.memset(small[:, 8:9], 0.0), w=["cand"])
                for it in range(NBIS):
                    stp = 64.0 * (0.5 ** it)
                    P.dve(lambda e: e.tensor_scalar(junk[:, 0:NK], SC[:, 0:NK], small[:, 8:9], 0.0, ALU.is_ge, ALU.add, accum_out=cntb[:, it:it + 1]),
                          r=["SC", "cand", "cntb"], w=["junk", "cntb"])
                    a, b_ = (stp, -0.5 * stp) if it < NBIS - 1 else (stp, -stp)
                    P.dve(lambda e: e.tensor_scalar(small[:, 9:10], cntb[:, it:it + 1], float(TOPK), a, ALU.is_ge, ALU.mult), r=["cntb"], w=["fl"])
                    P.dve(lambda e: e.scalar_tensor_tensor(small[:, 8:9], small[:, 9:10], b_, small[:, 8:9], ALU.add, ALU.add), r=["fl", "cand"], w=["cand"])
                P.dve(lambda e: e.tensor_scalar(junk[:, 0:NK], SC[:, 0:NK], small[:, 8:9], NEG, ALU.is_lt, ALU.mult), r=["SC", "cand"], w=["junk"])
                P.dma("pool", MBS[qb][:, 0:NK], junk[:, 0:NK], r=["junk"], w=[f"mbs{qb}"])

            def b_attn(qb):
                qs = slice(qb * 128, (qb + 1) * 128)
                nkq = 16 * m + 13 + qb
                at = Attn()
                for sbk in range((nkq + 3) // 4):
                    k0 = sbk * 512
                    nk = min(512, nkq * 128 - k0)
                    mi = nxt("mb", 2)
                    P.dma("sp", Mb[mi][:, 0:nk], MBS[qb][:, k0:k0 + nk], r=[f"mbs{qb}"], w=[f"Mb{mi}"])
                    bi = load_kv(KBl, VBl, 0, 2, k0, nk, hist)
                    for kb in range(nk // 128):
                        gkb = sbk * 4 + kb
                        for jj in range(2):
                            adds = [(0, 512, Mb[mi][:, kb * 128:(kb + 1) * 128], i4b[:].rearrange("p a b -> p (a b)"), [f"Mb{mi}", "i4b"])]
                            if gkb >= g0 - 1 and (gkb - g0 - qb) % 4 in (0, 3):
                                rr = gkb - g0 + 1
                                ci = nxt("cmb", 2)
                                sc0 = sel01[:, (qb * 17 + rr) * 2:(qb * 17 + rr) * 2 + 1]
                                sc1 = sel01[:, (qb * 17 + rr) * 2 + 1:(qb * 17 + rr) * 2 + 2]
                                P.dma("sp", comb[ci][:].rearrange("p a b -> p (a b)"), COMBS[qb, rr, jj], r=["~combs"], w=[f"comb{ci}"])
                                adds.append((0, 512, identb[:], comb[ci][:].rearrange("p a b -> p (a b)"), ["identb", f"comb{ci}"]))
                            at.tile(dict(kT=kbuf[bi][0:64, jj, kb * 128:(kb + 1) * 128], v=vbuf[bi][:, kb, jj, 0:65], n=128, qlo=0, qhi=512,
                                         qap=Q[0:64, 4 * jj:4 * jj + 4, qs], adds=adds, bias=None,
                                         names=[f"kbuf{bi}", f"vbuf{bi}"], O=ps[4 + jj], oname=psn[4 + jj],
                                         first=(gkb == 0), last=(gkb == nkq - 1)))
                at.flush()
                for jj in range(2):
                    finish_head(ps[4 + jj], psn[4 + jj], zg["b"], "zgb", (4 * jj, 4 * jj + 4), 512, qs)

            fm_unit(l, "bx", 512, evac_bx_q, blocks=(2, 3, 4, 5))
            b_topk(0)
            fm_unit(l, "za", 512, evac_z(zg["a"], "zga"))
            fm_unit(l, "qa", 512, evac_q(SCALE))
            for h in range(8):
                P.dma("sp", Q[64:65, h, :], rT[h:h + 1, :], r=["rT"], w=["Q"])
            a_half(0)
            b_topk(1)
            a_half(1)
            fm_unit(l, "zc", 512, evac_z(zg["c"], "zgc"))
            fm_unit(l, "qc", 512, evac_q(SCALE))
            c_half(0)
            c_half(1)
            fm_unit(l, "zb", 512, evac_z(zg["b"], "zgb"))
            fm_unit(l, "qb", 512, evac_q(SCALE))
            b_attn(0)
            b_topk(2)
            b_attn(1)
            b_topk(3)
            b_attn(2)
            b_attn(3)

            allz = [zg["a"], zg["b"], zg["c"]]
            alln = ["zga", "zgb", "zgc"]
            for u in range(6):
                Wo_, won = load_wo(l, u)
                for hq in range(4):
                    hidx = u * 4 + hq
                    zt, zn = allz[hidx // 8], alln[hidx // 8]
                    for tb in range(4):
                        for n_ in range(2):
                            P.pe(lambda e: e.matmul(ps[tb * 2 + n_][:, :], zt[:, hidx % 8, tb * 128:(tb + 1) * 128], Wo_[:, hq, n_ * 512:(n_ + 1) * 512],
                                                    start=(hidx == 0), stop=(hidx == 23)), r=[zn, won], w=[psn[tb * 2 + n_]])
            for tb in range(4):
                xi = nxt("x", 2)
                X = xt[xi]; xname = f"xt{xi}"
                P.dma("sp", X[:], own[tb * 128:(tb + 1) * 128, :], r=(own_r if own_r else []), w=[xname])
                for n_ in range(2):
                    P.dve(lambda e: e.tensor_tensor(X[:, n_ * 512:(n_ + 1) * 512], X[:, n_ * 512:(n_ + 1) * 512], ps[tb * 2 + n_][:, :], ALU.add),
                          r=[xname, psn[tb * 2 + n_]], w=[xname])
                r0 = m * 512 + tb * 128
                if l < nlayers - 1:
                    P.dma("pool", hp1q[r0:r0 + 128, :], X[:], r=[xname], w=[f"hp1qc{r0 // 256}"])
                else:
                    P.act(lambda e: e.activation(junk[:, 0:1024], X[:], AF.Square, accum_out=small[:, 16:17]), r=[xname], w=["junk", "fs0"])
                    P.dve(lambda e: e.tensor_scalar(small[:, 17:18], small[:, 16:17], 1.0 / D_MODEL, EPS, ALU.mult, ALU.add), r=["fs0"], w=["fs1"])
                    P.act(lambda e: e.activation(small[:, 18:19], small[:, 17:18], AF.Sqrt), r=["fs1"], w=["fs2"])
                    P.dve(lambda e: e.reciprocal(small[:, 19:20], small[:, 18:19]), r=["fs2"], w=["fs3"])
                    P.dve(lambda e: e.scalar_tensor_tensor(X[:], X[:], small[:, 19:20], fgb[:], ALU.mult, ALU.mult), r=[xname, "fs3", "fgb"], w=[xname])
                    P.dma("pool", y_q[r0:r0 + 128, :], X[:], r=[xname])
            def emit_exchange(m=m):
                for hf in range(2):
                    cidx = 2 * m + hf
                    P.dma("pool", ccs, hp1q[cidx * 256:(cidx + 1) * 256, :], r=[f"hp1qc{cidx}"], w=["ccs"])
                    P.add("pool", lambda e: e.collective_compute("AllGather", ALU.bypass, replica_groups=[[0, 1, 2, 3], [4, 5, 6, 7]],
                                                                 ins=[ccs.opt()], outs=[ccd.opt()]), r=["ccs"], w=["ccd"], cc=True)
                    for r in range(4):
                        a0 = (4 * m + r) * 512 + hf * 256
                        P.dma("pool", hpg[a0:a0 + 256, :], ccd[r * 256:(r + 1) * 256, :], r=["ccd"], w=["~hpgw"])
            if l < nlayers - 1:
                if m < nm - 1:
                    emit_exchange()
                else:
                    deferred_exchange.append(emit_exchange)
        if do_sample:
            NS = DEC_SEQ
            shist = f"~shist{l}"
            psb7 = ps[7].bitcast(BF16)

            def prep_cache(src2d, nrows, ncols, kdst, vdst, nheads):
                for b in range(nrows // 128):
                    xi = nxt("x", 2)
                    X = xt[xi]; xname = f"xt{xi}"
                    P.dma("sp", X[:, 0:ncols], src2d[b * 128:(b + 1) * 128, :], w=[xname])
                    P.dve(lambda e: e.tensor_copy(xn[:, 0:ncols], X[:, 0:ncols]), r=[xname], w=["xn"])
                    if vdst is not None:
                        P.dma("pool", vdst[b * 128:(b + 1) * 128, :], xn[:, 0:ncols], r=["xn"], w=[shist])
                    if kdst is not None:
                        for hh in range(nheads):
                            P.pe(lambda e: e.transpose(psb7[0:64, hh * 128:(hh + 1) * 128], xn[:, hh * 64:(hh + 1) * 64], identb[:]),
                                 r=["xn", "identb"], w=["ps7"])
                        P.act(lambda e: e.activation(kst[:, 0:nheads, 0:128], psb7[0:64, 0:nheads * 128].rearrange("p (h k) -> p h k", h=nheads), AF.Copy),
                              r=["ps7"], w=["kst"])
                        P.dma("pool", kdst[:, :, b * 128:(b + 1) * 128].rearrange("h d k -> d h k"), kst[:, 0:nheads, 0:128], r=["kst"], w=[shist])

            prep_cache(ca_k[l], PAST, 512, SS_KA[l], None, 8)
            prep_cache(ca_v[l], PAST, 512, None, SS_VA[l], 8)
            prep_cache(cb_k[l], PAST, 128, SS_KB[l], None, 2)
            prep_cache(cb_v[l], PAST, 128, None, SS_VB[l], 2)
            prep_cache(cc_k[l], 512, 512, SS_KC[l], None, 8)
            prep_cache(cc_v[l], 512, 512, None, SS_VC[l], 8)
            for b in range(PAST // 128):
                xi = nxt("x", 2)
                X = xt[xi]; xname = f"xt{xi}"
                P.dma("sp", X[:, 0:32], cb_ik[l, b * 128:(b + 1) * 128, :], w=[xname])
                P.dve(lambda e: e.tensor_copy(xn[:, 0:32], X[:, 0:32]), r=[xname], w=["xn"])
                P.dve(lambda e: e.tensor_copy(xn[:, 32:64], X[:, 0:32]), r=[xname], w=["xn"])
                P.pe(lambda e: e.transpose(psb7[0:64, 0:128], xn[:, 0:64], identb[:]), r=["xn", "identb"], w=["ps7"])
                P.act(lambda e: e.activation(kst[:, 0, 0:128], psb7[0:64, 0:128], AF.Copy), r=["ps7"], w=["kst"])
                P.dma("pool", SS_IK[l][:, b * 128:(b + 1) * 128], kst[:, 0, 0:128], r=["kst"], w=[shist])
            P.dve(lambda e: e.memset(tot[:], 0.0), w=["tot"])

            def cum_block(kb_, n):
                P.pe(lambda e: e.matmul(ps[5][0:n, 0:8], cstf[0:n, 2, 0:n], lfb[0:n, :], start=True, stop=False), r=["lfb", "cstf"], w=["ps5"])
                P.pe(lambda e: e.matmul(ps[5][0:n, 0:8], ones1[0:1, 0:n], tot[0:1, :], start=False, stop=True), r=["tot", "ones1"], w=["ps5"])
                P.dve(lambda e: e.tensor_copy(cstore[0:n, kb_, :], ps[5][0:n, 0:8]), r=["ps5"], w=["cstore"])
                P.pe(lambda e: e.matmul(ps[5][0:1, 8:16], cstf[0:n, 1, 128 - n:129 - n], cstore[0:n, kb_, :], start=True, stop=True), r=["cstore", "cstf"], w=["ps5"])
                P.dve(lambda e: e.tensor_copy(tot[:], ps[5][0:1, 8:16]), r=["ps5"], w=["tot"])

            for b in range(PAST // 128):
                P.dma("sp", lfb[:], ca_lf[l, b * 128:(b + 1) * 128, :], w=["lfb"])
                cum_block(b, 128)
            norm_block(l, (xs if l == 0 else hs1)[:, :], 0, nrow=NS, rname=("hs1" if l > 0 else None))
            for uname in TM_UNITS:
                W, wn = load_w(l, UNITS.index(uname))
                si = nxt("s", 4)
                for c in range(8):
                    P.pe(lambda e: e.matmul(ps[si][0:NS, :], hT[:, c, 0:NS], W[:, c, :], start=(c == 0), stop=(c == 7)), r=[wn, "hT"], w=[psn[si]])
                k = nxt("st", 2)
                S_ = st[k]; sn = f"st{k}"
                P.act(lambda e: e.activation(S_[0:NS, :], ps[si][0:NS, :], AF.Copy), r=[psn[si]], w=[sn])
                V_, vn = vst[k], f"vst{k}"
                if uname in ("tka", "tva", "tkc", "tvc"):
                    dst = {"tka": s_ak, "tva": s_av, "tkc": s_ck, "tvc": s_cv}[uname]
                    P.dma("pool", dst[l], S_[0:NS, :], r=[sn])
                    if uname in ("tva", "tvc"):
                        P.dve(lambda e: e.tensor_copy(V_[0:NS, :], S_[0:NS, :]), r=[sn], w=[vn])
                        vd = SS_VA[l][PAST:PAST + NS, :] if uname == "tva" else SS_VC[l][512:512 + NS, :]
                        P.dma("pool", vd, V_[0:NS, :], r=[vn], w=[shist])
                else:
                    P.dma("pool", s_bk[l], S_[0:NS, 0:128], r=[sn])
                    P.dma("pool", s_bv[l], S_[0:NS, 128:256], r=[sn])
                    P.dma("pool", s_ik[l], S_[0:NS, 256:288], r=[sn])
                    P.dve(lambda e: e.tensor_copy(V_[0:NS, 0:128], S_[0:NS, 128:256]), r=[sn], w=[vn])
                    P.dma("pool", SS_VB[l][PAST:PAST + NS, :], V_[0:NS, 0:128], r=[vn], w=[shist])
                    P.dve(lambda e: e.tensor_tensor(lfb[0:NS, :], S_[0:NS, 288:296], bfb[0:NS, l * 8:(l + 1) * 8], ALU.add), r=[sn, "bfb"], w=["lfb"])
                    P.act(lambda e: e.activation(lfb[0:NS, :], lfb[0:NS, :], AF.Exp, scale=-1.0), r=["lfb"], w=["lfb"])
                    P.act(lambda e: e.activation(lfb[0:NS, :], lfb[0:NS, :], AF.Ln, bias=1.0), r=["lfb"], w=["lfb"])
                    P.dve(lambda e: e.tensor_scalar(lfb[0:NS, :], lfb[0:NS, :], -1.0, None, ALU.mult), r=["lfb"], w=["lfb"])
                    P.dma("pool", s_lf[l], lfb[0:NS, :], r=["lfb"])
                    cum_block(8, NS)
                    P.dve(lambda e: e.tensor_scalar(wsgn[0:NS, 0, :], S_[0:NS, 296:304], 0.0, 2.0, ALU.is_ge, ALU.mult), r=[sn], w=["wsgn"])
                    P.dve(lambda e: e.tensor_scalar(wsgn[0:NS, 0, :], wsgn[0:NS, 0, :], -1.0, None, ALU.add), r=["wsgn"], w=["wsgn"])
                    P.dve(lambda e: e.scalar_tensor_tensor(wabs[0:NS, 0, :], S_[0:NS, 296:304], IDXS, wsgn[0:NS, 0, :], ALU.mult, ALU.mult), r=[sn, "wsgn"], w=["wabs"])
            P.pe(lambda e: e.matmul(ps[5][:, 16:24], cstf[0:NS, 3, :], cstore[0:NS, 8, :], start=True, stop=True), r=["cstore", "cstf"], w=["ps5"])
            P.dve(lambda e: e.tensor_copy(cbc[:], ps[5][:, 16:24]), r=["ps5"], w=["cbc"])
            for h in range(8):
                P.dve(lambda e: e.tensor_scalar(nbias[:, 0:9, h], cstore[:, 0:9, h], cbc[:, h:h + 1], -1.0, ALU.subtract, ALU.mult), r=["cstore", "cbc"], w=["nbias"])
            P.dve(lambda e: e.tensor_tensor(rq[0:NS, 0, :], cstore[0:NS, 8, :], cbc[0:NS, :], ALU.subtract), r=["cstore", "cbc"], w=["rq"])
            P.pe(lambda e: e.transpose(ps[6][0:8, 0:NS], rq[0:NS, 0, :], cstf[0:NS, 0, 0:NS]), r=["rq", "cstf"], w=["ps6"])
            P.dve(lambda e: e.tensor_copy(rT[:, 0:NS], ps[6][0:8, 0:NS]), r=["ps6"], w=["rT"])

            def s_evac_k(dst3, koff):
                def f(j, pt, pn):
                    P.act(lambda e: e.activation(kst[:, j, 0:NS], pt[0:64, 0:NS], AF.Copy), r=[pn], w=["kst"])
                    if j == 7:
                        P.dma("pool", dst3[:, :, koff:koff + NS].rearrange("h d k -> d h k"), kst[:, :, 0:NS], r=["kst"], w=[shist])
                return f

            def s_evac_q(j, pt, pn):
                P.act(lambda e: e.activation(Q[0:64, j, 0:NS], pt[0:64, 0:NS], AF.Copy, scale=SCALE), r=[pn], w=["Q"])

            def s_evac_z(zt, zn):
                def f(j, pt, pn):
                    P.act(lambda e: e.activation(zt[:, j, 0:NS], pt[0:64, 0:NS], AF.Silu), r=[pn], w=[zn])
                return f

            fm_unit(l, "ka", NS, s_evac_k(SS_KA[l], PAST))
            fm_unit(l, "za", NS, s_evac_z(zg["a"], "zga"))
            fm_unit(l, "qa", NS, s_evac_q)
            for h in range(8):
                P.dma("sp", Q[64:65, h, 0:NS], rT[h:h + 1, 0:NS], r=["rT"], w=["Q"])
            for half in range(2):
                at = Attn()
                for sbk in range(3):
                    nk = 512 if sbk < 2 else NS
                    bi = load_kv(SS_KA[l], SS_VA[l], half * 4, 4, sbk * 512, nk, shist)
                    for kb in range((nk + 127) // 128):
                        n = min(128, nk - kb * 128)
                        adds = [(0, NS, identb[0:NS, 0:NS], ma0b[0:NS, 0:NS], ["identb", "ma0b"])] if sbk == 2 else []
                        for i in range(4):
                            hh = half * 4 + i
                            at.tile(dict(kT=kbuf[bi][0:65, i, kb * 128:kb * 128 + n], v=vbuf[bi][0:n, kb, i, 0:65], n=n, qlo=0, qhi=NS,
                                         qap=Q[0:65, hh, 0:NS], adds=adds, bias=nbias[0:n, 4 * sbk + kb, hh:hh + 1],
                                         names=[f"kbuf{bi}", f"vbuf{bi}"], O=ps[4 + i], oname=psn[4 + i],
                                         first=(sbk == 0 and kb == 0), last=(sbk == 2)))
                at.flush()
                for i in range(4):
                    finish_head(ps[4 + i], psn[4 + i], zg["a"], "zga", half * 4 + i, NS, slice(0, NS))
            fm_unit(l, "kc", NS, s_evac_k(SS_KC[l], 512))
            fm_unit(l, "zc", NS, s_evac_z(zg["c"], "zgc"))
            fm_unit(l, "qc", NS, s_evac_q)
            for half in range(2):
                at = Attn()
                for sbk in range(2):
                    nk = 512 if sbk < 1 else NS
                    bi = load_kv(SS_KC[l], SS_VC[l], half * 4, 4, sbk * 512, nk, shist)
                    for kb in range((nk + 127) // 128):
                        n = min(128, nk - kb * 128)
                        for i in range(4):
                            hh = half * 4 + i
                            adds = []
                            if sbk == 0 and kb == 3:
                                adds = [(0, NS, identb[:], bc[:, 1, hh, 0:NS], ["identb", "btile"])]
                            if sbk == 1:
                                adds = [(0, NS, identb[0:NS, 0:NS], bc[0:NS, 0, hh, 0:NS], ["identb", "btile"])]
                            at.tile(dict(kT=kbuf[bi][0:64, i, kb * 128:kb * 128 + n], v=vbuf[bi][0:n, kb, i, 0:65], n=n, qlo=0, qhi=NS,
                                         qap=Q[0:64, hh, 0:NS], adds=adds, bias=None,
                                         names=[f"kbuf{bi}", f"vbuf{bi}"], O=ps[4 + i], oname=psn[4 + i],
                                         first=(sbk == 0 and kb == 0), last=(sbk == 1)))
                at.flush()
                for i in range(4):
                    finish_head(ps[4 + i], psn[4 + i], zg["c"], "zgc", half * 4 + i, NS, slice(0, NS))
            def s_evac_bx(j, pt, pn):
                if j < 2:
                    P.act(lambda e: e.activation(kst[:, j, 0:NS], pt[0:64, 0:NS], AF.Copy), r=[pn], w=["kst"])
                    if j == 1:
                        P.dma("pool", SS_KB[l][:, :, PAST:PAST + NS].rearrange("h d k -> d h k"), kst[:, 0:2, 0:NS], r=["kst"], w=[shist])
                elif j < 6:
                    P.act(lambda e: e.activation(iqT[:, j - 2, 0:NS], pt[0:64, 0:NS], AF.Copy), r=[pn], w=["iqT"])
                elif j == 6:
                    P.act(lambda e: e.activation(kst[:, 2, 0:NS], pt[0:64, 0:NS], AF.Copy), r=[pn], w=["kst"])
                    P.dma("pool", SS_IK[l][:, PAST:PAST + NS], kst[:, 2, 0:NS], r=["kst"], w=[shist])
            fm_unit(l, "bx", NS, s_evac_bx)
            fm_unit(l, "zb", NS, s_evac_z(zg["b"], "zgb"))
            fm_unit(l, "qb", NS, s_evac_q)
            NK = PAST + NS
            for k0 in range(0, NK, 512):
                nk = min(512, NK - k0)
                ii = nxt("ik", 2)
                P.dma("sp", ikbuf[ii][:, 0:nk], SS_IK[l][:, k0:k0 + nk], r=[shist], w=[f"ikbuf{ii}"])
                for h in range(8):
                    base = 32 * (h % 2)
                    pi_ = 2 + nxt("ips", 2)
                    P.pe(lambda e: e.matmul(ps[pi_][0:NS, 0:nk], iqT[base:base + 32, h // 2, 0:NS], ikbuf[ii][base:base + 32, 0:nk], start=True, stop=True),
                         r=["iqT", f"ikbuf{ii}"], w=[psn[pi_]])
                    ri = nxt("rr", 2)
                    P.act(lambda e: e.activation(Rr[ri][0:NS, 0:nk], ps[pi_][0:NS, 0:nk], AF.Relu, scale=wabs[0:NS, 0, h:h + 1]), r=[psn[pi_], "wabs"], w=[f"Rr{ri}"])
                    if h == 0:
                        P.dve(lambda e: e.tensor_scalar(SC[0:NS, k0:k0 + nk], Rr[ri][0:NS, 0:nk], wsgn[0:NS, 0, 0:1], None, ALU.mult), r=[f"Rr{ri}", "wsgn"], w=["SC"])
                    else:
                        P.dve(lambda e: e.scalar_tensor_tensor(SC[0:NS, k0:k0 + nk], Rr[ri][0:NS, 0:nk], wsgn[0:NS, 0, h:h + 1], SC[0:NS, k0:k0 + nk], ALU.mult, ALU.add),
                              r=[f"Rr{ri}", "wsgn", "SC"], w=["SC"])
            P.dve(lambda e: e.memset(cntb[:], 0.0), w=["cntb"])
            P.dve(lambda e: e.memset(small[:, 8:9], 0.0), w=["cand"])
            for it in range(NBIS):
                stp = 64.0 * (0.5 ** it)
                P.dve(lambda e: e.tensor_scalar(junk[0:NS, 0:NK], SC[0:NS, 0:NK], small[0:NS, 8:9], 0.0, ALU.is_ge, ALU.add, accum_out=cntb[0:NS, it:it + 1]),
                      r=["SC", "cand", "cntb"], w=["junk", "cntb"])
                a, b_ = (stp, -0.5 * stp) if it < NBIS - 1 else (stp, -stp)
                P.dve(lambda e: e.tensor_scalar(small[0:NS, 9:10], cntb[0:NS, it:it + 1], float(TOPK), a, ALU.is_ge, ALU.mult), r=["cntb"], w=["fl"])
                P.dve(lambda e: e.scalar_tensor_tensor(small[0:NS, 8:9], small[0:NS, 9:10], b_, small[0:NS, 8:9], ALU.add, ALU.add), r=["fl", "cand"], w=["cand"])
            at = Attn()
            for sbk in range(3):
                k0 = sbk * 512
                nk = min(512, NK - k0)
                mi = nxt("mb", 2)
                P.dve(lambda e: e.tensor_scalar(Mb[mi][0:NS, 0:nk], SC[0:NS, k0:k0 + nk], small[0:NS, 8:9], NEG, ALU.is_lt, ALU.mult), r=["SC", "cand"], w=[f"Mb{mi}"])
                bi = load_kv(SS_KB[l], SS_VB[l], 0, 2, k0, nk, shist)
                for kb in range((nk + 127) // 128):
                    n = min(128, nk - kb * 128)
                    gkb = sbk * 4 + kb
                    for j in range(2):
                        adds = [(0, 4 * NS, Mb[mi][0:NS, kb * 128:kb * 128 + n], i4b[0:NS, :, 0:NS], [f"Mb{mi}", "i4b"])]
                        if gkb >= 7:
                            for hq in range(4):
                                if gkb == 7:
                                    adds.append((hq * NS, (hq + 1) * NS, identb[:], b5[:, 1, 4 * j + hq, 0:NS], ["identb", "btile"]))
                                else:
                                    adds.append((hq * NS, (hq + 1) * NS, identb[0:NS, 0:NS], b5[0:NS, 0, 4 * j + hq, 0:NS], ["identb", "btile"]))
                        at.tile(dict(kT=kbuf[bi][0:64, j, kb * 128:kb * 128 + n], v=vbuf[bi][0:n, kb, j, 0:65], n=n, qlo=0, qhi=4 * NS,
                                     qap=Q[0:64, 4 * j:4 * j + 4, 0:NS], adds=adds, bias=None,
                                     names=[f"kbuf{bi}", f"vbuf{bi}"], O=ps[4 + j], oname=psn[4 + j],
                                     first=(gkb == 0), last=(gkb == 8)))
            at.flush()
            for j in range(2):
                finish_head(ps[4 + j], psn[4 + j], zg["b"], "zgb", (4 * j, 4 * j + 4), 4 * NS, slice(0, NS))
            allz = [zg["a"], zg["b"], zg["c"]]
            alln = ["zga", "zgb", "zgc"]
            for u in range(6):
                Wo_, won = load_wo(l, u)
                for hq in range(4):
                    hidx = u * 4 + hq
                    zt, zn = allz[hidx // 8], alln[hidx // 8]
                    for n_ in range(2):
                        P.pe(lambda e: e.matmul(ps[n_][0:NS, :], zt[:, hidx % 8, 0:NS], Wo_[:, hq, n_ * 512:(n_ + 1) * 512], start=(hidx == 0), stop=(hidx == 23)),
                             r=[zn, won], w=[psn[n_]])
            xi = nxt("x", 2)
            X = xt[xi]; xname = f"xt{xi}"
            P.dma("sp", X[0:NS, :], (xs if l == 0 else hs1)[:, :], r=(["hs1"] if l > 0 else []), w=[xname])
            for n_ in range(2):
                P.dve(lambda e: e.tensor_tensor(X[0:NS, n_ * 512:(n_ + 1) * 512], X[0:NS, n_ * 512:(n_ + 1) * 512], ps[n_][0:NS, :], ALU.add), r=[xname, psn[n_]], w=[xname])
            if l < nlayers - 1:
                P.dma("pool", hs1[:, :], X[0:NS, :], r=[xname], w=["hs1"])
            else:
                P.act(lambda e: e.activation(junk[0:NS, 0:1024], X[0:NS, :], AF.Square, accum_out=small[0:NS, 16:17]), r=[xname], w=["junk", "fs0"])
                P.dve(lambda e: e.tensor_scalar(small[0:NS, 17:18], small[0:NS, 16:17], 1.0 / D_MODEL, EPS, ALU.mult, ALU.add), r=["fs0"], w=["fs1"])
                P.act(lambda e: e.activation(small[0:NS, 18:19], small[0:NS, 17:18], AF.Sqrt), r=["fs1"], w=["fs2"])
                P.dve(lambda e: e.reciprocal(small[0:NS, 19:20], small[0:NS, 18:19]), r=["fs2"], w=["fs3"])
                P.dve(lambda e: e.scalar_tensor_tensor(X[0:NS, :], X[0:NS, :], small[0:NS, 19:20], fgb[0:NS, :], ALU.mult, ALU.mult), r=[xname, "fs3", "fgb"], w=[xname])
                P.dma("pool", y_s[:, :], X[0:NS, :], r=[xname])
        for fx in deferred_exchange:
            fx()
    P.finalize_and_emit()
    return nc, es, P


def _t5_bucket_np(rel):
    nb = 16
    max_exact = 8
    ret = np.where(rel > 0, nb, 0)
    n = np.abs(rel)
    nf = np.maximum(n, 1).astype(np.float32)
    large = max_exact + (np.log(nf / max_exact) / math.log(128 / max_exact) * (nb - max_exact)).astype(np.int32)
    large = np.minimum(large, nb - 1)
    return ret + np.where(n < max_exact, n, large)


def _constants():
    p = np.arange(128)[:, None]
    f = np.arange(128)[None, :]
    cst = np.zeros((128, 8, 128), np.float32)
    cst[:, 0] = (p == f)
    cst[:, 1] = (p + f == 127)
    cst[:, 2] = (p <= f)
    cst[0, 3, :] = 1.0
    cst[:, 4] = np.where(p > f, NEG, 0.0)
    cst[:, 5] = np.where((p >= 64) & (f < 64), NEG, 0.0)
    cst[:, 6] = np.where((p < 64) & (f >= 64), NEG, 0.0)
    cst[:, 7] = np.where((p < 64) & (f >= 64), -1e30, 0.0)
    rel = 127 - np.arange(LTAB)
    bk = _t5_bucket_np(rel.astype(np.int32))
    oh5 = np.zeros((32, LTAB), np.float32)
    oh5[bk, np.arange(LTAB)] += 1.0
    far = int(_t5_bucket_np(np.array([-100000], np.int32))[0])
    oh5[far, :] -= 1.0
    idx = np.clip(rel, -128, 128) + 128
    ohc = np.zeros((3 * 128, LTAB), np.float32)
    ohc[idx, np.arange(LTAB)] += 1.0
    ohc[0, :] -= 1.0
    return cst.reshape(128, 8 * 128), oh5, ohc.reshape(3, 128, LTAB)


_PROG = {}


def _get_prog(key=(True, 4, DEPTH)):
    if key not in _PROG:
        _PROG[key] = build_program(*key)
    return _PROG[key]


def _host_inputs(x_prompt, norm_g, w_in, b_f, t5_bias, c_rel_bias, w_out, final_g):
    cst, oh5, ohc = _constants()
    wus = np.zeros((DEPTH, NU, 128, 8, 512), np.float32)
    for l in range(DEPTH):
        for u, name in enumerate(UNITS):
            cols = np.array(_unit_cols(name))
            m = cols >= 0
            w = np.zeros((D_MODEL, 512), np.float32)
            w[:, m] = w_in[l][:, cols[m]]
            wus[l, u] = w.reshape(8, 128, 512).transpose(1, 0, 2)
    wos = np.ascontiguousarray(w_out.reshape(DEPTH, 6, 4, 64, D_MODEL).transpose(0, 1, 3, 2, 4))
    gcol = np.ascontiguousarray(norm_g.reshape(DEPTH, 8, 128).transpose(2, 0, 1).reshape(128, DEPTH * 8))
    crel = np.zeros((DEPTH, 384, 8), np.float32)
    crel[:, :257] = c_rel_bias
    common = dict(wu=wus, wo=wos, gcol=gcol, fg=np.ascontiguousarray(final_g.reshape(1, D_MODEL)),
                  bfb=np.ascontiguousarray(b_f.reshape(1, DEPTH * 8)), t5=np.ascontiguousarray(t5_bias),
                  crel=crel.reshape(DEPTH, 3, 128, 8), oh5=oh5, ohc=ohc, cst=cst)
    return common


def _percore(j):
    pc = np.zeros((128, 1024), np.float32)
    p = np.arange(128)
    pc[:, 0:512] = (j * 512 + np.arange(512))[None, :]
    for rr in range(16):
        pc[:, 512 + rr] = rr * 128 + p
    for qb in range(4):
        for ch in range(4):
            pc[:, 528 + qb * 4 + ch] = (8 * j + 2 * qb + (p >= 64) + 1) * 64 - ch * 512
        for rr in range(17):
            pc[:, 544 + (qb * 17 + rr) * 2] = 1.0 if (rr - 1) == 4 * j + qb else 0.0
            pc[:, 544 + (qb * 17 + rr) * 2 + 1] = 1.0 if (rr - 1) == 4 * j + qb - 1 else 0.0
    for r in range(4):
        pc[:, 680 + r] = 1.0 if j == r else 0.0
    pc[:, 684] = -30000.0 if j == 0 else 0.0
    return pc


def _in_maps(x_prompt, x_sample, cache_a_k, cache_a_v, cache_a_logf, cache_b_k, cache_b_v, cache_b_idx_k,
             cache_c_k, cache_c_v, norm_g, w_in, b_f, t5_bias, c_rel_bias, w_out, final_g):
    f = lambda a: np.ascontiguousarray(np.asarray(a, dtype=np.float32))
    x_prompt, norm_g, w_in, b_f, t5_bias, c_rel_bias, w_out, final_g = map(f, (x_prompt, norm_g, w_in, b_f, t5_bias, c_rel_bias, w_out, final_g))
    common = _host_inputs(x_prompt, norm_g, w_in, b_f, t5_bias, c_rel_bias, w_out, final_g)
    common["iota5"] = np.ascontiguousarray(np.broadcast_to(np.arange(512, dtype=np.float32)[None, :], (128, 512)))
    in_maps = []
    for c in range(8):
        b, j = c // 4, c % 4
        m = dict(common)
        xb = x_prompt[b].reshape(NG, 512, D_MODEL)
        m["xp"] = x_prompt[b]
        m["xq"] = np.ascontiguousarray(xb[j::4])
        xpv = np.zeros((4, 512, D_MODEL), np.float32)
        for mm in range(4):
            if 4 * mm + j - 1 >= 0:
                xpv[mm] = xb[4 * mm + j - 1]
        m["xprev"] = xpv
        m["pcore"] = _percore(j)
        m["xs"] = f(x_sample[c])
        m["ca_k"] = f(cache_a_k[:, c]).reshape(DEPTH, PAST, 512); m["ca_v"] = f(cache_a_v[:, c]).reshape(DEPTH, PAST, 512)
        m["ca_lf"] = f(cache_a_logf[:, c]).reshape(DEPTH, PAST, 8)
        m["cb_k"] = f(cache_b_k[:, c]).reshape(DEPTH, PAST, 128); m["cb_v"] = f(cache_b_v[:, c]).reshape(DEPTH, PAST, 128)
        m["cb_ik"] = f(cache_b_idx_k[:, c]).reshape(DEPTH, PAST, 32)
        m["cc_k"] = f(cache_c_k[:, c]).reshape(DEPTH, 512, 512); m["cc_v"] = f(cache_c_v[:, c]).reshape(DEPTH, 512, 512)
        in_maps.append(m)
    return in_maps


def kernel(x_prompt, x_sample, cache_a_k, cache_a_v, cache_a_logf, cache_b_k, cache_b_v, cache_b_idx_k,
           cache_c_k, cache_c_v, norm_g, w_in, b_f, t5_bias, c_rel_bias, w_out, final_g):
    nc, es, P = _get_prog()
    in_maps = _in_maps(x_prompt, x_sample, cache_a_k, cache_a_v, cache_a_logf, cache_b_k, cache_b_v, cache_b_idx_k,
                       cache_c_k, cache_c_v, norm_g, w_in, b_f, t5_bias, c_rel_bias, w_out, final_g)
    res = run_bass_kernel_spmd(nc, in_maps, core_ids=list(range(8)))
    R = res.results
    st = lambda name, shp: np.stack([R[4 * b][name] for b in range(2)], axis=1).reshape(shp)
    y_prompt = np.zeros((BATCH, NG, 512, D_MODEL), np.float32)
    for c in range(8):
        b, j = c // 4, c % 4
        y_prompt[b, j::4] = R[c]["y_q"].reshape(4, 512, D_MODEL)
    y_prompt = y_prompt.reshape(BATCH, SEQ, D_MODEL)
    ss = lambda name, shp: np.stack([R[b][name] for b in range(DEC_BATCH)], axis=1).reshape(shp)
    y_sample = np.stack([R[b]["y_s"] for b in range(DEC_BATCH)], axis=0)
    outs = [y_prompt, y_sample,
            st("o_ak", (DEPTH, BATCH, SEQ, H, HD)), st("o_av", (DEPTH, BATCH, SEQ, H, HD)), st("o_lf", (DEPTH, BATCH, SEQ, H)),
            st("o_bk", (DEPTH, BATCH, SEQ, KVB, HD)), st("o_bv", (DEPTH, BATCH, SEQ, KVB, HD)), st("o_ik", (DEPTH, BATCH, SEQ, IDX_D)),
            st("o_ck", (DEPTH, BATCH, 512, H, HD)), st("o_cv", (DEPTH, BATCH, 512, H, HD)),
            ss("s_ak", (DEPTH, DEC_BATCH, DEC_SEQ, H, HD)), ss("s_av", (DEPTH, DEC_BATCH, DEC_SEQ, H, HD)), ss("s_lf", (DEPTH, DEC_BATCH, DEC_SEQ, H)),
            ss("s_bk", (DEPTH, DEC_BATCH, DEC_SEQ, KVB, HD)), ss("s_bv", (DEPTH, DEC_BATCH, DEC_SEQ, KVB, HD)), ss("s_ik", (DEPTH, DEC_BATCH, DEC_SEQ, IDX_D)),
            ss("s_ck", (DEPTH, DEC_BATCH, DEC_SEQ, H, HD)), ss("s_cv", (DEPTH, DEC_BATCH, DEC_SEQ, H, HD))]
    return tuple(outs)
```

```python
import math
import types
import numpy as np
from contextlib import ExitStack
import concourse.bass as bass
import concourse.mybir as mybir
from concourse.bass_utils import run_bass_kernel_spmd

F32 = mybir.dt.float32
BF16 = mybir.dt.bfloat16
ALU = mybir.AluOpType
AF = mybir.ActivationFunctionType

D_MODEL = 1024; BATCH = 2; SEQ = 8192; DEPTH = 2; DEC_BATCH = 8; DEC_SEQ = 16; PAST = 1024
HD = 64; H = 8; KVB = 2; IDX_H = 8; IDX_D = 32; TOPK = 256
SCALE = HD ** -0.5
IDXS = (IDX_D ** -0.5) * (IDX_H ** -0.5)
EPS = 1e-6
NEG = -32768.0
NG = SEQ // 512
NBIS = 24
LTAB = 384

_SPLIT = (512, 512, 512, 512, 8, 512, 128, 128, 512, 256, 8, 32, 512, 512, 512, 512)
_OFF = np.concatenate([[0], np.cumsum(_SPLIT)])
(QA, KA, VA, ZA, FA, QB, KB, VB, ZB, IQ, IW, IK, QC, KC, VC, ZC) = [int(o) for o in _OFF[:-1]]
FM_UNITS = ["qa", "ka", "za", "qb", "zb", "bx", "qc", "kc", "zc"]
TM_UNITS = ["tka", "tva", "tb", "tkc", "tvc"]
UNITS = FM_UNITS + TM_UNITS
NU = len(UNITS)


def _unit_cols(name):
    r = lambda a, n: list(range(a, a + n))
    pad = lambda l: l + [-1] * (512 - len(l))
    if name == "qa": return r(QA, 512)
    if name == "ka": return r(KA, 512)
    if name == "za": return r(ZA, 512)
    if name == "qb": return r(QB, 512)
    if name == "zb": return r(ZB, 512)
    if name == "qc": return r(QC, 512)
    if name == "kc": return r(KC, 512)
    if name == "zc": return r(ZC, 512)
    if name == "bx": return pad(r(KB, 128) + r(IQ, 256) + r(IK, 32) + r(IK, 32))
    if name == "tka": return r(KA, 512)
    if name == "tva": return r(VA, 512)
    if name == "tkc": return r(KC, 512)
    if name == "tvc": return r(VC, 512)
    if name == "tb": return pad(r(KB, 128) + r(VB, 128) + r(IK, 32) + r(FA, 8) + r(IW, 8))
    raise KeyError(name)


class Prog:
    STREAMS = ("pe", "act", "dve", "pool", "sp")

    def __init__(self, nc):
        self.nc = nc
        self.ops = []
        self.ndma = {}

    @staticmethod
    def _freeze(fn):
        if fn.__closure__ is None:
            return fn
        cells = []
        for c in fn.__closure__:
            try:
                cells.append(types.CellType(c.cell_contents))
            except ValueError:
                cells.append(c)
        return types.FunctionType(fn.__code__, fn.__globals__, fn.__name__, fn.__defaults__, tuple(cells))

    NSUB = 16

    def add(self, stream, fn, r=(), w=(), dma=False, cc=False):
        fn = self._freeze(fn)
        if cc:
            track = "dma_cc"
        elif dma:
            k = self.ndma.get(stream, 0)
            self.ndma[stream] = k + 1
            track = f"dma_{stream}#{k % self.NSUB}"
        else:
            track = stream
        self.ops.append((stream, track, fn, tuple(r), tuple(w)))

    def pe(self, fn, r=(), w=()): self.add("pe", fn, r, w)
    def act(self, fn, r=(), w=()): self.add("act", fn, r, w)
    def dve(self, fn, r=(), w=()): self.add("dve", fn, r, w)
    def pool(self, fn, r=(), w=()): self.add("pool", fn, r, w)

    def dma(self, stream, out, in_, r=(), w=(), **kw):
        self.add(stream, lambda e: e.dma_start(out=out, in_=in_, **kw), r, w, dma=True)

    def finalize_and_emit(self):
        nc = self.nc
        ops = self.ops
        n = len(ops)
        writers = {}
        readers = {}
        prev_on = {}
        deps = [None] * n
        signal = [False] * n
        qof = lambda t: t.split("#")[0]
        for i, (stream, track, fn, R, W) in enumerate(ops):
            d = set()
            is_dma = track.startswith("dma_")
            if is_dma:
                j = prev_on.get(track)
                if j is not None:
                    d.add(j)
                prev_on[track] = i
            for res in R:
                for tj, j in writers.get(res, {}).items():
                    if tj != track or is_dma or track != "pe":
                        d.add(j)
            for res in W:
                lazy = res.startswith("~")
                for tj, j in writers.get(res, {}).items():
                    if lazy:
                        if qof(tj) != qof(track):
                            d.add(j)
                    elif tj != track or is_dma:
                        d.add(j)
                for tj, j in readers.get(res, {}).items():
                    if lazy:
                        if qof(tj) != qof(track):
                            d.add(j)
                    elif tj != track or is_dma:
                        d.add(j)
            for res in R:
                readers.setdefault(res, {})[track] = i
            for res in W:
                if res.startswith("~"):
                    writers.setdefault(res, {})[track] = i
                else:
                    writers[res] = {track: i}
                    readers[res] = {}
            d.discard(i)
            deps[i] = d
            for j in d:
                signal[j] = True
        tracks = sorted({o[1] for o in ops})
        cnt = {t: 0 for t in tracks}
        val = [0] * n
        for i, (stream, track, fn, R, W) in enumerate(ops):
            if track == "dma_cc":
                cnt[track] += 1
                val[i] = cnt[track]
            elif track.startswith("dma_"):
                cnt[track] += 16
                val[i] = cnt[track]
            elif signal[i]:
                cnt[track] += 1
                val[i] = cnt[track]
        known = {s: {t: 0 for t in tracks} for s in self.STREAMS}
        waits = [None] * n
        for i, (stream, track, fn, R, W) in enumerate(ops):
            need = {}
            for j in deps[i]:
                tj = ops[j][1]
                need[tj] = max(need.get(tj, 0), val[j])
            wl = []
            for tj, v in need.items():
                if v > known[stream][tj]:
                    wl.append((tj, v))
                    known[stream][tj] = v
            waits[i] = wl
        self.stats = dict(cnt)
        by_stream = {s: [] for s in self.STREAMS}
        for i, o in enumerate(ops):
            by_stream[o[0]].append(i)
        with ExitStack() as es:
            sems = {t: es.enter_context(nc.semaphore("s_" + t.replace("#", "_"))) for t in tracks}
            block = es.enter_context(nc.Block())

            def run(eng, stream):
                for i in by_stream[stream]:
                    _, track, fn, R, W = ops[i]
                    for tj, v in waits[i]:
                        eng.wait_ge(sems[tj], v)
                    inst = fn(eng)
                    if track == "dma_cc":
                        inst.then_inc(sems[track], 1)
                    elif track.startswith("dma_"):
                        inst.then_inc(sems[track], 16)
                    elif signal[i]:
                        inst.then_inc(sems[track], 1)
                if stream == "sp":
                    for t in tracks:
                        if t.startswith("dma_") and cnt[t] > known[stream][t]:
                            eng.wait_ge(sems[t], cnt[t])

            @block.tensor
            def _(e): run(e, "pe")

            @block.scalar
            def _(e): run(e, "act")

            @block.vector
            def _(e): run(e, "dve")

            @block.gpsimd
            def _(e): run(e, "pool")

            @block.sync
            def _(e): run(e, "sp")


def build_program(do_sample=True, nm=4, nlayers=DEPTH):
    nc = bass.Bass("TRN2", target_bir_lowering=False)
    es = ExitStack()
    din = lambda name, shape, dt=F32: nc.dram_tensor(name, list(shape), dt, kind="ExternalInput").ap()
    dout = lambda name, shape: nc.dram_tensor(name, list(shape), F32, kind="ExternalOutput").ap()
    dscr = lambda name, shape, dt=BF16: nc.dram_tensor(name, list(shape), dt, kind="Internal").ap()
    xp = din("xp", [SEQ, D_MODEL])
    wu = din("wu", [DEPTH, NU, 128, 8, 512])
    wo = din("wo", [DEPTH, 6, 64, 4, 1024])
    gcol_d = din("gcol", [128, DEPTH * 8])
    fg_d = din("fg", [1, D_MODEL])
    bf_d = din("bfb", [1, DEPTH * 8])
    t5_d = din("t5", [32, 8])
    crel_d = din("crel", [DEPTH, 3, 128, 8])
    oh5_d = din("oh5", [32, LTAB])
    ohc_d = din("ohc", [3, 128, LTAB])
    cst_d = din("cst", [128, 8 * 128])
    y_q = dout("y_q", [2048, D_MODEL])
    xq = din("xq", [4, 512, D_MODEL]); xprev = din("xprev", [4, 512, D_MODEL])
    pc_d = din("pcore", [128, 1024])
    iota_d = din("iota5", [128, 512])
    o_ak = dout("o_ak", [DEPTH, SEQ, 512]); o_av = dout("o_av", [DEPTH, SEQ, 512])
    o_lf = dout("o_lf", [DEPTH, SEQ, 8])
    o_bk = dout("o_bk", [DEPTH, SEQ, 128]); o_bv = dout("o_bv", [DEPTH, SEQ, 128])
    o_ik = dout("o_ik", [DEPTH, SEQ, 32])
    o_ck = dout("o_ck", [DEPTH, 512, 512]); o_cv = dout("o_cv", [DEPTH, 512, 512])
    wub = dscr("wub", [DEPTH, NU, 128, 8, 512])
    wob = dscr("wob", [DEPTH, 6, 64, 4, 1024])
    hp1q = dscr("hp1q", [2048, D_MODEL], F32)
    hpg = dscr("hpg", [SEQ, D_MODEL], F32)
    ccs = dscr("ccs", [256, D_MODEL], F32)
    COMBS = dscr("combs", [4, 17, 2, 128, 512])
    AMASK = dscr("amask", [16, 128, 512])
    ccd = dscr("ccd", [1024, D_MODEL], F32)
    S_KCL = dscr("scr_kcl", [DEPTH, 8, 64, 1024]); S_VCL = dscr("scr_vcl", [DEPTH, 1024, 512])
    tab5 = dscr("tab5", [8, LTAB], F32)
    tabc = dscr("tabc", [DEPTH, 8, LTAB], F32)
    S_KA = dscr("scr_s_ka", [DEPTH, 8, 64, SEQ]); S_KC = dscr("scr_s_kc", [DEPTH, 8, 64, SEQ])
    S_KB = dscr("scr_s_kb", [DEPTH, 2, 64, SEQ]); S_IK = dscr("scr_s_ik", [DEPTH, 64, SEQ])
    S_VA = dscr("scr_s_va", [DEPTH, SEQ, 512]); S_VC = dscr("scr_s_vc", [DEPTH, SEQ, 512])
    S_VB = dscr("scr_s_vb", [DEPTH, SEQ, 128])

    xs = din("xs", [DEC_SEQ, D_MODEL])
    ca_k = din("ca_k", [DEPTH, PAST, 512]); ca_v = din("ca_v", [DEPTH, PAST, 512]); ca_lf = din("ca_lf", [DEPTH, PAST, 8])
    cb_k = din("cb_k", [DEPTH, PAST, 128]); cb_v = din("cb_v", [DEPTH, PAST, 128]); cb_ik = din("cb_ik", [DEPTH, PAST, 32])
    cc_k = din("cc_k", [DEPTH, 512, 512]); cc_v = din("cc_v", [DEPTH, 512, 512])
    y_s = dout("y_s", [DEC_SEQ, D_MODEL])
    s_ak = dout("s_ak", [DEPTH, DEC_SEQ, 512]); s_av = dout("s_av", [DEPTH, DEC_SEQ, 512]); s_lf = dout("s_lf", [DEPTH, DEC_SEQ, 8])
    s_bk = dout("s_bk", [DEPTH, DEC_SEQ, 128]); s_bv = dout("s_bv", [DEPTH, DEC_SEQ, 128]); s_ik = dout("s_ik", [DEPTH, DEC_SEQ, 32])
    s_ck = dout("s_ck", [DEPTH, DEC_SEQ, 512]); s_cv = dout("s_cv", [DEPTH, DEC_SEQ, 512])
    hs1 = dscr("hs1", [DEC_SEQ, D_MODEL], F32)
    MBS = [dscr(f"mbs{i}", [128, SEQ]) for i in range(4)]
    SS_KA = dscr("ss_ka", [DEPTH, 8, 64, 1152]); SS_KC = dscr("ss_kc", [DEPTH, 8, 64, 640])
    SS_KB = dscr("ss_kb", [DEPTH, 2, 64, 1152]); SS_IK = dscr("ss_ik", [DEPTH, 64, 1152])
    SS_VA = dscr("ss_va", [DEPTH, 1152, 512]); SS_VC = dscr("ss_vc", [DEPTH, 640, 512]); SS_VB = dscr("ss_vb", [DEPTH, 1152, 128])

    sb = lambda name, shape, dt: es.enter_context(nc.sbuf_tensor(name, list(shape), dt))
    wring = [sb(f"wring{i}", [128, 8, 512], BF16) for i in range(2)]
    SC = sb("SC", [128, 8192], F32)
    junk = sb("junk", [128, 8192], BF16)
    hT = sb("hT", [128, 8, 512], BF16)
    Q = sb("Q", [65, 8, 512], BF16)
    zg = {t: sb("zg" + t, [64, 8, 512], BF16) for t in "abc"}
    iqT = sb("iqT", [64, 4, 512], BF16)
    kst = sb("kst", [64, 8, 512], BF16)
    xt = [sb(f"xt{i}", [128, 1024], F32) for i in range(2)]
    xn = sb("xn", [128, 1024], BF16)
    st = [sb(f"st{i}", [128, 512], F32) for i in range(2)]
    vst = [sb(f"vst{i}", [128, 512], BF16) for i in range(2)]
    kbuf = [sb(f"kbuf{i}", [65, 4, 512], BF16) for i in range(2)]
    vbuf = [sb(f"vbuf{i}", [128, 4, 4, 65], BF16) for i in range(2)]
    Pt = [sb(f"Pt{i}", [128, 512], BF16) for i in range(4)]
    Mb = [sb(f"Mb{i}", [128, 512], BF16) for i in range(2)]
    Rr = [sb(f"Rr{i}", [128, 512], F32) for i in range(3)]
    ikbuf = [sb(f"ikbuf{i}", [64, 512], BF16) for i in range(2)]
    b5 = sb("b5", [128, 2, 8, 128], BF16)
    bc = sb("bc", [128, 2, 8, 128], BF16)
    cstf = sb("cstf", [128, 8, 128], F32)
    identb = sb("identb", [128, 128], BF16)
    i4b = sb("i4b", [128, 4, 128], BF16)
    ma0b = sb("ma0b", [128, 128], BF16); cm0b = sb("cm0b", [128, 128], BF16); cm4b = sb("cm4b", [128, 128], BF16)
    cstore = sb("cstore", [128, 64, 8], F32)
    nbias = sb("nbias", [128, 64, 8], F32)
    gcol = sb("gcol_s", [128, DEPTH * 8], F32)
    fgb = sb("fgb", [128, D_MODEL], F32)
    bfb = sb("bfb_s", [128, DEPTH * 8], F32)
    small = sb("small", [128, 64], F32)
    cntb = sb("cntb", [128, NBIS], F32)
    wabs = sb("wabs", [128, 4, 8], F32); wsgn = sb("wsgn", [128, 4, 8], F32)
    lfb = sb("lfb", [128, 8], F32)
    tot = sb("tot", [1, 8], F32)
    tots = sb("tots", [1, 17, 8], F32)
    totbc = sb("totbc", [128, 8], F32)
    lf4 = sb("lf4", [128, 4, 8], F32)
    cown = sb("cown", [128, 4, 8], F32)
    xacc = sb("xacc", [128, D_MODEL], F32)
    pcore = sb("pcore_s", [128, 1024], F32)
    iota5 = sb("iota5_s", [128, 512], F32)
    comb = [sb(f"comb{i}", [128, 4, 128], BF16) for i in range(2)]
    ones1 = sb("ones1", [65, 128], F32)
    cbc = sb("cbc", [128, 8], F32)
    rq = sb("rq", [128, 4, 8], F32)
    rT = sb("rT", [8, 512], BF16)
    rden = sb("rden", [65, 512], F32)
    otmp = sb("otmp", [64, 512], F32)
    hank = sb("hank", [128, 128], F32)
    t5s = sb("t5s", [32, 8], F32); oh5s = sb("oh5s", [32, LTAB], F32)
    crs = sb("crs", [128, 3, 8], F32); ohcs = sb("ohcs", [128, 3, LTAB], F32)
    tabs = sb("tabs", [8, LTAB], F32)
    ps = [es.enter_context(nc.psum_tensor(f"ps{i}", [128, 512], F32)) for i in range(8)]
    psn = [f"ps{i}" for i in range(8)]

    P = Prog(nc)
    _early = {}

    def nxt_early(key, n):
        v = _early.get(key, 0)
        _early[key] = v + 1
        return v % n

    IDENT = cstf[:, 0, :]; JM = cstf[:, 1, :]; TRI = cstf[:, 2, :]; E0ROW = cstf[:, 3, :]
    ADM = cstf[:, 7, :]
    E127 = cstf[:, 1, 0:1]

    P.dma("sp", pcore[:], pc_d, w=["qrelb", "krel", "qlimc", "sel01", "selb", "pvb"])
    P.dma("sp", iota5[:], iota_d, w=["iota5"])
    qrelb = pcore[:, 0:512]; krel = pcore[:, 512:528]; qlimc = pcore[:, 528:544]; sel01 = pcore[:, 544:680]
    selb = pcore[:, 680:684]; pvb = pcore[:, 684:685]
    P.dma("sp", cstf[:].rearrange("p a b -> p (a b)"), cst_d, w=["cstf"])
    P.dma("sp", gcol[:], gcol_d, w=["gcol"])
    P.dma("sp", fgb[:], fg_d.to_broadcast([128, D_MODEL]) if hasattr(fg_d, "to_broadcast") else bass.AP(fg_d.tensor, 0, [[0, 128], [1, D_MODEL]]), w=["fgb"])
    P.dma("sp", bfb[:], bass.AP(bf_d.tensor, 0, [[0, 128], [1, DEPTH * 8]]), w=["bfb"])
    P.dma("sp", t5s[:], t5_d, w=["t5s"])
    P.dma("sp", oh5s[:], oh5_d, w=["oh5s"])
    P.dma("sp", ohcs[:], ohc_d.rearrange("c p l -> p c l"), w=["ohcs"])
    P.dve(lambda e: e.tensor_copy(identb[:], IDENT), r=["cstf"], w=["identb"])
    for k in range(4):
        P.dve(lambda e, k=k: e.tensor_copy(i4b[:, k, :], IDENT), r=["cstf"], w=["i4b"])
    P.dve(lambda e: e.tensor_copy(ma0b[:], cstf[:, 4, :]), r=["cstf"], w=["ma0b"])
    P.dve(lambda e: e.tensor_copy(cm0b[:], cstf[:, 5, :]), r=["cstf"], w=["cm0b"])
    P.dve(lambda e: e.tensor_copy(cm4b[:], cstf[:, 6, :]), r=["cstf"], w=["cm4b"])
    onesf = cstf[:, 4, :]
    P.dve(lambda e: e.memset(onesf, 1.0), r=["ma0b"], w=["cstf", "onesf"])
    P.dve(lambda e: e.memset(ones1[:], 1.0), w=["ones1"])
    for i in range(2):
        P.pool(lambda e, i=i: e.memset(kbuf[i][:], 1.0), w=[f"kbuf{i}"])
        P.pool(lambda e, i=i: e.memset(vbuf[i][:], 1.0), w=[f"vbuf{i}"])

    stages = [(SC[:, 0:4096], junk[:, 0:4096], "SCa", "junka"), (SC[:, 4096:8192], junk[:, 4096:8192], "SCb", "junkb")]
    sk = 0
    for l in range(nlayers):
        for u in range(NU):
            s32, s16, n32, n16 = stages[sk % 2]
            sk += 1
            stg = s32.rearrange("p (c n) -> p c n", c=8)
            stgb = s16.rearrange("p (c n) -> p c n", c=8)
            P.dma("sp", stg, wu[l, u], w=[n32])
            P.act(lambda e: e.activation(stgb, stg, AF.Copy), r=[n32], w=[n16])
            P.dma("pool", wub[l, u], stgb, r=[n16], w=[f"wub{l}_{u}"])
        for u in range(6):
            s32, s16, n32, n16 = stages[sk % 2]
            sk += 1
            so = s32[0:64].rearrange("p (c n) -> p c n", c=4)
            sob = s16[0:64].rearrange("p (c n) -> p c n", c=4)
            P.dma("sp", so, wo[l, u], w=[n32])
            P.act(lambda e: e.activation(sob, so, AF.Copy), r=[n32], w=[n16])
            P.dma("pool", wob[l, u], sob, r=[n16], w=[f"wob{l}_{u}"])
    P.dve(lambda e: e.memset(small[:, 50:51], 0.0), w=["SC", "junk", "SCa", "SCb", "junka", "junkb"])

    def build_tab(lhs_list, rhs_list, dst, rnames):
        for i, (a, b) in enumerate(zip(lhs_list, rhs_list)):
            P.pe(lambda e, a=a, b=b, i=i: e.matmul(ps[0][0:8, 0:LTAB], a, b, start=(i == 0), stop=(i == len(lhs_list) - 1)),
                 r=rnames, w=["ps0"])
        P.dve(lambda e: e.tensor_copy(tabs[:], ps[0][0:8, 0:LTAB]), r=["ps0"], w=["tabs"])
        P.dma("sp", dst, tabs[:], r=["tabs"], w=["tabdram"])

    def build_toeplitz(tab_ap2d, dst_tile):
        for k in range(2):
            for h in range(8):
                b0 = 128 * k
                src = bass.AP(tab_ap2d.tensor, tab_ap2d.offset + h * LTAB + b0, [[1, 128], [1, 128]])
                P.dma("sp", hank[:], src, r=["tabdram"], w=["hank"])
                P.pe(lambda e: e.matmul(ps[1][:, 0:128], JM, hank[:], start=True, stop=True), r=["hank", "cstf"], w=["ps1"])
                P.dve(lambda e, k=k, h=h: e.tensor_copy(dst_tile[:, k, h, :], ps[1][:, 0:128]), r=["ps1"], w=["btile"])

    build_tab([t5s[:]], [oh5s[:]], tab5, ["t5s", "oh5s"])
    build_toeplitz(tab5, b5)
    for qb in range(4):
        for rr in range(17):
            if (rr - 1 - qb) % 4 not in (0, 3):
                continue
            for jj in range(2):
                ci = nxt_early("cmb", 2)
                sc0 = sel01[:, (qb * 17 + rr) * 2:(qb * 17 + rr) * 2 + 1]
                sc1 = sel01[:, (qb * 17 + rr) * 2 + 1:(qb * 17 + rr) * 2 + 2]
                P.dve(lambda e: e.tensor_scalar(comb[ci][:, :, :], b5[:, 0, 4 * jj:4 * jj + 4, :], sc0, None, ALU.mult), r=["btile", "sel01"], w=[f"comb{ci}"])
                P.dve(lambda e: e.scalar_tensor_tensor(comb[ci][:, :, :], b5[:, 1, 4 * jj:4 * jj + 4, :], sc1, comb[ci][:, :, :], ALU.mult, ALU.add),
                      r=["btile", "sel01", f"comb{ci}"], w=[f"comb{ci}"])
                P.dma("pool", COMBS[qb, rr, jj], comb[ci][:].rearrange("p a b -> p (a b)"), r=[f"comb{ci}"], w=["~combs"])
    for rr in range(16):
        mi = nxt_early("mb", 2)
        P.dve(lambda e: e.tensor_scalar(Mb[mi][:, :], qrelb[:, :], krel[:, rr:rr + 1], NEG, ALU.is_lt, ALU.mult), r=["qrelb", "krel"], w=[f"Mb{mi}"])
        P.dma("pool", AMASK[rr], Mb[mi][:, :], r=[f"Mb{mi}"], w=["~amask"])

    wk = [0]

    def load_w(l, u):
        i = wk[0] % 2
        wk[0] += 1
        P.dma("sp", wring[i][:], wub[l, u], r=[f"wub{l}_{u}"], w=[f"wring{i}"])
        return wring[i], f"wring{i}"

    def load_wo(l, u):
        i = wk[0] % 2
        wk[0] += 1
        dst = wring[i][0:64].rearrange("p c n -> p (c n)").rearrange("p (c n) -> p c n", c=4)
        P.dma("sp", dst, wob[l, u], r=[f"wob{l}_{u}"], w=[f"wring{i}"])
        return dst, f"wring{i}"

    rot = {"s": 0, "pt": 0, "kv": 0, "st": 0, "x": 0, "ik": 0, "rr": 0, "mb": 0, "ips": 0, "cmb": 0}

    def nxt(key, n):
        v = rot[key] % n
        rot[key] += 1
        return v

    def norm_block(l, xsrc_ap, tb, nrow=128, rname=None, sb_src=None):
        if sb_src is not None:
            X, xname = sb_src
        else:
            xi = nxt("x", 2)
            X = xt[xi]; xname = f"xt{xi}"
        if sb_src is None:
            P.dma("sp", X[0:nrow, :], xsrc_ap, r=(list(rname) if isinstance(rname, (list, tuple)) else ([rname] if rname else [])), w=[xname])
        P.act(lambda e: e.activation(junk[0:nrow, 0:1024], X[0:nrow, :], AF.Square, accum_out=small[0:nrow, 0:1]),
              r=[xname], w=["junk", "small0"])
        P.dve(lambda e: e.tensor_scalar(small[0:nrow, 1:2], small[0:nrow, 0:1], 1.0 / D_MODEL, EPS, ALU.mult, ALU.add), r=["small0"], w=["small1"])
        P.act(lambda e: e.activation(small[0:nrow, 2:3], small[0:nrow, 1:2], AF.Sqrt), r=["small1"], w=["small2"])
        P.dve(lambda e: e.reciprocal(small[0:nrow, 3:4], small[0:nrow, 2:3]), r=["small2"], w=["small3"])
        P.dve(lambda e: e.tensor_scalar(xn[0:nrow, :], X[0:nrow, :], small[0:nrow, 3:4], None, ALU.mult), r=[xname, "small3"], w=["xn"])
        psb = ps[7].bitcast(BF16)
        for c in range(8):
            P.pe(lambda e, c=c: e.transpose(psb[:, c * 128:c * 128 + nrow], xn[0:nrow, c * 128:(c + 1) * 128], identb[0:nrow, 0:nrow]),
                 r=["xn", "identb"], w=["ps7"])
        for c in range(8):
            P.dve(lambda e, c=c: e.tensor_scalar(hT[:, c, tb * 128:tb * 128 + nrow], psb[:, c * 128:c * 128 + nrow],
                                                 gcol[:, l * 8 + c:l * 8 + c + 1], None, ALU.mult),
                  r=["ps7", "gcol"], w=["hT"])

    def fm_unit(l, uname, ntok, evac, blocks=tuple(range(8)), wres=None):
        W, wn = wres if wres is not None else load_w(l, UNITS.index(uname))
        for j in blocks:
            si = nxt("s", 4)
            for c in range(8):
                P.pe(lambda e, j=j, c=c, si=si: e.matmul(ps[si][0:64, 0:ntok], W[:, c, j * 64:(j + 1) * 64], hT[:, c, 0:ntok],
                                                        start=(c == 0), stop=(c == 7)), r=[wn, "hT"], w=[psn[si]])
            evac(j, ps[si], psn[si])

    def finish_head(O, oname, zt, zname, hsel, ncol, csl):
        if isinstance(hsel, tuple):
            nh_ = hsel[1] - hsel[0]
            zv = zt[:, hsel[0]:hsel[1], csl]
            ov = otmp[:, 0:ncol].rearrange("p (h q) -> p h q", h=nh_)
            Ov = O[0:64, 0:ncol].rearrange("p (h q) -> p h q", h=nh_)
        else:
            zv = zt[:, hsel, csl]
            ov = otmp[:, 0:ncol]
            Ov = O[0:64, 0:ncol]
        P.act(lambda e: e.activation(rden[64:65, 0:ncol], O[64:65, 0:ncol], AF.Ln), r=[oname], w=["rden"])
        P.act(lambda e: e.activation(rden[64:65, 0:ncol], rden[64:65, 0:ncol], AF.Exp, scale=-1.0), r=["rden"], w=["rden"])
        P.dve(lambda e: e.tensor_tensor(ov, Ov, zv, ALU.mult), r=[oname, zname], w=["otmp"])
        bi_ = nxt("s", 4)
        P.pe(lambda e: e.matmul(ps[bi_][0:64, 0:ncol], ones1[64:65, 0:64], rden[64:65, 0:ncol], start=True, stop=True),
             r=["rden", "ones1"], w=[psn[bi_]])
        bv = ps[bi_][0:64, 0:ncol].rearrange("p (h q) -> p h q", h=nh_) if isinstance(hsel, tuple) else ps[bi_][0:64, 0:ncol]
        P.dve(lambda e: e.tensor_tensor(zv, ov, bv, ALU.mult), r=["otmp", psn[bi_]], w=[zname])

    def load_kv(Ksrc, Vsrc, h0, nh, k0, nk, krows_name):
        i = nxt("kv", 2)
        P.dma("sp", kbuf[i][0:64, 0:nh, 0:nk], Ksrc[h0:h0 + nh, :, k0:k0 + nk].rearrange("h d k -> d h k"), r=[krows_name], w=[f"kbuf{i}"])
        nb = (nk + 127) // 128
        for b in range(nb):
            n = min(128, nk - b * 128)
            P.dma("sp", vbuf[i][0:n, b, 0:nh, 0:64],
                  Vsrc[k0 + b * 128:k0 + b * 128 + n, h0 * 64:(h0 + nh) * 64].rearrange("k (h d) -> k h d", h=nh),
                  r=[krows_name], w=[f"vbuf{i}"])
        return i

    class Attn:
        def __init__(self):
            self.pend = []

        def tile(self, t):
            si = nxt("s", 4)
            S = ps[si]; n = t["n"]; qlo, qhi = t["qlo"], t["qhi"]
            nadd = len(t["adds"])
            P.pe(lambda e: e.matmul(S[0:n, qlo:qhi], t["kT"], t["qap"], start=True, stop=(nadd == 0)),
                 r=t["names"] + ["Q"], w=[psn[si]])
            for ai, (clo, chi, la, ra, an) in enumerate(t["adds"]):
                P.pe(lambda e, clo=clo, chi=chi, la=la, ra=ra, ai=ai: e.matmul(S[0:n, clo:chi], la, ra, start=False, stop=(ai == nadd - 1)),
                     r=an, w=[psn[si]])
            pi = nxt("pt", 4)
            if t["bias"] is not None:
                P.act(lambda e: e.activation(Pt[pi][0:n, qlo:qhi], S[0:n, qlo:qhi], AF.Exp, bias=t["bias"]),
                      r=[psn[si], "nbias"], w=[f"Pt{pi}"])
            else:
                P.act(lambda e: e.activation(Pt[pi][0:n, qlo:qhi], S[0:n, qlo:qhi], AF.Exp), r=[psn[si]], w=[f"Pt{pi}"])
            t["pi"] = pi
            self.pend.append(t)
            if len(self.pend) > 2:
                self.pv(self.pend.pop(0))

        def pv(self, t):
            n = t["n"]; qlo, qhi = t["qlo"], t["qhi"]; pi = t["pi"]; O = t["O"]
            P.pe(lambda e: e.matmul(O[0:65, qlo:qhi], t["v"], Pt[pi][0:n, qlo:qhi], start=t["first"], stop=t["last"]),
                 r=[f"Pt{pi}"] + t["names"], w=[t["oname"]])

        def flush(self):
            while self.pend:
                self.pv(self.pend.pop(0))


    for l in range(nlayers):
        P.pool(lambda e: e.memset(crs[:], 0.0), w=["crs"])
        P.dma("sp", crs[:], crel_d[l].rearrange("c p h -> p c h"), w=["crs"])
        build_tab([crs[:, c, :] for c in range(3)], [ohcs[:, c, :] for c in range(3)], tabc[l], ["crs", "ohcs"])
        build_toeplitz(tabc[l], bc)
        P.dve(lambda e: e.memset(tot[:], 0.0), w=["tot"])
        P.dve(lambda e: e.memset(tots[:], 0.0), w=["tots"])
        P.dve(lambda e: e.memset(totbc[:], 0.0), w=["totbc"])
        KAl, KBl, IKl, VAl, VBl = S_KA[l], S_KB[l], S_IK[l], S_VA[l], S_VB[l]
        KCl, VCl = S_KCL[l], S_VCL[l]
        hist = f"~hist{l}"
        chist = f"~chist{l}"
        deferred_exchange = []

        def grow(gp):
            if l == 0:
                return xp[gp * 512:(gp + 1) * 512, :]
            return hpg[gp * 512:(gp + 1) * 512, :]

        def tm_unit(uname, handler, wres=None, cols=(0, 512)):
            W, wn = wres if wres is not None else load_w(l, UNITS.index(uname))
            c0_, c1_ = cols
            for tb in range(4):
                si = nxt("s", 4)
                for c in range(8):
                    P.pe(lambda e: e.matmul(ps[si][:, 0:c1_ - c0_], hT[:, c, tb * 128:(tb + 1) * 128], W[:, c, c0_:c1_], start=(c == 0), stop=(c == 7)),
                         r=[wn, "hT"], w=[psn[si]])
                k = nxt("st", 2)
                S_ = st[k]; sn = f"st{k}"
                P.act(lambda e: e.activation(S_[:, c0_:c1_], ps[si][:, 0:c1_ - c0_], AF.Copy), r=[psn[si]], w=[sn])
                handler(tb, S_, sn, k)

        def logf_of(S_, sn):
            P.dve(lambda e: e.tensor_tensor(lfb[:], S_[:, 288:296], bfb[:, l * 8:(l + 1) * 8], ALU.add), r=[sn, "bfb"], w=["lfb"])
            P.act(lambda e: e.activation(lfb[:], lfb[:], AF.Exp, scale=-1.0), r=["lfb"], w=["lfb"])
            P.act(lambda e: e.activation(lfb[:], lfb[:], AF.Ln, bias=1.0), r=["lfb"], w=["lfb"])
            P.dve(lambda e: e.tensor_scalar(lfb[:], lfb[:], -1.0, None, ALU.mult), r=["lfb"], w=["lfb"])

        def cum_into(dst_ap, dname):
            P.pe(lambda e: e.matmul(ps[5][:, 0:8], TRI, lfb[:], start=True, stop=False), r=["lfb", "cstf"], w=["ps5"])
            P.pe(lambda e: e.matmul(ps[5][:, 0:8], ones1[0:1, 0:128], tot[0:1, :], start=False, stop=True), r=["tot", "ones1"], w=["ps5"])
            P.dve(lambda e: e.tensor_copy(dst_ap, ps[5][:, 0:8]), r=["ps5"], w=[dname])
            P.pe(lambda e: e.matmul(ps[5][0:1, 8:16], E127, dst_ap, start=True, stop=True), r=[dname, "cstf"], w=["ps5"])
            P.dve(lambda e: e.tensor_copy(tot[:], ps[5][0:1, 8:16]), r=["ps5"], w=["tot"])

        def evac_k_to(dst3, k0, hname):
            def f(j, pt, pn):
                P.act(lambda e: e.activation(kst[:, j, :], pt[0:64, :], AF.Copy), r=[pn], w=["kst"])
                if j == 7:
                    P.dma("pool", dst3[:, :, k0:k0 + 512].rearrange("h d k -> d h k"), kst[:], r=["kst"], w=[hname])
            return f

        def evac_q(scale):
            def f(j, pt, pn):
                P.act(lambda e: e.activation(Q[0:64, j, :], pt[0:64, :], AF.Copy, scale=scale), r=[pn], w=["Q"])
            return f

        def evac_z(zt, zn):
            def f(j, pt, pn):
                P.act(lambda e: e.activation(zt[:, j, :], pt[0:64, :], AF.Silu), r=[pn], w=[zn])
            return f

        scb = SC.bitcast(BF16)
        kres = {}
        for ui, un in enumerate(("tka", "tva", "tb", "ka")):
            v = scb[:, ui * 4096:(ui + 1) * 4096].rearrange("p (c n) -> p c n", c=8)
            P.dma("sp", v, wub[l, UNITS.index(un)], r=[f"wub{l}_{UNITS.index(un)}"], w=["SC"])
            kres[un] = (v, "SC")
        v = junk[:, 4096:8192].rearrange("p (c n) -> p c n", c=8)
        P.dma("sp", v, wub[l, UNITS.index("bx")], r=[f"wub{l}_{UNITS.index('bx')}"], w=["junk", "junkW"])
        kres["bx"] = (v, "junkW")
        for gp in range(4 * nm):
            t0 = gp * 512
            src = grow(gp)
            for tb in range(4):
                norm_block(l, src[tb * 128:(tb + 1) * 128, :], tb, rname=("~hpgw" if l > 0 else None))

            def h_kv(uname):
                def f(tb, S_, sn, k):
                    r0 = t0 + tb * 128
                    dst = {"tka": o_ak, "tva": o_av, "tkc": o_ck, "tvc": o_cv}[uname]
                    if uname in ("tka", "tva"):
                        P.dma("pool", dst[l, r0:r0 + 128, :], S_[:], r=[sn])
                    else:
                        P.dma("pool", dst[l, r0 - (SEQ - 512):r0 - (SEQ - 512) + 128, :], S_[:], r=[sn])
                    if uname == "tva":
                        V_, vn = vst[k], f"vst{k}"
                        P.dve(lambda e: e.tensor_copy(V_[:], S_[:]), r=[sn], w=[vn])
                        P.dma("pool", VAl[r0:r0 + 128, :], V_[:], r=[vn], w=[hist])
                return f

            def h_tb(tb, S_, sn, k):
                r0 = t0 + tb * 128
                P.dma("pool", o_bk[l, r0:r0 + 128, :], S_[:, 0:128], r=[sn])
                P.dma("pool", o_bv[l, r0:r0 + 128, :], S_[:, 128:256], r=[sn])
                P.dma("pool", o_ik[l, r0:r0 + 128, :], S_[:, 256:288], r=[sn])
                V_, vn = vst[k], f"vst{k}"
                P.dve(lambda e: e.tensor_copy(V_[:, 0:128], S_[:, 128:256]), r=[sn], w=[vn])
                P.dma("pool", VBl[r0:r0 + 128, :], V_[:, 0:128], r=[vn], w=[hist])
                P.dve(lambda e: e.tensor_tensor(lf4[:, tb, :], S_[:, 288:296], bfb[:, l * 8:(l + 1) * 8], ALU.add), r=[sn, "bfb"], w=["lf4"])

            tm_unit("tka", h_kv("tka"), wres=kres["tka"])
            tm_unit("tva", h_kv("tva"), wres=kres["tva"])
            tm_unit("tb", h_tb, wres=kres["tb"])
            lf4f = lf4[:].rearrange("p b h -> p (b h)")
            P.act(lambda e: e.activation(lf4f, lf4f, AF.Exp, scale=-1.0), r=["lf4"], w=["lf4"])
            P.act(lambda e: e.activation(lf4f, lf4f, AF.Ln, bias=1.0), r=["lf4"], w=["lf4"])
            P.dve(lambda e: e.tensor_scalar(lf4f, lf4f, -1.0, None, ALU.mult), r=["lf4"], w=["lf4"])
            P.dma("pool", o_lf[l, t0:t0 + 512, :].rearrange("(b p) h -> p b h", p=128), lf4[:], r=["lf4"])
            if gp == NG - 1:
                tm_unit("tkc", h_kv("tkc"))
                tm_unit("tvc", h_kv("tvc"))
            fm_unit(l, "ka", 512, evac_k_to(KAl, t0, hist), wres=kres["ka"])
            for b_ in range(4):
                for b2 in range(b_ + 1):
                    P.pe(lambda e: e.matmul(ps[5][:, b_ * 8:(b_ + 1) * 8], (TRI if b2 == b_ else onesf[:, :]), lf4[:, b2, :], start=(b2 == 0), stop=(b2 == b_)),
                         r=["lf4", "cstf", "onesf"], w=["ps5"])
            for b2 in range(4):
                P.pe(lambda e: e.matmul(ps[5][:, 32:40], onesf[:, :], lf4[:, b2, :], start=(b2 == 0), stop=(b2 == 3)), r=["lf4", "onesf"], w=["ps5"])
            for b_ in range(4):
                P.dve(lambda e: e.tensor_tensor(cstore[:, 4 * gp + b_, :], ps[5][:, b_ * 8:(b_ + 1) * 8], totbc[:, :], ALU.add), r=["ps5", "totbc"], w=["cstore"])
            P.dve(lambda e: e.tensor_tensor(totbc[:, :], ps[5][:, 32:40], totbc[:, :], ALU.add), r=["ps5", "totbc"], w=["totbc"])
            P.dve(lambda e: e.tensor_copy(tots[0:1, gp + 1, :], totbc[0:1, :]), r=["totbc"], w=["tots"])

            def evac_bx_k(j, pt, pn):
                if j < 2:
                    P.act(lambda e: e.activation(kst[:, j, :], pt[0:64, :], AF.Copy), r=[pn], w=["kst"])
                    if j == 1:
                        P.dma("pool", KBl[:, :, t0:t0 + 512].rearrange("h d k -> d h k"), kst[:, 0:2, :], r=["kst"], w=[hist])
                elif j == 6:
                    P.act(lambda e: e.activation(kst[:, 2, :], pt[0:64, :], AF.Copy), r=[pn], w=["kst"])
                    P.dma("pool", IKl[:, t0:t0 + 512], kst[:, 2, :], r=["kst"], w=[hist])
            fm_unit(l, "bx", 512, evac_bx_k, blocks=(0, 1, 6), wres=kres["bx"])

        for m in range(nm):
            own = (xq[m] if l == 0 else hp1q[m * 512:(m + 1) * 512, :])
            own_r = (None if l == 0 else [f"hp1qc{2 * m}", f"hp1qc{2 * m + 1}"])
            for part in range(2):
                for tb in range(4):
                    if part == 1:
                        norm_block(l, own[tb * 128:(tb + 1) * 128, :], tb, rname=own_r)
                    elif l == 0:
                        norm_block(l, xprev[m][tb * 128:(tb + 1) * 128, :], tb)
                    else:
                        first = True
                        for r in range(4):
                            gq = 4 * m - 1 + r
                            if gq < 0:
                                continue
                            xi = nxt("x", 2)
                            X = xt[xi]; xname = f"xt{xi}"
                            P.dma("sp", X[:], grow(gq)[tb * 128:(tb + 1) * 128, :], r=["~hpgw"], w=[xname])
                            if first:
                                P.dve(lambda e: e.tensor_scalar(xacc[:], X[:], selb[:, r:r + 1], None, ALU.mult), r=[xname, "selb"], w=["xacc"])
                            else:
                                P.dve(lambda e: e.scalar_tensor_tensor(xacc[:], X[:], selb[:, r:r + 1], xacc[:], ALU.mult, ALU.add), r=[xname, "selb", "xacc"], w=["xacc"])
                            first = False
                        norm_block(l, None, tb, sb_src=(xacc, "xacc"))

                def h_c(uname):
                    def f(tb, S_, sn, k):
                        if uname == "tvc":
                            V_, vn = vst[k], f"vst{k}"
                            P.dve(lambda e: e.tensor_copy(V_[:], S_[:]), r=[sn], w=[vn])
                            P.dma("pool", VCl[part * 512 + tb * 128:part * 512 + (tb + 1) * 128, :], V_[:], r=[vn], w=[chist])
                    return f
                tm_unit("tvc", h_c("tvc"))
                fm_unit(l, "kc", 512, evac_k_to(KCl, part * 512, chist))
            g0 = 16 * m
            nkb = 16 * m + 16
            for r in range(4):
                if r == 0:
                    P.dve(lambda e: e.tensor_scalar(tot[0:1, :], tots[0:1, 4 * m + r, :], selb[0:1, r:r + 1], None, ALU.mult), r=["tots", "selb"], w=["tot"])
                else:
                    P.dve(lambda e: e.scalar_tensor_tensor(tot[0:1, :], tots[0:1, 4 * m + r, :], selb[0:1, r:r + 1], tot[0:1, :], ALU.mult, ALU.add),
                          r=["tots", "selb", "tot"], w=["tot"])

            def h_own(tb, S_, sn, k):
                P.dve(lambda e: e.tensor_tensor(lf4[:, tb, :], S_[:, 288:296], bfb[:, l * 8:(l + 1) * 8], ALU.add), r=[sn, "bfb"], w=["lf4"])
                P.dve(lambda e: e.tensor_scalar(wsgn[:, tb, :], S_[:, 296:304], 0.0, 2.0, ALU.is_ge, ALU.mult), r=[sn], w=["wsgn"])
                P.dve(lambda e: e.tensor_scalar(wsgn[:, tb, :], wsgn[:, tb, :], -1.0, None, ALU.add), r=["wsgn"], w=["wsgn"])
                P.dve(lambda e: e.scalar_tensor_tensor(wabs[:, tb, :], S_[:, 296:304], IDXS, wsgn[:, tb, :], ALU.mult, ALU.mult), r=[sn, "wsgn"], w=["wabs"])
            tm_unit("tb", h_own, cols=(288, 304))
            lf4q = lf4[:].rearrange("p b h -> p (b h)")
            P.act(lambda e: e.activation(lf4q, lf4q, AF.Exp, scale=-1.0), r=["lf4"], w=["lf4"])
            P.act(lambda e: e.activation(lf4q, lf4q, AF.Ln, bias=1.0), r=["lf4"], w=["lf4"])
            P.dve(lambda e: e.tensor_scalar(lf4q, lf4q, -1.0, None, ALU.mult), r=["lf4"], w=["lf4"])
            for b_ in range(4):
                P.pe(lambda e: e.matmul(ps[5][:, b_ * 8:(b_ + 1) * 8], ones1[0:1, 0:128], tot[0:1, :], start=True, stop=False), r=["tot", "ones1"], w=["ps5"])
                for b2 in range(b_ + 1):
                    P.pe(lambda e: e.matmul(ps[5][:, b_ * 8:(b_ + 1) * 8], (TRI if b2 == b_ else onesf[:, :]), lf4[:, b2, :], start=False, stop=(b2 == b_)),
                         r=["lf4", "cstf", "onesf"], w=["ps5"])
            P.dve(lambda e: e.tensor_copy(cown[:].rearrange("p b h -> p (b h)"), ps[5][:, 0:32]), r=["ps5"], w=["cown"])
            P.pe(lambda e: e.matmul(ps[5][:, 16:24], E0ROW, cstore[:, g0, :], start=True, stop=True), r=["cstore", "cstf"], w=["ps5"])
            P.dve(lambda e: e.tensor_copy(cbc[:], ps[5][:, 16:24]), r=["ps5"], w=["cbc"])
            for h in range(8):
                P.dve(lambda e: e.tensor_scalar(nbias[:, 0:nkb, h], cstore[:, 0:nkb, h], cbc[:, h:h + 1], -1.0, ALU.subtract, ALU.mult),
                      r=["cstore", "cbc"], w=["nbias"])
            for tb in range(4):
                P.dve(lambda e: e.tensor_tensor(rq[:, tb, :], cown[:, tb, :], cbc[:], ALU.subtract), r=["cown", "cbc"], w=["rq"])
                P.pe(lambda e: e.transpose(ps[6][0:8, tb * 128:(tb + 1) * 128], rq[:, tb, :], IDENT), r=["rq", "cstf"], w=["ps6"])
            P.dve(lambda e: e.tensor_copy(rT[:], ps[6][0:8, :]), r=["ps6"], w=["rT"])

            def evac_bx_q(j, pt, pn):
                P.act(lambda e: e.activation(iqT[:, j - 2, :], pt[0:64, :], AF.Copy), r=[pn], w=["iqT"])

            def a_half(half):
                at = Attn()
                for sbk in range(4 * m + 4):
                    bi = load_kv(KAl, VAl, half * 4, 4, sbk * 512, 512, hist)
                    trail = (sbk >= 4 * m)
                    for kb in range(4):
                        adds = []
                        if trail:
                            rr = (sbk - 4 * m) * 4 + kb
                            mi = nxt("mb", 2)
                            P.dma("sp", Mb[mi][:, :], AMASK[rr], r=["~amask"], w=[f"Mb{mi}"])
                            adds = [(0, 512, identb[:], Mb[mi][:, :], ["identb", f"Mb{mi}"])]
                        for i in range(4):
                            hh = half * 4 + i
                            at.tile(dict(kT=kbuf[bi][0:65, i, kb * 128:(kb + 1) * 128], v=vbuf[bi][:, kb, i, 0:65], n=128, qlo=0, qhi=512,
                                         qap=Q[0:65, hh, 0:512], adds=adds, bias=nbias[:, 4 * sbk + kb, hh:hh + 1],
                                         names=[f"kbuf{bi}", f"vbuf{bi}"], O=ps[4 + i], oname=psn[4 + i],
                                         first=(sbk == 0 and kb == 0), last=(sbk == 4 * m + 3 and kb == 3)))
                at.flush()
                for i in range(4):
                    finish_head(ps[4 + i], psn[4 + i], zg["a"], "zga", half * 4 + i, 512, slice(0, 512))

            def c_half(half):
                at = Attn()
                started = [False] * 4
                for sbl in range(2):
                    bi = load_kv(KCl, VCl, half * 4, 4, sbl * 512, 512, chist)
                    for kb in range(4):
                        r_ = 4 * sbl + kb
                        qb_lo, qb_hi = max(0, r_ - 4), min(3, r_)
                        qlo, qhi = qb_lo * 128, (qb_hi + 1) * 128
                        for i in range(4):
                            hh = half * 4 + i
                            adds = []
                            for qb in range(qb_lo, qb_hi + 1):
                                dl = r_ - 4 - qb
                                c0, c1 = qb * 128, (qb + 1) * 128
                                if dl == 0:
                                    adds.append((c0, c1, identb[:], bc[:, 0, hh, :], ["identb", "btile"]))
                                    adds.append((c0, c1, identb[:], cm0b[:], ["identb", "cm0b"]))
                                elif dl == -1:
                                    adds.append((c0, c1, identb[:], bc[:, 1, hh, :], ["identb", "btile"]))
                                elif dl == -4:
                                    adds.append((c0, c1, identb[:], cm4b[:], ["identb", "cm4b"]))
                            at.tile(dict(kT=kbuf[bi][0:64, i, kb * 128:(kb + 1) * 128], v=vbuf[bi][:, kb, i, 0:65], n=128, qlo=qlo, qhi=qhi,
                                         qap=Q[0:64, hh, qlo:qhi], adds=adds, bias=(pvb[:, 0:1] if (m == 0 and sbl == 0) else None),
                                         names=[f"kbuf{bi}", f"vbuf{bi}"], O=ps[4 + i], oname=psn[4 + i],
                                         first=(not started[i]), last=(sbl == 1 and kb == 3)))
                            started[i] = True
                at.flush()
                for i in range(4):
                    finish_head(ps[4 + i], psn[4 + i], zg["c"], "zgc", half * 4 + i, 512, slice(0, 512))

            def b_topk(qb):
                NK = (16 * m + 13 + qb) * 128
                qs = slice(qb * 128, (qb + 1) * 128)
                for k0 in range(0, NK, 512):
                    nk = min(512, NK - k0)
                    ii = nxt("ik", 2)
                    P.dma("sp", ikbuf[ii][:, 0:nk], IKl[:, k0:k0 + nk], r=[hist], w=[f"ikbuf{ii}"])
                    for h in range(8):
                        base = 32 * (h % 2)
                        pi_ = 1 + nxt("ips", 3)
                        P.pe(lambda e: e.matmul(ps[pi_][:, 0:nk], iqT[base:base + 32, h // 2, qs], ikbuf[ii][base:base + 32, 0:nk], start=True, stop=True),
                             r=["iqT", f"ikbuf{ii}"], w=[psn[pi_]])
                        ri = nxt("rr", 3)
                        P.act(lambda e: e.activation(Rr[ri][:, 0:nk], ps[pi_][:, 0:nk], AF.Relu, scale=wabs[:, qb, h:h + 1]), r=[psn[pi_], "wabs"], w=[f"Rr{ri}"])
                        if h == 0:
                            P.dve(lambda e: e.tensor_scalar(SC[:, k0:k0 + nk], Rr[ri][:, 0:nk], wsgn[:, qb, 0:1], None, ALU.mult), r=[f"Rr{ri}", "wsgn"], w=["SC"])
                        else:
                            P.dve(lambda e: e.scalar_tensor_tensor(SC[:, k0:k0 + nk], Rr[ri][:, 0:nk], wsgn[:, qb, h:h + 1], SC[:, k0:k0 + nk], ALU.mult, ALU.add),
                                  r=[f"Rr{ri}", "wsgn", "SC"], w=["SC"])
                for ch in range(4):
                    c0 = g0 * 128 + ch * 512
                    wd = min(512, NK - c0)
                    if wd <= 0:
                        continue
                    ri = nxt("rr", 3)
                    P.dve(lambda e: e.tensor_scalar(Rr[ri][:, 0:wd], iota5[:, 0:wd], qlimc[:, qb * 4 + ch:qb * 4 + ch + 1], -1e30, ALU.is_ge, ALU.mult),
                          r=["iota5", "qlimc"], w=[f"Rr{ri}"])
                    P.dve(lambda e: e.tensor_tensor(SC[:, c0:c0 + wd], SC[:, c0:c0 + wd], Rr[ri][:, 0:wd], ALU.add), r=["SC", f"Rr{ri}"], w=["SC"])
                P.dve(lambda e: e.memset(cntb[:], 0.0), w=["cntb"])
                P.dve(lambda e: e.memset(small[:, 8:9], 0.0), w=["cand"])
                for it in range(NBIS):
                    stp = 64.0 * (0.5 ** it)
                    P.dve(lambda e: e.tensor_scalar(junk[:, 0:NK], SC[:, 0:NK], small[:, 8:9], 0.0, ALU.is_ge, ALU.add, accum_out=cntb[:, it:it + 1]),
                          r=["SC", "cand", "cntb"], w=["junk", "cntb"])
                    a, b_ = (stp, -0.5 * stp) if it < NBIS - 1 else (stp, -stp)
                    P.dve(lambda e: e.tensor_scalar(small[:, 9:10], cntb[:, it:it + 1], float(TOPK), a, ALU.is_ge, ALU.mult), r=["cntb"], w=["fl"])
                    P.dve(lambda e: e.scalar_tensor_tensor(small[:, 8:9], small[:, 9:10], b_, small[:, 8:9], ALU.add, ALU.add), r=["fl", "cand"], w=["cand"])
                P.dve(lambda e: e.tensor_scalar(junk[:, 0:NK], SC[:, 0:NK], small[:, 8:9], NEG, ALU.is_lt, ALU.mult), r=["SC", "cand"], w=["junk"])
                P.dma("pool", MBS[qb][:, 0:NK], junk[:, 0:NK], r=["junk"], w=[f"mbs{qb}"])

            def b_attn(qb):
                qs = slice(qb * 128, (qb + 1) * 128)
                nkq = 16 * m + 13 + qb
                at = Attn()
                for sbk in range((nkq + 3) // 4):
                    k0 = sbk * 512
                    nk = min(512, nkq * 128 - k0)
                    mi = nxt("mb", 2)
                    P.dma("sp", Mb[mi][:, 0:nk], MBS[qb][:, k0:k0 + nk], r=[f"mbs{qb}"], w=[f"Mb{mi}"])
                    bi = load_kv(KBl, VBl, 0, 2, k0, nk, hist)
                    for kb in range(nk // 128):
                        gkb = sbk * 4 + kb
                        for jj in range(2):
                            adds = [(0, 512, Mb[mi][:, kb * 128:(kb + 1) * 128], i4b[:].rearrange("p a b -> p (a b)"), [f"Mb{mi}", "i4b"])]
                            if gkb >= g0 - 1 and (gkb - g0 - qb) % 4 in (0, 3):
                                rr = gkb - g0 + 1
                                ci = nxt("cmb", 2)
                                sc0 = sel01[:, (qb * 17 + rr) * 2:(qb * 17 + rr) * 2 + 1]
                                sc1 = sel01[:, (qb * 17 + rr) * 2 + 1:(qb * 17 + rr) * 2 + 2]
                                P.dma("sp", comb[ci][:].rearrange("p a b -> p (a b)"), COMBS[qb, rr, jj], r=["~combs"], w=[f"comb{ci}"])
                                adds.append((0, 512, identb[:], comb[ci][:].rearrange("p a b -> p (a b)"), ["identb", f"comb{ci}"]))
                            at.tile(dict(kT=kbuf[bi][0:64, jj, kb * 128:(kb + 1) * 128], v=vbuf[bi][:, kb, jj, 0:65], n=128, qlo=0, qhi=512,
                                         qap=Q[0:64, 4 * jj:4 * jj + 4, qs], adds=adds, bias=None,
                                         names=[f"kbuf{bi}", f"vbuf{bi}"], O=ps[4 + jj], oname=psn[4 + jj],
                                         first=(gkb == 0), last=(gkb == nkq - 1)))
                at.flush()
                for jj in range(2):
                    finish_head(ps[4 + jj], psn[4 + jj], zg["b"], "zgb", (4 * jj, 4 * jj + 4), 512, qs)

            fm_unit(l, "bx", 512, evac_bx_q, blocks=(2, 3, 4, 5))
            b_topk(0)
            fm_unit(l, "za", 512, evac_z(zg["a"], "zga"))
            fm_unit(l, "qa", 512, evac_q(SCALE))
            for h in range(8):
                P.dma("sp", Q[64:65, h, :], rT[h:h + 1, :], r=["rT"], w=["Q"])
            a_half(0)
            b_topk(1)
            a_half(1)
            fm_unit(l, "zc", 512, evac_z(zg["c"], "zgc"))
            fm_unit(l, "qc", 512, evac_q(SCALE))
            c_half(0)
            c_half(1)
            fm_unit(l, "zb", 512, evac_z(zg["b"], "zgb"))
            fm_unit(l, "qb", 512, evac_q(SCALE))
            b_attn(0)
            b_topk(2)
            b_attn(1)
            b_topk(3)
            b_attn(2)
            b_attn(3)

            allz = [zg["a"], zg["b"], zg["c"]]
            alln = ["zga", "zgb", "zgc"]
            for u in range(6):
                Wo_, won = load_wo(l, u)
                for hq in range(4):
                    hidx = u * 4 + hq
                    zt, zn = allz[hidx // 8], alln[hidx // 8]
                    for tb in range(4):
                        for n_ in range(2):
                            P.pe(lambda e: e.matmul(ps[tb * 2 + n_][:, :], zt[:, hidx % 8, tb * 128:(tb + 1) * 128], Wo_[:, hq, n_ * 512:(n_ + 1) * 512],
                                                    start=(hidx == 0), stop=(hidx == 23)), r=[zn, won], w=[psn[tb * 2 + n_]])
            for tb in range(4):
                xi = nxt("x", 2)
                X = xt[xi]; xname = f"xt{xi}"
                P.dma("sp", X[:], own[tb * 128:(tb + 1) * 128, :], r=(own_r if own_r else []), w=[xname])
                for n_ in range(2):
                    P.dve(lambda e: e.tensor_tensor(X[:, n_ * 512:(n_ + 1) * 512], X[:, n_ * 512:(n_ + 1) * 512], ps[tb * 2 + n_][:, :], ALU.add),
                          r=[xname, psn[tb * 2 + n_]], w=[xname])
                r0 = m * 512 + tb * 128
                if l < nlayers - 1:
                    P.dma("pool", hp1q[r0:r0 + 128, :], X[:], r=[xname], w=[f"hp1qc{r0 // 256}"])
                else:
                    P.act(lambda e: e.activation(junk[:, 0:1024], X[:], AF.Square, accum_out=small[:, 16:17]), r=[xname], w=["junk", "fs0"])
                    P.dve(lambda e: e.tensor_scalar(small[:, 17:18], small[:, 16:17], 1.0 / D_MODEL, EPS, ALU.mult, ALU.add), r=["fs0"], w=["fs1"])
                    P.act(lambda e: e.activation(small[:, 18:19], small[:, 17:18], AF.Sqrt), r=["fs1"], w=["fs2"])
                    P.dve(lambda e: e.reciprocal(small[:, 19:20], small[:, 18:19]), r=["fs2"], w=["fs3"])
                    P.dve(lambda e: e.scalar_tensor_tensor(X[:], X[:], small[:, 19:20], fgb[:], ALU.mult, ALU.mult), r=[xname, "fs3", "fgb"], w=[xname])
                    P.dma("pool", y_q[r0:r0 + 128, :], X[:], r=[xname])
            def emit_exchange(m=m):
                for hf in range(2):
                    cidx = 2 * m + hf
                    P.dma("pool", ccs, hp1q[cidx * 256:(cidx + 1) * 256, :], r=[f"hp1qc{cidx}"], w=["ccs"])
                    P.add("pool", lambda e: e.collective_compute("AllGather", ALU.bypass, replica_groups=[[0, 1, 2, 3], [4, 5, 6, 7]],
                                                                 ins=[ccs.opt()], outs=[ccd.opt()]), r=["ccs"], w=["ccd"], cc=True)
                    for r in range(4):
                        a0 = (4 * m + r) * 512 + hf * 256
                        P.dma("pool", hpg[a0:a0 + 256, :], ccd[r * 256:(r + 1) * 256, :], r=["ccd"], w=["~hpgw"])
            if l < nlayers - 1:
                if m < nm - 1:
                    emit_exchange()
                else:
                    deferred_exchange.append(emit_exchange)
        if do_sample:
            NS = DEC_SEQ
            shist = f"~shist{l}"
            psb7 = ps[7].bitcast(BF16)

            def prep_cache(src2d, nrows, ncols, kdst, vdst, nheads):
                for b in range(nrows // 128):
                    xi = nxt("x", 2)
                    X = xt[xi]; xname = f"xt{xi}"
                    P.dma("sp", X[:, 0:ncols], src2d[b * 128:(b + 1) * 128, :], w=[xname])
                    P.dve(lambda e: e.tensor_copy(xn[:, 0:ncols], X[:, 0:ncols]), r=[xname], w=["xn"])
                    if vdst is not None:
                        P.dma("pool", vdst[b * 128:(b + 1) * 128, :], xn[:, 0:ncols], r=["xn"], w=[shist])
                    if kdst is not None:
                        for hh in range(nheads):
                            P.pe(lambda e: e.transpose(psb7[0:64, hh * 128:(hh + 1) * 128], xn[:, hh * 64:(hh + 1) * 64], identb[:]),
                                 r=["xn", "identb"], w=["ps7"])
                        P.act(lambda e: e.activation(kst[:, 0:nheads, 0:128], psb7[0:64, 0:nheads * 128].rearrange("p (h k) -> p h k", h=nheads), AF.Copy),
                              r=["ps7"], w=["kst"])
                        P.dma("pool", kdst[:, :, b * 128:(b + 1) * 128].rearrange("h d k -> d h k"), kst[:, 0:nheads, 0:128], r=["kst"], w=[shist])

            prep_cache(ca_k[l], PAST, 512, SS_KA[l], None, 8)
            prep_cache(ca_v[l], PAST, 512, None, SS_VA[l], 8)
            prep_cache(cb_k[l], PAST, 128, SS_KB[l], None, 2)
            prep_cache(cb_v[l], PAST, 128, None, SS_VB[l], 2)
            prep_cache(cc_k[l], 512, 512, SS_KC[l], None, 8)
            prep_cache(cc_v[l], 512, 512, None, SS_VC[l], 8)
            for b in range(PAST // 128):
                xi = nxt("x", 2)
                X = xt[xi]; xname = f"xt{xi}"
                P.dma("sp", X[:, 0:32], cb_ik[l, b * 128:(b + 1) * 128, :], w=[xname])
                P.dve(lambda e: e.tensor_copy(xn[:, 0:32], X[:, 0:32]), r=[xname], w=["xn"])
                P.dve(lambda e: e.tensor_copy(xn[:, 32:64], X[:, 0:32]), r=[xname], w=["xn"])
                P.pe(lambda e: e.transpose(psb7[0:64, 0:128], xn[:, 0:64], identb[:]), r=["xn", "identb"], w=["ps7"])
                P.act(lambda e: e.activation(kst[:, 0, 0:128], psb7[0:64, 0:128], AF.Copy), r=["ps7"], w=["kst"])
                P.dma("pool", SS_IK[l][:, b * 128:(b + 1) * 128], kst[:, 0, 0:128], r=["kst"], w=[shist])
            P.dve(lambda e: e.memset(tot[:], 0.0), w=["tot"])

            def cum_block(kb_, n):
                P.pe(lambda e: e.matmul(ps[5][0:n, 0:8], cstf[0:n, 2, 0:n], lfb[0:n, :], start=True, stop=False), r=["lfb", "cstf"], w=["ps5"])
                P.pe(lambda e: e.matmul(ps[5][0:n, 0:8], ones1[0:1, 0:n], tot[0:1, :], start=False, stop=True), r=["tot", "ones1"], w=["ps5"])
                P.dve(lambda e: e.tensor_copy(cstore[0:n, kb_, :], ps[5][0:n, 0:8]), r=["ps5"], w=["cstore"])
                P.pe(lambda e: e.matmul(ps[5][0:1, 8:16], cstf[0:n, 1, 128 - n:129 - n], cstore[0:n, kb_, :], start=True, stop=True), r=["cstore", "cstf"], w=["ps5"])
                P.dve(lambda e: e.tensor_copy(tot[:], ps[5][0:1, 8:16]), r=["ps5"], w=["tot"])

            for b in range(PAST // 128):
                P.dma("sp", lfb[:], ca_lf[l, b * 128:(b + 1) * 128, :], w=["lfb"])
                cum_block(b, 128)
            norm_block(l, (xs if l == 0 else hs1)[:, :], 0, nrow=NS, rname=("hs1" if l > 0 else None))
            for uname in TM_UNITS:
                W, wn = load_w(l, UNITS.index(uname))
                si = nxt("s", 4)
                for c in range(8):
                    P.pe(lambda e: e.matmul(ps[si][0:NS, :], hT[:, c, 0:NS], W[:, c, :], start=(c == 0), stop=(c == 7)), r=[wn, "hT"], w=[psn[si]])
                k = nxt("st", 2)
                S_ = st[k]; sn = f"st{k}"
                P.act(lambda e: e.activation(S_[0:NS, :], ps[si][0:NS, :], AF.Copy), r=[psn[si]], w=[sn])
                V_, vn = vst[k], f"vst{k}"
                if uname in ("tka", "tva", "tkc", "tvc"):
                    dst = {"tka": s_ak, "tva": s_av, "tkc": s_ck, "tvc": s_cv}[uname]
                    P.dma("pool", dst[l], S_[0:NS, :], r=[sn])
                    if uname in ("tva", "tvc"):
                        P.dve(lambda e: e.tensor_copy(V_[0:NS, :], S_[0:NS, :]), r=[sn], w=[vn])
                        vd = SS_VA[l][PAST:PAST + NS, :] if uname == "tva" else SS_VC[l][512:512 + NS, :]
                        P.dma("pool", vd, V_[0:NS, :], r=[vn], w=[shist])
                else:
                    P.dma("pool", s_bk[l], S_[0:NS, 0:128], r=[sn])
                    P.dma("pool", s_bv[l], S_[0:NS, 128:256], r=[sn])
                    P.dma("pool", s_ik[l], S_[0:NS, 256:288], r=[sn])
                    P.dve(lambda e: e.tensor_copy(V_[0:NS, 0:128], S_[0:NS, 128:256]), r=[sn], w=[vn])
                    P.dma("pool", SS_VB[l][PAST:PAST + NS, :], V_[0:NS, 0:128], r=[vn], w=[shist])
                    P.dve(lambda e: e.tensor_tensor(lfb[0:NS, :], S_[0:NS, 288:296], bfb[0:NS, l * 8:(l + 1) * 8], ALU.add), r=[sn, "bfb"], w=["lfb"])
                    P.act(lambda e: e.activation(lfb[0:NS, :], lfb[0:NS, :], AF.Exp, scale=-1.0), r=["lfb"], w=["lfb"])
                    P.act(lambda e: e.activation(lfb[0:NS, :], lfb[0:NS, :], AF.Ln, bias=1.0), r=["lfb"], w=["lfb"])
                    P.dve(lambda e: e.tensor_scalar(lfb[0:NS, :], lfb[0:NS, :], -1.0, None, ALU.mult), r=["lfb"], w=["lfb"])
                    P.dma("pool", s_lf[l], lfb[0:NS, :], r=["lfb"])
                    cum_block(8, NS)
                    P.dve(lambda e: e.tensor_scalar(wsgn[0:NS, 0, :], S_[0:NS, 296:304], 0.0, 2.0, ALU.is_ge, ALU.mult), r=[sn], w=["wsgn"])
                    P.dve(lambda e: e.tensor_scalar(wsgn[0:NS, 0, :], wsgn[0:NS, 0, :], -1.0, None, ALU.add), r=["wsgn"], w=["wsgn"])
                    P.dve(lambda e: e.scalar_tensor_tensor(wabs[0:NS, 0, :], S_[0:NS, 296:304], IDXS, wsgn[0:NS, 0, :], ALU.mult, ALU.mult), r=[sn, "wsgn"], w=["wabs"])
            P.pe(lambda e: e.matmul(ps[5][:, 16:24], cstf[0:NS, 3, :], cstore[0:NS, 8, :], start=True, stop=True), r=["cstore", "cstf"], w=["ps5"])
            P.dve(lambda e: e.tensor_copy(cbc[:], ps[5][:, 16:24]), r=["ps5"], w=["cbc"])
            for h in range(8):
                P.dve(lambda e: e.tensor_scalar(nbias[:, 0:9, h], cstore[:, 0:9, h], cbc[:, h:h + 1], -1.0, ALU.subtract, ALU.mult), r=["cstore", "cbc"], w=["nbias"])
            P.dve(lambda e: e.tensor_tensor(rq[0:NS, 0, :], cstore[0:NS, 8, :], cbc[0:NS, :], ALU.subtract), r=["cstore", "cbc"], w=["rq"])
            P.pe(lambda e: e.transpose(ps[6][0:8, 0:NS], rq[0:NS, 0, :], cstf[0:NS, 0, 0:NS]), r=["rq", "cstf"], w=["ps6"])
            P.dve(lambda e: e.tensor_copy(rT[:, 0:NS], ps[6][0:8, 0:NS]), r=["ps6"], w=["rT"])

            def s_evac_k(dst3, koff):
                def f(j, pt, pn):
                    P.act(lambda e: e.activation(kst[:, j, 0:NS], pt[0:64, 0:NS], AF.Copy), r=[pn], w=["kst"])
                    if j == 7:
                        P.dma("pool", dst3[:, :, koff:koff + NS].rearrange("h d k -> d h k"), kst[:, :, 0:NS], r=["kst"], w=[shist])
                return f

            def s_evac_q(j, pt, pn):
                P.act(lambda e: e.activation(Q[0:64, j, 0:NS], pt[0:64, 0:NS], AF.Copy, scale=SCALE), r=[pn], w=["Q"])

            def s_evac_z(zt, zn):
                def f(j, pt, pn):
                    P.act(lambda e: e.activation(zt[:, j, 0:NS], pt[0:64, 0:NS], AF.Silu), r=[pn], w=[zn])
                return f

            fm_unit(l, "ka", NS, s_evac_k(SS_KA[l], PAST))
            fm_unit(l, "za", NS, s_evac_z(zg["a"], "zga"))
            fm_unit(l, "qa", NS, s_evac_q)
            for h in range(8):
                P.dma("sp", Q[64:65, h, 0:NS], rT[h:h + 1, 0:NS], r=["rT"], w=["Q"])
            for half in range(2):
                at = Attn()
                for sbk in range(3):
                    nk = 512 if sbk < 2 else NS
                    bi = load_kv(SS_KA[l], SS_VA[l], half * 4, 4, sbk * 512, nk, shist)
                    for kb in range((nk + 127) // 128):
                        n = min(128, nk - kb * 128)
                        adds = [(0, NS, identb[0:NS, 0:NS], ma0b[0:NS, 0:NS], ["identb", "ma0b"])] if sbk == 2 else []
                        for i in range(4):
                            hh = half * 4 + i
                            at.tile(dict(kT=kbuf[bi][0:65, i, kb * 128:kb * 128 + n], v=vbuf[bi][0:n, kb, i, 0:65], n=n, qlo=0, qhi=NS,
                                         qap=Q[0:65, hh, 0:NS], adds=adds, bias=nbias[0:n, 4 * sbk + kb, hh:hh + 1],
                                         names=[f"kbuf{bi}", f"vbuf{bi}"], O=ps[4 + i], oname=psn[4 + i],
                                         first=(sbk == 0 and kb == 0), last=(sbk == 2)))
                at.flush()
                for i in range(4):
                    finish_head(ps[4 + i], psn[4 + i], zg["a"], "zga", half * 4 + i, NS, slice(0, NS))
            fm_unit(l, "kc", NS, s_evac_k(SS_KC[l], 512))
            fm_unit(l, "zc", NS, s_evac_z(zg["c"], "zgc"))
            fm_unit(l, "qc", NS, s_evac_q)
            for half in range(2):
                at = Attn()
                for sbk in range(2):
                    nk = 512 if sbk < 1 else NS
                    bi = load_kv(SS_KC[l], SS_VC[l], half * 4, 4, sbk * 512, nk, shist)
                    for kb in range((nk + 127) // 128):
                        n = min(128, nk - kb * 128)
                        for i in range(4):
                            hh = half * 4 + i
                            adds = []
                            if sbk == 0 and kb == 3:
                                adds = [(0, NS, identb[:], bc[:, 1, hh, 0:NS], ["identb", "btile"])]
                            if sbk == 1:
                                adds = [(0, NS, identb[0:NS, 0:NS], bc[0:NS, 0, hh, 0:NS], ["identb", "btile"])]
                            at.tile(dict(kT=kbuf[bi][0:64, i, kb * 128:kb * 128 + n], v=vbuf[bi][0:n, kb, i, 0:65], n=n, qlo=0, qhi=NS,
                                         qap=Q[0:64, hh, 0:NS], adds=adds, bias=None,
                                         names=[f"kbuf{bi}", f"vbuf{bi}"], O=ps[4 + i], oname=psn[4 + i],
                                         first=(sbk == 0 and kb == 0), last=(sbk == 1)))
                at.flush()
                for i in range(4):
                    finish_head(ps[4 + i], psn[4 + i], zg["c"], "zgc", half * 4 + i, NS, slice(0, NS))
            def s_evac_bx(j, pt, pn):
                if j < 2:
                    P.act(lambda e: e.activation(kst[:, j, 0:NS], pt[0:64, 0:NS], AF.Copy), r=[pn], w=["kst"])
                    if j == 1:
                        P.dma("pool", SS_KB[l][:, :, PAST:PAST + NS].rearrange("h d k -> d h k"), kst[:, 0:2, 0:NS], r=["kst"], w=[shist])
                elif j < 6:
                    P.act(lambda e: e.activation(iqT[:, j - 2, 0:NS], pt[0:64, 0:NS], AF.Copy), r=[pn], w=["iqT"])
                elif j == 6:
                    P.act(lambda e: e.activation(kst[:, 2, 0:NS], pt[0:64, 0:NS], AF.Copy), r=[pn], w=["kst"])
                    P.dma("pool", SS_IK[l][:, PAST:PAST + NS], kst[:, 2, 0:NS], r=["kst"], w=[shist])
            fm_unit(l, "bx", NS, s_evac_bx)
            fm_unit(l, "zb", NS, s_evac_z(zg["b"], "zgb"))
            fm_unit(l, "qb", NS, s_evac_q)
            NK = PAST + NS
            for k0 in range(0, NK, 512):
                nk = min(512, NK - k0)
                ii = nxt("ik", 2)
                P.dma("sp", ikbuf[ii][:, 0:nk], SS_IK[l][:, k0:k0 + nk], r=[shist], w=[f"ikbuf{ii}"])
                for h in range(8):
                    base = 32 * (h % 2)
                    pi_ = 2 + nxt("ips", 2)
                    P.pe(lambda e: e.matmul(ps[pi_][0:NS, 0:nk], iqT[base:base + 32, h // 2, 0:NS], ikbuf[ii][base:base + 32, 0:nk], start=True, stop=True),
                         r=["iqT", f"ikbuf{ii}"], w=[psn[pi_]])
                    ri = nxt("rr", 2)
                    P.act(lambda e: e.activation(Rr[ri][0:NS, 0:nk], ps[pi_][0:NS, 0:nk], AF.Relu, scale=wabs[0:NS, 0, h:h + 1]), r=[psn[pi_], "wabs"], w=[f"Rr{ri}"])
                    if h == 0:
                        P.dve(lambda e: e.tensor_scalar(SC[0:NS, k0:k0 + nk], Rr[ri][0:NS, 0:nk], wsgn[0:NS, 0, 0:1], None, ALU.mult), r=[f"Rr{ri}", "wsgn"], w=["SC"])
                    else:
                        P.dve(lambda e: e.scalar_tensor_tensor(SC[0:NS, k0:k0 + nk], Rr[ri][0:NS, 0:nk], wsgn[0:NS, 0, h:h + 1], SC[0:NS, k0:k0 + nk], ALU.mult, ALU.add),
                              r=[f"Rr{ri}", "wsgn", "SC"], w=["SC"])
            P.dve(lambda e: e.memset(cntb[:], 0.0), w=["cntb"])
            P.dve(lambda e: e.memset(small[:, 8:9], 0.0), w=["cand"])
            for it in range(NBIS):
                stp = 64.0 * (0.5 ** it)
                P.dve(lambda e: e.tensor_scalar(junk[0:NS, 0:NK], SC[0:NS, 0:NK], small[0:NS, 8:9], 0.0, ALU.is_ge, ALU.add, accum_out=cntb[0:NS, it:it + 1]),
                      r=["SC", "cand", "cntb"], w=["junk", "cntb"])
                a, b_ = (stp, -0.5 * stp) if it < NBIS - 1 else (stp, -stp)
                P.dve(lambda e: e.tensor_scalar(small[0:NS, 9:10], cntb[0:NS, it:it + 1], float(TOPK), a, ALU.is_ge, ALU.mult), r=["cntb"], w=["fl"])
                P.dve(lambda e: e.scalar_tensor_tensor(small[0:NS, 8:9], small[0:NS, 9:10], b_, small[0:NS, 8:9], ALU.add, ALU.add), r=["fl", "cand"], w=["cand"])
            at = Attn()
            for sbk in range(3):
                k0 = sbk * 512
                nk = min(512, NK - k0)
                mi = nxt("mb", 2)
                P.dve(lambda e: e.tensor_scalar(Mb[mi][0:NS, 0:nk], SC[0:NS, k0:k0 + nk], small[0:NS, 8:9], NEG, ALU.is_lt, ALU.mult), r=["SC", "cand"], w=[f"Mb{mi}"])
                bi = load_kv(SS_KB[l], SS_VB[l], 0, 2, k0, nk, shist)
                for kb in range((nk + 127) // 128):
                    n = min(128, nk - kb * 128)
                    gkb = sbk * 4 + kb
                    for j in range(2):
                        adds = [(0, 4 * NS, Mb[mi][0:NS, kb * 128:kb * 128 + n], i4b[0:NS, :, 0:NS], [f"Mb{mi}", "i4b"])]
                        if gkb >= 7:
                            for hq in range(4):
                                if gkb == 7:
                                    adds.append((hq * NS, (hq + 1) * NS, identb[:], b5[:, 1, 4 * j + hq, 0:NS], ["identb", "btile"]))
                                else:
                                    adds.append((hq * NS, (hq + 1) * NS, identb[0:NS, 0:NS], b5[0:NS, 0, 4 * j + hq, 0:NS], ["identb", "btile"]))
                        at.tile(dict(kT=kbuf[bi][0:64, j, kb * 128:kb * 128 + n], v=vbuf[bi][0:n, kb, j, 0:65], n=n, qlo=0, qhi=4 * NS,
                                     qap=Q[0:64, 4 * j:4 * j + 4, 0:NS], adds=adds, bias=None,
                                     names=[f"kbuf{bi}", f"vbuf{bi}"], O=ps[4 + j], oname=psn[4 + j],
                                     first=(gkb == 0), last=(gkb == 8)))
            at.flush()
            for j in range(2):
                finish_head(ps[4 + j], psn[4 + j], zg["b"], "zgb", (4 * j, 4 * j + 4), 4 * NS, slice(0, NS))
            allz = [zg["a"], zg["b"], zg["c"]]
            alln = ["zga", "zgb", "zgc"]
            for u in range(6):
                Wo_, won = load_wo(l, u)
                for hq in range(4):
                    hidx = u * 4 + hq
                    zt, zn = allz[hidx // 8], alln[hidx // 8]
                    for n_ in range(2):
                        P.pe(lambda e: e.matmul(ps[n_][0:NS, :], zt[:, hidx % 8, 0:NS], Wo_[:, hq, n_ * 512:(n_ + 1) * 512], start=(hidx == 0), stop=(hidx == 23)),
                             r=[zn, won], w=[psn[n_]])
            xi = nxt("x", 2)
            X = xt[xi]; xname = f"xt{xi}"
            P.dma("sp", X[0:NS, :], (xs if l == 0 else hs1)[:, :], r=(["hs1"] if l > 0 else []), w=[xname])
            for n_ in range(2):
                P.dve(lambda e: e.tensor_tensor(X[0:NS, n_ * 512:(n_ + 1) * 512], X[0:NS, n_ * 512:(n_ + 1) * 512], ps[n_][0:NS, :], ALU.add), r=[xname, psn[n_]], w=[xname])
            if l < nlayers - 1:
                P.dma("pool", hs1[:, :], X[0:NS, :], r=[xname], w=["hs1"])
            else:
                P.act(lambda e: e.activation(junk[0:NS, 0:1024], X[0:NS, :], AF.Square, accum_out=small[0:NS, 16:17]), r=[xname], w=["junk", "fs0"])
                P.dve(lambda e: e.tensor_scalar(small[0:NS, 17:18], small[0:NS, 16:17], 1.0 / D_MODEL, EPS, ALU.mult, ALU.add), r=["fs0"], w=["fs1"])
                P.act(lambda e: e.activation(small[0:NS, 18:19], small[0:NS, 17:18], AF.Sqrt), r=["fs1"], w=["fs2"])
                P.dve(lambda e: e.reciprocal(small[0:NS, 19:20], small[0:NS, 18:19]), r=["fs2"], w=["fs3"])
                P.dve(lambda e: e.scalar_tensor_tensor(X[0:NS, :], X[0:NS, :], small[0:NS, 19:20], fgb[0:NS, :], ALU.mult, ALU.mult), r=[xname, "fs3", "fgb"], w=[xname])
                P.dma("pool", y_s[:, :], X[0:NS, :], r=[xname])
        for fx in deferred_exchange:
            fx()
    P.finalize_and_emit()
    return nc, es, P


def _t5_bucket_np(rel):
    nb = 16
    max_exact = 8
    ret = np.where(rel > 0, nb, 0)
    n = np.abs(rel)
    nf = np.maximum(n, 1).astype(np.float32)
    large = max_exact + (np.log(nf / max_exact) / math.log(128 / max_exact) * (nb - max_exact)).astype(np.int32)
    large = np.minimum(large, nb - 1)
    return ret + np.where(n < max_exact, n, large)


def _constants():
    p = np.arange(128)[:, None]
    f = np.arange(128)[None, :]
    cst = np.zeros((128, 8, 128), np.float32)
    cst[:, 0] = (p == f)
    cst[:, 1] = (p + f == 127)
    cst[:, 2] = (p <= f)
    cst[0, 3, :] = 1.0
    cst[:, 4] = np.where(p > f, NEG, 0.0)
    cst[:, 5] = np.where((p >= 64) & (f < 64), NEG, 0.0)
    cst[:, 6] = np.where((p < 64) & (f >= 64), NEG, 0.0)
    cst[:, 7] = np.where((p < 64) & (f >= 64), -1e30, 0.0)
    rel = 127 - np.arange(LTAB)
    bk = _t5_bucket_np(rel.astype(np.int32))
    oh5 = np.zeros((32, LTAB), np.float32)
    oh5[bk, np.arange(LTAB)] += 1.0
    far = int(_t5_bucket_np(np.array([-100000], np.int32))[0])
    oh5[far, :] -= 1.0
    idx = np.clip(rel, -128, 128) + 128
    ohc = np.zeros((3 * 128, LTAB), np.float32)
    ohc[idx, np.arange(LTAB)] += 1.0
    ohc[0, :] -= 1.0
    return cst.reshape(128, 8 * 128), oh5, ohc.reshape(3, 128, LTAB)


_PROG = {}


def _get_prog(key=(True, 4, DEPTH)):
    if key not in _PROG:
        _PROG[key] = build_program(*key)
    return _PROG[key]


def _host_inputs(x_prompt, norm_g, w_in, b_f, t5_bias, c_rel_bias, w_out, final_g):
    cst, oh5, ohc = _constants()
    wus = np.zeros((DEPTH, NU, 128, 8, 512), np.float32)
    for l in range(DEPTH):
        for u, name in enumerate(UNITS):
            cols = np.array(_unit_cols(name))
            m = cols >= 0
            w = np.zeros((D_MODEL, 512), np.float32)
            w[:, m] = w_in[l][:, cols[m]]
            wus[l, u] = w.reshape(8, 128, 512).transpose(1, 0, 2)
    wos = np.ascontiguousarray(w_out.reshape(DEPTH, 6, 4, 64, D_MODEL).transpose(0, 1, 3, 2, 4))
    gcol = np.ascontiguousarray(norm_g.reshape(DEPTH, 8, 128).transpose(2, 0, 1).reshape(128, DEPTH * 8))
    crel = np.zeros((DEPTH, 384, 8), np.float32)
    crel[:, :257] = c_rel_bias
    common = dict(wu=wus, wo=wos, gcol=gcol, fg=np.ascontiguousarray(final_g.reshape(1, D_MODEL)),
                  bfb=np.ascontiguousarray(b_f.reshape(1, DEPTH * 8)), t5=np.ascontiguousarray(t5_bias),
                  crel=crel.reshape(DEPTH, 3, 128, 8), oh5=oh5, ohc=ohc, cst=cst)
    return common


def _percore(j):
    pc = np.zeros((128, 1024), np.float32)
    p = np.arange(128)
    pc[:, 0:512] = (j * 512 + np.arange(512))[None, :]
    for rr in range(16):
        pc[:, 512 + rr] = rr * 128 + p
    for qb in range(4):
        for ch in range(4):
            pc[:, 528 + qb * 4 + ch] = (8 * j + 2 * qb + (p >= 64) + 1) * 64 - ch * 512
        for rr in range(17):
            pc[:, 544 + (qb * 17 + rr) * 2] = 1.0 if (rr - 1) == 4 * j + qb else 0.0
            pc[:, 544 + (qb * 17 + rr) * 2 + 1] = 1.0 if (rr - 1) == 4 * j + qb - 1 else 0.0
    for r in range(4):
        pc[:, 680 + r] = 1.0 if j == r else 0.0
    pc[:, 684] = -30000.0 if j == 0 else 0.0
    return pc


def _in_maps(x_prompt, x_sample, cache_a_k, cache_a_v, cache_a_logf, cache_b_k, cache_b_v, cache_b_idx_k,
             cache_c_k, cache_c_v, norm_g, w_in, b_f, t5_bias, c_rel_bias, w_out, final_g):
    f = lambda a: np.ascontiguousarray(np.asarray(a, dtype=np.float32))
    x_prompt, norm_g, w_in, b_f, t5_bias, c_rel_bias, w_out, final_g = map(f, (x_prompt, norm_g, w_in, b_f, t5_bias, c_rel_bias, w_out, final_g))
    common = _host_inputs(x_prompt, norm_g, w_in, b_f, t5_bias, c_rel_bias, w_out, final_g)
    common["iota5"] = np.ascontiguousarray(np.broadcast_to(np.arange(512, dtype=np.float32)[None, :], (128, 512)))
    in_maps = []
    for c in range(8):
        b, j = c // 4, c % 4
        m = dict(common)
        xb = x_prompt[b].reshape(NG, 512, D_MODEL)
        m["xp"] = x_prompt[b]
        m["xq"] = np.ascontiguousarray(xb[j::4])
        xpv = np.zeros((4, 512, D_MODEL), np.float32)
        for mm in range(4):
            if 4 * mm + j - 1 >= 0:
                xpv[mm] = xb[4 * mm + j - 1]
        m["xprev"] = xpv
        m["pcore"] = _percore(j)
        m["xs"] = f(x_sample[c])
        m["ca_k"] = f(cache_a_k[:, c]).reshape(DEPTH, PAST, 512); m["ca_v"] = f(cache_a_v[:, c]).reshape(DEPTH, PAST, 512)
        m["ca_lf"] = f(cache_a_logf[:, c]).reshape(DEPTH, PAST, 8)
        m["cb_k"] = f(cache_b_k[:, c]).reshape(DEPTH, PAST, 128); m["cb_v"] = f(cache_b_v[:, c]).reshape(DEPTH, PAST, 128)
        m["cb_ik"] = f(cache_b_idx_k[:, c]).reshape(DEPTH, PAST, 32)
        m["cc_k"] = f(cache_c_k[:, c]).reshape(DEPTH, 512, 512); m["cc_v"] = f(cache_c_v[:, c]).reshape(DEPTH, 512, 512)
        in_maps.append(m)
    return in_maps


def kernel(x_prompt, x_sample, cache_a_k, cache_a_v, cache_a_logf, cache_b_k, cache_b_v, cache_b_idx_k,
           cache_c_k, cache_c_v, norm_g, w_in, b_f, t5_bias, c_rel_bias, w_out, final_g):
    nc, es, P = _get_prog()
    in_maps = _in_maps(x_prompt, x_sample, cache_a_k, cache_a_v, cache_a_logf, cache_b_k, cache_b_v, cache_b_idx_k,
                       cache_c_k, cache_c_v, norm_g, w_in, b_f, t5_bias, c_rel_bias, w_out, final_g)
    res = run_bass_kernel_spmd(nc, in_maps, core_ids=list(range(8)))
    R = res.results
    st = lambda name, shp: np.stack([R[4 * b][name] for b in range(2)], axis=1).reshape(shp)
    y_prompt = np.zeros((BATCH, NG, 512, D_MODEL), np.float32)
    for c in range(8):
        b, j = c // 4, c % 4
        y_prompt[b, j::4] = R[c]["y_q"].reshape(4, 512, D_MODEL)
    y_prompt = y_prompt.reshape(BATCH, SEQ, D_MODEL)
    ss = lambda name, shp: np.stack([R[b][name] for b in range(DEC_BATCH)], axis=1).reshape(shp)
    y_sample = np.stack([R[b]["y_s"] for b in range(DEC_BATCH)], axis=0)
    outs = [y_prompt, y_sample,
            st("o_ak", (DEPTH, BATCH, SEQ, H, HD)), st("o_av", (DEPTH, BATCH, SEQ, H, HD)), st("o_lf", (DEPTH, BATCH, SEQ, H)),
            st("o_bk", (DEPTH, BATCH, SEQ, KVB, HD)), st("o_bv", (DEPTH, BATCH, SEQ, KVB, HD)), st("o_ik", (DEPTH, BATCH, SEQ, IDX_D)),
            st("o_ck", (DEPTH, BATCH, 512, H, HD)), st("o_cv", (DEPTH, BATCH, 512, H, HD)),
            ss("s_ak", (DEPTH, DEC_BATCH, DEC_SEQ, H, HD)), ss("s_av", (DEPTH, DEC_BATCH, DEC_SEQ, H, HD)), ss("s_lf", (DEPTH, DEC_BATCH, DEC_SEQ, H)),
            ss("s_bk", (DEPTH, DEC_BATCH, DEC_SEQ, KVB, HD)), ss("s_bv", (DEPTH, DEC_BATCH, DEC_SEQ, KVB, HD)), ss("s_ik", (DEPTH, DEC_BATCH, DEC_SEQ, IDX_D)),
            ss("s_ck", (DEPTH, DEC_BATCH, DEC_SEQ, H, HD)), ss("s_cv", (DEPTH, DEC_BATCH, DEC_SEQ, H, HD))]
    return tuple(outs)
```

```python
import math
import types
import numpy as np
from contextlib import ExitStack
import concourse.bass as bass
import concourse.mybir as mybir
from concourse.bass_utils import run_bass_kernel_spmd

F32 = mybir.dt.float32
BF16 = mybir.dt.bfloat16
ALU = mybir.AluOpType
AF = mybir.ActivationFunctionType

D_MODEL = 1024; BATCH = 2; SEQ = 8192; DEPTH = 2; DEC_BATCH = 8; DEC_SEQ = 16; PAST = 1024
HD = 64; H = 8; KVB = 2; IDX_H = 8; IDX_D = 32; TOPK = 256
SCALE = HD ** -0.5
IDXS = (IDX_D ** -0.5) * (IDX_H ** -0.5)
EPS = 1e-6
NEG = -32768.0
NG = SEQ // 512
NBIS = 24
LTAB = 384

_SPLIT = (512, 512, 512, 512, 8, 512, 128, 128, 512, 256, 8, 32, 512, 512, 512, 512)
_OFF = np.concatenate([[0], np.cumsum(_SPLIT)])
(QA, KA, VA, ZA, FA, QB, KB, VB, ZB, IQ, IW, IK, QC, KC, VC, ZC) = [int(o) for o in _OFF[:-1]]
FM_UNITS = ["qa", "ka", "za", "qb", "zb", "bx", "qc", "kc", "zc"]
TM_UNITS = ["tka", "tva", "tb", "tkc", "tvc"]
UNITS = FM_UNITS + TM_UNITS
NU = len(UNITS)


def _unit_cols(name):
    r = lambda a, n: list(range(a, a + n))
    pad = lambda l: l + [-1] * (512 - len(l))
    if name == "qa": return r(QA, 512)
    if name == "ka": return r(KA, 512)
    if name == "za": return r(ZA, 512)
    if name == "qb": return r(QB, 512)
    if name == "zb": return r(ZB, 512)
    if name == "qc": return r(QC, 512)
    if name == "kc": return r(KC, 512)
    if name == "zc": return r(ZC, 512)
    if name == "bx": return pad(r(KB, 128) + r(IQ, 256) + r(IK, 32) + r(IK, 32))
    if name == "tka": return r(KA, 512)
    if name == "tva": return r(VA, 512)
    if name == "tkc": return r(KC, 512)
    if name == "tvc": return r(VC, 512)
    if name == "tb": return pad(r(KB, 128) + r(VB, 128) + r(IK, 32) + r(FA, 8) + r(IW, 8))
    raise KeyError(name)


class Prog:
    STREAMS = ("pe", "act", "dve", "pool", "sp")

    def __init__(self, nc):
        self.nc = nc
        self.ops = []
        self.ndma = {}

    @staticmethod
    def _freeze(fn):
        if fn.__closure__ is None:
            return fn
        cells = []
        for c in fn.__closure__:
            try:
                cells.append(types.CellType(c.cell_contents))
            except ValueError:
                cells.append(c)
        return types.FunctionType(fn.__code__, fn.__globals__, fn.__name__, fn.__defaults__, tuple(cells))

    NSUB = 16

    def add(self, stream, fn, r=(), w=(), dma=False, cc=False):
        fn = self._freeze(fn)
        if cc:
            track = "dma_cc"
        elif dma:
            k = self.ndma.get(stream, 0)
            self.ndma[stream] = k + 1
            track = f"dma_{stream}#{k % self.NSUB}"
        else:
            track = stream
        self.ops.append((stream, track, fn, tuple(r), tuple(w)))

    def pe(self, fn, r=(), w=()): self.add("pe", fn, r, w)
    def act(self, fn, r=(), w=()): self.add("act", fn, r, w)
    def dve(self, fn, r=(), w=()): self.add("dve", fn, r, w)
    def pool(self, fn, r=(), w=()): self.add("pool", fn, r, w)

    def dma(self, stream, out, in_, r=(), w=(), **kw):
        self.add(stream, lambda e: e.dma_start(out=out, in_=in_, **kw), r, w, dma=True)

    def finalize_and_emit(self):
        nc = self.nc
        ops = self.ops
        n = len(ops)
        writers = {}
        readers = {}
        prev_on = {}
        deps = [None] * n
        signal = [False] * n
        qof = lambda t: t.split("#")[0]
        for i, (stream, track, fn, R, W) in enumerate(ops):
            d = set()
            is_dma = track.startswith("dma_")
            if is_dma:
                j = prev_on.get(track)
                if j is not None:
                    d.add(j)
                prev_on[track] = i
            for res in R:
                for tj, j in writers.get(res, {}).items():
                    if tj != track or is_dma or track != "pe":
                        d.add(j)
            for res in W:
                lazy = res.startswith("~")
                for tj, j in writers.get(res, {}).items():
                    if lazy:
                        if qof(tj) != qof(track):
                            d.add(j)
                    elif tj != track or is_dma:
                        d.add(j)
                for tj, j in readers.get(res, {}).items():
                    if lazy:
                        if qof(tj) != qof(track):
                            d.add(j)
                    elif tj != track or is_dma:
                        d.add(j)
            for res in R:
                readers.setdefault(res, {})[track] = i
            for res in W:
                if res.startswith("~"):
                    writers.setdefault(res, {})[track] = i
                else:
                    writers[res] = {track: i}
                    readers[res] = {}
            d.discard(i)
            deps[i] = d
            for j in d:
                signal[j] = True
        tracks = sorted({o[1] for o in ops})
        cnt = {t: 0 for t in tracks}
        val = [0] * n
        for i, (stream, track, fn, R, W) in enumerate(ops):
            if track == "dma_cc":
                cnt[track] += 1
                val[i] = cnt[track]
            elif track.startswith("dma_"):
                cnt[track] += 16
                val[i] = cnt[track]
            elif signal[i]:
                cnt[track] += 1
                val[i] = cnt[track]
        known = {s: {t: 0 for t in tracks} for s in self.STREAMS}
        waits = [None] * n
        for i, (stream, track, fn, R, W) in enumerate(ops):
            need = {}
            for j in deps[i]:
                tj = ops[j][1]
                need[tj] = max(need.get(tj, 0), val[j])
            wl = []
            for tj, v in need.items():
                if v > known[stream][tj]:
                    wl.append((tj, v))
                    known[stream][tj] = v
            waits[i] = wl
        self.stats = dict(cnt)
        by_stream = {s: [] for s in self.STREAMS}
        for i, o in enumerate(ops):
            by_stream[o[0]].append(i)
        with ExitStack() as es:
            sems = {t: es.enter_context(nc.semaphore("s_" + t.replace("#", "_"))) for t in tracks}
            block = es.enter_context(nc.Block())

            def run(eng, stream):
                for i in by_stream[stream]:
                    _, track, fn, R, W = ops[i]
                    for tj, v in waits[i]:
                        eng.wait_ge(sems[tj], v)
                    inst = fn(eng)
                    if track == "dma_cc":
                        inst.then_inc(sems[track], 1)
                    elif track.startswith("dma_"):
                        inst.then_inc(sems[track], 16)
                    elif signal[i]:
                        inst.then_inc(sems[track], 1)
                if stream == "sp":
                    for t in tracks:
                        if t.startswith("dma_") and cnt[t] > known[stream][t]:
                            eng.wait_ge(sems[t], cnt[t])

            @block.tensor
            def _(e): run(e, "pe")

            @block.scalar
            def _(e): run(e, "act")

            @block.vector
            def _(e): run(e, "dve")

            @block.gpsimd
            def _(e): run(e, "pool")

            @block.sync
            def _(e): run(e, "sp")


def build_program(do_sample=True, nm=4, nlayers=DEPTH):
    nc = bass.Bass("TRN2", target_bir_lowering=False)
    es = ExitStack()
    din = lambda name, shape, dt=F32: nc.dram_tensor(name, list(shape), dt, kind="ExternalInput").ap()
    dout = lambda name, shape: nc.dram_tensor(name, list(shape), F32, kind="ExternalOutput").ap()
    dscr = lambda name, shape, dt=BF16: nc.dram_tensor(name, list(shape), dt, kind="Internal").ap()
    xp = din("xp", [SEQ, D_MODEL])
    wu = din("wu", [DEPTH, NU, 128, 8, 512])
    wo = din("wo", [DEPTH, 6, 64, 4, 1024])
    gcol_d = din("gcol", [128, DEPTH * 8])
    fg_d = din("fg", [1, D_MODEL])
    bf_d = din("bfb", [1, DEPTH * 8])
    t5_d = din("t5", [32, 8])
    crel_d = din("crel", [DEPTH, 3, 128, 8])
    oh5_d = din("oh5", [32, LTAB])
    ohc_d = din("ohc", [3, 128, LTAB])
    cst_d = din("cst", [128, 8 * 128])
    y_q = dout("y_q", [2048, D_MODEL])
    xq = din("xq", [4, 512, D_MODEL]); xprev = din("xprev", [4, 512, D_MODEL])
    pc_d = din("pcore", [128, 1024])
    iota_d = din("iota5", [128, 512])
    o_ak = dout("o_ak", [DEPTH, SEQ, 512]); o_av = dout("o_av", [DEPTH, SEQ, 512])
    o_lf = dout("o_lf", [DEPTH, SEQ, 8])
    o_bk = dout("o_bk", [DEPTH, SEQ, 128]); o_bv = dout("o_bv", [DEPTH, SEQ, 128])
    o_ik = dout("o_ik", [DEPTH, SEQ, 32])
    o_ck = dout("o_ck", [DEPTH, 512, 512]); o_cv = dout("o_cv", [DEPTH, 512, 512])
    wub = dscr("wub", [DEPTH, NU, 128, 8, 512])
    wob = dscr("wob", [DEPTH, 6, 64, 4, 1024])
    hp1q = dscr("hp1q", [2048, D_MODEL], F32)
    hpg = dscr("hpg", [SEQ, D_MODEL], F32)
    ccs = dscr("ccs", [256, D_MODEL], F32)
    COMBS = dscr("combs", [4, 17, 2, 128, 512])
    AMASK = dscr("amask", [16, 128, 512])
    ccd = dscr("ccd", [1024, D_MODEL], F32)
    S_KCL = dscr("scr_kcl", [DEPTH, 8, 64, 1024]); S_VCL = dscr("scr_vcl", [DEPTH, 1024, 512])
    tab5 = dscr("tab5", [8, LTAB], F32)
    tabc = dscr("tabc", [DEPTH, 8, LTAB], F32)
    S_KA = dscr("scr_s_ka", [DEPTH, 8, 64, SEQ]); S_KC = dscr("scr_s_kc", [DEPTH, 8, 64, SEQ])
    S_KB = dscr("scr_s_kb", [DEPTH, 2, 64, SEQ]); S_IK = dscr("scr_s_ik", [DEPTH, 64, SEQ])
    S_VA = dscr("scr_s_va", [DEPTH, SEQ, 512]); S_VC = dscr("scr_s_vc", [DEPTH, SEQ, 512])
    S_VB = dscr("scr_s_vb", [DEPTH, SEQ, 128])

    xs = din("xs", [DEC_SEQ, D_MODEL])
    ca_k = din("ca_k", [DEPTH, PAST, 512]); ca_v = din("ca_v", [DEPTH, PAST, 512]); ca_lf = din("ca_lf", [DEPTH, PAST, 8])
    cb_k = din("cb_k", [DEPTH, PAST, 128]); cb_v = din("cb_v", [DEPTH, PAST, 128]); cb_ik = din("cb_ik", [DEPTH, PAST, 32])
    cc_k = din("cc_k", [DEPTH, 512, 512]); cc_v = din("cc_v", [DEPTH, 512, 512])
    y_s = dout("y_s", [DEC_SEQ, D_MODEL])
    s_ak = dout("s_ak", [DEPTH, DEC_SEQ, 512]); s_av = dout("s_av", [DEPTH, DEC_SEQ, 512]); s_lf = dout("s_lf", [DEPTH, DEC_SEQ, 8])
    s_bk = dout("s_bk", [DEPTH, DEC_SEQ, 128]); s_bv = dout("s_bv", [DEPTH, DEC_SEQ, 128]); s_ik = dout("s_ik", [DEPTH, DEC_SEQ, 32])
    s_ck = dout("s_ck", [DEPTH, DEC_SEQ, 512]); s_cv = dout("s_cv", [DEPTH, DEC_SEQ, 512])
    hs1 = dscr("hs1", [DEC_SEQ, D_MODEL], F32)
    MBS = [dscr(f"mbs{i}", [128, SEQ]) for i in range(4)]
    SS_KA = dscr("ss_ka", [DEPTH, 8, 64, 1152]); SS_KC = dscr("ss_kc", [DEPTH, 8, 64, 640])
    SS_KB = dscr("ss_kb", [DEPTH, 2, 64, 1152]); SS_IK = dscr("ss_ik", [DEPTH, 64, 1152])
    SS_VA = dscr("ss_va", [DEPTH, 1152, 512]); SS_VC = dscr("ss_vc", [DEPTH, 640, 512]); SS_VB = dscr("ss_vb", [DEPTH, 1152, 128])

    sb = lambda name, shape, dt: es.enter_context(nc.sbuf_tensor(name, list(shape), dt))
    wring = [sb(f"wring{i}", [128, 8, 512], BF16) for i in range(2)]
    SC = sb("SC", [128, 8192], F32)
    junk = sb("junk", [128, 8192], BF16)
    hT = sb("hT", [128, 8, 512], BF16)
    Q = sb("Q", [65, 8, 512], BF16)
    zg = {t: sb("zg" + t, [64, 8, 512], BF16) for t in "abc"}
    iqT = sb("iqT", [64, 4, 512], BF16)
    kst = sb("kst", [64, 8, 512], BF16)
    xt = [sb(f"xt{i}", [128, 1024], F32) for i in range(2)]
    xn = sb("xn", [128, 1024], BF16)
    st = [sb(f"st{i}", [128, 512], F32) for i in range(2)]
    vst = [sb(f"vst{i}", [128, 512], BF16) for i in range(2)]
    kbuf = [sb(f"kbuf{i}", [65, 4, 512], BF16) for i in range(2)]
    vbuf = [sb(f"vbuf{i}", [128, 4, 4, 65], BF16) for i in range(2)]
    Pt = [sb(f"Pt{i}", [128, 512], BF16) for i in range(4)]
    Mb = [sb(f"Mb{i}", [128, 512], BF16) for i in range(2)]
    Rr = [sb(f"Rr{i}", [128, 512], F32) for i in range(3)]
    ikbuf = [sb(f"ikbuf{i}", [64, 512], BF16) for i in range(2)]
    b5 = sb("b5", [128, 2, 8, 128], BF16)
    bc = sb("bc", [128, 2, 8, 128], BF16)
    cstf = sb("cstf", [128, 8, 128], F32)
    identb = sb("identb", [128, 128], BF16)
    i4b = sb("i4b", [128, 4, 128], BF16)
    ma0b = sb("ma0b", [128, 128], BF16); cm0b = sb("cm0b", [128, 128], BF16); cm4b = sb("cm4b", [128, 128], BF16)
    cstore = sb("cstore", [128, 64, 8], F32)
    nbias = sb("nbias", [128, 64, 8], F32)
    gcol = sb("gcol_s", [128, DEPTH * 8], F32)
    fgb = sb("fgb", [128, D_MODEL], F32)
    bfb = sb("bfb_s", [128, DEPTH * 8], F32)
    small = sb("small", [128, 64], F32)
    cntb = sb("cntb", [128, NBIS], F32)
    wabs = sb("wabs", [128, 4, 8], F32); wsgn = sb("wsgn", [128, 4, 8], F32)
    lfb = sb("lfb", [128, 8], F32)
    tot = sb("tot", [1, 8], F32)
    tots = sb("tots", [1, 17, 8], F32)
    totbc = sb("totbc", [128, 8], F32)
    lf4 = sb("lf4", [128, 4, 8], F32)
    cown = sb("cown", [128, 4, 8], F32)
    xacc = sb("xacc", [128, D_MODEL], F32)
    pcore = sb("pcore_s", [128, 1024], F32)
    iota5 = sb("iota5_s", [128, 512], F32)
    comb = [sb(f"comb{i}", [128, 4, 128], BF16) for i in range(2)]
    ones1 = sb("ones1", [65, 128], F32)
    cbc = sb("cbc", [128, 8], F32)
    rq = sb("rq", [128, 4, 8], F32)
    rT = sb("rT", [8, 512], BF16)
    rden = sb("rden", [65, 512], F32)
    otmp = sb("otmp", [64, 512], F32)
    hank = sb("hank", [128, 128], F32)
    t5s = sb("t5s", [32, 8], F32); oh5s = sb("oh5s", [32, LTAB], F32)
    crs = sb("crs", [128, 3, 8], F32); ohcs = sb("ohcs", [128, 3, LTAB], F32)
    tabs = sb("tabs", [8, LTAB], F32)
    ps = [es.enter_context(nc.psum_tensor(f"ps{i}", [128, 512], F32)) for i in range(8)]
    psn = [f"ps{i}" for i in range(8)]

    P = Prog(nc)
    _early = {}

    def nxt_early(key, n):
        v = _early.get(key, 0)
        _early[key] = v + 1
        return v % n

    IDENT = cstf[:, 0, :]; JM = cstf[:, 1, :]; TRI = cstf[:, 2, :]; E0ROW = cstf[:, 3, :]
    ADM = cstf[:, 7, :]
    E127 = cstf[:, 1, 0:1]

    P.dma("sp", pcore[:], pc_d, w=["qrelb", "krel", "qlimc", "sel01", "selb", "pvb"])
    P.dma("sp", iota5[:], iota_d, w=["iota5"])
    qrelb = pcore[:, 0:512]; krel = pcore[:, 512:528]; qlimc = pcore[:, 528:544]; sel01 = pcore[:, 544:680]
    selb = pcore[:, 680:684]; pvb = pcore[:, 684:685]
    P.dma("sp", cstf[:].rearrange("p a b -> p (a b)"), cst_d, w=["cstf"])
    P.dma("sp", gcol[:], gcol_d, w=["gcol"])
    P.dma("sp", fgb[:], fg_d.to_broadcast([128, D_MODEL]) if hasattr(fg_d, "to_broadcast") else bass.AP(fg_d.tensor, 0, [[0, 128], [1, D_MODEL]]), w=["fgb"])
    P.dma("sp", bfb[:], bass.AP(bf_d.tensor, 0, [[0, 128], [1, DEPTH * 8]]), w=["bfb"])
    P.dma("sp", t5s[:], t5_d, w=["t5s"])
    P.dma("sp", oh5s[:], oh5_d, w=["oh5s"])
    P.dma("sp", ohcs[:], ohc_d.rearrange("c p l -> p c l"), w=["ohcs"])
    P.dve(lambda e: e.tensor_copy(identb[:], IDENT), r=["cstf"], w=["identb"])
    for k in range(4):
        P.dve(lambda e, k=k: e.tensor_copy(i4b[:, k, :], IDENT), r=["cstf"], w=["i4b"])
    P.dve(lambda e: e.tensor_copy(ma0b[:], cstf[:, 4, :]), r=["cstf"], w=["ma0b"])
    P.dve(lambda e: e.tensor_copy(cm0b[:], cstf[:, 5, :]), r=["cstf"], w=["cm0b"])
    P.dve(lambda e: e.tensor_copy(cm4b[:], cstf[:, 6, :]), r=["cstf"], w=["cm4b"])
    onesf = cstf[:, 4, :]
    P.dve(lambda e: e.memset(onesf, 1.0), r=["ma0b"], w=["cstf", "onesf"])
    P.dve(lambda e: e.memset(ones1[:], 1.0), w=["ones1"])
    for i in range(2):
        P.pool(lambda e, i=i: e.memset(kbuf[i][:], 1.0), w=[f"kbuf{i}"])
        P.pool(lambda e, i=i: e.memset(vbuf[i][:], 1.0), w=[f"vbuf{i}"])

    stages = [(SC[:, 0:4096], junk[:, 0:4096], "SCa", "junka"), (SC[:, 4096:8192], junk[:, 4096:8192], "SCb", "junkb")]
    sk = 0
    for l in range(nlayers):
        for u in range(NU):
            s32, s16, n32, n16 = stages[sk % 2]
            sk += 1
            stg = s32.rearrange("p (c n) -> p c n", c=8)
            stgb = s16.rearrange("p (c n) -> p c n", c=8)
            P.dma("sp", stg, wu[l, u], w=[n32])
            P.act(lambda e: e.activation(stgb, stg, AF.Copy), r=[n32], w=[n16])
            P.dma("pool", wub[l, u], stgb, r=[n16], w=[f"wub{l}_{u}"])
        for u in range(6):
            s32, s16, n32, n16 = stages[sk % 2]
            sk += 1
            so = s32[0:64].rearrange("p (c n) -> p c n", c=4)
            sob = s16[0:64].rearrange("p (c n) -> p c n", c=4)
            P.dma("sp", so, wo[l, u], w=[n32])
            P.act(lambda e: e.activation(sob, so, AF.Copy), r=[n32], w=[n16])
            P.dma("pool", wob[l, u], sob, r=[n16], w=[f"wob{l}_{u}"])
    P.dve(lambda e: e.memset(small[:, 50:51], 0.0), w=["SC", "junk", "SCa", "SCb", "junka", "junkb"])

    def build_tab(lhs_list, rhs_list, dst, rnames):
        for i, (a, b) in enumerate(zip(lhs_list, rhs_list)):
            P.pe(lambda e, a=a, b=b, i=i: e.matmul(ps[0][0:8, 0:LTAB], a, b, start=(i == 0), stop=(i == len(lhs_list) - 1)),
                 r=rnames, w=["ps0"])
        P.dve(lambda e: e.tensor_copy(tabs[:], ps[0][0:8, 0:LTAB]), r=["ps0"], w=["tabs"])
        P.dma("sp", dst, tabs[:], r=["tabs"], w=["tabdram"])

    def build_toeplitz(tab_ap2d, dst_tile):
        for k in range(2):
            for h in range(8):
                b0 = 128 * k
                src = bass.AP(tab_ap2d.tensor, tab_ap2d.offset + h * LTAB + b0, [[1, 128], [1, 128]])
                P.dma("sp", hank[:], src, r=["tabdram"], w=["hank"])
                P.pe(lambda e: e.matmul(ps[1][:, 0:128], JM, hank[:], start=True, stop=True), r=["hank", "cstf"], w=["ps1"])
                P.dve(lambda e, k=k, h=h: e.tensor_copy(dst_tile[:, k, h, :], ps[1][:, 0:128]), r=["ps1"], w=["btile"])

    build_tab([t5s[:]], [oh5s[:]], tab5, ["t5s", "oh5s"])
    build_toeplitz(tab5, b5)
    for qb in range(4):
        for rr in range(17):
            if (rr - 1 - qb) % 4 not in (0, 3):
                continue
            for jj in range(2):
                ci = nxt_early("cmb", 2)
                sc0 = sel01[:, (qb * 17 + rr) * 2:(qb * 17 + rr) * 2 + 1]
                sc1 = sel01[:, (qb * 17 + rr) * 2 + 1:(qb * 17 + rr) * 2 + 2]
                P.dve(lambda e: e.tensor_scalar(comb[ci][:, :, :], b5[:, 0, 4 * jj:4 * jj + 4, :], sc0, None, ALU.mult), r=["btile", "sel01"], w=[f"comb{ci}"])
                P.dve(lambda e: e.scalar_tensor_tensor(comb[ci][:, :, :], b5[:, 1, 4 * jj:4 * jj + 4, :], sc1, comb[ci][:, :, :], ALU.mult, ALU.add),
                      r=["btile", "sel01", f"comb{ci}"], w=[f"comb{ci}"])
                P.dma("pool", COMBS[qb, rr, jj], comb[ci][:].rearrange("p a b -> p (a b)"), r=[f"comb{ci}"], w=["~combs"])
    for rr in range(16):
        mi = nxt_early("mb", 2)
        P.dve(lambda e: e.tensor_scalar(Mb[mi][:, :], qrelb[:, :], krel[:, rr:rr + 1], NEG, ALU.is_lt, ALU.mult), r=["qrelb", "krel"], w=[f"Mb{mi}"])
        P.dma("pool", AMASK[rr], Mb[mi][:, :], r=[f"Mb{mi}"], w=["~amask"])

    wk = [0]

    def load_w(l, u):
        i = wk[0] % 2
        wk[0] += 1
        P.dma("sp", wring[i][:], wub[l, u], r=[f"wub{l}_{u}"], w=[f"wring{i}"])
        return wring[i], f"wring{i}"

    def load_wo(l, u):
        i = wk[0] % 2
        wk[0] += 1
        dst = wring[i][0:64].rearrange("p c n -> p (c n)").rearrange("p (c n) -> p c n", c=4)
        P.dma("sp", dst, wob[l, u], r=[f"wob{l}_{u}"], w=[f"wring{i}"])
        return dst, f"wring{i}"

    rot = {"s": 0, "pt": 0, "kv": 0, "st": 0, "x": 0, "ik": 0, "rr": 0, "mb": 0, "ips": 0, "cmb": 0}

    def nxt(key, n):
        v = rot[key] % n
        rot[key] += 1
        return v

    def norm_block(l, xsrc_ap, tb, nrow=128, rname=None, sb_src=None):
        if sb_src is not None:
            X, xname = sb_src
        else:
            xi = nxt("x", 2)
            X = xt[xi]; xname = f"xt{xi}"
        if sb_src is None:
            P.dma("sp", X[0:nrow, :], xsrc_ap, r=(list(rname) if isinstance(rname, (list, tuple)) else ([rname] if rname else [])), w=[xname])
        P.act(lambda e: e.activation(junk[0:nrow, 0:1024], X[0:nrow, :], AF.Square, accum_out=small[0:nrow, 0:1]),
              r=[xname], w=["junk", "small0"])
        P.dve(lambda e: e.tensor_scalar(small[0:nrow, 1:2], small[0:nrow, 0:1], 1.0 / D_MODEL, EPS, ALU.mult, ALU.add), r=["small0"], w=["small1"])
        P.act(lambda e: e.activation(small[0:nrow, 2:3], small[0:nrow, 1:2], AF.Ln), r=["small1"], w=["small2"])
        P.act(lambda e: e.activation(small[0:nrow, 3:4], small[0:nrow, 2:3], AF.Exp, scale=-0.5), r=["small2"], w=["small3"])
        P.dve(lambda e: e.tensor_scalar(xn[0:nrow, :], X[0:nrow, :], small[0:nrow, 3:4], None, ALU.mult), r=[xname, "small3"], w=["xn"])
        psb = ps[7].bitcast(BF16)
        for c in range(8):
            P.pe(lambda e, c=c: e.transpose(psb[:, c * 128:c * 128 + nrow], xn[0:nrow, c * 128:(c + 1) * 128], identb[0:nrow, 0:nrow]),
                 r=["xn", "identb"], w=["ps7"])
        for c in range(8):
            P.dve(lambda e, c=c: e.tensor_scalar(hT[:, c, tb * 128:tb * 128 + nrow], psb[:, c * 128:c * 128 + nrow],
                                                 gcol[:, l * 8 + c:l * 8 + c + 1], None, ALU.mult),
                  r=["ps7", "gcol"], w=["hT"])

    def fm_unit(l, uname, ntok, evac, blocks=tuple(range(8)), wres=None):
        W, wn = wres if wres is not None else load_w(l, UNITS.index(uname))
        for j in blocks:
            si = nxt("s", 4)
            for c in range(8):
                P.pe(lambda e, j=j, c=c, si=si: e.matmul(ps[si][0:64, 0:ntok], W[:, c, j * 64:(j + 1) * 64], hT[:, c, 0:ntok],
                                                        start=(c == 0), stop=(c == 7)), r=[wn, "hT"], w=[psn[si]])
            evac(j, ps[si], psn[si])

    def finish_head(O, oname, zt, zname, hsel, ncol, csl):
        if isinstance(hsel, tuple):
            nh_ = hsel[1] - hsel[0]
            zv = zt[:, hsel[0]:hsel[1], csl]
            ov = otmp[:, 0:ncol].rearrange("p (h q) -> p h q", h=nh_)
            Ov = O[0:64, 0:ncol].rearrange("p (h q) -> p h q", h=nh_)
        else:
            zv = zt[:, hsel, csl]
            ov = otmp[:, 0:ncol]
            Ov = O[0:64, 0:ncol]
        P.act(lambda e: e.activation(rden[64:65, 0:ncol], O[64:65, 0:ncol], AF.Ln), r=[oname], w=["rden"])
        P.act(lambda e: e.activation(rden[64:65, 0:ncol], rden[64:65, 0:ncol], AF.Exp, scale=-1.0), r=["rden"], w=["rden"])
        P.dve(lambda e: e.tensor_tensor(ov, Ov, zv, ALU.mult), r=[oname, zname], w=["otmp"])
        bi_ = nxt("s", 4)
        P.pe(lambda e: e.matmul(ps[bi_][0:64, 0:ncol], ones1[64:65, 0:64], rden[64:65, 0:ncol], start=True, stop=True),
             r=["rden", "ones1"], w=[psn[bi_]])
        bv = ps[bi_][0:64, 0:ncol].rearrange("p (h q) -> p h q", h=nh_) if isinstance(hsel, tuple) else ps[bi_][0:64, 0:ncol]
        P.dve(lambda e: e.tensor_tensor(zv, ov, bv, ALU.mult), r=["otmp", psn[bi_]], w=[zname])

    def load_kv(Ksrc, Vsrc, h0, nh, k0, nk, krows_name):
        i = nxt("kv", 2)
        P.dma("sp", kbuf[i][0:64, 0:nh, 0:nk], Ksrc[h0:h0 + nh, :, k0:k0 + nk].rearrange("h d k -> d h k"), r=[krows_name], w=[f"kbuf{i}"])
        nb = (nk + 127) // 128
        for b in range(nb):
            n = min(128, nk - b * 128)
            P.dma("sp", vbuf[i][0:n, b, 0:nh, 0:64],
                  Vsrc[k0 + b * 128:k0 + b * 128 + n, h0 * 64:(h0 + nh) * 64].rearrange("k (h d) -> k h d", h=nh),
                  r=[krows_name], w=[f"vbuf{i}"])
        return i

    class Attn:
        def __init__(self):
            self.pend = []

        def tile(self, t):
            si = nxt("s", 4)
            S = ps[si]; n = t["n"]; qlo, qhi = t["qlo"], t["qhi"]
            nadd = len(t["adds"])
            P.pe(lambda e: e.matmul(S[0:n, qlo:qhi], t["kT"], t["qap"], start=True, stop=(nadd == 0)),
                 r=t["names"] + ["Q"], w=[psn[si]])
            for ai, (clo, chi, la, ra, an) in enumerate(t["adds"]):
                P.pe(lambda e, clo=clo, chi=chi, la=la, ra=ra, ai=ai: e.matmul(S[0:n, clo:chi], la, ra, start=False, stop=(ai == nadd - 1)),
                     r=an, w=[psn[si]])
            pi = nxt("pt", 4)
            if t["bias"] is not None:
                P.act(lambda e: e.activation(Pt[pi][0:n, qlo:qhi], S[0:n, qlo:qhi], AF.Exp, bias=t["bias"]),
                      r=[psn[si], "nbias"], w=[f"Pt{pi}"])
            else:
                P.act(lambda e: e.activation(Pt[pi][0:n, qlo:qhi], S[0:n, qlo:qhi], AF.Exp), r=[psn[si]], w=[f"Pt{pi}"])
            t["pi"] = pi
            self.pend.append(t)
            if len(self.pend) > 2:
                self.pv(self.pend.pop(0))

        def pv(self, t):
            n = t["n"]; qlo, qhi = t["qlo"], t["qhi"]; pi = t["pi"]; O = t["O"]
            P.pe(lambda e: e.matmul(O[0:65, qlo:qhi], t["v"], Pt[pi][0:n, qlo:qhi], start=t["first"], stop=t["last"]),
                 r=[f"Pt{pi}"] + t["names"], w=[t["oname"]])

        def flush(self):
            while self.pend:
                self.pv(self.pend.pop(0))


    for l in range(nlayers):
        P.pool(lambda e: e.memset(crs[:], 0.0), w=["crs"])
        P.dma("sp", crs[:], crel_d[l].rearrange("c p h -> p c h"), w=["crs"])
        build_tab([crs[:, c, :] for c in range(3)], [ohcs[:, c, :] for c in range(3)], tabc[l], ["crs", "ohcs"])
        build_toeplitz(tabc[l], bc)
        P.dve(lambda e: e.memset(tot[:], 0.0), w=["tot"])
        P.dve(lambda e: e.memset(tots[:], 0.0), w=["tots"])
        P.dve(lambda e: e.memset(totbc[:], 0.0), w=["totbc"])
        KAl, KBl, IKl, VAl, VBl = S_KA[l], S_KB[l], S_IK[l], S_VA[l], S_VB[l]
        KCl, VCl = S_KCL[l], S_VCL[l]
        hist = f"~hist{l}"
        chist = f"~chist{l}"
        deferred_exchange = []

        def grow(gp):
            if l == 0:
                return xp[gp * 512:(gp + 1) * 512, :]
            return hpg[gp * 512:(gp + 1) * 512, :]

        def tm_unit(uname, handler, wres=None, cols=(0, 512)):
            W, wn = wres if wres is not None else load_w(l, UNITS.index(uname))
            c0_, c1_ = cols
            for tb in range(4):
                si = nxt("s", 4)
                for c in range(8):
                    P.pe(lambda e: e.matmul(ps[si][:, 0:c1_ - c0_], hT[:, c, tb * 128:(tb + 1) * 128], W[:, c, c0_:c1_], start=(c == 0), stop=(c == 7)),
                         r=[wn, "hT"], w=[psn[si]])
                k = nxt("st", 2)
                S_ = st[k]; sn = f"st{k}"
                P.act(lambda e: e.activation(S_[:, c0_:c1_], ps[si][:, 0:c1_ - c0_], AF.Copy), r=[psn[si]], w=[sn])
                handler(tb, S_, sn, k)

        def logf_of(S_, sn):
            P.dve(lambda e: e.tensor_tensor(lfb[:], S_[:, 288:296], bfb[:, l * 8:(l + 1) * 8], ALU.add), r=[sn, "bfb"], w=["lfb"])
            P.act(lambda e: e.activation(lfb[:], lfb[:], AF.Exp, scale=-1.0), r=["lfb"], w=["lfb"])
            P.act(lambda e: e.activation(lfb[:], lfb[:], AF.Ln, bias=1.0), r=["lfb"], w=["lfb"])
            P.dve(lambda e: e.tensor_scalar(lfb[:], lfb[:], -1.0, None, ALU.mult), r=["lfb"], w=["lfb"])

        def cum_into(dst_ap, dname):
            P.pe(lambda e: e.matmul(ps[5][:, 0:8], TRI, lfb[:], start=True, stop=False), r=["lfb", "cstf"], w=["ps5"])
            P.pe(lambda e: e.matmul(ps[5][:, 0:8], ones1[0:1, 0:128], tot[0:1, :], start=False, stop=True), r=["tot", "ones1"], w=["ps5"])
            P.dve(lambda e: e.tensor_copy(dst_ap, ps[5][:, 0:8]), r=["ps5"], w=[dname])
            P.pe(lambda e: e.matmul(ps[5][0:1, 8:16], E127, dst_ap, start=True, stop=True), r=[dname, "cstf"], w=["ps5"])
            P.dve(lambda e: e.tensor_copy(tot[:], ps[5][0:1, 8:16]), r=["ps5"], w=["tot"])

        def evac_k_to(dst3, k0, hname):
            def f(j, pt, pn):
                P.act(lambda e: e.activation(kst[:, j, :], pt[0:64, :], AF.Copy), r=[pn], w=["kst"])
                if j == 7:
                    P.dma("pool", dst3[:, :, k0:k0 + 512].rearrange("h d k -> d h k"), kst[:], r=["kst"], w=[hname])
            return f

        def evac_q(scale):
            def f(j, pt, pn):
                P.act(lambda e: e.activation(Q[0:64, j, :], pt[0:64, :], AF.Copy, scale=scale), r=[pn], w=["Q"])
            return f

        def evac_z(zt, zn):
            def f(j, pt, pn):
                P.act(lambda e: e.activation(zt[:, j, :], pt[0:64, :], AF.Silu), r=[pn], w=[zn])
            return f

        scb = SC.bitcast(BF16)
        kres = {}
        for ui, un in enumerate(("tka", "tva", "tb", "ka")):
            v = scb[:, ui * 4096:(ui + 1) * 4096].rearrange("p (c n) -> p c n", c=8)
            P.dma("sp", v, wub[l, UNITS.index(un)], r=[f"wub{l}_{UNITS.index(un)}"], w=["SC"])
            kres[un] = (v, "SC")
        v = junk[:, 4096:8192].rearrange("p (c n) -> p c n", c=8)
        P.dma("sp", v, wub[l, UNITS.index("bx")], r=[f"wub{l}_{UNITS.index('bx')}"], w=["junk", "junkW"])
        kres["bx"] = (v, "junkW")
        for gp in range(4 * nm):
            t0 = gp * 512
            src = grow(gp)
            for tb in range(4):
                norm_block(l, src[tb * 128:(tb + 1) * 128, :], tb, rname=("~hpgw" if l > 0 else None))

            def h_kv(uname):
                def f(tb, S_, sn, k):
                    r0 = t0 + tb * 128
                    dst = {"tka": o_ak, "tva": o_av, "tkc": o_ck, "tvc": o_cv}[uname]
                    if uname in ("tka", "tva"):
                        P.dma("pool", dst[l, r0:r0 + 128, :], S_[:], r=[sn])
                    else:
                        P.dma("pool", dst[l, r0 - (SEQ - 512):r0 - (SEQ - 512) + 128, :], S_[:], r=[sn])
                    if uname == "tva":
                        V_, vn = vst[k], f"vst{k}"
                        P.dve(lambda e: e.tensor_copy(V_[:], S_[:]), r=[sn], w=[vn])
                        P.dma("pool", VAl[r0:r0 + 128, :], V_[:], r=[vn], w=[hist])
                return f

            def h_tb(tb, S_, sn, k):
                r0 = t0 + tb * 128
                P.dma("pool", o_bk[l, r0:r0 + 128, :], S_[:, 0:128], r=[sn])
                P.dma("pool", o_bv[l, r0:r0 + 128, :], S_[:, 128:256], r=[sn])
                P.dma("pool", o_ik[l, r0:r0 + 128, :], S_[:, 256:288], r=[sn])
                V_, vn = vst[k], f"vst{k}"
                P.dve(lambda e: e.tensor_copy(V_[:, 0:128], S_[:, 128:256]), r=[sn], w=[vn])
                P.dma("pool", VBl[r0:r0 + 128, :], V_[:, 0:128], r=[vn], w=[hist])
                P.dve(lambda e: e.tensor_tensor(lf4[:, tb, :], S_[:, 288:296], bfb[:, l * 8:(l + 1) * 8], ALU.add), r=[sn, "bfb"], w=["lf4"])

            tm_unit("tka", h_kv("tka"), wres=kres["tka"])
            tm_unit("tva", h_kv("tva"), wres=kres["tva"])
            tm_unit("tb", h_tb, wres=kres["tb"], cols=(0, 304))
            lf4f = lf4[:].rearrange("p b h -> p (b h)")
            P.act(lambda e: e.activation(lf4f, lf4f, AF.Exp, scale=-1.0), r=["lf4"], w=["lf4"])
            P.act(lambda e: e.activation(lf4f, lf4f, AF.Ln, bias=1.0), r=["lf4"], w=["lf4"])
            P.dve(lambda e: e.tensor_scalar(lf4f, lf4f, -1.0, None, ALU.mult), r=["lf4"], w=["lf4"])
            P.dma("pool", o_lf[l, t0:t0 + 512, :].rearrange("(b p) h -> p b h", p=128), lf4[:], r=["lf4"])
            if gp == NG - 1:
                tm_unit("tkc", h_kv("tkc"))
                tm_unit("tvc", h_kv("tvc"))
            fm_unit(l, "ka", 512, evac_k_to(KAl, t0, hist), wres=kres["ka"])
            for b_ in range(4):
                for b2 in range(b_ + 1):
                    P.pe(lambda e: e.matmul(ps[5][:, b_ * 8:(b_ + 1) * 8], (TRI if b2 == b_ else onesf[:, :]), lf4[:, b2, :], start=(b2 == 0), stop=(b2 == b_)),
                         r=["lf4", "cstf", "onesf"], w=["ps5"])
            for b2 in range(4):
                P.pe(lambda e: e.matmul(ps[5][:, 32:40], onesf[:, :], lf4[:, b2, :], start=(b2 == 0), stop=(b2 == 3)), r=["lf4", "onesf"], w=["ps5"])
            for b_ in range(4):
                P.dve(lambda e: e.tensor_tensor(cstore[:, 4 * gp + b_, :], ps[5][:, b_ * 8:(b_ + 1) * 8], totbc[:, :], ALU.add), r=["ps5", "totbc"], w=["cstore"])
            P.dve(lambda e: e.tensor_tensor(totbc[:, :], ps[5][:, 32:40], totbc[:, :], ALU.add), r=["ps5", "totbc"], w=["totbc"])
            P.dve(lambda e: e.tensor_copy(tots[0:1, gp + 1, :], totbc[0:1, :]), r=["totbc"], w=["tots"])

            def evac_bx_k(j, pt, pn):
                if j < 2:
                    P.act(lambda e: e.activation(kst[:, j, :], pt[0:64, :], AF.Copy), r=[pn], w=["kst"])
                    if j == 1:
                        P.dma("pool", KBl[:, :, t0:t0 + 512].rearrange("h d k -> d h k"), kst[:, 0:2, :], r=["kst"], w=[hist])
                elif j == 6:
                    P.act(lambda e: e.activation(kst[:, 2, :], pt[0:64, :], AF.Copy), r=[pn], w=["kst"])
                    P.dma("pool", IKl[:, t0:t0 + 512], kst[:, 2, :], r=["kst"], w=[hist])
            fm_unit(l, "bx", 512, evac_bx_k, blocks=(0, 1, 6), wres=kres["bx"])

        for m in range(nm):
            own = (xq[m] if l == 0 else hp1q[m * 512:(m + 1) * 512, :])
            own_r = (None if l == 0 else [f"hp1qc{2 * m}", f"hp1qc{2 * m + 1}"])
            for part in range(2):
                for tb in range(4):
                    if part == 1:
                        norm_block(l, own[tb * 128:(tb + 1) * 128, :], tb, rname=own_r)
                    elif l == 0:
                        norm_block(l, xprev[m][tb * 128:(tb + 1) * 128, :], tb)
                    else:
                        first = True
                        for r in range(4):
                            gq = 4 * m - 1 + r
                            if gq < 0:
                                continue
                            xi = nxt("x", 2)
                            X = xt[xi]; xname = f"xt{xi}"
                            P.dma("sp", X[:], grow(gq)[tb * 128:(tb + 1) * 128, :], r=["~hpgw"], w=[xname])
                            if first:
                                P.dve(lambda e: e.tensor_scalar(xacc[:], X[:], selb[:, r:r + 1], None, ALU.mult), r=[xname, "selb"], w=["xacc"])
                            else:
                                P.dve(lambda e: e.scalar_tensor_tensor(xacc[:], X[:], selb[:, r:r + 1], xacc[:], ALU.mult, ALU.add), r=[xname, "selb", "xacc"], w=["xacc"])
                            first = False
                        norm_block(l, None, tb, sb_src=(xacc, "xacc"))

                def h_c(uname):
                    def f(tb, S_, sn, k):
                        if uname == "tvc":
                            V_, vn = vst[k], f"vst{k}"
                            P.dve(lambda e: e.tensor_copy(V_[:], S_[:]), r=[sn], w=[vn])
                            P.dma("pool", VCl[part * 512 + tb * 128:part * 512 + (tb + 1) * 128, :], V_[:], r=[vn], w=[chist])
                    return f
                tm_unit("tvc", h_c("tvc"))
                fm_unit(l, "kc", 512, evac_k_to(KCl, part * 512, chist))
            g0 = 16 * m
            nkb = 16 * m + 16
            for r in range(4):
                if r == 0:
                    P.dve(lambda e: e.tensor_scalar(tot[0:1, :], tots[0:1, 4 * m + r, :], selb[0:1, r:r + 1], None, ALU.mult), r=["tots", "selb"], w=["tot"])
                else:
                    P.dve(lambda e: e.scalar_tensor_tensor(tot[0:1, :], tots[0:1, 4 * m + r, :], selb[0:1, r:r + 1], tot[0:1, :], ALU.mult, ALU.add),
                          r=["tots", "selb", "tot"], w=["tot"])

            def h_own(tb, S_, sn, k):
                P.dve(lambda e: e.tensor_tensor(lf4[:, tb, :], S_[:, 288:296], bfb[:, l * 8:(l + 1) * 8], ALU.add), r=[sn, "bfb"], w=["lf4"])
                P.dve(lambda e: e.tensor_scalar(wsgn[:, tb, :], S_[:, 296:304], 0.0, 2.0, ALU.is_ge, ALU.mult), r=[sn], w=["wsgn"])
                P.dve(lambda e: e.tensor_scalar(wsgn[:, tb, :], wsgn[:, tb, :], -1.0, None, ALU.add), r=["wsgn"], w=["wsgn"])
                P.dve(lambda e: e.scalar_tensor_tensor(wabs[:, tb, :], S_[:, 296:304], IDXS, wsgn[:, tb, :], ALU.mult, ALU.mult), r=[sn, "wsgn"], w=["wabs"])
            tm_unit("tb", h_own, cols=(288, 304))
            lf4q = lf4[:].rearrange("p b h -> p (b h)")
            P.act(lambda e: e.activation(lf4q, lf4q, AF.Exp, scale=-1.0), r=["lf4"], w=["lf4"])
            P.act(lambda e: e.activation(lf4q, lf4q, AF.Ln, bias=1.0), r=["lf4"], w=["lf4"])
            P.dve(lambda e: e.tensor_scalar(lf4q, lf4q, -1.0, None, ALU.mult), r=["lf4"], w=["lf4"])
            for b_ in range(4):
                P.pe(lambda e: e.matmul(ps[5][:, b_ * 8:(b_ + 1) * 8], ones1[0:1, 0:128], tot[0:1, :], start=True, stop=False), r=["tot", "ones1"], w=["ps5"])
                for b2 in range(b_ + 1):
                    P.pe(lambda e: e.matmul(ps[5][:, b_ * 8:(b_ + 1) * 8], (TRI if b2 == b_ else onesf[:, :]), lf4[:, b2, :], start=False, stop=(b2 == b_)),
                         r=["lf4", "cstf", "onesf"], w=["ps5"])
            P.dve(lambda e: e.tensor_copy(cown[:].rearrange("p b h -> p (b h)"), ps[5][:, 0:32]), r=["ps5"], w=["cown"])
            P.pe(lambda e: e.matmul(ps[5][:, 16:24], E0ROW, cstore[:, g0, :], start=True, stop=True), r=["cstore", "cstf"], w=["ps5"])
            P.dve(lambda e: e.tensor_copy(cbc[:], ps[5][:, 16:24]), r=["ps5"], w=["cbc"])
            for h in range(8):
                P.dve(lambda e: e.tensor_scalar(nbias[:, 0:nkb, h], cstore[:, 0:nkb, h], cbc[:, h:h + 1], -1.0, ALU.subtract, ALU.mult),
                      r=["cstore", "cbc"], w=["nbias"])
            for tb in range(4):
                P.dve(lambda e: e.tensor_tensor(rq[:, tb, :], cown[:, tb, :], cbc[:], ALU.subtract), r=["cown", "cbc"], w=["rq"])
                P.pe(lambda e: e.transpose(ps[6][0:8, tb * 128:(tb + 1) * 128], rq[:, tb, :], IDENT), r=["rq", "cstf"], w=["ps6"])
            P.dve(lambda e: e.tensor_copy(rT[:], ps[6][0:8, :]), r=["ps6"], w=["rT"])

            def evac_bx_q(j, pt, pn):
                P.act(lambda e: e.activation(iqT[:, j - 2, :], pt[0:64, :], AF.Copy), r=[pn], w=["iqT"])

            def a_half(half):
                at = Attn()
                for sbk in range(4 * m + 4):
                    bi = load_kv(KAl, VAl, half * 4, 4, sbk * 512, 512, hist)
                    trail = (sbk >= 4 * m)
                    for kb in range(4):
                        adds = []
                        if trail:
                            rr = (sbk - 4 * m) * 4 + kb
                            mi = nxt("mb", 2)
                            P.dma("sp", Mb[mi][:, :], AMASK[rr], r=["~amask"], w=[f"Mb{mi}"])
                            adds = [(0, 512, identb[:], Mb[mi][:, :], ["identb", f"Mb{mi}"])]
                        for i in range(4):
                            hh = half * 4 + i
                            at.tile(dict(kT=kbuf[bi][0:65, i, kb * 128:(kb + 1) * 128], v=vbuf[bi][:, kb, i, 0:65], n=128, qlo=0, qhi=512,
                                         qap=Q[0:65, hh, 0:512], adds=adds, bias=nbias[:, 4 * sbk + kb, hh:hh + 1],
                                         names=[f"kbuf{bi}", f"vbuf{bi}"], O=ps[4 + i], oname=psn[4 + i],
                                         first=(sbk == 0 and kb == 0), last=(sbk == 4 * m + 3 and kb == 3)))
                at.flush()
                for i in range(4):
                    finish_head(ps[4 + i], psn[4 + i], zg["a"], "zga", half * 4 + i, 512, slice(0, 512))

            def c_half(half):
                at = Attn()
                started = [False] * 4
                for sbl in range(2):
                    bi = load_kv(KCl, VCl, half * 4, 4, sbl * 512, 512, chist)
                    for kb in range(4):
                        r_ = 4 * sbl + kb
                        qb_lo, qb_hi = max(0, r_ - 4), min(3, r_)
                        qlo, qhi = qb_lo * 128, (qb_hi + 1) * 128
                        for i in range(4):
                            hh = half * 4 + i
                            adds = []
                            for qb in range(qb_lo, qb_hi + 1):
                                dl = r_ - 4 - qb
                                c0, c1 = qb * 128, (qb + 1) * 128
                                if dl == 0:
                                    adds.append((c0, c1, identb[:], bc[:, 0, hh, :], ["identb", "btile"]))
                                    adds.append((c0, c1, identb[:], cm0b[:], ["identb", "cm0b"]))
                                elif dl == -1:
                                    adds.append((c0, c1, identb[:], bc[:, 1, hh, :], ["identb", "btile"]))
                                elif dl == -4:
                                    adds.append((c0, c1, identb[:], cm4b[:], ["identb", "cm4b"]))
                            at.tile(dict(kT=kbuf[bi][0:64, i, kb * 128:(kb + 1) * 128], v=vbuf[bi][:, kb, i, 0:65], n=128, qlo=qlo, qhi=qhi,
                                         qap=Q[0:64, hh, qlo:qhi], adds=adds, bias=(pvb[:, 0:1] if (m == 0 and sbl == 0) else None),
                                         names=[f"kbuf{bi}", f"vbuf{bi}"], O=ps[4 + i], oname=psn[4 + i],
                                         first=(not started[i]), last=(sbl == 1 and kb == 3)))
                            started[i] = True
                at.flush()
                for i in range(4):
                    finish_head(ps[4 + i], psn[4 + i], zg["c"], "zgc", half * 4 + i, 512, slice(0, 512))

            def b_topk(qb):
                NK = (16 * m + 13 + qb) * 128
                qs = slice(qb * 128, (qb + 1) * 128)
                for k0 in range(0, NK, 512):
                    nk = min(512, NK - k0)
                    ii = nxt("ik", 2)
                    P.dma("sp", ikbuf[ii][:, 0:nk], IKl[:, k0:k0 + nk], r=[hist], w=[f"ikbuf{ii}"])
                    for h in range(8):
                        base = 32 * (h % 2)
                        pi_ = 1 + nxt("ips", 3)
                        P.pe(lambda e: e.matmul(ps[pi_][:, 0:nk], iqT[base:base + 32, h // 2, qs], ikbuf[ii][base:base + 32, 0:nk], start=True, stop=True),
                             r=["iqT", f"ikbuf{ii}"], w=[psn[pi_]])
                        ri = nxt("rr", 3)
                        P.act(lambda e: e.activation(Rr[ri][:, 0:nk], ps[pi_][:, 0:nk], AF.Relu, scale=wabs[:, qb, h:h + 1]), r=[psn[pi_], "wabs"], w=[f"Rr{ri}"])
                        if h == 0:
                            P.dve(lambda e: e.tensor_scalar(SC[:, k0:k0 + nk], Rr[ri][:, 0:nk], wsgn[:, qb, 0:1], None, ALU.mult), r=[f"Rr{ri}", "wsgn"], w=["SC"])
                        else:
                            P.dve(lambda e: e.scalar_tensor_tensor(SC[:, k0:k0 + nk], Rr[ri][:, 0:nk], wsgn[:, qb, h:h + 1], SC[:, k0:k0 + nk], ALU.mult, ALU.add),
                                  r=[f"Rr{ri}", "wsgn", "SC"], w=["SC"])
                for ch in range(4):
                    c0 = g0 * 128 + ch * 512
                    wd = min(512, NK - c0)
                    if wd <= 0:
                        continue
                    ri = nxt("rr", 3)
                    P.dve(lambda e: e.tensor_scalar(Rr[ri][:, 0:wd], iota5[:, 0:wd], qlimc[:, qb * 4 + ch:qb * 4 + ch + 1], -1e30, ALU.is_ge, ALU.mult),
                          r=["iota5", "qlimc"], w=[f"Rr{ri}"])
                    P.dve(lambda e: e.tensor_tensor(SC[:, c0:c0 + wd], SC[:, c0:c0 + wd], Rr[ri][:, 0:wd], ALU.add), r=["SC", f"Rr{ri}"], w=["SC"])
                P.dve(lambda e: e.memset(cntb[:], 0.0), w=["cntb"])
                P.dve(lambda e: e.memset(small[:, 8:9], 0.0), w=["cand"])
                for it in range(NBIS):
                    stp = 64.0 * (0.5 ** it)
                    P.dve(lambda e: e.tensor_scalar(junk[:, 0:NK], SC[:, 0:NK], small[:, 8:9], 0.0, ALU.is_ge, ALU.add, accum_out=cntb[:, it:it + 1]),
                          r=["SC", "cand", "cntb"], w=["junk", "cntb"])
                    a, b_ = (stp, -0.5 * stp) if it < NBIS - 1 else (stp, -stp)
                    P.dve(lambda e: e.tensor_scalar(small[:, 9:10], cntb[:, it:it + 1], float(TOPK), a, ALU.is_ge, ALU.mult), r=["cntb"], w=["fl"])
                    P.dve(lambda e: e.scalar_tensor_tensor(small[:, 8:9], small[:, 9:10], b_, small[:, 8:9], ALU.add, ALU.add), r=["fl", "cand"], w=["cand"])
                P.dve(lambda e: e.tensor_scalar(junk[:, 0:NK], SC[:, 0:NK], small[:, 8:9], NEG, ALU.is_lt, ALU.mult), r=["SC", "cand"], w=["junk"])
                P.dma("pool", MBS[qb][:, 0:NK], junk[:, 0:NK], r=["junk"], w=[f"mbs{qb}"])

            def b_attn(qb):
                qs = slice(qb * 128, (qb + 1) * 128)
                nkq = 16 * m + 13 + qb
                at = Attn()
                for sbk in range((nkq + 3) // 4):
                    k0 = sbk * 512
                    nk = min(512, nkq * 128 - k0)
                    mi = nxt("mb", 2)
                    P.dma("sp", Mb[mi][:, 0:nk], MBS[qb][:, k0:k0 + nk], r=[f"mbs{qb}"], w=[f"Mb{mi}"])
                    bi = load_kv(KBl, VBl, 0, 2, k0, nk, hist)
                    for kb in range(nk // 128):
                        gkb = sbk * 4 + kb
                        for jj in range(2):
                            adds = [(0, 512, Mb[mi][:, kb * 128:(kb + 1) * 128], i4b[:].rearrange("p a b -> p (a b)"), [f"Mb{mi}", "i4b"])]
                            if gkb >= g0 - 1 and (gkb - g0 - qb) % 4 in (0, 3):
                                rr = gkb - g0 + 1
                                ci = nxt("cmb", 2)
                                sc0 = sel01[:, (qb * 17 + rr) * 2:(qb * 17 + rr) * 2 + 1]
                                sc1 = sel01[:, (qb * 17 + rr) * 2 + 1:(qb * 17 + rr) * 2 + 2]
                                P.dma("sp", comb[ci][:].rearrange("p a b -> p (a b)"), COMBS[qb, rr, jj], r=["~combs"], w=[f"comb{ci}"])
                                adds.append((0, 512, identb[:], comb[ci][:].rearrange("p a b -> p (a b)"), ["identb", f"comb{ci}"]))
                            at.tile(dict(kT=kbuf[bi][0:64, jj, kb * 128:(kb + 1) * 128], v=vbuf[bi][:, kb, jj, 0:65], n=128, qlo=0, qhi=512,
                                         qap=Q[0:64, 4 * jj:4 * jj + 4, qs], adds=adds, bias=None,
                                         names=[f"kbuf{bi}", f"vbuf{bi}"], O=ps[4 + jj], oname=psn[4 + jj],
                                         first=(gkb == 0), last=(gkb == nkq - 1)))
                at.flush()
                for jj in range(2):
                    finish_head(ps[4 + jj], psn[4 + jj], zg["b"], "zgb", (4 * jj, 4 * jj + 4), 512, qs)

            fm_unit(l, "bx", 512, evac_bx_q, blocks=(2, 3, 4, 5))
            b_topk(0)
            fm_unit(l, "za", 512, evac_z(zg["a"], "zga"))
            fm_unit(l, "qa", 512, evac_q(SCALE))
            for h in range(8):
                P.dma("sp", Q[64:65, h, :], rT[h:h + 1, :], r=["rT"], w=["Q"])
            a_half(0)
            b_topk(1)
            a_half(1)
            fm_unit(l, "zc", 512, evac_z(zg["c"], "zgc"))
            fm_unit(l, "qc", 512, evac_q(SCALE))
            c_half(0)
            c_half(1)
            fm_unit(l, "zb", 512, evac_z(zg["b"], "zgb"))
            fm_unit(l, "qb", 512, evac_q(SCALE))
            b_attn(0)
            b_topk(2)
            b_attn(1)
            b_topk(3)
            b_attn(2)
            b_attn(3)

            allz = [zg["a"], zg["b"], zg["c"]]
            alln = ["zga", "zgb", "zgc"]
            for u in range(6):
                Wo_, won = load_wo(l, u)
                for hq in range(4):
                    hidx = u * 4 + hq
                    zt, zn = allz[hidx // 8], alln[hidx // 8]
                    for tb in range(4):
                        for n_ in range(2):
                            P.pe(lambda e: e.matmul(ps[tb * 2 + n_][:, :], zt[:, hidx % 8, tb * 128:(tb + 1) * 128], Wo_[:, hq, n_ * 512:(n_ + 1) * 512],
                                                    start=(hidx == 0), stop=(hidx == 23)), r=[zn, won], w=[psn[tb * 2 + n_]])
            for tb in range(4):
                xi = nxt("x", 2)
                X = xt[xi]; xname = f"xt{xi}"
                P.dma("sp", X[:], own[tb * 128:(tb + 1) * 128, :], r=(own_r if own_r else []), w=[xname])
                for n_ in range(2):
                    P.dve(lambda e: e.tensor_tensor(X[:, n_ * 512:(n_ + 1) * 512], X[:, n_ * 512:(n_ + 1) * 512], ps[tb * 2 + n_][:, :], ALU.add),
                          r=[xname, psn[tb * 2 + n_]], w=[xname])
                r0 = m * 512 + tb * 128
                if l < nlayers - 1:
                    P.dma("pool", hp1q[r0:r0 + 128, :], X[:], r=[xname], w=[f"hp1qc{r0 // 256}"])
                else:
                    P.act(lambda e: e.activation(junk[:, 0:1024], X[:], AF.Square, accum_out=small[:, 16:17]), r=[xname], w=["junk", "fs0"])
                    P.dve(lambda e: e.tensor_scalar(small[:, 17:18], small[:, 16:17], 1.0 / D_MODEL, EPS, ALU.mult, ALU.add), r=["fs0"], w=["fs1"])
                    P.act(lambda e: e.activation(small[:, 18:19], small[:, 17:18], AF.Ln), r=["fs1"], w=["fs2"])
                    P.act(lambda e: e.activation(small[:, 19:20], small[:, 18:19], AF.Exp, scale=-0.5), r=["fs2"], w=["fs3"])
                    P.dve(lambda e: e.scalar_tensor_tensor(X[:], X[:], small[:, 19:20], fgb[:], ALU.mult, ALU.mult), r=[xname, "fs3", "fgb"], w=[xname])
                    P.dma("pool", y_q[r0:r0 + 128, :], X[:], r=[xname])
            def emit_exchange(m=m):
                for hf in range(2):
                    cidx = 2 * m + hf
                    P.dma("pool", ccs, hp1q[cidx * 256:(cidx + 1) * 256, :], r=[f"hp1qc{cidx}"], w=["ccs"])
                    P.add("pool", lambda e: e.collective_compute("AllGather", ALU.bypass, replica_groups=[[0, 1, 2, 3], [4, 5, 6, 7]],
                                                                 ins=[ccs.opt()], outs=[ccd.opt()]), r=["ccs"], w=["ccd"], cc=True)
                    for r in range(4):
                        a0 = (4 * m + r) * 512 + hf * 256
                        P.dma("pool", hpg[a0:a0 + 256, :], ccd[r * 256:(r + 1) * 256, :], r=["ccd"], w=["~hpgw"])
            if l < nlayers - 1:
                if m < nm - 1:
                    emit_exchange()
                else:
                    deferred_exchange.append(emit_exchange)
        if do_sample:
            NS = DEC_SEQ
            shist = f"~shist{l}"
            psb7 = ps[7].bitcast(BF16)

            def prep_cache(src2d, nrows, ncols, kdst, vdst, nheads):
                for b in range(nrows // 128):
                    xi = nxt("x", 2)
                    X = xt[xi]; xname = f"xt{xi}"
                    P.dma("sp", X[:, 0:ncols], src2d[b * 128:(b + 1) * 128, :], w=[xname])
                    P.dve(lambda e: e.tensor_copy(xn[:, 0:ncols], X[:, 0:ncols]), r=[xname], w=["xn"])
                    if vdst is not None:
                        P.dma("pool", vdst[b * 128:(b + 1) * 128, :], xn[:, 0:ncols], r=["xn"], w=[shist])
                    if kdst is not None:
                        for hh in range(nheads):
                            P.pe(lambda e: e.transpose(psb7[0:64, hh * 128:(hh + 1) * 128], xn[:, hh * 64:(hh + 1) * 64], identb[:]),
                                 r=["xn", "identb"], w=["ps7"])
                        P.act(lambda e: e.activation(kst[:, 0:nheads, 0:128], psb7[0:64, 0:nheads * 128].rearrange("p (h k) -> p h k", h=nheads), AF.Copy),
                              r=["ps7"], w=["kst"])
                        P.dma("pool", kdst[:, :, b * 128:(b + 1) * 128].rearrange("h d k -> d h k"), kst[:, 0:nheads, 0:128], r=["kst"], w=[shist])

            prep_cache(ca_k[l], PAST, 512, SS_KA[l], None, 8)
            prep_cache(ca_v[l], PAST, 512, None, SS_VA[l], 8)
            prep_cache(cb_k[l], PAST, 128, SS_KB[l], None, 2)
            prep_cache(cb_v[l], PAST, 128, None, SS_VB[l], 2)
            prep_cache(cc_k[l], 512, 512, SS_KC[l], None, 8)
            prep_cache(cc_v[l], 512, 512, None, SS_VC[l], 8)
            for b in range(PAST // 128):
                xi = nxt("x", 2)
                X = xt[xi]; xname = f"xt{xi}"
                P.dma("sp", X[:, 0:32], cb_ik[l, b * 128:(b + 1) * 128, :], w=[xname])
                P.dve(lambda e: e.tensor_copy(xn[:, 0:32], X[:, 0:32]), r=[xname], w=["xn"])
                P.dve(lambda e: e.tensor_copy(xn[:, 32:64], X[:, 0:32]), r=[xname], w=["xn"])
                P.pe(lambda e: e.transpose(psb7[0:64, 0:128], xn[:, 0:64], identb[:]), r=["xn", "identb"], w=["ps7"])
                P.act(lambda e: e.activation(kst[:, 0, 0:128], psb7[0:64, 0:128], AF.Copy), r=["ps7"], w=["kst"])
                P.dma("pool", SS_IK[l][:, b * 128:(b + 1) * 128], kst[:, 0, 0:128], r=["kst"], w=[shist])
            P.dve(lambda e: e.memset(tot[:], 0.0), w=["tot"])

            def cum_block(kb_, n):
                P.pe(lambda e: e.matmul(ps[5][0:n, 0:8], cstf[0:n, 2, 0:n], lfb[0:n, :], start=True, stop=False), r=["lfb", "cstf"], w=["ps5"])
                P.pe(lambda e: e.matmul(ps[5][0:n, 0:8], ones1[0:1, 0:n], tot[0:1, :], start=False, stop=True), r=["tot", "ones1"], w=["ps5"])
                P.dve(lambda e: e.tensor_copy(cstore[0:n, kb_, :], ps[5][0:n, 0:8]), r=["ps5"], w=["cstore"])
                P.pe(lambda e: e.matmul(ps[5][0:1, 8:16], cstf[0:n, 1, 128 - n:129 - n], cstore[0:n, kb_, :], start=True, stop=True), r=["cstore", "cstf"], w=["ps5"])
                P.dve(lambda e: e.tensor_copy(tot[:], ps[5][0:1, 8:16]), r=["ps5"], w=["tot"])

            for b in range(PAST // 128):
                P.dma("sp", lfb[:], ca_lf[l, b * 128:(b + 1) * 128, :], w=["lfb"])
                cum_block(b, 128)
            norm_block(l, (xs if l == 0 else hs1)[:, :], 0, nrow=NS, rname=("hs1" if l > 0 else None))
            for uname in TM_UNITS:
                W, wn = load_w(l, UNITS.index(uname))
                si = nxt("s", 4)
                for c in range(8):
                    P.pe(lambda e: e.matmul(ps[si][0:NS, :], hT[:, c, 0:NS], W[:, c, :], start=(c == 0), stop=(c == 7)), r=[wn, "hT"], w=[psn[si]])
                k = nxt("st", 2)
                S_ = st[k]; sn = f"st{k}"
                P.act(lambda e: e.activation(S_[0:NS, :], ps[si][0:NS, :], AF.Copy), r=[psn[si]], w=[sn])
                V_, vn = vst[k], f"vst{k}"
                if uname in ("tka", "tva", "tkc", "tvc"):
                    dst = {"tka": s_ak, "tva": s_av, "tkc": s_ck, "tvc": s_cv}[uname]
                    P.dma("pool", dst[l], S_[0:NS, :], r=[sn])
                    if uname in ("tva", "tvc"):
                        P.dve(lambda e: e.tensor_copy(V_[0:NS, :], S_[0:NS, :]), r=[sn], w=[vn])
                        vd = SS_VA[l][PAST:PAST + NS, :] if uname == "tva" else SS_VC[l][512:512 + NS, :]
                        P.dma("pool", vd, V_[0:NS, :], r=[vn], w=[shist])
                else:
                    P.dma("pool", s_bk[l], S_[0:NS, 0:128], r=[sn])
                    P.dma("pool", s_bv[l], S_[0:NS, 128:256], r=[sn])
                    P.dma("pool", s_ik[l], S_[0:NS, 256:288], r=[sn])
                    P.dve(lambda e: e.tensor_copy(V_[0:NS, 0:128], S_[0:NS, 128:256]), r=[sn], w=[vn])
                    P.dma("pool", SS_VB[l][PAST:PAST + NS, :], V_[0:NS, 0:128], r=[vn], w=[shist])
                    P.dve(lambda e: e.tensor_tensor(lfb[0:NS, :], S_[0:NS, 288:296], bfb[0:NS, l * 8:(l + 1) * 8], ALU.add), r=[sn, "bfb"], w=["lfb"])
                    P.act(lambda e: e.activation(lfb[0:NS, :], lfb[0:NS, :], AF.Exp, scale=-1.0), r=["lfb"], w=["lfb"])
                    P.act(lambda e: e.activation(lfb[0:NS, :], lfb[0:NS, :], AF.Ln, bias=1.0), r=["lfb"], w=["lfb"])
                    P.dve(lambda e: e.tensor_scalar(lfb[0:NS, :], lfb[0:NS, :], -1.0, None, ALU.mult), r=["lfb"], w=["lfb"])
                    P.dma("pool", s_lf[l], lfb[0:NS, :], r=["lfb"])
                    cum_block(8, NS)
                    P.dve(lambda e: e.tensor_scalar(wsgn[0:NS, 0, :], S_[0:NS, 296:304], 0.0, 2.0, ALU.is_ge, ALU.mult), r=[sn], w=["wsgn"])
                    P.dve(lambda e: e.tensor_scalar(wsgn[0:NS, 0, :], wsgn[0:NS, 0, :], -1.0, None, ALU.add), r=["wsgn"], w=["wsgn"])
                    P.dve(lambda e: e.scalar_tensor_tensor(wabs[0:NS, 0, :], S_[0:NS, 296:304], IDXS, wsgn[0:NS, 0, :], ALU.mult, ALU.mult), r=[sn, "wsgn"], w=["wabs"])
            P.pe(lambda e: e.matmul(ps[5][:, 16:24], cstf[0:NS, 3, :], cstore[0:NS, 8, :], start=True, stop=True), r=["cstore", "cstf"], w=["ps5"])
            P.dve(lambda e: e.tensor_copy(cbc[:], ps[5][:, 16:24]), r=["ps5"], w=["cbc"])
            for h in range(8):
                P.dve(lambda e: e.tensor_scalar(nbias[:, 0:9, h], cstore[:, 0:9, h], cbc[:, h:h + 1], -1.0, ALU.subtract, ALU.mult), r=["cstore", "cbc"], w=["nbias"])
            P.dve(lambda e: e.tensor_tensor(rq[0:NS, 0, :], cstore[0:NS, 8, :], cbc[0:NS, :], ALU.subtract), r=["cstore", "cbc"], w=["rq"])
            P.pe(lambda e: e.transpose(ps[6][0:8, 0:NS], rq[0:NS, 0, :], cstf[0:NS, 0, 0:NS]), r=["rq", "cstf"], w=["ps6"])
            P.dve(lambda e: e.tensor_copy(rT[:, 0:NS], ps[6][0:8, 0:NS]), r=["ps6"], w=["rT"])

            def s_evac_k(dst3, koff):
                def f(j, pt, pn):
                    P.act(lambda e: e.activation(kst[:, j, 0:NS], pt[0:64, 0:NS], AF.Copy), r=[pn], w=["kst"])
                    if j == 7:
                        P.dma("pool", dst3[:, :, koff:koff + NS].rearrange("h d k -> d h k"), kst[:, :, 0:NS], r=["kst"], w=[shist])
                return f

            def s_evac_q(j, pt, pn):
                P.act(lambda e: e.activation(Q[0:64, j, 0:NS], pt[0:64, 0:NS], AF.Copy, scale=SCALE), r=[pn], w=["Q"])

            def s_evac_z(zt, zn):
                def f(j, pt, pn):
                    P.act(lambda e: e.activation(zt[:, j, 0:NS], pt[0:64, 0:NS], AF.Silu), r=[pn], w=[zn])
                return f

            fm_unit(l, "ka", NS, s_evac_k(SS_KA[l], PAST))
            fm_unit(l, "za", NS, s_evac_z(zg["a"], "zga"))
            fm_unit(l, "qa", NS, s_evac_q)
            for h in range(8):
                P.dma("sp", Q[64:65, h, 0:NS], rT[h:h + 1, 0:NS], r=["rT"], w=["Q"])
            for half in range(2):
                at = Attn()
                for sbk in range(3):
                    nk = 512 if sbk < 2 else NS
                    bi = load_kv(SS_KA[l], SS_VA[l], half * 4, 4, sbk * 512, nk, shist)
                    for kb in range((nk + 127) // 128):
                        n = min(128, nk - kb * 128)
                        adds = [(0, NS, identb[0:NS, 0:NS], ma0b[0:NS, 0:NS], ["identb", "ma0b"])] if sbk == 2 else []
                        for i in range(4):
                            hh = half * 4 + i
                            at.tile(dict(kT=kbuf[bi][0:65, i, kb * 128:kb * 128 + n], v=vbuf[bi][0:n, kb, i, 0:65], n=n, qlo=0, qhi=NS,
                                         qap=Q[0:65, hh, 0:NS], adds=adds, bias=nbias[0:n, 4 * sbk + kb, hh:hh + 1],
                                         names=[f"kbuf{bi}", f"vbuf{bi}"], O=ps[4 + i], oname=psn[4 + i],
                                         first=(sbk == 0 and kb == 0), last=(sbk == 2)))
                at.flush()
                for i in range(4):
                    finish_head(ps[4 + i], psn[4 + i], zg["a"], "zga", half * 4 + i, NS, slice(0, NS))
            fm_unit(l, "kc", NS, s_evac_k(SS_KC[l], 512))
            fm_unit(l, "zc", NS, s_evac_z(zg["c"], "zgc"))
            fm_unit(l, "qc", NS, s_evac_q)
            for half in range(2):
                at = Attn()
                for sbk in range(2):
                    nk = 512 if sbk < 1 else NS
                    bi = load_kv(SS_KC[l], SS_VC[l], half * 4, 4, sbk * 512, nk, shist)
                    for kb in range((nk + 127) // 128):
                        n = min(128, nk - kb * 128)
                        for i in range(4):
                            hh = half * 4 + i
                            adds = []
                            if sbk == 0 and kb == 3:
                                adds = [(0, NS, identb[:], bc[:, 1, hh, 0:NS], ["identb", "btile"])]
                            if sbk == 1:
                                adds = [(0, NS, identb[0:NS, 0:NS], bc[0:NS, 0, hh, 0:NS], ["identb", "btile"])]
                            at.tile(dict(kT=kbuf[bi][0:64, i, kb * 128:kb * 128 + n], v=vbuf[bi][0:n, kb, i, 0:65], n=n, qlo=0, qhi=NS,
                                         qap=Q[0:64, hh, 0:NS], adds=adds, bias=None,
                                         names=[f"kbuf{bi}", f"vbuf{bi}"], O=ps[4 + i], oname=psn[4 + i],
                                         first=(sbk == 0 and kb == 0), last=(sbk == 1)))
                at.flush()
                for i in range(4):
                    finish_head(ps[4 + i], psn[4 + i], zg["c"], "zgc", half * 4 + i, NS, slice(0, NS))
            def s_evac_bx(j, pt, pn):
                if j < 2:
                    P.act(lambda e: e.activation(kst[:, j, 0:NS], pt[0:64, 0:NS], AF.Copy), r=[pn], w=["kst"])
                    if j == 1:
                        P.dma("pool", SS_KB[l][:, :, PAST:PAST + NS].rearrange("h d k -> d h k"), kst[:, 0:2, 0:NS], r=["kst"], w=[shist])
                elif j < 6:
                    P.act(lambda e: e.activation(iqT[:, j - 2, 0:NS], pt[0:64, 0:NS], AF.Copy), r=[pn], w=["iqT"])
                elif j == 6:
                    P.act(lambda e: e.activation(kst[:, 2, 0:NS], pt[0:64, 0:NS], AF.Copy), r=[pn], w=["kst"])
                    P.dma("pool", SS_IK[l][:, PAST:PAST + NS], kst[:, 2, 0:NS], r=["kst"], w=[shist])
            fm_unit(l, "bx", NS, s_evac_bx)
            fm_unit(l, "zb", NS, s_evac_z(zg["b"], "zgb"))
            fm_unit(l, "qb", NS, s_evac_q)
            NK = PAST + NS
            for k0 in range(0, NK, 512):
                nk = min(512, NK - k0)
                ii = nxt("ik", 2)
                P.dma("sp", ikbuf[ii][:, 0:nk], SS_IK[l][:, k0:k0 + nk], r=[shist], w=[f"ikbuf{ii}"])
                for h in range(8):
                    base = 32 * (h % 2)
                    pi_ = 2 + nxt("ips", 2)
                    P.pe(lambda e: e.matmul(ps[pi_][0:NS, 0:nk], iqT[base:base + 32, h // 2, 0:NS], ikbuf[ii][base:base + 32, 0:nk], start=True, stop=True),
                         r=["iqT", f"ikbuf{ii}"], w=[psn[pi_]])
                    ri = nxt("rr", 2)
                    P.act(lambda e: e.activation(Rr[ri][0:NS, 0:nk], ps[pi_][0:NS, 0:nk], AF.Relu, scale=wabs[0:NS, 0, h:h + 1]), r=[psn[pi_], "wabs"], w=[f"Rr{ri}"])
                    if h == 0:
                        P.dve(lambda e: e.tensor_scalar(SC[0:NS, k0:k0 + nk], Rr[ri][0:NS, 0:nk], wsgn[0:NS, 0, 0:1], None, ALU.mult), r=[f"Rr{ri}", "wsgn"], w=["SC"])
                    else:
                        P.dve(lambda e: e.scalar_tensor_tensor(SC[0:NS, k0:k0 + nk], Rr[ri][0:NS, 0:nk], wsgn[0:NS, 0, h:h + 1], SC[0:NS, k0:k0 + nk], ALU.mult, ALU.add),
                              r=[f"Rr{ri}", "wsgn", "SC"], w=["SC"])
            P.dve(lambda e: e.memset(cntb[:], 0.0), w=["cntb"])
            P.dve(lambda e: e.memset(small[:, 8:9], 0.0), w=["cand"])
            for it in range(NBIS):
                stp = 64.0 * (0.5 ** it)
                P.dve(lambda e: e.tensor_scalar(junk[0:NS, 0:NK], SC[0:NS, 0:NK], small[0:NS, 8:9], 0.0, ALU.is_ge, ALU.add, accum_out=cntb[0:NS, it:it + 1]),
                      r=["SC", "cand", "cntb"], w=["junk", "cntb"])
                a, b_ = (stp, -0.5 * stp) if it < NBIS - 1 else (stp, -stp)
                P.dve(lambda e: e.tensor_scalar(small[0:NS, 9:10], cntb[0:NS, it:it + 1], float(TOPK), a, ALU.is_ge, ALU.mult), r=["cntb"], w=["fl"])
                P.dve(lambda e: e.scalar_tensor_tensor(small[0:NS, 8:9], small[0:NS, 9:10], b_, small[0:NS, 8:9], ALU.add, ALU.add), r=["fl", "cand"], w=["cand"])
            at = Attn()
            for sbk in range(3):
                k0 = sbk * 512
                nk = min(512, NK - k0)
                mi = nxt("mb", 2)
                P.dve(lambda e: e.tensor_scalar(Mb[mi][0:NS, 0:nk], SC[0:NS, k0:k0 + nk], small[0:NS, 8:9], NEG, ALU.is_lt, ALU.mult), r=["SC", "cand"], w=[f"Mb{mi}"])
                bi = load_kv(SS_KB[l], SS_VB[l], 0, 2, k0, nk, shist)
                for kb in range((nk + 127) // 128):
                    n = min(128, nk - kb * 128)
                    gkb = sbk * 4 + kb
                    for j in range(2):
                        adds = [(0, 4 * NS, Mb[mi][0:NS, kb * 128:kb * 128 + n], i4b[0:NS, :, 0:NS], [f"Mb{mi}", "i4b"])]
                        if gkb >= 7:
                            for hq in range(4):
                                if gkb == 7:
                                    adds.append((hq * NS, (hq + 1) * NS, identb[:], b5[:, 1, 4 * j + hq, 0:NS], ["identb", "btile"]))
                                else:
                                    adds.append((hq * NS, (hq + 1) * NS, identb[0:NS, 0:NS], b5[0:NS, 0, 4 * j + hq, 0:NS], ["identb", "btile"]))
                        at.tile(dict(kT=kbuf[bi][0:64, j, kb * 128:kb * 128 + n], v=vbuf[bi][0:n, kb, j, 0:65], n=n, qlo=0, qhi=4 * NS,
                                     qap=Q[0:64, 4 * j:4 * j + 4, 0:NS], adds=adds, bias=None,
                                     names=[f"kbuf{bi}", f"vbuf{bi}"], O=ps[4 + j], oname=psn[4 + j],
                                     first=(gkb == 0), last=(gkb == 8)))
            at.flush()
            for j in range(2):
                finish_head(ps[4 + j], psn[4 + j], zg["b"], "zgb", (4 * j, 4 * j + 4), 4 * NS, slice(0, NS))
            allz = [zg["a"], zg["b"], zg["c"]]
            alln = ["zga", "zgb", "zgc"]
            for u in range(6):
                Wo_, won = load_wo(l, u)
                for hq in range(4):
                    hidx = u * 4 + hq
                    zt, zn = allz[hidx // 8], alln[hidx // 8]
                    for n_ in range(2):
                        P.pe(lambda e: e.matmul(ps[n_][0:NS, :], zt[:, hidx % 8, 0:NS], Wo_[:, hq, n_ * 512:(n_ + 1) * 512], start=(hidx == 0), stop=(hidx == 23)),
                             r=[zn, won], w=[psn[n_]])
            xi = nxt("x", 2)
            X = xt[xi]; xname = f"xt{xi}"
            P.dma("sp", X[0:NS, :], (xs if l == 0 else hs1)[:, :], r=(["hs1"] if l > 0 else []), w=[xname])
            for n_ in range(2):
                P.dve(lambda e: e.tensor_tensor(X[0:NS, n_ * 512:(n_ + 1) * 512], X[0:NS, n_ * 512:(n_ + 1) * 512], ps[n_][0:NS, :], ALU.add), r=[xname, psn[n_]], w=[xname])
            if l < nlayers - 1:
                P.dma("pool", hs1[:, :], X[0:NS, :], r=[xname], w=["hs1"])
            else:
                P.act(lambda e: e.activation(junk[0:NS, 0:1024], X[0:NS, :], AF.Square, accum_out=small[0:NS, 16:17]), r=[xname], w=["junk", "fs0"])
                P.dve(lambda e: e.tensor_scalar(small[0:NS, 17:18], small[0:NS, 16:17], 1.0 / D_MODEL, EPS, ALU.mult, ALU.add), r=["fs0"], w=["fs1"])
                P.act(lambda e: e.activation(small[0:NS, 18:19], small[0:NS, 17:18], AF.Ln), r=["fs1"], w=["fs2"])
                P.act(lambda e: e.activation(small[0:NS, 19:20], small[0:NS, 18:19], AF.Exp, scale=-0.5), r=["fs2"], w=["fs3"])
                P.dve(lambda e: e.scalar_tensor_tensor(X[0:NS, :], X[0:NS, :], small[0:NS, 19:20], fgb[0:NS, :], ALU.mult, ALU.mult), r=[xname, "fs3", "fgb"], w=[xname])
                P.dma("pool", y_s[:, :], X[0:NS, :], r=[xname])
        for fx in deferred_exchange:
            fx()
    P.finalize_and_emit()
    return nc, es, P


def _t5_bucket_np(rel):
    nb = 16
    max_exact = 8
    ret = np.where(rel > 0, nb, 0)
    n = np.abs(rel)
    nf = np.maximum(n, 1).astype(np.float32)
    large = max_exact + (np.log(nf / max_exact) / math.log(128 / max_exact) * (nb - max_exact)).astype(np.int32)
    large = np.minimum(large, nb - 1)
    return ret + np.where(n < max_exact, n, large)


def _constants():
    p = np.arange(128)[:, None]
    f = np.arange(128)[None, :]
    cst = np.zeros((128, 8, 128), np.float32)
    cst[:, 0] = (p == f)
    cst[:, 1] = (p + f == 127)
    cst[:, 2] = (p <= f)
    cst[0, 3, :] = 1.0
    cst[:, 4] = np.where(p > f, NEG, 0.0)
    cst[:, 5] = np.where((p >= 64) & (f < 64), NEG, 0.0)
    cst[:, 6] = np.where((p < 64) & (f >= 64), NEG, 0.0)
    cst[:, 7] = np.where((p < 64) & (f >= 64), -1e30, 0.0)
    rel = 127 - np.arange(LTAB)
    bk = _t5_bucket_np(rel.astype(np.int32))
    oh5 = np.zeros((32, LTAB), np.float32)
    oh5[bk, np.arange(LTAB)] += 1.0
    far = int(_t5_bucket_np(np.array([-100000], np.int32))[0])
    oh5[far, :] -= 1.0
    idx = np.clip(rel, -128, 128) + 128
    ohc = np.zeros((3 * 128, LTAB), np.float32)
    ohc[idx, np.arange(LTAB)] += 1.0
    ohc[0, :] -= 1.0
    return cst.reshape(128, 8 * 128), oh5, ohc.reshape(3, 128, LTAB)


_PROG = {}


def _get_prog(key=(True, 4, DEPTH)):
    if key not in _PROG:
        _PROG[key] = build_program(*key)
    return _PROG[key]


def _host_inputs(x_prompt, norm_g, w_in, b_f, t5_bias, c_rel_bias, w_out, final_g):
    cst, oh5, ohc = _constants()
    wus = np.zeros((DEPTH, NU, 128, 8, 512), np.float32)
    for l in range(DEPTH):
        for u, name in enumerate(UNITS):
            cols = np.array(_unit_cols(name))
            m = cols >= 0
            w = np.zeros((D_MODEL, 512), np.float32)
            w[:, m] = w_in[l][:, cols[m]]
            wus[l, u] = w.reshape(8, 128, 512).transpose(1, 0, 2)
    wos = np.ascontiguousarray(w_out.reshape(DEPTH, 6, 4, 64, D_MODEL).transpose(0, 1, 3, 2, 4))
    gcol = np.ascontiguousarray(norm_g.reshape(DEPTH, 8, 128).transpose(2, 0, 1).reshape(128, DEPTH * 8))
    crel = np.zeros((DEPTH, 384, 8), np.float32)
    crel[:, :257] = c_rel_bias
    common = dict(wu=wus, wo=wos, gcol=gcol, fg=np.ascontiguousarray(final_g.reshape(1, D_MODEL)),
                  bfb=np.ascontiguousarray(b_f.reshape(1, DEPTH * 8)), t5=np.ascontiguousarray(t5_bias),
                  crel=crel.reshape(DEPTH, 3, 128, 8), oh5=oh5, ohc=ohc, cst=cst)
    return common


def _percore(j):
    pc = np.zeros((128, 1024), np.float32)
    p = np.arange(128)
    pc[:, 0:512] = (j * 512 + np.arange(512))[None, :]
    for rr in range(16):
        pc[:, 512 + rr] = rr * 128 + p
    for qb in range(4):
        for ch in range(4):
            pc[:, 528 + qb * 4 + ch] = (8 * j + 2 * qb + (p >= 64) + 1) * 64 - ch * 512
        for rr in range(17):
            pc[:, 544 + (qb * 17 + rr) * 2] = 1.0 if (rr - 1) == 4 * j + qb else 0.0
            pc[:, 544 + (qb * 17 + rr) * 2 + 1] = 1.0 if (rr - 1) == 4 * j + qb - 1 else 0.0
    for r in range(4):
        pc[:, 680 + r] = 1.0 if j == r else 0.0
    pc[:, 684] = -30000.0 if j == 0 else 0.0
    return pc


def _in_maps(x_prompt, x_sample, cache_a_k, cache_a_v, cache_a_logf, cache_b_k, cache_b_v, cache_b_idx_k,
             cache_c_k, cache_c_v, norm_g, w_in, b_f, t5_bias, c_rel_bias, w_out, final_g):
    f = lambda a: np.ascontiguousarray(np.asarray(a, dtype=np.float32))
    x_prompt, norm_g, w_in, b_f, t5_bias, c_rel_bias, w_out, final_g = map(f, (x_prompt, norm_g, w_in, b_f, t5_bias, c_rel_bias, w_out, final_g))
    common = _host_inputs(x_prompt, norm_g, w_in, b_f, t5_bias, c_rel_bias, w_out, final_g)
    common["iota5"] = np.ascontiguousarray(np.broadcast_to(np.arange(512, dtype=np.float32)[None, :], (128, 512)))
    in_maps = []
    for c in range(8):
        b, j = c // 4, c % 4
        m = dict(common)
        xb = x_prompt[b].reshape(NG, 512, D_MODEL)
        m["xp"] = x_prompt[b]
        m["xq"] = np.ascontiguousarray(xb[j::4])
        xpv = np.zeros((4, 512, D_MODEL), np.float32)
        for mm in range(4):
            if 4 * mm + j - 1 >= 0:
                xpv[mm] = xb[4 * mm + j - 1]
        m["xprev"] = xpv
        m["pcore"] = _percore(j)
        m["xs"] = f(x_sample[c])
        m["ca_k"] = f(cache_a_k[:, c]).reshape(DEPTH, PAST, 512); m["ca_v"] = f(cache_a_v[:, c]).reshape(DEPTH, PAST, 512)
        m["ca_lf"] = f(cache_a_logf[:, c]).reshape(DEPTH, PAST, 8)
        m["cb_k"] = f(cache_b_k[:, c]).reshape(DEPTH, PAST, 128); m["cb_v"] = f(cache_b_v[:, c]).reshape(DEPTH, PAST, 128)
        m["cb_ik"] = f(cache_b_idx_k[:, c]).reshape(DEPTH, PAST, 32)
        m["cc_k"] = f(cache_c_k[:, c]).reshape(DEPTH, 512, 512); m["cc_v"] = f(cache_c_v[:, c]).reshape(DEPTH, 512, 512)
        in_maps.append(m)
    return in_maps


def kernel(x_prompt, x_sample, cache_a_k, cache_a_v, cache_a_logf, cache_b_k, cache_b_v, cache_b_idx_k,
           cache_c_k, cache_c_v, norm_g, w_in, b_f, t5_bias, c_rel_bias, w_out, final_g):
    nc, es, P = _get_prog()
    in_maps = _in_maps(x_prompt, x_sample, cache_a_k, cache_a_v, cache_a_logf, cache_b_k, cache_b_v, cache_b_idx_k,
                       cache_c_k, cache_c_v, norm_g, w_in, b_f, t5_bias, c_rel_bias, w_out, final_g)
    res = run_bass_kernel_spmd(nc, in_maps, core_ids=list(range(8)))
    R = res.results
    st = lambda name, shp: np.stack([R[4 * b][name] for b in range(2)], axis=1).reshape(shp)
    y_prompt = np.zeros((BATCH, NG, 512, D_MODEL), np.float32)
    for c in range(8):
        b, j = c // 4, c % 4
        y_prompt[b, j::4] = R[c]["y_q"].reshape(4, 512, D_MODEL)
    y_prompt = y_prompt.reshape(BATCH, SEQ, D_MODEL)
    ss = lambda name, shp: np.stack([R[b][name] for b in range(DEC_BATCH)], axis=1).reshape(shp)
    y_sample = np.stack([R[b]["y_s"] for b in range(DEC_BATCH)], axis=0)
    outs = [y_prompt, y_sample,
            st("o_ak", (DEPTH, BATCH, SEQ, H, HD)), st("o_av", (DEPTH, BATCH, SEQ, H, HD)), st("o_lf", (DEPTH, BATCH, SEQ, H)),
            st("o_bk", (DEPTH, BATCH, SEQ, KVB, HD)), st("o_bv", (DEPTH, BATCH, SEQ, KVB, HD)), st("o_ik", (DEPTH, BATCH, SEQ, IDX_D)),
            st("o_ck", (DEPTH, BATCH, 512, H, HD)), st("o_cv", (DEPTH, BATCH, 512, H, HD)),
            ss("s_ak", (DEPTH, DEC_BATCH, DEC_SEQ, H, HD)), ss("s_av", (DEPTH, DEC_BATCH, DEC_SEQ, H, HD)), ss("s_lf", (DEPTH, DEC_BATCH, DEC_SEQ, H)),
            ss("s_bk", (DEPTH, DEC_BATCH, DEC_SEQ, KVB, HD)), ss("s_bv", (DEPTH, DEC_BATCH, DEC_SEQ, KVB, HD)), ss("s_ik", (DEPTH, DEC_BATCH, DEC_SEQ, IDX_D)),
            ss("s_ck", (DEPTH, DEC_BATCH, DEC_SEQ, H, HD)), ss("s_cv", (DEPTH, DEC_BATCH, DEC_SEQ, H, HD))]
    return tuple(outs)
```
